# Optimizing a Trainium2 kernel written in Bass

```python
import math
import jax
import jax.numpy as jnp
from jax import lax
import numpy as np

D_MODEL = 1024
BATCH = 8
SEQ = 2048
DEPTH = 4
DEC_BATCH = 128
DEC_SEQ = 4
PAST_LEN = 16384
PAGE_SIZE = 128

N_EVEN = (DEPTH + 1) // 2
N_ODD = DEPTH // 2
D_A = D_MODEL // 2
S5_GROUP = 16
G_A = D_A // S5_GROUP
S5_STATE = 64
D_B = D_MODEL // 2
H_B = 8
HD_B = D_B // H_B
CHUNK = 128
D_IN_EVEN = D_A + 2 * D_B
D_C = D_MODEL // 2
POOL_WINDOWS = (2, 4, 8, 16)
N_POOL = len(POOL_WINDOWS)
C_GROUP = D_C // N_POOL
POOL_BUF = max(POOL_WINDOWS) - 1
D_D = D_MODEL // 2
CONV_W = 3
D_IN_ODD = D_C + 3 * D_D
D_FF = ((8 * D_MODEL // 3 + 255) // 256) * 256
EPS = 1e-6

kernel_name = 'hybrid_s5_gmlp_pool_conv_decoder_step'


def rmsnorm(x, g):
    xf = x.astype(jnp.float32)
    y = xf * lax.rsqrt(jnp.mean(xf * xf, axis=-1, keepdims=True) + EPS)
    return (y * g.astype(jnp.float32)).astype(x.dtype)


def swiglu(x, w_gate, w_up, w_down):
    return (jax.nn.silu(x @ w_gate) * (x @ w_up)) @ w_down


def _complex_affine_combine(e1, e2):
    a1r, a1i, b1r, b1i = e1
    a2r, a2i, b2r, b2i = e2
    return (a2r * a1r - a2i * a1i,
            a2r * a1i + a2i * a1r,
            a2r * b1r - a2i * b1i + b2r,
            a2r * b1i + a2i * b1r + b2i)


def s5_mixer(u, h0_re, h0_im, lam_re, lam_im, log_dt, b_re, b_im, c_re, c_im, d_skip, glu_w, glu_b):
    f32 = jnp.float32
    bsz, l, _ = u.shape
    ug = u.astype(f32).reshape(bsz, l, G_A, S5_GROUP)
    lr = lam_re.astype(f32)
    li = lam_im.astype(f32)
    dt = jnp.exp(log_dt.astype(f32))[:, None]
    mag = jnp.exp(lr * dt)
    ang = li * dt
    ab_re = mag * jnp.cos(ang)
    ab_im = mag * jnp.sin(ang)
    den = lr * lr + li * li
    f_re = ((ab_re - 1.0) * lr + ab_im * li) / den
    f_im = (ab_im * lr - (ab_re - 1.0) * li) / den
    br = b_re.astype(f32)
    bi = b_im.astype(f32)
    bb_re = f_re[..., None] * br - f_im[..., None] * bi
    bb_im = f_re[..., None] * bi + f_im[..., None] * br
    x_re = jnp.einsum('blgp,gnp->blgn', ug, bb_re)
    x_im = jnp.einsum('blgp,gnp->blgn', ug, bb_im)
    a_re = jnp.broadcast_to(ab_re, x_re.shape)
    a_im = jnp.broadcast_to(ab_im, x_im.shape)
    p_re, p_im, s_re, s_im = lax.associative_scan(_complex_affine_combine, (a_re, a_im, x_re, x_im), axis=1)
    h0r = h0_re.astype(f32)[:, None]
    h0i = h0_im.astype(f32)[:, None]
    h_re = p_re * h0r - p_im * h0i + s_re
    h_im = p_re * h0i + p_im * h0r + s_im
    y = (jnp.einsum('blgn,gpn->blgp', h_re, c_re.astype(f32))
         - jnp.einsum('blgn,gpn->blgp', h_im, c_im.astype(f32))
         + d_skip.astype(f32) * ug)
    y = y.reshape(bsz, l, D_A)
    z = jax.nn.gelu(y)
    out = z * jax.nn.sigmoid(z @ glu_w.astype(f32) + glu_b.astype(f32))
    return out.astype(u.dtype), h_re[:, -1].astype(h0_re.dtype), h_im[:, -1].astype(h0_im.dtype)


def sgu_mixer(u, v, norm_g, w_s, b_s):
    vn = rmsnorm(v, norm_g)
    bsz, l, _ = u.shape
    cl = min(l, CHUNK)
    nch = l // cl
    mask = jnp.tril(jnp.ones((cl, cl), dtype=bool))
    w = jnp.where(mask, w_s[:, :cl, :cl], 0)
    vc = vn.reshape(bsz, nch, cl, H_B, HD_B)
    mixed = jnp.einsum('hij,bcjhd->bcihd', w, vc) + b_s[:, :cl].T[None, None, :, :, None]
    return u * mixed.reshape(bsz, l, D_B), vn


def pool_mixer(xc, buf, prefix_valid, pool_w, pool_scale):
    f32 = jnp.float32
    bsz, l, _ = xc.shape
    full = jnp.concatenate([buf.astype(xc.dtype), xc], axis=1)
    cs = jnp.pad(jnp.cumsum(full.astype(f32), axis=1), ((0, 0), (1, 0), (0, 0)))
    valid = jnp.concatenate([jnp.full((POOL_BUF,), prefix_valid, f32), jnp.ones((l,), f32)])
    cnt = jnp.pad(jnp.cumsum(valid), (1, 0))
    end = POOL_BUF + 1
    groups = []
    for gi, w in enumerate(POOL_WINDOWS):
        lo, hi = gi * C_GROUP, (gi + 1) * C_GROUP
        s = cs[:, end:end + l, lo:hi] - cs[:, end - w:end - w + l, lo:hi]
        n = cnt[end:end + l] - cnt[end - w:end - w + l]
        groups.append(s / n[None, :, None])
    pooled = jnp.concatenate(groups, axis=-1)
    diff = (pooled - xc.astype(f32)).reshape(bsz, l, N_POOL, C_GROUP)
    y = jnp.einsum('blgc,gcd->blgd', diff, pool_w.astype(f32)).reshape(bsz, l, D_C) * pool_scale.astype(f32)
    return y.astype(xc.dtype), full[:, -POOL_BUF:]


def conv_mixer(xd, b_gate, c_gate, buf, conv_w, conv_b):
    z = c_gate * xd
    l = z.shape[1]
    full = jnp.concatenate([buf.astype(z.dtype), z], axis=1)
    conv = conv_b
    for k in range(CONV_W):
        conv = conv + full[:, k:k + l] * conv_w[k]
    return b_gate * conv, full[:, -(CONV_W - 1):]


def run_trunk(x, s5_re, s5_im, pool_buf, conv_buf, prefix_valid, W):
    h = x
    new_re, new_im, new_v, new_pool, new_conv = [], [], [], [], []
    for layer in range(DEPTH):
        i = layer // 2
        hn = rmsnorm(h, W['norm_mix'][layer])
        if layer % 2 == 0:
            proj = hn @ W['w_in_even'][i]
            u_a = proj[..., :D_A]
            u_b = proj[..., D_A:D_A + D_B]
            v_b = proj[..., D_A + D_B:]
            y_a, hr, hi = s5_mixer(u_a, s5_re[i], s5_im[i], W['s5_lambda_re'][i], W['s5_lambda_im'][i],
                                   W['s5_log_dt'][i], W['s5_b_re'][i], W['s5_b_im'][i], W['s5_c_re'][i],
                                   W['s5_c_im'][i], W['s5_d'][i], W['s5_glu_w'][i], W['s5_glu_b'][i])
            y_b, vn = sgu_mixer(u_b, v_b, W['sgu_norm'][i], W['sgu_w'][i], W['sgu_b'][i])
            mix = jnp.concatenate([y_a, y_b], axis=-1) @ W['w_out_even'][i]
            new_re.append(hr)
            new_im.append(hi)
            new_v.append(vn)
        else:
            proj = hn @ W['w_in_odd'][i]
            x_c = proj[..., :D_C]
            x_d = proj[..., D_C:D_C + D_D]
            b_g = proj[..., D_C + D_D:D_C + 2 * D_D]
            c_g = proj[..., D_C + 2 * D_D:]
            y_c, nb_pool = pool_mixer(x_c, pool_buf[i], prefix_valid, W['pool_w'][i], W['pool_scale'][i])
            y_d, nb_conv = conv_mixer(x_d, b_g, c_g, conv_buf[i], W['conv_w'][i], W['conv_b'][i])
            mix = jnp.concatenate([y_c, y_d], axis=-1) @ W['w_out_odd'][i]
            new_pool.append(nb_pool)
            new_conv.append(nb_conv)
        h = h + mix
        h = h + swiglu(rmsnorm(h, W['norm_ffn'][layer]), W['ffn_w_gate'][layer], W['ffn_w_up'][layer],
                       W['ffn_w_down'][layer])
    y = rmsnorm(h, W['norm_final'])
    return y, jnp.stack(new_re), jnp.stack(new_im), jnp.stack(new_v), jnp.stack(new_pool), jnp.stack(new_conv)


def setup_inputs(seed: int = 0) -> dict:
    key = jax.random.key(seed)
    keys = jax.random.split(key, 40)
    f32 = jnp.float32

    def nrm(idx, shape, scale):
        return scale * jax.random.normal(keys[idx], shape, f32)

    n_idx = jnp.arange(S5_STATE, dtype=f32)
    return {
        'x_prompt': nrm(0, (BATCH, SEQ, D_MODEL), 1.0),
        'x_sample': nrm(1, (DEC_BATCH, DEC_SEQ, D_MODEL), 1.0),
        'state_s5_re': nrm(2, (N_EVEN, DEC_BATCH, G_A, S5_STATE), 0.5),
        'state_s5_im': nrm(3, (N_EVEN, DEC_BATCH, G_A, S5_STATE), 0.5),
        'state_pool': nrm(4, (N_ODD, DEC_BATCH, POOL_BUF, D_C), 1.0),
        'state_conv': nrm(5, (N_ODD, DEC_BATCH, CONV_W - 1, D_D), 0.5),
        'norm_mix': 1.0 + nrm(6, (DEPTH, D_MODEL), 0.02),
        'norm_ffn': 1.0 + nrm(7, (DEPTH, D_MODEL), 0.02),
        'norm_final': 1.0 + nrm(8, (D_MODEL,), 0.02),
        'w_in_even': nrm(9, (N_EVEN, D_MODEL, D_IN_EVEN), D_MODEL ** -0.5),
        'w_out_even': nrm(10, (N_EVEN, D_A + D_B, D_MODEL), (D_A + D_B) ** -0.5),
        's5_lambda_re': -0.5 + nrm(11, (N_EVEN, G_A, S5_STATE), 0.01),
        's5_lambda_im': jnp.broadcast_to(math.pi * n_idx, (N_EVEN, G_A, S5_STATE)) + nrm(12, (N_EVEN, G_A, S5_STATE), 0.01),
        's5_log_dt': jax.random.uniform(keys[13], (N_EVEN, G_A), f32, math.log(1e-3), math.log(1e-1)),
        's5_b_re': nrm(14, (N_EVEN, G_A, S5_STATE, S5_GROUP), S5_GROUP ** -0.5),
        's5_b_im': nrm(15, (N_EVEN, G_A, S5_STATE, S5_GROUP), S5_GROUP ** -0.5),
        's5_c_re': nrm(16, (N_EVEN, G_A, S5_GROUP, S5_STATE), (2 * S5_STATE) ** -0.5),
        's5_c_im': nrm(17, (N_EVEN, G_A, S5_GROUP, S5_STATE), (2 * S5_STATE) ** -0.5),
        's5_d': nrm(18, (N_EVEN, G_A, S5_GROUP), 1.0),
        's5_glu_w': nrm(19, (N_EVEN, D_A, D_A), D_A ** -0.5),
        's5_glu_b': nrm(20, (N_EVEN, D_A), 0.01),
        'sgu_norm': 1.0 + nrm(21, (N_EVEN, D_B), 0.02),
        'sgu_w': nrm(22, (N_EVEN, H_B, CHUNK, CHUNK), CHUNK ** -0.5),
        'sgu_b': 1.0 + nrm(23, (N_EVEN, H_B, CHUNK), 0.02),
        'w_in_odd': nrm(24, (N_ODD, D_MODEL, D_IN_ODD), D_MODEL ** -0.5),
        'w_out_odd': nrm(25, (N_ODD, D_C + D_D, D_MODEL), (D_C + D_D) ** -0.5),
        'pool_w': nrm(26, (N_ODD, N_POOL, C_GROUP, C_GROUP), C_GROUP ** -0.5),
        'pool_scale': 1.0 + nrm(27, (N_ODD, D_C), 0.02),
        'conv_w': nrm(28, (N_ODD, CONV_W, D_D), CONV_W ** -0.5),
        'conv_b': nrm(29, (N_ODD, D_D), 0.01),
        'ffn_w_gate': nrm(30, (DEPTH, D_MODEL, D_FF), D_MODEL ** -0.5),
        'ffn_w_up': nrm(31, (DEPTH, D_MODEL, D_FF), D_MODEL ** -0.5),
        'ffn_w_down': nrm(32, (DEPTH, D_FF, D_MODEL), D_FF ** -0.5),
    }


def reference(x_prompt, x_sample, state_s5_re, state_s5_im, state_pool, state_conv,
              norm_mix, norm_ffn, norm_final, w_in_even, w_out_even,
              s5_lambda_re, s5_lambda_im, s5_log_dt, s5_b_re, s5_b_im, s5_c_re, s5_c_im, s5_d,
              s5_glu_w, s5_glu_b, sgu_norm, sgu_w, sgu_b, w_in_odd, w_out_odd,
              pool_w, pool_scale, conv_w, conv_b, ffn_w_gate, ffn_w_up, ffn_w_down):
    W = dict(norm_mix=norm_mix, norm_ffn=norm_ffn, norm_final=norm_final,
             w_in_even=w_in_even, w_out_even=w_out_even,
             s5_lambda_re=s5_lambda_re, s5_lambda_im=s5_lambda_im, s5_log_dt=s5_log_dt,
             s5_b_re=s5_b_re, s5_b_im=s5_b_im, s5_c_re=s5_c_re, s5_c_im=s5_c_im, s5_d=s5_d,
             s5_glu_w=s5_glu_w, s5_glu_b=s5_glu_b, sgu_norm=sgu_norm, sgu_w=sgu_w, sgu_b=sgu_b,
             w_in_odd=w_in_odd, w_out_odd=w_out_odd, pool_w=pool_w, pool_scale=pool_scale,
             conv_w=conv_w, conv_b=conv_b, ffn_w_gate=ffn_w_gate, ffn_w_up=ffn_w_up, ffn_w_down=ffn_w_down)
    bp = x_prompt.shape[0]
    z_re = jnp.zeros((N_EVEN, bp, G_A, S5_STATE), state_s5_re.dtype)
    z_pool = jnp.zeros((N_ODD, bp, POOL_BUF, D_C), x_prompt.dtype)
    z_conv = jnp.zeros((N_ODD, bp, CONV_W - 1, D_D), x_prompt.dtype)
    y_prompt, p_s5_re, p_s5_im, _, p_pool, p_conv = run_trunk(x_prompt, z_re, z_re, z_pool, z_conv, 0.0, W)
    y_sample, s_s5_re, s_s5_im, s_sgu_v, s_pool, s_conv = run_trunk(
        x_sample, state_s5_re, state_s5_im, state_pool, state_conv, 1.0, W)
    return (y_prompt, y_sample, p_s5_re, p_s5_im, p_pool, p_conv, s_s5_re, s_s5_im, s_sgu_v, s_pool, s_conv)
```

```python
import math
import numpy as np
from contextlib import ExitStack
import concourse.bass as bass
import concourse.mybir as mybir
from concourse.bass_utils import run_bass_kernel_spmd

F32 = mybir.dt.float32
BF16 = mybir.dt.bfloat16
I32 = mybir.dt.int32
ALU = mybir.AluOpType
AF = mybir.ActivationFunctionType

ENGS = ("pe", "act", "dve", "pool", "sp")
NSLOT = 12
NCORES = 8
D = 1024
SEQ = 2048
NS = 64
T = SEQ + NS
DFF = 2816
EPS = 1e-6
TM = 256
TF = 448
FS = 256
NSL = DFF // FS


class _Op(object):
    __slots__ = ("idx", "eng", "emit", "deps", "signal", "epoch", "semval",
                 "is_dma", "slot", "dval", "prev_dval", "cost", "pos")


class Prog(object):
    def __init__(self, nc, es, n_epochs=6):
        self.nc = nc
        self.ops = []
        self.regions = {}
        self.epoch = 0
        self.n_epochs = n_epochs
        self.sems = {}
        self.cnt = {}
        for e in ENGS:
            for ep in range(n_epochs):
                self.sems[(e, ep)] = es.enter_context(nc.semaphore("s_%s_%d" % (e, ep)))
        self.dsems = {}
        self.dcount = {}
        self.dnext = {}
        for q in ("sp", "pool", "act"):
            self.dnext[q] = 0
            for s in range(NSLOT):
                self.dsems[(q, s)] = es.enter_context(nc.semaphore("d_%s_%d" % (q, s)))
                self.dcount[(q, s)] = 0
        self.nflush = 0
        self.cost = {"pe": 115.0, "act": 450.0, "dve": 430.0, "pool": 600.0, "sp": 100.0}
        self.reorder = True
        self.filler = None
        self.filler_cost = 170.0
        self.nfill = 0

    def next_epoch(self):
        assert not self.ops
        self.epoch = min(self.epoch + 1, self.n_epochs - 1)

    def _add(self, eng, emit, reads, writes, is_dma, cost):
        o = _Op()
        o.idx = len(self.ops)
        o.eng = eng
        o.emit = emit
        o.signal = False
        o.epoch = self.epoch
        o.semval = None
        o.is_dma = is_dma
        o.cost = cost if cost is not None else (3000.0 if is_dma else self.cost[eng])
        deps = set()
        for k in reads:
            r = self.regions.get(k)
            if r is not None and r[0] is not None:
                deps.add(r[0])
        for k in writes:
            r = self.regions.get(k)
            if r is not None:
                if r[0] is not None:
                    deps.add(r[0])
                deps.update(r[1])
        for k in reads:
            r = self.regions.get(k)
            if r is None:
                r = [None, []]
                self.regions[k] = r
            r[1].append(o.idx)
        for k in writes:
            self.regions[k] = [o.idx, []]
        deps.discard(o.idx)
        o.deps = deps
        o.slot = None
        self.ops.append(o)
        return o

    def op(self, eng, emit, reads=(), writes=(), cost=None):
        return self._add(eng, emit, reads, writes, False, cost)

    def dma(self, q, out, in_, reads=(), writes=(), cost=None, **kw):
        def emit(e, out=out, in_=in_, kw=kw):
            return e.dma_start(out=out, in_=in_, **kw)
        return self._add(q, emit, reads, writes, True, cost)

    def _schedule(self):
        ops = self.ops
        n = len(ops)
        succs = [[] for _ in range(n)]
        indeg = [0] * n
        for o in ops:
            for d in o.deps:
                succs[d].append(o.idx)
            indeg[o.idx] = len(o.deps)
        lastd = {}
        dchain = {}
        for o in ops:
            if o.is_dma:
                if o.eng in lastd:
                    dchain[o.idx] = lastd[o.eng]
                lastd[o.eng] = o.idx
        prio = [0.0] * n
        for i in range(n - 1, -1, -1):
            m = 0.0
            for s_ in succs[i]:
                if prio[s_] > m:
                    m = prio[s_]
            prio[i] = ops[i].cost + m
        order = {e: [] for e in ENGS}
        if not self.reorder:
            for o in ops:
                order[o.eng].append(o)
            return order
        ready = {e: [] for e in ENGS}
        ready_t = [0.0] * n
        fin = [0.0] * n
        issued = [False] * n
        free_at = {e: 0.0 for e in ENGS}
        for o in ops:
            if indeg[o.idx] == 0:
                ready[o.eng].append(o.idx)
        remaining = n
        HOP = 150.0
        while remaining:
            best = None
            for e in ENGS:
                rl = ready[e]
                if not rl:
                    continue
                fa = free_at[e]
                cb = None
                for i in rl:
                    o = ops[i]
                    if o.is_dma and i in dchain and not issued[dchain[i]]:
                        continue
                    st = ready_t[i] if ready_t[i] > fa else fa
                    key = (st, -prio[i], i)
                    if cb is None or key < cb:
                        cb = key
                if cb is not None and (best is None or cb < best[0]):
                    best = (cb, e)
            assert best is not None, "scheduler deadlock"
            (st, _, i), e = best
            o = ops[i]
            ready[e].remove(i)
            issued[i] = True
            if e == "pe" and self.filler is not None and free_at[e] > 0.0:
                gap = st - free_at[e]
                if gap > 1200.0:
                    k = min(int((gap - 500.0) / self.filler_cost), 60)
                    for _ in range(k):
                        f = _Op()
                        f.idx = -1
                        f.eng = "pe"
                        f.emit = self.filler
                        f.is_dma = False
                        f.signal = False
                        order[e].append(f)
                    self.nfill += k
            if o.is_dma:
                free_at[e] = st + 80.0
                fin[i] = st + o.cost
            else:
                free_at[e] = st + o.cost
                fin[i] = st + o.cost
            order[e].append(o)
            remaining -= 1
            for s_ in succs[i]:
                t = fin[i] + (0.0 if (ops[s_].eng == e and e == "pe") else HOP)
                if t > ready_t[s_]:
                    ready_t[s_] = t
                indeg[s_] -= 1
                if indeg[s_] == 0:
                    ready[ops[s_].eng].append(s_)
        self.est_time = max(fin) if n else 0.0
        return order

    def flush(self):
        nc = self.nc
        ops = self.ops
        if not ops:
            return
        per_eng = self._schedule()
        for e in ENGS:
            for p_, o in enumerate(per_eng[e]):
                o.pos = p_
            if e != "pe":
                assert all(o.idx >= 0 for o in per_eng[e])
        for e in ("sp", "pool", "act"):
            for o in per_eng[e]:
                if o.is_dma:
                    s = self.dnext[e]
                    self.dnext[e] = (s + 1) % NSLOT
                    o.slot = s
                    o.prev_dval = self.dcount[(e, s)]
                    self.dcount[(e, s)] += 16
                    o.dval = self.dcount[(e, s)]
        red = []
        for o in ops:
            comp = {}
            dmas = []
            for d in o.deps:
                p = ops[d]
                if p.is_dma:
                    dmas.append(d)
                else:
                    if p.eng == "pe" and o.eng == "pe" and not o.is_dma:
                        continue
                    if p.eng not in comp or ops[comp[p.eng]].pos < p.pos:
                        comp[p.eng] = d
            red.append((comp, dmas))
            for d in comp.values():
                ops[d].signal = True
        cnt = self.cnt
        for e in ENGS:
            for o in per_eng[e]:
                if o.idx < 0 or o.is_dma or not o.signal:
                    continue
                key = (o.eng, o.epoch)
                cnt[key] = cnt.get(key, 0) + 1
                o.semval = cnt[key]
        sems = self.sems
        dsems = self.dsems
        n_ep = self.n_epochs
        dcount = self.dcount

        def emit_engine(e, eng_name):
            waited = {}
            dwaited = {}
            for o in per_eng[eng_name]:
                if o.idx < 0:
                    o.emit(e)
                    continue
                comp, dmas = red[o.idx]
                for pe_name, d in comp.items():
                    p = ops[d]
                    done = False
                    for ep in range(p.epoch, n_ep):
                        w = waited.get((pe_name, ep), 0)
                        if ep == p.epoch and w >= p.semval:
                            done = True
                        if ep > p.epoch and w > 0:
                            done = True
                    if done:
                        continue
                    e.wait_ge(sems[(pe_name, p.epoch)], p.semval)
                    waited[(pe_name, p.epoch)] = p.semval
                for d in dmas:
                    p = ops[d]
                    k = (p.eng, p.slot)
                    if dwaited.get(k, 0) >= p.dval:
                        continue
                    e.wait_ge(dsems[k], p.dval)
                    dwaited[k] = p.dval
                if o.is_dma:
                    k = (o.eng, o.slot)
                    if o.prev_dval > 0 and dwaited.get(k, 0) < o.prev_dval:
                        e.wait_ge(dsems[k], o.prev_dval)
                        dwaited[k] = o.prev_dval
                    inst = o.emit(e)
                    inst.then_inc(dsems[k], 16)
                else:
                    inst = o.emit(e)
                    if o.signal:
                        inst.then_inc(sems[(o.eng, o.epoch)], 1)
            if eng_name in ("sp", "pool", "act"):
                for s in range(NSLOT):
                    k = (eng_name, s)
                    if dcount[k] > 0 and dwaited.get(k, 0) < dcount[k]:
                        e.wait_ge(dsems[k], dcount[k])

        with nc.Block() as block:
            @block.tensor
            def _(e):
                emit_engine(e, "pe")

            @block.scalar
            def _(e):
                emit_engine(e, "act")

            @block.vector
            def _(e):
                emit_engine(e, "dve")

            @block.gpsimd
            def _(e):
                emit_engine(e, "pool")

            @block.sync
            def _(e):
                emit_engine(e, "sp")
        self.ops = []
        self.regions = {}
        self.nflush += 1


def build_nc(debug=False):
    nc = bass.Bass("TRN2", target_bir_lowering=False)
    try:
        nc.allow_low_precision("bf16 matmul operands with fp32 accumulation by design")
    except Exception:
        pass

    def din(name, shape):
        return nc.dram_tensor(name, list(shape), F32, kind="ExternalInput").ap()

    def dout(name, shape):
        return nc.dram_tensor(name, list(shape), F32, kind="ExternalOutput").ap()

    x_p = din("x_p", (SEQ, D))
    x_s = din("x_s", (NS, D))
    st_re = din("st_re", (2, 16, 2048))
    st_im = din("st_im", (2, 16, 2048))
    st_pool = din("st_pool", (2, 16, 15, 512))
    st_conv = din("st_conv", (2, 16, 2, 512))
    norm_mix = din("norm_mix", (4, D))
    norm_ffn = din("norm_ffn", (4, D))
    norm_final = din("norm_final", (1, D))
    w_in_even = din("w_in_even", (2, D, 1536))
    w_out_even = din("w_out_even", (2, D, D))
    lam_re = din("s5_lambda_re", (2, 32, 64))
    lam_im = din("s5_lambda_im", (2, 32, 64))
    log_dt = din("s5_log_dt", (2, 32))
    b_re = din("s5_b_re", (2, 32, 64, 16))
    b_im = din("s5_b_im", (2, 32, 64, 16))
    c_re = din("s5_c_re", (2, 32, 16, 64))
    c_im = din("s5_c_im", (2, 32, 16, 64))
    s5_d = din("s5_d", (2, 512))
    glu_w = din("s5_glu_w", (2, 512, 512))
    glu_b = din("s5_glu_b", (2, 512))
    sgu_norm = din("sgu_norm", (2, 512))
    sgu_w = din("sgu_w", (2, 8, 128, 128))
    sgu_b = din("sgu_b", (2, 8, 128))
    w_in_odd = din("w_in_odd", (2, D, 2048))
    w_out_odd = din("w_out_odd", (2, D, D))
    pool_w = din("pool_w", (2, 4, 128, 128))
    pool_scale = din("pool_scale", (2, 512))
    conv_w = din("conv_w", (2, 3, 512))
    conv_b = din("conv_b", (2, 512))
    ffn_g = din("ffn_w_gate", (4, D, DFF))
    ffn_u = din("ffn_w_up", (4, D, DFF))
    ffn_d = din("ffn_w_down", (4, DFF, D))

    y_p = dout("y_p", (SEQ, D))
    y_s = dout("y_s", (NS, D))
    o_p_re = dout("o_p_re", (2, 16, 128))
    o_p_im = dout("o_p_im", (2, 16, 128))
    o_p_pool = dout("o_p_pool", (2, 15, 512))
    o_p_conv = dout("o_p_conv", (2, 2, 512))
    o_s_re = dout("o_s_re", (2, 16, 2048))
    o_s_im = dout("o_s_im", (2, 16, 2048))
    o_s_v = dout("o_s_v", (2, NS, 512))
    o_s_pool = dout("o_s_pool", (2, 240, 512))
    o_s_conv = dout("o_s_conv", (2, 32, 512))
    dbg = dout("dbg", (128, 4096)) if debug else None

    es = ExitStack()
    with es:
        es.enter_context(nc.allow_non_contiguous_dma(reason="small strided parameter loads"))
        P = Prog(nc, es)

        _uid = [0]

        def sb(stk, name, shape, dt=F32):
            _uid[0] += 1
            return stk.enter_context(nc.sbuf_tensor("%s_u%d" % (name, _uid[0]), list(shape), dt))

        PS = es.enter_context(nc.psum_tensor("PS", [128, 8, 512], F32))
        ps_rr = [0]

        def psget(n=1):
            b = ps_rr[0]
            if b + n > 8:
                b = 0
            ps_rr[0] = (b + n) % 8
            return b

        def pk(b, n=1):
            return [("ps", b + i) for i in range(n)]

        hres = sb(es, "hres", [128, 8, T])
        ident = sb(es, "ident", [128, 128])
        identb = sb(es, "identb", [128, 128], BF16)
        onesb = sb(es, "onesb", [128, 128], BF16)
        iot = sb(es, "iot", [128, 128])
        iop = sb(es, "iop", [128, 1])
        pstage = sb(es, "pstage", [128, 128])
        pvec = sb(es, "pvec", [128, 120])
        gmix = pvec[:, 0:32].rearrange("p (l k) -> p l k", l=4)
        gffn = pvec[:, 32:64].rearrange("p (l k) -> p l k", l=4)
        glub = pvec[:, 64:72].rearrange("p (l k) -> p l k", l=2)
        pscale = pvec[:, 72:80].rearrange("p (l k) -> p l k", l=2)
        cw = pvec[:, 80:104].rearrange("p (l c k) -> p l c k", l=2, c=3)
        cb = pvec[:, 104:112].rearrange("p (l k) -> p l k", l=2)
        dcol = pvec[:, 112:120].rearrange("p (l k) -> p l k", l=2)
        epsc = sb(es, "epsc", [128, 1])
        s5car = sb(es, "s5car", [128, 16, 2])
        xchalo = sb(es, "xchalo", [128, 4, 15])
        zhalo = sb(es, "zhalo", [128, 4, 2])

        def V(e):
            return e

        def load_w(dst, src3, key, nsplit):
            K = dst.shape[1]
            step = K // nsplit
            for i in range(nsplit):
                nb = 128 * step * dst.shape[2] * 4
                P.dma("pool", dst[:, i * step:(i + 1) * step, :],
                      src3.rearrange("(k p) n -> p k n", p=128)[:, i * step:(i + 1) * step, :], writes=[(key, i)],
                      cost=2500.0 + nb / 150.0)

        WinP = sb(es, "WinP", [128, 8, 2048], BF16)
        WoutP = sb(es, "WoutP", [128, 8, D], BF16)
        WsmP = sb(es, "WsmP", [128, 2048], BF16)

        def load_mixer_weights(layer):
            i = layer // 2
            if layer % 2 == 0:
                load_w(WinP[:, :, 0:1536], w_in_even[i], "Win", 4)
                load_w(WoutP, w_out_even[i], "Wout", 2)
                load_w(WsmP[:, :].rearrange("p (k n) -> p k n", k=4), glu_w[i], "Wsm", 1)
            else:
                load_w(WinP, w_in_odd[i], "Win", 4)
                load_w(WoutP, w_out_odd[i], "Wout", 2)
                P.dma("pool", WsmP[:, 0:512].rearrange("p (g d) -> p g d", g=4), pool_w[i].rearrange("g c d -> c g d"),
                      writes=["Wsm"])

        load_mixer_weights(0)
        P.op("pool", lambda e: e.iota(iot[:], pattern=[[1, 128]], base=0, channel_multiplier=0,
                                      allow_small_or_imprecise_dtypes=True), writes=["iot"])
        P.op("pool", lambda e: e.iota(iop[:], pattern=[[1, 1]], base=0, channel_multiplier=1,
                                      allow_small_or_imprecise_dtypes=True), writes=["iop"])
        P.op("dve", lambda e: e.tensor_scalar(out=ident[:], in0=iot[:], scalar1=iop[:, 0:1], scalar2=None,
                                              op0=ALU.is_equal), reads=["iot", "iop"], writes=["ident"])
        P.op("dve", lambda e: e.tensor_copy(out=identb[:], in_=ident[:]), reads=["ident"], writes=["identb"])
        P.op("dve", lambda e: e.memset(onesb[:], 1.0), writes=["onesb"])
        P.op("dve", lambda e: e.memset(epsc[:], EPS), writes=["epsc"])
        P.filler = None; _unused_filler = lambda e: e.matmul(PS[:, 7, 0:128], lhsT=onesb[:], rhs=onesb[:], start=True, stop=True)
        with ExitStack() as st:
            pass
        pst_rows = [(norm_mix.rearrange("l (k p) -> (l k) p", p=128), 32), (norm_ffn.rearrange("l (k p) -> (l k) p", p=128), 32),
                    (glu_b.rearrange("l (k p) -> (l k) p", p=128), 8), (pool_scale.rearrange("l (k p) -> (l k) p", p=128), 8),
                    (conv_w.rearrange("l c (k p) -> (l c k) p", p=128), 24), (conv_b.rearrange("l (k p) -> (l k) p", p=128), 8),
                    (s5_d.rearrange("l (k p) -> (l k) p", p=128), 8)]
        r0 = 0
        for j_, (src_, nr_) in enumerate(pst_rows):
            P.dma("sp", pstage[r0:r0 + nr_, :], src_, writes=[("pstage", j_)])
            r0 += nr_
        P.op("pe", lambda e: e.transpose(PS[:, 5, 0:120], pstage[0:120, :], ident[0:120, 0:120]),
             reads=[("pstage", j_) for j_ in range(7)] + ["ident"], writes=[("ps", 5)])
        P.op("act", lambda e: e.activation(out=pvec[:, :], in_=PS[:, 5, 0:120], func=AF.Copy), reads=[("ps", 5)],
             writes=["gmix", "gffn", "glub", "pscale", "cw", "cb", "dcol"])

        def load_x(st):
            xt = [sb(st, "xt%d" % i, [128, D]) for i in range(2)]
            nsub = SEQ // 128 + 1
            for si in range(nsub):
                n = 128 if si < SEQ // 128 else NS
                src = x_p[si * 128:(si + 1) * 128, :] if si < SEQ // 128 else x_s[:, :]
                xb = xt[si % 2]
                xk = "xt%d" % (si % 2)
                P.dma("sp", xb[0:n, :], src, writes=[xk])
                for half in range(2):
                    b = psget()
                    for q in range(4):
                        k = half * 4 + q
                        P.op("pe", lambda e, b=b, q=q, k=k, xb=xb, n=n: e.transpose(
                            PS[:, b, q * 128:q * 128 + n], xb[0:n, k * 128:(k + 1) * 128], ident[0:n, 0:n]),
                            reads=[xk, "ident"], writes=pk(b))
                    eng = "act" if half == 0 else "dve"
                    if eng == "act":
                        P.op("act", lambda e, b=b, half=half, si=si, n=n: e.activation(
                            out=hres[:, half * 4:half * 4 + 4, si * 128:si * 128 + n],
                            in_=PS[:, b, :].rearrange("p (q t) -> p q t", q=4)[:, :, 0:n], func=AF.Copy),
                            reads=pk(b), writes=[("h", si, half)])
                    else:
                        P.op("dve", lambda e, b=b, half=half, si=si, n=n: e.tensor_copy(
                            out=hres[:, half * 4:half * 4 + 4, si * 128:si * 128 + n],
                            in_=PS[:, b, :].rearrange("p (q t) -> p q t", q=4)[:, :, 0:n]),
                            reads=pk(b), writes=[("h", si, half)])

        mtiles = [(i * TM, TM, False) for i in range(SEQ // TM)] + [(SEQ, NS, True)]
        ftiles = [(0, 448, False), (448, 448, False), (896, 448, False), (1344, 448, False), (1792, 320, False)]

        def hkeys(t0, n):
            ks = []
            a = (t0 // TM) * TM
            while a < t0 + n:
                ks.append(("hres", a))
                a += TM
            return ks

        def s5_setup(stk, i, XB, YC, BD, TC, TS, R4, A4, bmask):
            with ExitStack() as st:
                def t16(name):
                    return sb(st, "s5_" + name, [128, 16])
                LRI = sb(st, "s5_LRI", [128, 32])
                LR = LRI[:, 0:16]
                LI = LRI[:, 16:32]
                LDT, DT, Z, MAG, ANG = [t16(n_) for n_ in ("LDT", "DT", "Z", "MAG", "ANG")]
                SN, CS, ta, tb, tc_, td = [t16(n_) for n_ in ("SN", "CS", "ta", "tb", "tc", "td")]
                FR, FI = t16("FR"), t16("FI")
                AR = [t16("AR%d" % k) for k in range(5)]
                AI = [t16("AI%d" % k) for k in range(5)]
                BR = sb(st, "s5_BR", [128, 16, 32]); BI = sb(st, "s5_BI", [128, 16, 32])
                BBr = sb(st, "s5_BBr", [128, 16, 32]); BBi = sb(st, "s5_BBi", [128, 16, 32])
                CTr = sb(st, "s5_CTr", [128, 16, 32]); CTi = sb(st, "s5_CTi", [128, 16, 32])
                Yr = sb(st, "s5_Yr", [128, 16, 32]); Yi = sb(st, "s5_Yi", [128, 16, 32])
                W1 = sb(st, "s5_W1", [128, 16, 32]); W2 = sb(st, "s5_W2", [128, 16, 32])
                CNr = sb(st, "s5_CNr", [128, 4, 128]); CNi = sb(st, "s5_CNi", [128, 4, 128])
                cnt = [0]

                def dv(fn, reads, writes, eng="dve"):
                    P.op(eng, fn, reads=reads, writes=writes)

                def tt(out, a, b, op, r, w):
                    dv(lambda e: e.tensor_tensor(out=out, in0=a, in1=b, op=op), r, w)

                def ts(out, a, s1, op0, r, w, s2=None, op1=None):
                    if op1 is None:
                        dv(lambda e: e.tensor_scalar(out=out, in0=a, scalar1=s1, scalar2=None, op0=op0), r, w)
                    else:
                        dv(lambda e: e.tensor_scalar(out=out, in0=a, scalar1=s1, scalar2=s2, op0=op0, op1=op1), r, w)

                lst = sb(st, "s5_lst", [32, 128])
                P.dma("sp", lst[0:16, :], lam_re[i].rearrange("(P g) n -> P (g n)", g=2), writes=[("lst", 0)])
                P.dma("sp", lst[16:32, :], lam_im[i].rearrange("(P g) n -> P (g n)", g=2), writes=[("lst", 1)])
                bl_ = psget()
                P.op("pe", lambda e: e.transpose(PS[:, bl_, 0:32], lst[:, :], ident[0:32, 0:32]),
                     reads=[("lst", 0), ("lst", 1), "ident"], writes=pk(bl_))
                P.op("act", lambda e: e.activation(out=LRI[:, :], in_=PS[:, bl_, 0:32], func=AF.Copy), reads=pk(bl_),
                     writes=["LR", "LI"])
                for g2 in range(2):
                    P.dma("sp", LDT[64 * g2:64 * g2 + 64, :],
                          log_dt[i:i + 1, :].rearrange("o (P g) -> o g P", g=2)[:, g2, :].broadcast_to([64, 16]),
                          writes=[("LDT", g2)])
                for tl in (BR, BI, CNr, CNi):
                    dv(lambda e, tl=tl: e.memset(tl[:], 0.0), [], ["z_" + tl.name], eng="pool")
                for (tl, src) in ((BR, b_re), (BI, b_im)):
                    for g2 in range(2):
                        P.dma("sp", tl[64 * g2:64 * g2 + 64, :, 16 * g2:16 * g2 + 16],
                              src[i].rearrange("(P g) n q -> g n P q", g=2)[g2], reads=["z_" + tl.name],
                              writes=[("ld_" + tl.name, g2)])
                for (tl, src) in ((CNr, c_re), (CNi, c_im)):
                    for p4 in range(4):
                        for g2 in range(2):
                            P.dma("sp", tl[32 * p4 + 16 * g2:32 * p4 + 16 * g2 + 16, :, 64 * g2:64 * g2 + 64],
                                  src[i].rearrange("(f a g) p n -> a g p f n", a=4, g=2)[p4, g2],
                                  reads=["z_" + tl.name], writes=[("ld_" + tl.name, p4, g2)])
                for (src, dst) in ((CNr, CTr), (CNi, CTi)):
                    b = psget()
                    for ft in range(4):
                        P.op("pe", lambda e, ft=ft, b=b, src=src: e.transpose(
                            PS[:, b, ft * 128:(ft + 1) * 128], src[:, ft, :], ident[:]),
                            reads=[("ld_" + src.name, a_, b_) for a_ in range(4) for b_ in range(2)] + ["ident"], writes=pk(b))
                    P.op("act", lambda e, b=b, dst=dst: e.activation(
                        out=dst[:].rearrange("p a b -> p (a b)"), in_=PS[:, b, :], func=AF.Copy),
                        reads=pk(b), writes=[dst.name])
                dv(lambda e: e.activation(out=DT[:], in_=LDT[:], func=AF.Exp), [("LDT", 0), ("LDT", 1)], ["DT"], eng="act")
                tt(Z[:], LR[:], DT[:], ALU.mult, ["LR", "DT"], ["Z"])
                ts(MAG[:], Z[:], 1.0 / 120.0, ALU.mult, ["Z"], ["MAG"], 1.0 / 24.0, ALU.add)
                for c in (1.0 / 6.0, 0.5, 1.0, 1.0):
                    tt(MAG[:], MAG[:], Z[:], ALU.mult, ["MAG", "Z"], ["MAG"])
                    ts(MAG[:], MAG[:], float(c), ALU.add, ["MAG"], ["MAG"])
                tt(ANG[:], LI[:], DT[:], ALU.mult, ["LI", "DT"], ["ANG"])
                C1 = 6.28125
                C2 = 2.0 * math.pi - C1
                MAGIC = 12582912.0
                for (shift, dst) in ((0.0, SN), (0.5 * math.pi, CS)):
                    ts(ta[:], ANG[:], 1.0 / (2 * math.pi), ALU.mult, ["ANG"], ["ta"], shift / (2 * math.pi), ALU.add)
                    ts(tb[:], ta[:], MAGIC, ALU.add, ["ta"], ["tb"])
                    ts(tb[:], tb[:], -MAGIC, ALU.add, ["tb"], ["tb"])
                    dv(lambda e: e.scalar_tensor_tensor(out=ta[:], in0=tb[:], scalar=-C1, in1=ANG[:],
                                                        op0=ALU.mult, op1=ALU.add), ["tb", "ANG"], ["ta"])
                    dv(lambda e: e.scalar_tensor_tensor(out=ta[:], in0=tb[:], scalar=-C2, in1=ta[:],
                                                        op0=ALU.mult, op1=ALU.add), ["tb", "ta"], ["ta"])
                    ts(ta[:], ta[:], float(shift), ALU.add, ["ta"], ["ta"], math.pi, ALU.min)
                    ts(ta[:], ta[:], -math.pi, ALU.max, ["ta"], ["ta"])
                    dv(lambda e, dst=dst: e.activation(out=dst[:], in_=ta[:], func=AF.Sin), ["ta"], [dst.name],
                       eng="act")
                tt(AR[1][:], MAG[:], CS[:], ALU.mult, ["MAG", CS.name], ["AR1"])
                tt(AI[1][:], MAG[:], SN[:], ALU.mult, ["MAG", SN.name], ["AI1"])
                dv(lambda e: e.memset(AR[0][:], 1.0), [], ["AR0"])
                dv(lambda e: e.memset(AI[0][:], 0.0), [], ["AI0"])

                def cmul(orr, oi, ar, ai, br, bi, rk, wk, t1=None, t2=None, k1="W1", k2="W2"):
                    tt(t1, ar, br, ALU.mult, rk, [k1])
                    tt(t2, ai, bi, ALU.mult, rk, [k2])
                    tt(orr, t1, t2, ALU.subtract, [k1, k2], [wk + "r"])
                    tt(t1, ar, bi, ALU.mult, rk + [wk + "r"], [k1])
                    tt(t2, ai, br, ALU.mult, rk + [wk + "r"], [k2])
                    tt(oi, t1, t2, ALU.add, [k1, k2], [wk + "i"])

                cmul(AR[2][:], AI[2][:], AR[1][:], AI[1][:], AR[1][:], AI[1][:], ["AR1", "AI1"], "A2", tc_[:], td[:], "tc", "td")
                cmul(AR[3][:], AI[3][:], AR[2][:], AI[2][:], AR[1][:], AI[1][:], ["AR1", "AI1", "A2r", "A2i"], "A3",
                     tc_[:], td[:], "tc", "td")
                cmul(AR[4][:], AI[4][:], AR[2][:], AI[2][:], AR[2][:], AI[2][:], ["A2r", "A2i"], "A4", tc_[:], td[:], "tc", "td")
                akeys = {0: ["AR0", "AI0"], 1: ["AR1", "AI1"], 2: ["A2r", "A2i"], 3: ["A3r", "A3i"], 4: ["A4r", "A4i"]}
                dv(lambda e: e.tensor_copy(out=A4[:, :, 0], in_=AR[4][:]), akeys[4], ["A4"])
                dv(lambda e: e.tensor_copy(out=A4[:, :, 1], in_=AI[4][:]), akeys[4] + ["A4"], ["A4"])
                tt(ta[:], MAG[:], MAG[:], ALU.mult, ["MAG"], ["ta"])
                tt(R4[:], ta[:], ta[:], ALU.mult, ["ta"], ["R4"])
                ts(ta[:], AR[1][:], -1.0, ALU.add, ["AR1"], ["ta"])
                tt(tb[:], LR[:], LR[:], ALU.mult, ["LR"], ["tb"])
                tt(tc_[:], LI[:], LI[:], ALU.mult, ["LI", "A4i"], ["tc"])
                tt(tb[:], tb[:], tc_[:], ALU.add, ["tb", "tc"], ["tb"])
                dv(lambda e: e.reciprocal(out=tb[:], in_=tb[:]), ["tb"], ["tb"])
                tt(tc_[:], ta[:], LR[:], ALU.mult, ["ta", "LR"], ["tc"])
                tt(td[:], AI[1][:], LI[:], ALU.mult, ["AI1", "LI", "A4i"], ["td"])
                tt(tc_[:], tc_[:], td[:], ALU.add, ["tc", "td"], ["tc"])
                tt(FR[:], tc_[:], tb[:], ALU.mult, ["tc", "tb"], ["FR"])
                tt(tc_[:], AI[1][:], LR[:], ALU.mult, ["AI1", "LR", "FR"], ["tc"])
                tt(td[:], ta[:], LI[:], ALU.mult, ["ta", "LI", "FR"], ["td"])
                tt(tc_[:], tc_[:], td[:], ALU.subtract, ["tc", "td"], ["tc"])
                tt(FI[:], tc_[:], tb[:], ALU.mult, ["tc", "tb"], ["FI"])
                dv(lambda e: e.reciprocal(out=ta[:], in_=R4[:]), ["R4", "FI"], ["ta"])
                tt(TC[:, :, 0], AR[4][:], ta[:], ALU.mult, akeys[4] + ["ta"], ["TC"])
                tt(TS[:, :, 0], AI[4][:], ta[:], ALU.mult, akeys[4] + ["ta"], ["TS"])
                m = 1
                NCH = TM // 4
                while m < NCH:
                    ur = TC[:, :, m - 1:m].broadcast_to([128, 16, m])
                    ui = TS[:, :, m - 1:m].broadcast_to([128, 16, m])
                    w1 = W1[:, :, 0:m]; w2 = W2[:, :, 0:m]
                    tt(w1, TC[:, :, 0:m], ur, ALU.mult, ["TC", "TS"], ["W1"])
                    tt(w2, TS[:, :, 0:m], ui, ALU.mult, ["TC", "TS"], ["W2"])
                    tt(TC[:, :, m:2 * m], w1, w2, ALU.subtract, ["W1", "W2"], ["TC"])
                    tt(w1, TC[:, :, 0:m], ui, ALU.mult, ["TC", "TS"], ["W1"])
                    tt(w2, TS[:, :, 0:m], ur, ALU.mult, ["TC", "TS"], ["W2"])
                    tt(TS[:, :, m:2 * m], w1, w2, ALU.add, ["W1", "W2"], ["TS"])
                    m *= 2

                def bc(a):
                    return a.unsqueeze(2).broadcast_to([128, 16, 32])

                cmul(BBr[:], BBi[:], BR[:], BI[:], bc(FR[:]), bc(FI[:]), [("ld_" + BR.name, 0), ("ld_" + BR.name, 1), ("ld_" + BI.name, 0), ("ld_" + BI.name, 1), "FR", "FI"],
                     "BB", W1[:], W2[:])
                for k in range(4):
                    if k == 0:
                        srcs = (BBr, BBi)
                        skeys = ["BBr", "BBi"]
                    else:
                        cmul(Yr[:], Yi[:], BBr[:], BBi[:], bc(AR[k][:]), bc(AI[k][:]), ["BBr", "BBi"] + akeys[k], "Y",
                             W1[:], W2[:])
                        srcs = (Yr, Yi)
                        skeys = ["Yr", "Yi"]
                    s = 3 - k
                    for ri in range(2):
                        b = psget()
                        for ft in range(4):
                            P.op("pe", lambda e, ft=ft, b=b, src=srcs[ri]: e.transpose(
                                PS[:, b, ft * 128:(ft + 1) * 128],
                                src[:, 4 * ft:4 * ft + 4, :].rearrange("p a b -> p (a b)"), ident[:]),
                                reads=skeys + ["ident"], writes=pk(b))
                        P.op("act", lambda e, b=b, ri=ri, s=s: e.activation(
                            out=XB[:, :, ri, s, :], in_=PS[:, b, :].rearrange("p (f n) -> p f n", f=4), func=AF.Copy),
                            reads=pk(b), writes=["XB"])
                bBD = psget()
                for k in range(5):
                    if k == 0:
                        dv(lambda e: e.tensor_copy(out=Yr[:], in_=CTr[:]), ["s5_CTr", "XB"], ["Yr"])
                        ts(Yi[:], CTi[:], -1.0, ALU.mult, ["s5_CTi", "XB"], ["Yi"])
                    else:
                        cmul(Yr[:], Yi[:], CTr[:], CTi[:], bc(AR[k][:]), bc(AI[k][:]),
                             ["s5_CTr", "s5_CTi", "BD%d" % (k - 1), "YC"] + akeys[k], "Y", W1[:], W2[:])
                        ts(Yi[:], Yi[:], -1.0, ALU.mult, ["Yi"], ["Yi"])
                        P.op("act", lambda e, k=k: e.activation(out=YC[:, :, 0, k - 1, :], in_=Yr[:], func=AF.Copy),
                             reads=["Yr"], writes=["YC"])
                        P.op("act", lambda e, k=k: e.activation(out=YC[:, :, 1, k - 1, :], in_=Yi[:], func=AF.Copy),
                             reads=["Yi"], writes=["YC"])
                    if k < 4:
                        for ft in range(4):
                            o_ = PS[:, bBD, ft * 128:(ft + 1) * 128]
                            P.op("pe", lambda e, ft=ft, o_=o_: e.matmul(
                                o_, lhsT=BBr[:, 4 * ft:4 * ft + 4, :].rearrange("p a b -> p (a b)"),
                                rhs=Yr[:, 4 * ft:4 * ft + 4, :].rearrange("p a b -> p (a b)"), start=True, stop=False),
                                reads=["BBr", "Yr"], writes=pk(bBD))
                            P.op("pe", lambda e, ft=ft, o_=o_: e.matmul(
                                o_, lhsT=BBi[:, 4 * ft:4 * ft + 4, :].rearrange("p a b -> p (a b)"),
                                rhs=Yi[:, 4 * ft:4 * ft + 4, :].rearrange("p a b -> p (a b)"), start=False, stop=True),
                                reads=["BBi", "Yi"], writes=pk(bBD))
                        for ft in range(4):
                            if k == 0:
                                dv(lambda e, ft=ft: e.tensor_tensor(out=W1[:, 0:4, :].rearrange("p a b -> p (a b)"),
                                                                    in0=PS[:, bBD, ft * 128:(ft + 1) * 128],
                                                                    in1=bmask[:], op=ALU.mult),
                                   pk(bBD) + ["bmask"], ["W1"])
                                dv(lambda e, ft=ft: e.scalar_tensor_tensor(
                                    out=BD[:, ft, 0, :], in0=ident[:], scalar=dcol[:, i, ft:ft + 1],
                                    in1=W1[:, 0:4, :].rearrange("p a b -> p (a b)"), op0=ALU.mult, op1=ALU.add),
                                    ["W1", "ident", "dcol"], ["BD0"])
                            else:
                                dv(lambda e, ft=ft, k=k: e.tensor_tensor(out=BD[:, ft, k, :],
                                                                         in0=PS[:, bBD, ft * 128:(ft + 1) * 128],
                                                                         in1=bmask[:], op=ALU.mult),
                                   pk(bBD) + ["bmask"], ["BD%d" % k])

        def even_mixer(layer):
            i = layer // 2
            P.cost.update({"pe": 115.0, "dve": 430.0, "act": 450.0})
            with ExitStack() as st:
                NCH = TM // 4
                Win = WinP
                Wout = WoutP
                Wglu = WsmP[:, :].rearrange("p (k n) -> p k n", k=4)
                XB = sb(st, "XB", [128, 4, 2, 4, 128], BF16)
                YC = sb(st, "YC", [128, 16, 2, 4, 32], BF16)
                BD = sb(st, "BD", [128, 4, 4, 128], BF16)
                TC = sb(st, "TC", [128, 16, NCH]); TS = sb(st, "TS", [128, 16, NCH])
                R4 = sb(st, "R4", [128, 16]); A4 = sb(st, "A4", [128, 16, 2])
                wT = sb(st, "wT", [128, 8, 128], BF16)
                bbc = sb(st, "bbc", [128, 4, 128])
                gsg = sb(st, "gsg", [128, 512])
                wsc = sb(st, "wsc", [128, 4, 16])
                P.dma("sp", gsg[:], sgu_norm[i:i + 1, :].broadcast_to([128, 512]), writes=["gsg"])
                for h in range(8):
                    P.dma("sp", bbc[64 * (h % 2):64 * (h % 2) + 64, h // 2, :],
                          sgu_b[i, h:h + 1, :].broadcast_to([64, 128]), writes=[("bbc", h)])
                for h in range(8):
                    P.dma("sp", wsc[64 * (h % 2):64 * (h % 2) + 64, h // 2, :].rearrange("p (a b) -> p a b", a=4),
                          sgu_w[i, h:h + 1, 0:4, 0:4].broadcast_to([64, 4, 4]), writes=[("wsc", h)])
                P.op("dve", lambda e: e.memset(s5car[:], 0.0), writes=[("s5car", q_) for q_ in range(4)])
                with ExitStack() as st2:
                    trilT = sb(st2, "trilT", [128, 128])
                    bmask = sb(st2, "bmask", [128, 128])
                    j32 = sb(st2, "bm_j32", [128, 128])
                    S4 = sb(st2, "bm_S4", [128, 128])
                    P.op("dve", lambda e: e.tensor_scalar(out=trilT[:], in0=iot[:], scalar1=iop[:, 0:1], scalar2=None,
                                                          op0=ALU.is_ge), reads=["iot", "iop"], writes=["trilT"])
                    P.op("pool", lambda e: e.iota(j32[:], pattern=[[1, 4], [0, 32]], base=0, channel_multiplier=0,
                                                  allow_small_or_imprecise_dtypes=True), writes=["j32"])
                    P.op("dve", lambda e: e.tensor_scalar(out=S4[:], in0=j32[:], scalar1=iop[:, 0:1], scalar2=None,
                                                          op0=ALU.is_equal), reads=["j32", "iop"], writes=["S4"])
                    P.op("pe", lambda e: e.matmul(PS[:, 6, 0:128], lhsT=S4[0:4, :], rhs=S4[0:4, :], start=True, stop=True),
                         reads=["S4"], writes=[("ps", 6)])
                    P.op("dve", lambda e: e.tensor_copy(out=bmask[:], in_=PS[:, 6, 0:128]), reads=[("ps", 6)], writes=["bmask"])
                    wld = [sb(st2, "wld%d" % q, [128, 128]) for q in range(2)]
                    for h in range(8):
                        wl = wld[h % 2]
                        P.dma("sp", wl[:], sgu_w[i, h], writes=["wld%d" % (h % 2)])
                        b = psget()
                        P.op("pe", lambda e, b=b, wl=wl: e.transpose(PS[:, b, 0:128], wl[:], ident[:]),
                             reads=["wld%d" % (h % 2), "ident"], writes=pk(b))
                        P.op("dve", lambda e, b=b, h=h: e.tensor_tensor(out=wT[:, h, :], in0=PS[:, b, 0:128],
                                                                        in1=trilT[:], op=ALU.mult),
                             reads=pk(b) + ["trilT"], writes=["wT"])
                    if layer == 0:
                        load_x(st2)
                    s5_setup(st2, i, XB, YC, BD, TC, TS, R4, A4, bmask)
                    P.flush()
                xnt = sb(st, "xnt", [128, 8, TM], BF16)
                uaL = [sb(st, "ua%d" % q, [128, 4, TM], BF16) for q in range(2)]
                ubL = [sb(st, "ub%d" % q, [128, 4, TM], BF16) for q in range(2)]
                vn = sb(st, "vn", [128, 512])
                vnbL = [sb(st, "vnb%d" % q, [128, 2, 512], BF16) for q in range(2)]
                vjunk = sb(st, "vjunk", [128, 512], BF16)
                vss = sb(st, "vss", [128, 2])
                ymix = sb(st, "ymix", [128, 8, TM], BF16)
                tA = sb(st, "tA", [128, 4, NCH]); tB = sb(st, "tB", [128, 4, NCH])
                Gin = sb(st, "Gin", [128, 4, 2, NCH])
                wtail = [WinP[:, k_, 1536:2048].bitcast(F32).rearrange("p (a c) -> p a c", a=4) for k_ in range(8)]
                tC, tD, tE, tF, tG, tH = wtail[0:6]
                GsL = [sb(st, "Gs0", [128, 4, 2, NCH]),
                       WinP[:, 6:8, 1536:2048].bitcast(F32).rearrange("p k (a c) -> p a k c", a=4)]
                Hf = sb(st, "Hf", [128, 4, 2, NCH + 1])
                Hb = sb(st, "Hb", [128, 4, 2, NCH], BF16)
                sqy = sb(st, "sqy", [128, TM])
                zf = sb(st, "zf", [128, 4, TM])
                zb = sb(st, "zb", [128, 4, TM], BF16)
                sg2 = sb(st, "sg2", [128, TM])
                stmp = sb(st, "stmp", [128, TM])
                vT = sb(st, "vT", [128, 4, NS])
                sacc = sb(st, "sacc", [128, 16, 4])
                h0s = sb(st, "h0s", [16, 1024])
                h0T = sb(st, "h0T", [128, 16, 2, 16])
                hend = sb(st, "hend", [128, 16, 2, 16])
                hoP = sb(st, "hoP", [16, 2, 128])
                def front(ti, t0, n, is_s):
                    nch = n // 4
                    par = ti % 2
                    ua = uaL[par]; ub = ubL[par]; vnb = vnbL[par]
                    hk = ("hres", t0)
                    rmsnorm_tile(st, "m", t0, n, gmix[:, layer, :], xnt, "xnt") if ti == 0 else \
                        rmsnorm_tile_again("m", t0, n, gmix[:, layer, :], xnt, "xnt")
                    for ft in range(4):
                        b = psget()
                        for k in range(8):
                            P.op("pe", lambda e, k=k, ft=ft, b=b: e.matmul(
                                PS[:, b, 0:n], lhsT=Win[:, k, ft * 128:(ft + 1) * 128], rhs=xnt[:, k, 0:n],
                                start=(k == 0), stop=(k == 7)), reads=["Win", "xnt"], writes=pk(b))
                        P.op("act", lambda e, ft=ft, b=b: e.activation(out=ua[:, ft, 0:n], in_=PS[:, b, 0:n],
                                                                       func=AF.Copy),
                             reads=pk(b), writes=[("ua", par, ft)])
                    for ft in range(4):
                        b = psget()
                        for k in range(8):
                            P.op("pe", lambda e, k=k, ft=ft, b=b: e.matmul(
                                PS[:, b, 0:n], lhsT=Win[:, k, 512 + ft * 128:512 + (ft + 1) * 128], rhs=xnt[:, k, 0:n],
                                start=(k == 0), stop=(k == 7)), reads=["Win", "xnt"], writes=pk(b))
                        P.op("act", lambda e, ft=ft, b=b: e.activation(out=ub[:, ft, 0:n], in_=PS[:, b, 0:n],
                                                                       func=AF.Copy),
                             reads=pk(b), writes=[("ub", par, ft)])
                    nsub = (n + 127) // 128
                    for sj in range(nsub):
                        m = min(128, n - sj * 128)
                        b = psget()
                        for k in range(8):
                            P.op("pe", lambda e, k=k, b=b, sj=sj, m=m: e.matmul(
                                PS[0:m, b, :], lhsT=xnt[:, k, sj * 128:sj * 128 + m], rhs=Win[:, k, 1024:1536],
                                start=(k == 0), stop=(k == 7)), reads=["Win", "xnt"], writes=pk(b))
                        P.op("act", lambda e, b=b, sj=sj, m=m: e.activation(
                            out=vjunk[0:m, :], in_=PS[0:m, b, :], func=AF.Square, accum_out=vss[0:m, sj:sj + 1]),
                            reads=pk(b), writes=["vjunk", ("vss", sj)])
                        P.op("act", lambda e, sj=sj, m=m: e.activation(
                            out=vss[0:m, sj:sj + 1], in_=vss[0:m, sj:sj + 1], func=AF.Sqrt, bias=epsc[0:m, 0:1],
                            scale=1.0 / 512.0), reads=[("vss", sj), "epsc"], writes=[("vss", sj)])
                        P.op("dve", lambda e, sj=sj, m=m: e.reciprocal(out=vss[0:m, sj:sj + 1], in_=vss[0:m, sj:sj + 1]),
                             reads=[("vss", sj)], writes=[("vss", sj)])
                        P.op("dve", lambda e, b=b, sj=sj, m=m: e.scalar_tensor_tensor(
                            out=vnb[0:m, sj, :], in0=PS[0:m, b, :], scalar=vss[0:m, sj:sj + 1], in1=gsg[0:m, :],
                            op0=ALU.mult, op1=ALU.mult), reads=pk(b) + [("vss", sj), "gsg"], writes=[("vnb", par, sj)])
                        if is_s:
                            P.op("dve", lambda e, b=b, sj=sj, m=m: e.scalar_tensor_tensor(
                                out=vn[0:m, :], in0=PS[0:m, b, :], scalar=vss[0:m, sj:sj + 1], in1=gsg[0:m, :],
                                op0=ALU.mult, op1=ALU.mult), reads=pk(b) + [("vss", sj), "gsg"], writes=[("vn", 0)])
                    if is_s:
                        P.dma("sp", o_s_v[i], vn[0:NS, :], reads=[("vn", 0)])
                        for ri in range(2):
                            b = psget()
                            for hf in range(2):
                                P.dma("sp", h0s[:, :], (st_re if ri == 0 else st_im)[i][:, hf * 1024:(hf + 1) * 1024],
                                      writes=["h0s"])
                                for q in range(8):
                                    Pp = hf * 8 + q
                                    P.op("pe", lambda e, b=b, Pp=Pp, q=q: e.transpose(
                                        PS[:, b, Pp * 16:(Pp + 1) * 16], h0s[:, q * 128:(q + 1) * 128],
                                        ident[0:16, 0:16]), reads=["h0s", "ident"], writes=pk(b))
                            P.op("dve", lambda e, b=b, ri=ri: e.tensor_copy(
                                out=h0T[:, :, ri, :], in_=PS[:, b, 0:256].rearrange("p (a b) -> p a b", a=16)),
                                reads=pk(b), writes=["h0T"])
                def back(ti, t0, n, is_s):
                    nch = n // 4
                    par = ti % 2
                    ua = uaL[par]; ub = ubL[par]; vnb = vnbL[par]
                    for ft in range(4):
                        if not is_s:
                            P.op("pool", lambda e, ft=ft: e.tensor_copy(out=Hf[:, :, :, 0], in_=s5car[:, 4 * ft:4 * ft + 4, :]),
                                 reads=[("s5car", ft)], writes=["Hf0"])
                        b4 = psget(4)
                        for p4 in range(4):
                            for ri in range(2):
                                for s in range(4):
                                    P.op("pe", lambda e, p4=p4, ri=ri, s=s, ft=ft, b4=b4: e.matmul(
                                        PS[:, b4 + p4, ri * NCH:ri * NCH + nch],
                                        lhsT=XB[32 * p4:32 * p4 + 32, ft, ri, s, :],
                                        rhs=ua[32 * p4:32 * p4 + 32, ft, s:n:4],
                                        start=(s == 0), stop=(s == 3), tile_position=(32 * p4, 0)),
                                        reads=["XB", ("ua", par, ft)], writes=pk(b4, 4), cost=40.0)
                        Xr = PS[:, b4:b4 + 4, 0:nch]
                        Xi = PS[:, b4:b4 + 4, NCH:NCH + nch]
                        if not is_s:
                            Cc = TC[:, 4 * ft:4 * ft + 4, 0:nch]
                            Ss = TS[:, 4 * ft:4 * ft + 4, 0:nch]
                            tAa = tA[:, :, 0:nch]; tBb = tB[:, :, 0:nch]
                            GinR = Gin[:, :, 0, 0:nch]; GinI = Gin[:, :, 1, 0:nch]
                            x4 = pk(b4, 4)
                            gq = ft % 2
                            Gsq = GsL[gq]

                            def tt(out, a, bb, op, r, w, eng="dve"):
                                P.op(eng, lambda e: e.tensor_tensor(out=out, in0=a, in1=bb, op=op), reads=r, writes=w)
                            tCc = tC[:, :, 0:nch]; tDd = tD[:, :, 0:nch]
                            tt(tAa, Xr, Cc, ALU.mult, x4 + ["TC"], ["tA"])
                            tt(tBb, Xi, Ss, ALU.mult, x4 + ["TS"], ["tB"])
                            tt(tCc, Xi, Cc, ALU.mult, x4 + ["TC"], ["tC"])
                            tt(tDd, Xr, Ss, ALU.mult, x4 + ["TS"], ["tD"])
                            tt(GinR, tAa, tBb, ALU.add, ["tA", "tB"], ["GinR"])
                            tt(GinI, tCc, tDd, ALU.subtract, ["tC", "tD"], ["GinI"])
                            for p4 in range(4):
                                Pp = 4 * ft + p4
                                for ri in range(2):
                                    P.op("dve", lambda e, p4=p4, ri=ri, Pp=Pp, Gsq=Gsq: e.tensor_tensor_scan(
                                        out=Gsq[:, p4, ri, 0:nch], data0=R4[:, Pp:Pp + 1].broadcast_to([128, nch]),
                                        data1=Gin[:, p4, ri, 0:nch], initial=s5car[:, Pp, ri:ri + 1],
                                        op0=ALU.mult, op1=ALU.add),
                                        reads=["GinR" if ri == 0 else "GinI", "R4", ("s5car", ft)],
                                        writes=[("Gs", gq, p4, ri)], cost=350.0)
                            GR = Gsq[:, :, 0, 0:nch]; GI = Gsq[:, :, 1, 0:nch]
                            gk = [("Gs", gq, a_, b_) for a_ in range(4) for b_ in range(2)]
                            tEe = tE[:, :, 0:nch]; tFf = tF[:, :, 0:nch]; tGg = tG[:, :, 0:nch]; tHh = tH[:, :, 0:nch]
                            tt(tEe, GR, Cc, ALU.mult, gk + ["TC"], ["tE"], "pool")
                            tt(tFf, GI, Ss, ALU.mult, gk + ["TS"], ["tF"], "pool")
                            tt(tGg, GR, Ss, ALU.mult, gk + ["TS"], ["tG"], "pool")
                            tt(tHh, GI, Cc, ALU.mult, gk + ["TC"], ["tH"], "pool")
                            tt(Hf[:, :, 0, 1:nch + 1], tEe, tFf, ALU.subtract, ["tE", "tF", "Hf0"], ["HfR"], "pool")
                            tt(Hf[:, :, 1, 1:nch + 1], tGg, tHh, ALU.add, ["tG", "tH", "Hf0"], ["HfI"], "pool")
                            P.op("act", lambda e: e.activation(out=Hb[:, :, :, 0:nch], in_=Hf[:, :, :, 0:nch], func=AF.Copy),
                                 reads=["HfR", "HfI", "Hf0"], writes=["Hb"])
                            P.op("pool", lambda e, ft=ft: e.tensor_copy(out=s5car[:, 4 * ft:4 * ft + 4, :],
                                                                        in_=Hf[:, :, :, nch]),
                                 reads=["HfR", "HfI"] + gk, writes=[("s5car", ft)])
                        else:
                            h0r = h0T[:, 4 * ft:4 * ft + 4, 0, :]; h0i = h0T[:, 4 * ft:4 * ft + 4, 1, :]
                            a4r = A4[:, 4 * ft:4 * ft + 4, 0:1].broadcast_to([128, 4, 16])
                            a4i = A4[:, 4 * ft:4 * ft + 4, 1:2].broadcast_to([128, 4, 16])
                            tAa = tA[:, :, 0:16]; tBb = tB[:, :, 0:16]
                            x4 = pk(b4, 4)

                            def tt(out, a, bb, op, r, w):
                                P.op("dve", lambda e: e.tensor_tensor(out=out, in0=a, in1=bb, op=op), reads=r, writes=w)
                            tt(tAa, h0r, a4r, ALU.mult, ["h0T", "A4"], ["tA"])
                            tt(tBb, h0i, a4i, ALU.mult, ["h0T", "A4"], ["tB"])
                            tt(tAa, tAa, tBb, ALU.subtract, ["tA", "tB"], ["tA"])
                            tt(hend[:, 4 * ft:4 * ft + 4, 0, :], tAa, Xr, ALU.add, ["tA"] + x4, [("hend", ft, 0)])
                            tt(tAa, h0r, a4i, ALU.mult, ["h0T", "A4", ("hend", ft, 0)], ["tA"])
                            tt(tBb, h0i, a4r, ALU.mult, ["h0T", "A4", ("hend", ft, 0)], ["tB"])
                            tt(tAa, tAa, tBb, ALU.add, ["tA", "tB"], ["tA"])
                            tt(hend[:, 4 * ft:4 * ft + 4, 1, :], tAa, Xi, ALU.add, ["tA"] + x4, [("hend", ft, 1)])
                            P.op("act", lambda e, ft=ft: e.activation(out=Hb[:, :, :, 0:16],
                                                                      in_=h0T[:, 4 * ft:4 * ft + 4, :, :], func=AF.Copy),
                                 reads=["h0T"], writes=["Hb"])
                        by = psget()
                        for t in range(4):
                            o_ = PS[:, by, t * NCH:t * NCH + nch]
                            for tau in range(t + 1):
                                P.op("pe", lambda e, t=t, tau=tau, ft=ft, o_=o_: e.matmul(
                                    o_, lhsT=BD[:, ft, tau, :], rhs=ua[:, ft, (t - tau):n:4],
                                    start=(tau == 0), stop=False), reads=[("ua", par, ft)], writes=pk(by), cost=60.0)
                            for p4 in range(4):
                                for ri in range(2):
                                    last = (ri == 1)
                                    P.op("pe", lambda e, t=t, p4=p4, ri=ri, ft=ft, by=by, last=last: e.matmul(
                                        PS[32 * p4:32 * p4 + 32, by, t * NCH:t * NCH + nch],
                                        lhsT=YC[:, 4 * ft + p4, ri, t, :], rhs=Hb[:, p4, ri, 0:nch],
                                        start=False, stop=last, tile_position=(0, 32 * p4)),
                                        reads=["Hb"], writes=pk(by), cost=45.0)
                        yv = PS[:, by, 0:4 * NCH].rearrange("p (t c) -> p c t", t=4)[:, 0:nch, :]
                        sq3 = sqy[:, 0:n].rearrange("p (c t) -> p c t", t=4)
                        z3 = zf[:, ft, 0:n].rearrange("p (c t) -> p c t", t=4)
                        P.op("act", lambda e, yv=yv, sq3=sq3: e.activation(out=sq3, in_=yv, func=AF.Square),
                             reads=pk(by), writes=["sqy"])
                        P.op("dve", lambda e: e.tensor_scalar(out=sqy[:, 0:n], in0=sqy[:, 0:n], scalar1=0.044715,
                                                              scalar2=1.0, op0=ALU.mult, op1=ALU.add),
                             reads=["sqy"], writes=["sqy"])
                        P.op("dve", lambda e, yv=yv, sq3=sq3: e.tensor_tensor(out=sq3, in0=sq3, in1=yv, op=ALU.mult),
                             reads=["sqy"] + pk(by), writes=["sqy"])
                        P.op("act", lambda e: e.activation(out=sqy[:, 0:n], in_=sqy[:, 0:n], func=AF.Sigmoid,
                                                           scale=1.5957691216057308),
                             reads=["sqy"], writes=["sqy"])
                        P.op("dve", lambda e, yv=yv, sq3=sq3, z3=z3: e.tensor_tensor(out=z3, in0=sq3, in1=yv, op=ALU.mult),
                             reads=["sqy"] + pk(by), writes=[("zf", ft)])
                        P.op("act", lambda e, ft=ft: e.activation(out=zb[:, ft, 0:n], in_=zf[:, ft, 0:n], func=AF.Copy),
                             reads=[("zf", ft)], writes=[("zb", ft)])
                    for fo in range(4):
                        b = psget()
                        for fi in range(4):
                            P.op("pe", lambda e, fi=fi, fo=fo, b=b: e.matmul(
                                PS[:, b, 0:n], lhsT=Wglu[:, fi, fo * 128:(fo + 1) * 128], rhs=zb[:, fi, 0:n],
                                start=(fi == 0), stop=(fi == 3)), reads=["Wsm"] + [("zb", q) for q in range(4)],
                                writes=pk(b))
                        P.op("act", lambda e, fo=fo, b=b: e.activation(out=sg2[:, 0:n], in_=PS[:, b, 0:n], func=AF.Sigmoid,
                                                                       bias=glub[:, i, fo:fo + 1], scale=1.0),
                             reads=pk(b) + ["glub"], writes=["sg2"])
                        P.op("dve", lambda e, fo=fo: e.tensor_tensor(out=ymix[:, fo, 0:n], in0=zf[:, fo, 0:n],
                                                                     in1=sg2[:, 0:n], op=ALU.mult),
                             reads=["sg2", ("zf", fo)], writes=["ymix"])
                    if not is_s:
                        for hp in range(4):
                            b = psget()
                            for j in range(n // 128):
                                for h2 in range(2):
                                    h = 2 * hp + h2
                                    P.op("pe", lambda e, b=b, j=j, h2=h2, h=h: e.matmul(
                                        PS[64 * h2:64 * h2 + 64, b, j * 128:(j + 1) * 128],
                                        lhsT=vnb[:, j, h * 64:(h + 1) * 64], rhs=wT[:, h, :],
                                        start=True, stop=True, tile_position=(0, 64 * h2)),
                                        reads=["wT", ("vnb", par, j)], writes=pk(b))
                            P.op("dve", lambda e, b=b, hp=hp: e.tensor_tensor(
                                out=stmp[:, 0:n].rearrange("p (j i) -> p j i", i=128),
                                in0=PS[:, b, 0:n].rearrange("p (j i) -> p j i", i=128),
                                in1=bbc[:, hp, :].unsqueeze(1).broadcast_to([128, n // 128, 128]), op=ALU.add),
                                reads=pk(b) + ["bbc"], writes=["stmp"])
                            P.op("dve", lambda e, hp=hp: e.tensor_tensor(out=ymix[:, 4 + hp, 0:n], in0=stmp[:, 0:n],
                                                                         in1=ub[:, hp, 0:n], op=ALU.mult),
                                 reads=["stmp", ("ub", par, hp)], writes=["ymix"])
                    else:
                        b = psget()
                        for ft in range(4):
                            P.op("pe", lambda e, b=b, ft=ft: e.transpose(PS[:, b, ft * NS:(ft + 1) * NS],
                                                                         vn[0:NS, ft * 128:(ft + 1) * 128],
                                                                         ident[0:NS, 0:NS]),
                                 reads=[("vn", 0), "ident"], writes=pk(b))
                        P.op("dve", lambda e, b=b: e.tensor_copy(out=vT[:], in_=PS[:, b, 0:4 * NS].rearrange("p (f t) -> p f t", f=4)),
                             reads=pk(b), writes=["vT"])
                        for ft in range(4):
                            v3 = vT[:, ft, :].rearrange("p (b j) -> p b j", j=4)
                            for ii in range(4):
                                P.op("dve", lambda e, ft=ft, ii=ii, v3=v3: e.tensor_scalar(
                                    out=sacc[:, :, ii], in0=v3[:, :, 0], scalar1=wsc[:, ft, 4 * ii:4 * ii + 1],
                                    scalar2=bbc[:, ft, ii:ii + 1], op0=ALU.mult, op1=ALU.add),
                                    reads=["vT", "wsc", "bbc"], writes=["sacc"])
                                for jj in range(1, ii + 1):
                                    P.op("dve", lambda e, ft=ft, ii=ii, jj=jj, v3=v3: e.scalar_tensor_tensor(
                                        out=sacc[:, :, ii], in0=v3[:, :, jj], scalar=wsc[:, ft, 4 * ii + jj:4 * ii + jj + 1],
                                        in1=sacc[:, :, ii], op0=ALU.mult, op1=ALU.add),
                                        reads=["vT", "wsc", "sacc"], writes=["sacc"])
                            P.op("dve", lambda e, ft=ft: e.tensor_tensor(
                                out=ymix[:, 4 + ft, 0:NS], in0=sacc[:].rearrange("p b i -> p (b i)"),
                                in1=ub[:, ft, 0:NS], op=ALU.mult), reads=["sacc", ("ub", par, ft)], writes=["ymix"])
                    out_proj_tile(Wout, "Wout", ymix, "ymix", t0, n)
                    if is_s:
                        for ri in range(2):
                            for hf in range(2):
                                for h2 in range(2):
                                    half = hf * 2 + h2
                                    b = psget()
                                    for q in range(4):
                                        Pp = half * 4 + q
                                        P.op("pe", lambda e, b=b, q=q, Pp=Pp, ri=ri: e.transpose(
                                            PS[0:16, b, q * 128:(q + 1) * 128], hend[:, Pp, ri, :], ident[:]),
                                            reads=[("hend", Pp // 4, ri), "ident"], writes=pk(b))
                                    P.op("act", lambda e, b=b, h2=h2: e.activation(
                                        out=h0s[:, h2 * 512:(h2 + 1) * 512], in_=PS[0:16, b, :], func=AF.Copy),
                                        reads=pk(b), writes=["h0s"])
                                P.dma("sp", (o_s_re if ri == 0 else o_s_im)[i][:, hf * 1024:(hf + 1) * 1024], h0s[:, :],
                                      reads=["h0s"])
                    if (not is_s) and t0 + n == SEQ:
                        for ri in range(2):
                            b = psget()
                            P.op("pe", lambda e, b=b, ri=ri: e.transpose(PS[0:16, b, 0:128], s5car[:, :, ri], ident[:]),
                                 reads=[("s5car", q_) for q_ in range(4)] + ["ident"], writes=pk(b))
                            P.op("act", lambda e, b=b, ri=ri: e.activation(out=hoP[:, ri, :], in_=PS[0:16, b, 0:128], func=AF.Copy),
                                 reads=pk(b), writes=["hoP"])
                        P.dma("sp", o_p_re[i], hoP[:, 0, :], reads=["hoP"])
                        P.dma("sp", o_p_im[i], hoP[:, 1, :], reads=["hoP"])
                seq = list(enumerate(mtiles))
                for idx, (ti, (t0, n, is_s)) in enumerate(seq):
                    front(ti, t0, n, is_s)
                    if idx >= 1:
                        pti, (pt0, pn, ps_) = seq[idx - 1]
                        back(pti, pt0, pn, ps_)
                lti, (lt0, ln, ls_) = seq[-1]
                back(lti, lt0, ln, ls_)
                P.flush()

        _norm_scr = {}

        def rmsnorm_tile_again(tag, t0, n, gvec, xn_out, xn_key):
            _rms_ops(tag, t0, n, gvec, xn_out, xn_key, _norm_scr[tag])

        def _rms_ops(tag, t0, n, gvec, xn_out, xn_key, srs, xoff=0):
            sr, sr2 = srs
            hk = hkeys(t0, n)
            sqv = xn_out[:, :, xoff:xoff + n]
            P.op("act", lambda e: e.activation(out=sqv, in_=hres[:, :, t0:t0 + n], func=AF.Square),
                 reads=hk, writes=[xn_key])
            b = psget()
            for k in range(8):
                P.op("pe", lambda e, k=k, b=b: e.matmul(PS[:, b, 0:n], lhsT=onesb[:], rhs=xn_out[:, k, xoff:xoff + n],
                                                        start=(k == 0), stop=(k == 7)),
                     reads=[xn_key, "onesb"], writes=pk(b))
            P.op("act", lambda e, b=b: e.activation(out=sr[:, 0:n], in_=PS[:, b, 0:n], func=AF.Ln,
                                                    bias=epsc[:, 0:1], scale=1.0 / D),
                 reads=pk(b) + ["epsc"], writes=["sr_" + tag])
            P.op("act", lambda e: e.activation(out=sr[:, 0:n], in_=sr[:, 0:n], func=AF.Exp, scale=-0.5),
                 reads=["sr_" + tag], writes=["sr_" + tag])
            for k in range(8):
                P.op("dve", lambda e, k=k: e.scalar_tensor_tensor(
                    out=xn_out[:, k, xoff:xoff + n], in0=hres[:, k, t0:t0 + n], scalar=gvec[:, k:k + 1],
                    in1=sr2[:, 0:n], op0=ALU.mult, op1=ALU.mult),
                    reads=hk + ["sr_" + tag, "gmix", "gffn"], writes=[xn_key])

        def rmsnorm_tile(stk, tag, t0, n, gvec, xn_out, xn_key, xoff=0):
            nmax = TM if tag == "m" else TF
            sr = sb(stk, "sr_" + tag, [128, nmax])
            sr2 = sr
            _norm_scr[tag] = (sr, sr2)
            _rms_ops(tag, t0, n, gvec, xn_out, xn_key, (sr, sr2), xoff=xoff)

        def out_proj_tile(Wout, wkey, ymix, ykey, t0, n):
            hk = hkeys(t0, n)
            for fo in range(8):
                b = psget()
                for k in range(8):
                    P.op("pe", lambda e, k=k, fo=fo, b=b: e.matmul(
                        PS[:, b, 0:n], lhsT=Wout[:, k, fo * 128:(fo + 1) * 128], rhs=ymix[:, k, 0:n],
                        start=(k == 0), stop=(k == 7)), reads=[wkey, ykey], writes=pk(b))
                P.op("dve", lambda e, fo=fo, b=b: e.tensor_tensor(
                    out=hres[:, fo, t0:t0 + n], in0=hres[:, fo, t0:t0 + n], in1=PS[:, b, 0:n], op=ALU.add),
                    reads=pk(b) + hk, writes=hk)

        def odd_mixer(layer):
            i = layer // 2
            P.cost.update({"pe": 115.0, "dve": 430.0, "act": 450.0})
            with ExitStack() as st:
                Win = WinP
                Wout = WoutP
                Wp = WsmP[:, 0:512].rearrange("p (g d) -> p g d", g=4)
                xnt = sb(st, "xnto", [128, 8, TM], BF16)
                XCL = [sb(st, "XC%d" % q, [128, 4, 15 + TM]) for q in range(2)]
                PA = sb(st, "PA", [128, 15 + TM]); PB = sb(st, "PB", [128, 15 + TM])
                diff = sb(st, "diff", [128, 4, TM], BF16)
                xdL = [sb(st, "xd%d" % q, [128, 4, TM]) for q in range(2)]
                bgL = [sb(st, "bg%d" % q, [128, 4, TM]) for q in range(2)]
                ZL = [sb(st, "Z%d" % q, [128, 4, 2 + TM]) for q in range(2)]
                ca = sb(st, "ca", [128, TM])
                ymix = sb(st, "ymixo", [128, 8, TM], BF16)
                invn = sb(st, "invn", [128, 4, 15])
                XCs = sb(st, "XCs", [128, 4, 16, 19])
                PAs = sb(st, "PAs", [128, 16, 19]); PBs = sb(st, "PBs", [128, 16, 19])
                Zs = sb(st, "Zs", [128, 4, 16, 6])
                spl = [sb(st, "spl%d" % q, [128, 512]) for q in range(2)]
                scl = sb(st, "scl", [32, 512])
                otp = sb(st, "otp", [128, 512])
                otc = sb(st, "otc", [32, 512])
                opp = sb(st, "opp", [16, 512])
                opc = sb(st, "opc", [2, 512])
                xct = sb(st, "xct", [128, 128])
                zct = sb(st, "zct", [128, 32])
                P.op("pool", lambda e: e.iota(invn[:], pattern=[[0, 4], [1, 15]], base=1, channel_multiplier=0,
                                              allow_small_or_imprecise_dtypes=True), writes=["invn"])
                for gi in range(4):
                    P.op("dve", lambda e, gi=gi: e.tensor_scalar(out=invn[:, gi, :], in0=invn[:, gi, :],
                                                                 scalar1=float(2 ** (gi + 1)), scalar2=None, op0=ALU.min),
                         reads=["invn"], writes=["invn"])
                P.op("dve", lambda e: e.reciprocal(out=invn[:], in_=invn[:]), reads=["invn"], writes=["invn"])
                P.op("dve", lambda e: e.memset(xchalo[:], 0.0), writes=["xchalo"])
                P.op("dve", lambda e: e.memset(zhalo[:], 0.0), writes=["zhalo"])
                P.op("pool", lambda e: e.memset(PA[:], 0.0), writes=["P0"])
                P.op("pool", lambda e: e.memset(PB[:], 0.0), writes=["P1"])
                P.op("pool", lambda e: e.memset(PAs[:], 0.0), writes=["Ps0"])
                P.op("pool", lambda e: e.memset(PBs[:], 0.0), writes=["Ps1"])
                def front(ti, t0, n, is_s):
                    par = ti % 2
                    XC = XCL[par]; xd = xdL[par]; bg = bgL[par]; Z = ZL[par]
                    if ti == 0:
                        rmsnorm_tile(st, "m", t0, n, gmix[:, layer, :], xnt, "xnt")
                    else:
                        rmsnorm_tile_again("m", t0, n, gmix[:, layer, :], xnt, "xnt")
                    if not is_s:
                        P.op("dve", lambda e: e.tensor_copy(out=XC[:, :, 0:15], in_=xchalo[:]), reads=["xchalo"],
                             writes=[("XChalo", par)])
                        P.op("dve", lambda e: e.tensor_copy(out=Z[:, :, 0:2], in_=zhalo[:]), reads=["zhalo"],
                             writes=[("Zhalo", par)])
                    else:
                        P.dma("sp", spl[0][:], st_pool[i].rearrange("b r c -> (b r) c")[0:128, :], writes=["spl0"])
                        P.dma("sp", spl[1][0:112, :], st_pool[i].rearrange("b r c -> (b r) c")[128:240, :], writes=["spl1"])
                        P.dma("sp", scl[:], st_conv[i].rearrange("b r c -> (b r) c"), writes=["scl"])
                        for ft in range(4):
                            b = psget()
                            P.op("pe", lambda e, b=b, ft=ft: e.transpose(PS[:, b, 0:128], spl[0][:, ft * 128:(ft + 1) * 128], ident[:]),
                                 reads=["spl0", "ident"], writes=pk(b))
                            P.op("pe", lambda e, b=b, ft=ft: e.transpose(PS[:, b, 128:240], spl[1][0:112, ft * 128:(ft + 1) * 128],
                                                                         ident[0:112, 0:112]),
                                 reads=["spl1", "ident"], writes=pk(b))
                            P.op("pe", lambda e, b=b, ft=ft: e.transpose(PS[:, b, 256:288], scl[:, ft * 128:(ft + 1) * 128],
                                                                         ident[0:32, 0:32]),
                                 reads=["scl", "ident"], writes=pk(b))
                            P.op("dve", lambda e, b=b, ft=ft: e.tensor_copy(
                                out=XCs[:, ft, :, 0:15], in_=PS[:, b, 0:240].rearrange("p (b r) -> p b r", r=15)),
                                reads=pk(b), writes=[("XCs", ft)])
                            P.op("dve", lambda e, b=b, ft=ft: e.tensor_copy(
                                out=Zs[:, ft, :, 0:2], in_=PS[:, b, 256:288].rearrange("p (b r) -> p b r", r=2)),
                                reads=pk(b), writes=[("Zs", ft)])
                    for ft in range(4):
                        b = psget()
                        for k in range(8):
                            P.op("pe", lambda e, k=k, ft=ft, b=b: e.matmul(
                                PS[:, b, 0:n], lhsT=Win[:, k, ft * 128:(ft + 1) * 128], rhs=xnt[:, k, 0:n],
                                start=(k == 0), stop=(k == 7)), reads=["Win", "xnt"], writes=pk(b))
                        if not is_s:
                            P.op("act", lambda e, ft=ft, b=b: e.activation(out=XC[:, ft, 15:15 + n], in_=PS[:, b, 0:n], func=AF.Copy),
                                 reads=pk(b), writes=[("XC", par, ft)])
                        else:
                            P.op("act", lambda e, ft=ft, b=b: e.activation(
                                out=XCs[:, ft, :, 15:19], in_=PS[:, b, 0:NS].rearrange("p (b t) -> p b t", t=4), func=AF.Copy),
                                reads=pk(b) + [("XCs", ft)], writes=[("XCs", ft)])
                    for ft in range(4):
                        b = psget()
                        for k in range(8):
                            P.op("pe", lambda e, k=k, ft=ft, b=b: e.matmul(
                                PS[:, b, 0:n], lhsT=Win[:, k, 512 + ft * 128:512 + (ft + 1) * 128], rhs=xnt[:, k, 0:n],
                                start=(k == 0), stop=(k == 7)), reads=["Win", "xnt"], writes=pk(b))
                        P.op("act", lambda e, ft=ft, b=b: e.activation(out=xd[:, ft, 0:n], in_=PS[:, b, 0:n], func=AF.Copy),
                             reads=pk(b), writes=[("xd", par, ft)])
                    for ft in range(4):
                        b = psget()
                        for k in range(8):
                            P.op("pe", lambda e, k=k, ft=ft, b=b: e.matmul(
                                PS[:, b, 0:n], lhsT=Win[:, k, 1024 + ft * 128:1024 + (ft + 1) * 128], rhs=xnt[:, k, 0:n],
                                start=(k == 0), stop=(k == 7)), reads=["Win", "xnt"], writes=pk(b))
                        P.op("act", lambda e, ft=ft, b=b: e.activation(out=bg[:, ft, 0:n], in_=PS[:, b, 0:n], func=AF.Copy),
                             reads=pk(b), writes=[("bg", par, ft)])
                    for ft in range(4):
                        b = psget()
                        for k in range(8):
                            P.op("pe", lambda e, k=k, ft=ft, b=b: e.matmul(
                                PS[:, b, 0:n], lhsT=Win[:, k, 1536 + ft * 128:1536 + (ft + 1) * 128], rhs=xnt[:, k, 0:n],
                                start=(k == 0), stop=(k == 7)), reads=["Win", "xnt"], writes=pk(b))
                        if not is_s:
                            P.op("dve", lambda e, ft=ft, b=b: e.tensor_tensor(out=Z[:, ft, 2:2 + n], in0=PS[:, b, 0:n],
                                                                              in1=xd[:, ft, 0:n], op=ALU.mult),
                                 reads=pk(b) + [("xd", par, ft)], writes=[("Z", par, ft)])
                        else:
                            P.op("dve", lambda e, ft=ft, b=b: e.tensor_tensor(
                                out=Zs[:, ft, :, 2:6], in0=PS[:, b, 0:NS].rearrange("p (b t) -> p b t", t=4),
                                in1=xd[:, ft, 0:NS].rearrange("p (b t) -> p b t", t=4), op=ALU.mult),
                                reads=pk(b) + [("xd", par, ft), ("Zs", ft)], writes=[("Zs", ft)])
                    if not is_s:
                        P.op("dve", lambda e: e.tensor_copy(out=xchalo[:], in_=XC[:, :, n:n + 15]),
                             reads=[("XC", par, q) for q in range(4)] + [(("XChalo", par), par)], writes=["xchalo"])
                        P.op("dve", lambda e: e.tensor_copy(out=zhalo[:], in_=Z[:, :, n:n + 2]),
                             reads=[("Z", par, q) for q in range(4)] + [(("Zhalo", par), par)], writes=["zhalo"])
                def back(ti, t0, n, is_s):
                    par = ti % 2
                    XC = XCL[par]; xd = xdL[par]; bg = bgL[par]; Z = ZL[par]
                    for gi in range(4):
                        w = 2 ** (gi + 1)
                        if not is_s:
                            L = 15 + n
                            src = XC[:, gi, 0:L]
                            bufs = [PA, PB]
                            cur = src
                            ckey = [("XC", par, gi), ("XChalo", par)]
                            d = 1
                            q = 0
                            while d < w:
                                dst = bufs[q % 2]
                                dk = "P%d" % (q % 2)
                                P.op("dve", lambda e, cur=cur, dst=dst, d=d, L=L: e.tensor_tensor(
                                    out=dst[:, d:L], in0=cur[:, d:L], in1=cur[:, 0:L - d], op=ALU.add),
                                    reads=ckey, writes=[dk])
                                cur = dst[:, 0:L]
                                ckey = [dk]
                                d *= 2
                                q += 1
                            P.op("dve", lambda e, cur=cur, gi=gi, w=w: e.scalar_tensor_tensor(
                                out=diff[:, gi, 0:n], in0=cur[:, 15:15 + n], scalar=1.0 / w, in1=XC[:, gi, 15:15 + n],
                                op0=ALU.mult, op1=ALU.subtract), reads=ckey + [("XC", par, gi)], writes=[("diff", gi)])
                            if t0 == 0:
                                P.op("dve", lambda e, cur=cur, gi=gi: e.tensor_tensor(
                                    out=ca[:, 0:15], in0=cur[:, 15:30], in1=invn[:, gi, :], op=ALU.mult),
                                    reads=ckey + ["invn"], writes=["ca"])
                                P.op("dve", lambda e, gi=gi: e.tensor_tensor(
                                    out=diff[:, gi, 0:15], in0=ca[:, 0:15], in1=XC[:, gi, 15:30], op=ALU.subtract),
                                    reads=["ca", ("XC", par, gi), ("diff", gi)], writes=[("diff", gi)])
                        else:
                            L = 19
                            cur = XCs[:, gi, :, :]
                            ckey = [("XCs", gi)]
                            bufs = [PAs, PBs]
                            d = 1
                            q = 0
                            while d < w:
                                dst = bufs[q % 2]
                                dk = "Ps%d" % (q % 2)
                                P.op("dve", lambda e, cur=cur, dst=dst, d=d: e.tensor_tensor(
                                    out=dst[:, :, d:19], in0=cur[:, :, d:19], in1=cur[:, :, 0:19 - d], op=ALU.add),
                                    reads=ckey, writes=[dk])
                                cur = dst[:, :, :]
                                ckey = [dk]
                                d *= 2
                                q += 1
                            P.op("dve", lambda e, cur=cur, gi=gi, w=w: e.scalar_tensor_tensor(
                                out=diff[:, gi, 0:NS].rearrange("p (b t) -> p b t", t=4), in0=cur[:, :, 15:19],
                                scalar=1.0 / w, in1=XCs[:, gi, :, 15:19], op0=ALU.mult, op1=ALU.subtract),
                                reads=ckey + [("XCs", gi)], writes=[("diff", gi)])
                        b = psget()
                        P.op("pe", lambda e, gi=gi, b=b: e.matmul(PS[:, b, 0:n], lhsT=Wp[:, gi, :], rhs=diff[:, gi, 0:n],
                                                                  start=True, stop=True),
                             reads=["Wsm", ("diff", gi)], writes=pk(b))
                        P.op("act", lambda e, gi=gi, b=b: e.activation(out=ymix[:, gi, 0:n], in_=PS[:, b, 0:n], func=AF.Copy,
                                                                       scale=pscale[:, i, gi:gi + 1]),
                             reads=pk(b) + ["pscale"], writes=["ymix"])
                    for ft in range(4):
                        if not is_s:
                            z0 = Z[:, ft, 0:n]; z1 = Z[:, ft, 1:n + 1]; z2 = Z[:, ft, 2:n + 2]
                            cav = ca[:, 0:n]
                            bgv = bg[:, ft, 0:n]
                            yv = ymix[:, 4 + ft, 0:n]
                            zk = [("Z", par, ft), ("Zhalo", par)]
                        else:
                            z0 = Zs[:, ft, :, 0:4]; z1 = Zs[:, ft, :, 1:5]; z2 = Zs[:, ft, :, 2:6]
                            cav = ca[:, 0:NS].rearrange("p (b t) -> p b t", t=4)
                            bgv = bg[:, ft, 0:NS].rearrange("p (b t) -> p b t", t=4)
                            yv = ymix[:, 4 + ft, 0:NS].rearrange("p (b t) -> p b t", t=4)
                            zk = [("Zs", ft)]
                        P.op("dve", lambda e, ft=ft, z0=z0, cav=cav: e.tensor_scalar(
                            out=cav, in0=z0, scalar1=cw[:, i, 0, ft:ft + 1], scalar2=cb[:, i, ft:ft + 1],
                            op0=ALU.mult, op1=ALU.add), reads=zk + ["cw", "cb"], writes=["ca"])
                        P.op("dve", lambda e, ft=ft, z1=z1, cav=cav: e.scalar_tensor_tensor(
                            out=cav, in0=z1, scalar=cw[:, i, 1, ft:ft + 1], in1=cav, op0=ALU.mult, op1=ALU.add),
                            reads=zk + ["cw", "ca"], writes=["ca"])
                        P.op("dve", lambda e, ft=ft, z2=z2, cav=cav: e.scalar_tensor_tensor(
                            out=cav, in0=z2, scalar=cw[:, i, 2, ft:ft + 1], in1=cav, op0=ALU.mult, op1=ALU.add),
                            reads=zk + ["cw", "ca"], writes=["ca"])
                        P.op("dve", lambda e, cav=cav, bgv=bgv, yv=yv: e.tensor_tensor(out=yv, in0=cav, in1=bgv, op=ALU.mult),
                             reads=["ca", ("bg", par, ft)], writes=["ymix"])
                    out_proj_tile(Wout, "Wout", ymix, "ymix", t0, n)
                    if (not is_s) and t0 + n == SEQ:
                        for ft in range(4):
                            b = psget()
                            P.op("pe", lambda e, b=b, ft=ft: e.transpose(PS[0:15, b, 0:128], xchalo[:, ft, :], ident[:]),
                                 reads=["xchalo", "ident"], writes=pk(b))
                            P.op("pe", lambda e, b=b, ft=ft: e.transpose(PS[0:2, b, 128:256], zhalo[:, ft, :], ident[:]),
                                 reads=["zhalo", "ident"], writes=pk(b))
                            P.op("act", lambda e, b=b, ft=ft: e.activation(out=opp[0:15, ft * 128:(ft + 1) * 128],
                                                                           in_=PS[0:15, b, 0:128], func=AF.Copy),
                                 reads=pk(b), writes=["opp"])
                            P.op("act", lambda e, b=b, ft=ft: e.activation(out=opc[0:2, ft * 128:(ft + 1) * 128],
                                                                           in_=PS[0:2, b, 128:256], func=AF.Copy),
                                 reads=pk(b), writes=["opc"])
                        P.dma("sp", o_p_pool[i], opp[0:15, :], reads=["opp"])
                        P.dma("sp", o_p_conv[i], opc[0:2, :], reads=["opc"])
                    if is_s:
                        for half in range(2):
                            for ft in range(4):
                                P.op("dve", lambda e, ft=ft, half=half: e.tensor_copy(
                                    out=xct[:, 0:120].rearrange("p (b r) -> p b r", r=15),
                                    in_=XCs[:, ft, half * 8:half * 8 + 8, 4:19]), reads=[("XCs", ft)], writes=["xct"])
                                b = psget()
                                P.op("pe", lambda e, b=b: e.transpose(PS[0:120, b, 0:128], xct[:, 0:120], ident[:]),
                                     reads=["xct", "ident"], writes=pk(b))
                                P.op("act", lambda e, b=b, ft=ft: e.activation(out=otp[0:120, ft * 128:(ft + 1) * 128],
                                                                               in_=PS[0:120, b, 0:128], func=AF.Copy),
                                     reads=pk(b), writes=["otp"])
                            P.dma("sp", o_s_pool[i, half * 120:half * 120 + 120, :], otp[0:120, :], reads=["otp"])
                        for ft in range(4):
                            P.op("dve", lambda e, ft=ft: e.tensor_copy(
                                out=zct[:, 0:32].rearrange("p (b r) -> p b r", r=2), in_=Zs[:, ft, :, 4:6]),
                                reads=[("Zs", ft)], writes=["zct"])
                            b = psget()
                            P.op("pe", lambda e, b=b: e.transpose(PS[0:32, b, 0:128], zct[:, 0:32], ident[:]),
                                 reads=["zct", "ident"], writes=pk(b))
                            P.op("act", lambda e, b=b, ft=ft: e.activation(out=otc[:, ft * 128:(ft + 1) * 128],
                                                                           in_=PS[0:32, b, 0:128], func=AF.Copy),
                                 reads=pk(b), writes=["otc"])
                        P.dma("sp", o_s_conv[i], otc[:, :], reads=["otc"])
                seq = list(enumerate(mtiles))
                for idx, (ti, (t0, n, is_s)) in enumerate(seq):
                    front(ti, t0, n, is_s)
                    if idx >= 1:
                        pti, (pt0, pn, ps_) = seq[idx - 1]
                        back(pti, pt0, pn, ps_)
                lti, (lt0, ln, ls_) = seq[-1]
                back(lti, lt0, ln, ls_)
                P.flush()

        def epilogue(st):
            gfin = WinP[:, 2, :].bitcast(F32)
            P.dma("sp", gfin, norm_final.broadcast_to([128, D]), writes=["gfin"])
            junk = sb(st, "fjunk", [128, 512], BF16)
            ss = sb(st, "fss", [128, 2, 2])
            yo = [WinP[:, q, :].bitcast(F32) for q in range(2)]
            nsub = SEQ // 128 + 1
            for si in range(nsub):
                n = 128 if si < SEQ // 128 else NS
                dst = y_p[si * 128:(si + 1) * 128, :] if si < SEQ // 128 else y_s[:, :]
                par = si % 2
                yb = yo[par]
                yk = "yo%d" % par
                hk = hkeys(si * 128, n)
                bb = psget(2)
                for k in range(8):
                    P.op("pe", lambda e, k=k, bb=bb, si=si, n=n: e.transpose(
                        PS[0:n, bb + k // 4, (k % 4) * 128:(k % 4 + 1) * 128], hres[:, k, si * 128:si * 128 + n], ident[:]),
                        reads=["ident"] + hk, writes=pk(bb, 2), cost=110.0)
                for half in range(2):
                    P.op("act", lambda e, bb=bb, half=half, n=n, par=par: e.activation(
                        out=junk[0:n, :], in_=PS[0:n, bb + half, :], func=AF.Square, accum_out=ss[0:n, par, half:half + 1]),
                        reads=pk(bb, 2), writes=["fjunk", ("fss", par, half)])
                P.op("dve", lambda e, n=n, par=par: e.tensor_tensor(out=ss[0:n, par, 0:1], in0=ss[0:n, par, 0:1],
                                                                     in1=ss[0:n, par, 1:2], op=ALU.add),
                     reads=[("fss", par, 0), ("fss", par, 1)], writes=[("fss", par, 0)], cost=100.0)
                P.op("act", lambda e, n=n, par=par: e.activation(out=ss[0:n, par, 0:1], in_=ss[0:n, par, 0:1], func=AF.Sqrt,
                                                                  bias=epsc[0:n, 0:1], scale=1.0 / D),
                     reads=[("fss", par, 0)], writes=[("fss", par, 0)], cost=250.0)
                P.op("dve", lambda e, n=n, par=par: e.reciprocal(out=ss[0:n, par, 0:1], in_=ss[0:n, par, 0:1]),
                     reads=[("fss", par, 0)], writes=[("fss", par, 0)], cost=100.0)
                for half in range(2):
                    P.op("dve", lambda e, bb=bb, half=half, n=n, yb=yb, par=par: e.scalar_tensor_tensor(
                        out=yb[0:n, half * 512:(half + 1) * 512], in0=PS[0:n, bb + half, :], scalar=ss[0:n, par, 0:1],
                        in1=gfin[0:n, half * 512:(half + 1) * 512], op0=ALU.mult, op1=ALU.mult),
                        reads=pk(bb, 2) + [("fss", par, 0), "gfin"], writes=[yk], cost=750.0)
                P.dma("sp", dst, yb[0:n, :], reads=[yk])

        def ffn(layer):
            widths = [384] * 7 + [128]
            offs = [sum(widths[:j]) for j in range(len(widths))]
            with ExitStack() as st:
                xn = sb(st, "xn_all", [128, 8, T], BF16)
                Wg = [sb(st, "Wg%d" % q, [128, 8, 384], BF16) for q in range(2)]
                Wu = [sb(st, "Wu%d" % q, [128, 8, 384], BF16) for q in range(2)]
                Wd = [sb(st, "Wd%d" % q, [128, 3, D], BF16) for q in range(2)]
                sl = [sb(st, "sl%d" % q, [128, TF]) for q in range(2)]
                hb = [sb(st, "hb%d" % q, [128, 3, TF], BF16) for q in range(2)]

                P.cost.update({"pe": 195.0, "dve": 630.0, "act": 560.0})

                def load_slice(j):
                    q = j % 2
                    w = widths[j]
                    o = offs[j]
                    c = 2500.0 + 128 * 8 * w * 4 / 150.0
                    for kh in range(2):
                        P.dma("pool", Wg[q][:, 4 * kh:4 * kh + 4, 0:w],
                              ffn_g[layer].rearrange("(k p) n -> p k n", p=128)[:, 4 * kh:4 * kh + 4, o:o + w],
                              writes=[("Wg", q, kh)], cost=c / 2)
                    for kh in range(2):
                        P.dma("pool", Wu[q][:, 4 * kh:4 * kh + 4, 0:w],
                              ffn_u[layer].rearrange("(k p) n -> p k n", p=128)[:, 4 * kh:4 * kh + 4, o:o + w],
                              writes=[("Wu", q, kh)], cost=c / 2)
                    P.dma("pool", Wd[q][:, 0:w // 128, :],
                          ffn_d[layer].rearrange("(k p) n -> p k n", p=128)[:, o // 128:(o + w) // 128, :],
                          writes=[("Wd", q)], cost=c)
                load_slice(0)
                load_slice(1)
                if layer + 1 < 4:
                    load_mixer_weights(layer + 1)
                for ti, (t0, n, is_s) in enumerate(ftiles):
                    if ti == 0:
                        rmsnorm_tile(st, "f", t0, n, gffn[:, layer, :], xn, ("xn", t0), xoff=t0)
                    else:
                        _rms_ops("f", t0, n, gffn[:, layer, :], xn, ("xn", t0), _norm_scr["f"], xoff=t0)
                hbi = 0
                for j in range(len(widths)):
                    q = j % 2
                    nhc = widths[j] // 128
                    for (t0, n, is_s) in ftiles:
                        hk = hkeys(t0, n)
                        hbuf = hb[hbi % 2]
                        hkey = "hb%d" % (hbi % 2)
                        hbi += 1
                        for hc in range(nhc):
                            bgt = psget()
                            for k in range(8):
                                P.op("pe", lambda e, k=k, hc=hc, bgt=bgt, q=q, t0=t0, n=n: e.matmul(
                                    PS[:, bgt, 0:n], lhsT=Wg[q][:, k, hc * 128:(hc + 1) * 128], rhs=xn[:, k, t0:t0 + n],
                                    start=(k == 0), stop=(k == 7)), reads=[("Wg", q, k // 4), ("xn", t0)], writes=pk(bgt),
                                    cost=n / 2.35 + 6)
                            but = psget()
                            for k in range(8):
                                P.op("pe", lambda e, k=k, hc=hc, but=but, q=q, t0=t0, n=n: e.matmul(
                                    PS[:, but, 0:n], lhsT=Wu[q][:, k, hc * 128:(hc + 1) * 128], rhs=xn[:, k, t0:t0 + n],
                                    start=(k == 0), stop=(k == 7)), reads=[("Wu", q, k // 4), ("xn", t0)], writes=pk(but),
                                    cost=n / 2.35 + 6)
                            slt = sl[hc % 2]
                            slk = "sl%d" % (hc % 2)
                            P.op("act", lambda e, bgt=bgt, slt=slt, n=n: e.activation(out=slt[:, 0:n], in_=PS[:, bgt, 0:n], func=AF.Silu),
                                 reads=pk(bgt), writes=[slk], cost=(224 + n) / 1.2)
                            P.op("dve", lambda e, but=but, slt=slt, hbuf=hbuf, hc=hc, n=n: e.tensor_tensor(
                                out=hbuf[:, hc, 0:n], in0=slt[:, 0:n], in1=PS[:, but, 0:n], op=ALU.mult),
                                reads=pk(but) + [slk], writes=[(hkey, hc)], cost=(160 + n) / 0.96)
                        for fo in range(8):
                            b = psget()
                            for hc in range(nhc):
                                P.op("pe", lambda e, hc=hc, fo=fo, b=b, q=q, hbuf=hbuf, n=n, nhc=nhc: e.matmul(
                                    PS[:, b, 0:n], lhsT=Wd[q][:, hc, fo * 128:(fo + 1) * 128], rhs=hbuf[:, hc, 0:n],
                                    start=(hc == 0), stop=(hc == nhc - 1)), reads=[("Wd", q), (hkey, hc)], writes=pk(b),
                                    cost=n / 2.35 + 6)
                            P.op("dve", lambda e, fo=fo, b=b, t0=t0, n=n: e.tensor_tensor(
                                out=hres[:, fo, t0:t0 + n], in0=hres[:, fo, t0:t0 + n], in1=PS[:, b, 0:n], op=ALU.add),
                                reads=pk(b) + hk, writes=hk, cost=(160 + n) / 0.96)
                    if j + 2 < len(widths):
                        load_slice(j + 2)
                if layer == 3:
                    epilogue(st)
                P.flush()

        for layer in range(4):
            if layer > 0:
                P.next_epoch()
            if layer % 2 == 0:
                even_mixer(layer)
            else:
                odd_mixer(layer)
            ffn(layer)

    return nc


_NC_CACHE = {}


def kernel(**inputs):
    f = lambda a: np.ascontiguousarray(np.asarray(a, dtype=np.float32))
    inp = {k: f(v) for k, v in inputs.items()}
    if "nc" not in _NC_CACHE:
        _NC_CACHE["nc"] = build_nc()
    nc = _NC_CACHE["nc"]
    shared = {}
    for k in ("norm_mix", "norm_ffn", "w_in_even", "w_out_even", "s5_lambda_re", "s5_lambda_im", "s5_log_dt",
              "s5_b_re", "s5_b_im", "s5_c_re", "s5_c_im", "s5_glu_w", "s5_glu_b", "sgu_norm", "sgu_w", "sgu_b",
              "w_in_odd", "w_out_odd", "pool_w", "pool_scale", "conv_w", "conv_b", "ffn_w_gate", "ffn_w_up",
              "ffn_w_down"):
        shared[k] = inp[k]
    shared["norm_final"] = inp["norm_final"].reshape(1, D)
    shared["s5_d"] = inp["s5_d"].reshape(2, 512)
    in_maps = []
    for c in range(NCORES):
        m = dict(shared)
        m["x_p"] = inp["x_prompt"][c]
        m["x_s"] = np.ascontiguousarray(inp["x_sample"][16 * c:16 * c + 16].reshape(NS, D))
        m["st_re"] = np.ascontiguousarray(inp["state_s5_re"][:, 16 * c:16 * c + 16].reshape(2, 16, 2048))
        m["st_im"] = np.ascontiguousarray(inp["state_s5_im"][:, 16 * c:16 * c + 16].reshape(2, 16, 2048))
        m["st_pool"] = np.ascontiguousarray(inp["state_pool"][:, 16 * c:16 * c + 16])
        m["st_conv"] = np.ascontiguousarray(inp["state_conv"][:, 16 * c:16 * c + 16])
        in_maps.append(m)
    res = run_bass_kernel_spmd(nc, in_maps, core_ids=list(range(NCORES)))
    R = res.results
    y_prompt = np.stack([R[c]["y_p"] for c in range(NCORES)], 0).reshape(8, SEQ, D)
    y_sample = np.concatenate([R[c]["y_s"].reshape(16, 4, D) for c in range(NCORES)], 0)
    p_re = np.stack([R[c]["o_p_re"].reshape(2, 32, 64) for c in range(NCORES)], 1)
    p_im = np.stack([R[c]["o_p_im"].reshape(2, 32, 64) for c in range(NCORES)], 1)
    p_pool = np.stack([R[c]["o_p_pool"] for c in range(NCORES)], 1)
    p_conv = np.stack([R[c]["o_p_conv"] for c in range(NCORES)], 1)
    s_re = np.concatenate([R[c]["o_s_re"].reshape(2, 16, 32, 64) for c in range(NCORES)], 1)
    s_im = np.concatenate([R[c]["o_s_im"].reshape(2, 16, 32, 64) for c in range(NCORES)], 1)
    s_v = np.concatenate([R[c]["o_s_v"].reshape(2, 16, 4, 512) for c in range(NCORES)], 1)
    s_pool = np.concatenate([R[c]["o_s_pool"].reshape(2, 16, 15, 512) for c in range(NCORES)], 1)
    s_conv = np.concatenate([R[c]["o_s_conv"].reshape(2, 16, 2, 512) for c in range(NCORES)], 1)
    outs = (y_prompt, y_sample, p_re, p_im, p_pool, p_conv, s_re, s_im, s_v, s_pool, s_conv)
    return tuple(np.ascontiguousarray(o.astype(np.float32)) for o in outs)
```

```python
import math
import numpy as np
from contextlib import ExitStack
import concourse.bass as bass
import concourse.mybir as mybir
from concourse.bass_utils import run_bass_kernel_spmd

F32 = mybir.dt.float32
BF16 = mybir.dt.bfloat16
I32 = mybir.dt.int32
ALU = mybir.AluOpType
AF = mybir.ActivationFunctionType

ENGS = ("pe", "act", "dve", "pool", "sp")
NSLOT = 12
NCORES = 8
D = 1024
SEQ = 2048
NS = 64
T = SEQ + NS
DFF = 2816
EPS = 1e-6
TM = 256
TF = 448
FS = 256
NSL = DFF // FS


class _Op(object):
    __slots__ = ("idx", "eng", "emit", "deps", "signal", "epoch", "semval",
                 "is_dma", "slot", "dval", "prev_dval", "cost", "pos")


class Prog(object):
    def __init__(self, nc, es, n_epochs=6):
        self.nc = nc
        self.ops = []
        self.regions = {}
        self.epoch = 0
        self.n_epochs = n_epochs
        self.sems = {}
        self.cnt = {}
        for e in ENGS:
            for ep in range(n_epochs):
                self.sems[(e, ep)] = es.enter_context(nc.semaphore("s_%s_%d" % (e, ep)))
        self.dsems = {}
        self.dcount = {}
        self.dnext = {}
        for q in ("sp", "pool", "act"):
            self.dnext[q] = 0
            for s in range(NSLOT):
                self.dsems[(q, s)] = es.enter_context(nc.semaphore("d_%s_%d" % (q, s)))
                self.dcount[(q, s)] = 0
        self.nflush = 0
        self.cost = {"pe": 115.0, "act": 450.0, "dve": 430.0, "pool": 600.0, "sp": 100.0}
        self.reorder = True
        self.filler = None
        self.filler_cost = 170.0
        self.nfill = 0

    def next_epoch(self):
        assert not self.ops
        self.epoch = min(self.epoch + 1, self.n_epochs - 1)

    def _add(self, eng, emit, reads, writes, is_dma, cost):
        o = _Op()
        o.idx = len(self.ops)
        o.eng = eng
        o.emit = emit
        o.signal = False
        o.epoch = self.epoch
        o.semval = None
        o.is_dma = is_dma
        o.cost = cost if cost is not None else (3000.0 if is_dma else self.cost[eng])
        deps = set()
        for k in reads:
            r = self.regions.get(k)
            if r is not None and r[0] is not None:
                deps.add(r[0])
        for k in writes:
            r = self.regions.get(k)
            if r is not None:
                if r[0] is not None:
                    deps.add(r[0])
                deps.update(r[1])
        for k in reads:
            r = self.regions.get(k)
            if r is None:
                r = [None, []]
                self.regions[k] = r
            r[1].append(o.idx)
        for k in writes:
            self.regions[k] = [o.idx, []]
        deps.discard(o.idx)
        o.deps = deps
        o.slot = None
        self.ops.append(o)
        return o

    def op(self, eng, emit, reads=(), writes=(), cost=None):
        return self._add(eng, emit, reads, writes, False, cost)

    def dma(self, q, out, in_, reads=(), writes=(), cost=None, **kw):
        def emit(e, out=out, in_=in_, kw=kw):
            return e.dma_start(out=out, in_=in_, **kw)
        return self._add(q, emit, reads, writes, True, cost)

    def _schedule(self):
        ops = self.ops
        n = len(ops)
        succs = [[] for _ in range(n)]
        indeg = [0] * n
        for o in ops:
            for d in o.deps:
                succs[d].append(o.idx)
            indeg[o.idx] = len(o.deps)
        lastd = {}
        dchain = {}
        for o in ops:
            if o.is_dma:
                if o.eng in lastd:
                    dchain[o.idx] = lastd[o.eng]
                lastd[o.eng] = o.idx
        prio = [0.0] * n
        for i in range(n - 1, -1, -1):
            m = 0.0
            for s_ in succs[i]:
                if prio[s_] > m:
                    m = prio[s_]
            prio[i] = ops[i].cost + m
        order = {e: [] for e in ENGS}
        if not self.reorder:
            for o in ops:
                order[o.eng].append(o)
            return order
        ready = {e: [] for e in ENGS}
        ready_t = [0.0] * n
        fin = [0.0] * n
        issued = [False] * n
        free_at = {e: 0.0 for e in ENGS}
        for o in ops:
            if indeg[o.idx] == 0:
                ready[o.eng].append(o.idx)
        remaining = n
        HOP = 150.0
        while remaining:
            best = None
            for e in ENGS:
                rl = ready[e]
                if not rl:
                    continue
                fa = free_at[e]
                cb = None
                for i in rl:
                    o = ops[i]
                    if o.is_dma and i in dchain and not issued[dchain[i]]:
                        continue
                    st = ready_t[i] if ready_t[i] > fa else fa
                    key = (st, -prio[i], i)
                    if cb is None or key < cb:
                        cb = key
                if cb is not None and (best is None or cb < best[0]):
                    best = (cb, e)
            assert best is not None, "scheduler deadlock"
            (st, _, i), e = best
            o = ops[i]
            ready[e].remove(i)
            issued[i] = True
            if e == "pe" and self.filler is not None and free_at[e] > 0.0:
                gap = st - free_at[e]
                if gap > 1200.0:
                    k = min(int((gap - 500.0) / self.filler_cost), 60)
                    for _ in range(k):
                        f = _Op()
                        f.idx = -1
                        f.eng = "pe"
                        f.emit = self.filler
                        f.is_dma = False
                        f.signal = False
                        order[e].append(f)
                    self.nfill += k
            if o.is_dma:
                free_at[e] = st + 80.0
                fin[i] = st + o.cost
            else:
                free_at[e] = st + o.cost
                fin[i] = st + o.cost
            order[e].append(o)
            remaining -= 1
            for s_ in succs[i]:
                t = fin[i] + (0.0 if (ops[s_].eng == e and e == "pe") else HOP)
                if t > ready_t[s_]:
                    ready_t[s_] = t
                indeg[s_] -= 1
                if indeg[s_] == 0:
                    ready[ops[s_].eng].append(s_)
        self.est_time = max(fin) if n else 0.0
        return order

    def flush(self):
        nc = self.nc
        ops = self.ops
        if not ops:
            return
        per_eng = self._schedule()
        for e in ENGS:
            for p_, o in enumerate(per_eng[e]):
                o.pos = p_
            if e != "pe":
                assert all(o.idx >= 0 for o in per_eng[e])
        for e in ("sp", "pool", "act"):
            for o in per_eng[e]:
                if o.is_dma:
                    s = self.dnext[e]
                    self.dnext[e] = (s + 1) % NSLOT
                    o.slot = s
                    o.prev_dval = self.dcount[(e, s)]
                    self.dcount[(e, s)] += 16
                    o.dval = self.dcount[(e, s)]
        red = []
        for o in ops:
            comp = {}
            dmas = []
            for d in o.deps:
                p = ops[d]
                if p.is_dma:
                    dmas.append(d)
                else:
                    if p.eng == "pe" and o.eng == "pe" and not o.is_dma:
                        continue
                    if p.eng not in comp or ops[comp[p.eng]].pos < p.pos:
                        comp[p.eng] = d
            red.append((comp, dmas))
            for d in comp.values():
                ops[d].signal = True
        cnt = self.cnt
        for e in ENGS:
            for o in per_eng[e]:
                if o.idx < 0 or o.is_dma or not o.signal:
                    continue
                key = (o.eng, o.epoch)
                cnt[key] = cnt.get(key, 0) + 1
                o.semval = cnt[key]
        sems = self.sems
        dsems = self.dsems
        n_ep = self.n_epochs
        dcount = self.dcount

        def emit_engine(e, eng_name):
            waited = {}
            dwaited = {}
            for o in per_eng[eng_name]:
                if o.idx < 0:
                    o.emit(e)
                    continue
                comp, dmas = red[o.idx]
                for pe_name, d in comp.items():
                    p = ops[d]
                    done = False
                    for ep in range(p.epoch, n_ep):
                        w = waited.get((pe_name, ep), 0)
                        if ep == p.epoch and w >= p.semval:
                            done = True
                        if ep > p.epoch and w > 0:
                            done = True
                    if done:
                        continue
                    e.wait_ge(sems[(pe_name, p.epoch)], p.semval)
                    waited[(pe_name, p.epoch)] = p.semval
                for d in dmas:
                    p = ops[d]
                    k = (p.eng, p.slot)
                    if dwaited.get(k, 0) >= p.dval:
                        continue
                    e.wait_ge(dsems[k], p.dval)
                    dwaited[k] = p.dval
                if o.is_dma:
                    k = (o.eng, o.slot)
                    if o.prev_dval > 0 and dwaited.get(k, 0) < o.prev_dval:
                        e.wait_ge(dsems[k], o.prev_dval)
                        dwaited[k] = o.prev_dval
                    inst = o.emit(e)
                    inst.then_inc(dsems[k], 16)
                else:
                    inst = o.emit(e)
                    if o.signal:
                        inst.then_inc(sems[(o.eng, o.epoch)], 1)
            if eng_name in ("sp", "pool", "act"):
                for s in range(NSLOT):
                    k = (eng_name, s)
                    if dcount[k] > 0 and dwaited.get(k, 0) < dcount[k]:
                        e.wait_ge(dsems[k], dcount[k])

        with nc.Block() as block:
            @block.tensor
            def _(e):
                emit_engine(e, "pe")

            @block.scalar
            def _(e):
                emit_engine(e, "act")

            @block.vector
            def _(e):
                emit_engine(e, "dve")

            @block.gpsimd
            def _(e):
                emit_engine(e, "pool")

            @block.sync
            def _(e):
                emit_engine(e, "sp")
        self.ops = []
        self.regions = {}
        self.nflush += 1


def build_nc(debug=False):
    nc = bass.Bass("TRN2", target_bir_lowering=False)
    try:
        nc.allow_low_precision("bf16 matmul operands with fp32 accumulation by design")
    except Exception:
        pass

    def din(name, shape):
        return nc.dram_tensor(name, list(shape), F32, kind="ExternalInput").ap()

    def dout(name, shape):
        return nc.dram_tensor(name, list(shape), F32, kind="ExternalOutput").ap()

    x_p = din("x_p", (SEQ, D))
    x_s = din("x_s", (NS, D))
    st_re = din("st_re", (2, 16, 2048))
    st_im = din("st_im", (2, 16, 2048))
    st_pool = din("st_pool", (2, 16, 15, 512))
    st_conv = din("st_conv", (2, 16, 2, 512))
    norm_mix = din("norm_mix", (4, D))
    norm_ffn = din("norm_ffn", (4, D))
    norm_final = din("norm_final", (1, D))
    w_in_even = din("w_in_even", (2, D, 1536))
    w_out_even = din("w_out_even", (2, D, D))
    lam_re = din("s5_lambda_re", (2, 32, 64))
    lam_im = din("s5_lambda_im", (2, 32, 64))
    log_dt = din("s5_log_dt", (2, 32))
    b_re = din("s5_b_re", (2, 32, 64, 16))
    b_im = din("s5_b_im", (2, 32, 64, 16))
    c_re = din("s5_c_re", (2, 32, 16, 64))
    c_im = din("s5_c_im", (2, 32, 16, 64))
    s5_d = din("s5_d", (2, 512))
    glu_w = din("s5_glu_w", (2, 512, 512))
    glu_b = din("s5_glu_b", (2, 512))
    sgu_norm = din("sgu_norm", (2, 512))
    sgu_w = din("sgu_w", (2, 8, 128, 128))
    sgu_b = din("sgu_b", (2, 8, 128))
    w_in_odd = din("w_in_odd", (2, D, 2048))
    w_out_odd = din("w_out_odd", (2, D, D))
    pool_w = din("pool_w", (2, 4, 128, 128))
    pool_scale = din("pool_scale", (2, 512))
    conv_w = din("conv_w", (2, 3, 512))
    conv_b = din("conv_b", (2, 512))
    ffn_g = din("ffn_w_gate", (4, D, DFF))
    ffn_u = din("ffn_w_up", (4, D, DFF))
    ffn_d = din("ffn_w_down", (4, DFF, D))

    y_p = dout("y_p", (SEQ, D))
    y_s = dout("y_s", (NS, D))
    o_p_re = dout("o_p_re", (2, 16, 128))
    o_p_im = dout("o_p_im", (2, 16, 128))
    o_p_pool = dout("o_p_pool", (2, 15, 512))
    o_p_conv = dout("o_p_conv", (2, 2, 512))
    o_s_re = dout("o_s_re", (2, 16, 2048))
    o_s_im = dout("o_s_im", (2, 16, 2048))
    o_s_v = dout("o_s_v", (2, NS, 512))
    o_s_pool = dout("o_s_pool", (2, 240, 512))
    o_s_conv = dout("o_s_conv", (2, 32, 512))
    dbg = dout("dbg", (128, 4096)) if debug else None

    es = ExitStack()
    with es:
        es.enter_context(nc.allow_non_contiguous_dma(reason="small strided parameter loads"))
        P = Prog(nc, es)

        _uid = [0]

        def sb(stk, name, shape, dt=F32):
            _uid[0] += 1
            return stk.enter_context(nc.sbuf_tensor("%s_u%d" % (name, _uid[0]), list(shape), dt))

        PS = es.enter_context(nc.psum_tensor("PS", [128, 8, 512], F32))
        ps_rr = [0]

        def psget(n=1):
            b = ps_rr[0]
            if b + n > 8:
                b = 0
            ps_rr[0] = (b + n) % 8
            return b

        def pk(b, n=1):
            return [("ps", b + i) for i in range(n)]

        hres = sb(es, "hres", [128, 8, T])
        ident = sb(es, "ident", [128, 128])
        identb = sb(es, "identb", [128, 128], BF16)
        onesb = sb(es, "onesb", [128, 128], BF16)
        iot = sb(es, "iot", [128, 128])
        iop = sb(es, "iop", [128, 1])
        pstage = sb(es, "pstage", [128, 128])
        pvec = sb(es, "pvec", [128, 120])
        gmix = pvec[:, 0:32].rearrange("p (l k) -> p l k", l=4)
        gffn = pvec[:, 32:64].rearrange("p (l k) -> p l k", l=4)
        glub = pvec[:, 64:72].rearrange("p (l k) -> p l k", l=2)
        pscale = pvec[:, 72:80].rearrange("p (l k) -> p l k", l=2)
        cw = pvec[:, 80:104].rearrange("p (l c k) -> p l c k", l=2, c=3)
        cb = pvec[:, 104:112].rearrange("p (l k) -> p l k", l=2)
        dcol = pvec[:, 112:120].rearrange("p (l k) -> p l k", l=2)
        epsc = sb(es, "epsc", [128, 1])
        s5car = sb(es, "s5car", [128, 16, 2])
        xchalo = sb(es, "xchalo", [128, 4, 15])
        zhalo = sb(es, "zhalo", [128, 4, 2])

        def V(e):
            return e

        def load_w(dst, src3, key, nsplit):
            K = dst.shape[1]
            step = K // nsplit
            for i in range(nsplit):
                nb = 128 * step * dst.shape[2] * 4
                P.dma("pool", dst[:, i * step:(i + 1) * step, :],
                      src3.rearrange("(k p) n -> p k n", p=128)[:, i * step:(i + 1) * step, :], writes=[(key, i)],
                      cost=2500.0 + nb / 150.0)

        WinP = sb(es, "WinP", [128, 8, 2048], BF16)
        WoutP = sb(es, "WoutP", [128, 8, D], BF16)
        WsmP = sb(es, "WsmP", [128, 2048], BF16)

        def load_mixer_weights(layer):
            i = layer // 2
            if layer % 2 == 0:
                load_w(WinP[:, :, 0:1536], w_in_even[i], "Win", 4)
                load_w(WoutP, w_out_even[i], "Wout", 2)
                load_w(WsmP[:, :].rearrange("p (k n) -> p k n", k=4), glu_w[i], "Wsm", 1)
            else:
                load_w(WinP, w_in_odd[i], "Win", 4)
                load_w(WoutP, w_out_odd[i], "Wout", 2)
                P.dma("pool", WsmP[:, 0:512].rearrange("p (g d) -> p g d", g=4), pool_w[i].rearrange("g c d -> c g d"),
                      writes=["Wsm"])

        load_mixer_weights(0)
        P.op("pool", lambda e: e.iota(iot[:], pattern=[[1, 128]], base=0, channel_multiplier=0,
                                      allow_small_or_imprecise_dtypes=True), writes=["iot"])
        P.op("pool", lambda e: e.iota(iop[:], pattern=[[1, 1]], base=0, channel_multiplier=1,
                                      allow_small_or_imprecise_dtypes=True), writes=["iop"])
        P.op("dve", lambda e: e.tensor_scalar(out=ident[:], in0=iot[:], scalar1=iop[:, 0:1], scalar2=None,
                                              op0=ALU.is_equal), reads=["iot", "iop"], writes=["ident"])
        P.op("dve", lambda e: e.tensor_copy(out=identb[:], in_=ident[:]), reads=["ident"], writes=["identb"])
        P.op("dve", lambda e: e.memset(onesb[:], 1.0), writes=["onesb"])
        P.op("dve", lambda e: e.memset(epsc[:], EPS), writes=["epsc"])
        P.filler = None; _unused_filler = lambda e: e.matmul(PS[:, 7, 0:128], lhsT=onesb[:], rhs=onesb[:], start=True, stop=True)
        with ExitStack() as st:
            pass
        pst_rows = [(norm_mix.rearrange("l (k p) -> (l k) p", p=128), 32), (norm_ffn.rearrange("l (k p) -> (l k) p", p=128), 32),
                    (glu_b.rearrange("l (k p) -> (l k) p", p=128), 8), (pool_scale.rearrange("l (k p) -> (l k) p", p=128), 8),
                    (conv_w.rearrange("l c (k p) -> (l c k) p", p=128), 24), (conv_b.rearrange("l (k p) -> (l k) p", p=128), 8),
                    (s5_d.rearrange("l (k p) -> (l k) p", p=128), 8)]
        r0 = 0
        for j_, (src_, nr_) in enumerate(pst_rows):
            P.dma("sp", pstage[r0:r0 + nr_, :], src_, writes=[("pstage", j_)])
            r0 += nr_
        P.op("pe", lambda e: e.transpose(PS[:, 5, 0:120], pstage[0:120, :], ident[0:120, 0:120]),
             reads=[("pstage", j_) for j_ in range(7)] + ["ident"], writes=[("ps", 5)])
        P.op("act", lambda e: e.activation(out=pvec[:, :], in_=PS[:, 5, 0:120], func=AF.Copy), reads=[("ps", 5)],
             writes=["gmix", "gffn", "glub", "pscale", "cw", "cb", "dcol"])

        def load_x(st):
            xt = [sb(st, "xt%d" % i, [128, D]) for i in range(2)]
            nsub = SEQ // 128 + 1
            for si in range(nsub):
                n = 128 if si < SEQ // 128 else NS
                src = x_p[si * 128:(si + 1) * 128, :] if si < SEQ // 128 else x_s[:, :]
                xb = xt[si % 2]
                xk = "xt%d" % (si % 2)
                P.dma("sp", xb[0:n, :], src, writes=[xk])
                for half in range(2):
                    b = psget()
                    for q in range(4):
                        k = half * 4 + q
                        P.op("pe", lambda e, b=b, q=q, k=k, xb=xb, n=n: e.transpose(
                            PS[:, b, q * 128:q * 128 + n], xb[0:n, k * 128:(k + 1) * 128], ident[0:n, 0:n]),
                            reads=[xk, "ident"], writes=pk(b))
                    eng = "act" if half == 0 else "dve"
                    if eng == "act":
                        P.op("act", lambda e, b=b, half=half, si=si, n=n: e.activation(
                            out=hres[:, half * 4:half * 4 + 4, si * 128:si * 128 + n],
                            in_=PS[:, b, :].rearrange("p (q t) -> p q t", q=4)[:, :, 0:n], func=AF.Copy),
                            reads=pk(b), writes=[("h", si, half)])
                    else:
                        P.op("dve", lambda e, b=b, half=half, si=si, n=n: e.tensor_copy(
                            out=hres[:, half * 4:half * 4 + 4, si * 128:si * 128 + n],
                            in_=PS[:, b, :].rearrange("p (q t) -> p q t", q=4)[:, :, 0:n]),
                            reads=pk(b), writes=[("h", si, half)])

        mtiles = [(i * TM, TM, False) for i in range(SEQ // TM)] + [(SEQ, NS, True)]
        ftiles = [(0, 448, False), (448, 448, False), (896, 448, False), (1344, 448, False), (1792, 320, False)]

        def hkeys(t0, n):
            ks = []
            a = (t0 // TM) * TM
            while a < t0 + n:
                ks.append(("hres", a))
                a += TM
            return ks

        def s5_setup(stk, i, XB, YC, BD, TC, TS, R4, A4, bmask):
            with ExitStack() as st:
                def t16(name):
                    return sb(st, "s5_" + name, [128, 16])
                LRI = sb(st, "s5_LRI", [128, 32])
                LR = LRI[:, 0:16]
                LI = LRI[:, 16:32]
                LDT, DT, Z, MAG, ANG = [t16(n_) for n_ in ("LDT", "DT", "Z", "MAG", "ANG")]
                SN, CS, ta, tb, tc_, td = [t16(n_) for n_ in ("SN", "CS", "ta", "tb", "tc", "td")]
                FR, FI = t16("FR"), t16("FI")
                AR = [t16("AR%d" % k) for k in range(5)]
                AI = [t16("AI%d" % k) for k in range(5)]
                BR = sb(st, "s5_BR", [128, 16, 32]); BI = sb(st, "s5_BI", [128, 16, 32])
                BBr = sb(st, "s5_BBr", [128, 16, 32]); BBi = sb(st, "s5_BBi", [128, 16, 32])
                CTr = sb(st, "s5_CTr", [128, 16, 32]); CTi = sb(st, "s5_CTi", [128, 16, 32])
                Yr = sb(st, "s5_Yr", [128, 16, 32]); Yi = sb(st, "s5_Yi", [128, 16, 32])
                W1 = sb(st, "s5_W1", [128, 16, 32]); W2 = sb(st, "s5_W2", [128, 16, 32])
                W3 = sb(st, "s5_W3", [128, 16, 32]); W4 = sb(st, "s5_W4", [128, 16, 32])
                Xr_ = sb(st, "s5_Xr", [128, 16, 32]); Xi_ = sb(st, "s5_Xi", [128, 16, 32])
                CNr = sb(st, "s5_CNr", [128, 4, 128]); CNi = sb(st, "s5_CNi", [128, 4, 128])
                cnt = [0]

                def dv(fn, reads, writes, eng="dve"):
                    P.op(eng, fn, reads=reads, writes=writes)

                def tt(out, a, b, op, r, w, eng="dve"):
                    dv(lambda e: e.tensor_tensor(out=out, in0=a, in1=b, op=op), r, w, eng=eng)

                def ts(out, a, s1, op0, r, w, s2=None, op1=None):
                    if op1 is None:
                        dv(lambda e: e.tensor_scalar(out=out, in0=a, scalar1=s1, scalar2=None, op0=op0), r, w)
                    else:
                        dv(lambda e: e.tensor_scalar(out=out, in0=a, scalar1=s1, scalar2=s2, op0=op0, op1=op1), r, w)

                lst = sb(st, "s5_lst", [32, 128])
                P.dma("sp", lst[0:16, :], lam_re[i].rearrange("(P g) n -> P (g n)", g=2), writes=[("lst", 0)])
                P.dma("sp", lst[16:32, :], lam_im[i].rearrange("(P g) n -> P (g n)", g=2), writes=[("lst", 1)])
                bl_ = psget()
                P.op("pe", lambda e: e.transpose(PS[:, bl_, 0:32], lst[:, :], ident[0:32, 0:32]),
                     reads=[("lst", 0), ("lst", 1), "ident"], writes=pk(bl_))
                P.op("act", lambda e: e.activation(out=LRI[:, :], in_=PS[:, bl_, 0:32], func=AF.Copy), reads=pk(bl_),
                     writes=["LR", "LI"])
                for g2 in range(2):
                    P.dma("sp", LDT[64 * g2:64 * g2 + 64, :],
                          log_dt[i:i + 1, :].rearrange("o (P g) -> o g P", g=2)[:, g2, :].broadcast_to([64, 16]),
                          writes=[("LDT", g2)])
                for tl in (BR, BI, CNr, CNi):
                    dv(lambda e, tl=tl: e.memset(tl[:], 0.0), [], ["z_" + tl.name], eng="pool")
                for (tl, src) in ((BR, b_re), (BI, b_im)):
                    for g2 in range(2):
                        P.dma("sp", tl[64 * g2:64 * g2 + 64, :, 16 * g2:16 * g2 + 16],
                              src[i].rearrange("(P g) n q -> g n P q", g=2)[g2], reads=["z_" + tl.name],
                              writes=[("ld_" + tl.name, g2)])
                for (tl, src) in ((CNr, c_re), (CNi, c_im)):
                    for p4 in range(4):
                        for g2 in range(2):
                            P.dma("sp", tl[32 * p4 + 16 * g2:32 * p4 + 16 * g2 + 16, :, 64 * g2:64 * g2 + 64],
                                  src[i].rearrange("(f a g) p n -> a g p f n", a=4, g=2)[p4, g2],
                                  reads=["z_" + tl.name], writes=[("ld_" + tl.name, p4, g2)])
                for (src, dst) in ((CNr, CTr), (CNi, CTi)):
                    b = psget()
                    for ft in range(4):
                        P.op("pe", lambda e, ft=ft, b=b, src=src: e.transpose(
                            PS[:, b, ft * 128:(ft + 1) * 128], src[:, ft, :], ident[:]),
                            reads=[("ld_" + src.name, a_, b_) for a_ in range(4) for b_ in range(2)] + ["ident"], writes=pk(b))
                    P.op("act", lambda e, b=b, dst=dst: e.activation(
                        out=dst[:].rearrange("p a b -> p (a b)"), in_=PS[:, b, :], func=AF.Copy),
                        reads=pk(b), writes=[dst.name])
                dv(lambda e: e.activation(out=DT[:], in_=LDT[:], func=AF.Exp), [("LDT", 0), ("LDT", 1)], ["DT"], eng="act")
                tt(Z[:], LR[:], DT[:], ALU.mult, ["LR", "DT"], ["Z"])
                ts(MAG[:], Z[:], 1.0 / 120.0, ALU.mult, ["Z"], ["MAG"], 1.0 / 24.0, ALU.add)
                for c in (1.0 / 6.0, 0.5, 1.0, 1.0):
                    tt(MAG[:], MAG[:], Z[:], ALU.mult, ["MAG", "Z"], ["MAG"])
                    ts(MAG[:], MAG[:], float(c), ALU.add, ["MAG"], ["MAG"])
                tt(ANG[:], LI[:], DT[:], ALU.mult, ["LI", "DT"], ["ANG"])
                C1 = 6.28125
                C2 = 2.0 * math.pi - C1
                MAGIC = 12582912.0
                for (shift, dst) in ((0.0, SN), (0.5 * math.pi, CS)):
                    ts(ta[:], ANG[:], 1.0 / (2 * math.pi), ALU.mult, ["ANG"], ["ta"], shift / (2 * math.pi), ALU.add)
                    ts(tb[:], ta[:], MAGIC, ALU.add, ["ta"], ["tb"])
                    ts(tb[:], tb[:], -MAGIC, ALU.add, ["tb"], ["tb"])
                    dv(lambda e: e.scalar_tensor_tensor(out=ta[:], in0=tb[:], scalar=-C1, in1=ANG[:],
                                                        op0=ALU.mult, op1=ALU.add), ["tb", "ANG"], ["ta"])
                    dv(lambda e: e.scalar_tensor_tensor(out=ta[:], in0=tb[:], scalar=-C2, in1=ta[:],
                                                        op0=ALU.mult, op1=ALU.add), ["tb", "ta"], ["ta"])
                    ts(ta[:], ta[:], float(shift), ALU.add, ["ta"], ["ta"], math.pi, ALU.min)
                    ts(ta[:], ta[:], -math.pi, ALU.max, ["ta"], ["ta"])
                    dv(lambda e, dst=dst: e.activation(out=dst[:], in_=ta[:], func=AF.Sin), ["ta"], [dst.name],
                       eng="act")
                tt(AR[1][:], MAG[:], CS[:], ALU.mult, ["MAG", CS.name], ["AR1"])
                tt(AI[1][:], MAG[:], SN[:], ALU.mult, ["MAG", SN.name], ["AI1"])
                dv(lambda e: e.memset(AR[0][:], 1.0), [], ["AR0"])
                dv(lambda e: e.memset(AI[0][:], 0.0), [], ["AI0"])

                def cmul(orr, oi, ar, ai, br, bi, rk, wk, t1=None, t2=None, k1="W1", k2="W2", eng="dve"):
                    tt(t1, ar, br, ALU.mult, rk, [k1], eng)
                    tt(t2, ai, bi, ALU.mult, rk, [k2], eng)
                    tt(orr, t1, t2, ALU.subtract, [k1, k2], [wk + "r"], eng)
                    tt(t1, ar, bi, ALU.mult, rk + [wk + "r"], [k1], eng)
                    tt(t2, ai, br, ALU.mult, rk + [wk + "r"], [k2], eng)
                    tt(oi, t1, t2, ALU.add, [k1, k2], [wk + "i"], eng)

                cmul(AR[2][:], AI[2][:], AR[1][:], AI[1][:], AR[1][:], AI[1][:], ["AR1", "AI1"], "A2", tc_[:], td[:], "tc", "td")
                cmul(AR[3][:], AI[3][:], AR[2][:], AI[2][:], AR[1][:], AI[1][:], ["AR1", "AI1", "A2r", "A2i"], "A3",
                     tc_[:], td[:], "tc", "td")
                cmul(AR[4][:], AI[4][:], AR[2][:], AI[2][:], AR[2][:], AI[2][:], ["A2r", "A2i"], "A4", tc_[:], td[:], "tc", "td")
                akeys = {0: ["AR0", "AI0"], 1: ["AR1", "AI1"], 2: ["A2r", "A2i"], 3: ["A3r", "A3i"], 4: ["A4r", "A4i"]}
                dv(lambda e: e.tensor_copy(out=A4[:, :, 0], in_=AR[4][:]), akeys[4], ["A4"])
                dv(lambda e: e.tensor_copy(out=A4[:, :, 1], in_=AI[4][:]), akeys[4] + ["A4"], ["A4"])
                tt(ta[:], MAG[:], MAG[:], ALU.mult, ["MAG"], ["ta"])
                tt(R4[:], ta[:], ta[:], ALU.mult, ["ta"], ["R4"])
                ts(ta[:], AR[1][:], -1.0, ALU.add, ["AR1"], ["ta"])
                tt(tb[:], LR[:], LR[:], ALU.mult, ["LR"], ["tb"])
                tt(tc_[:], LI[:], LI[:], ALU.mult, ["LI", "A4i"], ["tc"])
                tt(tb[:], tb[:], tc_[:], ALU.add, ["tb", "tc"], ["tb"])
                dv(lambda e: e.reciprocal(out=tb[:], in_=tb[:]), ["tb"], ["tb"])
                tt(tc_[:], ta[:], LR[:], ALU.mult, ["ta", "LR"], ["tc"])
                tt(td[:], AI[1][:], LI[:], ALU.mult, ["AI1", "LI", "A4i"], ["td"])
                tt(tc_[:], tc_[:], td[:], ALU.add, ["tc", "td"], ["tc"])
                tt(FR[:], tc_[:], tb[:], ALU.mult, ["tc", "tb"], ["FR"])
                tt(tc_[:], AI[1][:], LR[:], ALU.mult, ["AI1", "LR", "FR"], ["tc"])
                tt(td[:], ta[:], LI[:], ALU.mult, ["ta", "LI", "FR"], ["td"])
                tt(tc_[:], tc_[:], td[:], ALU.subtract, ["tc", "td"], ["tc"])
                tt(FI[:], tc_[:], tb[:], ALU.mult, ["tc", "tb"], ["FI"])
                dv(lambda e: e.reciprocal(out=ta[:], in_=R4[:]), ["R4", "FI"], ["ta"])
                tt(TC[:, :, 0], AR[4][:], ta[:], ALU.mult, akeys[4] + ["ta"], ["TC"])
                tt(TS[:, :, 0], AI[4][:], ta[:], ALU.mult, akeys[4] + ["ta"], ["TS"])
                m = 1
                NCH = TM // 4
                W5 = sb(st, "s5_W5", [128, 16, 32]); W6 = sb(st, "s5_W6", [128, 16, 32])
                while m < NCH:
                    ur = TC[:, :, m - 1:m].broadcast_to([128, 16, m])
                    ui = TS[:, :, m - 1:m].broadcast_to([128, 16, m])
                    w1 = W5[:, :, 0:m]; w2 = W6[:, :, 0:m]
                    tt(w1, TC[:, :, 0:m], ur, ALU.mult, ["TC", "TS"], ["W5"], "pool")
                    tt(w2, TS[:, :, 0:m], ui, ALU.mult, ["TC", "TS"], ["W6"], "pool")
                    tt(TC[:, :, m:2 * m], w1, w2, ALU.subtract, ["W5", "W6"], ["TC"], "pool")
                    tt(w1, TC[:, :, 0:m], ui, ALU.mult, ["TC", "TS"], ["W5"], "pool")
                    tt(w2, TS[:, :, 0:m], ur, ALU.mult, ["TC", "TS"], ["W6"], "pool")
                    tt(TS[:, :, m:2 * m], w1, w2, ALU.add, ["W5", "W6"], ["TS"], "pool")
                    m *= 2

                def bc(a):
                    return a.unsqueeze(2).broadcast_to([128, 16, 32])

                cmul(BBr[:], BBi[:], BR[:], BI[:], bc(FR[:]), bc(FI[:]), [("ld_" + BR.name, 0), ("ld_" + BR.name, 1), ("ld_" + BI.name, 0), ("ld_" + BI.name, 1), "FR", "FI"],
                     "BB", W1[:], W2[:])
                for k in range(4):
                    if k == 0:
                        srcs = (BBr, BBi)
                        skeys = ["BBr", "BBi"]
                    else:
                        cmul(Xr_[:], Xi_[:], BBr[:], BBi[:], bc(AR[k][:]), bc(AI[k][:]), ["BBr", "BBi"] + akeys[k], "Xq",
                             W3[:], W4[:], "W3", "W4", eng="pool")
                        srcs = (Xr_, Xi_)
                        skeys = ["Xqr", "Xqi"]
                    s = 3 - k
                    for ri in range(2):
                        b = psget()
                        for ft in range(4):
                            P.op("pe", lambda e, ft=ft, b=b, src=srcs[ri]: e.transpose(
                                PS[:, b, ft * 128:(ft + 1) * 128],
                                src[:, 4 * ft:4 * ft + 4, :].rearrange("p a b -> p (a b)"), ident[:]),
                                reads=skeys + ["ident"], writes=pk(b))
                        P.op("act", lambda e, b=b, ri=ri, s=s: e.activation(
                            out=XB[:, :, ri, s, :], in_=PS[:, b, :].rearrange("p (f n) -> p f n", f=4), func=AF.Copy),
                            reads=pk(b), writes=["XB"])
                bBD = psget()
                for k in range(5):
                    if k == 0:
                        dv(lambda e: e.tensor_copy(out=Yr[:], in_=CTr[:]), ["s5_CTr", "XB"], ["Yr"])
                        ts(Yi[:], CTi[:], -1.0, ALU.mult, ["s5_CTi", "XB"], ["Yi"])
                    else:
                        cmul(Yr[:], Yi[:], CTr[:], CTi[:], bc(AR[k][:]), bc(AI[k][:]),
                             ["s5_CTr", "s5_CTi", "BD%d" % (k - 1), "YC"] + akeys[k], "Y", W1[:], W2[:])
                        ts(Yi[:], Yi[:], -1.0, ALU.mult, ["Yi"], ["Yi"])
                        P.op("act", lambda e, k=k: e.activation(out=YC[:, :, 0, k - 1, :], in_=Yr[:], func=AF.Copy),
                             reads=["Yr"], writes=["YC"])
                        P.op("act", lambda e, k=k: e.activation(out=YC[:, :, 1, k - 1, :], in_=Yi[:], func=AF.Copy),
                             reads=["Yi"], writes=["YC"])
                    if k < 4:
                        for ft in range(4):
                            o_ = PS[:, bBD, ft * 128:(ft + 1) * 128]
                            P.op("pe", lambda e, ft=ft, o_=o_: e.matmul(
                                o_, lhsT=BBr[:, 4 * ft:4 * ft + 4, :].rearrange("p a b -> p (a b)"),
                                rhs=Yr[:, 4 * ft:4 * ft + 4, :].rearrange("p a b -> p (a b)"), start=True, stop=False),
                                reads=["BBr", "Yr"], writes=pk(bBD))
                            P.op("pe", lambda e, ft=ft, o_=o_: e.matmul(
                                o_, lhsT=BBi[:, 4 * ft:4 * ft + 4, :].rearrange("p a b -> p (a b)"),
                                rhs=Yi[:, 4 * ft:4 * ft + 4, :].rearrange("p a b -> p (a b)"), start=False, stop=True),
                                reads=["BBi", "Yi"], writes=pk(bBD))
                        for ft in range(4):
                            if k == 0:
                                dv(lambda e, ft=ft: e.tensor_tensor(out=W1[:, 0:4, :].rearrange("p a b -> p (a b)"),
                                                                    in0=PS[:, bBD, ft * 128:(ft + 1) * 128],
                                                                    in1=bmask[:], op=ALU.mult),
                                   pk(bBD) + ["bmask"], ["W1"])
                                dv(lambda e, ft=ft: e.scalar_tensor_tensor(
                                    out=BD[:, ft, 0, :], in0=ident[:], scalar=dcol[:, i, ft:ft + 1],
                                    in1=W1[:, 0:4, :].rearrange("p a b -> p (a b)"), op0=ALU.mult, op1=ALU.add),
                                    ["W1", "ident", "dcol"], ["BD0"])
                            else:
                                dv(lambda e, ft=ft, k=k: e.tensor_tensor(out=BD[:, ft, k, :],
                                                                         in0=PS[:, bBD, ft * 128:(ft + 1) * 128],
                                                                         in1=bmask[:], op=ALU.mult),
                                   pk(bBD) + ["bmask"], ["BD%d" % k])

        def even_mixer(layer):
            i = layer // 2
            P.cost.update({"pe": 115.0, "dve": 430.0, "act": 450.0})
            with ExitStack() as st:
                NCH = TM // 4
                Win = WinP
                Wout = WoutP
                Wglu = WsmP[:, :].rearrange("p (k n) -> p k n", k=4)
                XB = sb(st, "XB", [128, 4, 2, 4, 128], BF16)
                YC = sb(st, "YC", [128, 16, 2, 4, 32], BF16)
                BD = sb(st, "BD", [128, 4, 4, 128], BF16)
                TC = sb(st, "TC", [128, 16, NCH]); TS = sb(st, "TS", [128, 16, NCH])
                R4 = sb(st, "R4", [128, 16]); A4 = sb(st, "A4", [128, 16, 2])
                wT = sb(st, "wT", [128, 8, 128], BF16)
                bbc = sb(st, "bbc", [128, 4, 128])
                gsg = sb(st, "gsg", [128, 512])
                wsc = sb(st, "wsc", [128, 4, 16])
                P.dma("sp", gsg[:], sgu_norm[i:i + 1, :].broadcast_to([128, 512]), writes=["gsg"])
                for h in range(8):
                    P.dma("sp", bbc[64 * (h % 2):64 * (h % 2) + 64, h // 2, :],
                          sgu_b[i, h:h + 1, :].broadcast_to([64, 128]), writes=[("bbc", h)])
                for h in range(8):
                    P.dma("sp", wsc[64 * (h % 2):64 * (h % 2) + 64, h // 2, :].rearrange("p (a b) -> p a b", a=4),
                          sgu_w[i, h:h + 1, 0:4, 0:4].broadcast_to([64, 4, 4]), writes=[("wsc", h)])
                P.op("dve", lambda e: e.memset(s5car[:], 0.0), writes=[("s5car", q_) for q_ in range(4)])
                with ExitStack() as st2:
                    trilT = sb(st2, "trilT", [128, 128])
                    bmask = sb(st2, "bmask", [128, 128])
                    j32 = sb(st2, "bm_j32", [128, 128])
                    S4 = sb(st2, "bm_S4", [128, 128])
                    P.op("dve", lambda e: e.tensor_scalar(out=trilT[:], in0=iot[:], scalar1=iop[:, 0:1], scalar2=None,
                                                          op0=ALU.is_ge), reads=["iot", "iop"], writes=["trilT"])
                    P.op("pool", lambda e: e.iota(j32[:], pattern=[[1, 4], [0, 32]], base=0, channel_multiplier=0,
                                                  allow_small_or_imprecise_dtypes=True), writes=["j32"])
                    P.op("dve", lambda e: e.tensor_scalar(out=S4[:], in0=j32[:], scalar1=iop[:, 0:1], scalar2=None,
                                                          op0=ALU.is_equal), reads=["j32", "iop"], writes=["S4"])
                    P.op("pe", lambda e: e.matmul(PS[:, 6, 0:128], lhsT=S4[0:4, :], rhs=S4[0:4, :], start=True, stop=True),
                         reads=["S4"], writes=[("ps", 6)])
                    P.op("dve", lambda e: e.tensor_copy(out=bmask[:], in_=PS[:, 6, 0:128]), reads=[("ps", 6)], writes=["bmask"])
                    wld = [sb(st2, "wld%d" % q, [128, 128]) for q in range(2)]
                    for h in range(8):
                        wl = wld[h % 2]
                        P.dma("sp", wl[:], sgu_w[i, h], writes=["wld%d" % (h % 2)])
                        b = psget()
                        P.op("pe", lambda e, b=b, wl=wl: e.transpose(PS[:, b, 0:128], wl[:], ident[:]),
                             reads=["wld%d" % (h % 2), "ident"], writes=pk(b))
                        P.op("dve", lambda e, b=b, h=h: e.tensor_tensor(out=wT[:, h, :], in0=PS[:, b, 0:128],
                                                                        in1=trilT[:], op=ALU.mult),
                             reads=pk(b) + ["trilT"], writes=["wT"])
                    if layer == 0:
                        load_x(st2)
                    s5_setup(st2, i, XB, YC, BD, TC, TS, R4, A4, bmask)
                    P.flush()
                xnt = sb(st, "xnt", [128, 8, TM], BF16)
                uaL = [sb(st, "ua%d" % q, [128, 4, TM], BF16) for q in range(2)]
                ubL = [sb(st, "ub%d" % q, [128, 4, TM], BF16) for q in range(2)]
                vn = sb(st, "vn", [128, 512])
                vnbL = [sb(st, "vnb%d" % q, [128, 2, 512], BF16) for q in range(2)]
                vjunk = sb(st, "vjunk", [128, 512], BF16)
                vss = sb(st, "vss", [128, 2])
                ymix = sb(st, "ymix", [128, 8, TM], BF16)
                tA = sb(st, "tA", [128, 4, NCH]); tB = sb(st, "tB", [128, 4, NCH])
                Gin = sb(st, "Gin", [128, 4, 2, NCH])
                wtail = [WinP[:, k_, 1536:2048].bitcast(F32).rearrange("p (a c) -> p a c", a=4) for k_ in range(8)]
                tC, tD, tE, tF, tG, tH = wtail[0:6]
                GsL = [sb(st, "Gs0", [128, 4, 2, NCH]),
                       WinP[:, 6:8, 1536:2048].bitcast(F32).rearrange("p k (a c) -> p a k c", a=4)]
                Hf = sb(st, "Hf", [128, 4, 2, NCH + 1])
                Hb = sb(st, "Hb", [128, 4, 2, NCH], BF16)
                sqy = sb(st, "sqy", [128, TM])
                zf = sb(st, "zf", [128, 4, TM])
                zb = sb(st, "zb", [128, 4, TM], BF16)
                sg2 = sb(st, "sg2", [128, TM])
                stmp = sb(st, "stmp", [128, TM])
                vT = sb(st, "vT", [128, 4, NS])
                sacc = sb(st, "sacc", [128, 16, 4])
                h0s = sb(st, "h0s", [16, 1024])
                h0T = sb(st, "h0T", [128, 16, 2, 16])
                hend = sb(st, "hend", [128, 16, 2, 16])
                hoP = sb(st, "hoP", [16, 2, 128])
                def front(ti, t0, n, is_s):
                    nch = n // 4
                    par = ti % 2
                    ua = uaL[par]; ub = ubL[par]; vnb = vnbL[par]
                    hk = ("hres", t0)
                    rmsnorm_tile(st, "m", t0, n, gmix[:, layer, :], xnt, "xnt") if ti == 0 else \
                        rmsnorm_tile_again("m", t0, n, gmix[:, layer, :], xnt, "xnt")
                    for ft in range(4):
                        b = psget()
                        for k in range(8):
                            P.op("pe", lambda e, k=k, ft=ft, b=b: e.matmul(
                                PS[:, b, 0:n], lhsT=Win[:, k, ft * 128:(ft + 1) * 128], rhs=xnt[:, k, 0:n],
                                start=(k == 0), stop=(k == 7)), reads=["Win", "xnt"], writes=pk(b))
                        P.op("act", lambda e, ft=ft, b=b: e.activation(out=ua[:, ft, 0:n], in_=PS[:, b, 0:n],
                                                                       func=AF.Copy),
                             reads=pk(b), writes=[("ua", par, ft)])
                    for ft in range(4):
                        b = psget()
                        for k in range(8):
                            P.op("pe", lambda e, k=k, ft=ft, b=b: e.matmul(
                                PS[:, b, 0:n], lhsT=Win[:, k, 512 + ft * 128:512 + (ft + 1) * 128], rhs=xnt[:, k, 0:n],
                                start=(k == 0), stop=(k == 7)), reads=["Win", "xnt"], writes=pk(b))
                        P.op("act", lambda e, ft=ft, b=b: e.activation(out=ub[:, ft, 0:n], in_=PS[:, b, 0:n],
                                                                       func=AF.Copy),
                             reads=pk(b), writes=[("ub", par, ft)])
                    nsub = (n + 127) // 128
                    for sj in range(nsub):
                        m = min(128, n - sj * 128)
                        b = psget()
                        for k in range(8):
                            P.op("pe", lambda e, k=k, b=b, sj=sj, m=m: e.matmul(
                                PS[0:m, b, :], lhsT=xnt[:, k, sj * 128:sj * 128 + m], rhs=Win[:, k, 1024:1536],
                                start=(k == 0), stop=(k == 7)), reads=["Win", "xnt"], writes=pk(b))
                        P.op("act", lambda e, b=b, sj=sj, m=m: e.activation(
                            out=vjunk[0:m, :], in_=PS[0:m, b, :], func=AF.Square, accum_out=vss[0:m, sj:sj + 1]),
                            reads=pk(b), writes=["vjunk", ("vss", sj)])
                        P.op("act", lambda e, sj=sj, m=m: e.activation(
                            out=vss[0:m, sj:sj + 1], in_=vss[0:m, sj:sj + 1], func=AF.Sqrt, bias=epsc[0:m, 0:1],
                            scale=1.0 / 512.0), reads=[("vss", sj), "epsc"], writes=[("vss", sj)])
                        P.op("dve", lambda e, sj=sj, m=m: e.reciprocal(out=vss[0:m, sj:sj + 1], in_=vss[0:m, sj:sj + 1]),
                             reads=[("vss", sj)], writes=[("vss", sj)])
                        P.op("dve", lambda e, b=b, sj=sj, m=m: e.scalar_tensor_tensor(
                            out=vnb[0:m, sj, :], in0=PS[0:m, b, :], scalar=vss[0:m, sj:sj + 1], in1=gsg[0:m, :],
                            op0=ALU.mult, op1=ALU.mult), reads=pk(b) + [("vss", sj), "gsg"], writes=[("vnb", par, sj)])
                        if is_s:
                            P.op("dve", lambda e, b=b, sj=sj, m=m: e.scalar_tensor_tensor(
                                out=vn[0:m, :], in0=PS[0:m, b, :], scalar=vss[0:m, sj:sj + 1], in1=gsg[0:m, :],
                                op0=ALU.mult, op1=ALU.mult), reads=pk(b) + [("vss", sj), "gsg"], writes=[("vn", 0)])
                    if is_s:
                        P.dma("sp", o_s_v[i], vn[0:NS, :], reads=[("vn", 0)])
                        for ri in range(2):
                            b = psget()
                            for hf in range(2):
                                P.dma("sp", h0s[:, :], (st_re if ri == 0 else st_im)[i][:, hf * 1024:(hf + 1) * 1024],
                                      writes=["h0s"])
                                for q in range(8):
                                    Pp = hf * 8 + q
                                    P.op("pe", lambda e, b=b, Pp=Pp, q=q: e.transpose(
                                        PS[:, b, Pp * 16:(Pp + 1) * 16], h0s[:, q * 128:(q + 1) * 128],
                                        ident[0:16, 0:16]), reads=["h0s", "ident"], writes=pk(b))
                            P.op("dve", lambda e, b=b, ri=ri: e.tensor_copy(
                                out=h0T[:, :, ri, :], in_=PS[:, b, 0:256].rearrange("p (a b) -> p a b", a=16)),
                                reads=pk(b), writes=["h0T"])
                def back(ti, t0, n, is_s):
                    nch = n // 4
                    par = ti % 2
                    ua = uaL[par]; ub = ubL[par]; vnb = vnbL[par]
                    for ft in range(4):
                        if not is_s:
                            P.op("pool", lambda e, ft=ft: e.tensor_copy(out=Hf[:, :, :, 0], in_=s5car[:, 4 * ft:4 * ft + 4, :]),
                                 reads=[("s5car", ft)], writes=["Hf0"])
                        b4 = psget(4)
                        for p4 in range(4):
                            for ri in range(2):
                                for s in range(4):
                                    P.op("pe", lambda e, p4=p4, ri=ri, s=s, ft=ft, b4=b4: e.matmul(
                                        PS[:, b4 + p4, ri * NCH:ri * NCH + nch],
                                        lhsT=XB[32 * p4:32 * p4 + 32, ft, ri, s, :],
                                        rhs=ua[32 * p4:32 * p4 + 32, ft, s:n:4],
                                        start=(s == 0), stop=(s == 3), tile_position=(32 * p4, 0)),
                                        reads=["XB", ("ua", par, ft)], writes=pk(b4, 4), cost=40.0)
                        Xr = PS[:, b4:b4 + 4, 0:nch]
                        Xi = PS[:, b4:b4 + 4, NCH:NCH + nch]
                        if not is_s:
                            Cc = TC[:, 4 * ft:4 * ft + 4, 0:nch]
                            Ss = TS[:, 4 * ft:4 * ft + 4, 0:nch]
                            tAa = tA[:, :, 0:nch]; tBb = tB[:, :, 0:nch]
                            GinR = Gin[:, :, 0, 0:nch]; GinI = Gin[:, :, 1, 0:nch]
                            x4 = pk(b4, 4)
                            gq = ft % 2
                            Gsq = GsL[gq]

                            def tt(out, a, bb, op, r, w, eng="dve"):
                                P.op(eng, lambda e: e.tensor_tensor(out=out, in0=a, in1=bb, op=op), reads=r, writes=w)
                            tCc = tC[:, :, 0:nch]; tDd = tD[:, :, 0:nch]
                            tt(tAa, Xr, Cc, ALU.mult, x4 + ["TC"], ["tA"])
                            tt(tBb, Xi, Ss, ALU.mult, x4 + ["TS"], ["tB"])
                            tt(tCc, Xi, Cc, ALU.mult, x4 + ["TC"], ["tC"])
                            tt(tDd, Xr, Ss, ALU.mult, x4 + ["TS"], ["tD"])
                            tt(GinR, tAa, tBb, ALU.add, ["tA", "tB"], ["GinR"])
                            tt(GinI, tCc, tDd, ALU.subtract, ["tC", "tD"], ["GinI"])
                            for p4 in range(4):
                                Pp = 4 * ft + p4
                                for ri in range(2):
                                    P.op("dve", lambda e, p4=p4, ri=ri, Pp=Pp, Gsq=Gsq: e.tensor_tensor_scan(
                                        out=Gsq[:, p4, ri, 0:nch], data0=R4[:, Pp:Pp + 1].broadcast_to([128, nch]),
                                        data1=Gin[:, p4, ri, 0:nch], initial=s5car[:, Pp, ri:ri + 1],
                                        op0=ALU.mult, op1=ALU.add),
                                        reads=["GinR" if ri == 0 else "GinI", "R4", ("s5car", ft)],
                                        writes=[("Gs", gq, p4, ri)], cost=350.0)
                            GR = Gsq[:, :, 0, 0:nch]; GI = Gsq[:, :, 1, 0:nch]
                            gk = [("Gs", gq, a_, b_) for a_ in range(4) for b_ in range(2)]
                            tEe = tE[:, :, 0:nch]; tFf = tF[:, :, 0:nch]; tGg = tG[:, :, 0:nch]; tHh = tH[:, :, 0:nch]
                            tt(tEe, GR, Cc, ALU.mult, gk + ["TC"], ["tE"], "pool")
                            tt(tFf, GI, Ss, ALU.mult, gk + ["TS"], ["tF"], "pool")
                            tt(tGg, GR, Ss, ALU.mult, gk + ["TS"], ["tG"], "pool")
                            tt(tHh, GI, Cc, ALU.mult, gk + ["TC"], ["tH"], "pool")
                            tt(Hf[:, :, 0, 1:nch + 1], tEe, tFf, ALU.subtract, ["tE", "tF", "Hf0"], ["HfR"], "pool")
                            tt(Hf[:, :, 1, 1:nch + 1], tGg, tHh, ALU.add, ["tG", "tH", "Hf0"], ["HfI"], "pool")
                            P.op("act", lambda e: e.activation(out=Hb[:, :, :, 0:nch], in_=Hf[:, :, :, 0:nch], func=AF.Copy),
                                 reads=["HfR", "HfI", "Hf0"], writes=["Hb"])
                            P.op("pool", lambda e, ft=ft: e.tensor_copy(out=s5car[:, 4 * ft:4 * ft + 4, :],
                                                                        in_=Hf[:, :, :, nch]),
                                 reads=["HfR", "HfI"] + gk, writes=[("s5car", ft)])
                        else:
                            h0r = h0T[:, 4 * ft:4 * ft + 4, 0, :]; h0i = h0T[:, 4 * ft:4 * ft + 4, 1, :]
                            a4r = A4[:, 4 * ft:4 * ft + 4, 0:1].broadcast_to([128, 4, 16])
                            a4i = A4[:, 4 * ft:4 * ft + 4, 1:2].broadcast_to([128, 4, 16])
                            tAa = tA[:, :, 0:16]; tBb = tB[:, :, 0:16]
                            x4 = pk(b4, 4)

                            def tt(out, a, bb, op, r, w):
                                P.op("dve", lambda e: e.tensor_tensor(out=out, in0=a, in1=bb, op=op), reads=r, writes=w)
                            tt(tAa, h0r, a4r, ALU.mult, ["h0T", "A4"], ["tA"])
                            tt(tBb, h0i, a4i, ALU.mult, ["h0T", "A4"], ["tB"])
                            tt(tAa, tAa, tBb, ALU.subtract, ["tA", "tB"], ["tA"])
                            tt(hend[:, 4 * ft:4 * ft + 4, 0, :], tAa, Xr, ALU.add, ["tA"] + x4, [("hend", ft, 0)])
                            tt(tAa, h0r, a4i, ALU.mult, ["h0T", "A4", ("hend", ft, 0)], ["tA"])
                            tt(tBb, h0i, a4r, ALU.mult, ["h0T", "A4", ("hend", ft, 0)], ["tB"])
                            tt(tAa, tAa, tBb, ALU.add, ["tA", "tB"], ["tA"])
                            tt(hend[:, 4 * ft:4 * ft + 4, 1, :], tAa, Xi, ALU.add, ["tA"] + x4, [("hend", ft, 1)])
                            P.op("act", lambda e, ft=ft: e.activation(out=Hb[:, :, :, 0:16],
                                                                      in_=h0T[:, 4 * ft:4 * ft + 4, :, :], func=AF.Copy),
                                 reads=["h0T"], writes=["Hb"])
                        by = psget()
                        for t in range(4):
                            o_ = PS[:, by, t * NCH:t * NCH + nch]
                            for tau in range(t + 1):
                                P.op("pe", lambda e, t=t, tau=tau, ft=ft, o_=o_: e.matmul(
                                    o_, lhsT=BD[:, ft, tau, :], rhs=ua[:, ft, (t - tau):n:4],
                                    start=(tau == 0), stop=False), reads=[("ua", par, ft)], writes=pk(by), cost=60.0)
                            for p4 in range(4):
                                for ri in range(2):
                                    last = (ri == 1)
                                    P.op("pe", lambda e, t=t, p4=p4, ri=ri, ft=ft, by=by, last=last: e.matmul(
                                        PS[32 * p4:32 * p4 + 32, by, t * NCH:t * NCH + nch],
                                        lhsT=YC[:, 4 * ft + p4, ri, t, :], rhs=Hb[:, p4, ri, 0:nch],
                                        start=False, stop=last, tile_position=(0, 32 * p4)),
                                        reads=["Hb"], writes=pk(by), cost=45.0)
                        yv = PS[:, by, 0:4 * NCH].rearrange("p (t c) -> p c t", t=4)[:, 0:nch, :]
                        sq3 = sqy[:, 0:n].rearrange("p (c t) -> p c t", t=4)
                        z3 = zf[:, ft, 0:n].rearrange("p (c t) -> p c t", t=4)
                        P.op("act", lambda e, yv=yv, sq3=sq3: e.activation(out=sq3, in_=yv, func=AF.Square),
                             reads=pk(by), writes=["sqy"])
                        P.op("dve", lambda e: e.tensor_scalar(out=sqy[:, 0:n], in0=sqy[:, 0:n], scalar1=0.044715,
                                                              scalar2=1.0, op0=ALU.mult, op1=ALU.add),
                             reads=["sqy"], writes=["sqy"])
                        P.op("dve", lambda e, yv=yv, sq3=sq3: e.tensor_tensor(out=sq3, in0=sq3, in1=yv, op=ALU.mult),
                             reads=["sqy"] + pk(by), writes=["sqy"])
                        P.op("act", lambda e: e.activation(out=sqy[:, 0:n], in_=sqy[:, 0:n], func=AF.Sigmoid,
                                                           scale=1.5957691216057308),
                             reads=["sqy"], writes=["sqy"])
                        P.op("dve", lambda e, yv=yv, sq3=sq3, z3=z3: e.tensor_tensor(out=z3, in0=sq3, in1=yv, op=ALU.mult),
                             reads=["sqy"] + pk(by), writes=[("zf", ft)])
                        P.op("act", lambda e, ft=ft: e.activation(out=zb[:, ft, 0:n], in_=zf[:, ft, 0:n], func=AF.Copy),
                             reads=[("zf", ft)], writes=[("zb", ft)])
                    for fo in range(4):
                        b = psget()
                        for fi in range(4):
                            P.op("pe", lambda e, fi=fi, fo=fo, b=b: e.matmul(
                                PS[:, b, 0:n], lhsT=Wglu[:, fi, fo * 128:(fo + 1) * 128], rhs=zb[:, fi, 0:n],
                                start=(fi == 0), stop=(fi == 3)), reads=["Wsm"] + [("zb", q) for q in range(4)],
                                writes=pk(b))
                        P.op("act", lambda e, fo=fo, b=b: e.activation(out=sg2[:, 0:n], in_=PS[:, b, 0:n], func=AF.Sigmoid,
                                                                       bias=glub[:, i, fo:fo + 1], scale=1.0),
                             reads=pk(b) + ["glub"], writes=["sg2"])
                        P.op("dve", lambda e, fo=fo: e.tensor_tensor(out=ymix[:, fo, 0:n], in0=zf[:, fo, 0:n],
                                                                     in1=sg2[:, 0:n], op=ALU.mult),
                             reads=["sg2", ("zf", fo)], writes=["ymix"])
                    if not is_s:
                        for hp in range(4):
                            b = psget()
                            for j in range(n // 128):
                                for h2 in range(2):
                                    h = 2 * hp + h2
                                    P.op("pe", lambda e, b=b, j=j, h2=h2, h=h: e.matmul(
                                        PS[64 * h2:64 * h2 + 64, b, j * 128:(j + 1) * 128],
                                        lhsT=vnb[:, j, h * 64:(h + 1) * 64], rhs=wT[:, h, :],
                                        start=True, stop=True, tile_position=(0, 64 * h2)),
                                        reads=["wT", ("vnb", par, j)], writes=pk(b))
                            P.op("dve", lambda e, b=b, hp=hp: e.tensor_tensor(
                                out=stmp[:, 0:n].rearrange("p (j i) -> p j i", i=128),
                                in0=PS[:, b, 0:n].rearrange("p (j i) -> p j i", i=128),
                                in1=bbc[:, hp, :].unsqueeze(1).broadcast_to([128, n // 128, 128]), op=ALU.add),
                                reads=pk(b) + ["bbc"], writes=["stmp"])
                            P.op("dve", lambda e, hp=hp: e.tensor_tensor(out=ymix[:, 4 + hp, 0:n], in0=stmp[:, 0:n],
                                                                         in1=ub[:, hp, 0:n], op=ALU.mult),
                                 reads=["stmp", ("ub", par, hp)], writes=["ymix"])
                    else:
                        b = psget()
                        for ft in range(4):
                            P.op("pe", lambda e, b=b, ft=ft: e.transpose(PS[:, b, ft * NS:(ft + 1) * NS],
                                                                         vn[0:NS, ft * 128:(ft + 1) * 128],
                                                                         ident[0:NS, 0:NS]),
                                 reads=[("vn", 0), "ident"], writes=pk(b))
                        P.op("dve", lambda e, b=b: e.tensor_copy(out=vT[:], in_=PS[:, b, 0:4 * NS].rearrange("p (f t) -> p f t", f=4)),
                             reads=pk(b), writes=["vT"])
                        for ft in range(4):
                            v3 = vT[:, ft, :].rearrange("p (b j) -> p b j", j=4)
                            for ii in range(4):
                                P.op("dve", lambda e, ft=ft, ii=ii, v3=v3: e.tensor_scalar(
                                    out=sacc[:, :, ii], in0=v3[:, :, 0], scalar1=wsc[:, ft, 4 * ii:4 * ii + 1],
                                    scalar2=bbc[:, ft, ii:ii + 1], op0=ALU.mult, op1=ALU.add),
                                    reads=["vT", "wsc", "bbc"], writes=["sacc"])
                                for jj in range(1, ii + 1):
                                    P.op("dve", lambda e, ft=ft, ii=ii, jj=jj, v3=v3: e.scalar_tensor_tensor(
                                        out=sacc[:, :, ii], in0=v3[:, :, jj], scalar=wsc[:, ft, 4 * ii + jj:4 * ii + jj + 1],
                                        in1=sacc[:, :, ii], op0=ALU.mult, op1=ALU.add),
                                        reads=["vT", "wsc", "sacc"], writes=["sacc"])
                            P.op("dve", lambda e, ft=ft: e.tensor_tensor(
                                out=ymix[:, 4 + ft, 0:NS], in0=sacc[:].rearrange("p b i -> p (b i)"),
                                in1=ub[:, ft, 0:NS], op=ALU.mult), reads=["sacc", ("ub", par, ft)], writes=["ymix"])
                    out_proj_tile(Wout, "Wout", ymix, "ymix", t0, n)
                    if is_s:
                        for ri in range(2):
                            for hf in range(2):
                                for h2 in range(2):
                                    half = hf * 2 + h2
                                    b = psget()
                                    for q in range(4):
                                        Pp = half * 4 + q
                                        P.op("pe", lambda e, b=b, q=q, Pp=Pp, ri=ri: e.transpose(
                                            PS[0:16, b, q * 128:(q + 1) * 128], hend[:, Pp, ri, :], ident[:]),
                                            reads=[("hend", Pp // 4, ri), "ident"], writes=pk(b))
                                    P.op("act", lambda e, b=b, h2=h2: e.activation(
                                        out=h0s[:, h2 * 512:(h2 + 1) * 512], in_=PS[0:16, b, :], func=AF.Copy),
                                        reads=pk(b), writes=["h0s"])
                                P.dma("sp", (o_s_re if ri == 0 else o_s_im)[i][:, hf * 1024:(hf + 1) * 1024], h0s[:, :],
                                      reads=["h0s"])
                    if (not is_s) and t0 + n == SEQ:
                        for ri in range(2):
                            b = psget()
                            P.op("pe", lambda e, b=b, ri=ri: e.transpose(PS[0:16, b, 0:128], s5car[:, :, ri], ident[:]),
                                 reads=[("s5car", q_) for q_ in range(4)] + ["ident"], writes=pk(b))
                            P.op("act", lambda e, b=b, ri=ri: e.activation(out=hoP[:, ri, :], in_=PS[0:16, b, 0:128], func=AF.Copy),
                                 reads=pk(b), writes=["hoP"])
                        P.dma("sp", o_p_re[i], hoP[:, 0, :], reads=["hoP"])
                        P.dma("sp", o_p_im[i], hoP[:, 1, :], reads=["hoP"])
                seq = list(enumerate(mtiles))
                for idx, (ti, (t0, n, is_s)) in enumerate(seq):
                    front(ti, t0, n, is_s)
                    if idx >= 1:
                        pti, (pt0, pn, ps_) = seq[idx - 1]
                        back(pti, pt0, pn, ps_)
                lti, (lt0, ln, ls_) = seq[-1]
                back(lti, lt0, ln, ls_)
                P.flush()

        _norm_scr = {}

        def rmsnorm_tile_again(tag, t0, n, gvec, xn_out, xn_key):
            _rms_ops(tag, t0, n, gvec, xn_out, xn_key, _norm_scr[tag])

        def _rms_ops(tag, t0, n, gvec, xn_out, xn_key, srs, xoff=0):
            sr, sr2 = srs
            hk = hkeys(t0, n)
            sqv = xn_out[:, :, xoff:xoff + n]
            P.op("act", lambda e: e.activation(out=sqv, in_=hres[:, :, t0:t0 + n], func=AF.Square),
                 reads=hk, writes=[xn_key])
            b = psget()
            for k in range(8):
                P.op("pe", lambda e, k=k, b=b: e.matmul(PS[:, b, 0:n], lhsT=onesb[:], rhs=xn_out[:, k, xoff:xoff + n],
                                                        start=(k == 0), stop=(k == 7)),
                     reads=[xn_key, "onesb"], writes=pk(b))
            P.op("act", lambda e, b=b: e.activation(out=sr[:, 0:n], in_=PS[:, b, 0:n], func=AF.Ln,
                                                    bias=epsc[:, 0:1], scale=1.0 / D),
                 reads=pk(b) + ["epsc"], writes=["sr_" + tag])
            P.op("act", lambda e: e.activation(out=sr[:, 0:n], in_=sr[:, 0:n], func=AF.Exp, scale=-0.5),
                 reads=["sr_" + tag], writes=["sr_" + tag])
            for k in range(8):
                P.op("dve", lambda e, k=k: e.scalar_tensor_tensor(
                    out=xn_out[:, k, xoff:xoff + n], in0=hres[:, k, t0:t0 + n], scalar=gvec[:, k:k + 1],
                    in1=sr2[:, 0:n], op0=ALU.mult, op1=ALU.mult),
                    reads=hk + ["sr_" + tag, "gmix", "gffn"], writes=[xn_key])

        def rmsnorm_tile(stk, tag, t0, n, gvec, xn_out, xn_key, xoff=0):
            nmax = TM if tag == "m" else TF
            sr = sb(stk, "sr_" + tag, [128, nmax])
            sr2 = sr
            _norm_scr[tag] = (sr, sr2)
            _rms_ops(tag, t0, n, gvec, xn_out, xn_key, (sr, sr2), xoff=xoff)

        def out_proj_tile(Wout, wkey, ymix, ykey, t0, n):
            hk = hkeys(t0, n)
            for fo in range(8):
                b = psget()
                for k in range(8):
                    P.op("pe", lambda e, k=k, fo=fo, b=b: e.matmul(
                        PS[:, b, 0:n], lhsT=Wout[:, k, fo * 128:(fo + 1) * 128], rhs=ymix[:, k, 0:n],
                        start=(k == 0), stop=(k == 7)), reads=[wkey, ykey], writes=pk(b))
                P.op("dve", lambda e, fo=fo, b=b: e.tensor_tensor(
                    out=hres[:, fo, t0:t0 + n], in0=hres[:, fo, t0:t0 + n], in1=PS[:, b, 0:n], op=ALU.add),
                    reads=pk(b) + hk, writes=hk)

        def odd_mixer(layer):
            i = layer // 2
            P.cost.update({"pe": 115.0, "dve": 430.0, "act": 450.0})
            with ExitStack() as st:
                Win = WinP
                Wout = WoutP
                Wp = WsmP[:, 0:512].rearrange("p (g d) -> p g d", g=4)
                xnt = sb(st, "xnto", [128, 8, TM], BF16)
                XCL = [sb(st, "XC%d" % q, [128, 4, 15 + TM]) for q in range(2)]
                PA = sb(st, "PA", [128, 15 + TM]); PB = sb(st, "PB", [128, 15 + TM])
                diff = sb(st, "diff", [128, 4, TM], BF16)
                xdL = [sb(st, "xd%d" % q, [128, 4, TM]) for q in range(2)]
                bgL = [sb(st, "bg%d" % q, [128, 4, TM]) for q in range(2)]
                ZL = [sb(st, "Z%d" % q, [128, 4, 2 + TM]) for q in range(2)]
                ca = sb(st, "ca", [128, TM])
                ymix = sb(st, "ymixo", [128, 8, TM], BF16)
                invn = sb(st, "invn", [128, 4, 15])
                XCs = sb(st, "XCs", [128, 4, 16, 19])
                PAs = sb(st, "PAs", [128, 16, 19]); PBs = sb(st, "PBs", [128, 16, 19])
                Zs = sb(st, "Zs", [128, 4, 16, 6])
                spl = [sb(st, "spl%d" % q, [128, 512]) for q in range(2)]
                scl = sb(st, "scl", [32, 512])
                otp = sb(st, "otp", [128, 512])
                otc = sb(st, "otc", [32, 512])
                opp = sb(st, "opp", [16, 512])
                opc = sb(st, "opc", [2, 512])
                xct = sb(st, "xct", [128, 128])
                zct = sb(st, "zct", [128, 32])
                P.op("pool", lambda e: e.iota(invn[:], pattern=[[0, 4], [1, 15]], base=1, channel_multiplier=0,
                                              allow_small_or_imprecise_dtypes=True), writes=["invn"])
                for gi in range(4):
                    P.op("dve", lambda e, gi=gi: e.tensor_scalar(out=invn[:, gi, :], in0=invn[:, gi, :],
                                                                 scalar1=float(2 ** (gi + 1)), scalar2=None, op0=ALU.min),
                         reads=["invn"], writes=["invn"])
                P.op("dve", lambda e: e.reciprocal(out=invn[:], in_=invn[:]), reads=["invn"], writes=["invn"])
                P.op("dve", lambda e: e.memset(xchalo[:], 0.0), writes=["xchalo"])
                P.op("dve", lambda e: e.memset(zhalo[:], 0.0), writes=["zhalo"])
                P.op("pool", lambda e: e.memset(PA[:], 0.0), writes=["P0"])
                P.op("pool", lambda e: e.memset(PB[:], 0.0), writes=["P1"])
                P.op("pool", lambda e: e.memset(PAs[:], 0.0), writes=["Ps0"])
                P.op("pool", lambda e: e.memset(PBs[:], 0.0), writes=["Ps1"])
                def front(ti, t0, n, is_s):
                    par = ti % 2
                    XC = XCL[par]; xd = xdL[par]; bg = bgL[par]; Z = ZL[par]
                    if ti == 0:
                        rmsnorm_tile(st, "m", t0, n, gmix[:, layer, :], xnt, "xnt")
                    else:
                        rmsnorm_tile_again("m", t0, n, gmix[:, layer, :], xnt, "xnt")
                    if not is_s:
                        P.op("dve", lambda e: e.tensor_copy(out=XC[:, :, 0:15], in_=xchalo[:]), reads=["xchalo"],
                             writes=[("XChalo", par)])
                        P.op("dve", lambda e: e.tensor_copy(out=Z[:, :, 0:2], in_=zhalo[:]), reads=["zhalo"],
                             writes=[("Zhalo", par)])
                    else:
                        P.dma("sp", spl[0][:], st_pool[i].rearrange("b r c -> (b r) c")[0:128, :], writes=["spl0"])
                        P.dma("sp", spl[1][0:112, :], st_pool[i].rearrange("b r c -> (b r) c")[128:240, :], writes=["spl1"])
                        P.dma("sp", scl[:], st_conv[i].rearrange("b r c -> (b r) c"), writes=["scl"])
                        for ft in range(4):
                            b = psget()
                            P.op("pe", lambda e, b=b, ft=ft: e.transpose(PS[:, b, 0:128], spl[0][:, ft * 128:(ft + 1) * 128], ident[:]),
                                 reads=["spl0", "ident"], writes=pk(b))
                            P.op("pe", lambda e, b=b, ft=ft: e.transpose(PS[:, b, 128:240], spl[1][0:112, ft * 128:(ft + 1) * 128],
                                                                         ident[0:112, 0:112]),
                                 reads=["spl1", "ident"], writes=pk(b))
                            P.op("pe", lambda e, b=b, ft=ft: e.transpose(PS[:, b, 256:288], scl[:, ft * 128:(ft + 1) * 128],
                                                                         ident[0:32, 0:32]),
                                 reads=["scl", "ident"], writes=pk(b))
                            P.op("dve", lambda e, b=b, ft=ft: e.tensor_copy(
                                out=XCs[:, ft, :, 0:15], in_=PS[:, b, 0:240].rearrange("p (b r) -> p b r", r=15)),
                                reads=pk(b), writes=[("XCs", ft)])
                            P.op("dve", lambda e, b=b, ft=ft: e.tensor_copy(
                                out=Zs[:, ft, :, 0:2], in_=PS[:, b, 256:288].rearrange("p (b r) -> p b r", r=2)),
                                reads=pk(b), writes=[("Zs", ft)])
                    for ft in range(4):
                        b = psget()
                        for k in range(8):
                            P.op("pe", lambda e, k=k, ft=ft, b=b: e.matmul(
                                PS[:, b, 0:n], lhsT=Win[:, k, ft * 128:(ft + 1) * 128], rhs=xnt[:, k, 0:n],
                                start=(k == 0), stop=(k == 7)), reads=["Win", "xnt"], writes=pk(b))
                        if not is_s:
                            P.op("act", lambda e, ft=ft, b=b: e.activation(out=XC[:, ft, 15:15 + n], in_=PS[:, b, 0:n], func=AF.Copy),
                                 reads=pk(b), writes=[("XC", par, ft)])
                        else:
                            P.op("act", lambda e, ft=ft, b=b: e.activation(
                                out=XCs[:, ft, :, 15:19], in_=PS[:, b, 0:NS].rearrange("p (b t) -> p b t", t=4), func=AF.Copy),
                                reads=pk(b) + [("XCs", ft)], writes=[("XCs", ft)])
                    for ft in range(4):
                        b = psget()
                        for k in range(8):
                            P.op("pe", lambda e, k=k, ft=ft, b=b: e.matmul(
                                PS[:, b, 0:n], lhsT=Win[:, k, 512 + ft * 128:512 + (ft + 1) * 128], rhs=xnt[:, k, 0:n],
                                start=(k == 0), stop=(k == 7)), reads=["Win", "xnt"], writes=pk(b))
                        P.op("act", lambda e, ft=ft, b=b: e.activation(out=xd[:, ft, 0:n], in_=PS[:, b, 0:n], func=AF.Copy),
                             reads=pk(b), writes=[("xd", par, ft)])
                    for ft in range(4):
                        b = psget()
                        for k in range(8):
                            P.op("pe", lambda e, k=k, ft=ft, b=b: e.matmul(
                                PS[:, b, 0:n], lhsT=Win[:, k, 1024 + ft * 128:1024 + (ft + 1) * 128], rhs=xnt[:, k, 0:n],
                                start=(k == 0), stop=(k == 7)), reads=["Win", "xnt"], writes=pk(b))
                        P.op("act", lambda e, ft=ft, b=b: e.activation(out=bg[:, ft, 0:n], in_=PS[:, b, 0:n], func=AF.Copy),
                             reads=pk(b), writes=[("bg", par, ft)])
                    for ft in range(4):
                        b = psget()
                        for k in range(8):
                            P.op("pe", lambda e, k=k, ft=ft, b=b: e.matmul(
                                PS[:, b, 0:n], lhsT=Win[:, k, 1536 + ft * 128:1536 + (ft + 1) * 128], rhs=xnt[:, k, 0:n],
                                start=(k == 0), stop=(k == 7)), reads=["Win", "xnt"], writes=pk(b))
                        if not is_s:
                            P.op("dve", lambda e, ft=ft, b=b: e.tensor_tensor(out=Z[:, ft, 2:2 + n], in0=PS[:, b, 0:n],
                                                                              in1=xd[:, ft, 0:n], op=ALU.mult),
                                 reads=pk(b) + [("xd", par, ft)], writes=[("Z", par, ft)])
                        else:
                            P.op("dve", lambda e, ft=ft, b=b: e.tensor_tensor(
                                out=Zs[:, ft, :, 2:6], in0=PS[:, b, 0:NS].rearrange("p (b t) -> p b t", t=4),
                                in1=xd[:, ft, 0:NS].rearrange("p (b t) -> p b t", t=4), op=ALU.mult),
                                reads=pk(b) + [("xd", par, ft), ("Zs", ft)], writes=[("Zs", ft)])
                    if not is_s:
                        P.op("dve", lambda e: e.tensor_copy(out=xchalo[:], in_=XC[:, :, n:n + 15]),
                             reads=[("XC", par, q) for q in range(4)] + [(("XChalo", par), par)], writes=["xchalo"])
                        P.op("dve", lambda e: e.tensor_copy(out=zhalo[:], in_=Z[:, :, n:n + 2]),
                             reads=[("Z", par, q) for q in range(4)] + [(("Zhalo", par), par)], writes=["zhalo"])
                def back(ti, t0, n, is_s):
                    par = ti % 2
                    XC = XCL[par]; xd = xdL[par]; bg = bgL[par]; Z = ZL[par]
                    for gi in range(4):
                        w = 2 ** (gi + 1)
                        if not is_s:
                            L = 15 + n
                            src = XC[:, gi, 0:L]
                            bufs = [PA, PB]
                            cur = src
                            ckey = [("XC", par, gi), ("XChalo", par)]
                            d = 1
                            q = 0
                            while d < w:
                                dst = bufs[q % 2]
                                dk = "P%d" % (q % 2)
                                P.op("pool", lambda e, cur=cur, dst=dst, d=d, L=L: e.tensor_tensor(
                                    out=dst[:, d:L], in0=cur[:, d:L], in1=cur[:, 0:L - d], op=ALU.add),
                                    reads=ckey, writes=[dk], cost=800.0)
                                cur = dst[:, 0:L]
                                ckey = [dk]
                                d *= 2
                                q += 1
                            P.op("dve", lambda e, cur=cur, gi=gi, w=w: e.scalar_tensor_tensor(
                                out=diff[:, gi, 0:n], in0=cur[:, 15:15 + n], scalar=1.0 / w, in1=XC[:, gi, 15:15 + n],
                                op0=ALU.mult, op1=ALU.subtract), reads=ckey + [("XC", par, gi)], writes=[("diff", gi)])
                            if t0 == 0:
                                P.op("dve", lambda e, cur=cur, gi=gi: e.tensor_tensor(
                                    out=ca[:, 0:15], in0=cur[:, 15:30], in1=invn[:, gi, :], op=ALU.mult),
                                    reads=ckey + ["invn"], writes=["ca"])
                                P.op("dve", lambda e, gi=gi: e.tensor_tensor(
                                    out=diff[:, gi, 0:15], in0=ca[:, 0:15], in1=XC[:, gi, 15:30], op=ALU.subtract),
                                    reads=["ca", ("XC", par, gi), ("diff", gi)], writes=[("diff", gi)])
                        else:
                            L = 19
                            cur = XCs[:, gi, :, :]
                            ckey = [("XCs", gi)]
                            bufs = [PAs, PBs]
                            d = 1
                            q = 0
                            while d < w:
                                dst = bufs[q % 2]
                                dk = "Ps%d" % (q % 2)
                                P.op("pool", lambda e, cur=cur, dst=dst, d=d: e.tensor_tensor(
                                    out=dst[:, :, d:19], in0=cur[:, :, d:19], in1=cur[:, :, 0:19 - d], op=ALU.add),
                                    reads=ckey, writes=[dk], cost=800.0)
                                cur = dst[:, :, :]
                                ckey = [dk]
                                d *= 2
                                q += 1
                            P.op("dve", lambda e, cur=cur, gi=gi, w=w: e.scalar_tensor_tensor(
                                out=diff[:, gi, 0:NS].rearrange("p (b t) -> p b t", t=4), in0=cur[:, :, 15:19],
                                scalar=1.0 / w, in1=XCs[:, gi, :, 15:19], op0=ALU.mult, op1=ALU.subtract),
                                reads=ckey + [("XCs", gi)], writes=[("diff", gi)])
                        b = psget()
                        P.op("pe", lambda e, gi=gi, b=b: e.matmul(PS[:, b, 0:n], lhsT=Wp[:, gi, :], rhs=diff[:, gi, 0:n],
                                                                  start=True, stop=True),
                             reads=["Wsm", ("diff", gi)], writes=pk(b))
                        P.op("act", lambda e, gi=gi, b=b: e.activation(out=ymix[:, gi, 0:n], in_=PS[:, b, 0:n], func=AF.Copy,
                                                                       scale=pscale[:, i, gi:gi + 1]),
                             reads=pk(b) + ["pscale"], writes=["ymix"])
                    for ft in range(4):
                        if not is_s:
                            z0 = Z[:, ft, 0:n]; z1 = Z[:, ft, 1:n + 1]; z2 = Z[:, ft, 2:n + 2]
                            cav = ca[:, 0:n]
                            bgv = bg[:, ft, 0:n]
                            yv = ymix[:, 4 + ft, 0:n]
                            zk = [("Z", par, ft), ("Zhalo", par)]
                        else:
                            z0 = Zs[:, ft, :, 0:4]; z1 = Zs[:, ft, :, 1:5]; z2 = Zs[:, ft, :, 2:6]
                            cav = ca[:, 0:NS].rearrange("p (b t) -> p b t", t=4)
                            bgv = bg[:, ft, 0:NS].rearrange("p (b t) -> p b t", t=4)
                            yv = ymix[:, 4 + ft, 0:NS].rearrange("p (b t) -> p b t", t=4)
                            zk = [("Zs", ft)]
                        P.op("dve", lambda e, ft=ft, z0=z0, cav=cav: e.tensor_scalar(
                            out=cav, in0=z0, scalar1=cw[:, i, 0, ft:ft + 1], scalar2=cb[:, i, ft:ft + 1],
                            op0=ALU.mult, op1=ALU.add), reads=zk + ["cw", "cb"], writes=["ca"])
                        P.op("dve", lambda e, ft=ft, z1=z1, cav=cav: e.scalar_tensor_tensor(
                            out=cav, in0=z1, scalar=cw[:, i, 1, ft:ft + 1], in1=cav, op0=ALU.mult, op1=ALU.add),
                            reads=zk + ["cw", "ca"], writes=["ca"])
                        P.op("dve", lambda e, ft=ft, z2=z2, cav=cav: e.scalar_tensor_tensor(
                            out=cav, in0=z2, scalar=cw[:, i, 2, ft:ft + 1], in1=cav, op0=ALU.mult, op1=ALU.add),
                            reads=zk + ["cw", "ca"], writes=["ca"])
                        P.op("dve", lambda e, cav=cav, bgv=bgv, yv=yv: e.tensor_tensor(out=yv, in0=cav, in1=bgv, op=ALU.mult),
                             reads=["ca", ("bg", par, ft)], writes=["ymix"])
                    out_proj_tile(Wout, "Wout", ymix, "ymix", t0, n)
                    if (not is_s) and t0 + n == SEQ:
                        for ft in range(4):
                            b = psget()
                            P.op("pe", lambda e, b=b, ft=ft: e.transpose(PS[0:15, b, 0:128], xchalo[:, ft, :], ident[:]),
                                 reads=["xchalo", "ident"], writes=pk(b))
                            P.op("pe", lambda e, b=b, ft=ft: e.transpose(PS[0:2, b, 128:256], zhalo[:, ft, :], ident[:]),
                                 reads=["zhalo", "ident"], writes=pk(b))
                            P.op("act", lambda e, b=b, ft=ft: e.activation(out=opp[0:15, ft * 128:(ft + 1) * 128],
                                                                           in_=PS[0:15, b, 0:128], func=AF.Copy),
                                 reads=pk(b), writes=["opp"])
                            P.op("act", lambda e, b=b, ft=ft: e.activation(out=opc[0:2, ft * 128:(ft + 1) * 128],
                                                                           in_=PS[0:2, b, 128:256], func=AF.Copy),
                                 reads=pk(b), writes=["opc"])
                        P.dma("sp", o_p_pool[i], opp[0:15, :], reads=["opp"])
                        P.dma("sp", o_p_conv[i], opc[0:2, :], reads=["opc"])
                    if is_s:
                        for half in range(2):
                            for ft in range(4):
                                P.op("dve", lambda e, ft=ft, half=half: e.tensor_copy(
                                    out=xct[:, 0:120].rearrange("p (b r) -> p b r", r=15),
                                    in_=XCs[:, ft, half * 8:half * 8 + 8, 4:19]), reads=[("XCs", ft)], writes=["xct"])
                                b = psget()
                                P.op("pe", lambda e, b=b: e.transpose(PS[0:120, b, 0:128], xct[:, 0:120], ident[:]),
                                     reads=["xct", "ident"], writes=pk(b))
                                P.op("act", lambda e, b=b, ft=ft: e.activation(out=otp[0:120, ft * 128:(ft + 1) * 128],
                                                                               in_=PS[0:120, b, 0:128], func=AF.Copy),
                                     reads=pk(b), writes=["otp"])
                            P.dma("sp", o_s_pool[i, half * 120:half * 120 + 120, :], otp[0:120, :], reads=["otp"])
                        for ft in range(4):
                            P.op("dve", lambda e, ft=ft: e.tensor_copy(
                                out=zct[:, 0:32].rearrange("p (b r) -> p b r", r=2), in_=Zs[:, ft, :, 4:6]),
                                reads=[("Zs", ft)], writes=["zct"])
                            b = psget()
                            P.op("pe", lambda e, b=b: e.transpose(PS[0:32, b, 0:128], zct[:, 0:32], ident[:]),
                                 reads=["zct", "ident"], writes=pk(b))
                            P.op("act", lambda e, b=b, ft=ft: e.activation(out=otc[:, ft * 128:(ft + 1) * 128],
                                                                           in_=PS[0:32, b, 0:128], func=AF.Copy),
                                 reads=pk(b), writes=["otc"])
                        P.dma("sp", o_s_conv[i], otc[:, :], reads=["otc"])
                seq = list(enumerate(mtiles))
                for idx, (ti, (t0, n, is_s)) in enumerate(seq):
                    front(ti, t0, n, is_s)
                    if idx >= 1:
                        pti, (pt0, pn, ps_) = seq[idx - 1]
                        back(pti, pt0, pn, ps_)
                lti, (lt0, ln, ls_) = seq[-1]
                back(lti, lt0, ln, ls_)
                P.flush()

        def epilogue(st):
            gfin = WinP[:, 2, :].bitcast(F32)
            P.dma("sp", gfin, norm_final.broadcast_to([128, D]), writes=["gfin"])
            junk = sb(st, "fjunk", [128, 512], BF16)
            ss = sb(st, "fss", [128, 2, 2])
            yo = [WinP[:, q, :].bitcast(F32) for q in range(2)]
            nsub = SEQ // 128 + 1
            for si in range(nsub):
                n = 128 if si < SEQ // 128 else NS
                dst = y_p[si * 128:(si + 1) * 128, :] if si < SEQ // 128 else y_s[:, :]
                par = si % 2
                yb = yo[par]
                yk = "yo%d" % par
                hk = hkeys(si * 128, n)
                bb = psget(2)
                for k in range(8):
                    P.op("pe", lambda e, k=k, bb=bb, si=si, n=n: e.transpose(
                        PS[0:n, bb + k // 4, (k % 4) * 128:(k % 4 + 1) * 128], hres[:, k, si * 128:si * 128 + n], ident[:]),
                        reads=["ident"] + hk, writes=pk(bb, 2), cost=110.0)
                for half in range(2):
                    P.op("act", lambda e, bb=bb, half=half, n=n, par=par: e.activation(
                        out=junk[0:n, :], in_=PS[0:n, bb + half, :], func=AF.Square, accum_out=ss[0:n, par, half:half + 1]),
                        reads=pk(bb, 2), writes=["fjunk", ("fss", par, half)])
                P.op("dve", lambda e, n=n, par=par: e.tensor_tensor(out=ss[0:n, par, 0:1], in0=ss[0:n, par, 0:1],
                                                                     in1=ss[0:n, par, 1:2], op=ALU.add),
                     reads=[("fss", par, 0), ("fss", par, 1)], writes=[("fss", par, 0)], cost=100.0)
                P.op("act", lambda e, n=n, par=par: e.activation(out=ss[0:n, par, 0:1], in_=ss[0:n, par, 0:1], func=AF.Sqrt,
                                                                  bias=epsc[0:n, 0:1], scale=1.0 / D),
                     reads=[("fss", par, 0)], writes=[("fss", par, 0)], cost=250.0)
                P.op("dve", lambda e, n=n, par=par: e.reciprocal(out=ss[0:n, par, 0:1], in_=ss[0:n, par, 0:1]),
                     reads=[("fss", par, 0)], writes=[("fss", par, 0)], cost=100.0)
                for half in range(2):
                    P.op("dve", lambda e, bb=bb, half=half, n=n, yb=yb, par=par: e.scalar_tensor_tensor(
                        out=yb[0:n, half * 512:(half + 1) * 512], in0=PS[0:n, bb + half, :], scalar=ss[0:n, par, 0:1],
                        in1=gfin[0:n, half * 512:(half + 1) * 512], op0=ALU.mult, op1=ALU.mult),
                        reads=pk(bb, 2) + [("fss", par, 0), "gfin"], writes=[yk], cost=750.0)
                P.dma("sp", dst, yb[0:n, :], reads=[yk])

        def ffn(layer):
            widths = [384] * 7 + [128]
            offs = [sum(widths[:j]) for j in range(len(widths))]
            with ExitStack() as st:
                xn = sb(st, "xn_all", [128, 8, T], BF16)
                Wg = [sb(st, "Wg%d" % q, [128, 8, 384], BF16) for q in range(2)]
                Wu = [sb(st, "Wu%d" % q, [128, 8, 384], BF16) for q in range(2)]
                Wd = [sb(st, "Wd%d" % q, [128, 3, D], BF16) for q in range(2)]
                sl = [sb(st, "sl%d" % q, [128, TF]) for q in range(2)]
                hb = [sb(st, "hb%d" % q, [128, 3, TF], BF16) for q in range(2)]

                P.cost.update({"pe": 195.0, "dve": 630.0, "act": 560.0})

                def load_slice(j):
                    q = j % 2
                    w = widths[j]
                    o = offs[j]
                    c = 2500.0 + 128 * 8 * w * 4 / 150.0
                    for kh in range(2):
                        P.dma("pool", Wg[q][:, 4 * kh:4 * kh + 4, 0:w],
                              ffn_g[layer].rearrange("(k p) n -> p k n", p=128)[:, 4 * kh:4 * kh + 4, o:o + w],
                              writes=[("Wg", q, kh)], cost=c / 2)
                    for kh in range(2):
                        P.dma("pool", Wu[q][:, 4 * kh:4 * kh + 4, 0:w],
                              ffn_u[layer].rearrange("(k p) n -> p k n", p=128)[:, 4 * kh:4 * kh + 4, o:o + w],
                              writes=[("Wu", q, kh)], cost=c / 2)
                    P.dma("pool", Wd[q][:, 0:w // 128, :],
                          ffn_d[layer].rearrange("(k p) n -> p k n", p=128)[:, o // 128:(o + w) // 128, :],
                          writes=[("Wd", q)], cost=c)
                load_slice(0)
                load_slice(1)
                if layer + 1 < 4:
                    load_mixer_weights(layer + 1)
                for ti, (t0, n, is_s) in enumerate(ftiles):
                    if ti == 0:
                        rmsnorm_tile(st, "f", t0, n, gffn[:, layer, :], xn, ("xn", t0), xoff=t0)
                    else:
                        _rms_ops("f", t0, n, gffn[:, layer, :], xn, ("xn", t0), _norm_scr["f"], xoff=t0)
                hbi = 0
                for j in range(len(widths)):
                    q = j % 2
                    nhc = widths[j] // 128
                    for (t0, n, is_s) in ftiles:
                        hk = hkeys(t0, n)
                        hbuf = hb[hbi % 2]
                        hkey = "hb%d" % (hbi % 2)
                        hbi += 1
                        for hc in range(nhc):
                            bgt = psget()
                            for k in range(8):
                                P.op("pe", lambda e, k=k, hc=hc, bgt=bgt, q=q, t0=t0, n=n: e.matmul(
                                    PS[:, bgt, 0:n], lhsT=Wg[q][:, k, hc * 128:(hc + 1) * 128], rhs=xn[:, k, t0:t0 + n],
                                    start=(k == 0), stop=(k == 7)), reads=[("Wg", q, k // 4), ("xn", t0)], writes=pk(bgt),
                                    cost=n / 2.35 + 6)
                            but = psget()
                            for k in range(8):
                                P.op("pe", lambda e, k=k, hc=hc, but=but, q=q, t0=t0, n=n: e.matmul(
                                    PS[:, but, 0:n], lhsT=Wu[q][:, k, hc * 128:(hc + 1) * 128], rhs=xn[:, k, t0:t0 + n],
                                    start=(k == 0), stop=(k == 7)), reads=[("Wu", q, k // 4), ("xn", t0)], writes=pk(but),
                                    cost=n / 2.35 + 6)
                            slt = sl[hc % 2]
                            slk = "sl%d" % (hc % 2)
                            P.op("act", lambda e, bgt=bgt, slt=slt, n=n: e.activation(out=slt[:, 0:n], in_=PS[:, bgt, 0:n], func=AF.Silu),
                                 reads=pk(bgt), writes=[slk], cost=(224 + n) / 1.2)
                            P.op("dve", lambda e, but=but, slt=slt, hbuf=hbuf, hc=hc, n=n: e.tensor_tensor(
                                out=hbuf[:, hc, 0:n], in0=slt[:, 0:n], in1=PS[:, but, 0:n], op=ALU.mult),
                                reads=pk(but) + [slk], writes=[(hkey, hc)], cost=(160 + n) / 0.96)
                        for fo in range(8):
                            b = psget()
                            for hc in range(nhc):
                                P.op("pe", lambda e, hc=hc, fo=fo, b=b, q=q, hbuf=hbuf, n=n, nhc=nhc: e.matmul(
                                    PS[:, b, 0:n], lhsT=Wd[q][:, hc, fo * 128:(fo + 1) * 128], rhs=hbuf[:, hc, 0:n],
                                    start=(hc == 0), stop=(hc == nhc - 1)), reads=[("Wd", q), (hkey, hc)], writes=pk(b),
                                    cost=n / 2.35 + 6)
                            P.op("dve", lambda e, fo=fo, b=b, t0=t0, n=n: e.tensor_tensor(
                                out=hres[:, fo, t0:t0 + n], in0=hres[:, fo, t0:t0 + n], in1=PS[:, b, 0:n], op=ALU.add),
                                reads=pk(b) + hk, writes=hk, cost=(160 + n) / 0.96)
                    if j + 2 < len(widths):
                        load_slice(j + 2)
                if layer == 3:
                    epilogue(st)
                P.flush()

        for layer in range(4):
            if layer > 0:
                P.next_epoch()
            if layer % 2 == 0:
                even_mixer(layer)
            else:
                odd_mixer(layer)
            ffn(layer)

    return nc


_NC_CACHE = {}


def kernel(**inputs):
    f = lambda a: np.ascontiguousarray(np.asarray(a, dtype=np.float32))
    inp = {k: f(v) for k, v in inputs.items()}
    if "nc" not in _NC_CACHE:
        _NC_CACHE["nc"] = build_nc()
    nc = _NC_CACHE["nc"]
    shared = {}
    for k in ("norm_mix", "norm_ffn", "w_in_even", "w_out_even", "s5_lambda_re", "s5_lambda_im", "s5_log_dt",
              "s5_b_re", "s5_b_im", "s5_c_re", "s5_c_im", "s5_glu_w", "s5_glu_b", "sgu_norm", "sgu_w", "sgu_b",
              "w_in_odd", "w_out_odd", "pool_w", "pool_scale", "conv_w", "conv_b", "ffn_w_gate", "ffn_w_up",
              "ffn_w_down"):
        shared[k] = inp[k]
    shared["norm_final"] = inp["norm_final"].reshape(1, D)
    shared["s5_d"] = inp["s5_d"].reshape(2, 512)
    in_maps = []
    for c in range(NCORES):
        m = dict(shared)
        m["x_p"] = inp["x_prompt"][c]
        m["x_s"] = np.ascontiguousarray(inp["x_sample"][16 * c:16 * c + 16].reshape(NS, D))
        m["st_re"] = np.ascontiguousarray(inp["state_s5_re"][:, 16 * c:16 * c + 16].reshape(2, 16, 2048))
        m["st_im"] = np.ascontiguousarray(inp["state_s5_im"][:, 16 * c:16 * c + 16].reshape(2, 16, 2048))
        m["st_pool"] = np.ascontiguousarray(inp["state_pool"][:, 16 * c:16 * c + 16])
        m["st_conv"] = np.ascontiguousarray(inp["state_conv"][:, 16 * c:16 * c + 16])
        in_maps.append(m)
    res = run_bass_kernel_spmd(nc, in_maps, core_ids=list(range(NCORES)))
    R = res.results
    y_prompt = np.stack([R[c]["y_p"] for c in range(NCORES)], 0).reshape(8, SEQ, D)
    y_sample = np.concatenate([R[c]["y_s"].reshape(16, 4, D) for c in range(NCORES)], 0)
    p_re = np.stack([R[c]["o_p_re"].reshape(2, 32, 64) for c in range(NCORES)], 1)
    p_im = np.stack([R[c]["o_p_im"].reshape(2, 32, 64) for c in range(NCORES)], 1)
    p_pool = np.stack([R[c]["o_p_pool"] for c in range(NCORES)], 1)
    p_conv = np.stack([R[c]["o_p_conv"] for c in range(NCORES)], 1)
    s_re = np.concatenate([R[c]["o_s_re"].reshape(2, 16, 32, 64) for c in range(NCORES)], 1)
    s_im = np.concatenate([R[c]["o_s_im"].reshape(2, 16, 32, 64) for c in range(NCORES)], 1)
    s_v = np.concatenate([R[c]["o_s_v"].reshape(2, 16, 4, 512) for c in range(NCORES)], 1)
    s_pool = np.concatenate([R[c]["o_s_pool"].reshape(2, 16, 15, 512) for c in range(NCORES)], 1)
    s_conv = np.concatenate([R[c]["o_s_conv"].reshape(2, 16, 2, 512) for c in range(NCORES)], 1)
    outs = (y_prompt, y_sample, p_re, p_im, p_pool, p_conv, s_re, s_im, s_v, s_pool, s_conv)
    return tuple(np.ascontiguousarray(o.astype(np.float32)) for o in outs)
```

```python
import math
import numpy as np
from contextlib import ExitStack
import concourse.bass as bass
import concourse.mybir as mybir
from concourse.bass_utils import run_bass_kernel_spmd

F32 = mybir.dt.float32
BF16 = mybir.dt.bfloat16
I32 = mybir.dt.int32
ALU = mybir.AluOpType
AF = mybir.ActivationFunctionType

ENGS = ("pe", "act", "dve", "pool", "sp")
NSLOT = 12
NCORES = 8
D = 1024
SEQ = 2048
NS = 64
T = SEQ + NS
DFF = 2816
EPS = 1e-6
TM = 256
TF = 448
FS = 256
NSL = DFF // FS


class _Op(object):
    __slots__ = ("idx", "eng", "emit", "deps", "signal", "epoch", "semval",
                 "is_dma", "slot", "dval", "prev_dval", "cost", "pos")


class Prog(object):
    def __init__(self, nc, es, n_epochs=6):
        self.nc = nc
        self.ops = []
        self.regions = {}
        self.epoch = 0
        self.n_epochs = n_epochs
        self.sems = {}
        self.cnt = {}
        for e in ENGS:
            for ep in range(n_epochs):
                self.sems[(e, ep)] = es.enter_context(nc.semaphore("s_%s_%d" % (e, ep)))
        self.dsems = {}
        self.dcount = {}
        self.dnext = {}
        for q in ("sp", "pool", "act"):
            self.dnext[q] = 0
            for s in range(NSLOT):
                self.dsems[(q, s)] = es.enter_context(nc.semaphore("d_%s_%d" % (q, s)))
                self.dcount[(q, s)] = 0
        self.nflush = 0
        self.cost = {"pe": 115.0, "act": 450.0, "dve": 430.0, "pool": 600.0, "sp": 100.0}
        self.reorder = True
        self.filler = None
        self.filler_cost = 170.0
        self.nfill = 0

    def next_epoch(self):
        assert not self.ops
        self.epoch = min(self.epoch + 1, self.n_epochs - 1)

    def _add(self, eng, emit, reads, writes, is_dma, cost):
        o = _Op()
        o.idx = len(self.ops)
        o.eng = eng
        o.emit = emit
        o.signal = False
        o.epoch = self.epoch
        o.semval = None
        o.is_dma = is_dma
        o.cost = cost if cost is not None else (3000.0 if is_dma else self.cost[eng])
        deps = set()
        for k in reads:
            r = self.regions.get(k)
            if r is not None and r[0] is not None:
                deps.add(r[0])
        for k in writes:
            r = self.regions.get(k)
            if r is not None:
                if r[0] is not None:
                    deps.add(r[0])
                deps.update(r[1])
        for k in reads:
            r = self.regions.get(k)
            if r is None:
                r = [None, []]
                self.regions[k] = r
            r[1].append(o.idx)
        for k in writes:
            self.regions[k] = [o.idx, []]
        deps.discard(o.idx)
        o.deps = deps
        o.slot = None
        self.ops.append(o)
        return o

    def op(self, eng, emit, reads=(), writes=(), cost=None):
        return self._add(eng, emit, reads, writes, False, cost)

    def dma(self, q, out, in_, reads=(), writes=(), cost=None, **kw):
        def emit(e, out=out, in_=in_, kw=kw):
            return e.dma_start(out=out, in_=in_, **kw)
        return self._add(q, emit, reads, writes, True, cost)

    def _schedule(self):
        ops = self.ops
        n = len(ops)
        succs = [[] for _ in range(n)]
        indeg = [0] * n
        for o in ops:
            for d in o.deps:
                succs[d].append(o.idx)
            indeg[o.idx] = len(o.deps)
        lastd = {}
        dchain = {}
        for o in ops:
            if o.is_dma:
                if o.eng in lastd:
                    dchain[o.idx] = lastd[o.eng]
                lastd[o.eng] = o.idx
        prio = [0.0] * n
        for i in range(n - 1, -1, -1):
            m = 0.0
            for s_ in succs[i]:
                if prio[s_] > m:
                    m = prio[s_]
            prio[i] = ops[i].cost + m
        order = {e: [] for e in ENGS}
        if not self.reorder:
            for o in ops:
                order[o.eng].append(o)
            return order
        ready = {e: [] for e in ENGS}
        ready_t = [0.0] * n
        fin = [0.0] * n
        issued = [False] * n
        free_at = {e: 0.0 for e in ENGS}
        for o in ops:
            if indeg[o.idx] == 0:
                ready[o.eng].append(o.idx)
        remaining = n
        HOP = 150.0
        while remaining:
            best = None
            for e in ENGS:
                rl = ready[e]
                if not rl:
                    continue
                fa = free_at[e]
                cb = None
                for i in rl:
                    o = ops[i]
                    if o.is_dma and i in dchain and not issued[dchain[i]]:
                        continue
                    st = ready_t[i] if ready_t[i] > fa else fa
                    key = (st, -prio[i], i)
                    if cb is None or key < cb:
                        cb = key
                if cb is not None and (best is None or cb < best[0]):
                    best = (cb, e)
            assert best is not None, "scheduler deadlock"
            (st, _, i), e = best
            o = ops[i]
            ready[e].remove(i)
            issued[i] = True
            if e == "pe" and self.filler is not None and free_at[e] > 0.0:
                gap = st - free_at[e]
                if gap > 1200.0:
                    k = min(int((gap - 500.0) / self.filler_cost), 60)
                    for _ in range(k):
                        f = _Op()
                        f.idx = -1
                        f.eng = "pe"
                        f.emit = self.filler
                        f.is_dma = False
                        f.signal = False
                        order[e].append(f)
                    self.nfill += k
            if o.is_dma:
                free_at[e] = st + 80.0
                fin[i] = st + o.cost
            else:
                free_at[e] = st + o.cost
                fin[i] = st + o.cost
            order[e].append(o)
            remaining -= 1
            for s_ in succs[i]:
                t = fin[i] + (0.0 if (ops[s_].eng == e and e == "pe") else HOP)
                if t > ready_t[s_]:
                    ready_t[s_] = t
                indeg[s_] -= 1
                if indeg[s_] == 0:
                    ready[ops[s_].eng].append(s_)
        self.est_time = max(fin) if n else 0.0
        return order

    def flush(self):
        nc = self.nc
        ops = self.ops
        if not ops:
            return
        per_eng = self._schedule()
        for e in ENGS:
            for p_, o in enumerate(per_eng[e]):
                o.pos = p_
            if e != "pe":
                assert all(o.idx >= 0 for o in per_eng[e])
        for e in ("sp", "pool", "act"):
            for o in per_eng[e]:
                if o.is_dma:
                    s = self.dnext[e]
                    self.dnext[e] = (s + 1) % NSLOT
                    o.slot = s
                    o.prev_dval = self.dcount[(e, s)]
                    self.dcount[(e, s)] += 16
                    o.dval = self.dcount[(e, s)]
        red = []
        for o in ops:
            comp = {}
            dmas = []
            for d in o.deps:
                p = ops[d]
                if p.is_dma:
                    dmas.append(d)
                else:
                    if p.eng == "pe" and o.eng == "pe" and not o.is_dma:
                        continue
                    if p.eng not in comp or ops[comp[p.eng]].pos < p.pos:
                        comp[p.eng] = d
            red.append((comp, dmas))
            for d in comp.values():
                ops[d].signal = True
        cnt = self.cnt
        for e in ENGS:
            for o in per_eng[e]:
                if o.idx < 0 or o.is_dma or not o.signal:
                    continue
                key = (o.eng, o.epoch)
                cnt[key] = cnt.get(key, 0) + 1
                o.semval = cnt[key]
        sems = self.sems
        dsems = self.dsems
        n_ep = self.n_epochs
        dcount = self.dcount

        def emit_engine(e, eng_name):
            waited = {}
            dwaited = {}
            for o in per_eng[eng_name]:
                if o.idx < 0:
                    o.emit(e)
                    continue
                comp, dmas = red[o.idx]
                for pe_name, d in comp.items():
                    p = ops[d]
                    done = False
                    for ep in range(p.epoch, n_ep):
                        w = waited.get((pe_name, ep), 0)
                        if ep == p.epoch and w >= p.semval:
                            done = True
                        if ep > p.epoch and w > 0:
                            done = True
                    if done:
                        continue
                    e.wait_ge(sems[(pe_name, p.epoch)], p.semval)
                    waited[(pe_name, p.epoch)] = p.semval
                for d in dmas:
                    p = ops[d]
                    k = (p.eng, p.slot)
                    if dwaited.get(k, 0) >= p.dval:
                        continue
                    e.wait_ge(dsems[k], p.dval)
                    dwaited[k] = p.dval
                if o.is_dma:
                    k = (o.eng, o.slot)
                    if o.prev_dval > 0 and dwaited.get(k, 0) < o.prev_dval:
                        e.wait_ge(dsems[k], o.prev_dval)
                        dwaited[k] = o.prev_dval
                    inst = o.emit(e)
                    inst.then_inc(dsems[k], 16)
                else:
                    inst = o.emit(e)
                    if o.signal:
                        inst.then_inc(sems[(o.eng, o.epoch)], 1)
            if eng_name in ("sp", "pool", "act"):
                for s in range(NSLOT):
                    k = (eng_name, s)
                    if dcount[k] > 0 and dwaited.get(k, 0) < dcount[k]:
                        e.wait_ge(dsems[k], dcount[k])

        with nc.Block() as block:
            @block.tensor
            def _(e):
                emit_engine(e, "pe")

            @block.scalar
            def _(e):
                emit_engine(e, "act")

            @block.vector
            def _(e):
                emit_engine(e, "dve")

            @block.gpsimd
            def _(e):
                emit_engine(e, "pool")

            @block.sync
            def _(e):
                emit_engine(e, "sp")
        self.ops = []
        self.regions = {}
        self.nflush += 1


def build_nc(debug=False):
    nc = bass.Bass("TRN2", target_bir_lowering=False)
    try:
        nc.allow_low_precision("bf16 matmul operands with fp32 accumulation by design")
    except Exception:
        pass

    def din(name, shape):
        return nc.dram_tensor(name, list(shape), F32, kind="ExternalInput").ap()

    def dout(name, shape):
        return nc.dram_tensor(name, list(shape), F32, kind="ExternalOutput").ap()

    x_p = din("x_p", (SEQ, D))
    x_s = din("x_s", (NS, D))
    st_re = din("st_re", (2, 16, 2048))
    st_im = din("st_im", (2, 16, 2048))
    st_pool = din("st_pool", (2, 16, 15, 512))
    st_conv = din("st_conv", (2, 16, 2, 512))
    norm_mix = din("norm_mix", (4, D))
    norm_ffn = din("norm_ffn", (4, D))
    norm_final = din("norm_final", (1, D))
    w_in_even = din("w_in_even", (2, D, 1536))
    w_out_even = din("w_out_even", (2, D, D))
    lam_re = din("s5_lambda_re", (2, 32, 64))
    lam_im = din("s5_lambda_im", (2, 32, 64))
    log_dt = din("s5_log_dt", (2, 32))
    b_re = din("s5_b_re", (2, 32, 64, 16))
    b_im = din("s5_b_im", (2, 32, 64, 16))
    c_re = din("s5_c_re", (2, 32, 16, 64))
    c_im = din("s5_c_im", (2, 32, 16, 64))
    s5_d = din("s5_d", (2, 512))
    glu_w = din("s5_glu_w", (2, 512, 512))
    glu_b = din("s5_glu_b", (2, 512))
    sgu_norm = din("sgu_norm", (2, 512))
    sgu_w = din("sgu_w", (2, 8, 128, 128))
    sgu_b = din("sgu_b", (2, 8, 128))
    w_in_odd = din("w_in_odd", (2, D, 2048))
    w_out_odd = din("w_out_odd", (2, D, D))
    pool_w = din("pool_w", (2, 4, 128, 128))
    pool_scale = din("pool_scale", (2, 512))
    conv_w = din("conv_w", (2, 3, 512))
    conv_b = din("conv_b", (2, 512))
    ffn_g = din("ffn_w_gate", (4, D, DFF))
    ffn_u = din("ffn_w_up", (4, D, DFF))
    ffn_d = din("ffn_w_down", (4, DFF, D))

    y_p = dout("y_p", (SEQ, D))
    y_s = dout("y_s", (NS, D))
    o_p_re = dout("o_p_re", (2, 16, 128))
    o_p_im = dout("o_p_im", (2, 16, 128))
    o_p_pool = dout("o_p_pool", (2, 15, 512))
    o_p_conv = dout("o_p_conv", (2, 2, 512))
    o_s_re = dout("o_s_re", (2, 16, 2048))
    o_s_im = dout("o_s_im", (2, 16, 2048))
    o_s_v = dout("o_s_v", (2, NS, 512))
    o_s_pool = dout("o_s_pool", (2, 240, 512))
    o_s_conv = dout("o_s_conv", (2, 32, 512))
    dbg = dout("dbg", (128, 4096)) if debug else None

    es = ExitStack()
    with es:
        es.enter_context(nc.allow_non_contiguous_dma(reason="small strided parameter loads"))
        P = Prog(nc, es)

        _uid = [0]

        def sb(stk, name, shape, dt=F32):
            _uid[0] += 1
            return stk.enter_context(nc.sbuf_tensor("%s_u%d" % (name, _uid[0]), list(shape), dt))

        PS = es.enter_context(nc.psum_tensor("PS", [128, 8, 512], F32))
        ps_rr = [0]

        def psget(n=1):
            b = ps_rr[0]
            if b + n > 8:
                b = 0
            ps_rr[0] = (b + n) % 8
            return b

        def pk(b, n=1):
            return [("ps", b + i) for i in range(n)]

        hres = sb(es, "hres", [128, 8, T])
        ident = sb(es, "ident", [128, 128])
        identb = sb(es, "identb", [128, 128], BF16)
        onesb = sb(es, "onesb", [128, 128], BF16)
        iot = sb(es, "iot", [128, 128])
        iop = sb(es, "iop", [128, 1])
        pstage = sb(es, "pstage", [128, 128])
        pvec = sb(es, "pvec", [128, 120])
        gmix = pvec[:, 0:32].rearrange("p (l k) -> p l k", l=4)
        gffn = pvec[:, 32:64].rearrange("p (l k) -> p l k", l=4)
        glub = pvec[:, 64:72].rearrange("p (l k) -> p l k", l=2)
        pscale = pvec[:, 72:80].rearrange("p (l k) -> p l k", l=2)
        cw = pvec[:, 80:104].rearrange("p (l c k) -> p l c k", l=2, c=3)
        cb = pvec[:, 104:112].rearrange("p (l k) -> p l k", l=2)
        dcol = pvec[:, 112:120].rearrange("p (l k) -> p l k", l=2)
        epsc = sb(es, "epsc", [128, 1])
        s5car = sb(es, "s5car", [128, 16, 2])
        xchalo = sb(es, "xchalo", [128, 4, 15])
        zhalo = sb(es, "zhalo", [128, 4, 2])

        def V(e):
            return e

        def load_w(dst, src3, key, nsplit):
            K = dst.shape[1]
            step = K // nsplit
            for i in range(nsplit):
                nb = 128 * step * dst.shape[2] * 4
                P.dma("pool", dst[:, i * step:(i + 1) * step, :],
                      src3.rearrange("(k p) n -> p k n", p=128)[:, i * step:(i + 1) * step, :], writes=[(key, i)],
                      cost=2500.0 + nb / 150.0)

        WinP = sb(es, "WinP", [128, 8, 2048], BF16)
        WoutP = sb(es, "WoutP", [128, 8, D], BF16)
        WsmP = sb(es, "WsmP", [128, 2048], BF16)

        def load_mixer_weights(layer):
            i = layer // 2
            if layer % 2 == 0:
                load_w(WinP[:, :, 0:1536], w_in_even[i], "Win", 4)
                load_w(WoutP, w_out_even[i], "Wout", 2)
                load_w(WsmP[:, :].rearrange("p (k n) -> p k n", k=4), glu_w[i], "Wsm", 1)
            else:
                load_w(WinP, w_in_odd[i], "Win", 4)
                load_w(WoutP, w_out_odd[i], "Wout", 2)
                P.dma("pool", WsmP[:, 0:512].rearrange("p (g d) -> p g d", g=4), pool_w[i].rearrange("g c d -> c g d"),
                      writes=["Wsm"])

        load_mixer_weights(0)
        P.op("pool", lambda e: e.iota(iot[:], pattern=[[1, 128]], base=0, channel_multiplier=0,
                                      allow_small_or_imprecise_dtypes=True), writes=["iot"])
        P.op("pool", lambda e: e.iota(iop[:], pattern=[[1, 1]], base=0, channel_multiplier=1,
                                      allow_small_or_imprecise_dtypes=True), writes=["iop"])
        P.op("dve", lambda e: e.tensor_scalar(out=ident[:], in0=iot[:], scalar1=iop[:, 0:1], scalar2=None,
                                              op0=ALU.is_equal), reads=["iot", "iop"], writes=["ident"])
        P.op("dve", lambda e: e.tensor_copy(out=identb[:], in_=ident[:]), reads=["ident"], writes=["identb"])
        P.op("dve", lambda e: e.memset(onesb[:], 1.0), writes=["onesb"])
        P.op("dve", lambda e: e.memset(epsc[:], EPS), writes=["epsc"])
        P.filler = None; _unused_filler = lambda e: e.matmul(PS[:, 7, 0:128], lhsT=onesb[:], rhs=onesb[:], start=True, stop=True)
        with ExitStack() as st:
            pass
        pst_rows = [(norm_mix.rearrange("l (k p) -> (l k) p", p=128), 32), (norm_ffn.rearrange("l (k p) -> (l k) p", p=128), 32),
                    (glu_b.rearrange("l (k p) -> (l k) p", p=128), 8), (pool_scale.rearrange("l (k p) -> (l k) p", p=128), 8),
                    (conv_w.rearrange("l c (k p) -> (l c k) p", p=128), 24), (conv_b.rearrange("l (k p) -> (l k) p", p=128), 8),
                    (s5_d.rearrange("l (k p) -> (l k) p", p=128), 8)]
        r0 = 0
        for j_, (src_, nr_) in enumerate(pst_rows):
            P.dma("sp", pstage[r0:r0 + nr_, :], src_, writes=[("pstage", j_)])
            r0 += nr_
        P.op("pe", lambda e: e.transpose(PS[:, 5, 0:120], pstage[0:120, :], ident[0:120, 0:120]),
             reads=[("pstage", j_) for j_ in range(7)] + ["ident"], writes=[("ps", 5)])
        P.op("act", lambda e: e.activation(out=pvec[:, :], in_=PS[:, 5, 0:120], func=AF.Copy), reads=[("ps", 5)],
             writes=["gmix", "gffn", "glub", "pscale", "cw", "cb", "dcol"])

        def load_x(st):
            xt = [sb(st, "xt%d" % i, [128, D]) for i in range(2)]
            nsub = SEQ // 128 + 1
            for si in range(nsub):
                n = 128 if si < SEQ // 128 else NS
                src = x_p[si * 128:(si + 1) * 128, :] if si < SEQ // 128 else x_s[:, :]
                xb = xt[si % 2]
                xk = "xt%d" % (si % 2)
                P.dma("sp", xb[0:n, :], src, writes=[xk])
                for half in range(2):
                    b = psget()
                    for q in range(4):
                        k = half * 4 + q
                        P.op("pe", lambda e, b=b, q=q, k=k, xb=xb, n=n: e.transpose(
                            PS[:, b, q * 128:q * 128 + n], xb[0:n, k * 128:(k + 1) * 128], ident[0:n, 0:n]),
                            reads=[xk, "ident"], writes=pk(b))
                    eng = "act" if half == 0 else "dve"
                    if eng == "act":
                        P.op("act", lambda e, b=b, half=half, si=si, n=n: e.activation(
                            out=hres[:, half * 4:half * 4 + 4, si * 128:si * 128 + n],
                            in_=PS[:, b, :].rearrange("p (q t) -> p q t", q=4)[:, :, 0:n], func=AF.Copy),
                            reads=pk(b), writes=[("h", si, half)])
                    else:
                        P.op("dve", lambda e, b=b, half=half, si=si, n=n: e.tensor_copy(
                            out=hres[:, half * 4:half * 4 + 4, si * 128:si * 128 + n],
                            in_=PS[:, b, :].rearrange("p (q t) -> p q t", q=4)[:, :, 0:n]),
                            reads=pk(b), writes=[("h", si, half)])

        mtiles = [(i * TM, TM, False) for i in range(SEQ // TM)] + [(SEQ, NS, True)]
        ftiles = [(0, 448, False), (448, 448, False), (896, 448, False), (1344, 448, False), (1792, 320, False)]

        def hkeys(t0, n):
            ks = []
            a = (t0 // TM) * TM
            while a < t0 + n:
                ks.append(("hres", a))
                a += TM
            return ks

        def s5_setup(stk, i, XB, YC, BD, TC, TS, R4, A4, bmask):
            with ExitStack() as st:
                def t16(name):
                    return sb(st, "s5_" + name, [128, 16])
                LRI = sb(st, "s5_LRI", [128, 32])
                LR = LRI[:, 0:16]
                LI = LRI[:, 16:32]
                LDT, DT, Z, MAG, ANG = [t16(n_) for n_ in ("LDT", "DT", "Z", "MAG", "ANG")]
                SN, CS, ta, tb, tc_, td = [t16(n_) for n_ in ("SN", "CS", "ta", "tb", "tc", "td")]
                FR, FI = t16("FR"), t16("FI")
                AR = [t16("AR%d" % k) for k in range(5)]
                AI = [t16("AI%d" % k) for k in range(5)]
                BR = sb(st, "s5_BR", [128, 16, 32]); BI = sb(st, "s5_BI", [128, 16, 32])
                BBr = sb(st, "s5_BBr", [128, 16, 32]); BBi = sb(st, "s5_BBi", [128, 16, 32])
                CTr = sb(st, "s5_CTr", [128, 16, 32]); CTi = sb(st, "s5_CTi", [128, 16, 32])
                Yr = sb(st, "s5_Yr", [128, 16, 32]); Yi = sb(st, "s5_Yi", [128, 16, 32])
                W1 = sb(st, "s5_W1", [128, 16, 32]); W2 = sb(st, "s5_W2", [128, 16, 32])
                W3 = sb(st, "s5_W3", [128, 16, 32]); W4 = sb(st, "s5_W4", [128, 16, 32])
                Xr_ = sb(st, "s5_Xr", [128, 16, 32]); Xi_ = sb(st, "s5_Xi", [128, 16, 32])
                CNr = sb(st, "s5_CNr", [128, 4, 128]); CNi = sb(st, "s5_CNi", [128, 4, 128])
                cnt = [0]

                def dv(fn, reads, writes, eng="dve"):
                    P.op(eng, fn, reads=reads, writes=writes)

                def tt(out, a, b, op, r, w, eng="dve"):
                    dv(lambda e: e.tensor_tensor(out=out, in0=a, in1=b, op=op), r, w, eng=eng)

                def ts(out, a, s1, op0, r, w, s2=None, op1=None):
                    if op1 is None:
                        dv(lambda e: e.tensor_scalar(out=out, in0=a, scalar1=s1, scalar2=None, op0=op0), r, w)
                    else:
                        dv(lambda e: e.tensor_scalar(out=out, in0=a, scalar1=s1, scalar2=s2, op0=op0, op1=op1), r, w)

                lst = sb(st, "s5_lst", [32, 128])
                P.dma("sp", lst[0:16, :], lam_re[i].rearrange("(P g) n -> P (g n)", g=2), writes=[("lst", 0)])
                P.dma("sp", lst[16:32, :], lam_im[i].rearrange("(P g) n -> P (g n)", g=2), writes=[("lst", 1)])
                bl_ = psget()
                P.op("pe", lambda e: e.transpose(PS[:, bl_, 0:32], lst[:, :], ident[0:32, 0:32]),
                     reads=[("lst", 0), ("lst", 1), "ident"], writes=pk(bl_))
                P.op("act", lambda e: e.activation(out=LRI[:, :], in_=PS[:, bl_, 0:32], func=AF.Copy), reads=pk(bl_),
                     writes=["LR", "LI"])
                for g2 in range(2):
                    P.dma("sp", LDT[64 * g2:64 * g2 + 64, :],
                          log_dt[i:i + 1, :].rearrange("o (P g) -> o g P", g=2)[:, g2, :].broadcast_to([64, 16]),
                          writes=[("LDT", g2)])
                for tl in (BR, BI, CNr, CNi):
                    dv(lambda e, tl=tl: e.memset(tl[:], 0.0), [], ["z_" + tl.name], eng="pool")
                for (tl, src) in ((BR, b_re), (BI, b_im)):
                    for g2 in range(2):
                        P.dma("sp", tl[64 * g2:64 * g2 + 64, :, 16 * g2:16 * g2 + 16],
                              src[i].rearrange("(P g) n q -> g n P q", g=2)[g2], reads=["z_" + tl.name],
                              writes=[("ld_" + tl.name, g2)])
                for (tl, src) in ((CNr, c_re), (CNi, c_im)):
                    for p4 in range(4):
                        for g2 in range(2):
                            P.dma("sp", tl[32 * p4 + 16 * g2:32 * p4 + 16 * g2 + 16, :, 64 * g2:64 * g2 + 64],
                                  src[i].rearrange("(f a g) p n -> a g p f n", a=4, g=2)[p4, g2],
                                  reads=["z_" + tl.name], writes=[("ld_" + tl.name, p4, g2)])
                for (src, dst) in ((CNr, CTr), (CNi, CTi)):
                    b = psget()
                    for ft in range(4):
                        P.op("pe", lambda e, ft=ft, b=b, src=src: e.transpose(
                            PS[:, b, ft * 128:(ft + 1) * 128], src[:, ft, :], ident[:]),
                            reads=[("ld_" + src.name, a_, b_) for a_ in range(4) for b_ in range(2)] + ["ident"], writes=pk(b))
                    P.op("act", lambda e, b=b, dst=dst: e.activation(
                        out=dst[:].rearrange("p a b -> p (a b)"), in_=PS[:, b, :], func=AF.Copy),
                        reads=pk(b), writes=[dst.name])
                dv(lambda e: e.activation(out=DT[:], in_=LDT[:], func=AF.Exp), [("LDT", 0), ("LDT", 1)], ["DT"], eng="act")
                tt(Z[:], LR[:], DT[:], ALU.mult, ["LR", "DT"], ["Z"])
                ts(MAG[:], Z[:], 1.0 / 120.0, ALU.mult, ["Z"], ["MAG"], 1.0 / 24.0, ALU.add)
                for c in (1.0 / 6.0, 0.5, 1.0, 1.0):
                    tt(MAG[:], MAG[:], Z[:], ALU.mult, ["MAG", "Z"], ["MAG"])
                    ts(MAG[:], MAG[:], float(c), ALU.add, ["MAG"], ["MAG"])
                tt(ANG[:], LI[:], DT[:], ALU.mult, ["LI", "DT"], ["ANG"])
                C1 = 6.28125
                C2 = 2.0 * math.pi - C1
                MAGIC = 12582912.0
                for (shift, dst) in ((0.0, SN), (0.5 * math.pi, CS)):
                    ts(ta[:], ANG[:], 1.0 / (2 * math.pi), ALU.mult, ["ANG"], ["ta"], shift / (2 * math.pi), ALU.add)
                    ts(tb[:], ta[:], MAGIC, ALU.add, ["ta"], ["tb"])
                    ts(tb[:], tb[:], -MAGIC, ALU.add, ["tb"], ["tb"])
                    dv(lambda e: e.scalar_tensor_tensor(out=ta[:], in0=tb[:], scalar=-C1, in1=ANG[:],
                                                        op0=ALU.mult, op1=ALU.add), ["tb", "ANG"], ["ta"])
                    dv(lambda e: e.scalar_tensor_tensor(out=ta[:], in0=tb[:], scalar=-C2, in1=ta[:],
                                                        op0=ALU.mult, op1=ALU.add), ["tb", "ta"], ["ta"])
                    ts(ta[:], ta[:], float(shift), ALU.add, ["ta"], ["ta"], math.pi, ALU.min)
                    ts(ta[:], ta[:], -math.pi, ALU.max, ["ta"], ["ta"])
                    dv(lambda e, dst=dst: e.activation(out=dst[:], in_=ta[:], func=AF.Sin), ["ta"], [dst.name],
                       eng="act")
                tt(AR[1][:], MAG[:], CS[:], ALU.mult, ["MAG", CS.name], ["AR1"])
                tt(AI[1][:], MAG[:], SN[:], ALU.mult, ["MAG", SN.name], ["AI1"])
                dv(lambda e: e.memset(AR[0][:], 1.0), [], ["AR0"])
                dv(lambda e: e.memset(AI[0][:], 0.0), [], ["AI0"])

                def cmul(orr, oi, ar, ai, br, bi, rk, wk, t1=None, t2=None, k1="W1", k2="W2", eng="dve"):
                    tt(t1, ar, br, ALU.mult, rk, [k1], eng)
                    tt(t2, ai, bi, ALU.mult, rk, [k2], eng)
                    tt(orr, t1, t2, ALU.subtract, [k1, k2], [wk + "r"], eng)
                    tt(t1, ar, bi, ALU.mult, rk + [wk + "r"], [k1], eng)
                    tt(t2, ai, br, ALU.mult, rk + [wk + "r"], [k2], eng)
                    tt(oi, t1, t2, ALU.add, [k1, k2], [wk + "i"], eng)

                cmul(AR[2][:], AI[2][:], AR[1][:], AI[1][:], AR[1][:], AI[1][:], ["AR1", "AI1"], "A2", tc_[:], td[:], "tc", "td")
                cmul(AR[3][:], AI[3][:], AR[2][:], AI[2][:], AR[1][:], AI[1][:], ["AR1", "AI1", "A2r", "A2i"], "A3",
                     tc_[:], td[:], "tc", "td")
                cmul(AR[4][:], AI[4][:], AR[2][:], AI[2][:], AR[2][:], AI[2][:], ["A2r", "A2i"], "A4", tc_[:], td[:], "tc", "td")
                akeys = {0: ["AR0", "AI0"], 1: ["AR1", "AI1"], 2: ["A2r", "A2i"], 3: ["A3r", "A3i"], 4: ["A4r", "A4i"]}
                dv(lambda e: e.tensor_copy(out=A4[:, :, 0], in_=AR[4][:]), akeys[4], ["A4"])
                dv(lambda e: e.tensor_copy(out=A4[:, :, 1], in_=AI[4][:]), akeys[4] + ["A4"], ["A4"])
                tt(ta[:], MAG[:], MAG[:], ALU.mult, ["MAG"], ["ta"])
                tt(R4[:], ta[:], ta[:], ALU.mult, ["ta"], ["R4"])
                ts(ta[:], AR[1][:], -1.0, ALU.add, ["AR1"], ["ta"])
                tt(tb[:], LR[:], LR[:], ALU.mult, ["LR"], ["tb"])
                tt(tc_[:], LI[:], LI[:], ALU.mult, ["LI", "A4i"], ["tc"])
                tt(tb[:], tb[:], tc_[:], ALU.add, ["tb", "tc"], ["tb"])
                dv(lambda e: e.reciprocal(out=tb[:], in_=tb[:]), ["tb"], ["tb"])
                tt(tc_[:], ta[:], LR[:], ALU.mult, ["ta", "LR"], ["tc"])
                tt(td[:], AI[1][:], LI[:], ALU.mult, ["AI1", "LI", "A4i"], ["td"])
                tt(tc_[:], tc_[:], td[:], ALU.add, ["tc", "td"], ["tc"])
                tt(FR[:], tc_[:], tb[:], ALU.mult, ["tc", "tb"], ["FR"])
                tt(tc_[:], AI[1][:], LR[:], ALU.mult, ["AI1", "LR", "FR"], ["tc"])
                tt(td[:], ta[:], LI[:], ALU.mult, ["ta", "LI", "FR"], ["td"])
                tt(tc_[:], tc_[:], td[:], ALU.subtract, ["tc", "td"], ["tc"])
                tt(FI[:], tc_[:], tb[:], ALU.mult, ["tc", "tb"], ["FI"])
                dv(lambda e: e.reciprocal(out=ta[:], in_=R4[:]), ["R4", "FI"], ["ta"])
                tt(TC[:, :, 0], AR[4][:], ta[:], ALU.mult, akeys[4] + ["ta"], ["TC"])
                tt(TS[:, :, 0], AI[4][:], ta[:], ALU.mult, akeys[4] + ["ta"], ["TS"])
                m = 1
                NCH = TM // 4
                W5 = sb(st, "s5_W5", [128, 16, 32]); W6 = sb(st, "s5_W6", [128, 16, 32])
                while m < NCH:
                    ur = TC[:, :, m - 1:m].broadcast_to([128, 16, m])
                    ui = TS[:, :, m - 1:m].broadcast_to([128, 16, m])
                    w1 = W5[:, :, 0:m]; w2 = W6[:, :, 0:m]
                    tt(w1, TC[:, :, 0:m], ur, ALU.mult, ["TC", "TS"], ["W5"], "pool")
                    tt(w2, TS[:, :, 0:m], ui, ALU.mult, ["TC", "TS"], ["W6"], "pool")
                    tt(TC[:, :, m:2 * m], w1, w2, ALU.subtract, ["W5", "W6"], ["TC"], "pool")
                    tt(w1, TC[:, :, 0:m], ui, ALU.mult, ["TC", "TS"], ["W5"], "pool")
                    tt(w2, TS[:, :, 0:m], ur, ALU.mult, ["TC", "TS"], ["W6"], "pool")
                    tt(TS[:, :, m:2 * m], w1, w2, ALU.add, ["W5", "W6"], ["TS"], "pool")
                    m *= 2

                def bc(a):
                    return a.unsqueeze(2).broadcast_to([128, 16, 32])

                cmul(BBr[:], BBi[:], BR[:], BI[:], bc(FR[:]), bc(FI[:]), [("ld_" + BR.name, 0), ("ld_" + BR.name, 1), ("ld_" + BI.name, 0), ("ld_" + BI.name, 1), "FR", "FI"],
                     "BB", W1[:], W2[:])
                for k in range(4):
                    if k == 0:
                        srcs = (BBr, BBi)
                        skeys = ["BBr", "BBi"]
                    else:
                        cmul(Xr_[:], Xi_[:], BBr[:], BBi[:], bc(AR[k][:]), bc(AI[k][:]), ["BBr", "BBi"] + akeys[k], "Xq",
                             W3[:], W4[:], "W3", "W4", eng="pool")
                        srcs = (Xr_, Xi_)
                        skeys = ["Xqr", "Xqi"]
                    s = 3 - k
                    for ri in range(2):
                        b = psget()
                        for ft in range(4):
                            P.op("pe", lambda e, ft=ft, b=b, src=srcs[ri]: e.transpose(
                                PS[:, b, ft * 128:(ft + 1) * 128],
                                src[:, 4 * ft:4 * ft + 4, :].rearrange("p a b -> p (a b)"), ident[:]),
                                reads=skeys + ["ident"], writes=pk(b))
                        P.op("act", lambda e, b=b, ri=ri, s=s: e.activation(
                            out=XB[:, :, ri, s, :], in_=PS[:, b, :].rearrange("p (f n) -> p f n", f=4), func=AF.Copy),
                            reads=pk(b), writes=["XB"])
                bBD = psget()
                for k in range(5):
                    if k == 0:
                        dv(lambda e: e.tensor_copy(out=Yr[:], in_=CTr[:]), ["s5_CTr", "XB"], ["Yr"])
                        ts(Yi[:], CTi[:], -1.0, ALU.mult, ["s5_CTi", "XB"], ["Yi"])
                    else:
                        cmul(Yr[:], Yi[:], CTr[:], CTi[:], bc(AR[k][:]), bc(AI[k][:]),
                             ["s5_CTr", "s5_CTi", "BD%d" % (k - 1), "YC"] + akeys[k], "Y", W1[:], W2[:])
                        ts(Yi[:], Yi[:], -1.0, ALU.mult, ["Yi"], ["Yi"])
                        P.op("act", lambda e, k=k: e.activation(out=YC[:, :, 0, k - 1, :], in_=Yr[:], func=AF.Copy),
                             reads=["Yr"], writes=["YC"])
                        P.op("act", lambda e, k=k: e.activation(out=YC[:, :, 1, k - 1, :], in_=Yi[:], func=AF.Copy),
                             reads=["Yi"], writes=["YC"])
                    if k < 4:
                        for ft in range(4):
                            o_ = PS[:, bBD, ft * 128:(ft + 1) * 128]
                            P.op("pe", lambda e, ft=ft, o_=o_: e.matmul(
                                o_, lhsT=BBr[:, 4 * ft:4 * ft + 4, :].rearrange("p a b -> p (a b)"),
                                rhs=Yr[:, 4 * ft:4 * ft + 4, :].rearrange("p a b -> p (a b)"), start=True, stop=False),
                                reads=["BBr", "Yr"], writes=pk(bBD))
                            P.op("pe", lambda e, ft=ft, o_=o_: e.matmul(
                                o_, lhsT=BBi[:, 4 * ft:4 * ft + 4, :].rearrange("p a b -> p (a b)"),
                                rhs=Yi[:, 4 * ft:4 * ft + 4, :].rearrange("p a b -> p (a b)"), start=False, stop=True),
                                reads=["BBi", "Yi"], writes=pk(bBD))
                        for ft in range(4):
                            if k == 0:
                                dv(lambda e, ft=ft: e.tensor_tensor(out=W1[:, 0:4, :].rearrange("p a b -> p (a b)"),
                                                                    in0=PS[:, bBD, ft * 128:(ft + 1) * 128],
                                                                    in1=bmask[:], op=ALU.mult),
                                   pk(bBD) + ["bmask"], ["W1"])
                                dv(lambda e, ft=ft: e.scalar_tensor_tensor(
                                    out=BD[:, ft, 0, :], in0=ident[:], scalar=dcol[:, i, ft:ft + 1],
                                    in1=W1[:, 0:4, :].rearrange("p a b -> p (a b)"), op0=ALU.mult, op1=ALU.add),
                                    ["W1", "ident", "dcol"], ["BD0"])
                            else:
                                dv(lambda e, ft=ft, k=k: e.tensor_tensor(out=BD[:, ft, k, :],
                                                                         in0=PS[:, bBD, ft * 128:(ft + 1) * 128],
                                                                         in1=bmask[:], op=ALU.mult),
                                   pk(bBD) + ["bmask"], ["BD%d" % k])

        def even_mixer(layer):
            i = layer // 2
            P.cost.update({"pe": 115.0, "dve": 430.0, "act": 450.0})
            with ExitStack() as st:
                NCH = TM // 4
                Win = WinP
                Wout = WoutP
                Wglu = WsmP[:, :].rearrange("p (k n) -> p k n", k=4)
                XB = sb(st, "XB", [128, 4, 2, 4, 128], BF16)
                YC = sb(st, "YC", [128, 16, 2, 4, 32], BF16)
                BD = sb(st, "BD", [128, 4, 4, 128], BF16)
                TC = sb(st, "TC", [128, 16, NCH]); TS = sb(st, "TS", [128, 16, NCH])
                R4 = sb(st, "R4", [128, 16]); A4 = sb(st, "A4", [128, 16, 2])
                wT = sb(st, "wT", [128, 8, 128], BF16)
                bbc = sb(st, "bbc", [128, 4, 128])
                gsg = sb(st, "gsg", [128, 512])
                wsc = sb(st, "wsc", [128, 4, 16])
                P.dma("sp", gsg[:], sgu_norm[i:i + 1, :].broadcast_to([128, 512]), writes=["gsg"])
                for h in range(8):
                    P.dma("sp", bbc[64 * (h % 2):64 * (h % 2) + 64, h // 2, :],
                          sgu_b[i, h:h + 1, :].broadcast_to([64, 128]), writes=[("bbc", h)])
                for h in range(8):
                    P.dma("sp", wsc[64 * (h % 2):64 * (h % 2) + 64, h // 2, :].rearrange("p (a b) -> p a b", a=4),
                          sgu_w[i, h:h + 1, 0:4, 0:4].broadcast_to([64, 4, 4]), writes=[("wsc", h)])
                P.op("dve", lambda e: e.memset(s5car[:], 0.0), writes=[("s5car", q_) for q_ in range(4)])
                with ExitStack() as st2:
                    trilT = sb(st2, "trilT", [128, 128])
                    bmask = sb(st2, "bmask", [128, 128])
                    j32 = sb(st2, "bm_j32", [128, 128])
                    S4 = sb(st2, "bm_S4", [128, 128])
                    P.op("dve", lambda e: e.tensor_scalar(out=trilT[:], in0=iot[:], scalar1=iop[:, 0:1], scalar2=None,
                                                          op0=ALU.is_ge), reads=["iot", "iop"], writes=["trilT"])
                    P.op("pool", lambda e: e.iota(j32[:], pattern=[[1, 4], [0, 32]], base=0, channel_multiplier=0,
                                                  allow_small_or_imprecise_dtypes=True), writes=["j32"])
                    P.op("dve", lambda e: e.tensor_scalar(out=S4[:], in0=j32[:], scalar1=iop[:, 0:1], scalar2=None,
                                                          op0=ALU.is_equal), reads=["j32", "iop"], writes=["S4"])
                    P.op("pe", lambda e: e.matmul(PS[:, 6, 0:128], lhsT=S4[0:4, :], rhs=S4[0:4, :], start=True, stop=True),
                         reads=["S4"], writes=[("ps", 6)])
                    P.op("dve", lambda e: e.tensor_copy(out=bmask[:], in_=PS[:, 6, 0:128]), reads=[("ps", 6)], writes=["bmask"])
                    wld = [sb(st2, "wld%d" % q, [128, 128]) for q in range(2)]
                    for h in range(8):
                        wl = wld[h % 2]
                        P.dma("sp", wl[:], sgu_w[i, h], writes=["wld%d" % (h % 2)])
                        b = psget()
                        P.op("pe", lambda e, b=b, wl=wl: e.transpose(PS[:, b, 0:128], wl[:], ident[:]),
                             reads=["wld%d" % (h % 2), "ident"], writes=pk(b))
                        P.op("dve", lambda e, b=b, h=h: e.tensor_tensor(out=wT[:, h, :], in0=PS[:, b, 0:128],
                                                                        in1=trilT[:], op=ALU.mult),
                             reads=pk(b) + ["trilT"], writes=["wT"])
                    if layer == 0:
                        load_x(st2)
                    s5_setup(st2, i, XB, YC, BD, TC, TS, R4, A4, bmask)
                    P.flush()
                glubh = sb(st, "glubh", [128, 4])
                P.op("dve", lambda e: e.tensor_scalar(out=glubh[:], in0=glub[:, i, :], scalar1=0.5, scalar2=None, op0=ALU.mult),
                     writes=["glubh"])
                P.op("dve", lambda e: e.tensor_scalar(out=WoutP[:, 0:4, :], in0=WoutP[:, 0:4, :], scalar1=0.5, scalar2=None,
                                                      op0=ALU.mult), writes=["Wout"])
                xnt = sb(st, "xnt", [128, 8, TM], BF16)
                uaL = [sb(st, "ua%d" % q, [128, 4, TM], BF16) for q in range(2)]
                ubL = [sb(st, "ub%d" % q, [128, 4, TM], BF16) for q in range(2)]
                vn = sb(st, "vn", [128, 512])
                vnbL = [sb(st, "vnb%d" % q, [128, 2, 512], BF16) for q in range(2)]
                vjunk = sb(st, "vjunk", [128, 512], BF16)
                vss = sb(st, "vss", [128, 2])
                ymix = sb(st, "ymix", [128, 8, TM], BF16)
                tA = sb(st, "tA", [128, 4, NCH]); tB = sb(st, "tB", [128, 4, NCH])
                Gin = sb(st, "Gin", [128, 4, 2, NCH])
                wtail = [WinP[:, k_, 1536:2048].bitcast(F32).rearrange("p (a c) -> p a c", a=4) for k_ in range(8)]
                tC, tD, tE, tF, tG, tH = wtail[0:6]
                GsL = [sb(st, "Gs0", [128, 4, 2, NCH]),
                       WinP[:, 6:8, 1536:2048].bitcast(F32).rearrange("p k (a c) -> p a k c", a=4)]
                Hf = sb(st, "Hf", [128, 4, 2, NCH + 1])
                Hb = sb(st, "Hb", [128, 4, 2, NCH], BF16)
                sqy = sb(st, "sqy", [128, TM])
                zf = sb(st, "zf", [128, 4, TM])
                zb = sb(st, "zb", [128, 4, TM], BF16)
                sg2 = sb(st, "sg2", [128, TM])
                stmp = sb(st, "stmp", [128, TM])
                vT = sb(st, "vT", [128, 4, NS])
                sacc = sb(st, "sacc", [128, 16, 4])
                h0s = sb(st, "h0s", [16, 1024])
                h0T = sb(st, "h0T", [128, 16, 2, 16])
                hend = sb(st, "hend", [128, 16, 2, 16])
                hoP = sb(st, "hoP", [16, 2, 128])
                def front(ti, t0, n, is_s):
                    nch = n // 4
                    par = ti % 2
                    ua = uaL[par]; ub = ubL[par]; vnb = vnbL[par]
                    hk = ("hres", t0)
                    rmsnorm_tile(st, "m", t0, n, gmix[:, layer, :], xnt, "xnt") if ti == 0 else \
                        rmsnorm_tile_again("m", t0, n, gmix[:, layer, :], xnt, "xnt")
                    for ft in range(4):
                        b = psget()
                        for k in range(8):
                            P.op("pe", lambda e, k=k, ft=ft, b=b: e.matmul(
                                PS[:, b, 0:n], lhsT=Win[:, k, ft * 128:(ft + 1) * 128], rhs=xnt[:, k, 0:n],
                                start=(k == 0), stop=(k == 7)), reads=["Win", "xnt"], writes=pk(b))
                        P.op("act", lambda e, ft=ft, b=b: e.activation(out=ua[:, ft, 0:n], in_=PS[:, b, 0:n],
                                                                       func=AF.Copy),
                             reads=pk(b), writes=[("ua", par, ft)])
                    for ft in range(4):
                        b = psget()
                        for k in range(8):
                            P.op("pe", lambda e, k=k, ft=ft, b=b: e.matmul(
                                PS[:, b, 0:n], lhsT=Win[:, k, 512 + ft * 128:512 + (ft + 1) * 128], rhs=xnt[:, k, 0:n],
                                start=(k == 0), stop=(k == 7)), reads=["Win", "xnt"], writes=pk(b))
                        P.op("act", lambda e, ft=ft, b=b: e.activation(out=ub[:, ft, 0:n], in_=PS[:, b, 0:n],
                                                                       func=AF.Copy),
                             reads=pk(b), writes=[("ub", par, ft)])
                    nsub = (n + 127) // 128
                    for sj in range(nsub):
                        m = min(128, n - sj * 128)
                        b = psget()
                        for k in range(8):
                            P.op("pe", lambda e, k=k, b=b, sj=sj, m=m: e.matmul(
                                PS[0:m, b, :], lhsT=xnt[:, k, sj * 128:sj * 128 + m], rhs=Win[:, k, 1024:1536],
                                start=(k == 0), stop=(k == 7)), reads=["Win", "xnt"], writes=pk(b))
                        P.op("act", lambda e, b=b, sj=sj, m=m: e.activation(
                            out=vjunk[0:m, :], in_=PS[0:m, b, :], func=AF.Square, accum_out=vss[0:m, sj:sj + 1]),
                            reads=pk(b), writes=["vjunk", ("vss", sj)])
                        P.op("act", lambda e, sj=sj, m=m: e.activation(
                            out=vss[0:m, sj:sj + 1], in_=vss[0:m, sj:sj + 1], func=AF.Ln, bias=epsc[0:m, 0:1],
                            scale=1.0 / 512.0), reads=[("vss", sj), "epsc"], writes=[("vss", sj)], cost=250.0)
                        P.op("act", lambda e, sj=sj, m=m: e.activation(
                            out=vss[0:m, sj:sj + 1], in_=vss[0:m, sj:sj + 1], func=AF.Exp, scale=-0.5),
                            reads=[("vss", sj)], writes=[("vss", sj)], cost=250.0)
                        P.op("dve", lambda e, b=b, sj=sj, m=m: e.scalar_tensor_tensor(
                            out=vnb[0:m, sj, :], in0=PS[0:m, b, :], scalar=vss[0:m, sj:sj + 1], in1=gsg[0:m, :],
                            op0=ALU.mult, op1=ALU.mult), reads=pk(b) + [("vss", sj), "gsg"], writes=[("vnb", par, sj)])
                        if is_s:
                            P.op("dve", lambda e, b=b, sj=sj, m=m: e.scalar_tensor_tensor(
                                out=vn[0:m, :], in0=PS[0:m, b, :], scalar=vss[0:m, sj:sj + 1], in1=gsg[0:m, :],
                                op0=ALU.mult, op1=ALU.mult), reads=pk(b) + [("vss", sj), "gsg"], writes=[("vn", 0)])
                    if is_s:
                        P.dma("sp", o_s_v[i], vn[0:NS, :], reads=[("vn", 0)])
                        for ri in range(2):
                            b = psget()
                            for hf in range(2):
                                P.dma("sp", h0s[:, :], (st_re if ri == 0 else st_im)[i][:, hf * 1024:(hf + 1) * 1024],
                                      writes=["h0s"])
                                for q in range(8):
                                    Pp = hf * 8 + q
                                    P.op("pe", lambda e, b=b, Pp=Pp, q=q: e.transpose(
                                        PS[:, b, Pp * 16:(Pp + 1) * 16], h0s[:, q * 128:(q + 1) * 128],
                                        ident[0:16, 0:16]), reads=["h0s", "ident"], writes=pk(b))
                            P.op("dve", lambda e, b=b, ri=ri: e.tensor_copy(
                                out=h0T[:, :, ri, :], in_=PS[:, b, 0:256].rearrange("p (a b) -> p a b", a=16)),
                                reads=pk(b), writes=["h0T"])
                def back(ti, t0, n, is_s):
                    nch = n // 4
                    par = ti % 2
                    ua = uaL[par]; ub = ubL[par]; vnb = vnbL[par]
                    for ft in range(4):
                        if not is_s:
                            P.op("pool", lambda e, ft=ft: e.tensor_copy(out=Hf[:, :, :, 0], in_=s5car[:, 4 * ft:4 * ft + 4, :]),
                                 reads=[("s5car", ft)], writes=["Hf0"])
                        b4 = psget(4)
                        for p4 in range(4):
                            for ri in range(2):
                                for s in range(4):
                                    P.op("pe", lambda e, p4=p4, ri=ri, s=s, ft=ft, b4=b4: e.matmul(
                                        PS[:, b4 + p4, ri * NCH:ri * NCH + nch],
                                        lhsT=XB[32 * p4:32 * p4 + 32, ft, ri, s, :],
                                        rhs=ua[32 * p4:32 * p4 + 32, ft, s:n:4],
                                        start=(s == 0), stop=(s == 3), tile_position=(32 * p4, 0)),
                                        reads=["XB", ("ua", par, ft)], writes=pk(b4, 4), cost=40.0)
                        Xr = PS[:, b4:b4 + 4, 0:nch]
                        Xi = PS[:, b4:b4 + 4, NCH:NCH + nch]
                        if not is_s:
                            Cc = TC[:, 4 * ft:4 * ft + 4, 0:nch]
                            Ss = TS[:, 4 * ft:4 * ft + 4, 0:nch]
                            tAa = tA[:, :, 0:nch]; tBb = tB[:, :, 0:nch]
                            GinR = Gin[:, :, 0, 0:nch]; GinI = Gin[:, :, 1, 0:nch]
                            x4 = pk(b4, 4)
                            gq = ft % 2
                            Gsq = GsL[gq]

                            def tt(out, a, bb, op, r, w, eng="dve"):
                                P.op(eng, lambda e: e.tensor_tensor(out=out, in0=a, in1=bb, op=op), reads=r, writes=w)
                            tCc = tC[:, :, 0:nch]; tDd = tD[:, :, 0:nch]
                            tt(tAa, Xr, Cc, ALU.mult, x4 + ["TC"], ["tA"])
                            tt(tBb, Xi, Ss, ALU.mult, x4 + ["TS"], ["tB"])
                            tt(tCc, Xi, Cc, ALU.mult, x4 + ["TC"], ["tC"])
                            tt(tDd, Xr, Ss, ALU.mult, x4 + ["TS"], ["tD"])
                            tt(GinR, tAa, tBb, ALU.add, ["tA", "tB"], ["GinR"])
                            tt(GinI, tCc, tDd, ALU.subtract, ["tC", "tD"], ["GinI"])
                            for p4 in range(4):
                                Pp = 4 * ft + p4
                                for ri in range(2):
                                    P.op("dve", lambda e, p4=p4, ri=ri, Pp=Pp, Gsq=Gsq: e.tensor_tensor_scan(
                                        out=Gsq[:, p4, ri, 0:nch], data0=R4[:, Pp:Pp + 1].broadcast_to([128, nch]),
                                        data1=Gin[:, p4, ri, 0:nch], initial=s5car[:, Pp, ri:ri + 1],
                                        op0=ALU.mult, op1=ALU.add),
                                        reads=["GinR" if ri == 0 else "GinI", "R4", ("s5car", ft)],
                                        writes=[("Gs", gq, p4, ri)], cost=350.0)
                            GR = Gsq[:, :, 0, 0:nch]; GI = Gsq[:, :, 1, 0:nch]
                            gk = [("Gs", gq, a_, b_) for a_ in range(4) for b_ in range(2)]
                            tEe = tE[:, :, 0:nch]; tFf = tF[:, :, 0:nch]; tGg = tG[:, :, 0:nch]; tHh = tH[:, :, 0:nch]
                            tt(tEe, GR, Cc, ALU.mult, gk + ["TC"], ["tE"], "pool")
                            tt(tFf, GI, Ss, ALU.mult, gk + ["TS"], ["tF"], "pool")
                            tt(tGg, GR, Ss, ALU.mult, gk + ["TS"], ["tG"], "pool")
                            tt(tHh, GI, Cc, ALU.mult, gk + ["TC"], ["tH"], "pool")
                            tt(Hf[:, :, 0, 1:nch + 1], tEe, tFf, ALU.subtract, ["tE", "tF", "Hf0"], ["HfR"], "pool")
                            tt(Hf[:, :, 1, 1:nch + 1], tGg, tHh, ALU.add, ["tG", "tH", "Hf0"], ["HfI"], "pool")
                            P.op("act", lambda e: e.activation(out=Hb[:, :, :, 0:nch], in_=Hf[:, :, :, 0:nch], func=AF.Copy),
                                 reads=["HfR", "HfI", "Hf0"], writes=["Hb"])
                            P.op("pool", lambda e, ft=ft: e.tensor_copy(out=s5car[:, 4 * ft:4 * ft + 4, :],
                                                                        in_=Hf[:, :, :, nch]),
                                 reads=["HfR", "HfI"] + gk, writes=[("s5car", ft)])
                        else:
                            h0r = h0T[:, 4 * ft:4 * ft + 4, 0, :]; h0i = h0T[:, 4 * ft:4 * ft + 4, 1, :]
                            a4r = A4[:, 4 * ft:4 * ft + 4, 0:1].broadcast_to([128, 4, 16])
                            a4i = A4[:, 4 * ft:4 * ft + 4, 1:2].broadcast_to([128, 4, 16])
                            tAa = tA[:, :, 0:16]; tBb = tB[:, :, 0:16]
                            x4 = pk(b4, 4)

                            def tt(out, a, bb, op, r, w):
                                P.op("dve", lambda e: e.tensor_tensor(out=out, in0=a, in1=bb, op=op), reads=r, writes=w)
                            tt(tAa, h0r, a4r, ALU.mult, ["h0T", "A4"], ["tA"])
                            tt(tBb, h0i, a4i, ALU.mult, ["h0T", "A4"], ["tB"])
                            tt(tAa, tAa, tBb, ALU.subtract, ["tA", "tB"], ["tA"])
                            tt(hend[:, 4 * ft:4 * ft + 4, 0, :], tAa, Xr, ALU.add, ["tA"] + x4, [("hend", ft, 0)])
                            tt(tAa, h0r, a4i, ALU.mult, ["h0T", "A4", ("hend", ft, 0)], ["tA"])
                            tt(tBb, h0i, a4r, ALU.mult, ["h0T", "A4", ("hend", ft, 0)], ["tB"])
                            tt(tAa, tAa, tBb, ALU.add, ["tA", "tB"], ["tA"])
                            tt(hend[:, 4 * ft:4 * ft + 4, 1, :], tAa, Xi, ALU.add, ["tA"] + x4, [("hend", ft, 1)])
                            P.op("act", lambda e, ft=ft: e.activation(out=Hb[:, :, :, 0:16],
                                                                      in_=h0T[:, 4 * ft:4 * ft + 4, :, :], func=AF.Copy),
                                 reads=["h0T"], writes=["Hb"])
                        by = psget()
                        for t in range(4):
                            o_ = PS[:, by, t * NCH:t * NCH + nch]
                            for tau in range(t + 1):
                                P.op("pe", lambda e, t=t, tau=tau, ft=ft, o_=o_: e.matmul(
                                    o_, lhsT=BD[:, ft, tau, :], rhs=ua[:, ft, (t - tau):n:4],
                                    start=(tau == 0), stop=False), reads=[("ua", par, ft)], writes=pk(by), cost=60.0)
                            for p4 in range(4):
                                for ri in range(2):
                                    last = (ri == 1)
                                    P.op("pe", lambda e, t=t, p4=p4, ri=ri, ft=ft, by=by, last=last: e.matmul(
                                        PS[32 * p4:32 * p4 + 32, by, t * NCH:t * NCH + nch],
                                        lhsT=YC[:, 4 * ft + p4, ri, t, :], rhs=Hb[:, p4, ri, 0:nch],
                                        start=False, stop=last, tile_position=(0, 32 * p4)),
                                        reads=["Hb"], writes=pk(by), cost=45.0)
                        yv = PS[:, by, 0:4 * NCH].rearrange("p (t c) -> p c t", t=4)[:, 0:nch, :]
                        sq3 = sqy[:, 0:n].rearrange("p (c t) -> p c t", t=4)
                        z3 = zf[:, ft, 0:n].rearrange("p (c t) -> p c t", t=4)
                        P.op("act", lambda e, yv=yv, z3=z3: e.activation(out=z3, in_=yv, func=AF.Gelu_apprx_tanh),
                             reads=pk(by), writes=[("zf", ft)])
                        P.op("act", lambda e, ft=ft: e.activation(out=zb[:, ft, 0:n], in_=zf[:, ft, 0:n], func=AF.Copy),
                             reads=[("zf", ft)], writes=[("zb", ft)])
                    for fo in range(4):
                        b = psget()
                        for fi in range(4):
                            P.op("pe", lambda e, fi=fi, fo=fo, b=b: e.matmul(
                                PS[:, b, 0:n], lhsT=Wglu[:, fi, fo * 128:(fo + 1) * 128], rhs=zb[:, fi, 0:n],
                                start=(fi == 0), stop=(fi == 3)), reads=["Wsm"] + [("zb", q) for q in range(4)],
                                writes=pk(b))
                        P.op("act", lambda e, fo=fo, b=b: e.activation(out=sg2[:, 0:n], in_=PS[:, b, 0:n], func=AF.Tanh,
                                                                       bias=glubh[:, fo:fo + 1], scale=0.5),
                             reads=pk(b) + ["glubh"], writes=["sg2"])
                        P.op("dve", lambda e, fo=fo: e.scalar_tensor_tensor(out=ymix[:, fo, 0:n], in0=sg2[:, 0:n], scalar=1.0,
                                                                            in1=zf[:, fo, 0:n], op0=ALU.add, op1=ALU.mult),
                             reads=["sg2", ("zf", fo)], writes=["ymix"])
                    if not is_s:
                        for hp in range(4):
                            b = psget()
                            for j in range(n // 128):
                                for h2 in range(2):
                                    h = 2 * hp + h2
                                    P.op("pe", lambda e, b=b, j=j, h2=h2, h=h: e.matmul(
                                        PS[64 * h2:64 * h2 + 64, b, j * 128:(j + 1) * 128],
                                        lhsT=vnb[:, j, h * 64:(h + 1) * 64], rhs=wT[:, h, :],
                                        start=True, stop=True, tile_position=(0, 64 * h2)),
                                        reads=["wT", ("vnb", par, j)], writes=pk(b))
                            P.op("dve", lambda e, b=b, hp=hp: e.tensor_tensor(
                                out=stmp[:, 0:n].rearrange("p (j i) -> p j i", i=128),
                                in0=PS[:, b, 0:n].rearrange("p (j i) -> p j i", i=128),
                                in1=bbc[:, hp, :].unsqueeze(1).broadcast_to([128, n // 128, 128]), op=ALU.add),
                                reads=pk(b) + ["bbc"], writes=["stmp"])
                            P.op("dve", lambda e, hp=hp: e.tensor_tensor(out=ymix[:, 4 + hp, 0:n], in0=stmp[:, 0:n],
                                                                         in1=ub[:, hp, 0:n], op=ALU.mult),
                                 reads=["stmp", ("ub", par, hp)], writes=["ymix"])
                    else:
                        b = psget()
                        for ft in range(4):
                            P.op("pe", lambda e, b=b, ft=ft: e.transpose(PS[:, b, ft * NS:(ft + 1) * NS],
                                                                         vn[0:NS, ft * 128:(ft + 1) * 128],
                                                                         ident[0:NS, 0:NS]),
                                 reads=[("vn", 0), "ident"], writes=pk(b))
                        P.op("dve", lambda e, b=b: e.tensor_copy(out=vT[:], in_=PS[:, b, 0:4 * NS].rearrange("p (f t) -> p f t", f=4)),
                             reads=pk(b), writes=["vT"])
                        for ft in range(4):
                            v3 = vT[:, ft, :].rearrange("p (b j) -> p b j", j=4)
                            for ii in range(4):
                                P.op("dve", lambda e, ft=ft, ii=ii, v3=v3: e.tensor_scalar(
                                    out=sacc[:, :, ii], in0=v3[:, :, 0], scalar1=wsc[:, ft, 4 * ii:4 * ii + 1],
                                    scalar2=bbc[:, ft, ii:ii + 1], op0=ALU.mult, op1=ALU.add),
                                    reads=["vT", "wsc", "bbc"], writes=["sacc"])
                                for jj in range(1, ii + 1):
                                    P.op("dve", lambda e, ft=ft, ii=ii, jj=jj, v3=v3: e.scalar_tensor_tensor(
                                        out=sacc[:, :, ii], in0=v3[:, :, jj], scalar=wsc[:, ft, 4 * ii + jj:4 * ii + jj + 1],
                                        in1=sacc[:, :, ii], op0=ALU.mult, op1=ALU.add),
                                        reads=["vT", "wsc", "sacc"], writes=["sacc"])
                            P.op("dve", lambda e, ft=ft: e.tensor_tensor(
                                out=ymix[:, 4 + ft, 0:NS], in0=sacc[:].rearrange("p b i -> p (b i)"),
                                in1=ub[:, ft, 0:NS], op=ALU.mult), reads=["sacc", ("ub", par, ft)], writes=["ymix"])
                    out_proj_tile(Wout, "Wout", ymix, "ymix", t0, n)
                    if is_s:
                        for ri in range(2):
                            for hf in range(2):
                                for h2 in range(2):
                                    half = hf * 2 + h2
                                    b = psget()
                                    for q in range(4):
                                        Pp = half * 4 + q
                                        P.op("pe", lambda e, b=b, q=q, Pp=Pp, ri=ri: e.transpose(
                                            PS[0:16, b, q * 128:(q + 1) * 128], hend[:, Pp, ri, :], ident[:]),
                                            reads=[("hend", Pp // 4, ri), "ident"], writes=pk(b))
                                    P.op("act", lambda e, b=b, h2=h2: e.activation(
                                        out=h0s[:, h2 * 512:(h2 + 1) * 512], in_=PS[0:16, b, :], func=AF.Copy),
                                        reads=pk(b), writes=["h0s"])
                                P.dma("sp", (o_s_re if ri == 0 else o_s_im)[i][:, hf * 1024:(hf + 1) * 1024], h0s[:, :],
                                      reads=["h0s"])
                    if (not is_s) and t0 + n == SEQ:
                        for ri in range(2):
                            b = psget()
                            P.op("pe", lambda e, b=b, ri=ri: e.transpose(PS[0:16, b, 0:128], s5car[:, :, ri], ident[:]),
                                 reads=[("s5car", q_) for q_ in range(4)] + ["ident"], writes=pk(b))
                            P.op("act", lambda e, b=b, ri=ri: e.activation(out=hoP[:, ri, :], in_=PS[0:16, b, 0:128], func=AF.Copy),
                                 reads=pk(b), writes=["hoP"])
                        P.dma("sp", o_p_re[i], hoP[:, 0, :], reads=["hoP"])
                        P.dma("sp", o_p_im[i], hoP[:, 1, :], reads=["hoP"])
                seq = list(enumerate(mtiles))
                for idx, (ti, (t0, n, is_s)) in enumerate(seq):
                    front(ti, t0, n, is_s)
                    if idx >= 1:
                        pti, (pt0, pn, ps_) = seq[idx - 1]
                        back(pti, pt0, pn, ps_)
                lti, (lt0, ln, ls_) = seq[-1]
                back(lti, lt0, ln, ls_)
                P.flush()

        _norm_scr = {}

        def rmsnorm_tile_again(tag, t0, n, gvec, xn_out, xn_key):
            _rms_ops(tag, t0, n, gvec, xn_out, xn_key, _norm_scr[tag])

        def _rms_ops(tag, t0, n, gvec, xn_out, xn_key, srs, xoff=0):
            sr, sr2 = srs
            hk = hkeys(t0, n)
            sqv = xn_out[:, :, xoff:xoff + n]
            P.op("act", lambda e: e.activation(out=sqv, in_=hres[:, :, t0:t0 + n], func=AF.Square),
                 reads=hk, writes=[xn_key])
            b = psget()
            for k in range(8):
                P.op("pe", lambda e, k=k, b=b: e.matmul(PS[:, b, 0:n], lhsT=onesb[:], rhs=xn_out[:, k, xoff:xoff + n],
                                                        start=(k == 0), stop=(k == 7)),
                     reads=[xn_key, "onesb"], writes=pk(b))
            P.op("act", lambda e, b=b: e.activation(out=sr[:, 0:n], in_=PS[:, b, 0:n], func=AF.Ln,
                                                    bias=epsc[:, 0:1], scale=1.0 / D),
                 reads=pk(b) + ["epsc"], writes=["sr_" + tag])
            P.op("act", lambda e: e.activation(out=sr[:, 0:n], in_=sr[:, 0:n], func=AF.Exp, scale=-0.5),
                 reads=["sr_" + tag], writes=["sr_" + tag])
            for k in range(8):
                P.op("dve", lambda e, k=k: e.scalar_tensor_tensor(
                    out=xn_out[:, k, xoff:xoff + n], in0=hres[:, k, t0:t0 + n], scalar=gvec[:, k:k + 1],
                    in1=sr2[:, 0:n], op0=ALU.mult, op1=ALU.mult),
                    reads=hk + ["sr_" + tag, "gmix", "gffn"], writes=[xn_key])

        def rmsnorm_tile(stk, tag, t0, n, gvec, xn_out, xn_key, xoff=0):
            nmax = TM if tag == "m" else TF
            sr = sb(stk, "sr_" + tag, [128, nmax])
            sr2 = sr
            _norm_scr[tag] = (sr, sr2)
            _rms_ops(tag, t0, n, gvec, xn_out, xn_key, (sr, sr2), xoff=xoff)

        def out_proj_tile(Wout, wkey, ymix, ykey, t0, n):
            hk = hkeys(t0, n)
            for fo in range(8):
                b = psget()
                for k in range(8):
                    P.op("pe", lambda e, k=k, fo=fo, b=b: e.matmul(
                        PS[:, b, 0:n], lhsT=Wout[:, k, fo * 128:(fo + 1) * 128], rhs=ymix[:, k, 0:n],
                        start=(k == 0), stop=(k == 7)), reads=[wkey, ykey], writes=pk(b))
                P.op("dve", lambda e, fo=fo, b=b: e.tensor_tensor(
                    out=hres[:, fo, t0:t0 + n], in0=hres[:, fo, t0:t0 + n], in1=PS[:, b, 0:n], op=ALU.add),
                    reads=pk(b) + hk, writes=hk)

        def odd_mixer(layer):
            i = layer // 2
            P.cost.update({"pe": 115.0, "dve": 430.0, "act": 450.0})
            with ExitStack() as st:
                Win = WinP
                Wout = WoutP
                Wp = WsmP[:, 0:512].rearrange("p (g d) -> p g d", g=4)
                xnt = sb(st, "xnto", [128, 8, TM], BF16)
                XCL = [sb(st, "XC%d" % q, [128, 4, 15 + TM]) for q in range(2)]
                PA = sb(st, "PA", [128, 15 + TM]); PB = sb(st, "PB", [128, 15 + TM])
                diff = sb(st, "diff", [128, 4, TM], BF16)
                xdL = [sb(st, "xd%d" % q, [128, 4, TM]) for q in range(2)]
                bgL = [sb(st, "bg%d" % q, [128, 4, TM]) for q in range(2)]
                ZL = [sb(st, "Z%d" % q, [128, 4, 2 + TM]) for q in range(2)]
                ca = sb(st, "ca", [128, TM])
                ymix = sb(st, "ymixo", [128, 8, TM], BF16)
                invn = sb(st, "invn", [128, 4, 15])
                XCs = sb(st, "XCs", [128, 4, 16, 19])
                PAs = sb(st, "PAs", [128, 16, 19]); PBs = sb(st, "PBs", [128, 16, 19])
                Zs = sb(st, "Zs", [128, 4, 16, 6])
                spl = [sb(st, "spl%d" % q, [128, 512]) for q in range(2)]
                scl = sb(st, "scl", [32, 512])
                otp = sb(st, "otp", [128, 512])
                otc = sb(st, "otc", [32, 512])
                opp = sb(st, "opp", [16, 512])
                opc = sb(st, "opc", [2, 512])
                xct = sb(st, "xct", [128, 128])
                zct = sb(st, "zct", [128, 32])
                P.op("pool", lambda e: e.iota(invn[:], pattern=[[0, 4], [1, 15]], base=1, channel_multiplier=0,
                                              allow_small_or_imprecise_dtypes=True), writes=["invn"])
                for gi in range(4):
                    P.op("dve", lambda e, gi=gi: e.tensor_scalar(out=invn[:, gi, :], in0=invn[:, gi, :],
                                                                 scalar1=float(2 ** (gi + 1)), scalar2=None, op0=ALU.min),
                         reads=["invn"], writes=["invn"])
                P.op("dve", lambda e: e.reciprocal(out=invn[:], in_=invn[:]), reads=["invn"], writes=["invn"])
                P.op("dve", lambda e: e.memset(xchalo[:], 0.0), writes=["xchalo"])
                P.op("dve", lambda e: e.memset(zhalo[:], 0.0), writes=["zhalo"])
                P.op("pool", lambda e: e.memset(PA[:], 0.0), writes=["P0"])
                P.op("pool", lambda e: e.memset(PB[:], 0.0), writes=["P1"])
                P.op("pool", lambda e: e.memset(PAs[:], 0.0), writes=["Ps0"])
                P.op("pool", lambda e: e.memset(PBs[:], 0.0), writes=["Ps1"])
                def front(ti, t0, n, is_s):
                    par = ti % 2
                    XC = XCL[par]; xd = xdL[par]; bg = bgL[par]; Z = ZL[par]
                    if ti == 0:
                        rmsnorm_tile(st, "m", t0, n, gmix[:, layer, :], xnt, "xnt")
                    else:
                        rmsnorm_tile_again("m", t0, n, gmix[:, layer, :], xnt, "xnt")
                    if not is_s:
                        P.op("dve", lambda e: e.tensor_copy(out=XC[:, :, 0:15], in_=xchalo[:]), reads=["xchalo"],
                             writes=[("XChalo", par)])
                        P.op("dve", lambda e: e.tensor_copy(out=Z[:, :, 0:2], in_=zhalo[:]), reads=["zhalo"],
                             writes=[("Zhalo", par)])
                    else:
                        P.dma("sp", spl[0][:], st_pool[i].rearrange("b r c -> (b r) c")[0:128, :], writes=["spl0"])
                        P.dma("sp", spl[1][0:112, :], st_pool[i].rearrange("b r c -> (b r) c")[128:240, :], writes=["spl1"])
                        P.dma("sp", scl[:], st_conv[i].rearrange("b r c -> (b r) c"), writes=["scl"])
                        for ft in range(4):
                            b = psget()
                            P.op("pe", lambda e, b=b, ft=ft: e.transpose(PS[:, b, 0:128], spl[0][:, ft * 128:(ft + 1) * 128], ident[:]),
                                 reads=["spl0", "ident"], writes=pk(b))
                            P.op("pe", lambda e, b=b, ft=ft: e.transpose(PS[:, b, 128:240], spl[1][0:112, ft * 128:(ft + 1) * 128],
                                                                         ident[0:112, 0:112]),
                                 reads=["spl1", "ident"], writes=pk(b))
                            P.op("pe", lambda e, b=b, ft=ft: e.transpose(PS[:, b, 256:288], scl[:, ft * 128:(ft + 1) * 128],
                                                                         ident[0:32, 0:32]),
                                 reads=["scl", "ident"], writes=pk(b))
                            P.op("dve", lambda e, b=b, ft=ft: e.tensor_copy(
                                out=XCs[:, ft, :, 0:15], in_=PS[:, b, 0:240].rearrange("p (b r) -> p b r", r=15)),
                                reads=pk(b), writes=[("XCs", ft)])
                            P.op("dve", lambda e, b=b, ft=ft: e.tensor_copy(
                                out=Zs[:, ft, :, 0:2], in_=PS[:, b, 256:288].rearrange("p (b r) -> p b r", r=2)),
                                reads=pk(b), writes=[("Zs", ft)])
                    for ft in range(4):
                        b = psget()
                        for k in range(8):
                            P.op("pe", lambda e, k=k, ft=ft, b=b: e.matmul(
                                PS[:, b, 0:n], lhsT=Win[:, k, ft * 128:(ft + 1) * 128], rhs=xnt[:, k, 0:n],
                                start=(k == 0), stop=(k == 7)), reads=["Win", "xnt"], writes=pk(b))
                        if not is_s:
                            P.op("act", lambda e, ft=ft, b=b: e.activation(out=XC[:, ft, 15:15 + n], in_=PS[:, b, 0:n], func=AF.Copy),
                                 reads=pk(b), writes=[("XC", par, ft)])
                        else:
                            P.op("act", lambda e, ft=ft, b=b: e.activation(
                                out=XCs[:, ft, :, 15:19], in_=PS[:, b, 0:NS].rearrange("p (b t) -> p b t", t=4), func=AF.Copy),
                                reads=pk(b) + [("XCs", ft)], writes=[("XCs", ft)])
                    for ft in range(4):
                        b = psget()
                        for k in range(8):
                            P.op("pe", lambda e, k=k, ft=ft, b=b: e.matmul(
                                PS[:, b, 0:n], lhsT=Win[:, k, 512 + ft * 128:512 + (ft + 1) * 128], rhs=xnt[:, k, 0:n],
                                start=(k == 0), stop=(k == 7)), reads=["Win", "xnt"], writes=pk(b))
                        P.op("act", lambda e, ft=ft, b=b: e.activation(out=xd[:, ft, 0:n], in_=PS[:, b, 0:n], func=AF.Copy),
                             reads=pk(b), writes=[("xd", par, ft)])
                    for ft in range(4):
                        b = psget()
                        for k in range(8):
                            P.op("pe", lambda e, k=k, ft=ft, b=b: e.matmul(
                                PS[:, b, 0:n], lhsT=Win[:, k, 1024 + ft * 128:1024 + (ft + 1) * 128], rhs=xnt[:, k, 0:n],
                                start=(k == 0), stop=(k == 7)), reads=["Win", "xnt"], writes=pk(b))
                        P.op("act", lambda e, ft=ft, b=b: e.activation(out=bg[:, ft, 0:n], in_=PS[:, b, 0:n], func=AF.Copy),
                             reads=pk(b), writes=[("bg", par, ft)])
                    for ft in range(4):
                        b = psget()
                        for k in range(8):
                            P.op("pe", lambda e, k=k, ft=ft, b=b: e.matmul(
                                PS[:, b, 0:n], lhsT=Win[:, k, 1536 + ft * 128:1536 + (ft + 1) * 128], rhs=xnt[:, k, 0:n],
                                start=(k == 0), stop=(k == 7)), reads=["Win", "xnt"], writes=pk(b))
                        if not is_s:
                            P.op("dve", lambda e, ft=ft, b=b: e.tensor_tensor(out=Z[:, ft, 2:2 + n], in0=PS[:, b, 0:n],
                                                                              in1=xd[:, ft, 0:n], op=ALU.mult),
                                 reads=pk(b) + [("xd", par, ft)], writes=[("Z", par, ft)])
                        else:
                            P.op("dve", lambda e, ft=ft, b=b: e.tensor_tensor(
                                out=Zs[:, ft, :, 2:6], in0=PS[:, b, 0:NS].rearrange("p (b t) -> p b t", t=4),
                                in1=xd[:, ft, 0:NS].rearrange("p (b t) -> p b t", t=4), op=ALU.mult),
                                reads=pk(b) + [("xd", par, ft), ("Zs", ft)], writes=[("Zs", ft)])
                    if not is_s:
                        P.op("dve", lambda e: e.tensor_copy(out=xchalo[:], in_=XC[:, :, n:n + 15]),
                             reads=[("XC", par, q) for q in range(4)] + [(("XChalo", par), par)], writes=["xchalo"])
                        P.op("dve", lambda e: e.tensor_copy(out=zhalo[:], in_=Z[:, :, n:n + 2]),
                             reads=[("Z", par, q) for q in range(4)] + [(("Zhalo", par), par)], writes=["zhalo"])
                def back(ti, t0, n, is_s):
                    par = ti % 2
                    XC = XCL[par]; xd = xdL[par]; bg = bgL[par]; Z = ZL[par]
                    for gi in range(4):
                        w = 2 ** (gi + 1)
                        if not is_s:
                            L = 15 + n
                            src = XC[:, gi, 0:L]
                            bufs = [PA, PB]
                            cur = src
                            ckey = [("XC", par, gi), ("XChalo", par)]
                            d = 1
                            q = 0
                            while d < w:
                                dst = bufs[q % 2]
                                dk = "P%d" % (q % 2)
                                P.op("pool", lambda e, cur=cur, dst=dst, d=d, L=L: e.tensor_tensor(
                                    out=dst[:, d:L], in0=cur[:, d:L], in1=cur[:, 0:L - d], op=ALU.add),
                                    reads=ckey, writes=[dk], cost=800.0)
                                cur = dst[:, 0:L]
                                ckey = [dk]
                                d *= 2
                                q += 1
                            P.op("dve", lambda e, cur=cur, gi=gi, w=w: e.scalar_tensor_tensor(
                                out=diff[:, gi, 0:n], in0=cur[:, 15:15 + n], scalar=1.0 / w, in1=XC[:, gi, 15:15 + n],
                                op0=ALU.mult, op1=ALU.subtract), reads=ckey + [("XC", par, gi)], writes=[("diff", gi)])
                            if t0 == 0:
                                P.op("dve", lambda e, cur=cur, gi=gi: e.tensor_tensor(
                                    out=ca[:, 0:15], in0=cur[:, 15:30], in1=invn[:, gi, :], op=ALU.mult),
                                    reads=ckey + ["invn"], writes=["ca"])
                                P.op("dve", lambda e, gi=gi: e.tensor_tensor(
                                    out=diff[:, gi, 0:15], in0=ca[:, 0:15], in1=XC[:, gi, 15:30], op=ALU.subtract),
                                    reads=["ca", ("XC", par, gi), ("diff", gi)], writes=[("diff", gi)])
                        else:
                            L = 19
                            cur = XCs[:, gi, :, :]
                            ckey = [("XCs", gi)]
                            bufs = [PAs, PBs]
                            d = 1
                            q = 0
                            while d < w:
                                dst = bufs[q % 2]
                                dk = "Ps%d" % (q % 2)
                                P.op("pool", lambda e, cur=cur, dst=dst, d=d: e.tensor_tensor(
                                    out=dst[:, :, d:19], in0=cur[:, :, d:19], in1=cur[:, :, 0:19 - d], op=ALU.add),
                                    reads=ckey, writes=[dk], cost=800.0)
                                cur = dst[:, :, :]
                                ckey = [dk]
                                d *= 2
                                q += 1
                            P.op("dve", lambda e, cur=cur, gi=gi, w=w: e.scalar_tensor_tensor(
                                out=diff[:, gi, 0:NS].rearrange("p (b t) -> p b t", t=4), in0=cur[:, :, 15:19],
                                scalar=1.0 / w, in1=XCs[:, gi, :, 15:19], op0=ALU.mult, op1=ALU.subtract),
                                reads=ckey + [("XCs", gi)], writes=[("diff", gi)])
                        b = psget()
                        P.op("pe", lambda e, gi=gi, b=b: e.matmul(PS[:, b, 0:n], lhsT=Wp[:, gi, :], rhs=diff[:, gi, 0:n],
                                                                  start=True, stop=True),
                             reads=["Wsm", ("diff", gi)], writes=pk(b))
                        P.op("act", lambda e, gi=gi, b=b: e.activation(out=ymix[:, gi, 0:n], in_=PS[:, b, 0:n], func=AF.Copy,
                                                                       scale=pscale[:, i, gi:gi + 1]),
                             reads=pk(b) + ["pscale"], writes=["ymix"])
                    for ft in range(4):
                        if not is_s:
                            z0 = Z[:, ft, 0:n]; z1 = Z[:, ft, 1:n + 1]; z2 = Z[:, ft, 2:n + 2]
                            cav = ca[:, 0:n]
                            bgv = bg[:, ft, 0:n]
                            yv = ymix[:, 4 + ft, 0:n]
                            zk = [("Z", par, ft), ("Zhalo", par)]
                        else:
                            z0 = Zs[:, ft, :, 0:4]; z1 = Zs[:, ft, :, 1:5]; z2 = Zs[:, ft, :, 2:6]
                            cav = ca[:, 0:NS].rearrange("p (b t) -> p b t", t=4)
                            bgv = bg[:, ft, 0:NS].rearrange("p (b t) -> p b t", t=4)
                            yv = ymix[:, 4 + ft, 0:NS].rearrange("p (b t) -> p b t", t=4)
                            zk = [("Zs", ft)]
                        P.op("dve", lambda e, ft=ft, z0=z0, cav=cav: e.tensor_scalar(
                            out=cav, in0=z0, scalar1=cw[:, i, 0, ft:ft + 1], scalar2=cb[:, i, ft:ft + 1],
                            op0=ALU.mult, op1=ALU.add), reads=zk + ["cw", "cb"], writes=["ca"])
                        P.op("dve", lambda e, ft=ft, z1=z1, cav=cav: e.scalar_tensor_tensor(
                            out=cav, in0=z1, scalar=cw[:, i, 1, ft:ft + 1], in1=cav, op0=ALU.mult, op1=ALU.add),
                            reads=zk + ["cw", "ca"], writes=["ca"])
                        P.op("dve", lambda e, ft=ft, z2=z2, cav=cav: e.scalar_tensor_tensor(
                            out=cav, in0=z2, scalar=cw[:, i, 2, ft:ft + 1], in1=cav, op0=ALU.mult, op1=ALU.add),
                            reads=zk + ["cw", "ca"], writes=["ca"])
                        P.op("dve", lambda e, cav=cav, bgv=bgv, yv=yv: e.tensor_tensor(out=yv, in0=cav, in1=bgv, op=ALU.mult),
                             reads=["ca", ("bg", par, ft)], writes=["ymix"])
                    out_proj_tile(Wout, "Wout", ymix, "ymix", t0, n)
                    if (not is_s) and t0 + n == SEQ:
                        for ft in range(4):
                            b = psget()
                            P.op("pe", lambda e, b=b, ft=ft: e.transpose(PS[0:15, b, 0:128], xchalo[:, ft, :], ident[:]),
                                 reads=["xchalo", "ident"], writes=pk(b))
                            P.op("pe", lambda e, b=b, ft=ft: e.transpose(PS[0:2, b, 128:256], zhalo[:, ft, :], ident[:]),
                                 reads=["zhalo", "ident"], writes=pk(b))
                            P.op("act", lambda e, b=b, ft=ft: e.activation(out=opp[0:15, ft * 128:(ft + 1) * 128],
                                                                           in_=PS[0:15, b, 0:128], func=AF.Copy),
                                 reads=pk(b), writes=["opp"])
                            P.op("act", lambda e, b=b, ft=ft: e.activation(out=opc[0:2, ft * 128:(ft + 1) * 128],
                                                                           in_=PS[0:2, b, 128:256], func=AF.Copy),
                                 reads=pk(b), writes=["opc"])
                        P.dma("sp", o_p_pool[i], opp[0:15, :], reads=["opp"])
                        P.dma("sp", o_p_conv[i], opc[0:2, :], reads=["opc"])
                    if is_s:
                        for half in range(2):
                            for ft in range(4):
                                P.op("dve", lambda e, ft=ft, half=half: e.tensor_copy(
                                    out=xct[:, 0:120].rearrange("p (b r) -> p b r", r=15),
                                    in_=XCs[:, ft, half * 8:half * 8 + 8, 4:19]), reads=[("XCs", ft)], writes=["xct"])
                                b = psget()
                                P.op("pe", lambda e, b=b: e.transpose(PS[0:120, b, 0:128], xct[:, 0:120], ident[:]),
                                     reads=["xct", "ident"], writes=pk(b))
                                P.op("act", lambda e, b=b, ft=ft: e.activation(out=otp[0:120, ft * 128:(ft + 1) * 128],
                                                                               in_=PS[0:120, b, 0:128], func=AF.Copy),
                                     reads=pk(b), writes=["otp"])
                            P.dma("sp", o_s_pool[i, half * 120:half * 120 + 120, :], otp[0:120, :], reads=["otp"])
                        for ft in range(4):
                            P.op("dve", lambda e, ft=ft: e.tensor_copy(
                                out=zct[:, 0:32].rearrange("p (b r) -> p b r", r=2), in_=Zs[:, ft, :, 4:6]),
                                reads=[("Zs", ft)], writes=["zct"])
                            b = psget()
                            P.op("pe", lambda e, b=b: e.transpose(PS[0:32, b, 0:128], zct[:, 0:32], ident[:]),
                                 reads=["zct", "ident"], writes=pk(b))
                            P.op("act", lambda e, b=b, ft=ft: e.activation(out=otc[:, ft * 128:(ft + 1) * 128],
                                                                           in_=PS[0:32, b, 0:128], func=AF.Copy),
                                 reads=pk(b), writes=["otc"])
                        P.dma("sp", o_s_conv[i], otc[:, :], reads=["otc"])
                seq = list(enumerate(mtiles))
                for idx, (ti, (t0, n, is_s)) in enumerate(seq):
                    front(ti, t0, n, is_s)
                    if idx >= 1:
                        pti, (pt0, pn, ps_) = seq[idx - 1]
                        back(pti, pt0, pn, ps_)
                lti, (lt0, ln, ls_) = seq[-1]
                back(lti, lt0, ln, ls_)
                P.flush()

        def epilogue(st):
            gfin = WinP[:, 2, :].bitcast(F32)
            P.dma("sp", gfin, norm_final.broadcast_to([128, D]), writes=["gfin"])
            junk = sb(st, "fjunk", [128, 512], BF16)
            ss = sb(st, "fss", [128, 2, 2])
            yo = [WinP[:, q, :].bitcast(F32) for q in range(2)]
            nsub = SEQ // 128 + 1
            for si in range(nsub):
                n = 128 if si < SEQ // 128 else NS
                dst = y_p[si * 128:(si + 1) * 128, :] if si < SEQ // 128 else y_s[:, :]
                par = si % 2
                yb = yo[par]
                yk = "yo%d" % par
                hk = hkeys(si * 128, n)
                bb = psget(2)
                for k in range(8):
                    P.op("pe", lambda e, k=k, bb=bb, si=si, n=n: e.transpose(
                        PS[0:n, bb + k // 4, (k % 4) * 128:(k % 4 + 1) * 128], hres[:, k, si * 128:si * 128 + n], ident[:]),
                        reads=["ident"] + hk, writes=pk(bb, 2), cost=110.0)
                for half in range(2):
                    P.op("act", lambda e, bb=bb, half=half, n=n, par=par: e.activation(
                        out=junk[0:n, :], in_=PS[0:n, bb + half, :], func=AF.Square, accum_out=ss[0:n, par, half:half + 1]),
                        reads=pk(bb, 2), writes=["fjunk", ("fss", par, half)])
                P.op("dve", lambda e, n=n, par=par: e.tensor_tensor(out=ss[0:n, par, 0:1], in0=ss[0:n, par, 0:1],
                                                                     in1=ss[0:n, par, 1:2], op=ALU.add),
                     reads=[("fss", par, 0), ("fss", par, 1)], writes=[("fss", par, 0)], cost=100.0)
                P.op("act", lambda e, n=n, par=par: e.activation(out=ss[0:n, par, 0:1], in_=ss[0:n, par, 0:1], func=AF.Sqrt,
                                                                  bias=epsc[0:n, 0:1], scale=1.0 / D),
                     reads=[("fss", par, 0)], writes=[("fss", par, 0)], cost=250.0)
                P.op("dve", lambda e, n=n, par=par: e.reciprocal(out=ss[0:n, par, 0:1], in_=ss[0:n, par, 0:1]),
                     reads=[("fss", par, 0)], writes=[("fss", par, 0)], cost=100.0)
                for half in range(2):
                    P.op("dve", lambda e, bb=bb, half=half, n=n, yb=yb, par=par: e.scalar_tensor_tensor(
                        out=yb[0:n, half * 512:(half + 1) * 512], in0=PS[0:n, bb + half, :], scalar=ss[0:n, par, 0:1],
                        in1=gfin[0:n, half * 512:(half + 1) * 512], op0=ALU.mult, op1=ALU.mult),
                        reads=pk(bb, 2) + [("fss", par, 0), "gfin"], writes=[yk], cost=750.0)
                P.dma("sp", dst, yb[0:n, :], reads=[yk])

        def ffn(layer):
            widths = [384] * 7 + [128]
            offs = [sum(widths[:j]) for j in range(len(widths))]
            with ExitStack() as st:
                xn = sb(st, "xn_all", [128, 8, T], BF16)
                Wg = [sb(st, "Wg%d" % q, [128, 8, 384], BF16) for q in range(2)]
                Wu = [sb(st, "Wu%d" % q, [128, 8, 384], BF16) for q in range(2)]
                Wd = [sb(st, "Wd%d" % q, [128, 3, D], BF16) for q in range(2)]
                sl = [sb(st, "sl%d" % q, [128, TF]) for q in range(2)]
                hb = [sb(st, "hb%d" % q, [128, 3, TF], BF16) for q in range(2)]

                P.cost.update({"pe": 195.0, "dve": 630.0, "act": 560.0})

                def load_slice(j):
                    q = j % 2
                    w = widths[j]
                    o = offs[j]
                    c = 2500.0 + 128 * 8 * w * 4 / 150.0
                    for kh in range(2):
                        P.dma("pool", Wg[q][:, 4 * kh:4 * kh + 4, 0:w],
                              ffn_g[layer].rearrange("(k p) n -> p k n", p=128)[:, 4 * kh:4 * kh + 4, o:o + w],
                              writes=[("Wg", q, kh)], cost=c / 2)
                    for kh in range(2):
                        P.dma("pool", Wu[q][:, 4 * kh:4 * kh + 4, 0:w],
                              ffn_u[layer].rearrange("(k p) n -> p k n", p=128)[:, 4 * kh:4 * kh + 4, o:o + w],
                              writes=[("Wu", q, kh)], cost=c / 2)
                    P.dma("pool", Wd[q][:, 0:w // 128, :],
                          ffn_d[layer].rearrange("(k p) n -> p k n", p=128)[:, o // 128:(o + w) // 128, :],
                          writes=[("Wd", q)], cost=c)
                load_slice(0)
                load_slice(1)
                if layer + 1 < 4:
                    load_mixer_weights(layer + 1)
                for ti, (t0, n, is_s) in enumerate(ftiles):
                    if ti == 0:
                        rmsnorm_tile(st, "f", t0, n, gffn[:, layer, :], xn, ("xn", t0), xoff=t0)
                    else:
                        _rms_ops("f", t0, n, gffn[:, layer, :], xn, ("xn", t0), _norm_scr["f"], xoff=t0)
                hbi = 0
                for j in range(len(widths)):
                    q = j % 2
                    nhc = widths[j] // 128
                    for (t0, n, is_s) in ftiles:
                        hk = hkeys(t0, n)
                        hbuf = hb[hbi % 2]
                        hkey = "hb%d" % (hbi % 2)
                        hbi += 1
                        for hc in range(nhc):
                            bgt = psget()
                            for k in range(8):
                                P.op("pe", lambda e, k=k, hc=hc, bgt=bgt, q=q, t0=t0, n=n: e.matmul(
                                    PS[:, bgt, 0:n], lhsT=Wg[q][:, k, hc * 128:(hc + 1) * 128], rhs=xn[:, k, t0:t0 + n],
                                    start=(k == 0), stop=(k == 7)), reads=[("Wg", q, k // 4), ("xn", t0)], writes=pk(bgt),
                                    cost=n / 2.35 + 6)
                            but = psget()
                            for k in range(8):
                                P.op("pe", lambda e, k=k, hc=hc, but=but, q=q, t0=t0, n=n: e.matmul(
                                    PS[:, but, 0:n], lhsT=Wu[q][:, k, hc * 128:(hc + 1) * 128], rhs=xn[:, k, t0:t0 + n],
                                    start=(k == 0), stop=(k == 7)), reads=[("Wu", q, k // 4), ("xn", t0)], writes=pk(but),
                                    cost=n / 2.35 + 6)
                            slt = sl[hc % 2]
                            slk = "sl%d" % (hc % 2)
                            P.op("act", lambda e, bgt=bgt, slt=slt, n=n: e.activation(out=slt[:, 0:n], in_=PS[:, bgt, 0:n], func=AF.Silu),
                                 reads=pk(bgt), writes=[slk], cost=(224 + n) / 1.2)
                            P.op("dve", lambda e, but=but, slt=slt, hbuf=hbuf, hc=hc, n=n: e.tensor_tensor(
                                out=hbuf[:, hc, 0:n], in0=slt[:, 0:n], in1=PS[:, but, 0:n], op=ALU.mult),
                                reads=pk(but) + [slk], writes=[(hkey, hc)], cost=(160 + n) / 0.96)
                        for fo in range(8):
                            b = psget()
                            for hc in range(nhc):
                                P.op("pe", lambda e, hc=hc, fo=fo, b=b, q=q, hbuf=hbuf, n=n, nhc=nhc: e.matmul(
                                    PS[:, b, 0:n], lhsT=Wd[q][:, hc, fo * 128:(fo + 1) * 128], rhs=hbuf[:, hc, 0:n],
                                    start=(hc == 0), stop=(hc == nhc - 1)), reads=[("Wd", q), (hkey, hc)], writes=pk(b),
                                    cost=n / 2.35 + 6)
                            P.op("dve", lambda e, fo=fo, b=b, t0=t0, n=n: e.tensor_tensor(
                                out=hres[:, fo, t0:t0 + n], in0=hres[:, fo, t0:t0 + n], in1=PS[:, b, 0:n], op=ALU.add),
                                reads=pk(b) + hk, writes=hk, cost=(160 + n) / 0.96)
                    if j + 2 < len(widths):
                        load_slice(j + 2)
                if layer == 3:
                    epilogue(st)
                P.flush()

        for layer in range(4):
            if layer > 0:
                P.next_epoch()
            if layer % 2 == 0:
                even_mixer(layer)
            else:
                odd_mixer(layer)
            ffn(layer)

    return nc


_NC_CACHE = {}


def kernel(**inputs):
    f = lambda a: np.ascontiguousarray(np.asarray(a, dtype=np.float32))
    inp = {k: f(v) for k, v in inputs.items()}
    if "nc" not in _NC_CACHE:
        _NC_CACHE["nc"] = build_nc()
    nc = _NC_CACHE["nc"]
    shared = {}
    for k in ("norm_mix", "norm_ffn", "w_in_even", "w_out_even", "s5_lambda_re", "s5_lambda_im", "s5_log_dt",
              "s5_b_re", "s5_b_im", "s5_c_re", "s5_c_im", "s5_glu_w", "s5_glu_b", "sgu_norm", "sgu_w", "sgu_b",
              "w_in_odd", "w_out_odd", "pool_w", "pool_scale", "conv_w", "conv_b", "ffn_w_gate", "ffn_w_up",
              "ffn_w_down"):
        shared[k] = inp[k]
    shared["norm_final"] = inp["norm_final"].reshape(1, D)
    shared["s5_d"] = inp["s5_d"].reshape(2, 512)
    in_maps = []
    for c in range(NCORES):
        m = dict(shared)
        m["x_p"] = inp["x_prompt"][c]
        m["x_s"] = np.ascontiguousarray(inp["x_sample"][16 * c:16 * c + 16].reshape(NS, D))
        m["st_re"] = np.ascontiguousarray(inp["state_s5_re"][:, 16 * c:16 * c + 16].reshape(2, 16, 2048))
        m["st_im"] = np.ascontiguousarray(inp["state_s5_im"][:, 16 * c:16 * c + 16].reshape(2, 16, 2048))
        m["st_pool"] = np.ascontiguousarray(inp["state_pool"][:, 16 * c:16 * c + 16])
        m["st_conv"] = np.ascontiguousarray(inp["state_conv"][:, 16 * c:16 * c + 16])
        in_maps.append(m)
    res = run_bass_kernel_spmd(nc, in_maps, core_ids=list(range(NCORES)))
    R = res.results
    y_prompt = np.stack([R[c]["y_p"] for c in range(NCORES)], 0).reshape(8, SEQ, D)
    y_sample = np.concatenate([R[c]["y_s"].reshape(16, 4, D) for c in range(NCORES)], 0)
    p_re = np.stack([R[c]["o_p_re"].reshape(2, 32, 64) for c in range(NCORES)], 1)
    p_im = np.stack([R[c]["o_p_im"].reshape(2, 32, 64) for c in range(NCORES)], 1)
    p_pool = np.stack([R[c]["o_p_pool"] for c in range(NCORES)], 1)
    p_conv = np.stack([R[c]["o_p_conv"] for c in range(NCORES)], 1)
    s_re = np.concatenate([R[c]["o_s_re"].reshape(2, 16, 32, 64) for c in range(NCORES)], 1)
    s_im = np.concatenate([R[c]["o_s_im"].reshape(2, 16, 32, 64) for c in range(NCORES)], 1)
    s_v = np.concatenate([R[c]["o_s_v"].reshape(2, 16, 4, 512) for c in range(NCORES)], 1)
    s_pool = np.concatenate([R[c]["o_s_pool"].reshape(2, 16, 15, 512) for c in range(NCORES)], 1)
    s_conv = np.concatenate([R[c]["o_s_conv"].reshape(2, 16, 2, 512) for c in range(NCORES)], 1)
    outs = (y_prompt, y_sample, p_re, p_im, p_pool, p_conv, s_re, s_im, s_v, s_pool, s_conv)
    return tuple(np.ascontiguousarray(o.astype(np.float32)) for o in outs)
```

```python
import math
import numpy as np
from contextlib import ExitStack
import concourse.bass as bass
import concourse.mybir as mybir
from concourse.bass_utils import run_bass_kernel_spmd

F32 = mybir.dt.float32
BF16 = mybir.dt.bfloat16
I32 = mybir.dt.int32
ALU = mybir.AluOpType
AF = mybir.ActivationFunctionType

ENGS = ("pe", "act", "dve", "pool", "sp")
NSLOT = 12
NCORES = 8
D = 1024
SEQ = 2048
NS = 64
T = SEQ + NS
DFF = 2816
EPS = 1e-6
TM = 256
TF = 448
FS = 256
NSL = DFF // FS


class _Op(object):
    __slots__ = ("idx", "eng", "emit", "deps", "signal", "epoch", "semval",
                 "is_dma", "slot", "dval", "prev_dval", "cost", "pos")


class Prog(object):
    def __init__(self, nc, es, n_epochs=6):
        self.nc = nc
        self.ops = []
        self.regions = {}
        self.epoch = 0
        self.n_epochs = n_epochs
        self.sems = {}
        self.cnt = {}
        for e in ENGS:
            for ep in range(n_epochs):
                self.sems[(e, ep)] = es.enter_context(nc.semaphore("s_%s_%d" % (e, ep)))
        self.dsems = {}
        self.dcount = {}
        self.dnext = {}
        for q in ("sp", "pool", "act"):
            self.dnext[q] = 0
            for s in range(NSLOT):
                self.dsems[(q, s)] = es.enter_context(nc.semaphore("d_%s_%d" % (q, s)))
                self.dcount[(q, s)] = 0
        self.nflush = 0
        self.cost = {"pe": 115.0, "act": 450.0, "dve": 430.0, "pool": 600.0, "sp": 100.0}
        self.reorder = True
        self.filler = None
        self.filler_cost = 170.0
        self.nfill = 0

    def next_epoch(self):
        assert not self.ops
        self.epoch = min(self.epoch + 1, self.n_epochs - 1)

    def _add(self, eng, emit, reads, writes, is_dma, cost):
        o = _Op()
        o.idx = len(self.ops)
        o.eng = eng
        o.emit = emit
        o.signal = False
        o.epoch = self.epoch
        o.semval = None
        o.is_dma = is_dma
        o.cost = cost if cost is not None else (3000.0 if is_dma else self.cost[eng])
        deps = set()
        for k in reads:
            r = self.regions.get(k)
            if r is not None and r[0] is not None:
                deps.add(r[0])
        for k in writes:
            r = self.regions.get(k)
            if r is not None:
                if r[0] is not None:
                    deps.add(r[0])
                deps.update(r[1])
        for k in reads:
            r = self.regions.get(k)
            if r is None:
                r = [None, []]
                self.regions[k] = r
            r[1].append(o.idx)
        for k in writes:
            self.regions[k] = [o.idx, []]
        deps.discard(o.idx)
        o.deps = deps
        o.slot = None
        self.ops.append(o)
        return o

    def op(self, eng, emit, reads=(), writes=(), cost=None):
        return self._add(eng, emit, reads, writes, False, cost)

    def dma(self, q, out, in_, reads=(), writes=(), cost=None, **kw):
        def emit(e, out=out, in_=in_, kw=kw):
            return e.dma_start(out=out, in_=in_, **kw)
        return self._add(q, emit, reads, writes, True, cost)

    def _schedule(self):
        ops = self.ops
        n = len(ops)
        succs = [[] for _ in range(n)]
        indeg = [0] * n
        for o in ops:
            for d in o.deps:
                succs[d].append(o.idx)
            indeg[o.idx] = len(o.deps)
        lastd = {}
        dchain = {}
        for o in ops:
            if o.is_dma:
                if o.eng in lastd:
                    dchain[o.idx] = lastd[o.eng]
                lastd[o.eng] = o.idx
        prio = [0.0] * n
        for i in range(n - 1, -1, -1):
            m = 0.0
            for s_ in succs[i]:
                if prio[s_] > m:
                    m = prio[s_]
            prio[i] = ops[i].cost + m
        order = {e: [] for e in ENGS}
        if not self.reorder:
            for o in ops:
                order[o.eng].append(o)
            return order
        ready = {e: [] for e in ENGS}
        ready_t = [0.0] * n
        fin = [0.0] * n
        issued = [False] * n
        free_at = {e: 0.0 for e in ENGS}
        for o in ops:
            if indeg[o.idx] == 0:
                ready[o.eng].append(o.idx)
        remaining = n
        HOP = 150.0
        while remaining:
            best = None
            for e in ENGS:
                rl = ready[e]
                if not rl:
                    continue
                fa = free_at[e]
                cb = None
                for i in rl:
                    o = ops[i]
                    if o.is_dma and i in dchain and not issued[dchain[i]]:
                        continue
                    st = ready_t[i] if ready_t[i] > fa else fa
                    key = (st, -prio[i], i)
                    if cb is None or key < cb:
                        cb = key
                if cb is not None and (best is None or cb < best[0]):
                    best = (cb, e)
            assert best is not None, "scheduler deadlock"
            (st, _, i), e = best
            o = ops[i]
            ready[e].remove(i)
            issued[i] = True
            if e == "pe" and self.filler is not None and free_at[e] > 0.0:
                gap = st - free_at[e]
                if gap > 1200.0:
                    k = min(int((gap - 500.0) / self.filler_cost), 60)
                    for _ in range(k):
                        f = _Op()
                        f.idx = -1
                        f.eng = "pe"
                        f.emit = self.filler
                        f.is_dma = False
                        f.signal = False
                        order[e].append(f)
                    self.nfill += k
            if o.is_dma:
                free_at[e] = st + 80.0
                fin[i] = st + o.cost
            else:
                free_at[e] = st + o.cost
                fin[i] = st + o.cost
            order[e].append(o)
            remaining -= 1
            for s_ in succs[i]:
                t = fin[i] + (0.0 if (ops[s_].eng == e and e == "pe") else HOP)
                if t > ready_t[s_]:
                    ready_t[s_] = t
                indeg[s_] -= 1
                if indeg[s_] == 0:
                    ready[ops[s_].eng].append(s_)
        self.est_time = max(fin) if n else 0.0
        return order

    def flush(self):
        nc = self.nc
        ops = self.ops
        if not ops:
            return
        per_eng = self._schedule()
        for e in ENGS:
            for p_, o in enumerate(per_eng[e]):
                o.pos = p_
            if e != "pe":
                assert all(o.idx >= 0 for o in per_eng[e])
        for e in ("sp", "pool", "act"):
            for o in per_eng[e]:
                if o.is_dma:
                    s = self.dnext[e]
                    self.dnext[e] = (s + 1) % NSLOT
                    o.slot = s
                    o.prev_dval = self.dcount[(e, s)]
                    self.dcount[(e, s)] += 16
                    o.dval = self.dcount[(e, s)]
        red = []
        for o in ops:
            comp = {}
            dmas = []
            for d in o.deps:
                p = ops[d]
                if p.is_dma:
                    dmas.append(d)
                else:
                    if p.eng == "pe" and o.eng == "pe" and not o.is_dma:
                        continue
                    if p.eng not in comp or ops[comp[p.eng]].pos < p.pos:
                        comp[p.eng] = d
            red.append((comp, dmas))
            for d in comp.values():
                ops[d].signal = True
        cnt = self.cnt
        for e in ENGS:
            for o in per_eng[e]:
                if o.idx < 0 or o.is_dma or not o.signal:
                    continue
                key = (o.eng, o.epoch)
                cnt[key] = cnt.get(key, 0) + 1
                o.semval = cnt[key]
        sems = self.sems
        dsems = self.dsems
        n_ep = self.n_epochs
        dcount = self.dcount

        def emit_engine(e, eng_name):
            waited = {}
            dwaited = {}
            for o in per_eng[eng_name]:
                if o.idx < 0:
                    o.emit(e)
                    continue
                comp, dmas = red[o.idx]
                for pe_name, d in comp.items():
                    p = ops[d]
                    done = False
                    for ep in range(p.epoch, n_ep):
                        w = waited.get((pe_name, ep), 0)
                        if ep == p.epoch and w >= p.semval:
                            done = True
                        if ep > p.epoch and w > 0:
                            done = True
                    if done:
                        continue
                    e.wait_ge(sems[(pe_name, p.epoch)], p.semval)
                    waited[(pe_name, p.epoch)] = p.semval
                for d in dmas:
                    p = ops[d]
                    k = (p.eng, p.slot)
                    if dwaited.get(k, 0) >= p.dval:
                        continue
                    e.wait_ge(dsems[k], p.dval)
                    dwaited[k] = p.dval
                if o.is_dma:
                    k = (o.eng, o.slot)
                    if o.prev_dval > 0 and dwaited.get(k, 0) < o.prev_dval:
                        e.wait_ge(dsems[k], o.prev_dval)
                        dwaited[k] = o.prev_dval
                    inst = o.emit(e)
                    inst.then_inc(dsems[k], 16)
                else:
                    inst = o.emit(e)
                    if o.signal:
                        inst.then_inc(sems[(o.eng, o.epoch)], 1)
            if eng_name in ("sp", "pool", "act"):
                for s in range(NSLOT):
                    k = (eng_name, s)
                    if dcount[k] > 0 and dwaited.get(k, 0) < dcount[k]:
                        e.wait_ge(dsems[k], dcount[k])

        with nc.Block() as block:
            @block.tensor
            def _(e):
                emit_engine(e, "pe")

            @block.scalar
            def _(e):
                emit_engine(e, "act")

            @block.vector
            def _(e):
                emit_engine(e, "dve")

            @block.gpsimd
            def _(e):
                emit_engine(e, "pool")

            @block.sync
            def _(e):
                emit_engine(e, "sp")
        self.ops = []
        self.regions = {}
        self.nflush += 1


def build_nc(debug=False):
    nc = bass.Bass("TRN2", target_bir_lowering=False)
    try:
        nc.allow_low_precision("bf16 matmul operands with fp32 accumulation by design")
    except Exception:
        pass

    def din(name, shape):
        return nc.dram_tensor(name, list(shape), F32, kind="ExternalInput").ap()

    def dout(name, shape):
        return nc.dram_tensor(name, list(shape), F32, kind="ExternalOutput").ap()

    x_p = din("x_p", (SEQ, D))
    x_s = din("x_s", (NS, D))
    st_re = din("st_re", (2, 16, 2048))
    st_im = din("st_im", (2, 16, 2048))
    st_pool = din("st_pool", (2, 16, 15, 512))
    st_conv = din("st_conv", (2, 16, 2, 512))
    norm_mix = din("norm_mix", (4, D))
    norm_ffn = din("norm_ffn", (4, D))
    norm_final = din("norm_final", (1, D))
    w_in_even = din("w_in_even", (2, D, 1536))
    w_out_even = din("w_out_even", (2, D, D))
    lam_re = din("s5_lambda_re", (2, 32, 64))
    lam_im = din("s5_lambda_im", (2, 32, 64))
    log_dt = din("s5_log_dt", (2, 32))
    b_re = din("s5_b_re", (2, 32, 64, 16))
    b_im = din("s5_b_im", (2, 32, 64, 16))
    c_re = din("s5_c_re", (2, 32, 16, 64))
    c_im = din("s5_c_im", (2, 32, 16, 64))
    s5_d = din("s5_d", (2, 512))
    glu_w = din("s5_glu_w", (2, 512, 512))
    glu_b = din("s5_glu_b", (2, 512))
    sgu_norm = din("sgu_norm", (2, 512))
    sgu_w = din("sgu_w", (2, 8, 128, 128))
    sgu_b = din("sgu_b", (2, 8, 128))
    w_in_odd = din("w_in_odd", (2, D, 2048))
    w_out_odd = din("w_out_odd", (2, D, D))
    pool_w = din("pool_w", (2, 4, 128, 128))
    pool_scale = din("pool_scale", (2, 512))
    conv_w = din("conv_w", (2, 3, 512))
    conv_b = din("conv_b", (2, 512))
    ffn_g = din("ffn_w_gate", (4, D, DFF))
    ffn_u = din("ffn_w_up", (4, D, DFF))
    ffn_d = din("ffn_w_down", (4, DFF, D))

    y_p = dout("y_p", (SEQ, D))
    y_s = dout("y_s", (NS, D))
    o_p_re = dout("o_p_re", (2, 16, 128))
    o_p_im = dout("o_p_im", (2, 16, 128))
    o_p_pool = dout("o_p_pool", (2, 15, 512))
    o_p_conv = dout("o_p_conv", (2, 2, 512))
    o_s_re = dout("o_s_re", (2, 16, 2048))
    o_s_im = dout("o_s_im", (2, 16, 2048))
    o_s_v = dout("o_s_v", (2, NS, 512))
    o_s_pool = dout("o_s_pool", (2, 240, 512))
    o_s_conv = dout("o_s_conv", (2, 32, 512))
    dbg = dout("dbg", (128, 4096)) if debug else None

    es = ExitStack()
    with es:
        es.enter_context(nc.allow_non_contiguous_dma(reason="small strided parameter loads"))
        P = Prog(nc, es)

        _uid = [0]

        def sb(stk, name, shape, dt=F32):
            _uid[0] += 1
            return stk.enter_context(nc.sbuf_tensor("%s_u%d" % (name, _uid[0]), list(shape), dt))

        PS = es.enter_context(nc.psum_tensor("PS", [128, 8, 512], F32))
        ps_rr = [0]

        def psget(n=1):
            b = ps_rr[0]
            if b + n > 8:
                b = 0
            ps_rr[0] = (b + n) % 8
            return b

        def pk(b, n=1):
            return [("ps", b + i) for i in range(n)]

        hres = sb(es, "hres", [128, 8, T])
        ident = sb(es, "ident", [128, 128])
        identb = sb(es, "identb", [128, 128], BF16)
        onesb = sb(es, "onesb", [128, 128], BF16)
        iot = sb(es, "iot", [128, 128])
        iop = sb(es, "iop", [128, 1])
        pstage = sb(es, "pstage", [128, 128])
        pvec = sb(es, "pvec", [128, 120])
        gmix = pvec[:, 0:32].rearrange("p (l k) -> p l k", l=4)
        gffn = pvec[:, 32:64].rearrange("p (l k) -> p l k", l=4)
        glub = pvec[:, 64:72].rearrange("p (l k) -> p l k", l=2)
        pscale = pvec[:, 72:80].rearrange("p (l k) -> p l k", l=2)
        cw = pvec[:, 80:104].rearrange("p (l c k) -> p l c k", l=2, c=3)
        cb = pvec[:, 104:112].rearrange("p (l k) -> p l k", l=2)
        dcol = pvec[:, 112:120].rearrange("p (l k) -> p l k", l=2)
        epsc = sb(es, "epsc", [128, 1])
        s5car = sb(es, "s5car", [128, 16, 2])
        xchalo = sb(es, "xchalo", [128, 4, 15])
        zhalo = sb(es, "zhalo", [128, 4, 2])

        def V(e):
            return e

        def load_w(dst, src3, key, nsplit):
            K = dst.shape[1]
            step = K // nsplit
            for i in range(nsplit):
                nb = 128 * step * dst.shape[2] * 4
                P.dma("pool", dst[:, i * step:(i + 1) * step, :],
                      src3.rearrange("(k p) n -> p k n", p=128)[:, i * step:(i + 1) * step, :], writes=[(key, i)],
                      cost=2500.0 + nb / 150.0)

        WinP = sb(es, "WinP", [128, 8, 2048], BF16)
        WoutP = sb(es, "WoutP", [128, 8, D], BF16)
        WsmP = sb(es, "WsmP", [128, 2048], BF16)

        def load_mixer_weights(layer, only_in=False, skip_in=False):
            i = layer // 2
            if layer % 2 == 0:
                if not skip_in:
                    load_w(WinP[:, :, 0:1536], w_in_even[i], "Win", 4)
                if only_in:
                    return
                load_w(WsmP[:, :].rearrange("p (k n) -> p k n", k=4), glu_w[i], "Wsm", 1)
                load_w(WoutP, w_out_even[i], "Wout", 2)
            else:
                load_w(WinP, w_in_odd[i], "Win", 4)
                load_w(WoutP, w_out_odd[i], "Wout", 2)
                P.dma("pool", WsmP[:, 0:512].rearrange("p (g d) -> p g d", g=4), pool_w[i].rearrange("g c d -> c g d"),
                      writes=["Wsm"])

        load_mixer_weights(0, only_in=True)
        P.op("pool", lambda e: e.iota(iot[:], pattern=[[1, 128]], base=0, channel_multiplier=0,
                                      allow_small_or_imprecise_dtypes=True), writes=["iot"])
        P.op("pool", lambda e: e.iota(iop[:], pattern=[[1, 1]], base=0, channel_multiplier=1,
                                      allow_small_or_imprecise_dtypes=True), writes=["iop"])
        P.op("dve", lambda e: e.tensor_scalar(out=ident[:], in0=iot[:], scalar1=iop[:, 0:1], scalar2=None,
                                              op0=ALU.is_equal), reads=["iot", "iop"], writes=["ident"])
        P.op("dve", lambda e: e.tensor_copy(out=identb[:], in_=ident[:]), reads=["ident"], writes=["identb"])
        P.op("dve", lambda e: e.memset(onesb[:], 1.0), writes=["onesb"])
        P.op("dve", lambda e: e.memset(epsc[:], EPS), writes=["epsc"])
        P.filler = None; _unused_filler = lambda e: e.matmul(PS[:, 7, 0:128], lhsT=onesb[:], rhs=onesb[:], start=True, stop=True)
        with ExitStack() as st:
            pass
        pst_rows = [(norm_mix.rearrange("l (k p) -> (l k) p", p=128), 32), (norm_ffn.rearrange("l (k p) -> (l k) p", p=128), 32),
                    (glu_b.rearrange("l (k p) -> (l k) p", p=128), 8), (pool_scale.rearrange("l (k p) -> (l k) p", p=128), 8),
                    (conv_w.rearrange("l c (k p) -> (l c k) p", p=128), 24), (conv_b.rearrange("l (k p) -> (l k) p", p=128), 8),
                    (s5_d.rearrange("l (k p) -> (l k) p", p=128), 8)]
        r0 = 0
        for j_, (src_, nr_) in enumerate(pst_rows):
            P.dma("sp", pstage[r0:r0 + nr_, :], src_, writes=[("pstage", j_)])
            r0 += nr_
        P.op("pe", lambda e: e.transpose(PS[:, 5, 0:120], pstage[0:120, :], ident[0:120, 0:120]),
             reads=[("pstage", j_) for j_ in range(7)] + ["ident"], writes=[("ps", 5)])
        P.op("act", lambda e: e.activation(out=pvec[:, :], in_=PS[:, 5, 0:120], func=AF.Copy), reads=[("ps", 5)],
             writes=["gmix", "gffn", "glub", "pscale", "cw", "cb", "dcol"])

        def load_x(st):
            xt = [sb(st, "xt%d" % i, [128, D]) for i in range(2)]
            nsub = SEQ // 128 + 1
            for si in range(nsub):
                n = 128 if si < SEQ // 128 else NS
                src = x_p[si * 128:(si + 1) * 128, :] if si < SEQ // 128 else x_s[:, :]
                xb = xt[si % 2]
                xk = "xt%d" % (si % 2)
                P.dma("sp", xb[0:n, :], src, writes=[xk])
                for half in range(2):
                    b = psget()
                    for q in range(4):
                        k = half * 4 + q
                        P.op("pe", lambda e, b=b, q=q, k=k, xb=xb, n=n: e.transpose(
                            PS[:, b, q * 128:q * 128 + n], xb[0:n, k * 128:(k + 1) * 128], ident[0:n, 0:n]),
                            reads=[xk, "ident"], writes=pk(b))
                    eng = "act" if half == 0 else "dve"
                    if eng == "act":
                        P.op("act", lambda e, b=b, half=half, si=si, n=n: e.activation(
                            out=hres[:, half * 4:half * 4 + 4, si * 128:si * 128 + n],
                            in_=PS[:, b, :].rearrange("p (q t) -> p q t", q=4)[:, :, 0:n], func=AF.Copy),
                            reads=pk(b), writes=[("h", si, half)])
                    else:
                        P.op("dve", lambda e, b=b, half=half, si=si, n=n: e.tensor_copy(
                            out=hres[:, half * 4:half * 4 + 4, si * 128:si * 128 + n],
                            in_=PS[:, b, :].rearrange("p (q t) -> p q t", q=4)[:, :, 0:n]),
                            reads=pk(b), writes=[("h", si, half)])

        mtiles = [(i * TM, TM, False) for i in range(SEQ // TM)] + [(SEQ, NS, True)]
        ftiles = [(0, 448, False), (448, 448, False), (896, 448, False), (1344, 448, False), (1792, 320, False)]

        def hkeys(t0, n):
            ks = []
            a = (t0 // TM) * TM
            while a < t0 + n:
                ks.append(("hres", a))
                a += TM
            return ks

        def s5_setup(stk, i, XB, YC, BD, TC, TS, R4, A4, bmask):
            with ExitStack() as st:
                def t16(name):
                    return sb(st, "s5_" + name, [128, 16])
                LRI = sb(st, "s5_LRI", [128, 32])
                LR = LRI[:, 0:16]
                LI = LRI[:, 16:32]
                LDT, DT, Z, MAG, ANG = [t16(n_) for n_ in ("LDT", "DT", "Z", "MAG", "ANG")]
                SN, CS, ta, tb, tc_, td = [t16(n_) for n_ in ("SN", "CS", "ta", "tb", "tc", "td")]
                FR, FI = t16("FR"), t16("FI")
                AR = [t16("AR%d" % k) for k in range(5)]
                AI = [t16("AI%d" % k) for k in range(5)]
                BR = sb(st, "s5_BR", [128, 16, 32]); BI = sb(st, "s5_BI", [128, 16, 32])
                BBr = sb(st, "s5_BBr", [128, 16, 32]); BBi = sb(st, "s5_BBi", [128, 16, 32])
                CTr = sb(st, "s5_CTr", [128, 16, 32]); CTi = sb(st, "s5_CTi", [128, 16, 32])
                Yr = sb(st, "s5_Yr", [128, 16, 32]); Yi = sb(st, "s5_Yi", [128, 16, 32])
                W1 = sb(st, "s5_W1", [128, 16, 32]); W2 = sb(st, "s5_W2", [128, 16, 32])
                W3 = sb(st, "s5_W3", [128, 16, 32]); W4 = sb(st, "s5_W4", [128, 16, 32])
                Xr_ = sb(st, "s5_Xr", [128, 16, 32]); Xi_ = sb(st, "s5_Xi", [128, 16, 32])
                CNr = sb(st, "s5_CNr", [128, 4, 128]); CNi = sb(st, "s5_CNi", [128, 4, 128])
                cnt = [0]

                def dv(fn, reads, writes, eng="dve"):
                    P.op(eng, fn, reads=reads, writes=writes)

                def tt(out, a, b, op, r, w, eng="dve"):
                    dv(lambda e: e.tensor_tensor(out=out, in0=a, in1=b, op=op), r, w, eng=eng)

                def ts(out, a, s1, op0, r, w, s2=None, op1=None):
                    if op1 is None:
                        dv(lambda e: e.tensor_scalar(out=out, in0=a, scalar1=s1, scalar2=None, op0=op0), r, w)
                    else:
                        dv(lambda e: e.tensor_scalar(out=out, in0=a, scalar1=s1, scalar2=s2, op0=op0, op1=op1), r, w)

                lst = sb(st, "s5_lst", [32, 128])
                P.dma("sp", lst[0:16, :], lam_re[i].rearrange("(P g) n -> P (g n)", g=2), writes=[("lst", 0)])
                P.dma("sp", lst[16:32, :], lam_im[i].rearrange("(P g) n -> P (g n)", g=2), writes=[("lst", 1)])
                bl_ = psget()
                P.op("pe", lambda e: e.transpose(PS[:, bl_, 0:32], lst[:, :], ident[0:32, 0:32]),
                     reads=[("lst", 0), ("lst", 1), "ident"], writes=pk(bl_))
                P.op("act", lambda e: e.activation(out=LRI[:, :], in_=PS[:, bl_, 0:32], func=AF.Copy), reads=pk(bl_),
                     writes=["LR", "LI"])
                for g2 in range(2):
                    P.dma("sp", LDT[64 * g2:64 * g2 + 64, :],
                          log_dt[i:i + 1, :].rearrange("o (P g) -> o g P", g=2)[:, g2, :].broadcast_to([64, 16]),
                          writes=[("LDT", g2)])
                for tl in (BR, BI, CNr, CNi):
                    dv(lambda e, tl=tl: e.memset(tl[:], 0.0), [], ["z_" + tl.name], eng="pool")
                for (tl, src) in ((BR, b_re), (BI, b_im)):
                    for g2 in range(2):
                        P.dma("sp", tl[64 * g2:64 * g2 + 64, :, 16 * g2:16 * g2 + 16],
                              src[i].rearrange("(P g) n q -> g n P q", g=2)[g2], reads=["z_" + tl.name],
                              writes=[("ld_" + tl.name, g2)])
                for (tl, src) in ((CNr, c_re), (CNi, c_im)):
                    for p4 in range(4):
                        for g2 in range(2):
                            P.dma("sp", tl[32 * p4 + 16 * g2:32 * p4 + 16 * g2 + 16, :, 64 * g2:64 * g2 + 64],
                                  src[i].rearrange("(f a g) p n -> a g p f n", a=4, g=2)[p4, g2],
                                  reads=["z_" + tl.name], writes=[("ld_" + tl.name, p4, g2)])
                for (src, dst) in ((CNr, CTr), (CNi, CTi)):
                    b = psget()
                    for ft in range(4):
                        P.op("pe", lambda e, ft=ft, b=b, src=src: e.transpose(
                            PS[:, b, ft * 128:(ft + 1) * 128], src[:, ft, :], ident[:]),
                            reads=[("ld_" + src.name, a_, b_) for a_ in range(4) for b_ in range(2)] + ["ident"], writes=pk(b))
                    P.op("act", lambda e, b=b, dst=dst: e.activation(
                        out=dst[:].rearrange("p a b -> p (a b)"), in_=PS[:, b, :], func=AF.Copy),
                        reads=pk(b), writes=[dst.name])
                dv(lambda e: e.activation(out=DT[:], in_=LDT[:], func=AF.Exp), [("LDT", 0), ("LDT", 1)], ["DT"], eng="act")
                tt(Z[:], LR[:], DT[:], ALU.mult, ["LR", "DT"], ["Z"])
                ts(MAG[:], Z[:], 1.0 / 120.0, ALU.mult, ["Z"], ["MAG"], 1.0 / 24.0, ALU.add)
                for c in (1.0 / 6.0, 0.5, 1.0, 1.0):
                    tt(MAG[:], MAG[:], Z[:], ALU.mult, ["MAG", "Z"], ["MAG"])
                    ts(MAG[:], MAG[:], float(c), ALU.add, ["MAG"], ["MAG"])
                tt(ANG[:], LI[:], DT[:], ALU.mult, ["LI", "DT"], ["ANG"])
                C1 = 6.28125
                C2 = 2.0 * math.pi - C1
                MAGIC = 12582912.0
                for (shift, dst) in ((0.0, SN), (0.5 * math.pi, CS)):
                    ts(ta[:], ANG[:], 1.0 / (2 * math.pi), ALU.mult, ["ANG"], ["ta"], shift / (2 * math.pi), ALU.add)
                    ts(tb[:], ta[:], MAGIC, ALU.add, ["ta"], ["tb"])
                    ts(tb[:], tb[:], -MAGIC, ALU.add, ["tb"], ["tb"])
                    dv(lambda e: e.scalar_tensor_tensor(out=ta[:], in0=tb[:], scalar=-C1, in1=ANG[:],
                                                        op0=ALU.mult, op1=ALU.add), ["tb", "ANG"], ["ta"])
                    dv(lambda e: e.scalar_tensor_tensor(out=ta[:], in0=tb[:], scalar=-C2, in1=ta[:],
                                                        op0=ALU.mult, op1=ALU.add), ["tb", "ta"], ["ta"])
                    ts(ta[:], ta[:], float(shift), ALU.add, ["ta"], ["ta"], math.pi, ALU.min)
                    ts(ta[:], ta[:], -math.pi, ALU.max, ["ta"], ["ta"])
                    dv(lambda e, dst=dst: e.activation(out=dst[:], in_=ta[:], func=AF.Sin), ["ta"], [dst.name],
                       eng="act")
                tt(AR[1][:], MAG[:], CS[:], ALU.mult, ["MAG", CS.name], ["AR1"])
                tt(AI[1][:], MAG[:], SN[:], ALU.mult, ["MAG", SN.name], ["AI1"])
                dv(lambda e: e.memset(AR[0][:], 1.0), [], ["AR0"])
                dv(lambda e: e.memset(AI[0][:], 0.0), [], ["AI0"])

                def cmul(orr, oi, ar, ai, br, bi, rk, wk, t1=None, t2=None, k1="W1", k2="W2", eng="dve"):
                    tt(t1, ar, br, ALU.mult, rk, [k1], eng)
                    tt(t2, ai, bi, ALU.mult, rk, [k2], eng)
                    tt(orr, t1, t2, ALU.subtract, [k1, k2], [wk + "r"], eng)
                    tt(t1, ar, bi, ALU.mult, rk + [wk + "r"], [k1], eng)
                    tt(t2, ai, br, ALU.mult, rk + [wk + "r"], [k2], eng)
                    tt(oi, t1, t2, ALU.add, [k1, k2], [wk + "i"], eng)

                cmul(AR[2][:], AI[2][:], AR[1][:], AI[1][:], AR[1][:], AI[1][:], ["AR1", "AI1"], "A2", tc_[:], td[:], "tc", "td")
                cmul(AR[3][:], AI[3][:], AR[2][:], AI[2][:], AR[1][:], AI[1][:], ["AR1", "AI1", "A2r", "A2i"], "A3",
                     tc_[:], td[:], "tc", "td")
                cmul(AR[4][:], AI[4][:], AR[2][:], AI[2][:], AR[2][:], AI[2][:], ["A2r", "A2i"], "A4", tc_[:], td[:], "tc", "td")
                akeys = {0: ["AR0", "AI0"], 1: ["AR1", "AI1"], 2: ["A2r", "A2i"], 3: ["A3r", "A3i"], 4: ["A4r", "A4i"]}
                dv(lambda e: e.tensor_copy(out=A4[:, :, 0], in_=AR[4][:]), akeys[4], ["A4"])
                dv(lambda e: e.tensor_copy(out=A4[:, :, 1], in_=AI[4][:]), akeys[4] + ["A4"], ["A4"])
                tt(ta[:], MAG[:], MAG[:], ALU.mult, ["MAG"], ["ta"])
                tt(R4[:], ta[:], ta[:], ALU.mult, ["ta"], ["R4"])
                ts(ta[:], AR[1][:], -1.0, ALU.add, ["AR1"], ["ta"])
                tt(tb[:], LR[:], LR[:], ALU.mult, ["LR"], ["tb"])
                tt(tc_[:], LI[:], LI[:], ALU.mult, ["LI", "A4i"], ["tc"])
                tt(tb[:], tb[:], tc_[:], ALU.add, ["tb", "tc"], ["tb"])
                dv(lambda e: e.reciprocal(out=tb[:], in_=tb[:]), ["tb"], ["tb"])
                tt(tc_[:], ta[:], LR[:], ALU.mult, ["ta", "LR"], ["tc"])
                tt(td[:], AI[1][:], LI[:], ALU.mult, ["AI1", "LI", "A4i"], ["td"])
                tt(tc_[:], tc_[:], td[:], ALU.add, ["tc", "td"], ["tc"])
                tt(FR[:], tc_[:], tb[:], ALU.mult, ["tc", "tb"], ["FR"])
                tt(tc_[:], AI[1][:], LR[:], ALU.mult, ["AI1", "LR", "FR"], ["tc"])
                tt(td[:], ta[:], LI[:], ALU.mult, ["ta", "LI", "FR"], ["td"])
                tt(tc_[:], tc_[:], td[:], ALU.subtract, ["tc", "td"], ["tc"])
                tt(FI[:], tc_[:], tb[:], ALU.mult, ["tc", "tb"], ["FI"])
                dv(lambda e: e.reciprocal(out=ta[:], in_=R4[:]), ["R4", "FI"], ["ta"])
                tt(TC[:, :, 0], AR[4][:], ta[:], ALU.mult, akeys[4] + ["ta"], ["TC"])
                tt(TS[:, :, 0], AI[4][:], ta[:], ALU.mult, akeys[4] + ["ta"], ["TS"])
                m = 1
                NCH = TM // 4
                W5 = sb(st, "s5_W5", [128, 16, 32]); W6 = sb(st, "s5_W6", [128, 16, 32])
                while m < NCH:
                    ur = TC[:, :, m - 1:m].broadcast_to([128, 16, m])
                    ui = TS[:, :, m - 1:m].broadcast_to([128, 16, m])
                    w1 = W5[:, :, 0:m]; w2 = W6[:, :, 0:m]
                    tt(w1, TC[:, :, 0:m], ur, ALU.mult, ["TC", "TS"], ["W5"], "pool")
                    tt(w2, TS[:, :, 0:m], ui, ALU.mult, ["TC", "TS"], ["W6"], "pool")
                    tt(TC[:, :, m:2 * m], w1, w2, ALU.subtract, ["W5", "W6"], ["TC"], "pool")
                    tt(w1, TC[:, :, 0:m], ui, ALU.mult, ["TC", "TS"], ["W5"], "pool")
                    tt(w2, TS[:, :, 0:m], ur, ALU.mult, ["TC", "TS"], ["W6"], "pool")
                    tt(TS[:, :, m:2 * m], w1, w2, ALU.add, ["W5", "W6"], ["TS"], "pool")
                    m *= 2

                def bc(a):
                    return a.unsqueeze(2).broadcast_to([128, 16, 32])

                cmul(BBr[:], BBi[:], BR[:], BI[:], bc(FR[:]), bc(FI[:]), [("ld_" + BR.name, 0), ("ld_" + BR.name, 1), ("ld_" + BI.name, 0), ("ld_" + BI.name, 1), "FR", "FI"],
                     "BB", W1[:], W2[:])
                for k in range(4):
                    if k == 0:
                        srcs = (BBr, BBi)
                        skeys = ["BBr", "BBi"]
                    else:
                        cmul(Xr_[:], Xi_[:], BBr[:], BBi[:], bc(AR[k][:]), bc(AI[k][:]), ["BBr", "BBi"] + akeys[k], "Xq",
                             W3[:], W4[:], "W3", "W4", eng="pool")
                        srcs = (Xr_, Xi_)
                        skeys = ["Xqr", "Xqi"]
                    s = 3 - k
                    for ri in range(2):
                        b = psget()
                        for ft in range(4):
                            P.op("pe", lambda e, ft=ft, b=b, src=srcs[ri]: e.transpose(
                                PS[:, b, ft * 128:(ft + 1) * 128],
                                src[:, 4 * ft:4 * ft + 4, :].rearrange("p a b -> p (a b)"), ident[:]),
                                reads=skeys + ["ident"], writes=pk(b))
                        P.op("act", lambda e, b=b, ri=ri, s=s: e.activation(
                            out=XB[:, :, ri, s, :], in_=PS[:, b, :].rearrange("p (f n) -> p f n", f=4), func=AF.Copy),
                            reads=pk(b), writes=["XB"])
                bBD = psget()
                for k in range(5):
                    if k == 0:
                        dv(lambda e: e.tensor_copy(out=Yr[:], in_=CTr[:]), ["s5_CTr", "XB"], ["Yr"])
                        ts(Yi[:], CTi[:], -1.0, ALU.mult, ["s5_CTi", "XB"], ["Yi"])
                    else:
                        cmul(Yr[:], Yi[:], CTr[:], CTi[:], bc(AR[k][:]), bc(AI[k][:]),
                             ["s5_CTr", "s5_CTi", "BD%d" % (k - 1), "YC"] + akeys[k], "Y", W1[:], W2[:])
                        ts(Yi[:], Yi[:], -1.0, ALU.mult, ["Yi"], ["Yi"])
                        P.op("act", lambda e, k=k: e.activation(out=YC[:, :, 0, k - 1, :], in_=Yr[:], func=AF.Copy),
                             reads=["Yr"], writes=["YC"])
                        P.op("act", lambda e, k=k: e.activation(out=YC[:, :, 1, k - 1, :], in_=Yi[:], func=AF.Copy),
                             reads=["Yi"], writes=["YC"])
                    if k < 4:
                        for ft in range(4):
                            o_ = PS[:, bBD, ft * 128:(ft + 1) * 128]
                            P.op("pe", lambda e, ft=ft, o_=o_: e.matmul(
                                o_, lhsT=BBr[:, 4 * ft:4 * ft + 4, :].rearrange("p a b -> p (a b)"),
                                rhs=Yr[:, 4 * ft:4 * ft + 4, :].rearrange("p a b -> p (a b)"), start=True, stop=False),
                                reads=["BBr", "Yr"], writes=pk(bBD))
                            P.op("pe", lambda e, ft=ft, o_=o_: e.matmul(
                                o_, lhsT=BBi[:, 4 * ft:4 * ft + 4, :].rearrange("p a b -> p (a b)"),
                                rhs=Yi[:, 4 * ft:4 * ft + 4, :].rearrange("p a b -> p (a b)"), start=False, stop=True),
                                reads=["BBi", "Yi"], writes=pk(bBD))
                        for ft in range(4):
                            if k == 0:
                                dv(lambda e, ft=ft: e.tensor_tensor(out=W1[:, 0:4, :].rearrange("p a b -> p (a b)"),
                                                                    in0=PS[:, bBD, ft * 128:(ft + 1) * 128],
                                                                    in1=bmask[:], op=ALU.mult),
                                   pk(bBD) + ["bmask"], ["W1"])
                                dv(lambda e, ft=ft: e.scalar_tensor_tensor(
                                    out=BD[:, ft, 0, :], in0=ident[:], scalar=dcol[:, i, ft:ft + 1],
                                    in1=W1[:, 0:4, :].rearrange("p a b -> p (a b)"), op0=ALU.mult, op1=ALU.add),
                                    ["W1", "ident", "dcol"], ["BD0"])
                            else:
                                dv(lambda e, ft=ft, k=k: e.tensor_tensor(out=BD[:, ft, k, :],
                                                                         in0=PS[:, bBD, ft * 128:(ft + 1) * 128],
                                                                         in1=bmask[:], op=ALU.mult),
                                   pk(bBD) + ["bmask"], ["BD%d" % k])

        def even_mixer(layer):
            i = layer // 2
            P.cost.update({"pe": 115.0, "dve": 430.0, "act": 450.0})
            with ExitStack() as st:
                NCH = TM // 4
                Win = WinP
                Wout = WoutP
                Wglu = WsmP[:, :].rearrange("p (k n) -> p k n", k=4)
                XB = sb(st, "XB", [128, 4, 2, 4, 128], BF16)
                YC = sb(st, "YC", [128, 16, 2, 4, 32], BF16)
                BD = sb(st, "BD", [128, 4, 4, 128], BF16)
                TC = sb(st, "TC", [128, 16, NCH]); TS = sb(st, "TS", [128, 16, NCH])
                R4 = sb(st, "R4", [128, 16]); A4 = sb(st, "A4", [128, 16, 2])
                wT = sb(st, "wT", [128, 8, 128], BF16)
                bbc = sb(st, "bbc", [128, 4, 128])
                gsg = sb(st, "gsg", [128, 512])
                wsc = sb(st, "wsc", [128, 4, 16])
                P.dma("sp", gsg[:], sgu_norm[i:i + 1, :].broadcast_to([128, 512]), writes=["gsg"])
                for h in range(8):
                    P.dma("sp", bbc[64 * (h % 2):64 * (h % 2) + 64, h // 2, :],
                          sgu_b[i, h:h + 1, :].broadcast_to([64, 128]), writes=[("bbc", h)])
                for h in range(8):
                    P.dma("sp", wsc[64 * (h % 2):64 * (h % 2) + 64, h // 2, :].rearrange("p (a b) -> p a b", a=4),
                          sgu_w[i, h:h + 1, 0:4, 0:4].broadcast_to([64, 4, 4]), writes=[("wsc", h)])
                P.op("dve", lambda e: e.memset(s5car[:], 0.0), writes=[("s5car", q_) for q_ in range(4)])
                with ExitStack() as st2:
                    trilT = sb(st2, "trilT", [128, 128])
                    bmask = sb(st2, "bmask", [128, 128])
                    j32 = sb(st2, "bm_j32", [128, 128])
                    S4 = sb(st2, "bm_S4", [128, 128])
                    P.op("dve", lambda e: e.tensor_scalar(out=trilT[:], in0=iot[:], scalar1=iop[:, 0:1], scalar2=None,
                                                          op0=ALU.is_ge), reads=["iot", "iop"], writes=["trilT"])
                    P.op("pool", lambda e: e.iota(j32[:], pattern=[[1, 4], [0, 32]], base=0, channel_multiplier=0,
                                                  allow_small_or_imprecise_dtypes=True), writes=["j32"])
                    P.op("dve", lambda e: e.tensor_scalar(out=S4[:], in0=j32[:], scalar1=iop[:, 0:1], scalar2=None,
                                                          op0=ALU.is_equal), reads=["j32", "iop"], writes=["S4"])
                    P.op("pe", lambda e: e.matmul(PS[:, 6, 0:128], lhsT=S4[0:4, :], rhs=S4[0:4, :], start=True, stop=True),
                         reads=["S4"], writes=[("ps", 6)])
                    P.op("dve", lambda e: e.tensor_copy(out=bmask[:], in_=PS[:, 6, 0:128]), reads=[("ps", 6)], writes=["bmask"])
                    wld = [sb(st2, "wld%d" % q, [128, 128]) for q in range(2)]
                    for h in range(8):
                        wl = wld[h % 2]
                        P.dma("sp", wl[:], sgu_w[i, h], writes=["wld%d" % (h % 2)])
                        b = psget()
                        P.op("pe", lambda e, b=b, wl=wl: e.transpose(PS[:, b, 0:128], wl[:], ident[:]),
                             reads=["wld%d" % (h % 2), "ident"], writes=pk(b))
                        P.op("dve", lambda e, b=b, h=h: e.tensor_tensor(out=wT[:, h, :], in0=PS[:, b, 0:128],
                                                                        in1=trilT[:], op=ALU.mult),
                             reads=pk(b) + ["trilT"], writes=["wT"])
                    if layer == 0:
                        load_x(st2)
                    s5_setup(st2, i, XB, YC, BD, TC, TS, R4, A4, bmask)
                    P.flush()
                glubh = sb(st, "glubh", [128, 4])
                P.op("dve", lambda e: e.tensor_scalar(out=glubh[:], in0=glub[:, i, :], scalar1=0.5, scalar2=None, op0=ALU.mult),
                     writes=["glubh"])
                if layer == 0:
                    load_mixer_weights(0, skip_in=True)
                P.op("dve", lambda e: e.tensor_scalar(out=WoutP[:, 0:4, :], in0=WoutP[:, 0:4, :], scalar1=0.5, scalar2=None,
                                                      op0=ALU.mult), reads=[("Wout", 0), ("Wout", 1)], writes=["Wout"])
                xnt = sb(st, "xnt", [128, 8, TM], BF16)
                uaL = [sb(st, "ua%d" % q, [128, 4, TM], BF16) for q in range(2)]
                ubL = [sb(st, "ub%d" % q, [128, 4, TM], BF16) for q in range(2)]
                vn = sb(st, "vn", [128, 512])
                vnbL = [sb(st, "vnb%d" % q, [128, 2, 512], BF16) for q in range(2)]
                vjunk = sb(st, "vjunk", [128, 512], BF16)
                vss = sb(st, "vss", [128, 2])
                ymix = sb(st, "ymix", [128, 8, TM], BF16)
                tA = sb(st, "tA", [128, 4, NCH]); tB = sb(st, "tB", [128, 4, NCH])
                Gin = sb(st, "Gin", [128, 4, 2, NCH])
                wtail = [WinP[:, k_, 1536:2048].bitcast(F32).rearrange("p (a c) -> p a c", a=4) for k_ in range(8)]
                tC, tD, tE, tF, tG, tH = wtail[0:6]
                GsL = [sb(st, "Gs0", [128, 4, 2, NCH]),
                       WinP[:, 6:8, 1536:2048].bitcast(F32).rearrange("p k (a c) -> p a k c", a=4)]
                Hf = sb(st, "Hf", [128, 4, 2, NCH + 1])
                Hb = sb(st, "Hb", [128, 4, 2, NCH], BF16)
                sqy = sb(st, "sqy", [128, TM])
                zf = sb(st, "zf", [128, 4, TM])
                zb = sb(st, "zb", [128, 4, TM], BF16)
                sg2 = sb(st, "sg2", [128, TM])
                stmp = sb(st, "stmp", [128, TM])
                vT = sb(st, "vT", [128, 4, NS])
                sacc = sb(st, "sacc", [128, 16, 4])
                h0s = sb(st, "h0s", [16, 1024])
                h0T = sb(st, "h0T", [128, 16, 2, 16])
                hend = sb(st, "hend", [128, 16, 2, 16])
                hoP = sb(st, "hoP", [16, 2, 128])
                def front(ti, t0, n, is_s):
                    nch = n // 4
                    par = ti % 2
                    ua = uaL[par]; ub = ubL[par]; vnb = vnbL[par]
                    hk = ("hres", t0)
                    rmsnorm_tile(st, "m", t0, n, gmix[:, layer, :], xnt, "xnt") if ti == 0 else \
                        rmsnorm_tile_again("m", t0, n, gmix[:, layer, :], xnt, "xnt")
                    for ft in range(4):
                        b = psget()
                        for k in range(8):
                            P.op("pe", lambda e, k=k, ft=ft, b=b: e.matmul(
                                PS[:, b, 0:n], lhsT=Win[:, k, ft * 128:(ft + 1) * 128], rhs=xnt[:, k, 0:n],
                                start=(k == 0), stop=(k == 7)), reads=["Win", "xnt"], writes=pk(b))
                        P.op("act", lambda e, ft=ft, b=b: e.activation(out=ua[:, ft, 0:n], in_=PS[:, b, 0:n],
                                                                       func=AF.Copy),
                             reads=pk(b), writes=[("ua", par, ft)])
                    for ft in range(4):
                        b = psget()
                        for k in range(8):
                            P.op("pe", lambda e, k=k, ft=ft, b=b: e.matmul(
                                PS[:, b, 0:n], lhsT=Win[:, k, 512 + ft * 128:512 + (ft + 1) * 128], rhs=xnt[:, k, 0:n],
                                start=(k == 0), stop=(k == 7)), reads=["Win", "xnt"], writes=pk(b))
                        P.op("act", lambda e, ft=ft, b=b: e.activation(out=ub[:, ft, 0:n], in_=PS[:, b, 0:n],
                                                                       func=AF.Copy),
                             reads=pk(b), writes=[("ub", par, ft)])
                    nsub = (n + 127) // 128
                    for sj in range(nsub):
                        m = min(128, n - sj * 128)
                        b = psget()
                        for k in range(8):
                            P.op("pe", lambda e, k=k, b=b, sj=sj, m=m: e.matmul(
                                PS[0:m, b, :], lhsT=xnt[:, k, sj * 128:sj * 128 + m], rhs=Win[:, k, 1024:1536],
                                start=(k == 0), stop=(k == 7)), reads=["Win", "xnt"], writes=pk(b))
                        P.op("act", lambda e, b=b, sj=sj, m=m: e.activation(
                            out=vjunk[0:m, :], in_=PS[0:m, b, :], func=AF.Square, accum_out=vss[0:m, sj:sj + 1]),
                            reads=pk(b), writes=["vjunk", ("vss", sj)])
                        P.op("act", lambda e, sj=sj, m=m: e.activation(
                            out=vss[0:m, sj:sj + 1], in_=vss[0:m, sj:sj + 1], func=AF.Ln, bias=epsc[0:m, 0:1],
                            scale=1.0 / 512.0), reads=[("vss", sj), "epsc"], writes=[("vss", sj)], cost=250.0)
                        P.op("act", lambda e, sj=sj, m=m: e.activation(
                            out=vss[0:m, sj:sj + 1], in_=vss[0:m, sj:sj + 1], func=AF.Exp, scale=-0.5),
                            reads=[("vss", sj)], writes=[("vss", sj)], cost=250.0)
                        P.op("dve", lambda e, b=b, sj=sj, m=m: e.scalar_tensor_tensor(
                            out=vnb[0:m, sj, :], in0=PS[0:m, b, :], scalar=vss[0:m, sj:sj + 1], in1=gsg[0:m, :],
                            op0=ALU.mult, op1=ALU.mult), reads=pk(b) + [("vss", sj), "gsg"], writes=[("vnb", par, sj)])
                        if is_s:
                            P.op("dve", lambda e, b=b, sj=sj, m=m: e.scalar_tensor_tensor(
                                out=vn[0:m, :], in0=PS[0:m, b, :], scalar=vss[0:m, sj:sj + 1], in1=gsg[0:m, :],
                                op0=ALU.mult, op1=ALU.mult), reads=pk(b) + [("vss", sj), "gsg"], writes=[("vn", 0)])
                    if is_s:
                        P.dma("sp", o_s_v[i], vn[0:NS, :], reads=[("vn", 0)])
                        for ri in range(2):
                            b = psget()
                            for hf in range(2):
                                P.dma("sp", h0s[:, :], (st_re if ri == 0 else st_im)[i][:, hf * 1024:(hf + 1) * 1024],
                                      writes=["h0s"])
                                for q in range(8):
                                    Pp = hf * 8 + q
                                    P.op("pe", lambda e, b=b, Pp=Pp, q=q: e.transpose(
                                        PS[:, b, Pp * 16:(Pp + 1) * 16], h0s[:, q * 128:(q + 1) * 128],
                                        ident[0:16, 0:16]), reads=["h0s", "ident"], writes=pk(b))
                            P.op("dve", lambda e, b=b, ri=ri: e.tensor_copy(
                                out=h0T[:, :, ri, :], in_=PS[:, b, 0:256].rearrange("p (a b) -> p a b", a=16)),
                                reads=pk(b), writes=["h0T"])
                def back(ti, t0, n, is_s):
                    nch = n // 4
                    par = ti % 2
                    ua = uaL[par]; ub = ubL[par]; vnb = vnbL[par]
                    for ft in range(4):
                        if not is_s:
                            P.op("pool", lambda e, ft=ft: e.tensor_copy(out=Hf[:, :, :, 0], in_=s5car[:, 4 * ft:4 * ft + 4, :]),
                                 reads=[("s5car", ft)], writes=["Hf0"])
                        b4 = psget(4)
                        for p4 in range(4):
                            for ri in range(2):
                                for s in range(4):
                                    P.op("pe", lambda e, p4=p4, ri=ri, s=s, ft=ft, b4=b4: e.matmul(
                                        PS[:, b4 + p4, ri * NCH:ri * NCH + nch],
                                        lhsT=XB[32 * p4:32 * p4 + 32, ft, ri, s, :],
                                        rhs=ua[32 * p4:32 * p4 + 32, ft, s:n:4],
                                        start=(s == 0), stop=(s == 3), tile_position=(32 * p4, 0)),
                                        reads=["XB", ("ua", par, ft)], writes=pk(b4, 4), cost=40.0)
                        Xr = PS[:, b4:b4 + 4, 0:nch]
                        Xi = PS[:, b4:b4 + 4, NCH:NCH + nch]
                        if not is_s:
                            Cc = TC[:, 4 * ft:4 * ft + 4, 0:nch]
                            Ss = TS[:, 4 * ft:4 * ft + 4, 0:nch]
                            tAa = tA[:, :, 0:nch]; tBb = tB[:, :, 0:nch]
                            GinR = Gin[:, :, 0, 0:nch]; GinI = Gin[:, :, 1, 0:nch]
                            x4 = pk(b4, 4)
                            gq = ft % 2
                            Gsq = GsL[gq]

                            def tt(out, a, bb, op, r, w, eng="dve"):
                                P.op(eng, lambda e: e.tensor_tensor(out=out, in0=a, in1=bb, op=op), reads=r, writes=w)
                            tCc = tC[:, :, 0:nch]; tDd = tD[:, :, 0:nch]
                            tt(tAa, Xr, Cc, ALU.mult, x4 + ["TC"], ["tA"])
                            tt(tBb, Xi, Ss, ALU.mult, x4 + ["TS"], ["tB"])
                            tt(tCc, Xi, Cc, ALU.mult, x4 + ["TC"], ["tC"])
                            tt(tDd, Xr, Ss, ALU.mult, x4 + ["TS"], ["tD"])
                            tt(GinR, tAa, tBb, ALU.add, ["tA", "tB"], ["GinR"])
                            tt(GinI, tCc, tDd, ALU.subtract, ["tC", "tD"], ["GinI"])
                            for p4 in range(4):
                                Pp = 4 * ft + p4
                                for ri in range(2):
                                    P.op("dve", lambda e, p4=p4, ri=ri, Pp=Pp, Gsq=Gsq: e.tensor_tensor_scan(
                                        out=Gsq[:, p4, ri, 0:nch], data0=R4[:, Pp:Pp + 1].broadcast_to([128, nch]),
                                        data1=Gin[:, p4, ri, 0:nch], initial=s5car[:, Pp, ri:ri + 1],
                                        op0=ALU.mult, op1=ALU.add),
                                        reads=["GinR" if ri == 0 else "GinI", "R4", ("s5car", ft)],
                                        writes=[("Gs", gq, p4, ri)], cost=350.0)
                            GR = Gsq[:, :, 0, 0:nch]; GI = Gsq[:, :, 1, 0:nch]
                            gk = [("Gs", gq, a_, b_) for a_ in range(4) for b_ in range(2)]
                            tEe = tE[:, :, 0:nch]; tFf = tF[:, :, 0:nch]; tGg = tG[:, :, 0:nch]; tHh = tH[:, :, 0:nch]
                            tt(tEe, GR, Cc, ALU.mult, gk + ["TC"], ["tE"], "pool")
                            tt(tFf, GI, Ss, ALU.mult, gk + ["TS"], ["tF"], "pool")
                            tt(tGg, GR, Ss, ALU.mult, gk + ["TS"], ["tG"], "pool")
                            tt(tHh, GI, Cc, ALU.mult, gk + ["TC"], ["tH"], "pool")
                            tt(Hf[:, :, 0, 1:nch + 1], tEe, tFf, ALU.subtract, ["tE", "tF", "Hf0"], ["HfR"], "pool")
                            tt(Hf[:, :, 1, 1:nch + 1], tGg, tHh, ALU.add, ["tG", "tH", "Hf0"], ["HfI"], "pool")
                            P.op("act", lambda e: e.activation(out=Hb[:, :, :, 0:nch], in_=Hf[:, :, :, 0:nch], func=AF.Copy),
                                 reads=["HfR", "HfI", "Hf0"], writes=["Hb"])
                            P.op("pool", lambda e, ft=ft: e.tensor_copy(out=s5car[:, 4 * ft:4 * ft + 4, :],
                                                                        in_=Hf[:, :, :, nch]),
                                 reads=["HfR", "HfI"] + gk, writes=[("s5car", ft)])
                        else:
                            h0r = h0T[:, 4 * ft:4 * ft + 4, 0, :]; h0i = h0T[:, 4 * ft:4 * ft + 4, 1, :]
                            a4r = A4[:, 4 * ft:4 * ft + 4, 0:1].broadcast_to([128, 4, 16])
                            a4i = A4[:, 4 * ft:4 * ft + 4, 1:2].broadcast_to([128, 4, 16])
                            tAa = tA[:, :, 0:16]; tBb = tB[:, :, 0:16]
                            x4 = pk(b4, 4)

                            def tt(out, a, bb, op, r, w):
                                P.op("dve", lambda e: e.tensor_tensor(out=out, in0=a, in1=bb, op=op), reads=r, writes=w)
                            tt(tAa, h0r, a4r, ALU.mult, ["h0T", "A4"], ["tA"])
                            tt(tBb, h0i, a4i, ALU.mult, ["h0T", "A4"], ["tB"])
                            tt(tAa, tAa, tBb, ALU.subtract, ["tA", "tB"], ["tA"])
                            tt(hend[:, 4 * ft:4 * ft + 4, 0, :], tAa, Xr, ALU.add, ["tA"] + x4, [("hend", ft, 0)])
                            tt(tAa, h0r, a4i, ALU.mult, ["h0T", "A4", ("hend", ft, 0)], ["tA"])
                            tt(tBb, h0i, a4r, ALU.mult, ["h0T", "A4", ("hend", ft, 0)], ["tB"])
                            tt(tAa, tAa, tBb, ALU.add, ["tA", "tB"], ["tA"])
                            tt(hend[:, 4 * ft:4 * ft + 4, 1, :], tAa, Xi, ALU.add, ["tA"] + x4, [("hend", ft, 1)])
                            P.op("act", lambda e, ft=ft: e.activation(out=Hb[:, :, :, 0:16],
                                                                      in_=h0T[:, 4 * ft:4 * ft + 4, :, :], func=AF.Copy),
                                 reads=["h0T"], writes=["Hb"])
                        by = psget()
                        for t in range(4):
                            o_ = PS[:, by, t * NCH:t * NCH + nch]
                            for tau in range(t + 1):
                                P.op("pe", lambda e, t=t, tau=tau, ft=ft, o_=o_: e.matmul(
                                    o_, lhsT=BD[:, ft, tau, :], rhs=ua[:, ft, (t - tau):n:4],
                                    start=(tau == 0), stop=False), reads=[("ua", par, ft)], writes=pk(by), cost=60.0)
                            for p4 in range(4):
                                for ri in range(2):
                                    last = (ri == 1)
                                    P.op("pe", lambda e, t=t, p4=p4, ri=ri, ft=ft, by=by, last=last: e.matmul(
                                        PS[32 * p4:32 * p4 + 32, by, t * NCH:t * NCH + nch],
                                        lhsT=YC[:, 4 * ft + p4, ri, t, :], rhs=Hb[:, p4, ri, 0:nch],
                                        start=False, stop=last, tile_position=(0, 32 * p4)),
                                        reads=["Hb"], writes=pk(by), cost=45.0)
                        yv = PS[:, by, 0:4 * NCH].rearrange("p (t c) -> p c t", t=4)[:, 0:nch, :]
                        sq3 = sqy[:, 0:n].rearrange("p (c t) -> p c t", t=4)
                        z3 = zf[:, ft, 0:n].rearrange("p (c t) -> p c t", t=4)
                        P.op("act", lambda e, yv=yv, z3=z3: e.activation(out=z3, in_=yv, func=AF.Gelu_apprx_tanh),
                             reads=pk(by), writes=[("zf", ft)])
                        P.op("act", lambda e, ft=ft: e.activation(out=zb[:, ft, 0:n], in_=zf[:, ft, 0:n], func=AF.Copy),
                             reads=[("zf", ft)], writes=[("zb", ft)])
                    for fo in range(4):
                        b = psget()
                        for fi in range(4):
                            P.op("pe", lambda e, fi=fi, fo=fo, b=b: e.matmul(
                                PS[:, b, 0:n], lhsT=Wglu[:, fi, fo * 128:(fo + 1) * 128], rhs=zb[:, fi, 0:n],
                                start=(fi == 0), stop=(fi == 3)), reads=["Wsm", ("Wsm", 0)] + [("zb", q) for q in range(4)],
                                writes=pk(b))
                        P.op("act", lambda e, fo=fo, b=b: e.activation(out=sg2[:, 0:n], in_=PS[:, b, 0:n], func=AF.Tanh,
                                                                       bias=glubh[:, fo:fo + 1], scale=0.5),
                             reads=pk(b) + ["glubh"], writes=["sg2"])
                        P.op("dve", lambda e, fo=fo: e.scalar_tensor_tensor(out=ymix[:, fo, 0:n], in0=sg2[:, 0:n], scalar=1.0,
                                                                            in1=zf[:, fo, 0:n], op0=ALU.add, op1=ALU.mult),
                             reads=["sg2", ("zf", fo)], writes=["ymix"])
                    if not is_s:
                        for hp in range(4):
                            b = psget()
                            for j in range(n // 128):
                                for h2 in range(2):
                                    h = 2 * hp + h2
                                    P.op("pe", lambda e, b=b, j=j, h2=h2, h=h: e.matmul(
                                        PS[64 * h2:64 * h2 + 64, b, j * 128:(j + 1) * 128],
                                        lhsT=vnb[:, j, h * 64:(h + 1) * 64], rhs=wT[:, h, :],
                                        start=True, stop=True, tile_position=(0, 64 * h2)),
                                        reads=["wT", ("vnb", par, j)], writes=pk(b))
                            P.op("dve", lambda e, b=b, hp=hp: e.tensor_tensor(
                                out=stmp[:, 0:n].rearrange("p (j i) -> p j i", i=128),
                                in0=PS[:, b, 0:n].rearrange("p (j i) -> p j i", i=128),
                                in1=bbc[:, hp, :].unsqueeze(1).broadcast_to([128, n // 128, 128]), op=ALU.add),
                                reads=pk(b) + ["bbc"], writes=["stmp"])
                            P.op("dve", lambda e, hp=hp: e.tensor_tensor(out=ymix[:, 4 + hp, 0:n], in0=stmp[:, 0:n],
                                                                         in1=ub[:, hp, 0:n], op=ALU.mult),
                                 reads=["stmp", ("ub", par, hp)], writes=["ymix"])
                    else:
                        b = psget()
                        for ft in range(4):
                            P.op("pe", lambda e, b=b, ft=ft: e.transpose(PS[:, b, ft * NS:(ft + 1) * NS],
                                                                         vn[0:NS, ft * 128:(ft + 1) * 128],
                                                                         ident[0:NS, 0:NS]),
                                 reads=[("vn", 0), "ident"], writes=pk(b))
                        P.op("dve", lambda e, b=b: e.tensor_copy(out=vT[:], in_=PS[:, b, 0:4 * NS].rearrange("p (f t) -> p f t", f=4)),
                             reads=pk(b), writes=["vT"])
                        for ft in range(4):
                            v3 = vT[:, ft, :].rearrange("p (b j) -> p b j", j=4)
                            for ii in range(4):
                                P.op("dve", lambda e, ft=ft, ii=ii, v3=v3: e.tensor_scalar(
                                    out=sacc[:, :, ii], in0=v3[:, :, 0], scalar1=wsc[:, ft, 4 * ii:4 * ii + 1],
                                    scalar2=bbc[:, ft, ii:ii + 1], op0=ALU.mult, op1=ALU.add),
                                    reads=["vT", "wsc", "bbc"], writes=["sacc"])
                                for jj in range(1, ii + 1):
                                    P.op("dve", lambda e, ft=ft, ii=ii, jj=jj, v3=v3: e.scalar_tensor_tensor(
                                        out=sacc[:, :, ii], in0=v3[:, :, jj], scalar=wsc[:, ft, 4 * ii + jj:4 * ii + jj + 1],
                                        in1=sacc[:, :, ii], op0=ALU.mult, op1=ALU.add),
                                        reads=["vT", "wsc", "sacc"], writes=["sacc"])
                            P.op("dve", lambda e, ft=ft: e.tensor_tensor(
                                out=ymix[:, 4 + ft, 0:NS], in0=sacc[:].rearrange("p b i -> p (b i)"),
                                in1=ub[:, ft, 0:NS], op=ALU.mult), reads=["sacc", ("ub", par, ft)], writes=["ymix"])
                    out_proj_tile(Wout, "Wout", ymix, "ymix", t0, n)
                    if is_s:
                        for ri in range(2):
                            for hf in range(2):
                                for h2 in range(2):
                                    half = hf * 2 + h2
                                    b = psget()
                                    for q in range(4):
                                        Pp = half * 4 + q
                                        P.op("pe", lambda e, b=b, q=q, Pp=Pp, ri=ri: e.transpose(
                                            PS[0:16, b, q * 128:(q + 1) * 128], hend[:, Pp, ri, :], ident[:]),
                                            reads=[("hend", Pp // 4, ri), "ident"], writes=pk(b))
                                    P.op("act", lambda e, b=b, h2=h2: e.activation(
                                        out=h0s[:, h2 * 512:(h2 + 1) * 512], in_=PS[0:16, b, :], func=AF.Copy),
                                        reads=pk(b), writes=["h0s"])
                                P.dma("sp", (o_s_re if ri == 0 else o_s_im)[i][:, hf * 1024:(hf + 1) * 1024], h0s[:, :],
                                      reads=["h0s"])
                    if (not is_s) and t0 + n == SEQ:
                        for ri in range(2):
                            b = psget()
                            P.op("pe", lambda e, b=b, ri=ri: e.transpose(PS[0:16, b, 0:128], s5car[:, :, ri], ident[:]),
                                 reads=[("s5car", q_) for q_ in range(4)] + ["ident"], writes=pk(b))
                            P.op("act", lambda e, b=b, ri=ri: e.activation(out=hoP[:, ri, :], in_=PS[0:16, b, 0:128], func=AF.Copy),
                                 reads=pk(b), writes=["hoP"])
                        P.dma("sp", o_p_re[i], hoP[:, 0, :], reads=["hoP"])
                        P.dma("sp", o_p_im[i], hoP[:, 1, :], reads=["hoP"])
                seq = list(enumerate(mtiles))
                for idx, (ti, (t0, n, is_s)) in enumerate(seq):
                    front(ti, t0, n, is_s)
                    if idx >= 1:
                        pti, (pt0, pn, ps_) = seq[idx - 1]
                        back(pti, pt0, pn, ps_)
                lti, (lt0, ln, ls_) = seq[-1]
                back(lti, lt0, ln, ls_)
                P.flush()

        _norm_scr = {}

        def rmsnorm_tile_again(tag, t0, n, gvec, xn_out, xn_key):
            _rms_ops(tag, t0, n, gvec, xn_out, xn_key, _norm_scr[tag])

        def _rms_ops(tag, t0, n, gvec, xn_out, xn_key, srs, xoff=0):
            sr, sr2 = srs
            hk = hkeys(t0, n)
            sqv = xn_out[:, :, xoff:xoff + n]
            P.op("act", lambda e: e.activation(out=sqv, in_=hres[:, :, t0:t0 + n], func=AF.Square),
                 reads=hk, writes=[xn_key])
            b = psget()
            for k in range(8):
                P.op("pe", lambda e, k=k, b=b: e.matmul(PS[:, b, 0:n], lhsT=onesb[:], rhs=xn_out[:, k, xoff:xoff + n],
                                                        start=(k == 0), stop=(k == 7)),
                     reads=[xn_key, "onesb"], writes=pk(b))
            P.op("act", lambda e, b=b: e.activation(out=sr[:, 0:n], in_=PS[:, b, 0:n], func=AF.Ln,
                                                    bias=epsc[:, 0:1], scale=1.0 / D),
                 reads=pk(b) + ["epsc"], writes=["sr_" + tag])
            P.op("act", lambda e: e.activation(out=sr[:, 0:n], in_=sr[:, 0:n], func=AF.Exp, scale=-0.5),
                 reads=["sr_" + tag], writes=["sr_" + tag])
            for k in range(8):
                P.op("dve", lambda e, k=k: e.scalar_tensor_tensor(
                    out=xn_out[:, k, xoff:xoff + n], in0=hres[:, k, t0:t0 + n], scalar=gvec[:, k:k + 1],
                    in1=sr2[:, 0:n], op0=ALU.mult, op1=ALU.mult),
                    reads=hk + ["sr_" + tag, "gmix", "gffn"], writes=[xn_key])

        def rmsnorm_tile(stk, tag, t0, n, gvec, xn_out, xn_key, xoff=0):
            nmax = TM if tag == "m" else TF
            sr = sb(stk, "sr_" + tag, [128, nmax])
            sr2 = sr
            _norm_scr[tag] = (sr, sr2)
            _rms_ops(tag, t0, n, gvec, xn_out, xn_key, (sr, sr2), xoff=xoff)

        def out_proj_tile(Wout, wkey, ymix, ykey, t0, n):
            hk = hkeys(t0, n)
            for fo in range(8):
                b = psget()
                for k in range(8):
                    P.op("pe", lambda e, k=k, fo=fo, b=b: e.matmul(
                        PS[:, b, 0:n], lhsT=Wout[:, k, fo * 128:(fo + 1) * 128], rhs=ymix[:, k, 0:n],
                        start=(k == 0), stop=(k == 7)), reads=[wkey, ykey], writes=pk(b))
                P.op("dve", lambda e, fo=fo, b=b: e.tensor_tensor(
                    out=hres[:, fo, t0:t0 + n], in0=hres[:, fo, t0:t0 + n], in1=PS[:, b, 0:n], op=ALU.add),
                    reads=pk(b) + hk, writes=hk)

        def odd_mixer(layer):
            i = layer // 2
            P.cost.update({"pe": 115.0, "dve": 430.0, "act": 450.0})
            with ExitStack() as st:
                Win = WinP
                Wout = WoutP
                Wp = WsmP[:, 0:512].rearrange("p (g d) -> p g d", g=4)
                xnt = sb(st, "xnto", [128, 8, TM], BF16)
                XCL = [sb(st, "XC%d" % q, [128, 4, 15 + TM]) for q in range(2)]
                PA = sb(st, "PA", [128, 15 + TM]); PB = sb(st, "PB", [128, 15 + TM])
                diff = sb(st, "diff", [128, 4, TM], BF16)
                xdL = [sb(st, "xd%d" % q, [128, 4, TM]) for q in range(2)]
                bgL = [sb(st, "bg%d" % q, [128, 4, TM]) for q in range(2)]
                ZL = [sb(st, "Z%d" % q, [128, 4, 2 + TM]) for q in range(2)]
                ca = sb(st, "ca", [128, TM])
                ymix = sb(st, "ymixo", [128, 8, TM], BF16)
                invn = sb(st, "invn", [128, 4, 15])
                XCs = sb(st, "XCs", [128, 4, 16, 19])
                PAs = sb(st, "PAs", [128, 16, 19]); PBs = sb(st, "PBs", [128, 16, 19])
                Zs = sb(st, "Zs", [128, 4, 16, 6])
                spl = [sb(st, "spl%d" % q, [128, 512]) for q in range(2)]
                scl = sb(st, "scl", [32, 512])
                otp = sb(st, "otp", [128, 512])
                otc = sb(st, "otc", [32, 512])
                opp = sb(st, "opp", [16, 512])
                opc = sb(st, "opc", [2, 512])
                xct = sb(st, "xct", [128, 128])
                zct = sb(st, "zct", [128, 32])
                P.op("pool", lambda e: e.iota(invn[:], pattern=[[0, 4], [1, 15]], base=1, channel_multiplier=0,
                                              allow_small_or_imprecise_dtypes=True), writes=["invn"])
                for gi in range(4):
                    P.op("dve", lambda e, gi=gi: e.tensor_scalar(out=invn[:, gi, :], in0=invn[:, gi, :],
                                                                 scalar1=float(2 ** (gi + 1)), scalar2=None, op0=ALU.min),
                         reads=["invn"], writes=["invn"])
                P.op("dve", lambda e: e.reciprocal(out=invn[:], in_=invn[:]), reads=["invn"], writes=["invn"])
                P.op("dve", lambda e: e.memset(xchalo[:], 0.0), writes=["xchalo"])
                P.op("dve", lambda e: e.memset(zhalo[:], 0.0), writes=["zhalo"])
                P.op("pool", lambda e: e.memset(PA[:], 0.0), writes=["P0"])
                P.op("pool", lambda e: e.memset(PB[:], 0.0), writes=["P1"])
                P.op("pool", lambda e: e.memset(PAs[:], 0.0), writes=["Ps0"])
                P.op("pool", lambda e: e.memset(PBs[:], 0.0), writes=["Ps1"])
                def front(ti, t0, n, is_s):
                    par = ti % 2
                    XC = XCL[par]; xd = xdL[par]; bg = bgL[par]; Z = ZL[par]
                    if ti == 0:
                        rmsnorm_tile(st, "m", t0, n, gmix[:, layer, :], xnt, "xnt")
                    else:
                        rmsnorm_tile_again("m", t0, n, gmix[:, layer, :], xnt, "xnt")
                    if not is_s:
                        P.op("dve", lambda e: e.tensor_copy(out=XC[:, :, 0:15], in_=xchalo[:]), reads=["xchalo"],
                             writes=[("XChalo", par)])
                        P.op("dve", lambda e: e.tensor_copy(out=Z[:, :, 0:2], in_=zhalo[:]), reads=["zhalo"],
                             writes=[("Zhalo", par)])
                    else:
                        P.dma("sp", spl[0][:], st_pool[i].rearrange("b r c -> (b r) c")[0:128, :], writes=["spl0"])
                        P.dma("sp", spl[1][0:112, :], st_pool[i].rearrange("b r c -> (b r) c")[128:240, :], writes=["spl1"])
                        P.dma("sp", scl[:], st_conv[i].rearrange("b r c -> (b r) c"), writes=["scl"])
                        for ft in range(4):
                            b = psget()
                            P.op("pe", lambda e, b=b, ft=ft: e.transpose(PS[:, b, 0:128], spl[0][:, ft * 128:(ft + 1) * 128], ident[:]),
                                 reads=["spl0", "ident"], writes=pk(b))
                            P.op("pe", lambda e, b=b, ft=ft: e.transpose(PS[:, b, 128:240], spl[1][0:112, ft * 128:(ft + 1) * 128],
                                                                         ident[0:112, 0:112]),
                                 reads=["spl1", "ident"], writes=pk(b))
                            P.op("pe", lambda e, b=b, ft=ft: e.transpose(PS[:, b, 256:288], scl[:, ft * 128:(ft + 1) * 128],
                                                                         ident[0:32, 0:32]),
                                 reads=["scl", "ident"], writes=pk(b))
                            P.op("dve", lambda e, b=b, ft=ft: e.tensor_copy(
                                out=XCs[:, ft, :, 0:15], in_=PS[:, b, 0:240].rearrange("p (b r) -> p b r", r=15)),
                                reads=pk(b), writes=[("XCs", ft)])
                            P.op("dve", lambda e, b=b, ft=ft: e.tensor_copy(
                                out=Zs[:, ft, :, 0:2], in_=PS[:, b, 256:288].rearrange("p (b r) -> p b r", r=2)),
                                reads=pk(b), writes=[("Zs", ft)])
                    for ft in range(4):
                        b = psget()
                        for k in range(8):
                            P.op("pe", lambda e, k=k, ft=ft, b=b: e.matmul(
                                PS[:, b, 0:n], lhsT=Win[:, k, ft * 128:(ft + 1) * 128], rhs=xnt[:, k, 0:n],
                                start=(k == 0), stop=(k == 7)), reads=["Win", "xnt"], writes=pk(b))
                        if not is_s:
                            P.op("act", lambda e, ft=ft, b=b: e.activation(out=XC[:, ft, 15:15 + n], in_=PS[:, b, 0:n], func=AF.Copy),
                                 reads=pk(b), writes=[("XC", par, ft)])
                        else:
                            P.op("act", lambda e, ft=ft, b=b: e.activation(
                                out=XCs[:, ft, :, 15:19], in_=PS[:, b, 0:NS].rearrange("p (b t) -> p b t", t=4), func=AF.Copy),
                                reads=pk(b) + [("XCs", ft)], writes=[("XCs", ft)])
                    for ft in range(4):
                        b = psget()
                        for k in range(8):
                            P.op("pe", lambda e, k=k, ft=ft, b=b: e.matmul(
                                PS[:, b, 0:n], lhsT=Win[:, k, 512 + ft * 128:512 + (ft + 1) * 128], rhs=xnt[:, k, 0:n],
                                start=(k == 0), stop=(k == 7)), reads=["Win", "xnt"], writes=pk(b))
                        P.op("act", lambda e, ft=ft, b=b: e.activation(out=xd[:, ft, 0:n], in_=PS[:, b, 0:n], func=AF.Copy),
                             reads=pk(b), writes=[("xd", par, ft)])
                    for ft in range(4):
                        b = psget()
                        for k in range(8):
                            P.op("pe", lambda e, k=k, ft=ft, b=b: e.matmul(
                                PS[:, b, 0:n], lhsT=Win[:, k, 1024 + ft * 128:1024 + (ft + 1) * 128], rhs=xnt[:, k, 0:n],
                                start=(k == 0), stop=(k == 7)), reads=["Win", "xnt"], writes=pk(b))
                        P.op("act", lambda e, ft=ft, b=b: e.activation(out=bg[:, ft, 0:n], in_=PS[:, b, 0:n], func=AF.Copy),
                             reads=pk(b), writes=[("bg", par, ft)])
                    for ft in range(4):
                        b = psget()
                        for k in range(8):
                            P.op("pe", lambda e, k=k, ft=ft, b=b: e.matmul(
                                PS[:, b, 0:n], lhsT=Win[:, k, 1536 + ft * 128:1536 + (ft + 1) * 128], rhs=xnt[:, k, 0:n],
                                start=(k == 0), stop=(k == 7)), reads=["Win", "xnt"], writes=pk(b))
                        if not is_s:
                            P.op("dve", lambda e, ft=ft, b=b: e.tensor_tensor(out=Z[:, ft, 2:2 + n], in0=PS[:, b, 0:n],
                                                                              in1=xd[:, ft, 0:n], op=ALU.mult),
                                 reads=pk(b) + [("xd", par, ft)], writes=[("Z", par, ft)])
                        else:
                            P.op("dve", lambda e, ft=ft, b=b: e.tensor_tensor(
                                out=Zs[:, ft, :, 2:6], in0=PS[:, b, 0:NS].rearrange("p (b t) -> p b t", t=4),
                                in1=xd[:, ft, 0:NS].rearrange("p (b t) -> p b t", t=4), op=ALU.mult),
                                reads=pk(b) + [("xd", par, ft), ("Zs", ft)], writes=[("Zs", ft)])
                    if not is_s:
                        P.op("dve", lambda e: e.tensor_copy(out=xchalo[:], in_=XC[:, :, n:n + 15]),
                             reads=[("XC", par, q) for q in range(4)] + [(("XChalo", par), par)], writes=["xchalo"])
                        P.op("dve", lambda e: e.tensor_copy(out=zhalo[:], in_=Z[:, :, n:n + 2]),
                             reads=[("Z", par, q) for q in range(4)] + [(("Zhalo", par), par)], writes=["zhalo"])
                def back(ti, t0, n, is_s):
                    par = ti % 2
                    XC = XCL[par]; xd = xdL[par]; bg = bgL[par]; Z = ZL[par]
                    for gi in range(4):
                        w = 2 ** (gi + 1)
                        if not is_s:
                            L = 15 + n
                            src = XC[:, gi, 0:L]
                            bufs = [PA, PB]
                            cur = src
                            ckey = [("XC", par, gi), ("XChalo", par)]
                            d = 1
                            q = 0
                            while d < w:
                                dst = bufs[q % 2]
                                dk = "P%d" % (q % 2)
                                P.op("pool", lambda e, cur=cur, dst=dst, d=d, L=L: e.tensor_tensor(
                                    out=dst[:, d:L], in0=cur[:, d:L], in1=cur[:, 0:L - d], op=ALU.add),
                                    reads=ckey, writes=[dk], cost=800.0)
                                cur = dst[:, 0:L]
                                ckey = [dk]
                                d *= 2
                                q += 1
                            P.op("dve", lambda e, cur=cur, gi=gi, w=w: e.scalar_tensor_tensor(
                                out=diff[:, gi, 0:n], in0=cur[:, 15:15 + n], scalar=1.0 / w, in1=XC[:, gi, 15:15 + n],
                                op0=ALU.mult, op1=ALU.subtract), reads=ckey + [("XC", par, gi)], writes=[("diff", gi)])
                            if t0 == 0:
                                P.op("dve", lambda e, cur=cur, gi=gi: e.tensor_tensor(
                                    out=ca[:, 0:15], in0=cur[:, 15:30], in1=invn[:, gi, :], op=ALU.mult),
                                    reads=ckey + ["invn"], writes=["ca"])
                                P.op("dve", lambda e, gi=gi: e.tensor_tensor(
                                    out=diff[:, gi, 0:15], in0=ca[:, 0:15], in1=XC[:, gi, 15:30], op=ALU.subtract),
                                    reads=["ca", ("XC", par, gi), ("diff", gi)], writes=[("diff", gi)])
                        else:
                            L = 19
                            cur = XCs[:, gi, :, :]
                            ckey = [("XCs", gi)]
                            bufs = [PAs, PBs]
                            d = 1
                            q = 0
                            while d < w:
                                dst = bufs[q % 2]
                                dk = "Ps%d" % (q % 2)
                                P.op("pool", lambda e, cur=cur, dst=dst, d=d: e.tensor_tensor(
                                    out=dst[:, :, d:19], in0=cur[:, :, d:19], in1=cur[:, :, 0:19 - d], op=ALU.add),
                                    reads=ckey, writes=[dk], cost=800.0)
                                cur = dst[:, :, :]
                                ckey = [dk]
                                d *= 2
                                q += 1
                            P.op("dve", lambda e, cur=cur, gi=gi, w=w: e.scalar_tensor_tensor(
                                out=diff[:, gi, 0:NS].rearrange("p (b t) -> p b t", t=4), in0=cur[:, :, 15:19],
                                scalar=1.0 / w, in1=XCs[:, gi, :, 15:19], op0=ALU.mult, op1=ALU.subtract),
                                reads=ckey + [("XCs", gi)], writes=[("diff", gi)])
                        b = psget()
                        P.op("pe", lambda e, gi=gi, b=b: e.matmul(PS[:, b, 0:n], lhsT=Wp[:, gi, :], rhs=diff[:, gi, 0:n],
                                                                  start=True, stop=True),
                             reads=["Wsm", ("diff", gi)], writes=pk(b))
                        P.op("act", lambda e, gi=gi, b=b: e.activation(out=ymix[:, gi, 0:n], in_=PS[:, b, 0:n], func=AF.Copy,
                                                                       scale=pscale[:, i, gi:gi + 1]),
                             reads=pk(b) + ["pscale"], writes=["ymix"])
                    for ft in range(4):
                        if not is_s:
                            z0 = Z[:, ft, 0:n]; z1 = Z[:, ft, 1:n + 1]; z2 = Z[:, ft, 2:n + 2]
                            cav = ca[:, 0:n]
                            bgv = bg[:, ft, 0:n]
                            yv = ymix[:, 4 + ft, 0:n]
                            zk = [("Z", par, ft), ("Zhalo", par)]
                        else:
                            z0 = Zs[:, ft, :, 0:4]; z1 = Zs[:, ft, :, 1:5]; z2 = Zs[:, ft, :, 2:6]
                            cav = ca[:, 0:NS].rearrange("p (b t) -> p b t", t=4)
                            bgv = bg[:, ft, 0:NS].rearrange("p (b t) -> p b t", t=4)
                            yv = ymix[:, 4 + ft, 0:NS].rearrange("p (b t) -> p b t", t=4)
                            zk = [("Zs", ft)]
                        P.op("dve", lambda e, ft=ft, z0=z0, cav=cav: e.tensor_scalar(
                            out=cav, in0=z0, scalar1=cw[:, i, 0, ft:ft + 1], scalar2=cb[:, i, ft:ft + 1],
                            op0=ALU.mult, op1=ALU.add), reads=zk + ["cw", "cb"], writes=["ca"])
                        P.op("dve", lambda e, ft=ft, z1=z1, cav=cav: e.scalar_tensor_tensor(
                            out=cav, in0=z1, scalar=cw[:, i, 1, ft:ft + 1], in1=cav, op0=ALU.mult, op1=ALU.add),
                            reads=zk + ["cw", "ca"], writes=["ca"])
                        P.op("dve", lambda e, ft=ft, z2=z2, cav=cav: e.scalar_tensor_tensor(
                            out=cav, in0=z2, scalar=cw[:, i, 2, ft:ft + 1], in1=cav, op0=ALU.mult, op1=ALU.add),
                            reads=zk + ["cw", "ca"], writes=["ca"])
                        P.op("dve", lambda e, cav=cav, bgv=bgv, yv=yv: e.tensor_tensor(out=yv, in0=cav, in1=bgv, op=ALU.mult),
                             reads=["ca", ("bg", par, ft)], writes=["ymix"])
                    out_proj_tile(Wout, "Wout", ymix, "ymix", t0, n)
                    if (not is_s) and t0 + n == SEQ:
                        for ft in range(4):
                            b = psget()
                            P.op("pe", lambda e, b=b, ft=ft: e.transpose(PS[0:15, b, 0:128], xchalo[:, ft, :], ident[:]),
                                 reads=["xchalo", "ident"], writes=pk(b))
                            P.op("pe", lambda e, b=b, ft=ft: e.transpose(PS[0:2, b, 128:256], zhalo[:, ft, :], ident[:]),
                                 reads=["zhalo", "ident"], writes=pk(b))
                            P.op("act", lambda e, b=b, ft=ft: e.activation(out=opp[0:15, ft * 128:(ft + 1) * 128],
                                                                           in_=PS[0:15, b, 0:128], func=AF.Copy),
                                 reads=pk(b), writes=["opp"])
                            P.op("act", lambda e, b=b, ft=ft: e.activation(out=opc[0:2, ft * 128:(ft + 1) * 128],
                                                                           in_=PS[0:2, b, 128:256], func=AF.Copy),
                                 reads=pk(b), writes=["opc"])
                        P.dma("sp", o_p_pool[i], opp[0:15, :], reads=["opp"])
                        P.dma("sp", o_p_conv[i], opc[0:2, :], reads=["opc"])
                    if is_s:
                        for half in range(2):
                            for ft in range(4):
                                P.op("dve", lambda e, ft=ft, half=half: e.tensor_copy(
                                    out=xct[:, 0:120].rearrange("p (b r) -> p b r", r=15),
                                    in_=XCs[:, ft, half * 8:half * 8 + 8, 4:19]), reads=[("XCs", ft)], writes=["xct"])
                                b = psget()
                                P.op("pe", lambda e, b=b: e.transpose(PS[0:120, b, 0:128], xct[:, 0:120], ident[:]),
                                     reads=["xct", "ident"], writes=pk(b))
                                P.op("act", lambda e, b=b, ft=ft: e.activation(out=otp[0:120, ft * 128:(ft + 1) * 128],
                                                                               in_=PS[0:120, b, 0:128], func=AF.Copy),
                                     reads=pk(b), writes=["otp"])
                            P.dma("sp", o_s_pool[i, half * 120:half * 120 + 120, :], otp[0:120, :], reads=["otp"])
                        for ft in range(4):
                            P.op("dve", lambda e, ft=ft: e.tensor_copy(
                                out=zct[:, 0:32].rearrange("p (b r) -> p b r", r=2), in_=Zs[:, ft, :, 4:6]),
                                reads=[("Zs", ft)], writes=["zct"])
                            b = psget()
                            P.op("pe", lambda e, b=b: e.transpose(PS[0:32, b, 0:128], zct[:, 0:32], ident[:]),
                                 reads=["zct", "ident"], writes=pk(b))
                            P.op("act", lambda e, b=b, ft=ft: e.activation(out=otc[:, ft * 128:(ft + 1) * 128],
                                                                           in_=PS[0:32, b, 0:128], func=AF.Copy),
                                 reads=pk(b), writes=["otc"])
                        P.dma("sp", o_s_conv[i], otc[:, :], reads=["otc"])
                seq = list(enumerate(mtiles))
                for idx, (ti, (t0, n, is_s)) in enumerate(seq):
                    front(ti, t0, n, is_s)
                    if idx >= 1:
                        pti, (pt0, pn, ps_) = seq[idx - 1]
                        back(pti, pt0, pn, ps_)
                lti, (lt0, ln, ls_) = seq[-1]
                back(lti, lt0, ln, ls_)
                P.flush()

        def epilogue(st):
            gfin = WinP[:, 2, :].bitcast(F32)
            P.dma("sp", gfin, norm_final.broadcast_to([128, D]), writes=["gfin"])
            junk = sb(st, "fjunk", [128, 512], BF16)
            ss = sb(st, "fss", [128, 2, 2])
            yo = [WinP[:, q, :].bitcast(F32) for q in range(2)]
            nsub = SEQ // 128 + 1
            for si in range(nsub):
                n = 128 if si < SEQ // 128 else NS
                dst = y_p[si * 128:(si + 1) * 128, :] if si < SEQ // 128 else y_s[:, :]
                par = si % 2
                yb = yo[par]
                yk = "yo%d" % par
                hk = hkeys(si * 128, n)
                bb = psget(2)
                for k in range(8):
                    P.op("pe", lambda e, k=k, bb=bb, si=si, n=n: e.transpose(
                        PS[0:n, bb + k // 4, (k % 4) * 128:(k % 4 + 1) * 128], hres[:, k, si * 128:si * 128 + n], ident[:]),
                        reads=["ident"] + hk, writes=pk(bb, 2), cost=110.0)
                for half in range(2):
                    P.op("act", lambda e, bb=bb, half=half, n=n, par=par: e.activation(
                        out=junk[0:n, :], in_=PS[0:n, bb + half, :], func=AF.Square, accum_out=ss[0:n, par, half:half + 1]),
                        reads=pk(bb, 2), writes=["fjunk", ("fss", par, half)])
                P.op("dve", lambda e, n=n, par=par: e.tensor_tensor(out=ss[0:n, par, 0:1], in0=ss[0:n, par, 0:1],
                                                                     in1=ss[0:n, par, 1:2], op=ALU.add),
                     reads=[("fss", par, 0), ("fss", par, 1)], writes=[("fss", par, 0)], cost=100.0)
                P.op("act", lambda e, n=n, par=par: e.activation(out=ss[0:n, par, 0:1], in_=ss[0:n, par, 0:1], func=AF.Sqrt,
                                                                  bias=epsc[0:n, 0:1], scale=1.0 / D),
                     reads=[("fss", par, 0)], writes=[("fss", par, 0)], cost=250.0)
                P.op("dve", lambda e, n=n, par=par: e.reciprocal(out=ss[0:n, par, 0:1], in_=ss[0:n, par, 0:1]),
                     reads=[("fss", par, 0)], writes=[("fss", par, 0)], cost=100.0)
                for half in range(2):
                    P.op("dve", lambda e, bb=bb, half=half, n=n, yb=yb, par=par: e.scalar_tensor_tensor(
                        out=yb[0:n, half * 512:(half + 1) * 512], in0=PS[0:n, bb + half, :], scalar=ss[0:n, par, 0:1],
                        in1=gfin[0:n, half * 512:(half + 1) * 512], op0=ALU.mult, op1=ALU.mult),
                        reads=pk(bb, 2) + [("fss", par, 0), "gfin"], writes=[yk], cost=750.0)
                P.dma("sp", dst, yb[0:n, :], reads=[yk])

        def ffn(layer):
            widths = [384] * 7 + [128]
            offs = [sum(widths[:j]) for j in range(len(widths))]
            with ExitStack() as st:
                xn = sb(st, "xn_all", [128, 8, T], BF16)
                Wg = [sb(st, "Wg%d" % q, [128, 8, 384], BF16) for q in range(2)]
                Wu = [sb(st, "Wu%d" % q, [128, 8, 384], BF16) for q in range(2)]
                Wd = [sb(st, "Wd%d" % q, [128, 3, D], BF16) for q in range(2)]
                sl = [sb(st, "sl%d" % q, [128, TF]) for q in range(2)]
                hb = [sb(st, "hb%d" % q, [128, 3, TF], BF16) for q in range(2)]

                P.cost.update({"pe": 195.0, "dve": 630.0, "act": 560.0})

                def load_slice(j):
                    q = j % 2
                    w = widths[j]
                    o = offs[j]
                    c = 2500.0 + 128 * 8 * w * 4 / 150.0
                    for kh in range(2):
                        P.dma("pool", Wg[q][:, 4 * kh:4 * kh + 4, 0:w],
                              ffn_g[layer].rearrange("(k p) n -> p k n", p=128)[:, 4 * kh:4 * kh + 4, o:o + w],
                              writes=[("Wg", q, kh)], cost=c / 2)
                    for kh in range(2):
                        P.dma("pool", Wu[q][:, 4 * kh:4 * kh + 4, 0:w],
                              ffn_u[layer].rearrange("(k p) n -> p k n", p=128)[:, 4 * kh:4 * kh + 4, o:o + w],
                              writes=[("Wu", q, kh)], cost=c / 2)
                    P.dma("pool", Wd[q][:, 0:w // 128, :],
                          ffn_d[layer].rearrange("(k p) n -> p k n", p=128)[:, o // 128:(o + w) // 128, :],
                          writes=[("Wd", q)], cost=c)
                load_slice(0)
                load_slice(1)
                if layer + 1 < 4:
                    load_mixer_weights(layer + 1)
                for ti, (t0, n, is_s) in enumerate(ftiles):
                    if ti == 0:
                        rmsnorm_tile(st, "f", t0, n, gffn[:, layer, :], xn, ("xn", t0), xoff=t0)
                    else:
                        _rms_ops("f", t0, n, gffn[:, layer, :], xn, ("xn", t0), _norm_scr["f"], xoff=t0)
                hbi = 0
                for j in range(len(widths)):
                    q = j % 2
                    nhc = widths[j] // 128
                    for (t0, n, is_s) in ftiles:
                        hk = hkeys(t0, n)
                        hbuf = hb[hbi % 2]
                        hkey = "hb%d" % (hbi % 2)
                        hbi += 1
                        for hc in range(nhc):
                            bgt = psget()
                            for k in range(8):
                                P.op("pe", lambda e, k=k, hc=hc, bgt=bgt, q=q, t0=t0, n=n: e.matmul(
                                    PS[:, bgt, 0:n], lhsT=Wg[q][:, k, hc * 128:(hc + 1) * 128], rhs=xn[:, k, t0:t0 + n],
                                    start=(k == 0), stop=(k == 7)), reads=[("Wg", q, k // 4), ("xn", t0)], writes=pk(bgt),
                                    cost=n / 2.35 + 6)
                            but = psget()
                            for k in range(8):
                                P.op("pe", lambda e, k=k, hc=hc, but=but, q=q, t0=t0, n=n: e.matmul(
                                    PS[:, but, 0:n], lhsT=Wu[q][:, k, hc * 128:(hc + 1) * 128], rhs=xn[:, k, t0:t0 + n],
                                    start=(k == 0), stop=(k == 7)), reads=[("Wu", q, k // 4), ("xn", t0)], writes=pk(but),
                                    cost=n / 2.35 + 6)
                            slt = sl[hc % 2]
                            slk = "sl%d" % (hc % 2)
                            P.op("act", lambda e, bgt=bgt, slt=slt, n=n: e.activation(out=slt[:, 0:n], in_=PS[:, bgt, 0:n], func=AF.Silu),
                                 reads=pk(bgt), writes=[slk], cost=(224 + n) / 1.2)
                            P.op("dve", lambda e, but=but, slt=slt, hbuf=hbuf, hc=hc, n=n: e.tensor_tensor(
                                out=hbuf[:, hc, 0:n], in0=slt[:, 0:n], in1=PS[:, but, 0:n], op=ALU.mult),
                                reads=pk(but) + [slk], writes=[(hkey, hc)], cost=(160 + n) / 0.96)
                        for fo in range(8):
                            b = psget()
                            for hc in range(nhc):
                                P.op("pe", lambda e, hc=hc, fo=fo, b=b, q=q, hbuf=hbuf, n=n, nhc=nhc: e.matmul(
                                    PS[:, b, 0:n], lhsT=Wd[q][:, hc, fo * 128:(fo + 1) * 128], rhs=hbuf[:, hc, 0:n],
                                    start=(hc == 0), stop=(hc == nhc - 1)), reads=[("Wd", q), (hkey, hc)], writes=pk(b),
                                    cost=n / 2.35 + 6)
                            P.op("dve", lambda e, fo=fo, b=b, t0=t0, n=n: e.tensor_tensor(
                                out=hres[:, fo, t0:t0 + n], in0=hres[:, fo, t0:t0 + n], in1=PS[:, b, 0:n], op=ALU.add),
                                reads=pk(b) + hk, writes=hk, cost=(160 + n) / 0.96)
                    if j + 2 < len(widths):
                        load_slice(j + 2)
                if layer == 3:
                    epilogue(st)
                P.flush()

        for layer in range(4):
            if layer > 0:
                P.next_epoch()
            if layer % 2 == 0:
                even_mixer(layer)
            else:
                odd_mixer(layer)
            ffn(layer)

    return nc


_NC_CACHE = {}


def kernel(**inputs):
    f = lambda a: np.ascontiguousarray(np.asarray(a, dtype=np.float32))
    inp = {k: f(v) for k, v in inputs.items()}
    if "nc" not in _NC_CACHE:
        _NC_CACHE["nc"] = build_nc()
    nc = _NC_CACHE["nc"]
    shared = {}
    for k in ("norm_mix", "norm_ffn", "w_in_even", "w_out_even", "s5_lambda_re", "s5_lambda_im", "s5_log_dt",
              "s5_b_re", "s5_b_im", "s5_c_re", "s5_c_im", "s5_glu_w", "s5_glu_b", "sgu_norm", "sgu_w", "sgu_b",
              "w_in_odd", "w_out_odd", "pool_w", "pool_scale", "conv_w", "conv_b", "ffn_w_gate", "ffn_w_up",
              "ffn_w_down"):
        shared[k] = inp[k]
    shared["norm_final"] = inp["norm_final"].reshape(1, D)
    shared["s5_d"] = inp["s5_d"].reshape(2, 512)
    in_maps = []
    for c in range(NCORES):
        m = dict(shared)
        m["x_p"] = inp["x_prompt"][c]
        m["x_s"] = np.ascontiguousarray(inp["x_sample"][16 * c:16 * c + 16].reshape(NS, D))
        m["st_re"] = np.ascontiguousarray(inp["state_s5_re"][:, 16 * c:16 * c + 16].reshape(2, 16, 2048))
        m["st_im"] = np.ascontiguousarray(inp["state_s5_im"][:, 16 * c:16 * c + 16].reshape(2, 16, 2048))
        m["st_pool"] = np.ascontiguousarray(inp["state_pool"][:, 16 * c:16 * c + 16])
        m["st_conv"] = np.ascontiguousarray(inp["state_conv"][:, 16 * c:16 * c + 16])
        in_maps.append(m)
    res = run_bass_kernel_spmd(nc, in_maps, core_ids=list(range(NCORES)))
    R = res.results
    y_prompt = np.stack([R[c]["y_p"] for c in range(NCORES)], 0).reshape(8, SEQ, D)
    y_sample = np.concatenate([R[c]["y_s"].reshape(16, 4, D) for c in range(NCORES)], 0)
    p_re = np.stack([R[c]["o_p_re"].reshape(2, 32, 64) for c in range(NCORES)], 1)
    p_im = np.stack([R[c]["o_p_im"].reshape(2, 32, 64) for c in range(NCORES)], 1)
    p_pool = np.stack([R[c]["o_p_pool"] for c in range(NCORES)], 1)
    p_conv = np.stack([R[c]["o_p_conv"] for c in range(NCORES)], 1)
    s_re = np.concatenate([R[c]["o_s_re"].reshape(2, 16, 32, 64) for c in range(NCORES)], 1)
    s_im = np.concatenate([R[c]["o_s_im"].reshape(2, 16, 32, 64) for c in range(NCORES)], 1)
    s_v = np.concatenate([R[c]["o_s_v"].reshape(2, 16, 4, 512) for c in range(NCORES)], 1)
    s_pool = np.concatenate([R[c]["o_s_pool"].reshape(2, 16, 15, 512) for c in range(NCORES)], 1)
    s_conv = np.concatenate([R[c]["o_s_conv"].reshape(2, 16, 2, 512) for c in range(NCORES)], 1)
    outs = (y_prompt, y_sample, p_re, p_im, p_pool, p_conv, s_re, s_im, s_v, s_pool, s_conv)
    return tuple(np.ascontiguousarray(o.astype(np.float32)) for o in outs)
```

```python
import math
import numpy as np
from contextlib import ExitStack
import concourse.bass as bass
import concourse.mybir as mybir
from concourse.bass_utils import run_bass_kernel_spmd

F32 = mybir.dt.float32
BF16 = mybir.dt.bfloat16
I32 = mybir.dt.int32
ALU = mybir.AluOpType
AF = mybir.ActivationFunctionType

ENGS = ("pe", "act", "dve", "pool", "sp")
NSLOT = 12
NCORES = 8
D = 1024
SEQ = 2048
NS = 64
T = SEQ + NS
DFF = 2816
EPS = 1e-6
TM = 256
TF = 448
FS = 256
NSL = DFF // FS


class _Op(object):
    __slots__ = ("idx", "eng", "emit", "deps", "signal", "epoch", "semval",
                 "is_dma", "slot", "dval", "prev_dval", "cost", "pos")


class Prog(object):
    def __init__(self, nc, es, n_epochs=6):
        self.nc = nc
        self.ops = []
        self.regions = {}
        self.epoch = 0
        self.n_epochs = n_epochs
        self.sems = {}
        self.cnt = {}
        for e in ENGS:
            for ep in range(n_epochs):
                self.sems[(e, ep)] = es.enter_context(nc.semaphore("s_%s_%d" % (e, ep)))
        self.dsems = {}
        self.dcount = {}
        self.dnext = {}
        for q in ("sp", "pool", "act"):
            self.dnext[q] = 0
            for s in range(NSLOT):
                self.dsems[(q, s)] = es.enter_context(nc.semaphore("d_%s_%d" % (q, s)))
                self.dcount[(q, s)] = 0
        self.nflush = 0
        self.cost = {"pe": 115.0, "act": 450.0, "dve": 430.0, "pool": 600.0, "sp": 100.0}
        self.reorder = True
        self.filler = None
        self.filler_cost = 170.0
        self.nfill = 0

    def next_epoch(self):
        assert not self.ops
        self.epoch = min(self.epoch + 1, self.n_epochs - 1)

    def _add(self, eng, emit, reads, writes, is_dma, cost):
        o = _Op()
        o.idx = len(self.ops)
        o.eng = eng
        o.emit = emit
        o.signal = False
        o.epoch = self.epoch
        o.semval = None
        o.is_dma = is_dma
        o.cost = cost if cost is not None else (3000.0 if is_dma else self.cost[eng])
        deps = set()
        for k in reads:
            r = self.regions.get(k)
            if r is not None and r[0] is not None:
                deps.add(r[0])
        for k in writes:
            r = self.regions.get(k)
            if r is not None:
                if r[0] is not None:
                    deps.add(r[0])
                deps.update(r[1])
        for k in reads:
            r = self.regions.get(k)
            if r is None:
                r = [None, []]
                self.regions[k] = r
            r[1].append(o.idx)
        for k in writes:
            self.regions[k] = [o.idx, []]
        deps.discard(o.idx)
        o.deps = deps
        o.slot = None
        self.ops.append(o)
        return o

    def op(self, eng, emit, reads=(), writes=(), cost=None):
        return self._add(eng, emit, reads, writes, False, cost)

    def dma(self, q, out, in_, reads=(), writes=(), cost=None, **kw):
        def emit(e, out=out, in_=in_, kw=kw):
            return e.dma_start(out=out, in_=in_, **kw)
        return self._add(q, emit, reads, writes, True, cost)

    def _schedule(self):
        ops = self.ops
        n = len(ops)
        succs = [[] for _ in range(n)]
        indeg = [0] * n
        for o in ops:
            for d in o.deps:
                succs[d].append(o.idx)
            indeg[o.idx] = len(o.deps)
        lastd = {}
        dchain = {}
        for o in ops:
            if o.is_dma:
                if o.eng in lastd:
                    dchain[o.idx] = lastd[o.eng]
                lastd[o.eng] = o.idx
        prio = [0.0] * n
        for i in range(n - 1, -1, -1):
            m = 0.0
            for s_ in succs[i]:
                if prio[s_] > m:
                    m = prio[s_]
            prio[i] = ops[i].cost + m
        order = {e: [] for e in ENGS}
        if not self.reorder:
            for o in ops:
                order[o.eng].append(o)
            return order
        ready = {e: [] for e in ENGS}
        ready_t = [0.0] * n
        fin = [0.0] * n
        issued = [False] * n
        free_at = {e: 0.0 for e in ENGS}
        for o in ops:
            if indeg[o.idx] == 0:
                ready[o.eng].append(o.idx)
        remaining = n
        HOP = 150.0
        while remaining:
            best = None
            for e in ENGS:
                rl = ready[e]
                if not rl:
                    continue
                fa = free_at[e]
                cb = None
                for i in rl:
                    o = ops[i]
                    if o.is_dma and i in dchain and not issued[dchain[i]]:
                        continue
                    st = ready_t[i] if ready_t[i] > fa else fa
                    key = (st, -prio[i], i)
                    if cb is None or key < cb:
                        cb = key
                if cb is not None and (best is None or cb < best[0]):
                    best = (cb, e)
            assert best is not None, "scheduler deadlock"
            (st, _, i), e = best
            o = ops[i]
            ready[e].remove(i)
            issued[i] = True
            if e == "pe" and self.filler is not None and free_at[e] > 0.0:
                gap = st - free_at[e]
                if gap > 1200.0:
                    k = min(int((gap - 500.0) / self.filler_cost), 60)
                    for _ in range(k):
                        f = _Op()
                        f.idx = -1
                        f.eng = "pe"
                        f.emit = self.filler
                        f.is_dma = False
                        f.signal = False
                        order[e].append(f)
                    self.nfill += k
            if o.is_dma:
                free_at[e] = st + 80.0
                fin[i] = st + o.cost
            else:
                free_at[e] = st + o.cost
                fin[i] = st + o.cost
            order[e].append(o)
            remaining -= 1
            for s_ in succs[i]:
                t = fin[i] + (0.0 if (ops[s_].eng == e and e == "pe") else HOP)
                if t > ready_t[s_]:
                    ready_t[s_] = t
                indeg[s_] -= 1
                if indeg[s_] == 0:
                    ready[ops[s_].eng].append(s_)
        self.est_time = max(fin) if n else 0.0
        return order

    def flush(self):
        nc = self.nc
        ops = self.ops
        if not ops:
            return
        per_eng = self._schedule()
        for e in ENGS:
            for p_, o in enumerate(per_eng[e]):
                o.pos = p_
            if e != "pe":
                assert all(o.idx >= 0 for o in per_eng[e])
        for e in ("sp", "pool", "act"):
            for o in per_eng[e]:
                if o.is_dma:
                    s = self.dnext[e]
                    self.dnext[e] = (s + 1) % NSLOT
                    o.slot = s
                    o.prev_dval = self.dcount[(e, s)]
                    self.dcount[(e, s)] += 16
                    o.dval = self.dcount[(e, s)]
        red = []
        for o in ops:
            comp = {}
            dmas = []
            for d in o.deps:
                p = ops[d]
                if p.is_dma:
                    dmas.append(d)
                else:
                    if p.eng == "pe" and o.eng == "pe" and not o.is_dma:
                        continue
                    if p.eng not in comp or ops[comp[p.eng]].pos < p.pos:
                        comp[p.eng] = d
            red.append((comp, dmas))
            for d in comp.values():
                ops[d].signal = True
        cnt = self.cnt
        for e in ENGS:
            for o in per_eng[e]:
                if o.idx < 0 or o.is_dma or not o.signal:
                    continue
                key = (o.eng, o.epoch)
                cnt[key] = cnt.get(key, 0) + 1
                o.semval = cnt[key]
        sems = self.sems
        dsems = self.dsems
        n_ep = self.n_epochs
        dcount = self.dcount

        def emit_engine(e, eng_name):
            waited = {}
            dwaited = {}
            for o in per_eng[eng_name]:
                if o.idx < 0:
                    o.emit(e)
                    continue
                comp, dmas = red[o.idx]
                for pe_name, d in comp.items():
                    p = ops[d]
                    done = False
                    for ep in range(p.epoch, n_ep):
                        w = waited.get((pe_name, ep), 0)
                        if ep == p.epoch and w >= p.semval:
                            done = True
                        if ep > p.epoch and w > 0:
                            done = True
                    if done:
                        continue
                    e.wait_ge(sems[(pe_name, p.epoch)], p.semval)
                    waited[(pe_name, p.epoch)] = p.semval
                for d in dmas:
                    p = ops[d]
                    k = (p.eng, p.slot)
                    if dwaited.get(k, 0) >= p.dval:
                        continue
                    e.wait_ge(dsems[k], p.dval)
                    dwaited[k] = p.dval
                if o.is_dma:
                    k = (o.eng, o.slot)
                    if o.prev_dval > 0 and dwaited.get(k, 0) < o.prev_dval:
                        e.wait_ge(dsems[k], o.prev_dval)
                        dwaited[k] = o.prev_dval
                    inst = o.emit(e)
                    inst.then_inc(dsems[k], 16)
                else:
                    inst = o.emit(e)
                    if o.signal:
                        inst.then_inc(sems[(o.eng, o.epoch)], 1)
            if eng_name in ("sp", "pool", "act"):
                for s in range(NSLOT):
                    k = (eng_name, s)
                    if dcount[k] > 0 and dwaited.get(k, 0) < dcount[k]:
                        e.wait_ge(dsems[k], dcount[k])

        with nc.Block() as block:
            @block.tensor
            def _(e):
                emit_engine(e, "pe")

            @block.scalar
            def _(e):
                emit_engine(e, "act")

            @block.vector
            def _(e):
                emit_engine(e, "dve")

            @block.gpsimd
            def _(e):
                emit_engine(e, "pool")

            @block.sync
            def _(e):
                emit_engine(e, "sp")
        self.ops = []
        self.regions = {}
        self.nflush += 1


def build_nc(debug=False):
    nc = bass.Bass("TRN2", target_bir_lowering=False)
    try:
        nc.allow_low_precision("bf16 matmul operands with fp32 accumulation by design")
    except Exception:
        pass

    def din(name, shape):
        return nc.dram_tensor(name, list(shape), F32, kind="ExternalInput").ap()

    def dout(name, shape):
        return nc.dram_tensor(name, list(shape), F32, kind="ExternalOutput").ap()

    x_p = din("x_p", (SEQ, D))
    x_s = din("x_s", (NS, D))
    st_re = din("st_re", (2, 16, 2048))
    st_im = din("st_im", (2, 16, 2048))
    st_pool = din("st_pool", (2, 16, 15, 512))
    st_conv = din("st_conv", (2, 16, 2, 512))
    norm_mix = din("norm_mix", (4, D))
    norm_ffn = din("norm_ffn", (4, D))
    norm_final = din("norm_final", (1, D))
    w_in_even = din("w_in_even", (2, D, 1536))
    w_out_even = din("w_out_even", (2, D, D))
    lam_re = din("s5_lambda_re", (2, 32, 64))
    lam_im = din("s5_lambda_im", (2, 32, 64))
    log_dt = din("s5_log_dt", (2, 32))
    b_re = din("s5_b_re", (2, 32, 64, 16))
    b_im = din("s5_b_im", (2, 32, 64, 16))
    c_re = din("s5_c_re", (2, 32, 16, 64))
    c_im = din("s5_c_im", (2, 32, 16, 64))
    s5_d = din("s5_d", (2, 512))
    glu_w = din("s5_glu_w", (2, 512, 512))
    glu_b = din("s5_glu_b", (2, 512))
    sgu_norm = din("sgu_norm", (2, 512))
    sgu_w = din("sgu_w", (2, 8, 128, 128))
    sgu_b = din("sgu_b", (2, 8, 128))
    w_in_odd = din("w_in_odd", (2, D, 2048))
    w_out_odd = din("w_out_odd", (2, D, D))
    pool_w = din("pool_w", (2, 4, 128, 128))
    pool_scale = din("pool_scale", (2, 512))
    conv_w = din("conv_w", (2, 3, 512))
    conv_b = din("conv_b", (2, 512))
    ffn_g = din("ffn_w_gate", (4, D, DFF))
    ffn_u = din("ffn_w_up", (4, D, DFF))
    ffn_d = din("ffn_w_down", (4, DFF, D))

    y_p = dout("y_p", (SEQ, D))
    y_s = dout("y_s", (NS, D))
    o_p_re = dout("o_p_re", (2, 16, 128))
    o_p_im = dout("o_p_im", (2, 16, 128))
    o_p_pool = dout("o_p_pool", (2, 15, 512))
    o_p_conv = dout("o_p_conv", (2, 2, 512))
    o_s_re = dout("o_s_re", (2, 16, 2048))
    o_s_im = dout("o_s_im", (2, 16, 2048))
    o_s_v = dout("o_s_v", (2, NS, 512))
    o_s_pool = dout("o_s_pool", (2, 240, 512))
    o_s_conv = dout("o_s_conv", (2, 32, 512))
    dbg = dout("dbg", (128, 4096)) if debug else None

    es = ExitStack()
    with es:
        es.enter_context(nc.allow_non_contiguous_dma(reason="small strided parameter loads"))
        P = Prog(nc, es)

        _uid = [0]

        def sb(stk, name, shape, dt=F32):
            _uid[0] += 1
            return stk.enter_context(nc.sbuf_tensor("%s_u%d" % (name, _uid[0]), list(shape), dt))

        PS = es.enter_context(nc.psum_tensor("PS", [128, 8, 512], F32))
        ps_rr = [0]

        def psget(n=1):
            b = ps_rr[0]
            if b + n > 8:
                b = 0
            ps_rr[0] = (b + n) % 8
            return b

        def pk(b, n=1):
            return [("ps", b + i) for i in range(n)]

        hres = sb(es, "hres", [128, 8, T])
        ident = sb(es, "ident", [128, 128])
        identb = sb(es, "identb", [128, 128], BF16)
        onesb = sb(es, "onesb", [128, 128], BF16)
        iot = sb(es, "iot", [128, 128])
        iop = sb(es, "iop", [128, 1])
        pstage = sb(es, "pstage", [128, 128])
        pvec = sb(es, "pvec", [128, 120])
        gmix = pvec[:, 0:32].rearrange("p (l k) -> p l k", l=4)
        gffn = pvec[:, 32:64].rearrange("p (l k) -> p l k", l=4)
        glub = pvec[:, 64:72].rearrange("p (l k) -> p l k", l=2)
        pscale = pvec[:, 72:80].rearrange("p (l k) -> p l k", l=2)
        cw = pvec[:, 80:104].rearrange("p (l c k) -> p l c k", l=2, c=3)
        cb = pvec[:, 104:112].rearrange("p (l k) -> p l k", l=2)
        dcol = pvec[:, 112:120].rearrange("p (l k) -> p l k", l=2)
        epsc = sb(es, "epsc", [128, 1])
        s5car = sb(es, "s5car", [128, 16, 2])
        xchalo = sb(es, "xchalo", [128, 4, 15])
        zhalo = sb(es, "zhalo", [128, 4, 2])

        def V(e):
            return e

        def load_w(dst, src3, key, nsplit):
            K = dst.shape[1]
            step = K // nsplit
            for i in range(nsplit):
                nb = 128 * step * dst.shape[2] * 4
                P.dma("pool", dst[:, i * step:(i + 1) * step, :],
                      src3.rearrange("(k p) n -> p k n", p=128)[:, i * step:(i + 1) * step, :], writes=[(key, i)],
                      cost=2500.0 + nb / 150.0)

        WinP = sb(es, "WinP", [128, 8, 2048], BF16)
        WoutP = sb(es, "WoutP", [128, 8, D], BF16)
        WsmP = sb(es, "WsmP", [128, 2048], BF16)

        def load_mixer_weights(layer, only_in=False, skip_in=False):
            i = layer // 2
            if layer % 2 == 0:
                if not skip_in:
                    load_w(WinP[:, :, 0:1536], w_in_even[i], "Win", 4)
                if only_in:
                    return
                load_w(WsmP[:, :].rearrange("p (k n) -> p k n", k=4), glu_w[i], "Wsm", 1)
                load_w(WoutP, w_out_even[i], "Wout", 2)
            else:
                load_w(WinP, w_in_odd[i], "Win", 4)
                load_w(WoutP, w_out_odd[i], "Wout", 2)
                P.dma("pool", WsmP[:, 0:512].rearrange("p (g d) -> p g d", g=4), pool_w[i].rearrange("g c d -> c g d"),
                      writes=["Wsm"])

        load_mixer_weights(0, only_in=True)
        P.op("pool", lambda e: e.iota(iot[:], pattern=[[1, 128]], base=0, channel_multiplier=0,
                                      allow_small_or_imprecise_dtypes=True), writes=["iot"])
        P.op("pool", lambda e: e.iota(iop[:], pattern=[[1, 1]], base=0, channel_multiplier=1,
                                      allow_small_or_imprecise_dtypes=True), writes=["iop"])
        P.op("dve", lambda e: e.tensor_scalar(out=ident[:], in0=iot[:], scalar1=iop[:, 0:1], scalar2=None,
                                              op0=ALU.is_equal), reads=["iot", "iop"], writes=["ident"])
        P.op("dve", lambda e: e.tensor_copy(out=identb[:], in_=ident[:]), reads=["ident"], writes=["identb"])
        P.op("dve", lambda e: e.memset(onesb[:], 1.0), writes=["onesb"])
        P.op("dve", lambda e: e.memset(epsc[:], EPS), writes=["epsc"])
        P.filler = None; _unused_filler = lambda e: e.matmul(PS[:, 7, 0:128], lhsT=onesb[:], rhs=onesb[:], start=True, stop=True)
        with ExitStack() as st:
            pass
        pst_rows = [(norm_mix.rearrange("l (k p) -> (l k) p", p=128), 32), (norm_ffn.rearrange("l (k p) -> (l k) p", p=128), 32),
                    (glu_b.rearrange("l (k p) -> (l k) p", p=128), 8), (pool_scale.rearrange("l (k p) -> (l k) p", p=128), 8),
                    (conv_w.rearrange("l c (k p) -> (l c k) p", p=128), 24), (conv_b.rearrange("l (k p) -> (l k) p", p=128), 8),
                    (s5_d.rearrange("l (k p) -> (l k) p", p=128), 8)]
        r0 = 0
        for j_, (src_, nr_) in enumerate(pst_rows):
            P.dma("sp", pstage[r0:r0 + nr_, :], src_, writes=[("pstage", j_)])
            r0 += nr_
        P.op("pe", lambda e: e.transpose(PS[:, 5, 0:120], pstage[0:120, :], ident[0:120, 0:120]),
             reads=[("pstage", j_) for j_ in range(7)] + ["ident"], writes=[("ps", 5)])
        P.op("act", lambda e: e.activation(out=pvec[:, :], in_=PS[:, 5, 0:120], func=AF.Copy), reads=[("ps", 5)],
             writes=["gmix", "gffn", "glub", "pscale", "cw", "cb", "dcol"])

        def load_x(xt):
            nsub = SEQ // 128 + 1
            for si in range(nsub):
                n = 128 if si < SEQ // 128 else NS
                src = x_p[si * 128:(si + 1) * 128, :] if si < SEQ // 128 else x_s[:, :]
                xb = xt[si % 2]
                xk = "xt%d" % (si % 2)
                P.dma("sp", xb[0:n, :], src, writes=[xk])
                for half in range(2):
                    b = psget()
                    for q in range(4):
                        k = half * 4 + q
                        P.op("pe", lambda e, b=b, q=q, k=k, xb=xb, n=n: e.transpose(
                            PS[:, b, q * 128:q * 128 + n], xb[0:n, k * 128:(k + 1) * 128], ident[0:n, 0:n]),
                            reads=[xk, "ident"], writes=pk(b))
                    eng = "act" if half == 0 else "dve"
                    if eng == "act":
                        P.op("act", lambda e, b=b, half=half, si=si, n=n: e.activation(
                            out=hres[:, half * 4:half * 4 + 4, si * 128:si * 128 + n],
                            in_=PS[:, b, :].rearrange("p (q t) -> p q t", q=4)[:, :, 0:n], func=AF.Copy),
                            reads=pk(b), writes=[("h", si, half)])
                    else:
                        P.op("dve", lambda e, b=b, half=half, si=si, n=n: e.tensor_copy(
                            out=hres[:, half * 4:half * 4 + 4, si * 128:si * 128 + n],
                            in_=PS[:, b, :].rearrange("p (q t) -> p q t", q=4)[:, :, 0:n]),
                            reads=pk(b), writes=[("h", si, half)])

        mtiles = [(i * TM, TM, False) for i in range(SEQ // TM)] + [(SEQ, NS, True)]
        ftiles = [(0, 448, False), (448, 448, False), (896, 448, False), (1344, 448, False), (1792, 320, False)]

        def hkeys(t0, n):
            ks = []
            a = (t0 // TM) * TM
            while a < t0 + n:
                ks.append(("hres", a))
                a += TM
            return ks

        def s5_setup(stk, i, XB, YC, BD, TC, TS, R4, A4, bmask):
            with ExitStack() as st:
                def t16(name):
                    return sb(st, "s5_" + name, [128, 16])
                LRI = sb(st, "s5_LRI", [128, 32])
                LR = LRI[:, 0:16]
                LI = LRI[:, 16:32]
                LDT, DT, Z, MAG, ANG = [t16(n_) for n_ in ("LDT", "DT", "Z", "MAG", "ANG")]
                SN, CS, ta, tb, tc_, td = [t16(n_) for n_ in ("SN", "CS", "ta", "tb", "tc", "td")]
                FR, FI = t16("FR"), t16("FI")
                AR = [t16("AR%d" % k) for k in range(5)]
                AI = [t16("AI%d" % k) for k in range(5)]
                BR = sb(st, "s5_BR", [128, 16, 32]); BI = sb(st, "s5_BI", [128, 16, 32])
                BBr = sb(st, "s5_BBr", [128, 16, 32]); BBi = sb(st, "s5_BBi", [128, 16, 32])
                CTr = sb(st, "s5_CTr", [128, 16, 32]); CTi = sb(st, "s5_CTi", [128, 16, 32])
                Yr = sb(st, "s5_Yr", [128, 16, 32]); Yi = sb(st, "s5_Yi", [128, 16, 32])
                W1 = sb(st, "s5_W1", [128, 16, 32]); W2 = sb(st, "s5_W2", [128, 16, 32])
                W3 = sb(st, "s5_W3", [128, 16, 32]); W4 = sb(st, "s5_W4", [128, 16, 32])
                Xr_ = sb(st, "s5_Xr", [128, 16, 32]); Xi_ = sb(st, "s5_Xi", [128, 16, 32])
                CNr = sb(st, "s5_CNr", [128, 4, 128]); CNi = sb(st, "s5_CNi", [128, 4, 128])
                cnt = [0]

                def dv(fn, reads, writes, eng="dve"):
                    P.op(eng, fn, reads=reads, writes=writes)

                def tt(out, a, b, op, r, w, eng="dve"):
                    dv(lambda e: e.tensor_tensor(out=out, in0=a, in1=b, op=op), r, w, eng=eng)

                def ts(out, a, s1, op0, r, w, s2=None, op1=None):
                    if op1 is None:
                        dv(lambda e: e.tensor_scalar(out=out, in0=a, scalar1=s1, scalar2=None, op0=op0), r, w)
                    else:
                        dv(lambda e: e.tensor_scalar(out=out, in0=a, scalar1=s1, scalar2=s2, op0=op0, op1=op1), r, w)

                lst = sb(st, "s5_lst", [32, 128])
                P.dma("sp", lst[0:16, :], lam_re[i].rearrange("(P g) n -> P (g n)", g=2), writes=[("lst", 0)])
                P.dma("sp", lst[16:32, :], lam_im[i].rearrange("(P g) n -> P (g n)", g=2), writes=[("lst", 1)])
                bl_ = psget()
                P.op("pe", lambda e: e.transpose(PS[:, bl_, 0:32], lst[:, :], ident[0:32, 0:32]),
                     reads=[("lst", 0), ("lst", 1), "ident"], writes=pk(bl_))
                P.op("act", lambda e: e.activation(out=LRI[:, :], in_=PS[:, bl_, 0:32], func=AF.Copy), reads=pk(bl_),
                     writes=["LR", "LI"])
                for g2 in range(2):
                    P.dma("sp", LDT[64 * g2:64 * g2 + 64, :],
                          log_dt[i:i + 1, :].rearrange("o (P g) -> o g P", g=2)[:, g2, :].broadcast_to([64, 16]),
                          writes=[("LDT", g2)])
                for tl in (BR, BI, CNr, CNi):
                    dv(lambda e, tl=tl: e.memset(tl[:], 0.0), [], ["z_" + tl.name], eng="pool")
                for (tl, src) in ((BR, b_re), (BI, b_im)):
                    for g2 in range(2):
                        P.dma("sp", tl[64 * g2:64 * g2 + 64, :, 16 * g2:16 * g2 + 16],
                              src[i].rearrange("(P g) n q -> g n P q", g=2)[g2], reads=["z_" + tl.name],
                              writes=[("ld_" + tl.name, g2)])
                for (tl, src) in ((CNr, c_re), (CNi, c_im)):
                    for p4 in range(4):
                        for g2 in range(2):
                            P.dma("sp", tl[32 * p4 + 16 * g2:32 * p4 + 16 * g2 + 16, :, 64 * g2:64 * g2 + 64],
                                  src[i].rearrange("(f a g) p n -> a g p f n", a=4, g=2)[p4, g2],
                                  reads=["z_" + tl.name], writes=[("ld_" + tl.name, p4, g2)])
                for (src, dst) in ((CNr, CTr), (CNi, CTi)):
                    b = psget()
                    for ft in range(4):
                        P.op("pe", lambda e, ft=ft, b=b, src=src: e.transpose(
                            PS[:, b, ft * 128:(ft + 1) * 128], src[:, ft, :], ident[:]),
                            reads=[("ld_" + src.name, a_, b_) for a_ in range(4) for b_ in range(2)] + ["ident"], writes=pk(b))
                    P.op("act", lambda e, b=b, dst=dst: e.activation(
                        out=dst[:].rearrange("p a b -> p (a b)"), in_=PS[:, b, :], func=AF.Copy),
                        reads=pk(b), writes=[dst.name])
                dv(lambda e: e.activation(out=DT[:], in_=LDT[:], func=AF.Exp), [("LDT", 0), ("LDT", 1)], ["DT"], eng="act")
                tt(Z[:], LR[:], DT[:], ALU.mult, ["LR", "DT"], ["Z"])
                ts(MAG[:], Z[:], 1.0 / 120.0, ALU.mult, ["Z"], ["MAG"], 1.0 / 24.0, ALU.add)
                for c in (1.0 / 6.0, 0.5, 1.0, 1.0):
                    tt(MAG[:], MAG[:], Z[:], ALU.mult, ["MAG", "Z"], ["MAG"])
                    ts(MAG[:], MAG[:], float(c), ALU.add, ["MAG"], ["MAG"])
                tt(ANG[:], LI[:], DT[:], ALU.mult, ["LI", "DT"], ["ANG"])
                C1 = 6.28125
                C2 = 2.0 * math.pi - C1
                MAGIC = 12582912.0
                for (shift, dst) in ((0.0, SN), (0.5 * math.pi, CS)):
                    ts(ta[:], ANG[:], 1.0 / (2 * math.pi), ALU.mult, ["ANG"], ["ta"], shift / (2 * math.pi), ALU.add)
                    ts(tb[:], ta[:], MAGIC, ALU.add, ["ta"], ["tb"])
                    ts(tb[:], tb[:], -MAGIC, ALU.add, ["tb"], ["tb"])
                    dv(lambda e: e.scalar_tensor_tensor(out=ta[:], in0=tb[:], scalar=-C1, in1=ANG[:],
                                                        op0=ALU.mult, op1=ALU.add), ["tb", "ANG"], ["ta"])
                    dv(lambda e: e.scalar_tensor_tensor(out=ta[:], in0=tb[:], scalar=-C2, in1=ta[:],
                                                        op0=ALU.mult, op1=ALU.add), ["tb", "ta"], ["ta"])
                    ts(ta[:], ta[:], float(shift), ALU.add, ["ta"], ["ta"], math.pi, ALU.min)
                    ts(ta[:], ta[:], -math.pi, ALU.max, ["ta"], ["ta"])
                    dv(lambda e, dst=dst: e.activation(out=dst[:], in_=ta[:], func=AF.Sin), ["ta"], [dst.name],
                       eng="act")
                tt(AR[1][:], MAG[:], CS[:], ALU.mult, ["MAG", CS.name], ["AR1"])
                tt(AI[1][:], MAG[:], SN[:], ALU.mult, ["MAG", SN.name], ["AI1"])
                dv(lambda e: e.memset(AR[0][:], 1.0), [], ["AR0"])
                dv(lambda e: e.memset(AI[0][:], 0.0), [], ["AI0"])

                def cmul(orr, oi, ar, ai, br, bi, rk, wk, t1=None, t2=None, k1="W1", k2="W2", eng="dve"):
                    tt(t1, ar, br, ALU.mult, rk, [k1], eng)
                    tt(t2, ai, bi, ALU.mult, rk, [k2], eng)
                    tt(orr, t1, t2, ALU.subtract, [k1, k2], [wk + "r"], eng)
                    tt(t1, ar, bi, ALU.mult, rk + [wk + "r"], [k1], eng)
                    tt(t2, ai, br, ALU.mult, rk + [wk + "r"], [k2], eng)
                    tt(oi, t1, t2, ALU.add, [k1, k2], [wk + "i"], eng)

                cmul(AR[2][:], AI[2][:], AR[1][:], AI[1][:], AR[1][:], AI[1][:], ["AR1", "AI1"], "A2", tc_[:], td[:], "tc", "td")
                cmul(AR[3][:], AI[3][:], AR[2][:], AI[2][:], AR[1][:], AI[1][:], ["AR1", "AI1", "A2r", "A2i"], "A3",
                     tc_[:], td[:], "tc", "td")
                cmul(AR[4][:], AI[4][:], AR[2][:], AI[2][:], AR[2][:], AI[2][:], ["A2r", "A2i"], "A4", tc_[:], td[:], "tc", "td")
                akeys = {0: ["AR0", "AI0"], 1: ["AR1", "AI1"], 2: ["A2r", "A2i"], 3: ["A3r", "A3i"], 4: ["A4r", "A4i"]}
                dv(lambda e: e.tensor_copy(out=A4[:, :, 0], in_=AR[4][:]), akeys[4], ["A4"])
                dv(lambda e: e.tensor_copy(out=A4[:, :, 1], in_=AI[4][:]), akeys[4] + ["A4"], ["A4"])
                tt(ta[:], MAG[:], MAG[:], ALU.mult, ["MAG"], ["ta"])
                tt(R4[:], ta[:], ta[:], ALU.mult, ["ta"], ["R4"])
                ts(ta[:], AR[1][:], -1.0, ALU.add, ["AR1"], ["ta"])
                tt(tb[:], LR[:], LR[:], ALU.mult, ["LR"], ["tb"])
                tt(tc_[:], LI[:], LI[:], ALU.mult, ["LI", "A4i"], ["tc"])
                tt(tb[:], tb[:], tc_[:], ALU.add, ["tb", "tc"], ["tb"])
                dv(lambda e: e.reciprocal(out=tb[:], in_=tb[:]), ["tb"], ["tb"])
                tt(tc_[:], ta[:], LR[:], ALU.mult, ["ta", "LR"], ["tc"])
                tt(td[:], AI[1][:], LI[:], ALU.mult, ["AI1", "LI", "A4i"], ["td"])
                tt(tc_[:], tc_[:], td[:], ALU.add, ["tc", "td"], ["tc"])
                tt(FR[:], tc_[:], tb[:], ALU.mult, ["tc", "tb"], ["FR"])
                tt(tc_[:], AI[1][:], LR[:], ALU.mult, ["AI1", "LR", "FR"], ["tc"])
                tt(td[:], ta[:], LI[:], ALU.mult, ["ta", "LI", "FR"], ["td"])
                tt(tc_[:], tc_[:], td[:], ALU.subtract, ["tc", "td"], ["tc"])
                tt(FI[:], tc_[:], tb[:], ALU.mult, ["tc", "tb"], ["FI"])
                dv(lambda e: e.reciprocal(out=ta[:], in_=R4[:]), ["R4", "FI"], ["ta"])
                tt(TC[:, :, 0], AR[4][:], ta[:], ALU.mult, akeys[4] + ["ta"], ["TC"])
                tt(TS[:, :, 0], AI[4][:], ta[:], ALU.mult, akeys[4] + ["ta"], ["TS"])
                m = 1
                NCH = TM // 4
                W5 = sb(st, "s5_W5", [128, 16, 32]); W6 = sb(st, "s5_W6", [128, 16, 32])
                while m < NCH:
                    ur = TC[:, :, m - 1:m].broadcast_to([128, 16, m])
                    ui = TS[:, :, m - 1:m].broadcast_to([128, 16, m])
                    w1 = W5[:, :, 0:m]; w2 = W6[:, :, 0:m]
                    tt(w1, TC[:, :, 0:m], ur, ALU.mult, ["TC", "TS"], ["W5"], "pool")
                    tt(w2, TS[:, :, 0:m], ui, ALU.mult, ["TC", "TS"], ["W6"], "pool")
                    tt(TC[:, :, m:2 * m], w1, w2, ALU.subtract, ["W5", "W6"], ["TC"], "pool")
                    tt(w1, TC[:, :, 0:m], ui, ALU.mult, ["TC", "TS"], ["W5"], "pool")
                    tt(w2, TS[:, :, 0:m], ur, ALU.mult, ["TC", "TS"], ["W6"], "pool")
                    tt(TS[:, :, m:2 * m], w1, w2, ALU.add, ["W5", "W6"], ["TS"], "pool")
                    m *= 2

                def bc(a):
                    return a.unsqueeze(2).broadcast_to([128, 16, 32])

                cmul(BBr[:], BBi[:], BR[:], BI[:], bc(FR[:]), bc(FI[:]), [("ld_" + BR.name, 0), ("ld_" + BR.name, 1), ("ld_" + BI.name, 0), ("ld_" + BI.name, 1), "FR", "FI"],
                     "BB", W1[:], W2[:])
                for k in range(4):
                    if k == 0:
                        srcs = (BBr, BBi)
                        skeys = ["BBr", "BBi"]
                    else:
                        cmul(Xr_[:], Xi_[:], BBr[:], BBi[:], bc(AR[k][:]), bc(AI[k][:]), ["BBr", "BBi"] + akeys[k], "Xq",
                             W3[:], W4[:], "W3", "W4", eng="pool")
                        srcs = (Xr_, Xi_)
                        skeys = ["Xqr", "Xqi"]
                    s = 3 - k
                    for ri in range(2):
                        b = psget()
                        for ft in range(4):
                            P.op("pe", lambda e, ft=ft, b=b, src=srcs[ri]: e.transpose(
                                PS[:, b, ft * 128:(ft + 1) * 128],
                                src[:, 4 * ft:4 * ft + 4, :].rearrange("p a b -> p (a b)"), ident[:]),
                                reads=skeys + ["ident"], writes=pk(b))
                        P.op("act", lambda e, b=b, ri=ri, s=s: e.activation(
                            out=XB[:, :, ri, s, :], in_=PS[:, b, :].rearrange("p (f n) -> p f n", f=4), func=AF.Copy),
                            reads=pk(b), writes=["XB"])
                bBD = psget()
                for k in range(5):
                    if k == 0:
                        dv(lambda e: e.tensor_copy(out=Yr[:], in_=CTr[:]), ["s5_CTr", "XB"], ["Yr"])
                        ts(Yi[:], CTi[:], -1.0, ALU.mult, ["s5_CTi", "XB"], ["Yi"])
                    else:
                        cmul(Yr[:], Yi[:], CTr[:], CTi[:], bc(AR[k][:]), bc(AI[k][:]),
                             ["s5_CTr", "s5_CTi", "BD%d" % (k - 1), "YC"] + akeys[k], "Y", W1[:], W2[:])
                        ts(Yi[:], Yi[:], -1.0, ALU.mult, ["Yi"], ["Yi"])
                        P.op("act", lambda e, k=k: e.activation(out=YC[:, :, 0, k - 1, :], in_=Yr[:], func=AF.Copy),
                             reads=["Yr"], writes=["YC"])
                        P.op("act", lambda e, k=k: e.activation(out=YC[:, :, 1, k - 1, :], in_=Yi[:], func=AF.Copy),
                             reads=["Yi"], writes=["YC"])
                    if k < 4:
                        for ft in range(4):
                            o_ = PS[:, bBD, ft * 128:(ft + 1) * 128]
                            P.op("pe", lambda e, ft=ft, o_=o_: e.matmul(
                                o_, lhsT=BBr[:, 4 * ft:4 * ft + 4, :].rearrange("p a b -> p (a b)"),
                                rhs=Yr[:, 4 * ft:4 * ft + 4, :].rearrange("p a b -> p (a b)"), start=True, stop=False),
                                reads=["BBr", "Yr"], writes=pk(bBD))
                            P.op("pe", lambda e, ft=ft, o_=o_: e.matmul(
                                o_, lhsT=BBi[:, 4 * ft:4 * ft + 4, :].rearrange("p a b -> p (a b)"),
                                rhs=Yi[:, 4 * ft:4 * ft + 4, :].rearrange("p a b -> p (a b)"), start=False, stop=True),
                                reads=["BBi", "Yi"], writes=pk(bBD))
                        for ft in range(4):
                            if k == 0:
                                dv(lambda e, ft=ft: e.tensor_tensor(out=W1[:, 0:4, :].rearrange("p a b -> p (a b)"),
                                                                    in0=PS[:, bBD, ft * 128:(ft + 1) * 128],
                                                                    in1=bmask[:], op=ALU.mult),
                                   pk(bBD) + ["bmask"], ["W1"])
                                dv(lambda e, ft=ft: e.scalar_tensor_tensor(
                                    out=BD[:, ft, 0, :], in0=ident[:], scalar=dcol[:, i, ft:ft + 1],
                                    in1=W1[:, 0:4, :].rearrange("p a b -> p (a b)"), op0=ALU.mult, op1=ALU.add),
                                    ["W1", "ident", "dcol"], ["BD0"])
                            else:
                                dv(lambda e, ft=ft, k=k: e.tensor_tensor(out=BD[:, ft, k, :],
                                                                         in0=PS[:, bBD, ft * 128:(ft + 1) * 128],
                                                                         in1=bmask[:], op=ALU.mult),
                                   pk(bBD) + ["bmask"], ["BD%d" % k])

        def even_mixer(layer):
            i = layer // 2
            P.cost.update({"pe": 115.0, "dve": 430.0, "act": 450.0})
            with ExitStack() as st:
                NCH = TM // 4
                Win = WinP
                Wout = WoutP
                Wglu = WsmP[:, :].rearrange("p (k n) -> p k n", k=4)
                XB = sb(st, "XB", [128, 4, 2, 4, 128], BF16)
                YC = sb(st, "YC", [128, 16, 2, 4, 32], BF16)
                BD = sb(st, "BD", [128, 4, 4, 128], BF16)
                TC = sb(st, "TC", [128, 16, NCH]); TS = sb(st, "TS", [128, 16, NCH])
                R4 = sb(st, "R4", [128, 16]); A4 = sb(st, "A4", [128, 16, 2])
                wT = sb(st, "wT", [128, 8, 128], BF16)
                bbc = sb(st, "bbc", [128, 4, 128])
                gsg = sb(st, "gsg", [128, 512])
                wsc = sb(st, "wsc", [128, 4, 16])
                P.dma("sp", gsg[:], sgu_norm[i:i + 1, :].broadcast_to([128, 512]), writes=["gsg"])
                for h in range(8):
                    P.dma("sp", bbc[64 * (h % 2):64 * (h % 2) + 64, h // 2, :],
                          sgu_b[i, h:h + 1, :].broadcast_to([64, 128]), writes=[("bbc", h)])
                for h in range(8):
                    P.dma("sp", wsc[64 * (h % 2):64 * (h % 2) + 64, h // 2, :].rearrange("p (a b) -> p a b", a=4),
                          sgu_w[i, h:h + 1, 0:4, 0:4].broadcast_to([64, 4, 4]), writes=[("wsc", h)])
                P.op("dve", lambda e: e.memset(s5car[:], 0.0), writes=[("s5car", q_) for q_ in range(4)])
                with ExitStack() as st2:
                    trilT = sb(st2, "trilT", [128, 128])
                    bmask = sb(st2, "bmask", [128, 128])
                    j32 = sb(st2, "bm_j32", [128, 128])
                    S4 = sb(st2, "bm_S4", [128, 128])
                    P.op("dve", lambda e: e.tensor_scalar(out=trilT[:], in0=iot[:], scalar1=iop[:, 0:1], scalar2=None,
                                                          op0=ALU.is_ge), reads=["iot", "iop"], writes=["trilT"])
                    P.op("pool", lambda e: e.iota(j32[:], pattern=[[1, 4], [0, 32]], base=0, channel_multiplier=0,
                                                  allow_small_or_imprecise_dtypes=True), writes=["j32"])
                    P.op("dve", lambda e: e.tensor_scalar(out=S4[:], in0=j32[:], scalar1=iop[:, 0:1], scalar2=None,
                                                          op0=ALU.is_equal), reads=["j32", "iop"], writes=["S4"])
                    P.op("pe", lambda e: e.matmul(PS[:, 6, 0:128], lhsT=S4[0:4, :], rhs=S4[0:4, :], start=True, stop=True),
                         reads=["S4"], writes=[("ps", 6)])
                    P.op("dve", lambda e: e.tensor_copy(out=bmask[:], in_=PS[:, 6, 0:128]), reads=[("ps", 6)], writes=["bmask"])
                    wld = [sb(st2, "wld%d" % q, [128, 128]) for q in range(2)]
                    for h in range(8):
                        wl = wld[h % 2]
                        P.dma("sp", wl[:], sgu_w[i, h], writes=["wld%d" % (h % 2)])
                        b = psget()
                        P.op("pe", lambda e, b=b, wl=wl: e.transpose(PS[:, b, 0:128], wl[:], ident[:]),
                             reads=["wld%d" % (h % 2), "ident"], writes=pk(b))
                        P.op("dve", lambda e, b=b, h=h: e.tensor_tensor(out=wT[:, h, :], in0=PS[:, b, 0:128],
                                                                        in1=trilT[:], op=ALU.mult),
                             reads=pk(b) + ["trilT"], writes=["wT"])
                    xt_ = [sb(st2, "xt%d" % q_, [128, D]) for q_ in range(2)] if layer == 0 else None
                    s5_setup(st2, i, XB, YC, BD, TC, TS, R4, A4, bmask)
                    if layer == 0:
                        load_x(xt_)
                    P.flush()
                glubh = sb(st, "glubh", [128, 4])
                P.op("dve", lambda e: e.tensor_scalar(out=glubh[:], in0=glub[:, i, :], scalar1=0.5, scalar2=None, op0=ALU.mult),
                     writes=["glubh"])
                if layer == 0:
                    load_mixer_weights(0, skip_in=True)
                P.op("dve", lambda e: e.tensor_scalar(out=WoutP[:, 0:4, :], in0=WoutP[:, 0:4, :], scalar1=0.5, scalar2=None,
                                                      op0=ALU.mult), reads=[("Wout", 0), ("Wout", 1)], writes=["Wout"])
                xnt = sb(st, "xnt", [128, 8, TM], BF16)
                uaL = [sb(st, "ua%d" % q, [128, 4, TM], BF16) for q in range(2)]
                ubL = [sb(st, "ub%d" % q, [128, 4, TM], BF16) for q in range(2)]
                vn = sb(st, "vn", [128, 512])
                vnbL = [sb(st, "vnb%d" % q, [128, 2, 512], BF16) for q in range(2)]
                vjunk = sb(st, "vjunk", [128, 512], BF16)
                vss = sb(st, "vss", [128, 2])
                ymix = sb(st, "ymix", [128, 8, TM], BF16)
                tA = sb(st, "tA", [128, 4, NCH]); tB = sb(st, "tB", [128, 4, NCH])
                Gin = sb(st, "Gin", [128, 4, 2, NCH])
                wtail = [WinP[:, k_, 1536:2048].bitcast(F32).rearrange("p (a c) -> p a c", a=4) for k_ in range(8)]
                tC, tD, tE, tF, tG, tH = wtail[0:6]
                GsL = [sb(st, "Gs0", [128, 4, 2, NCH]),
                       WinP[:, 6:8, 1536:2048].bitcast(F32).rearrange("p k (a c) -> p a k c", a=4)]
                Hf = sb(st, "Hf", [128, 4, 2, NCH + 1])
                Hb = sb(st, "Hb", [128, 4, 2, NCH], BF16)
                sqy = sb(st, "sqy", [128, TM])
                zf = sb(st, "zf", [128, 4, TM])
                zb = sb(st, "zb", [128, 4, TM], BF16)
                sg2 = sb(st, "sg2", [128, TM])
                stmp = sb(st, "stmp", [128, TM])
                vT = sb(st, "vT", [128, 4, NS])
                sacc = sb(st, "sacc", [128, 16, 4])
                h0s = sb(st, "h0s", [16, 1024])
                h0T = sb(st, "h0T", [128, 16, 2, 16])
                hend = sb(st, "hend", [128, 16, 2, 16])
                hoP = sb(st, "hoP", [16, 2, 128])
                def front(ti, t0, n, is_s):
                    nch = n // 4
                    par = ti % 2
                    ua = uaL[par]; ub = ubL[par]; vnb = vnbL[par]
                    hk = ("hres", t0)
                    rmsnorm_tile(st, "m", t0, n, gmix[:, layer, :], xnt, "xnt") if ti == 0 else \
                        rmsnorm_tile_again("m", t0, n, gmix[:, layer, :], xnt, "xnt")
                    for ft in range(4):
                        b = psget()
                        for k in range(8):
                            P.op("pe", lambda e, k=k, ft=ft, b=b: e.matmul(
                                PS[:, b, 0:n], lhsT=Win[:, k, ft * 128:(ft + 1) * 128], rhs=xnt[:, k, 0:n],
                                start=(k == 0), stop=(k == 7)), reads=["Win", "xnt"], writes=pk(b))
                        P.op("act", lambda e, ft=ft, b=b: e.activation(out=ua[:, ft, 0:n], in_=PS[:, b, 0:n],
                                                                       func=AF.Copy),
                             reads=pk(b), writes=[("ua", par, ft)])
                    for ft in range(4):
                        b = psget()
                        for k in range(8):
                            P.op("pe", lambda e, k=k, ft=ft, b=b: e.matmul(
                                PS[:, b, 0:n], lhsT=Win[:, k, 512 + ft * 128:512 + (ft + 1) * 128], rhs=xnt[:, k, 0:n],
                                start=(k == 0), stop=(k == 7)), reads=["Win", "xnt"], writes=pk(b))
                        P.op("act", lambda e, ft=ft, b=b: e.activation(out=ub[:, ft, 0:n], in_=PS[:, b, 0:n],
                                                                       func=AF.Copy),
                             reads=pk(b), writes=[("ub", par, ft)])
                    nsub = (n + 127) // 128
                    for sj in range(nsub):
                        m = min(128, n - sj * 128)
                        b = psget()
                        for k in range(8):
                            P.op("pe", lambda e, k=k, b=b, sj=sj, m=m: e.matmul(
                                PS[0:m, b, :], lhsT=xnt[:, k, sj * 128:sj * 128 + m], rhs=Win[:, k, 1024:1536],
                                start=(k == 0), stop=(k == 7)), reads=["Win", "xnt"], writes=pk(b))
                        P.op("act", lambda e, b=b, sj=sj, m=m: e.activation(
                            out=vjunk[0:m, :], in_=PS[0:m, b, :], func=AF.Square, accum_out=vss[0:m, sj:sj + 1]),
                            reads=pk(b), writes=["vjunk", ("vss", sj)])
                        P.op("act", lambda e, sj=sj, m=m: e.activation(
                            out=vss[0:m, sj:sj + 1], in_=vss[0:m, sj:sj + 1], func=AF.Ln, bias=epsc[0:m, 0:1],
                            scale=1.0 / 512.0), reads=[("vss", sj), "epsc"], writes=[("vss", sj)], cost=250.0)
                        P.op("act", lambda e, sj=sj, m=m: e.activation(
                            out=vss[0:m, sj:sj + 1], in_=vss[0:m, sj:sj + 1], func=AF.Exp, scale=-0.5),
                            reads=[("vss", sj)], writes=[("vss", sj)], cost=250.0)
                        P.op("dve", lambda e, b=b, sj=sj, m=m: e.scalar_tensor_tensor(
                            out=vnb[0:m, sj, :], in0=PS[0:m, b, :], scalar=vss[0:m, sj:sj + 1], in1=gsg[0:m, :],
                            op0=ALU.mult, op1=ALU.mult), reads=pk(b) + [("vss", sj), "gsg"], writes=[("vnb", par, sj)])
                        if is_s:
                            P.op("dve", lambda e, b=b, sj=sj, m=m: e.scalar_tensor_tensor(
                                out=vn[0:m, :], in0=PS[0:m, b, :], scalar=vss[0:m, sj:sj + 1], in1=gsg[0:m, :],
                                op0=ALU.mult, op1=ALU.mult), reads=pk(b) + [("vss", sj), "gsg"], writes=[("vn", 0)])
                    if is_s:
                        P.dma("sp", o_s_v[i], vn[0:NS, :], reads=[("vn", 0)])
                        for ri in range(2):
                            b = psget()
                            for hf in range(2):
                                P.dma("sp", h0s[:, :], (st_re if ri == 0 else st_im)[i][:, hf * 1024:(hf + 1) * 1024],
                                      writes=["h0s"])
                                for q in range(8):
                                    Pp = hf * 8 + q
                                    P.op("pe", lambda e, b=b, Pp=Pp, q=q: e.transpose(
                                        PS[:, b, Pp * 16:(Pp + 1) * 16], h0s[:, q * 128:(q + 1) * 128],
                                        ident[0:16, 0:16]), reads=["h0s", "ident"], writes=pk(b))
                            P.op("dve", lambda e, b=b, ri=ri: e.tensor_copy(
                                out=h0T[:, :, ri, :], in_=PS[:, b, 0:256].rearrange("p (a b) -> p a b", a=16)),
                                reads=pk(b), writes=["h0T"])
                def back(ti, t0, n, is_s):
                    nch = n // 4
                    par = ti % 2
                    ua = uaL[par]; ub = ubL[par]; vnb = vnbL[par]
                    for ft in range(4):
                        if not is_s:
                            P.op("pool", lambda e, ft=ft: e.tensor_copy(out=Hf[:, :, :, 0], in_=s5car[:, 4 * ft:4 * ft + 4, :]),
                                 reads=[("s5car", ft)], writes=["Hf0"])
                        b4 = psget(4)
                        for p4 in range(4):
                            for ri in range(2):
                                for s in range(4):
                                    P.op("pe", lambda e, p4=p4, ri=ri, s=s, ft=ft, b4=b4: e.matmul(
                                        PS[:, b4 + p4, ri * NCH:ri * NCH + nch],
                                        lhsT=XB[32 * p4:32 * p4 + 32, ft, ri, s, :],
                                        rhs=ua[32 * p4:32 * p4 + 32, ft, s:n:4],
                                        start=(s == 0), stop=(s == 3), tile_position=(32 * p4, 0)),
                                        reads=["XB", ("ua", par, ft)], writes=pk(b4, 4), cost=40.0)
                        Xr = PS[:, b4:b4 + 4, 0:nch]
                        Xi = PS[:, b4:b4 + 4, NCH:NCH + nch]
                        if not is_s:
                            Cc = TC[:, 4 * ft:4 * ft + 4, 0:nch]
                            Ss = TS[:, 4 * ft:4 * ft + 4, 0:nch]
                            tAa = tA[:, :, 0:nch]; tBb = tB[:, :, 0:nch]
                            GinR = Gin[:, :, 0, 0:nch]; GinI = Gin[:, :, 1, 0:nch]
                            x4 = pk(b4, 4)
                            gq = ft % 2
                            Gsq = GsL[gq]

                            def tt(out, a, bb, op, r, w, eng="dve"):
                                P.op(eng, lambda e: e.tensor_tensor(out=out, in0=a, in1=bb, op=op), reads=r, writes=w)
                            tCc = tC[:, :, 0:nch]; tDd = tD[:, :, 0:nch]
                            tt(tAa, Xr, Cc, ALU.mult, x4 + ["TC"], ["tA"])
                            tt(tBb, Xi, Ss, ALU.mult, x4 + ["TS"], ["tB"])
                            tt(tCc, Xi, Cc, ALU.mult, x4 + ["TC"], ["tC"])
                            tt(tDd, Xr, Ss, ALU.mult, x4 + ["TS"], ["tD"])
                            tt(GinR, tAa, tBb, ALU.add, ["tA", "tB"], ["GinR"])
                            tt(GinI, tCc, tDd, ALU.subtract, ["tC", "tD"], ["GinI"])
                            for p4 in range(4):
                                Pp = 4 * ft + p4
                                for ri in range(2):
                                    P.op("dve", lambda e, p4=p4, ri=ri, Pp=Pp, Gsq=Gsq: e.tensor_tensor_scan(
                                        out=Gsq[:, p4, ri, 0:nch], data0=R4[:, Pp:Pp + 1].broadcast_to([128, nch]),
                                        data1=Gin[:, p4, ri, 0:nch], initial=s5car[:, Pp, ri:ri + 1],
                                        op0=ALU.mult, op1=ALU.add),
                                        reads=["GinR" if ri == 0 else "GinI", "R4", ("s5car", ft)],
                                        writes=[("Gs", gq, p4, ri)], cost=350.0)
                            GR = Gsq[:, :, 0, 0:nch]; GI = Gsq[:, :, 1, 0:nch]
                            gk = [("Gs", gq, a_, b_) for a_ in range(4) for b_ in range(2)]
                            tEe = tE[:, :, 0:nch]; tFf = tF[:, :, 0:nch]; tGg = tG[:, :, 0:nch]; tHh = tH[:, :, 0:nch]
                            tt(tEe, GR, Cc, ALU.mult, gk + ["TC"], ["tE"], "pool")
                            tt(tFf, GI, Ss, ALU.mult, gk + ["TS"], ["tF"], "pool")
                            tt(tGg, GR, Ss, ALU.mult, gk + ["TS"], ["tG"], "pool")
                            tt(tHh, GI, Cc, ALU.mult, gk + ["TC"], ["tH"], "pool")
                            tt(Hf[:, :, 0, 1:nch + 1], tEe, tFf, ALU.subtract, ["tE", "tF", "Hf0"], ["HfR"], "pool")
                            tt(Hf[:, :, 1, 1:nch + 1], tGg, tHh, ALU.add, ["tG", "tH", "Hf0"], ["HfI"], "pool")
                            P.op("act", lambda e: e.activation(out=Hb[:, :, :, 0:nch], in_=Hf[:, :, :, 0:nch], func=AF.Copy),
                                 reads=["HfR", "HfI", "Hf0"], writes=["Hb"])
                            P.op("pool", lambda e, ft=ft: e.tensor_copy(out=s5car[:, 4 * ft:4 * ft + 4, :],
                                                                        in_=Hf[:, :, :, nch]),
                                 reads=["HfR", "HfI"] + gk, writes=[("s5car", ft)])
                        else:
                            h0r = h0T[:, 4 * ft:4 * ft + 4, 0, :]; h0i = h0T[:, 4 * ft:4 * ft + 4, 1, :]
                            a4r = A4[:, 4 * ft:4 * ft + 4, 0:1].broadcast_to([128, 4, 16])
                            a4i = A4[:, 4 * ft:4 * ft + 4, 1:2].broadcast_to([128, 4, 16])
                            tAa = tA[:, :, 0:16]; tBb = tB[:, :, 0:16]
                            x4 = pk(b4, 4)

                            def tt(out, a, bb, op, r, w):
                                P.op("dve", lambda e: e.tensor_tensor(out=out, in0=a, in1=bb, op=op), reads=r, writes=w)
                            tt(tAa, h0r, a4r, ALU.mult, ["h0T", "A4"], ["tA"])
                            tt(tBb, h0i, a4i, ALU.mult, ["h0T", "A4"], ["tB"])
                            tt(tAa, tAa, tBb, ALU.subtract, ["tA", "tB"], ["tA"])
                            tt(hend[:, 4 * ft:4 * ft + 4, 0, :], tAa, Xr, ALU.add, ["tA"] + x4, [("hend", ft, 0)])
                            tt(tAa, h0r, a4i, ALU.mult, ["h0T", "A4", ("hend", ft, 0)], ["tA"])
                            tt(tBb, h0i, a4r, ALU.mult, ["h0T", "A4", ("hend", ft, 0)], ["tB"])
                            tt(tAa, tAa, tBb, ALU.add, ["tA", "tB"], ["tA"])
                            tt(hend[:, 4 * ft:4 * ft + 4, 1, :], tAa, Xi, ALU.add, ["tA"] + x4, [("hend", ft, 1)])
                            P.op("act", lambda e, ft=ft: e.activation(out=Hb[:, :, :, 0:16],
                                                                      in_=h0T[:, 4 * ft:4 * ft + 4, :, :], func=AF.Copy),
                                 reads=["h0T"], writes=["Hb"])
                        by = psget()
                        for t in range(4):
                            o_ = PS[:, by, t * NCH:t * NCH + nch]
                            for tau in range(t + 1):
                                P.op("pe", lambda e, t=t, tau=tau, ft=ft, o_=o_: e.matmul(
                                    o_, lhsT=BD[:, ft, tau, :], rhs=ua[:, ft, (t - tau):n:4],
                                    start=(tau == 0), stop=False), reads=[("ua", par, ft)], writes=pk(by), cost=60.0)
                            for p4 in range(4):
                                for ri in range(2):
                                    last = (ri == 1)
                                    P.op("pe", lambda e, t=t, p4=p4, ri=ri, ft=ft, by=by, last=last: e.matmul(
                                        PS[32 * p4:32 * p4 + 32, by, t * NCH:t * NCH + nch],
                                        lhsT=YC[:, 4 * ft + p4, ri, t, :], rhs=Hb[:, p4, ri, 0:nch],
                                        start=False, stop=last, tile_position=(0, 32 * p4)),
                                        reads=["Hb"], writes=pk(by), cost=45.0)
                        yv = PS[:, by, 0:4 * NCH].rearrange("p (t c) -> p c t", t=4)[:, 0:nch, :]
                        sq3 = sqy[:, 0:n].rearrange("p (c t) -> p c t", t=4)
                        z3 = zf[:, ft, 0:n].rearrange("p (c t) -> p c t", t=4)
                        P.op("act", lambda e, yv=yv, z3=z3: e.activation(out=z3, in_=yv, func=AF.Gelu_apprx_tanh),
                             reads=pk(by), writes=[("zf", ft)])
                        P.op("act", lambda e, ft=ft: e.activation(out=zb[:, ft, 0:n], in_=zf[:, ft, 0:n], func=AF.Copy),
                             reads=[("zf", ft)], writes=[("zb", ft)])
                    for fo in range(4):
                        b = psget()
                        for fi in range(4):
                            P.op("pe", lambda e, fi=fi, fo=fo, b=b: e.matmul(
                                PS[:, b, 0:n], lhsT=Wglu[:, fi, fo * 128:(fo + 1) * 128], rhs=zb[:, fi, 0:n],
                                start=(fi == 0), stop=(fi == 3)), reads=["Wsm", ("Wsm", 0)] + [("zb", q) for q in range(4)],
                                writes=pk(b))
                        P.op("act", lambda e, fo=fo, b=b: e.activation(out=sg2[:, 0:n], in_=PS[:, b, 0:n], func=AF.Tanh,
                                                                       bias=glubh[:, fo:fo + 1], scale=0.5),
                             reads=pk(b) + ["glubh"], writes=["sg2"])
                        P.op("dve", lambda e, fo=fo: e.scalar_tensor_tensor(out=ymix[:, fo, 0:n], in0=sg2[:, 0:n], scalar=1.0,
                                                                            in1=zf[:, fo, 0:n], op0=ALU.add, op1=ALU.mult),
                             reads=["sg2", ("zf", fo)], writes=["ymix"])
                    if not is_s:
                        for hp in range(4):
                            b = psget()
                            for j in range(n // 128):
                                for h2 in range(2):
                                    h = 2 * hp + h2
                                    P.op("pe", lambda e, b=b, j=j, h2=h2, h=h: e.matmul(
                                        PS[64 * h2:64 * h2 + 64, b, j * 128:(j + 1) * 128],
                                        lhsT=vnb[:, j, h * 64:(h + 1) * 64], rhs=wT[:, h, :],
                                        start=True, stop=True, tile_position=(0, 64 * h2)),
                                        reads=["wT", ("vnb", par, j)], writes=pk(b))
                            P.op("dve", lambda e, b=b, hp=hp: e.tensor_tensor(
                                out=stmp[:, 0:n].rearrange("p (j i) -> p j i", i=128),
                                in0=PS[:, b, 0:n].rearrange("p (j i) -> p j i", i=128),
                                in1=bbc[:, hp, :].unsqueeze(1).broadcast_to([128, n // 128, 128]), op=ALU.add),
                                reads=pk(b) + ["bbc"], writes=["stmp"])
                            P.op("dve", lambda e, hp=hp: e.tensor_tensor(out=ymix[:, 4 + hp, 0:n], in0=stmp[:, 0:n],
                                                                         in1=ub[:, hp, 0:n], op=ALU.mult),
                                 reads=["stmp", ("ub", par, hp)], writes=["ymix"])
                    else:
                        b = psget()
                        for ft in range(4):
                            P.op("pe", lambda e, b=b, ft=ft: e.transpose(PS[:, b, ft * NS:(ft + 1) * NS],
                                                                         vn[0:NS, ft * 128:(ft + 1) * 128],
                                                                         ident[0:NS, 0:NS]),
                                 reads=[("vn", 0), "ident"], writes=pk(b))
                        P.op("dve", lambda e, b=b: e.tensor_copy(out=vT[:], in_=PS[:, b, 0:4 * NS].rearrange("p (f t) -> p f t", f=4)),
                             reads=pk(b), writes=["vT"])
                        for ft in range(4):
                            v3 = vT[:, ft, :].rearrange("p (b j) -> p b j", j=4)
                            for ii in range(4):
                                P.op("dve", lambda e, ft=ft, ii=ii, v3=v3: e.tensor_scalar(
                                    out=sacc[:, :, ii], in0=v3[:, :, 0], scalar1=wsc[:, ft, 4 * ii:4 * ii + 1],
                                    scalar2=bbc[:, ft, ii:ii + 1], op0=ALU.mult, op1=ALU.add),
                                    reads=["vT", "wsc", "bbc"], writes=["sacc"])
                                for jj in range(1, ii + 1):
                                    P.op("dve", lambda e, ft=ft, ii=ii, jj=jj, v3=v3: e.scalar_tensor_tensor(
                                        out=sacc[:, :, ii], in0=v3[:, :, jj], scalar=wsc[:, ft, 4 * ii + jj:4 * ii + jj + 1],
                                        in1=sacc[:, :, ii], op0=ALU.mult, op1=ALU.add),
                                        reads=["vT", "wsc", "sacc"], writes=["sacc"])
                            P.op("dve", lambda e, ft=ft: e.tensor_tensor(
                                out=ymix[:, 4 + ft, 0:NS], in0=sacc[:].rearrange("p b i -> p (b i)"),
                                in1=ub[:, ft, 0:NS], op=ALU.mult), reads=["sacc", ("ub", par, ft)], writes=["ymix"])
                    out_proj_tile(Wout, "Wout", ymix, "ymix", t0, n)
                    if is_s:
                        for ri in range(2):
                            for hf in range(2):
                                for h2 in range(2):
                                    half = hf * 2 + h2
                                    b = psget()
                                    for q in range(4):
                                        Pp = half * 4 + q
                                        P.op("pe", lambda e, b=b, q=q, Pp=Pp, ri=ri: e.transpose(
                                            PS[0:16, b, q * 128:(q + 1) * 128], hend[:, Pp, ri, :], ident[:]),
                                            reads=[("hend", Pp // 4, ri), "ident"], writes=pk(b))
                                    P.op("act", lambda e, b=b, h2=h2: e.activation(
                                        out=h0s[:, h2 * 512:(h2 + 1) * 512], in_=PS[0:16, b, :], func=AF.Copy),
                                        reads=pk(b), writes=["h0s"])
                                P.dma("sp", (o_s_re if ri == 0 else o_s_im)[i][:, hf * 1024:(hf + 1) * 1024], h0s[:, :],
                                      reads=["h0s"])
                    if (not is_s) and t0 + n == SEQ:
                        for ri in range(2):
                            b = psget()
                            P.op("pe", lambda e, b=b, ri=ri: e.transpose(PS[0:16, b, 0:128], s5car[:, :, ri], ident[:]),
                                 reads=[("s5car", q_) for q_ in range(4)] + ["ident"], writes=pk(b))
                            P.op("act", lambda e, b=b, ri=ri: e.activation(out=hoP[:, ri, :], in_=PS[0:16, b, 0:128], func=AF.Copy),
                                 reads=pk(b), writes=["hoP"])
                        P.dma("sp", o_p_re[i], hoP[:, 0, :], reads=["hoP"])
                        P.dma("sp", o_p_im[i], hoP[:, 1, :], reads=["hoP"])
                seq = list(enumerate(mtiles))
                for idx, (ti, (t0, n, is_s)) in enumerate(seq):
                    front(ti, t0, n, is_s)
                    if idx >= 1:
                        pti, (pt0, pn, ps_) = seq[idx - 1]
                        back(pti, pt0, pn, ps_)
                lti, (lt0, ln, ls_) = seq[-1]
                back(lti, lt0, ln, ls_)
                P.flush()

        _norm_scr = {}

        def rmsnorm_tile_again(tag, t0, n, gvec, xn_out, xn_key):
            _rms_ops(tag, t0, n, gvec, xn_out, xn_key, _norm_scr[tag])

        def _rms_ops(tag, t0, n, gvec, xn_out, xn_key, srs, xoff=0):
            sr, sr2 = srs
            hk = hkeys(t0, n)
            sqv = xn_out[:, :, xoff:xoff + n]
            P.op("act", lambda e: e.activation(out=sqv, in_=hres[:, :, t0:t0 + n], func=AF.Square),
                 reads=hk, writes=[xn_key])
            b = psget()
            for k in range(8):
                P.op("pe", lambda e, k=k, b=b: e.matmul(PS[:, b, 0:n], lhsT=onesb[:], rhs=xn_out[:, k, xoff:xoff + n],
                                                        start=(k == 0), stop=(k == 7)),
                     reads=[xn_key, "onesb"], writes=pk(b))
            P.op("act", lambda e, b=b: e.activation(out=sr[:, 0:n], in_=PS[:, b, 0:n], func=AF.Ln,
                                                    bias=epsc[:, 0:1], scale=1.0 / D),
                 reads=pk(b) + ["epsc"], writes=["sr_" + tag])
            P.op("act", lambda e: e.activation(out=sr[:, 0:n], in_=sr[:, 0:n], func=AF.Exp, scale=-0.5),
                 reads=["sr_" + tag], writes=["sr_" + tag])
            for k in range(8):
                P.op("dve", lambda e, k=k: e.scalar_tensor_tensor(
                    out=xn_out[:, k, xoff:xoff + n], in0=hres[:, k, t0:t0 + n], scalar=gvec[:, k:k + 1],
                    in1=sr2[:, 0:n], op0=ALU.mult, op1=ALU.mult),
                    reads=hk + ["sr_" + tag, "gmix", "gffn"], writes=[xn_key])

        def rmsnorm_tile(stk, tag, t0, n, gvec, xn_out, xn_key, xoff=0):
            nmax = TM if tag == "m" else TF
            sr = sb(stk, "sr_" + tag, [128, nmax])
            sr2 = sr
            _norm_scr[tag] = (sr, sr2)
            _rms_ops(tag, t0, n, gvec, xn_out, xn_key, (sr, sr2), xoff=xoff)

        def out_proj_tile(Wout, wkey, ymix, ykey, t0, n):
            hk = hkeys(t0, n)
            for fo in range(8):
                b = psget()
                for k in range(8):
                    P.op("pe", lambda e, k=k, fo=fo, b=b: e.matmul(
                        PS[:, b, 0:n], lhsT=Wout[:, k, fo * 128:(fo + 1) * 128], rhs=ymix[:, k, 0:n],
                        start=(k == 0), stop=(k == 7)), reads=[wkey, ykey], writes=pk(b))
                P.op("dve", lambda e, fo=fo, b=b: e.tensor_tensor(
                    out=hres[:, fo, t0:t0 + n], in0=hres[:, fo, t0:t0 + n], in1=PS[:, b, 0:n], op=ALU.add),
                    reads=pk(b) + hk, writes=hk)

        def odd_mixer(layer):
            i = layer // 2
            P.cost.update({"pe": 115.0, "dve": 430.0, "act": 450.0})
            with ExitStack() as st:
                Win = WinP
                Wout = WoutP
                Wp = WsmP[:, 0:512].rearrange("p (g d) -> p g d", g=4)
                xnt = sb(st, "xnto", [128, 8, TM], BF16)
                XCL = [sb(st, "XC%d" % q, [128, 4, 15 + TM]) for q in range(2)]
                PA = sb(st, "PA", [128, 15 + TM]); PB = sb(st, "PB", [128, 15 + TM])
                diff = sb(st, "diff", [128, 4, TM], BF16)
                xdL = [sb(st, "xd%d" % q, [128, 4, TM]) for q in range(2)]
                bgL = [sb(st, "bg%d" % q, [128, 4, TM]) for q in range(2)]
                ZL = [sb(st, "Z%d" % q, [128, 4, 2 + TM]) for q in range(2)]
                ca = sb(st, "ca", [128, TM])
                ymix = sb(st, "ymixo", [128, 8, TM], BF16)
                invn = sb(st, "invn", [128, 4, 15])
                XCs = sb(st, "XCs", [128, 4, 16, 19])
                PAs = sb(st, "PAs", [128, 16, 19]); PBs = sb(st, "PBs", [128, 16, 19])
                Zs = sb(st, "Zs", [128, 4, 16, 6])
                spl = [sb(st, "spl%d" % q, [128, 512]) for q in range(2)]
                scl = sb(st, "scl", [32, 512])
                otp = sb(st, "otp", [128, 512])
                otc = sb(st, "otc", [32, 512])
                opp = sb(st, "opp", [16, 512])
                opc = sb(st, "opc", [2, 512])
                xct = sb(st, "xct", [128, 128])
                zct = sb(st, "zct", [128, 32])
                P.op("pool", lambda e: e.iota(invn[:], pattern=[[0, 4], [1, 15]], base=1, channel_multiplier=0,
                                              allow_small_or_imprecise_dtypes=True), writes=["invn"])
                for gi in range(4):
                    P.op("dve", lambda e, gi=gi: e.tensor_scalar(out=invn[:, gi, :], in0=invn[:, gi, :],
                                                                 scalar1=float(2 ** (gi + 1)), scalar2=None, op0=ALU.min),
                         reads=["invn"], writes=["invn"])
                P.op("dve", lambda e: e.reciprocal(out=invn[:], in_=invn[:]), reads=["invn"], writes=["invn"])
                P.op("dve", lambda e: e.memset(xchalo[:], 0.0), writes=["xchalo"])
                P.op("dve", lambda e: e.memset(zhalo[:], 0.0), writes=["zhalo"])
                P.op("pool", lambda e: e.memset(PA[:], 0.0), writes=["P0"])
                P.op("pool", lambda e: e.memset(PB[:], 0.0), writes=["P1"])
                P.op("pool", lambda e: e.memset(PAs[:], 0.0), writes=["Ps0"])
                P.op("pool", lambda e: e.memset(PBs[:], 0.0), writes=["Ps1"])
                def front(ti, t0, n, is_s):
                    par = ti % 2
                    XC = XCL[par]; xd = xdL[par]; bg = bgL[par]; Z = ZL[par]
                    if ti == 0:
                        rmsnorm_tile(st, "m", t0, n, gmix[:, layer, :], xnt, "xnt")
                    else:
                        rmsnorm_tile_again("m", t0, n, gmix[:, layer, :], xnt, "xnt")
                    if not is_s:
                        P.op("dve", lambda e: e.tensor_copy(out=XC[:, :, 0:15], in_=xchalo[:]), reads=["xchalo"],
                             writes=[("XChalo", par)])
                        P.op("dve", lambda e: e.tensor_copy(out=Z[:, :, 0:2], in_=zhalo[:]), reads=["zhalo"],
                             writes=[("Zhalo", par)])
                    else:
                        P.dma("sp", spl[0][:], st_pool[i].rearrange("b r c -> (b r) c")[0:128, :], writes=["spl0"])
                        P.dma("sp", spl[1][0:112, :], st_pool[i].rearrange("b r c -> (b r) c")[128:240, :], writes=["spl1"])
                        P.dma("sp", scl[:], st_conv[i].rearrange("b r c -> (b r) c"), writes=["scl"])
                        for ft in range(4):
                            b = psget()
                            P.op("pe", lambda e, b=b, ft=ft: e.transpose(PS[:, b, 0:128], spl[0][:, ft * 128:(ft + 1) * 128], ident[:]),
                                 reads=["spl0", "ident"], writes=pk(b))
                            P.op("pe", lambda e, b=b, ft=ft: e.transpose(PS[:, b, 128:240], spl[1][0:112, ft * 128:(ft + 1) * 128],
                                                                         ident[0:112, 0:112]),
                                 reads=["spl1", "ident"], writes=pk(b))
                            P.op("pe", lambda e, b=b, ft=ft: e.transpose(PS[:, b, 256:288], scl[:, ft * 128:(ft + 1) * 128],
                                                                         ident[0:32, 0:32]),
                                 reads=["scl", "ident"], writes=pk(b))
                            P.op("dve", lambda e, b=b, ft=ft: e.tensor_copy(
                                out=XCs[:, ft, :, 0:15], in_=PS[:, b, 0:240].rearrange("p (b r) -> p b r", r=15)),
                                reads=pk(b), writes=[("XCs", ft)])
                            P.op("dve", lambda e, b=b, ft=ft: e.tensor_copy(
                                out=Zs[:, ft, :, 0:2], in_=PS[:, b, 256:288].rearrange("p (b r) -> p b r", r=2)),
                                reads=pk(b), writes=[("Zs", ft)])
                    for ft in range(4):
                        b = psget()
                        for k in range(8):
                            P.op("pe", lambda e, k=k, ft=ft, b=b: e.matmul(
                                PS[:, b, 0:n], lhsT=Win[:, k, ft * 128:(ft + 1) * 128], rhs=xnt[:, k, 0:n],
                                start=(k == 0), stop=(k == 7)), reads=["Win", "xnt"], writes=pk(b))
                        if not is_s:
                            P.op("act", lambda e, ft=ft, b=b: e.activation(out=XC[:, ft, 15:15 + n], in_=PS[:, b, 0:n], func=AF.Copy),
                                 reads=pk(b), writes=[("XC", par, ft)])
                        else:
                            P.op("act", lambda e, ft=ft, b=b: e.activation(
                                out=XCs[:, ft, :, 15:19], in_=PS[:, b, 0:NS].rearrange("p (b t) -> p b t", t=4), func=AF.Copy),
                                reads=pk(b) + [("XCs", ft)], writes=[("XCs", ft)])
                    for ft in range(4):
                        b = psget()
                        for k in range(8):
                            P.op("pe", lambda e, k=k, ft=ft, b=b: e.matmul(
                                PS[:, b, 0:n], lhsT=Win[:, k, 512 + ft * 128:512 + (ft + 1) * 128], rhs=xnt[:, k, 0:n],
                                start=(k == 0), stop=(k == 7)), reads=["Win", "xnt"], writes=pk(b))
                        P.op("act", lambda e, ft=ft, b=b: e.activation(out=xd[:, ft, 0:n], in_=PS[:, b, 0:n], func=AF.Copy),
                             reads=pk(b), writes=[("xd", par, ft)])
                    for ft in range(4):
                        b = psget()
                        for k in range(8):
                            P.op("pe", lambda e, k=k, ft=ft, b=b: e.matmul(
                                PS[:, b, 0:n], lhsT=Win[:, k, 1024 + ft * 128:1024 + (ft + 1) * 128], rhs=xnt[:, k, 0:n],
                                start=(k == 0), stop=(k == 7)), reads=["Win", "xnt"], writes=pk(b))
                        P.op("act", lambda e, ft=ft, b=b: e.activation(out=bg[:, ft, 0:n], in_=PS[:, b, 0:n], func=AF.Copy),
                             reads=pk(b), writes=[("bg", par, ft)])
                    for ft in range(4):
                        b = psget()
                        for k in range(8):
                            P.op("pe", lambda e, k=k, ft=ft, b=b: e.matmul(
                                PS[:, b, 0:n], lhsT=Win[:, k, 1536 + ft * 128:1536 + (ft + 1) * 128], rhs=xnt[:, k, 0:n],
                                start=(k == 0), stop=(k == 7)), reads=["Win", "xnt"], writes=pk(b))
                        if not is_s:
                            P.op("dve", lambda e, ft=ft, b=b: e.tensor_tensor(out=Z[:, ft, 2:2 + n], in0=PS[:, b, 0:n],
                                                                              in1=xd[:, ft, 0:n], op=ALU.mult),
                                 reads=pk(b) + [("xd", par, ft)], writes=[("Z", par, ft)])
                        else:
                            P.op("dve", lambda e, ft=ft, b=b: e.tensor_tensor(
                                out=Zs[:, ft, :, 2:6], in0=PS[:, b, 0:NS].rearrange("p (b t) -> p b t", t=4),
                                in1=xd[:, ft, 0:NS].rearrange("p (b t) -> p b t", t=4), op=ALU.mult),
                                reads=pk(b) + [("xd", par, ft), ("Zs", ft)], writes=[("Zs", ft)])
                    if not is_s:
                        P.op("dve", lambda e: e.tensor_copy(out=xchalo[:], in_=XC[:, :, n:n + 15]),
                             reads=[("XC", par, q) for q in range(4)] + [(("XChalo", par), par)], writes=["xchalo"])
                        P.op("dve", lambda e: e.tensor_copy(out=zhalo[:], in_=Z[:, :, n:n + 2]),
                             reads=[("Z", par, q) for q in range(4)] + [(("Zhalo", par), par)], writes=["zhalo"])
                def back(ti, t0, n, is_s):
                    par = ti % 2
                    XC = XCL[par]; xd = xdL[par]; bg = bgL[par]; Z = ZL[par]
                    for gi in range(4):
                        w = 2 ** (gi + 1)
                        if not is_s:
                            L = 15 + n
                            src = XC[:, gi, 0:L]
                            bufs = [PA, PB]
                            cur = src
                            ckey = [("XC", par, gi), ("XChalo", par)]
                            d = 1
                            q = 0
                            while d < w:
                                dst = bufs[q % 2]
                                dk = "P%d" % (q % 2)
                                P.op("pool", lambda e, cur=cur, dst=dst, d=d, L=L: e.tensor_tensor(
                                    out=dst[:, d:L], in0=cur[:, d:L], in1=cur[:, 0:L - d], op=ALU.add),
                                    reads=ckey, writes=[dk], cost=800.0)
                                cur = dst[:, 0:L]
                                ckey = [dk]
                                d *= 2
                                q += 1
                            P.op("dve", lambda e, cur=cur, gi=gi, w=w: e.scalar_tensor_tensor(
                                out=diff[:, gi, 0:n], in0=cur[:, 15:15 + n], scalar=1.0 / w, in1=XC[:, gi, 15:15 + n],
                                op0=ALU.mult, op1=ALU.subtract), reads=ckey + [("XC", par, gi)], writes=[("diff", gi)])
                            if t0 == 0:
                                P.op("dve", lambda e, cur=cur, gi=gi: e.tensor_tensor(
                                    out=ca[:, 0:15], in0=cur[:, 15:30], in1=invn[:, gi, :], op=ALU.mult),
                                    reads=ckey + ["invn"], writes=["ca"])
                                P.op("dve", lambda e, gi=gi: e.tensor_tensor(
                                    out=diff[:, gi, 0:15], in0=ca[:, 0:15], in1=XC[:, gi, 15:30], op=ALU.subtract),
                                    reads=["ca", ("XC", par, gi), ("diff", gi)], writes=[("diff", gi)])
                        else:
                            L = 19
                            cur = XCs[:, gi, :, :]
                            ckey = [("XCs", gi)]
                            bufs = [PAs, PBs]
                            d = 1
                            q = 0
                            while d < w:
                                dst = bufs[q % 2]
                                dk = "Ps%d" % (q % 2)
                                P.op("pool", lambda e, cur=cur, dst=dst, d=d: e.tensor_tensor(
                                    out=dst[:, :, d:19], in0=cur[:, :, d:19], in1=cur[:, :, 0:19 - d], op=ALU.add),
                                    reads=ckey, writes=[dk], cost=800.0)
                                cur = dst[:, :, :]
                                ckey = [dk]
                                d *= 2
                                q += 1
                            P.op("dve", lambda e, cur=cur, gi=gi, w=w: e.scalar_tensor_tensor(
                                out=diff[:, gi, 0:NS].rearrange("p (b t) -> p b t", t=4), in0=cur[:, :, 15:19],
                                scalar=1.0 / w, in1=XCs[:, gi, :, 15:19], op0=ALU.mult, op1=ALU.subtract),
                                reads=ckey + [("XCs", gi)], writes=[("diff", gi)])
                        b = psget()
                        P.op("pe", lambda e, gi=gi, b=b: e.matmul(PS[:, b, 0:n], lhsT=Wp[:, gi, :], rhs=diff[:, gi, 0:n],
                                                                  start=True, stop=True),
                             reads=["Wsm", ("diff", gi)], writes=pk(b))
                        P.op("act", lambda e, gi=gi, b=b: e.activation(out=ymix[:, gi, 0:n], in_=PS[:, b, 0:n], func=AF.Copy,
                                                                       scale=pscale[:, i, gi:gi + 1]),
                             reads=pk(b) + ["pscale"], writes=["ymix"])
                    for ft in range(4):
                        if not is_s:
                            z0 = Z[:, ft, 0:n]; z1 = Z[:, ft, 1:n + 1]; z2 = Z[:, ft, 2:n + 2]
                            cav = ca[:, 0:n]
                            bgv = bg[:, ft, 0:n]
                            yv = ymix[:, 4 + ft, 0:n]
                            zk = [("Z", par, ft), ("Zhalo", par)]
                        else:
                            z0 = Zs[:, ft, :, 0:4]; z1 = Zs[:, ft, :, 1:5]; z2 = Zs[:, ft, :, 2:6]
                            cav = ca[:, 0:NS].rearrange("p (b t) -> p b t", t=4)
                            bgv = bg[:, ft, 0:NS].rearrange("p (b t) -> p b t", t=4)
                            yv = ymix[:, 4 + ft, 0:NS].rearrange("p (b t) -> p b t", t=4)
                            zk = [("Zs", ft)]
                        P.op("dve", lambda e, ft=ft, z0=z0, cav=cav: e.tensor_scalar(
                            out=cav, in0=z0, scalar1=cw[:, i, 0, ft:ft + 1], scalar2=cb[:, i, ft:ft + 1],
                            op0=ALU.mult, op1=ALU.add), reads=zk + ["cw", "cb"], writes=["ca"])
                        P.op("dve", lambda e, ft=ft, z1=z1, cav=cav: e.scalar_tensor_tensor(
                            out=cav, in0=z1, scalar=cw[:, i, 1, ft:ft + 1], in1=cav, op0=ALU.mult, op1=ALU.add),
                            reads=zk + ["cw", "ca"], writes=["ca"])
                        P.op("dve", lambda e, ft=ft, z2=z2, cav=cav: e.scalar_tensor_tensor(
                            out=cav, in0=z2, scalar=cw[:, i, 2, ft:ft + 1], in1=cav, op0=ALU.mult, op1=ALU.add),
                            reads=zk + ["cw", "ca"], writes=["ca"])
                        P.op("dve", lambda e, cav=cav, bgv=bgv, yv=yv: e.tensor_tensor(out=yv, in0=cav, in1=bgv, op=ALU.mult),
                             reads=["ca", ("bg", par, ft)], writes=["ymix"])
                    out_proj_tile(Wout, "Wout", ymix, "ymix", t0, n)
                    if (not is_s) and t0 + n == SEQ:
                        for ft in range(4):
                            b = psget()
                            P.op("pe", lambda e, b=b, ft=ft: e.transpose(PS[0:15, b, 0:128], xchalo[:, ft, :], ident[:]),
                                 reads=["xchalo", "ident"], writes=pk(b))
                            P.op("pe", lambda e, b=b, ft=ft: e.transpose(PS[0:2, b, 128:256], zhalo[:, ft, :], ident[:]),
                                 reads=["zhalo", "ident"], writes=pk(b))
                            P.op("act", lambda e, b=b, ft=ft: e.activation(out=opp[0:15, ft * 128:(ft + 1) * 128],
                                                                           in_=PS[0:15, b, 0:128], func=AF.Copy),
                                 reads=pk(b), writes=["opp"])
                            P.op("act", lambda e, b=b, ft=ft: e.activation(out=opc[0:2, ft * 128:(ft + 1) * 128],
                                                                           in_=PS[0:2, b, 128:256], func=AF.Copy),
                                 reads=pk(b), writes=["opc"])
                        P.dma("sp", o_p_pool[i], opp[0:15, :], reads=["opp"])
                        P.dma("sp", o_p_conv[i], opc[0:2, :], reads=["opc"])
                    if is_s:
                        for half in range(2):
                            for ft in range(4):
                                P.op("dve", lambda e, ft=ft, half=half: e.tensor_copy(
                                    out=xct[:, 0:120].rearrange("p (b r) -> p b r", r=15),
                                    in_=XCs[:, ft, half * 8:half * 8 + 8, 4:19]), reads=[("XCs", ft)], writes=["xct"])
                                b = psget()
                                P.op("pe", lambda e, b=b: e.transpose(PS[0:120, b, 0:128], xct[:, 0:120], ident[:]),
                                     reads=["xct", "ident"], writes=pk(b))
                                P.op("act", lambda e, b=b, ft=ft: e.activation(out=otp[0:120, ft * 128:(ft + 1) * 128],
                                                                               in_=PS[0:120, b, 0:128], func=AF.Copy),
                                     reads=pk(b), writes=["otp"])
                            P.dma("sp", o_s_pool[i, half * 120:half * 120 + 120, :], otp[0:120, :], reads=["otp"])
                        for ft in range(4):
                            P.op("dve", lambda e, ft=ft: e.tensor_copy(
                                out=zct[:, 0:32].rearrange("p (b r) -> p b r", r=2), in_=Zs[:, ft, :, 4:6]),
                                reads=[("Zs", ft)], writes=["zct"])
                            b = psget()
                            P.op("pe", lambda e, b=b: e.transpose(PS[0:32, b, 0:128], zct[:, 0:32], ident[:]),
                                 reads=["zct", "ident"], writes=pk(b))
                            P.op("act", lambda e, b=b, ft=ft: e.activation(out=otc[:, ft * 128:(ft + 1) * 128],
                                                                           in_=PS[0:32, b, 0:128], func=AF.Copy),
                                 reads=pk(b), writes=["otc"])
                        P.dma("sp", o_s_conv[i], otc[:, :], reads=["otc"])
                seq = list(enumerate(mtiles))
                for idx, (ti, (t0, n, is_s)) in enumerate(seq):
                    front(ti, t0, n, is_s)
                    if idx >= 1:
                        pti, (pt0, pn, ps_) = seq[idx - 1]
                        back(pti, pt0, pn, ps_)
                lti, (lt0, ln, ls_) = seq[-1]
                back(lti, lt0, ln, ls_)
                P.flush()

        def epilogue(st):
            gfin = WinP[:, 2, :].bitcast(F32)
            P.dma("sp", gfin, norm_final.broadcast_to([128, D]), writes=["gfin"])
            junk = sb(st, "fjunk", [128, 512], BF16)
            ss = sb(st, "fss", [128, 2, 2])
            yo = [WinP[:, q, :].bitcast(F32) for q in range(2)]
            nsub = SEQ // 128 + 1
            for si in range(nsub):
                n = 128 if si < SEQ // 128 else NS
                dst = y_p[si * 128:(si + 1) * 128, :] if si < SEQ // 128 else y_s[:, :]
                par = si % 2
                yb = yo[par]
                yk = "yo%d" % par
                hk = hkeys(si * 128, n)
                bb = psget(2)
                for k in range(8):
                    P.op("pe", lambda e, k=k, bb=bb, si=si, n=n: e.transpose(
                        PS[0:n, bb + k // 4, (k % 4) * 128:(k % 4 + 1) * 128], hres[:, k, si * 128:si * 128 + n], ident[:]),
                        reads=["ident"] + hk, writes=pk(bb, 2), cost=110.0)
                for half in range(2):
                    P.op("act", lambda e, bb=bb, half=half, n=n, par=par: e.activation(
                        out=junk[0:n, :], in_=PS[0:n, bb + half, :], func=AF.Square, accum_out=ss[0:n, par, half:half + 1]),
                        reads=pk(bb, 2), writes=["fjunk", ("fss", par, half)])
                P.op("dve", lambda e, n=n, par=par: e.tensor_tensor(out=ss[0:n, par, 0:1], in0=ss[0:n, par, 0:1],
                                                                     in1=ss[0:n, par, 1:2], op=ALU.add),
                     reads=[("fss", par, 0), ("fss", par, 1)], writes=[("fss", par, 0)], cost=100.0)
                P.op("act", lambda e, n=n, par=par: e.activation(out=ss[0:n, par, 0:1], in_=ss[0:n, par, 0:1], func=AF.Sqrt,
                                                                  bias=epsc[0:n, 0:1], scale=1.0 / D),
                     reads=[("fss", par, 0)], writes=[("fss", par, 0)], cost=250.0)
                P.op("dve", lambda e, n=n, par=par: e.reciprocal(out=ss[0:n, par, 0:1], in_=ss[0:n, par, 0:1]),
                     reads=[("fss", par, 0)], writes=[("fss", par, 0)], cost=100.0)
                for half in range(2):
                    P.op("dve", lambda e, bb=bb, half=half, n=n, yb=yb, par=par: e.scalar_tensor_tensor(
                        out=yb[0:n, half * 512:(half + 1) * 512], in0=PS[0:n, bb + half, :], scalar=ss[0:n, par, 0:1],
                        in1=gfin[0:n, half * 512:(half + 1) * 512], op0=ALU.mult, op1=ALU.mult),
                        reads=pk(bb, 2) + [("fss", par, 0), "gfin"], writes=[yk], cost=750.0)
                P.dma("sp", dst, yb[0:n, :], reads=[yk])

        def ffn(layer):
            widths = [384] * 7 + [128]
            offs = [sum(widths[:j]) for j in range(len(widths))]
            with ExitStack() as st:
                xn = sb(st, "xn_all", [128, 8, T], BF16)
                Wg = [sb(st, "Wg%d" % q, [128, 8, 384], BF16) for q in range(2)]
                Wu = [sb(st, "Wu%d" % q, [128, 8, 384], BF16) for q in range(2)]
                Wd = [sb(st, "Wd%d" % q, [128, 3, D], BF16) for q in range(2)]
                sl = [sb(st, "sl%d" % q, [128, TF]) for q in range(2)]
                hb = [sb(st, "hb%d" % q, [128, 3, TF], BF16) for q in range(2)]

                P.cost.update({"pe": 195.0, "dve": 630.0, "act": 560.0})

                def load_slice(j):
                    q = j % 2
                    w = widths[j]
                    o = offs[j]
                    c = 2500.0 + 128 * 8 * w * 4 / 150.0
                    for kh in range(2):
                        P.dma("pool", Wg[q][:, 4 * kh:4 * kh + 4, 0:w],
                              ffn_g[layer].rearrange("(k p) n -> p k n", p=128)[:, 4 * kh:4 * kh + 4, o:o + w],
                              writes=[("Wg", q, kh)], cost=c / 2)
                    for kh in range(2):
                        P.dma("pool", Wu[q][:, 4 * kh:4 * kh + 4, 0:w],
                              ffn_u[layer].rearrange("(k p) n -> p k n", p=128)[:, 4 * kh:4 * kh + 4, o:o + w],
                              writes=[("Wu", q, kh)], cost=c / 2)
                    P.dma("pool", Wd[q][:, 0:w // 128, :],
                          ffn_d[layer].rearrange("(k p) n -> p k n", p=128)[:, o // 128:(o + w) // 128, :],
                          writes=[("Wd", q)], cost=c)
                load_slice(0)
                load_slice(1)
                if layer + 1 < 4:
                    load_mixer_weights(layer + 1)
                for ti, (t0, n, is_s) in enumerate(ftiles):
                    if ti == 0:
                        rmsnorm_tile(st, "f", t0, n, gffn[:, layer, :], xn, ("xn", t0), xoff=t0)
                    else:
                        _rms_ops("f", t0, n, gffn[:, layer, :], xn, ("xn", t0), _norm_scr["f"], xoff=t0)
                hbi = 0
                for j in range(len(widths)):
                    q = j % 2
                    nhc = widths[j] // 128
                    for (t0, n, is_s) in ftiles:
                        hk = hkeys(t0, n)
                        hbuf = hb[hbi % 2]
                        hkey = "hb%d" % (hbi % 2)
                        hbi += 1
                        for hc in range(nhc):
                            bgt = psget()
                            for k in range(8):
                                P.op("pe", lambda e, k=k, hc=hc, bgt=bgt, q=q, t0=t0, n=n: e.matmul(
                                    PS[:, bgt, 0:n], lhsT=Wg[q][:, k, hc * 128:(hc + 1) * 128], rhs=xn[:, k, t0:t0 + n],
                                    start=(k == 0), stop=(k == 7)), reads=[("Wg", q, k // 4), ("xn", t0)], writes=pk(bgt),
                                    cost=n / 2.35 + 6)
                            but = psget()
                            for k in range(8):
                                P.op("pe", lambda e, k=k, hc=hc, but=but, q=q, t0=t0, n=n: e.matmul(
                                    PS[:, but, 0:n], lhsT=Wu[q][:, k, hc * 128:(hc + 1) * 128], rhs=xn[:, k, t0:t0 + n],
                                    start=(k == 0), stop=(k == 7)), reads=[("Wu", q, k // 4), ("xn", t0)], writes=pk(but),
                                    cost=n / 2.35 + 6)
                            slt = sl[hc % 2]
                            slk = "sl%d" % (hc % 2)
                            P.op("act", lambda e, bgt=bgt, slt=slt, n=n: e.activation(out=slt[:, 0:n], in_=PS[:, bgt, 0:n], func=AF.Silu),
                                 reads=pk(bgt), writes=[slk], cost=(224 + n) / 1.2)
                            P.op("dve", lambda e, but=but, slt=slt, hbuf=hbuf, hc=hc, n=n: e.tensor_tensor(
                                out=hbuf[:, hc, 0:n], in0=slt[:, 0:n], in1=PS[:, but, 0:n], op=ALU.mult),
                                reads=pk(but) + [slk], writes=[(hkey, hc)], cost=(160 + n) / 0.96)
                        for fo in range(8):
                            b = psget()
                            for hc in range(nhc):
                                P.op("pe", lambda e, hc=hc, fo=fo, b=b, q=q, hbuf=hbuf, n=n, nhc=nhc: e.matmul(
                                    PS[:, b, 0:n], lhsT=Wd[q][:, hc, fo * 128:(fo + 1) * 128], rhs=hbuf[:, hc, 0:n],
                                    start=(hc == 0), stop=(hc == nhc - 1)), reads=[("Wd", q), (hkey, hc)], writes=pk(b),
                                    cost=n / 2.35 + 6)
                            P.op("dve", lambda e, fo=fo, b=b, t0=t0, n=n: e.tensor_tensor(
                                out=hres[:, fo, t0:t0 + n], in0=hres[:, fo, t0:t0 + n], in1=PS[:, b, 0:n], op=ALU.add),
                                reads=pk(b) + hk, writes=hk, cost=(160 + n) / 0.96)
                    if j + 2 < len(widths):
                        load_slice(j + 2)
                if layer == 3:
                    epilogue(st)
                P.flush()

        for layer in range(4):
            if layer > 0:
                P.next_epoch()
            if layer % 2 == 0:
                even_mixer(layer)
            else:
                odd_mixer(layer)
            ffn(layer)

    return nc


_NC_CACHE = {}


def kernel(**inputs):
    f = lambda a: np.ascontiguousarray(np.asarray(a, dtype=np.float32))
    inp = {k: f(v) for k, v in inputs.items()}
    if "nc" not in _NC_CACHE:
        _NC_CACHE["nc"] = build_nc()
    nc = _NC_CACHE["nc"]
    shared = {}
    for k in ("norm_mix", "norm_ffn", "w_in_even", "w_out_even", "s5_lambda_re", "s5_lambda_im", "s5_log_dt",
              "s5_b_re", "s5_b_im", "s5_c_re", "s5_c_im", "s5_glu_w", "s5_glu_b", "sgu_norm", "sgu_w", "sgu_b",
              "w_in_odd", "w_out_odd", "pool_w", "pool_scale", "conv_w", "conv_b", "ffn_w_gate", "ffn_w_up",
              "ffn_w_down"):
        shared[k] = inp[k]
    shared["norm_final"] = inp["norm_final"].reshape(1, D)
    shared["s5_d"] = inp["s5_d"].reshape(2, 512)
    in_maps = []
    for c in range(NCORES):
        m = dict(shared)
        m["x_p"] = inp["x_prompt"][c]
        m["x_s"] = np.ascontiguousarray(inp["x_sample"][16 * c:16 * c + 16].reshape(NS, D))
        m["st_re"] = np.ascontiguousarray(inp["state_s5_re"][:, 16 * c:16 * c + 16].reshape(2, 16, 2048))
        m["st_im"] = np.ascontiguousarray(inp["state_s5_im"][:, 16 * c:16 * c + 16].reshape(2, 16, 2048))
        m["st_pool"] = np.ascontiguousarray(inp["state_pool"][:, 16 * c:16 * c + 16])
        m["st_conv"] = np.ascontiguousarray(inp["state_conv"][:, 16 * c:16 * c + 16])
        in_maps.append(m)
    res = run_bass_kernel_spmd(nc, in_maps, core_ids=list(range(NCORES)))
    R = res.results
    y_prompt = np.stack([R[c]["y_p"] for c in range(NCORES)], 0).reshape(8, SEQ, D)
    y_sample = np.concatenate([R[c]["y_s"].reshape(16, 4, D) for c in range(NCORES)], 0)
    p_re = np.stack([R[c]["o_p_re"].reshape(2, 32, 64) for c in range(NCORES)], 1)
    p_im = np.stack([R[c]["o_p_im"].reshape(2, 32, 64) for c in range(NCORES)], 1)
    p_pool = np.stack([R[c]["o_p_pool"] for c in range(NCORES)], 1)
    p_conv = np.stack([R[c]["o_p_conv"] for c in range(NCORES)], 1)
    s_re = np.concatenate([R[c]["o_s_re"].reshape(2, 16, 32, 64) for c in range(NCORES)], 1)
    s_im = np.concatenate([R[c]["o_s_im"].reshape(2, 16, 32, 64) for c in range(NCORES)], 1)
    s_v = np.concatenate([R[c]["o_s_v"].reshape(2, 16, 4, 512) for c in range(NCORES)], 1)
    s_pool = np.concatenate([R[c]["o_s_pool"].reshape(2, 16, 15, 512) for c in range(NCORES)], 1)
    s_conv = np.concatenate([R[c]["o_s_conv"].reshape(2, 16, 2, 512) for c in range(NCORES)], 1)
    outs = (y_prompt, y_sample, p_re, p_im, p_pool, p_conv, s_re, s_im, s_v, s_pool, s_conv)
    return tuple(np.ascontiguousarray(o.astype(np.float32)) for o in outs)
```

```python
import math
import numpy as np
from contextlib import ExitStack
import concourse.bass as bass
import concourse.mybir as mybir
from concourse.bass_utils import run_bass_kernel_spmd

F32 = mybir.dt.float32
BF16 = mybir.dt.bfloat16
I32 = mybir.dt.int32
ALU = mybir.AluOpType
AF = mybir.ActivationFunctionType

ENGS = ("pe", "act", "dve", "pool", "sp")
NSLOT = 12
NCORES = 8
D = 1024
SEQ = 2048
NS = 64
T = SEQ + NS
DFF = 2816
EPS = 1e-6
TM = 256
TF = 448
FS = 256
NSL = DFF // FS


class _Op(object):
    __slots__ = ("idx", "eng", "emit", "deps", "signal", "epoch", "semval",
                 "is_dma", "slot", "dval", "prev_dval", "cost", "pos")


class Prog(object):
    def __init__(self, nc, es, n_epochs=6):
        self.nc = nc
        self.ops = []
        self.regions = {}
        self.epoch = 0
        self.n_epochs = n_epochs
        self.sems = {}
        self.cnt = {}
        for e in ENGS:
            for ep in range(n_epochs):
                self.sems[(e, ep)] = es.enter_context(nc.semaphore("s_%s_%d" % (e, ep)))
        self.dsems = {}
        self.dcount = {}
        self.dnext = {}
        for q in ("sp", "pool", "act"):
            self.dnext[q] = 0
            for s in range(NSLOT):
                self.dsems[(q, s)] = es.enter_context(nc.semaphore("d_%s_%d" % (q, s)))
                self.dcount[(q, s)] = 0
        self.nflush = 0
        self.cost = {"pe": 115.0, "act": 450.0, "dve": 430.0, "pool": 600.0, "sp": 100.0}
        self.reorder = True
        self.filler = None
        self.filler_cost = 170.0
        self.nfill = 0

    def next_epoch(self):
        assert not self.ops
        self.epoch = min(self.epoch + 1, self.n_epochs - 1)

    def _add(self, eng, emit, reads, writes, is_dma, cost):
        o = _Op()
        o.idx = len(self.ops)
        o.eng = eng
        o.emit = emit
        o.signal = False
        o.epoch = self.epoch
        o.semval = None
        o.is_dma = is_dma
        o.cost = cost if cost is not None else (3000.0 if is_dma else self.cost[eng])
        deps = set()
        for k in reads:
            r = self.regions.get(k)
            if r is not None and r[0] is not None:
                deps.add(r[0])
        for k in writes:
            r = self.regions.get(k)
            if r is not None:
                if r[0] is not None:
                    deps.add(r[0])
                deps.update(r[1])
        for k in reads:
            r = self.regions.get(k)
            if r is None:
                r = [None, []]
                self.regions[k] = r
            r[1].append(o.idx)
        for k in writes:
            self.regions[k] = [o.idx, []]
        deps.discard(o.idx)
        o.deps = deps
        o.slot = None
        self.ops.append(o)
        return o

    def op(self, eng, emit, reads=(), writes=(), cost=None):
        return self._add(eng, emit, reads, writes, False, cost)

    def dma(self, q, out, in_, reads=(), writes=(), cost=None, **kw):
        def emit(e, out=out, in_=in_, kw=kw):
            return e.dma_start(out=out, in_=in_, **kw)
        return self._add(q, emit, reads, writes, True, cost)

    def _schedule(self):
        ops = self.ops
        n = len(ops)
        succs = [[] for _ in range(n)]
        indeg = [0] * n
        for o in ops:
            for d in o.deps:
                succs[d].append(o.idx)
            indeg[o.idx] = len(o.deps)
        lastd = {}
        dchain = {}
        for o in ops:
            if o.is_dma:
                if o.eng in lastd:
                    dchain[o.idx] = lastd[o.eng]
                lastd[o.eng] = o.idx
        prio = [0.0] * n
        for i in range(n - 1, -1, -1):
            m = 0.0
            for s_ in succs[i]:
                if prio[s_] > m:
                    m = prio[s_]
            prio[i] = ops[i].cost + m
        order = {e: [] for e in ENGS}
        if not self.reorder:
            for o in ops:
                order[o.eng].append(o)
            return order
        ready = {e: [] for e in ENGS}
        ready_t = [0.0] * n
        fin = [0.0] * n
        issued = [False] * n
        free_at = {e: 0.0 for e in ENGS}
        for o in ops:
            if indeg[o.idx] == 0:
                ready[o.eng].append(o.idx)
        remaining = n
        HOP = 150.0
        while remaining:
            best = None
            for e in ENGS:
                rl = ready[e]
                if not rl:
                    continue
                fa = free_at[e]
                cb = None
                for i in rl:
                    o = ops[i]
                    if o.is_dma and i in dchain and not issued[dchain[i]]:
                        continue
                    st = ready_t[i] if ready_t[i] > fa else fa
                    key = (st, -prio[i], i)
                    if cb is None or key < cb:
                        cb = key
                if cb is not None and (best is None or cb < best[0]):
                    best = (cb, e)
            assert best is not None, "scheduler deadlock"
            (st, _, i), e = best
            o = ops[i]
            ready[e].remove(i)
            issued[i] = True
            if e == "pe" and self.filler is not None and free_at[e] > 0.0:
                gap = st - free_at[e]
                if gap > 1200.0:
                    k = min(int((gap - 500.0) / self.filler_cost), 60)
                    for _ in range(k):
                        f = _Op()
                        f.idx = -1
                        f.eng = "pe"
                        f.emit = self.filler
                        f.is_dma = False
                        f.signal = False
                        order[e].append(f)
                    self.nfill += k
            if o.is_dma:
                free_at[e] = st + 80.0
                fin[i] = st + o.cost
            else:
                free_at[e] = st + o.cost
                fin[i] = st + o.cost
            order[e].append(o)
            remaining -= 1
            for s_ in succs[i]:
                t = fin[i] + (0.0 if (ops[s_].eng == e and e == "pe") else HOP)
                if t > ready_t[s_]:
                    ready_t[s_] = t
                indeg[s_] -= 1
                if indeg[s_] == 0:
                    ready[ops[s_].eng].append(s_)
        self.est_time = max(fin) if n else 0.0
        return order

    def flush(self):
        nc = self.nc
        ops = self.ops
        if not ops:
            return
        per_eng = self._schedule()
        for e in ENGS:
            for p_, o in enumerate(per_eng[e]):
                o.pos = p_
            if e != "pe":
                assert all(o.idx >= 0 for o in per_eng[e])
        for e in ("sp", "pool", "act"):
            for o in per_eng[e]:
                if o.is_dma:
                    s = self.dnext[e]
                    self.dnext[e] = (s + 1) % NSLOT
                    o.slot = s
                    o.prev_dval = self.dcount[(e, s)]
                    self.dcount[(e, s)] += 16
                    o.dval = self.dcount[(e, s)]
        red = []
        for o in ops:
            comp = {}
            dmas = []
            for d in o.deps:
                p = ops[d]
                if p.is_dma:
                    dmas.append(d)
                else:
                    if p.eng == "pe" and o.eng == "pe" and not o.is_dma:
                        continue
                    if p.eng not in comp or ops[comp[p.eng]].pos < p.pos:
                        comp[p.eng] = d
            red.append((comp, dmas))
            for d in comp.values():
                ops[d].signal = True
        cnt = self.cnt
        for e in ENGS:
            for o in per_eng[e]:
                if o.idx < 0 or o.is_dma or not o.signal:
                    continue
                key = (o.eng, o.epoch)
                cnt[key] = cnt.get(key, 0) + 1
                o.semval = cnt[key]
        sems = self.sems
        dsems = self.dsems
        n_ep = self.n_epochs
        dcount = self.dcount

        def emit_engine(e, eng_name):
            waited = {}
            dwaited = {}
            for o in per_eng[eng_name]:
                if o.idx < 0:
                    o.emit(e)
                    continue
                comp, dmas = red[o.idx]
                for pe_name, d in comp.items():
                    p = ops[d]
                    done = False
                    for ep in range(p.epoch, n_ep):
                        w = waited.get((pe_name, ep), 0)
                        if ep == p.epoch and w >= p.semval:
                            done = True
                        if ep > p.epoch and w > 0:
                            done = True
                    if done:
                        continue
                    e.wait_ge(sems[(pe_name, p.epoch)], p.semval)
                    waited[(pe_name, p.epoch)] = p.semval
                for d in dmas:
                    p = ops[d]
                    k = (p.eng, p.slot)
                    if dwaited.get(k, 0) >= p.dval:
                        continue
                    e.wait_ge(dsems[k], p.dval)
                    dwaited[k] = p.dval
                if o.is_dma:
                    k = (o.eng, o.slot)
                    if o.prev_dval > 0 and dwaited.get(k, 0) < o.prev_dval:
                        e.wait_ge(dsems[k], o.prev_dval)
                        dwaited[k] = o.prev_dval
                    inst = o.emit(e)
                    inst.then_inc(dsems[k], 16)
                else:
                    inst = o.emit(e)
                    if o.signal:
                        inst.then_inc(sems[(o.eng, o.epoch)], 1)
            if eng_name in ("sp", "pool", "act"):
                for s in range(NSLOT):
                    k = (eng_name, s)
                    if dcount[k] > 0 and dwaited.get(k, 0) < dcount[k]:
                        e.wait_ge(dsems[k], dcount[k])

        with nc.Block() as block:
            @block.tensor
            def _(e):
                emit_engine(e, "pe")

            @block.scalar
            def _(e):
                emit_engine(e, "act")

            @block.vector
            def _(e):
                emit_engine(e, "dve")

            @block.gpsimd
            def _(e):
                emit_engine(e, "pool")

            @block.sync
            def _(e):
                emit_engine(e, "sp")
        self.ops = []
        self.regions = {}
        self.nflush += 1


def build_nc(debug=False):
    nc = bass.Bass("TRN2", target_bir_lowering=False)
    try:
        nc.allow_low_precision("bf16 matmul operands with fp32 accumulation by design")
    except Exception:
        pass

    def din(name, shape):
        return nc.dram_tensor(name, list(shape), F32, kind="ExternalInput").ap()

    def dout(name, shape):
        return nc.dram_tensor(name, list(shape), F32, kind="ExternalOutput").ap()

    x_p = din("x_p", (SEQ, D))
    x_s = din("x_s", (NS, D))
    st_re = din("st_re", (2, 16, 2048))
    st_im = din("st_im", (2, 16, 2048))
    st_pool = din("st_pool", (2, 16, 15, 512))
    st_conv = din("st_conv", (2, 16, 2, 512))
    norm_mix = din("norm_mix", (4, D))
    norm_ffn = din("norm_ffn", (4, D))
    norm_final = din("norm_final", (1, D))
    w_in_even = din("w_in_even", (2, D, 1536))
    w_out_even = din("w_out_even", (2, D, D))
    lam_re = din("s5_lambda_re", (2, 32, 64))
    lam_im = din("s5_lambda_im", (2, 32, 64))
    log_dt = din("s5_log_dt", (2, 32))
    b_re = din("s5_b_re", (2, 32, 64, 16))
    b_im = din("s5_b_im", (2, 32, 64, 16))
    c_re = din("s5_c_re", (2, 32, 16, 64))
    c_im = din("s5_c_im", (2, 32, 16, 64))
    s5_d = din("s5_d", (2, 512))
    glu_w = din("s5_glu_w", (2, 512, 512))
    glu_b = din("s5_glu_b", (2, 512))
    sgu_norm = din("sgu_norm", (2, 512))
    sgu_w = din("sgu_w", (2, 8, 128, 128))
    sgu_b = din("sgu_b", (2, 8, 128))
    w_in_odd = din("w_in_odd", (2, D, 2048))
    w_out_odd = din("w_out_odd", (2, D, D))
    pool_w = din("pool_w", (2, 4, 128, 128))
    pool_scale = din("pool_scale", (2, 512))
    conv_w = din("conv_w", (2, 3, 512))
    conv_b = din("conv_b", (2, 512))
    ffn_g = din("ffn_w_gate", (4, D, DFF))
    ffn_u = din("ffn_w_up", (4, D, DFF))
    ffn_d = din("ffn_w_down", (4, DFF, D))

    y_p = dout("y_p", (SEQ, D))
    y_s = dout("y_s", (NS, D))
    o_p_re = dout("o_p_re", (2, 16, 128))
    o_p_im = dout("o_p_im", (2, 16, 128))
    o_p_pool = dout("o_p_pool", (2, 15, 512))
    o_p_conv = dout("o_p_conv", (2, 2, 512))
    o_s_re = dout("o_s_re", (2, 16, 2048))
    o_s_im = dout("o_s_im", (2, 16, 2048))
    o_s_v = dout("o_s_v", (2, NS, 512))
    o_s_pool = dout("o_s_pool", (2, 240, 512))
    o_s_conv = dout("o_s_conv", (2, 32, 512))
    dbg = dout("dbg", (128, 4096)) if debug else None

    es = ExitStack()
    with es:
        es.enter_context(nc.allow_non_contiguous_dma(reason="small strided parameter loads"))
        P = Prog(nc, es)

        _uid = [0]

        def sb(stk, name, shape, dt=F32):
            _uid[0] += 1
            return stk.enter_context(nc.sbuf_tensor("%s_u%d" % (name, _uid[0]), list(shape), dt))

        PS = es.enter_context(nc.psum_tensor("PS", [128, 8, 512], F32))
        ps_rr = [0]

        def psget(n=1):
            b = ps_rr[0]
            if b + n > 8:
                b = 0
            ps_rr[0] = (b + n) % 8
            return b

        def pk(b, n=1):
            return [("ps", b + i) for i in range(n)]

        hres = sb(es, "hres", [128, 8, T])
        ident = sb(es, "ident", [128, 128])
        identb = sb(es, "identb", [128, 128], BF16)
        onesb = sb(es, "onesb", [128, 128], BF16)
        iot = sb(es, "iot", [128, 128])
        iop = sb(es, "iop", [128, 1])
        pstage = sb(es, "pstage", [128, 128])
        pvec = sb(es, "pvec", [128, 120])
        gmix = pvec[:, 0:32].rearrange("p (l k) -> p l k", l=4)
        gffn = pvec[:, 32:64].rearrange("p (l k) -> p l k", l=4)
        glub = pvec[:, 64:72].rearrange("p (l k) -> p l k", l=2)
        pscale = pvec[:, 72:80].rearrange("p (l k) -> p l k", l=2)
        cw = pvec[:, 80:104].rearrange("p (l c k) -> p l c k", l=2, c=3)
        cb = pvec[:, 104:112].rearrange("p (l k) -> p l k", l=2)
        dcol = pvec[:, 112:120].rearrange("p (l k) -> p l k", l=2)
        epsc = sb(es, "epsc", [128, 1])
        s5car = sb(es, "s5car", [128, 16, 2])
        xchalo = sb(es, "xchalo", [128, 4, 15])
        zhalo = sb(es, "zhalo", [128, 4, 2])

        def V(e):
            return e

        def load_w(dst, src3, key, nsplit):
            K = dst.shape[1]
            step = K // nsplit
            for i in range(nsplit):
                nb = 128 * step * dst.shape[2] * 4
                P.dma("pool", dst[:, i * step:(i + 1) * step, :],
                      src3.rearrange("(k p) n -> p k n", p=128)[:, i * step:(i + 1) * step, :], writes=[(key, i)],
                      cost=2500.0 + nb / 150.0)

        WinP = sb(es, "WinP", [128, 8, 2048], BF16)
        WoutP = sb(es, "WoutP", [128, 8, D], BF16)
        WsmP = sb(es, "WsmP", [128, 2048], BF16)

        def load_mixer_weights(layer, only_in=False, skip_in=False):
            i = layer // 2
            if layer % 2 == 0:
                if not skip_in:
                    load_w(WinP[:, :, 0:1536], w_in_even[i], "Win", 4)
                if only_in:
                    return
                load_w(WsmP[:, :].rearrange("p (k n) -> p k n", k=4), glu_w[i], "Wsm", 1)
                load_w(WoutP, w_out_even[i], "Wout", 2)
            else:
                load_w(WinP, w_in_odd[i], "Win", 4)
                load_w(WoutP, w_out_odd[i], "Wout", 2)
                P.dma("pool", WsmP[:, 0:512].rearrange("p (g d) -> p g d", g=4), pool_w[i].rearrange("g c d -> c g d"),
                      writes=["Wsm"])

        load_mixer_weights(0, only_in=True)
        P.op("pool", lambda e: e.iota(iot[:], pattern=[[1, 128]], base=0, channel_multiplier=0,
                                      allow_small_or_imprecise_dtypes=True), writes=["iot"])
        P.op("pool", lambda e: e.iota(iop[:], pattern=[[1, 1]], base=0, channel_multiplier=1,
                                      allow_small_or_imprecise_dtypes=True), writes=["iop"])
        P.op("dve", lambda e: e.tensor_scalar(out=ident[:], in0=iot[:], scalar1=iop[:, 0:1], scalar2=None,
                                              op0=ALU.is_equal), reads=["iot", "iop"], writes=["ident"])
        P.op("dve", lambda e: e.tensor_copy(out=identb[:], in_=ident[:]), reads=["ident"], writes=["identb"])
        P.op("dve", lambda e: e.memset(onesb[:], 1.0), writes=["onesb"])
        P.op("dve", lambda e: e.memset(epsc[:], EPS), writes=["epsc"])
        P.filler = None; _unused_filler = lambda e: e.matmul(PS[:, 7, 0:128], lhsT=onesb[:], rhs=onesb[:], start=True, stop=True)
        with ExitStack() as st:
            pass
        pst_rows = [(norm_mix.rearrange("l (k p) -> (l k) p", p=128), 32), (norm_ffn.rearrange("l (k p) -> (l k) p", p=128), 32),
                    (glu_b.rearrange("l (k p) -> (l k) p", p=128), 8), (pool_scale.rearrange("l (k p) -> (l k) p", p=128), 8),
                    (conv_w.rearrange("l c (k p) -> (l c k) p", p=128), 24), (conv_b.rearrange("l (k p) -> (l k) p", p=128), 8),
                    (s5_d.rearrange("l (k p) -> (l k) p", p=128), 8)]
        r0 = 0
        for j_, (src_, nr_) in enumerate(pst_rows):
            P.dma("sp", pstage[r0:r0 + nr_, :], src_, writes=[("pstage", j_)])
            r0 += nr_
        P.op("pe", lambda e: e.transpose(PS[:, 5, 0:120], pstage[0:120, :], ident[0:120, 0:120]),
             reads=[("pstage", j_) for j_ in range(7)] + ["ident"], writes=[("ps", 5)])
        P.op("act", lambda e: e.activation(out=pvec[:, :], in_=PS[:, 5, 0:120], func=AF.Copy), reads=[("ps", 5)],
             writes=["gmix", "gffn", "glub", "pscale", "cw", "cb", "dcol"])

        def load_x(xt):
            nsub = SEQ // 128 + 1
            for si in range(nsub):
                n = 128 if si < SEQ // 128 else NS
                src = x_p[si * 128:(si + 1) * 128, :] if si < SEQ // 128 else x_s[:, :]
                xb = xt[si % 2]
                xk = "xt%d" % (si % 2)
                P.dma("sp", xb[0:n, :], src, writes=[xk])
                for half in range(2):
                    b = psget()
                    for q in range(4):
                        k = half * 4 + q
                        P.op("pe", lambda e, b=b, q=q, k=k, xb=xb, n=n: e.transpose(
                            PS[:, b, q * 128:q * 128 + n], xb[0:n, k * 128:(k + 1) * 128], ident[0:n, 0:n]),
                            reads=[xk, "ident"], writes=pk(b))
                    eng = "act" if half == 0 else "dve"
                    if eng == "act":
                        P.op("act", lambda e, b=b, half=half, si=si, n=n: e.activation(
                            out=hres[:, half * 4:half * 4 + 4, si * 128:si * 128 + n],
                            in_=PS[:, b, :].rearrange("p (q t) -> p q t", q=4)[:, :, 0:n], func=AF.Copy),
                            reads=pk(b), writes=[("h", si, half)])
                    else:
                        P.op("dve", lambda e, b=b, half=half, si=si, n=n: e.tensor_copy(
                            out=hres[:, half * 4:half * 4 + 4, si * 128:si * 128 + n],
                            in_=PS[:, b, :].rearrange("p (q t) -> p q t", q=4)[:, :, 0:n]),
                            reads=pk(b), writes=[("h", si, half)])

        mtiles = [(i * TM, TM, False) for i in range(SEQ // TM)] + [(SEQ, NS, True)]
        ftiles = [(0, 448, False), (448, 448, False), (896, 448, False), (1344, 448, False), (1792, 320, False)]

        def hkeys(t0, n):
            ks = []
            a = (t0 // TM) * TM
            while a < t0 + n:
                ks.append(("hres", a))
                a += TM
            return ks

        def s5_setup(stk, i, XB, YC, BD, TC, TS, R4, A4, bmask):
            with ExitStack() as st:
                def t16(name):
                    return sb(st, "s5_" + name, [128, 16])
                LRI = sb(st, "s5_LRI", [128, 32])
                LR = LRI[:, 0:16]
                LI = LRI[:, 16:32]
                LDT, DT, Z, MAG, ANG = [t16(n_) for n_ in ("LDT", "DT", "Z", "MAG", "ANG")]
                SN, CS, ta, tb, tc_, td = [t16(n_) for n_ in ("SN", "CS", "ta", "tb", "tc", "td")]
                FR, FI = t16("FR"), t16("FI")
                AR = [t16("AR%d" % k) for k in range(5)]
                AI = [t16("AI%d" % k) for k in range(5)]
                BR = sb(st, "s5_BR", [128, 16, 32]); BI = sb(st, "s5_BI", [128, 16, 32])
                BBr = sb(st, "s5_BBr", [128, 16, 32]); BBi = sb(st, "s5_BBi", [128, 16, 32])
                CTr = sb(st, "s5_CTr", [128, 16, 32]); CTi = sb(st, "s5_CTi", [128, 16, 32])
                Yr = sb(st, "s5_Yr", [128, 16, 32]); Yi = sb(st, "s5_Yi", [128, 16, 32])
                W1 = sb(st, "s5_W1", [128, 16, 32]); W2 = sb(st, "s5_W2", [128, 16, 32])
                W3 = sb(st, "s5_W3", [128, 16, 32]); W4 = sb(st, "s5_W4", [128, 16, 32])
                Xr_ = sb(st, "s5_Xr", [128, 16, 32]); Xi_ = sb(st, "s5_Xi", [128, 16, 32])
                CNr = sb(st, "s5_CNr", [128, 4, 128]); CNi = sb(st, "s5_CNi", [128, 4, 128])
                cnt = [0]

                def dv(fn, reads, writes, eng="dve"):
                    P.op(eng, fn, reads=reads, writes=writes)

                def tt(out, a, b, op, r, w, eng="dve"):
                    dv(lambda e: e.tensor_tensor(out=out, in0=a, in1=b, op=op), r, w, eng=eng)

                def ts(out, a, s1, op0, r, w, s2=None, op1=None):
                    if op1 is None:
                        dv(lambda e: e.tensor_scalar(out=out, in0=a, scalar1=s1, scalar2=None, op0=op0), r, w)
                    else:
                        dv(lambda e: e.tensor_scalar(out=out, in0=a, scalar1=s1, scalar2=s2, op0=op0, op1=op1), r, w)

                lst = sb(st, "s5_lst", [32, 128])
                P.dma("sp", lst[0:16, :], lam_re[i].rearrange("(P g) n -> P (g n)", g=2), writes=[("lst", 0)])
                P.dma("sp", lst[16:32, :], lam_im[i].rearrange("(P g) n -> P (g n)", g=2), writes=[("lst", 1)])
                bl_ = psget()
                P.op("pe", lambda e: e.transpose(PS[:, bl_, 0:32], lst[:, :], ident[0:32, 0:32]),
                     reads=[("lst", 0), ("lst", 1), "ident"], writes=pk(bl_))
                P.op("act", lambda e: e.activation(out=LRI[:, :], in_=PS[:, bl_, 0:32], func=AF.Copy), reads=pk(bl_),
                     writes=["LR", "LI"])
                for g2 in range(2):
                    P.dma("sp", LDT[64 * g2:64 * g2 + 64, :],
                          log_dt[i:i + 1, :].rearrange("o (P g) -> o g P", g=2)[:, g2, :].broadcast_to([64, 16]),
                          writes=[("LDT", g2)])
                for tl in (BR, BI, CNr, CNi):
                    dv(lambda e, tl=tl: e.memset(tl[:], 0.0), [], ["z_" + tl.name], eng="pool")
                for (tl, src) in ((BR, b_re), (BI, b_im)):
                    for g2 in range(2):
                        P.dma("sp", tl[64 * g2:64 * g2 + 64, :, 16 * g2:16 * g2 + 16],
                              src[i].rearrange("(P g) n q -> g n P q", g=2)[g2], reads=["z_" + tl.name],
                              writes=[("ld_" + tl.name, g2)])
                for (tl, src) in ((CNr, c_re), (CNi, c_im)):
                    for p4 in range(4):
                        for g2 in range(2):
                            P.dma("act", tl[32 * p4 + 16 * g2:32 * p4 + 16 * g2 + 16, :, 64 * g2:64 * g2 + 64],
                                  src[i].rearrange("(f a g) p n -> a g p f n", a=4, g=2)[p4, g2],
                                  reads=["z_" + tl.name], writes=[("ld_" + tl.name, p4, g2)])
                for (src, dst) in ((CNr, CTr), (CNi, CTi)):
                    b = psget()
                    for ft in range(4):
                        P.op("pe", lambda e, ft=ft, b=b, src=src: e.transpose(
                            PS[:, b, ft * 128:(ft + 1) * 128], src[:, ft, :], ident[:]),
                            reads=[("ld_" + src.name, a_, b_) for a_ in range(4) for b_ in range(2)] + ["ident"], writes=pk(b))
                    P.op("act", lambda e, b=b, dst=dst: e.activation(
                        out=dst[:].rearrange("p a b -> p (a b)"), in_=PS[:, b, :], func=AF.Copy),
                        reads=pk(b), writes=[dst.name])
                dv(lambda e: e.activation(out=DT[:], in_=LDT[:], func=AF.Exp), [("LDT", 0), ("LDT", 1)], ["DT"], eng="act")
                tt(Z[:], LR[:], DT[:], ALU.mult, ["LR", "DT"], ["Z"])
                ts(MAG[:], Z[:], 1.0 / 120.0, ALU.mult, ["Z"], ["MAG"], 1.0 / 24.0, ALU.add)
                for c in (1.0 / 6.0, 0.5, 1.0, 1.0):
                    tt(MAG[:], MAG[:], Z[:], ALU.mult, ["MAG", "Z"], ["MAG"])
                    ts(MAG[:], MAG[:], float(c), ALU.add, ["MAG"], ["MAG"])
                tt(ANG[:], LI[:], DT[:], ALU.mult, ["LI", "DT"], ["ANG"])
                C1 = 6.28125
                C2 = 2.0 * math.pi - C1
                MAGIC = 12582912.0
                for (shift, dst) in ((0.0, SN), (0.5 * math.pi, CS)):
                    ts(ta[:], ANG[:], 1.0 / (2 * math.pi), ALU.mult, ["ANG"], ["ta"], shift / (2 * math.pi), ALU.add)
                    ts(tb[:], ta[:], MAGIC, ALU.add, ["ta"], ["tb"])
                    ts(tb[:], tb[:], -MAGIC, ALU.add, ["tb"], ["tb"])
                    dv(lambda e: e.scalar_tensor_tensor(out=ta[:], in0=tb[:], scalar=-C1, in1=ANG[:],
                                                        op0=ALU.mult, op1=ALU.add), ["tb", "ANG"], ["ta"])
                    dv(lambda e: e.scalar_tensor_tensor(out=ta[:], in0=tb[:], scalar=-C2, in1=ta[:],
                                                        op0=ALU.mult, op1=ALU.add), ["tb", "ta"], ["ta"])
                    ts(ta[:], ta[:], float(shift), ALU.add, ["ta"], ["ta"], math.pi, ALU.min)
                    ts(ta[:], ta[:], -math.pi, ALU.max, ["ta"], ["ta"])
                    dv(lambda e, dst=dst: e.activation(out=dst[:], in_=ta[:], func=AF.Sin), ["ta"], [dst.name],
                       eng="act")
                tt(AR[1][:], MAG[:], CS[:], ALU.mult, ["MAG", CS.name], ["AR1"])
                tt(AI[1][:], MAG[:], SN[:], ALU.mult, ["MAG", SN.name], ["AI1"])
                dv(lambda e: e.memset(AR[0][:], 1.0), [], ["AR0"])
                dv(lambda e: e.memset(AI[0][:], 0.0), [], ["AI0"])

                def cmul(orr, oi, ar, ai, br, bi, rk, wk, t1=None, t2=None, k1="W1", k2="W2", eng="dve"):
                    tt(t1, ar, br, ALU.mult, rk, [k1], eng)
                    tt(t2, ai, bi, ALU.mult, rk, [k2], eng)
                    tt(orr, t1, t2, ALU.subtract, [k1, k2], [wk + "r"], eng)
                    tt(t1, ar, bi, ALU.mult, rk + [wk + "r"], [k1], eng)
                    tt(t2, ai, br, ALU.mult, rk + [wk + "r"], [k2], eng)
                    tt(oi, t1, t2, ALU.add, [k1, k2], [wk + "i"], eng)

                cmul(AR[2][:], AI[2][:], AR[1][:], AI[1][:], AR[1][:], AI[1][:], ["AR1", "AI1"], "A2", tc_[:], td[:], "tc", "td")
                cmul(AR[3][:], AI[3][:], AR[2][:], AI[2][:], AR[1][:], AI[1][:], ["AR1", "AI1", "A2r", "A2i"], "A3",
                     tc_[:], td[:], "tc", "td")
                cmul(AR[4][:], AI[4][:], AR[2][:], AI[2][:], AR[2][:], AI[2][:], ["A2r", "A2i"], "A4", tc_[:], td[:], "tc", "td")
                akeys = {0: ["AR0", "AI0"], 1: ["AR1", "AI1"], 2: ["A2r", "A2i"], 3: ["A3r", "A3i"], 4: ["A4r", "A4i"]}
                dv(lambda e: e.tensor_copy(out=A4[:, :, 0], in_=AR[4][:]), akeys[4], ["A4"])
                dv(lambda e: e.tensor_copy(out=A4[:, :, 1], in_=AI[4][:]), akeys[4] + ["A4"], ["A4"])
                tt(ta[:], MAG[:], MAG[:], ALU.mult, ["MAG"], ["ta"])
                tt(R4[:], ta[:], ta[:], ALU.mult, ["ta"], ["R4"])
                ts(ta[:], AR[1][:], -1.0, ALU.add, ["AR1"], ["ta"])
                tt(tb[:], LR[:], LR[:], ALU.mult, ["LR"], ["tb"])
                tt(tc_[:], LI[:], LI[:], ALU.mult, ["LI", "A4i"], ["tc"])
                tt(tb[:], tb[:], tc_[:], ALU.add, ["tb", "tc"], ["tb"])
                dv(lambda e: e.reciprocal(out=tb[:], in_=tb[:]), ["tb"], ["tb"])
                tt(tc_[:], ta[:], LR[:], ALU.mult, ["ta", "LR"], ["tc"])
                tt(td[:], AI[1][:], LI[:], ALU.mult, ["AI1", "LI", "A4i"], ["td"])
                tt(tc_[:], tc_[:], td[:], ALU.add, ["tc", "td"], ["tc"])
                tt(FR[:], tc_[:], tb[:], ALU.mult, ["tc", "tb"], ["FR"])
                tt(tc_[:], AI[1][:], LR[:], ALU.mult, ["AI1", "LR", "FR"], ["tc"])
                tt(td[:], ta[:], LI[:], ALU.mult, ["ta", "LI", "FR"], ["td"])
                tt(tc_[:], tc_[:], td[:], ALU.subtract, ["tc", "td"], ["tc"])
                tt(FI[:], tc_[:], tb[:], ALU.mult, ["tc", "tb"], ["FI"])
                dv(lambda e: e.reciprocal(out=ta[:], in_=R4[:]), ["R4", "FI"], ["ta"])
                tt(TC[:, :, 0], AR[4][:], ta[:], ALU.mult, akeys[4] + ["ta"], ["TC"])
                tt(TS[:, :, 0], AI[4][:], ta[:], ALU.mult, akeys[4] + ["ta"], ["TS"])
                m = 1
                NCH = TM // 4
                W5 = sb(st, "s5_W5", [128, 16, 32]); W6 = sb(st, "s5_W6", [128, 16, 32])
                while m < NCH:
                    ur = TC[:, :, m - 1:m].broadcast_to([128, 16, m])
                    ui = TS[:, :, m - 1:m].broadcast_to([128, 16, m])
                    w1 = W5[:, :, 0:m]; w2 = W6[:, :, 0:m]
                    tt(w1, TC[:, :, 0:m], ur, ALU.mult, ["TC", "TS"], ["W5"], "pool")
                    tt(w2, TS[:, :, 0:m], ui, ALU.mult, ["TC", "TS"], ["W6"], "pool")
                    tt(TC[:, :, m:2 * m], w1, w2, ALU.subtract, ["W5", "W6"], ["TC"], "pool")
                    tt(w1, TC[:, :, 0:m], ui, ALU.mult, ["TC", "TS"], ["W5"], "pool")
                    tt(w2, TS[:, :, 0:m], ur, ALU.mult, ["TC", "TS"], ["W6"], "pool")
                    tt(TS[:, :, m:2 * m], w1, w2, ALU.add, ["W5", "W6"], ["TS"], "pool")
                    m *= 2

                def bc(a):
                    return a.unsqueeze(2).broadcast_to([128, 16, 32])

                cmul(BBr[:], BBi[:], BR[:], BI[:], bc(FR[:]), bc(FI[:]), [("ld_" + BR.name, 0), ("ld_" + BR.name, 1), ("ld_" + BI.name, 0), ("ld_" + BI.name, 1), "FR", "FI"],
                     "BB", W1[:], W2[:])
                for k in range(4):
                    if k == 0:
                        srcs = (BBr, BBi)
                        skeys = ["BBr", "BBi"]
                    else:
                        cmul(Xr_[:], Xi_[:], BBr[:], BBi[:], bc(AR[k][:]), bc(AI[k][:]), ["BBr", "BBi"] + akeys[k], "Xq",
                             W3[:], W4[:], "W3", "W4", eng="pool")
                        srcs = (Xr_, Xi_)
                        skeys = ["Xqr", "Xqi"]
                    s = 3 - k
                    for ri in range(2):
                        b = psget()
                        for ft in range(4):
                            P.op("pe", lambda e, ft=ft, b=b, src=srcs[ri]: e.transpose(
                                PS[:, b, ft * 128:(ft + 1) * 128],
                                src[:, 4 * ft:4 * ft + 4, :].rearrange("p a b -> p (a b)"), ident[:]),
                                reads=skeys + ["ident"], writes=pk(b))
                        P.op("act", lambda e, b=b, ri=ri, s=s: e.activation(
                            out=XB[:, :, ri, s, :], in_=PS[:, b, :].rearrange("p (f n) -> p f n", f=4), func=AF.Copy),
                            reads=pk(b), writes=["XB"])
                bBD = psget()
                for k in range(5):
                    if k == 0:
                        dv(lambda e: e.tensor_copy(out=Yr[:], in_=CTr[:]), ["s5_CTr", "XB"], ["Yr"])
                        ts(Yi[:], CTi[:], -1.0, ALU.mult, ["s5_CTi", "XB"], ["Yi"])
                    else:
                        cmul(Yr[:], Yi[:], CTr[:], CTi[:], bc(AR[k][:]), bc(AI[k][:]),
                             ["s5_CTr", "s5_CTi", "BD%d" % (k - 1), "YC"] + akeys[k], "Y", W1[:], W2[:])
                        ts(Yi[:], Yi[:], -1.0, ALU.mult, ["Yi"], ["Yi"])
                        P.op("act", lambda e, k=k: e.activation(out=YC[:, :, 0, k - 1, :], in_=Yr[:], func=AF.Copy),
                             reads=["Yr"], writes=["YC"])
                        P.op("act", lambda e, k=k: e.activation(out=YC[:, :, 1, k - 1, :], in_=Yi[:], func=AF.Copy),
                             reads=["Yi"], writes=["YC"])
                    if k < 4:
                        for ft in range(4):
                            o_ = PS[:, bBD, ft * 128:(ft + 1) * 128]
                            P.op("pe", lambda e, ft=ft, o_=o_: e.matmul(
                                o_, lhsT=BBr[:, 4 * ft:4 * ft + 4, :].rearrange("p a b -> p (a b)"),
                                rhs=Yr[:, 4 * ft:4 * ft + 4, :].rearrange("p a b -> p (a b)"), start=True, stop=False),
                                reads=["BBr", "Yr"], writes=pk(bBD))
                            P.op("pe", lambda e, ft=ft, o_=o_: e.matmul(
                                o_, lhsT=BBi[:, 4 * ft:4 * ft + 4, :].rearrange("p a b -> p (a b)"),
                                rhs=Yi[:, 4 * ft:4 * ft + 4, :].rearrange("p a b -> p (a b)"), start=False, stop=True),
                                reads=["BBi", "Yi"], writes=pk(bBD))
                        for ft in range(4):
                            if k == 0:
                                dv(lambda e, ft=ft: e.tensor_tensor(out=W1[:, 0:4, :].rearrange("p a b -> p (a b)"),
                                                                    in0=PS[:, bBD, ft * 128:(ft + 1) * 128],
                                                                    in1=bmask[:], op=ALU.mult),
                                   pk(bBD) + ["bmask"], ["W1"])
                                dv(lambda e, ft=ft: e.scalar_tensor_tensor(
                                    out=BD[:, ft, 0, :], in0=ident[:], scalar=dcol[:, i, ft:ft + 1],
                                    in1=W1[:, 0:4, :].rearrange("p a b -> p (a b)"), op0=ALU.mult, op1=ALU.add),
                                    ["W1", "ident", "dcol"], ["BD0"])
                            else:
                                dv(lambda e, ft=ft, k=k: e.tensor_tensor(out=BD[:, ft, k, :],
                                                                         in0=PS[:, bBD, ft * 128:(ft + 1) * 128],
                                                                         in1=bmask[:], op=ALU.mult),
                                   pk(bBD) + ["bmask"], ["BD%d" % k])

        def even_mixer(layer):
            i = layer // 2
            P.cost.update({"pe": 115.0, "dve": 430.0, "act": 450.0})
            with ExitStack() as st:
                NCH = TM // 4
                Win = WinP
                Wout = WoutP
                Wglu = WsmP[:, :].rearrange("p (k n) -> p k n", k=4)
                XB = sb(st, "XB", [128, 4, 2, 4, 128], BF16)
                YC = sb(st, "YC", [128, 16, 2, 4, 32], BF16)
                BD = sb(st, "BD", [128, 4, 4, 128], BF16)
                TC = sb(st, "TC", [128, 16, NCH]); TS = sb(st, "TS", [128, 16, NCH])
                R4 = sb(st, "R4", [128, 16]); A4 = sb(st, "A4", [128, 16, 2])
                wT = sb(st, "wT", [128, 8, 128], BF16)
                bbc = sb(st, "bbc", [128, 4, 128])
                gsg = sb(st, "gsg", [128, 512])
                wsc = sb(st, "wsc", [128, 4, 16])
                P.dma("sp", gsg[:], sgu_norm[i:i + 1, :].broadcast_to([128, 512]), writes=["gsg"])
                for h in range(8):
                    P.dma("sp", bbc[64 * (h % 2):64 * (h % 2) + 64, h // 2, :],
                          sgu_b[i, h:h + 1, :].broadcast_to([64, 128]), writes=[("bbc", h)])
                for h in range(8):
                    P.dma("sp", wsc[64 * (h % 2):64 * (h % 2) + 64, h // 2, :].rearrange("p (a b) -> p a b", a=4),
                          sgu_w[i, h:h + 1, 0:4, 0:4].broadcast_to([64, 4, 4]), writes=[("wsc", h)])
                P.op("dve", lambda e: e.memset(s5car[:], 0.0), writes=[("s5car", q_) for q_ in range(4)])
                with ExitStack() as st2:
                    trilT = sb(st2, "trilT", [128, 128])
                    bmask = sb(st2, "bmask", [128, 128])
                    j32 = sb(st2, "bm_j32", [128, 128])
                    S4 = sb(st2, "bm_S4", [128, 128])
                    P.op("dve", lambda e: e.tensor_scalar(out=trilT[:], in0=iot[:], scalar1=iop[:, 0:1], scalar2=None,
                                                          op0=ALU.is_ge), reads=["iot", "iop"], writes=["trilT"])
                    P.op("pool", lambda e: e.iota(j32[:], pattern=[[1, 4], [0, 32]], base=0, channel_multiplier=0,
                                                  allow_small_or_imprecise_dtypes=True), writes=["j32"])
                    P.op("dve", lambda e: e.tensor_scalar(out=S4[:], in0=j32[:], scalar1=iop[:, 0:1], scalar2=None,
                                                          op0=ALU.is_equal), reads=["j32", "iop"], writes=["S4"])
                    P.op("pe", lambda e: e.matmul(PS[:, 6, 0:128], lhsT=S4[0:4, :], rhs=S4[0:4, :], start=True, stop=True),
                         reads=["S4"], writes=[("ps", 6)])
                    P.op("dve", lambda e: e.tensor_copy(out=bmask[:], in_=PS[:, 6, 0:128]), reads=[("ps", 6)], writes=["bmask"])
                    wld = [sb(st2, "wld%d" % q, [128, 128]) for q in range(2)]
                    for h in range(8):
                        wl = wld[h % 2]
                        P.dma("sp", wl[:], sgu_w[i, h], writes=["wld%d" % (h % 2)])
                        b = psget()
                        P.op("pe", lambda e, b=b, wl=wl: e.transpose(PS[:, b, 0:128], wl[:], ident[:]),
                             reads=["wld%d" % (h % 2), "ident"], writes=pk(b))
                        P.op("dve", lambda e, b=b, h=h: e.tensor_tensor(out=wT[:, h, :], in0=PS[:, b, 0:128],
                                                                        in1=trilT[:], op=ALU.mult),
                             reads=pk(b) + ["trilT"], writes=["wT"])
                    xt_ = [sb(st2, "xt%d" % q_, [128, D]) for q_ in range(2)] if layer == 0 else None
                    s5_setup(st2, i, XB, YC, BD, TC, TS, R4, A4, bmask)
                    if layer == 0:
                        load_x(xt_)
                    P.flush()
                glubh = sb(st, "glubh", [128, 4])
                P.op("dve", lambda e: e.tensor_scalar(out=glubh[:], in0=glub[:, i, :], scalar1=0.5, scalar2=None, op0=ALU.mult),
                     writes=["glubh"])
                if layer == 0:
                    load_mixer_weights(0, skip_in=True)
                P.op("dve", lambda e: e.tensor_scalar(out=WoutP[:, 0:4, :], in0=WoutP[:, 0:4, :], scalar1=0.5, scalar2=None,
                                                      op0=ALU.mult), reads=[("Wout", 0), ("Wout", 1)], writes=["Wout"])
                xnt = sb(st, "xnt", [128, 8, TM], BF16)
                uaL = [sb(st, "ua%d" % q, [128, 4, TM], BF16) for q in range(2)]
                ubL = [sb(st, "ub%d" % q, [128, 4, TM], BF16) for q in range(2)]
                vn = sb(st, "vn", [128, 512])
                vnbL = [sb(st, "vnb%d" % q, [128, 2, 512], BF16) for q in range(2)]
                vjunk = sb(st, "vjunk", [128, 512], BF16)
                vss = sb(st, "vss", [128, 2])
                ymix = sb(st, "ymix", [128, 8, TM], BF16)
                tA = sb(st, "tA", [128, 4, NCH]); tB = sb(st, "tB", [128, 4, NCH])
                Gin = sb(st, "Gin", [128, 4, 2, NCH])
                wtail = [WinP[:, k_, 1536:2048].bitcast(F32).rearrange("p (a c) -> p a c", a=4) for k_ in range(8)]
                tC, tD, tE, tF, tG, tH = wtail[0:6]
                GsL = [sb(st, "Gs0", [128, 4, 2, NCH]),
                       WinP[:, 6:8, 1536:2048].bitcast(F32).rearrange("p k (a c) -> p a k c", a=4)]
                Hf = sb(st, "Hf", [128, 4, 2, NCH + 1])
                Hb = sb(st, "Hb", [128, 4, 2, NCH], BF16)
                sqy = sb(st, "sqy", [128, TM])
                zf = sb(st, "zf", [128, 4, TM])
                zb = sb(st, "zb", [128, 4, TM], BF16)
                sg2 = sb(st, "sg2", [128, TM])
                stmp = sb(st, "stmp", [128, TM])
                vT = sb(st, "vT", [128, 4, NS])
                sacc = sb(st, "sacc", [128, 16, 4])
                h0s = sb(st, "h0s", [16, 1024])
                h0T = sb(st, "h0T", [128, 16, 2, 16])
                hend = sb(st, "hend", [128, 16, 2, 16])
                hoP = sb(st, "hoP", [16, 2, 128])
                def front(ti, t0, n, is_s):
                    nch = n // 4
                    par = ti % 2
                    ua = uaL[par]; ub = ubL[par]; vnb = vnbL[par]
                    hk = ("hres", t0)
                    rmsnorm_tile(st, "m", t0, n, gmix[:, layer, :], xnt, "xnt") if ti == 0 else \
                        rmsnorm_tile_again("m", t0, n, gmix[:, layer, :], xnt, "xnt")
                    for ft in range(4):
                        b = psget()
                        for k in range(8):
                            P.op("pe", lambda e, k=k, ft=ft, b=b: e.matmul(
                                PS[:, b, 0:n], lhsT=Win[:, k, ft * 128:(ft + 1) * 128], rhs=xnt[:, k, 0:n],
                                start=(k == 0), stop=(k == 7)), reads=["Win", "xnt"], writes=pk(b))
                        P.op("act", lambda e, ft=ft, b=b: e.activation(out=ua[:, ft, 0:n], in_=PS[:, b, 0:n],
                                                                       func=AF.Copy),
                             reads=pk(b), writes=[("ua", par, ft)])
                    for ft in range(4):
                        b = psget()
                        for k in range(8):
                            P.op("pe", lambda e, k=k, ft=ft, b=b: e.matmul(
                                PS[:, b, 0:n], lhsT=Win[:, k, 512 + ft * 128:512 + (ft + 1) * 128], rhs=xnt[:, k, 0:n],
                                start=(k == 0), stop=(k == 7)), reads=["Win", "xnt"], writes=pk(b))
                        P.op("act", lambda e, ft=ft, b=b: e.activation(out=ub[:, ft, 0:n], in_=PS[:, b, 0:n],
                                                                       func=AF.Copy),
                             reads=pk(b), writes=[("ub", par, ft)])
                    nsub = (n + 127) // 128
                    for sj in range(nsub):
                        m = min(128, n - sj * 128)
                        b = psget()
                        for k in range(8):
                            P.op("pe", lambda e, k=k, b=b, sj=sj, m=m: e.matmul(
                                PS[0:m, b, :], lhsT=xnt[:, k, sj * 128:sj * 128 + m], rhs=Win[:, k, 1024:1536],
                                start=(k == 0), stop=(k == 7)), reads=["Win", "xnt"], writes=pk(b))
                        P.op("act", lambda e, b=b, sj=sj, m=m: e.activation(
                            out=vjunk[0:m, :], in_=PS[0:m, b, :], func=AF.Square, accum_out=vss[0:m, sj:sj + 1]),
                            reads=pk(b), writes=["vjunk", ("vss", sj)])
                        P.op("act", lambda e, sj=sj, m=m: e.activation(
                            out=vss[0:m, sj:sj + 1], in_=vss[0:m, sj:sj + 1], func=AF.Ln, bias=epsc[0:m, 0:1],
                            scale=1.0 / 512.0), reads=[("vss", sj), "epsc"], writes=[("vss", sj)], cost=250.0)
                        P.op("act", lambda e, sj=sj, m=m: e.activation(
                            out=vss[0:m, sj:sj + 1], in_=vss[0:m, sj:sj + 1], func=AF.Exp, scale=-0.5),
                            reads=[("vss", sj)], writes=[("vss", sj)], cost=250.0)
                        P.op("dve", lambda e, b=b, sj=sj, m=m: e.scalar_tensor_tensor(
                            out=vnb[0:m, sj, :], in0=PS[0:m, b, :], scalar=vss[0:m, sj:sj + 1], in1=gsg[0:m, :],
                            op0=ALU.mult, op1=ALU.mult), reads=pk(b) + [("vss", sj), "gsg"], writes=[("vnb", par, sj)])
                        if is_s:
                            P.op("dve", lambda e, b=b, sj=sj, m=m: e.scalar_tensor_tensor(
                                out=vn[0:m, :], in0=PS[0:m, b, :], scalar=vss[0:m, sj:sj + 1], in1=gsg[0:m, :],
                                op0=ALU.mult, op1=ALU.mult), reads=pk(b) + [("vss", sj), "gsg"], writes=[("vn", 0)])
                    if is_s:
                        P.dma("sp", o_s_v[i], vn[0:NS, :], reads=[("vn", 0)])
                        for ri in range(2):
                            b = psget()
                            for hf in range(2):
                                P.dma("sp", h0s[:, :], (st_re if ri == 0 else st_im)[i][:, hf * 1024:(hf + 1) * 1024],
                                      writes=["h0s"])
                                for q in range(8):
                                    Pp = hf * 8 + q
                                    P.op("pe", lambda e, b=b, Pp=Pp, q=q: e.transpose(
                                        PS[:, b, Pp * 16:(Pp + 1) * 16], h0s[:, q * 128:(q + 1) * 128],
                                        ident[0:16, 0:16]), reads=["h0s", "ident"], writes=pk(b))
                            P.op("dve", lambda e, b=b, ri=ri: e.tensor_copy(
                                out=h0T[:, :, ri, :], in_=PS[:, b, 0:256].rearrange("p (a b) -> p a b", a=16)),
                                reads=pk(b), writes=["h0T"])
                def back(ti, t0, n, is_s):
                    nch = n // 4
                    par = ti % 2
                    ua = uaL[par]; ub = ubL[par]; vnb = vnbL[par]
                    for ft in range(4):
                        if not is_s:
                            P.op("pool", lambda e, ft=ft: e.tensor_copy(out=Hf[:, :, :, 0], in_=s5car[:, 4 * ft:4 * ft + 4, :]),
                                 reads=[("s5car", ft)], writes=["Hf0"])
                        b4 = psget(4)
                        for p4 in range(4):
                            for ri in range(2):
                                for s in range(4):
                                    P.op("pe", lambda e, p4=p4, ri=ri, s=s, ft=ft, b4=b4: e.matmul(
                                        PS[:, b4 + p4, ri * NCH:ri * NCH + nch],
                                        lhsT=XB[32 * p4:32 * p4 + 32, ft, ri, s, :],
                                        rhs=ua[32 * p4:32 * p4 + 32, ft, s:n:4],
                                        start=(s == 0), stop=(s == 3), tile_position=(32 * p4, 0)),
                                        reads=["XB", ("ua", par, ft)], writes=pk(b4, 4), cost=40.0)
                        Xr = PS[:, b4:b4 + 4, 0:nch]
                        Xi = PS[:, b4:b4 + 4, NCH:NCH + nch]
                        if not is_s:
                            Cc = TC[:, 4 * ft:4 * ft + 4, 0:nch]
                            Ss = TS[:, 4 * ft:4 * ft + 4, 0:nch]
                            tAa = tA[:, :, 0:nch]; tBb = tB[:, :, 0:nch]
                            GinR = Gin[:, :, 0, 0:nch]; GinI = Gin[:, :, 1, 0:nch]
                            x4 = pk(b4, 4)
                            gq = ft % 2
                            Gsq = GsL[gq]

                            def tt(out, a, bb, op, r, w, eng="dve"):
                                P.op(eng, lambda e: e.tensor_tensor(out=out, in0=a, in1=bb, op=op), reads=r, writes=w)
                            tCc = tC[:, :, 0:nch]; tDd = tD[:, :, 0:nch]
                            tt(tAa, Xr, Cc, ALU.mult, x4 + ["TC"], ["tA"])
                            tt(tBb, Xi, Ss, ALU.mult, x4 + ["TS"], ["tB"])
                            tt(tCc, Xi, Cc, ALU.mult, x4 + ["TC"], ["tC"])
                            tt(tDd, Xr, Ss, ALU.mult, x4 + ["TS"], ["tD"])
                            tt(GinR, tAa, tBb, ALU.add, ["tA", "tB"], ["GinR"])
                            tt(GinI, tCc, tDd, ALU.subtract, ["tC", "tD"], ["GinI"])
                            for p4 in range(4):
                                Pp = 4 * ft + p4
                                for ri in range(2):
                                    P.op("dve", lambda e, p4=p4, ri=ri, Pp=Pp, Gsq=Gsq: e.tensor_tensor_scan(
                                        out=Gsq[:, p4, ri, 0:nch], data0=R4[:, Pp:Pp + 1].broadcast_to([128, nch]),
                                        data1=Gin[:, p4, ri, 0:nch], initial=s5car[:, Pp, ri:ri + 1],
                                        op0=ALU.mult, op1=ALU.add),
                                        reads=["GinR" if ri == 0 else "GinI", "R4", ("s5car", ft)],
                                        writes=[("Gs", gq, p4, ri)], cost=350.0)
                            GR = Gsq[:, :, 0, 0:nch]; GI = Gsq[:, :, 1, 0:nch]
                            gk = [("Gs", gq, a_, b_) for a_ in range(4) for b_ in range(2)]
                            tEe = tE[:, :, 0:nch]; tFf = tF[:, :, 0:nch]; tGg = tG[:, :, 0:nch]; tHh = tH[:, :, 0:nch]
                            tt(tEe, GR, Cc, ALU.mult, gk + ["TC"], ["tE"], "pool")
                            tt(tFf, GI, Ss, ALU.mult, gk + ["TS"], ["tF"], "pool")
                            tt(tGg, GR, Ss, ALU.mult, gk + ["TS"], ["tG"], "pool")
                            tt(tHh, GI, Cc, ALU.mult, gk + ["TC"], ["tH"], "pool")
                            tt(Hf[:, :, 0, 1:nch + 1], tEe, tFf, ALU.subtract, ["tE", "tF", "Hf0"], ["HfR"], "pool")
                            tt(Hf[:, :, 1, 1:nch + 1], tGg, tHh, ALU.add, ["tG", "tH", "Hf0"], ["HfI"], "pool")
                            P.op("act", lambda e: e.activation(out=Hb[:, :, :, 0:nch], in_=Hf[:, :, :, 0:nch], func=AF.Copy),
                                 reads=["HfR", "HfI", "Hf0"], writes=["Hb"])
                            P.op("pool", lambda e, ft=ft: e.tensor_copy(out=s5car[:, 4 * ft:4 * ft + 4, :],
                                                                        in_=Hf[:, :, :, nch]),
                                 reads=["HfR", "HfI"] + gk, writes=[("s5car", ft)])
                        else:
                            h0r = h0T[:, 4 * ft:4 * ft + 4, 0, :]; h0i = h0T[:, 4 * ft:4 * ft + 4, 1, :]
                            a4r = A4[:, 4 * ft:4 * ft + 4, 0:1].broadcast_to([128, 4, 16])
                            a4i = A4[:, 4 * ft:4 * ft + 4, 1:2].broadcast_to([128, 4, 16])
                            tAa = tA[:, :, 0:16]; tBb = tB[:, :, 0:16]
                            x4 = pk(b4, 4)

                            def tt(out, a, bb, op, r, w):
                                P.op("dve", lambda e: e.tensor_tensor(out=out, in0=a, in1=bb, op=op), reads=r, writes=w)
                            tt(tAa, h0r, a4r, ALU.mult, ["h0T", "A4"], ["tA"])
                            tt(tBb, h0i, a4i, ALU.mult, ["h0T", "A4"], ["tB"])
                            tt(tAa, tAa, tBb, ALU.subtract, ["tA", "tB"], ["tA"])
                            tt(hend[:, 4 * ft:4 * ft + 4, 0, :], tAa, Xr, ALU.add, ["tA"] + x4, [("hend", ft, 0)])
                            tt(tAa, h0r, a4i, ALU.mult, ["h0T", "A4", ("hend", ft, 0)], ["tA"])
                            tt(tBb, h0i, a4r, ALU.mult, ["h0T", "A4", ("hend", ft, 0)], ["tB"])
                            tt(tAa, tAa, tBb, ALU.add, ["tA", "tB"], ["tA"])
                            tt(hend[:, 4 * ft:4 * ft + 4, 1, :], tAa, Xi, ALU.add, ["tA"] + x4, [("hend", ft, 1)])
                            P.op("act", lambda e, ft=ft: e.activation(out=Hb[:, :, :, 0:16],
                                                                      in_=h0T[:, 4 * ft:4 * ft + 4, :, :], func=AF.Copy),
                                 reads=["h0T"], writes=["Hb"])
                        by = psget()
                        for t in range(4):
                            o_ = PS[:, by, t * NCH:t * NCH + nch]
                            for tau in range(t + 1):
                                P.op("pe", lambda e, t=t, tau=tau, ft=ft, o_=o_: e.matmul(
                                    o_, lhsT=BD[:, ft, tau, :], rhs=ua[:, ft, (t - tau):n:4],
                                    start=(tau == 0), stop=False), reads=[("ua", par, ft)], writes=pk(by), cost=60.0)
                            for p4 in range(4):
                                for ri in range(2):
                                    last = (ri == 1)
                                    P.op("pe", lambda e, t=t, p4=p4, ri=ri, ft=ft, by=by, last=last: e.matmul(
                                        PS[32 * p4:32 * p4 + 32, by, t * NCH:t * NCH + nch],
                                        lhsT=YC[:, 4 * ft + p4, ri, t, :], rhs=Hb[:, p4, ri, 0:nch],
                                        start=False, stop=last, tile_position=(0, 32 * p4)),
                                        reads=["Hb"], writes=pk(by), cost=45.0)
                        yv = PS[:, by, 0:4 * NCH].rearrange("p (t c) -> p c t", t=4)[:, 0:nch, :]
                        sq3 = sqy[:, 0:n].rearrange("p (c t) -> p c t", t=4)
                        z3 = zf[:, ft, 0:n].rearrange("p (c t) -> p c t", t=4)
                        P.op("act", lambda e, yv=yv, z3=z3: e.activation(out=z3, in_=yv, func=AF.Gelu_apprx_tanh),
                             reads=pk(by), writes=[("zf", ft)])
                        P.op("act", lambda e, ft=ft: e.activation(out=zb[:, ft, 0:n], in_=zf[:, ft, 0:n], func=AF.Copy),
                             reads=[("zf", ft)], writes=[("zb", ft)])
                    for fo in range(4):
                        b = psget()
                        for fi in range(4):
                            P.op("pe", lambda e, fi=fi, fo=fo, b=b: e.matmul(
                                PS[:, b, 0:n], lhsT=Wglu[:, fi, fo * 128:(fo + 1) * 128], rhs=zb[:, fi, 0:n],
                                start=(fi == 0), stop=(fi == 3)), reads=["Wsm", ("Wsm", 0)] + [("zb", q) for q in range(4)],
                                writes=pk(b))
                        P.op("act", lambda e, fo=fo, b=b: e.activation(out=sg2[:, 0:n], in_=PS[:, b, 0:n], func=AF.Tanh,
                                                                       bias=glubh[:, fo:fo + 1], scale=0.5),
                             reads=pk(b) + ["glubh"], writes=["sg2"])
                        P.op("dve", lambda e, fo=fo: e.scalar_tensor_tensor(out=ymix[:, fo, 0:n], in0=sg2[:, 0:n], scalar=1.0,
                                                                            in1=zf[:, fo, 0:n], op0=ALU.add, op1=ALU.mult),
                             reads=["sg2", ("zf", fo)], writes=["ymix"])
                    if not is_s:
                        for hp in range(4):
                            b = psget()
                            for j in range(n // 128):
                                for h2 in range(2):
                                    h = 2 * hp + h2
                                    P.op("pe", lambda e, b=b, j=j, h2=h2, h=h: e.matmul(
                                        PS[64 * h2:64 * h2 + 64, b, j * 128:(j + 1) * 128],
                                        lhsT=vnb[:, j, h * 64:(h + 1) * 64], rhs=wT[:, h, :],
                                        start=True, stop=True, tile_position=(0, 64 * h2)),
                                        reads=["wT", ("vnb", par, j)], writes=pk(b))
                            P.op("dve", lambda e, b=b, hp=hp: e.tensor_tensor(
                                out=stmp[:, 0:n].rearrange("p (j i) -> p j i", i=128),
                                in0=PS[:, b, 0:n].rearrange("p (j i) -> p j i", i=128),
                                in1=bbc[:, hp, :].unsqueeze(1).broadcast_to([128, n // 128, 128]), op=ALU.add),
                                reads=pk(b) + ["bbc"], writes=["stmp"])
                            P.op("dve", lambda e, hp=hp: e.tensor_tensor(out=ymix[:, 4 + hp, 0:n], in0=stmp[:, 0:n],
                                                                         in1=ub[:, hp, 0:n], op=ALU.mult),
                                 reads=["stmp", ("ub", par, hp)], writes=["ymix"])
                    else:
                        b = psget()
                        for ft in range(4):
                            P.op("pe", lambda e, b=b, ft=ft: e.transpose(PS[:, b, ft * NS:(ft + 1) * NS],
                                                                         vn[0:NS, ft * 128:(ft + 1) * 128],
                                                                         ident[0:NS, 0:NS]),
                                 reads=[("vn", 0), "ident"], writes=pk(b))
                        P.op("dve", lambda e, b=b: e.tensor_copy(out=vT[:], in_=PS[:, b, 0:4 * NS].rearrange("p (f t) -> p f t", f=4)),
                             reads=pk(b), writes=["vT"])
                        for ft in range(4):
                            v3 = vT[:, ft, :].rearrange("p (b j) -> p b j", j=4)
                            for ii in range(4):
                                P.op("dve", lambda e, ft=ft, ii=ii, v3=v3: e.tensor_scalar(
                                    out=sacc[:, :, ii], in0=v3[:, :, 0], scalar1=wsc[:, ft, 4 * ii:4 * ii + 1],
                                    scalar2=bbc[:, ft, ii:ii + 1], op0=ALU.mult, op1=ALU.add),
                                    reads=["vT", "wsc", "bbc"], writes=["sacc"])
                                for jj in range(1, ii + 1):
                                    P.op("dve", lambda e, ft=ft, ii=ii, jj=jj, v3=v3: e.scalar_tensor_tensor(
                                        out=sacc[:, :, ii], in0=v3[:, :, jj], scalar=wsc[:, ft, 4 * ii + jj:4 * ii + jj + 1],
                                        in1=sacc[:, :, ii], op0=ALU.mult, op1=ALU.add),
                                        reads=["vT", "wsc", "sacc"], writes=["sacc"])
                            P.op("dve", lambda e, ft=ft: e.tensor_tensor(
                                out=ymix[:, 4 + ft, 0:NS], in0=sacc[:].rearrange("p b i -> p (b i)"),
                                in1=ub[:, ft, 0:NS], op=ALU.mult), reads=["sacc", ("ub", par, ft)], writes=["ymix"])
                    out_proj_tile(Wout, "Wout", ymix, "ymix", t0, n)
                    if is_s:
                        for ri in range(2):
                            for hf in range(2):
                                for h2 in range(2):
                                    half = hf * 2 + h2
                                    b = psget()
                                    for q in range(4):
                                        Pp = half * 4 + q
                                        P.op("pe", lambda e, b=b, q=q, Pp=Pp, ri=ri: e.transpose(
                                            PS[0:16, b, q * 128:(q + 1) * 128], hend[:, Pp, ri, :], ident[:]),
                                            reads=[("hend", Pp // 4, ri), "ident"], writes=pk(b))
                                    P.op("act", lambda e, b=b, h2=h2: e.activation(
                                        out=h0s[:, h2 * 512:(h2 + 1) * 512], in_=PS[0:16, b, :], func=AF.Copy),
                                        reads=pk(b), writes=["h0s"])
                                P.dma("sp", (o_s_re if ri == 0 else o_s_im)[i][:, hf * 1024:(hf + 1) * 1024], h0s[:, :],
                                      reads=["h0s"])
                    if (not is_s) and t0 + n == SEQ:
                        for ri in range(2):
                            b = psget()
                            P.op("pe", lambda e, b=b, ri=ri: e.transpose(PS[0:16, b, 0:128], s5car[:, :, ri], ident[:]),
                                 reads=[("s5car", q_) for q_ in range(4)] + ["ident"], writes=pk(b))
                            P.op("act", lambda e, b=b, ri=ri: e.activation(out=hoP[:, ri, :], in_=PS[0:16, b, 0:128], func=AF.Copy),
                                 reads=pk(b), writes=["hoP"])
                        P.dma("sp", o_p_re[i], hoP[:, 0, :], reads=["hoP"])
                        P.dma("sp", o_p_im[i], hoP[:, 1, :], reads=["hoP"])
                seq = list(enumerate(mtiles))
                for idx, (ti, (t0, n, is_s)) in enumerate(seq):
                    front(ti, t0, n, is_s)
                    if idx >= 1:
                        pti, (pt0, pn, ps_) = seq[idx - 1]
                        back(pti, pt0, pn, ps_)
                lti, (lt0, ln, ls_) = seq[-1]
                back(lti, lt0, ln, ls_)
                P.flush()

        _norm_scr = {}

        def rmsnorm_tile_again(tag, t0, n, gvec, xn_out, xn_key):
            _rms_ops(tag, t0, n, gvec, xn_out, xn_key, _norm_scr[tag])

        def _rms_ops(tag, t0, n, gvec, xn_out, xn_key, srs, xoff=0):
            sr, sr2 = srs
            hk = hkeys(t0, n)
            sqv = xn_out[:, :, xoff:xoff + n]
            P.op("act", lambda e: e.activation(out=sqv, in_=hres[:, :, t0:t0 + n], func=AF.Square),
                 reads=hk, writes=[xn_key])
            b = psget()
            for k in range(8):
                P.op("pe", lambda e, k=k, b=b: e.matmul(PS[:, b, 0:n], lhsT=onesb[:], rhs=xn_out[:, k, xoff:xoff + n],
                                                        start=(k == 0), stop=(k == 7)),
                     reads=[xn_key, "onesb"], writes=pk(b))
            P.op("act", lambda e, b=b: e.activation(out=sr[:, 0:n], in_=PS[:, b, 0:n], func=AF.Ln,
                                                    bias=epsc[:, 0:1], scale=1.0 / D),
                 reads=pk(b) + ["epsc"], writes=["sr_" + tag])
            P.op("act", lambda e: e.activation(out=sr[:, 0:n], in_=sr[:, 0:n], func=AF.Exp, scale=-0.5),
                 reads=["sr_" + tag], writes=["sr_" + tag])
            for k in range(8):
                P.op("dve", lambda e, k=k: e.scalar_tensor_tensor(
                    out=xn_out[:, k, xoff:xoff + n], in0=hres[:, k, t0:t0 + n], scalar=gvec[:, k:k + 1],
                    in1=sr2[:, 0:n], op0=ALU.mult, op1=ALU.mult),
                    reads=hk + ["sr_" + tag, "gmix", "gffn"], writes=[xn_key])

        def rmsnorm_tile(stk, tag, t0, n, gvec, xn_out, xn_key, xoff=0):
            nmax = TM if tag == "m" else TF
            sr = sb(stk, "sr_" + tag, [128, nmax])
            sr2 = sr
            _norm_scr[tag] = (sr, sr2)
            _rms_ops(tag, t0, n, gvec, xn_out, xn_key, (sr, sr2), xoff=xoff)

        def out_proj_tile(Wout, wkey, ymix, ykey, t0, n):
            hk = hkeys(t0, n)
            for fo in range(8):
                b = psget()
                for k in range(8):
                    P.op("pe", lambda e, k=k, fo=fo, b=b: e.matmul(
                        PS[:, b, 0:n], lhsT=Wout[:, k, fo * 128:(fo + 1) * 128], rhs=ymix[:, k, 0:n],
                        start=(k == 0), stop=(k == 7)), reads=[wkey, ykey], writes=pk(b))
                P.op("dve", lambda e, fo=fo, b=b: e.tensor_tensor(
                    out=hres[:, fo, t0:t0 + n], in0=hres[:, fo, t0:t0 + n], in1=PS[:, b, 0:n], op=ALU.add),
                    reads=pk(b) + hk, writes=hk)

        def odd_mixer(layer):
            i = layer // 2
            P.cost.update({"pe": 115.0, "dve": 430.0, "act": 450.0})
            with ExitStack() as st:
                Win = WinP
                Wout = WoutP
                Wp = WsmP[:, 0:512].rearrange("p (g d) -> p g d", g=4)
                xnt = sb(st, "xnto", [128, 8, TM], BF16)
                XCL = [sb(st, "XC%d" % q, [128, 4, 15 + TM]) for q in range(2)]
                PA = sb(st, "PA", [128, 15 + TM]); PB = sb(st, "PB", [128, 15 + TM])
                diff = sb(st, "diff", [128, 4, TM], BF16)
                xdL = [sb(st, "xd%d" % q, [128, 4, TM]) for q in range(2)]
                bgL = [sb(st, "bg%d" % q, [128, 4, TM]) for q in range(2)]
                ZL = [sb(st, "Z%d" % q, [128, 4, 2 + TM]) for q in range(2)]
                ca = sb(st, "ca", [128, TM])
                ymix = sb(st, "ymixo", [128, 8, TM], BF16)
                invn = sb(st, "invn", [128, 4, 15])
                XCs = sb(st, "XCs", [128, 4, 16, 19])
                PAs = sb(st, "PAs", [128, 16, 19]); PBs = sb(st, "PBs", [128, 16, 19])
                Zs = sb(st, "Zs", [128, 4, 16, 6])
                spl = [sb(st, "spl%d" % q, [128, 512]) for q in range(2)]
                scl = sb(st, "scl", [32, 512])
                otp = sb(st, "otp", [128, 512])
                otc = sb(st, "otc", [32, 512])
                opp = sb(st, "opp", [16, 512])
                opc = sb(st, "opc", [2, 512])
                xct = sb(st, "xct", [128, 128])
                zct = sb(st, "zct", [128, 32])
                P.op("pool", lambda e: e.iota(invn[:], pattern=[[0, 4], [1, 15]], base=1, channel_multiplier=0,
                                              allow_small_or_imprecise_dtypes=True), writes=["invn"])
                for gi in range(4):
                    P.op("dve", lambda e, gi=gi: e.tensor_scalar(out=invn[:, gi, :], in0=invn[:, gi, :],
                                                                 scalar1=float(2 ** (gi + 1)), scalar2=None, op0=ALU.min),
                         reads=["invn"], writes=["invn"])
                P.op("dve", lambda e: e.reciprocal(out=invn[:], in_=invn[:]), reads=["invn"], writes=["invn"])
                P.op("dve", lambda e: e.memset(xchalo[:], 0.0), writes=["xchalo"])
                P.op("dve", lambda e: e.memset(zhalo[:], 0.0), writes=["zhalo"])
                P.op("pool", lambda e: e.memset(PA[:], 0.0), writes=["P0"])
                P.op("pool", lambda e: e.memset(PB[:], 0.0), writes=["P1"])
                P.op("pool", lambda e: e.memset(PAs[:], 0.0), writes=["Ps0"])
                P.op("pool", lambda e: e.memset(PBs[:], 0.0), writes=["Ps1"])
                def front(ti, t0, n, is_s):
                    par = ti % 2
                    XC = XCL[par]; xd = xdL[par]; bg = bgL[par]; Z = ZL[par]
                    if ti == 0:
                        rmsnorm_tile(st, "m", t0, n, gmix[:, layer, :], xnt, "xnt")
                    else:
                        rmsnorm_tile_again("m", t0, n, gmix[:, layer, :], xnt, "xnt")
                    if not is_s:
                        P.op("dve", lambda e: e.tensor_copy(out=XC[:, :, 0:15], in_=xchalo[:]), reads=["xchalo"],
                             writes=[("XChalo", par)])
                        P.op("dve", lambda e: e.tensor_copy(out=Z[:, :, 0:2], in_=zhalo[:]), reads=["zhalo"],
                             writes=[("Zhalo", par)])
                    else:
                        P.dma("sp", spl[0][:], st_pool[i].rearrange("b r c -> (b r) c")[0:128, :], writes=["spl0"])
                        P.dma("sp", spl[1][0:112, :], st_pool[i].rearrange("b r c -> (b r) c")[128:240, :], writes=["spl1"])
                        P.dma("sp", scl[:], st_conv[i].rearrange("b r c -> (b r) c"), writes=["scl"])
                        for ft in range(4):
                            b = psget()
                            P.op("pe", lambda e, b=b, ft=ft: e.transpose(PS[:, b, 0:128], spl[0][:, ft * 128:(ft + 1) * 128], ident[:]),
                                 reads=["spl0", "ident"], writes=pk(b))
                            P.op("pe", lambda e, b=b, ft=ft: e.transpose(PS[:, b, 128:240], spl[1][0:112, ft * 128:(ft + 1) * 128],
                                                                         ident[0:112, 0:112]),
                                 reads=["spl1", "ident"], writes=pk(b))
                            P.op("pe", lambda e, b=b, ft=ft: e.transpose(PS[:, b, 256:288], scl[:, ft * 128:(ft + 1) * 128],
                                                                         ident[0:32, 0:32]),
                                 reads=["scl", "ident"], writes=pk(b))
                            P.op("dve", lambda e, b=b, ft=ft: e.tensor_copy(
                                out=XCs[:, ft, :, 0:15], in_=PS[:, b, 0:240].rearrange("p (b r) -> p b r", r=15)),
                                reads=pk(b), writes=[("XCs", ft)])
                            P.op("dve", lambda e, b=b, ft=ft: e.tensor_copy(
                                out=Zs[:, ft, :, 0:2], in_=PS[:, b, 256:288].rearrange("p (b r) -> p b r", r=2)),
                                reads=pk(b), writes=[("Zs", ft)])
                    for ft in range(4):
                        b = psget()
                        for k in range(8):
                            P.op("pe", lambda e, k=k, ft=ft, b=b: e.matmul(
                                PS[:, b, 0:n], lhsT=Win[:, k, ft * 128:(ft + 1) * 128], rhs=xnt[:, k, 0:n],
                                start=(k == 0), stop=(k == 7)), reads=["Win", "xnt"], writes=pk(b))
                        if not is_s:
                            P.op("act", lambda e, ft=ft, b=b: e.activation(out=XC[:, ft, 15:15 + n], in_=PS[:, b, 0:n], func=AF.Copy),
                                 reads=pk(b), writes=[("XC", par, ft)])
                        else:
                            P.op("act", lambda e, ft=ft, b=b: e.activation(
                                out=XCs[:, ft, :, 15:19], in_=PS[:, b, 0:NS].rearrange("p (b t) -> p b t", t=4), func=AF.Copy),
                                reads=pk(b) + [("XCs", ft)], writes=[("XCs", ft)])
                    for ft in range(4):
                        b = psget()
                        for k in range(8):
                            P.op("pe", lambda e, k=k, ft=ft, b=b: e.matmul(
                                PS[:, b, 0:n], lhsT=Win[:, k, 512 + ft * 128:512 + (ft + 1) * 128], rhs=xnt[:, k, 0:n],
                                start=(k == 0), stop=(k == 7)), reads=["Win", "xnt"], writes=pk(b))
                        P.op("act", lambda e, ft=ft, b=b: e.activation(out=xd[:, ft, 0:n], in_=PS[:, b, 0:n], func=AF.Copy),
                             reads=pk(b), writes=[("xd", par, ft)])
                    for ft in range(4):
                        b = psget()
                        for k in range(8):
                            P.op("pe", lambda e, k=k, ft=ft, b=b: e.matmul(
                                PS[:, b, 0:n], lhsT=Win[:, k, 1024 + ft * 128:1024 + (ft + 1) * 128], rhs=xnt[:, k, 0:n],
                                start=(k == 0), stop=(k == 7)), reads=["Win", "xnt"], writes=pk(b))
                        P.op("act", lambda e, ft=ft, b=b: e.activation(out=bg[:, ft, 0:n], in_=PS[:, b, 0:n], func=AF.Copy),
                             reads=pk(b), writes=[("bg", par, ft)])
                    for ft in range(4):
                        b = psget()
                        for k in range(8):
                            P.op("pe", lambda e, k=k, ft=ft, b=b: e.matmul(
                                PS[:, b, 0:n], lhsT=Win[:, k, 1536 + ft * 128:1536 + (ft + 1) * 128], rhs=xnt[:, k, 0:n],
                                start=(k == 0), stop=(k == 7)), reads=["Win", "xnt"], writes=pk(b))
                        if not is_s:
                            P.op("dve", lambda e, ft=ft, b=b: e.tensor_tensor(out=Z[:, ft, 2:2 + n], in0=PS[:, b, 0:n],
                                                                              in1=xd[:, ft, 0:n], op=ALU.mult),
                                 reads=pk(b) + [("xd", par, ft)], writes=[("Z", par, ft)])
                        else:
                            P.op("dve", lambda e, ft=ft, b=b: e.tensor_tensor(
                                out=Zs[:, ft, :, 2:6], in0=PS[:, b, 0:NS].rearrange("p (b t) -> p b t", t=4),
                                in1=xd[:, ft, 0:NS].rearrange("p (b t) -> p b t", t=4), op=ALU.mult),
                                reads=pk(b) + [("xd", par, ft), ("Zs", ft)], writes=[("Zs", ft)])
                    if not is_s:
                        P.op("dve", lambda e: e.tensor_copy(out=xchalo[:], in_=XC[:, :, n:n + 15]),
                             reads=[("XC", par, q) for q in range(4)] + [(("XChalo", par), par)], writes=["xchalo"])
                        P.op("dve", lambda e: e.tensor_copy(out=zhalo[:], in_=Z[:, :, n:n + 2]),
                             reads=[("Z", par, q) for q in range(4)] + [(("Zhalo", par), par)], writes=["zhalo"])
                def back(ti, t0, n, is_s):
                    par = ti % 2
                    XC = XCL[par]; xd = xdL[par]; bg = bgL[par]; Z = ZL[par]
                    for gi in range(4):
                        w = 2 ** (gi + 1)
                        if not is_s:
                            L = 15 + n
                            src = XC[:, gi, 0:L]
                            bufs = [PA, PB]
                            cur = src
                            ckey = [("XC", par, gi), ("XChalo", par)]
                            d = 1
                            q = 0
                            while d < w:
                                dst = bufs[q % 2]
                                dk = "P%d" % (q % 2)
                                P.op("pool", lambda e, cur=cur, dst=dst, d=d, L=L: e.tensor_tensor(
                                    out=dst[:, d:L], in0=cur[:, d:L], in1=cur[:, 0:L - d], op=ALU.add),
                                    reads=ckey, writes=[dk], cost=800.0)
                                cur = dst[:, 0:L]
                                ckey = [dk]
                                d *= 2
                                q += 1
                            P.op("dve", lambda e, cur=cur, gi=gi, w=w: e.scalar_tensor_tensor(
                                out=diff[:, gi, 0:n], in0=cur[:, 15:15 + n], scalar=1.0 / w, in1=XC[:, gi, 15:15 + n],
                                op0=ALU.mult, op1=ALU.subtract), reads=ckey + [("XC", par, gi)], writes=[("diff", gi)])
                            if t0 == 0:
                                P.op("dve", lambda e, cur=cur, gi=gi: e.tensor_tensor(
                                    out=ca[:, 0:15], in0=cur[:, 15:30], in1=invn[:, gi, :], op=ALU.mult),
                                    reads=ckey + ["invn"], writes=["ca"])
                                P.op("dve", lambda e, gi=gi: e.tensor_tensor(
                                    out=diff[:, gi, 0:15], in0=ca[:, 0:15], in1=XC[:, gi, 15:30], op=ALU.subtract),
                                    reads=["ca", ("XC", par, gi), ("diff", gi)], writes=[("diff", gi)])
                        else:
                            L = 19
                            cur = XCs[:, gi, :, :]
                            ckey = [("XCs", gi)]
                            bufs = [PAs, PBs]
                            d = 1
                            q = 0
                            while d < w:
                                dst = bufs[q % 2]
                                dk = "Ps%d" % (q % 2)
                                P.op("pool", lambda e, cur=cur, dst=dst, d=d: e.tensor_tensor(
                                    out=dst[:, :, d:19], in0=cur[:, :, d:19], in1=cur[:, :, 0:19 - d], op=ALU.add),
                                    reads=ckey, writes=[dk], cost=800.0)
                                cur = dst[:, :, :]
                                ckey = [dk]
                                d *= 2
                                q += 1
                            P.op("dve", lambda e, cur=cur, gi=gi, w=w: e.scalar_tensor_tensor(
                                out=diff[:, gi, 0:NS].rearrange("p (b t) -> p b t", t=4), in0=cur[:, :, 15:19],
                                scalar=1.0 / w, in1=XCs[:, gi, :, 15:19], op0=ALU.mult, op1=ALU.subtract),
                                reads=ckey + [("XCs", gi)], writes=[("diff", gi)])
                        b = psget()
                        P.op("pe", lambda e, gi=gi, b=b: e.matmul(PS[:, b, 0:n], lhsT=Wp[:, gi, :], rhs=diff[:, gi, 0:n],
                                                                  start=True, stop=True),
                             reads=["Wsm", ("diff", gi)], writes=pk(b))
                        P.op("act", lambda e, gi=gi, b=b: e.activation(out=ymix[:, gi, 0:n], in_=PS[:, b, 0:n], func=AF.Copy,
                                                                       scale=pscale[:, i, gi:gi + 1]),
                             reads=pk(b) + ["pscale"], writes=["ymix"])
                    for ft in range(4):
                        if not is_s:
                            z0 = Z[:, ft, 0:n]; z1 = Z[:, ft, 1:n + 1]; z2 = Z[:, ft, 2:n + 2]
                            cav = ca[:, 0:n]
                            bgv = bg[:, ft, 0:n]
                            yv = ymix[:, 4 + ft, 0:n]
                            zk = [("Z", par, ft), ("Zhalo", par)]
                        else:
                            z0 = Zs[:, ft, :, 0:4]; z1 = Zs[:, ft, :, 1:5]; z2 = Zs[:, ft, :, 2:6]
                            cav = ca[:, 0:NS].rearrange("p (b t) -> p b t", t=4)
                            bgv = bg[:, ft, 0:NS].rearrange("p (b t) -> p b t", t=4)
                            yv = ymix[:, 4 + ft, 0:NS].rearrange("p (b t) -> p b t", t=4)
                            zk = [("Zs", ft)]
                        P.op("dve", lambda e, ft=ft, z0=z0, cav=cav: e.tensor_scalar(
                            out=cav, in0=z0, scalar1=cw[:, i, 0, ft:ft + 1], scalar2=cb[:, i, ft:ft + 1],
                            op0=ALU.mult, op1=ALU.add), reads=zk + ["cw", "cb"], writes=["ca"])
                        P.op("dve", lambda e, ft=ft, z1=z1, cav=cav: e.scalar_tensor_tensor(
                            out=cav, in0=z1, scalar=cw[:, i, 1, ft:ft + 1], in1=cav, op0=ALU.mult, op1=ALU.add),
                            reads=zk + ["cw", "ca"], writes=["ca"])
                        P.op("dve", lambda e, ft=ft, z2=z2, cav=cav: e.scalar_tensor_tensor(
                            out=cav, in0=z2, scalar=cw[:, i, 2, ft:ft + 1], in1=cav, op0=ALU.mult, op1=ALU.add),
                            reads=zk + ["cw", "ca"], writes=["ca"])
                        P.op("dve", lambda e, cav=cav, bgv=bgv, yv=yv: e.tensor_tensor(out=yv, in0=cav, in1=bgv, op=ALU.mult),
                             reads=["ca", ("bg", par, ft)], writes=["ymix"])
                    out_proj_tile(Wout, "Wout", ymix, "ymix", t0, n)
                    if (not is_s) and t0 + n == SEQ:
                        for ft in range(4):
                            b = psget()
                            P.op("pe", lambda e, b=b, ft=ft: e.transpose(PS[0:15, b, 0:128], xchalo[:, ft, :], ident[:]),
                                 reads=["xchalo", "ident"], writes=pk(b))
                            P.op("pe", lambda e, b=b, ft=ft: e.transpose(PS[0:2, b, 128:256], zhalo[:, ft, :], ident[:]),
                                 reads=["zhalo", "ident"], writes=pk(b))
                            P.op("act", lambda e, b=b, ft=ft: e.activation(out=opp[0:15, ft * 128:(ft + 1) * 128],
                                                                           in_=PS[0:15, b, 0:128], func=AF.Copy),
                                 reads=pk(b), writes=["opp"])
                            P.op("act", lambda e, b=b, ft=ft: e.activation(out=opc[0:2, ft * 128:(ft + 1) * 128],
                                                                           in_=PS[0:2, b, 128:256], func=AF.Copy),
                                 reads=pk(b), writes=["opc"])
                        P.dma("sp", o_p_pool[i], opp[0:15, :], reads=["opp"])
                        P.dma("sp", o_p_conv[i], opc[0:2, :], reads=["opc"])
                    if is_s:
                        for half in range(2):
                            for ft in range(4):
                                P.op("dve", lambda e, ft=ft, half=half: e.tensor_copy(
                                    out=xct[:, 0:120].rearrange("p (b r) -> p b r", r=15),
                                    in_=XCs[:, ft, half * 8:half * 8 + 8, 4:19]), reads=[("XCs", ft)], writes=["xct"])
                                b = psget()
                                P.op("pe", lambda e, b=b: e.transpose(PS[0:120, b, 0:128], xct[:, 0:120], ident[:]),
                                     reads=["xct", "ident"], writes=pk(b))
                                P.op("act", lambda e, b=b, ft=ft: e.activation(out=otp[0:120, ft * 128:(ft + 1) * 128],
                                                                               in_=PS[0:120, b, 0:128], func=AF.Copy),
                                     reads=pk(b), writes=["otp"])
                            P.dma("sp", o_s_pool[i, half * 120:half * 120 + 120, :], otp[0:120, :], reads=["otp"])
                        for ft in range(4):
                            P.op("dve", lambda e, ft=ft: e.tensor_copy(
                                out=zct[:, 0:32].rearrange("p (b r) -> p b r", r=2), in_=Zs[:, ft, :, 4:6]),
                                reads=[("Zs", ft)], writes=["zct"])
                            b = psget()
                            P.op("pe", lambda e, b=b: e.transpose(PS[0:32, b, 0:128], zct[:, 0:32], ident[:]),
                                 reads=["zct", "ident"], writes=pk(b))
                            P.op("act", lambda e, b=b, ft=ft: e.activation(out=otc[:, ft * 128:(ft + 1) * 128],
                                                                           in_=PS[0:32, b, 0:128], func=AF.Copy),
                                 reads=pk(b), writes=["otc"])
                        P.dma("sp", o_s_conv[i], otc[:, :], reads=["otc"])
                seq = list(enumerate(mtiles))
                for idx, (ti, (t0, n, is_s)) in enumerate(seq):
                    front(ti, t0, n, is_s)
                    if idx >= 1:
                        pti, (pt0, pn, ps_) = seq[idx - 1]
                        back(pti, pt0, pn, ps_)
                lti, (lt0, ln, ls_) = seq[-1]
                back(lti, lt0, ln, ls_)
                P.flush()

        def epilogue(st):
            gfin = WinP[:, 2, :].bitcast(F32)
            P.dma("sp", gfin, norm_final.broadcast_to([128, D]), writes=["gfin"])
            junk = sb(st, "fjunk", [128, 512], BF16)
            ss = sb(st, "fss", [128, 2, 2])
            yo = [WinP[:, q, :].bitcast(F32) for q in range(2)]
            nsub = SEQ // 128 + 1
            for si in range(nsub):
                n = 128 if si < SEQ // 128 else NS
                dst = y_p[si * 128:(si + 1) * 128, :] if si < SEQ // 128 else y_s[:, :]
                par = si % 2
                yb = yo[par]
                yk = "yo%d" % par
                hk = hkeys(si * 128, n)
                bb = psget(2)
                for k in range(8):
                    P.op("pe", lambda e, k=k, bb=bb, si=si, n=n: e.transpose(
                        PS[0:n, bb + k // 4, (k % 4) * 128:(k % 4 + 1) * 128], hres[:, k, si * 128:si * 128 + n], ident[:]),
                        reads=["ident"] + hk, writes=pk(bb, 2), cost=110.0)
                for half in range(2):
                    P.op("act", lambda e, bb=bb, half=half, n=n, par=par: e.activation(
                        out=junk[0:n, :], in_=PS[0:n, bb + half, :], func=AF.Square, accum_out=ss[0:n, par, half:half + 1]),
                        reads=pk(bb, 2), writes=["fjunk", ("fss", par, half)])
                P.op("dve", lambda e, n=n, par=par: e.tensor_tensor(out=ss[0:n, par, 0:1], in0=ss[0:n, par, 0:1],
                                                                     in1=ss[0:n, par, 1:2], op=ALU.add),
                     reads=[("fss", par, 0), ("fss", par, 1)], writes=[("fss", par, 0)], cost=100.0)
                P.op("act", lambda e, n=n, par=par: e.activation(out=ss[0:n, par, 0:1], in_=ss[0:n, par, 0:1], func=AF.Sqrt,
                                                                  bias=epsc[0:n, 0:1], scale=1.0 / D),
                     reads=[("fss", par, 0)], writes=[("fss", par, 0)], cost=250.0)
                P.op("dve", lambda e, n=n, par=par: e.reciprocal(out=ss[0:n, par, 0:1], in_=ss[0:n, par, 0:1]),
                     reads=[("fss", par, 0)], writes=[("fss", par, 0)], cost=100.0)
                for half in range(2):
                    P.op("dve", lambda e, bb=bb, half=half, n=n, yb=yb, par=par: e.scalar_tensor_tensor(
                        out=yb[0:n, half * 512:(half + 1) * 512], in0=PS[0:n, bb + half, :], scalar=ss[0:n, par, 0:1],
                        in1=gfin[0:n, half * 512:(half + 1) * 512], op0=ALU.mult, op1=ALU.mult),
                        reads=pk(bb, 2) + [("fss", par, 0), "gfin"], writes=[yk], cost=750.0)
                P.dma("sp", dst, yb[0:n, :], reads=[yk])

        def ffn(layer):
            widths = [384] * 7 + [128]
            offs = [sum(widths[:j]) for j in range(len(widths))]
            with ExitStack() as st:
                xn = sb(st, "xn_all", [128, 8, T], BF16)
                Wg = [sb(st, "Wg%d" % q, [128, 8, 384], BF16) for q in range(2)]
                Wu = [sb(st, "Wu%d" % q, [128, 8, 384], BF16) for q in range(2)]
                Wd = [sb(st, "Wd%d" % q, [128, 3, D], BF16) for q in range(2)]
                sl = [sb(st, "sl%d" % q, [128, TF]) for q in range(2)]
                hb = [sb(st, "hb%d" % q, [128, 3, TF], BF16) for q in range(2)]

                P.cost.update({"pe": 195.0, "dve": 630.0, "act": 560.0})

                def load_slice(j):
                    q = j % 2
                    w = widths[j]
                    o = offs[j]
                    c = 2500.0 + 128 * 8 * w * 4 / 150.0
                    for kh in range(2):
                        P.dma("pool", Wg[q][:, 4 * kh:4 * kh + 4, 0:w],
                              ffn_g[layer].rearrange("(k p) n -> p k n", p=128)[:, 4 * kh:4 * kh + 4, o:o + w],
                              writes=[("Wg", q, kh)], cost=c / 2)
                    for kh in range(2):
                        P.dma("pool", Wu[q][:, 4 * kh:4 * kh + 4, 0:w],
                              ffn_u[layer].rearrange("(k p) n -> p k n", p=128)[:, 4 * kh:4 * kh + 4, o:o + w],
                              writes=[("Wu", q, kh)], cost=c / 2)
                    P.dma("pool", Wd[q][:, 0:w // 128, :],
                          ffn_d[layer].rearrange("(k p) n -> p k n", p=128)[:, o // 128:(o + w) // 128, :],
                          writes=[("Wd", q)], cost=c)
                load_slice(0)
                load_slice(1)
                if layer + 1 < 4:
                    load_mixer_weights(layer + 1)
                for ti, (t0, n, is_s) in enumerate(ftiles):
                    if ti == 0:
                        rmsnorm_tile(st, "f", t0, n, gffn[:, layer, :], xn, ("xn", t0), xoff=t0)
                    else:
                        _rms_ops("f", t0, n, gffn[:, layer, :], xn, ("xn", t0), _norm_scr["f"], xoff=t0)
                hbi = 0
                for j in range(len(widths)):
                    q = j % 2
                    nhc = widths[j] // 128
                    for (t0, n, is_s) in ftiles:
                        hk = hkeys(t0, n)
                        hbuf = hb[hbi % 2]
                        hkey = "hb%d" % (hbi % 2)
                        hbi += 1
                        for hc in range(nhc):
                            bgt = psget()
                            for k in range(8):
                                P.op("pe", lambda e, k=k, hc=hc, bgt=bgt, q=q, t0=t0, n=n: e.matmul(
                                    PS[:, bgt, 0:n], lhsT=Wg[q][:, k, hc * 128:(hc + 1) * 128], rhs=xn[:, k, t0:t0 + n],
                                    start=(k == 0), stop=(k == 7)), reads=[("Wg", q, k // 4), ("xn", t0)], writes=pk(bgt),
                                    cost=n / 2.35 + 6)
                            but = psget()
                            for k in range(8):
                                P.op("pe", lambda e, k=k, hc=hc, but=but, q=q, t0=t0, n=n: e.matmul(
                                    PS[:, but, 0:n], lhsT=Wu[q][:, k, hc * 128:(hc + 1) * 128], rhs=xn[:, k, t0:t0 + n],
                                    start=(k == 0), stop=(k == 7)), reads=[("Wu", q, k // 4), ("xn", t0)], writes=pk(but),
                                    cost=n / 2.35 + 6)
                            slt = sl[hc % 2]
                            slk = "sl%d" % (hc % 2)
                            P.op("act", lambda e, bgt=bgt, slt=slt, n=n: e.activation(out=slt[:, 0:n], in_=PS[:, bgt, 0:n], func=AF.Silu),
                                 reads=pk(bgt), writes=[slk], cost=(224 + n) / 1.2)
                            P.op("dve", lambda e, but=but, slt=slt, hbuf=hbuf, hc=hc, n=n: e.tensor_tensor(
                                out=hbuf[:, hc, 0:n], in0=slt[:, 0:n], in1=PS[:, but, 0:n], op=ALU.mult),
                                reads=pk(but) + [slk], writes=[(hkey, hc)], cost=(160 + n) / 0.96)
                        for fo in range(8):
                            b = psget()
                            for hc in range(nhc):
                                P.op("pe", lambda e, hc=hc, fo=fo, b=b, q=q, hbuf=hbuf, n=n, nhc=nhc: e.matmul(
                                    PS[:, b, 0:n], lhsT=Wd[q][:, hc, fo * 128:(fo + 1) * 128], rhs=hbuf[:, hc, 0:n],
                                    start=(hc == 0), stop=(hc == nhc - 1)), reads=[("Wd", q), (hkey, hc)], writes=pk(b),
                                    cost=n / 2.35 + 6)
                            P.op("dve", lambda e, fo=fo, b=b, t0=t0, n=n: e.tensor_tensor(
                                out=hres[:, fo, t0:t0 + n], in0=hres[:, fo, t0:t0 + n], in1=PS[:, b, 0:n], op=ALU.add),
                                reads=pk(b) + hk, writes=hk, cost=(160 + n) / 0.96)
                    if j + 2 < len(widths):
                        load_slice(j + 2)
                if layer == 3:
                    epilogue(st)
                P.flush()

        for layer in range(4):
            if layer > 0:
                P.next_epoch()
            if layer % 2 == 0:
                even_mixer(layer)
            else:
                odd_mixer(layer)
            ffn(layer)

    return nc


_NC_CACHE = {}


def kernel(**inputs):
    f = lambda a: np.ascontiguousarray(np.asarray(a, dtype=np.float32))
    inp = {k: f(v) for k, v in inputs.items()}
    if "nc" not in _NC_CACHE:
        _NC_CACHE["nc"] = build_nc()
    nc = _NC_CACHE["nc"]
    shared = {}
    for k in ("norm_mix", "norm_ffn", "w_in_even", "w_out_even", "s5_lambda_re", "s5_lambda_im", "s5_log_dt",
              "s5_b_re", "s5_b_im", "s5_c_re", "s5_c_im", "s5_glu_w", "s5_glu_b", "sgu_norm", "sgu_w", "sgu_b",
              "w_in_odd", "w_out_odd", "pool_w", "pool_scale", "conv_w", "conv_b", "ffn_w_gate", "ffn_w_up",
              "ffn_w_down"):
        shared[k] = inp[k]
    shared["norm_final"] = inp["norm_final"].reshape(1, D)
    shared["s5_d"] = inp["s5_d"].reshape(2, 512)
    in_maps = []
    for c in range(NCORES):
        m = dict(shared)
        m["x_p"] = inp["x_prompt"][c]
        m["x_s"] = np.ascontiguousarray(inp["x_sample"][16 * c:16 * c + 16].reshape(NS, D))
        m["st_re"] = np.ascontiguousarray(inp["state_s5_re"][:, 16 * c:16 * c + 16].reshape(2, 16, 2048))
        m["st_im"] = np.ascontiguousarray(inp["state_s5_im"][:, 16 * c:16 * c + 16].reshape(2, 16, 2048))
        m["st_pool"] = np.ascontiguousarray(inp["state_pool"][:, 16 * c:16 * c + 16])
        m["st_conv"] = np.ascontiguousarray(inp["state_conv"][:, 16 * c:16 * c + 16])
        in_maps.append(m)
    res = run_bass_kernel_spmd(nc, in_maps, core_ids=list(range(NCORES)))
    R = res.results
    y_prompt = np.stack([R[c]["y_p"] for c in range(NCORES)], 0).reshape(8, SEQ, D)
    y_sample = np.concatenate([R[c]["y_s"].reshape(16, 4, D) for c in range(NCORES)], 0)
    p_re = np.stack([R[c]["o_p_re"].reshape(2, 32, 64) for c in range(NCORES)], 1)
    p_im = np.stack([R[c]["o_p_im"].reshape(2, 32, 64) for c in range(NCORES)], 1)
    p_pool = np.stack([R[c]["o_p_pool"] for c in range(NCORES)], 1)
    p_conv = np.stack([R[c]["o_p_conv"] for c in range(NCORES)], 1)
    s_re = np.concatenate([R[c]["o_s_re"].reshape(2, 16, 32, 64) for c in range(NCORES)], 1)
    s_im = np.concatenate([R[c]["o_s_im"].reshape(2, 16, 32, 64) for c in range(NCORES)], 1)
    s_v = np.concatenate([R[c]["o_s_v"].reshape(2, 16, 4, 512) for c in range(NCORES)], 1)
    s_pool = np.concatenate([R[c]["o_s_pool"].reshape(2, 16, 15, 512) for c in range(NCORES)], 1)
    s_conv = np.concatenate([R[c]["o_s_conv"].reshape(2, 16, 2, 512) for c in range(NCORES)], 1)
    outs = (y_prompt, y_sample, p_re, p_im, p_pool, p_conv, s_re, s_im, s_v, s_pool, s_conv)
    return tuple(np.ascontiguousarray(o.astype(np.float32)) for o in outs)
```

```python
import math
import numpy as np
from contextlib import ExitStack
import concourse.bass as bass
import concourse.mybir as mybir
from concourse.bass_utils import run_bass_kernel_spmd

F32 = mybir.dt.float32
BF16 = mybir.dt.bfloat16
I32 = mybir.dt.int32
ALU = mybir.AluOpType
AF = mybir.ActivationFunctionType

ENGS = ("pe", "act", "dve", "pool", "sp")
NSLOT = 12
NCORES = 8
D = 1024
SEQ = 2048
NS = 64
T = SEQ + NS
DFF = 2816
EPS = 1e-6
TM = 256
TF = 448
FS = 256
NSL = DFF // FS


class _Op(object):
    __slots__ = ("idx", "eng", "emit", "deps", "signal", "epoch", "semval",
                 "is_dma", "slot", "dval", "prev_dval", "cost", "pos")


class Prog(object):
    def __init__(self, nc, es, n_epochs=6):
        self.nc = nc
        self.ops = []
        self.regions = {}
        self.epoch = 0
        self.n_epochs = n_epochs
        self.sems = {}
        self.cnt = {}
        for e in ENGS:
            for ep in range(n_epochs):
                self.sems[(e, ep)] = es.enter_context(nc.semaphore("s_%s_%d" % (e, ep)))
        self.dsems = {}
        self.dcount = {}
        self.dnext = {}
        for q in ("sp", "pool", "act"):
            self.dnext[q] = 0
            for s in range(NSLOT):
                self.dsems[(q, s)] = es.enter_context(nc.semaphore("d_%s_%d" % (q, s)))
                self.dcount[(q, s)] = 0
        self.nflush = 0
        self.cost = {"pe": 115.0, "act": 450.0, "dve": 430.0, "pool": 600.0, "sp": 100.0}
        self.reorder = True
        self.filler = None
        self.filler_cost = 170.0
        self.nfill = 0

    def next_epoch(self):
        assert not self.ops
        self.epoch = min(self.epoch + 1, self.n_epochs - 1)

    def _add(self, eng, emit, reads, writes, is_dma, cost):
        o = _Op()
        o.idx = len(self.ops)
        o.eng = eng
        o.emit = emit
        o.signal = False
        o.epoch = self.epoch
        o.semval = None
        o.is_dma = is_dma
        o.cost = cost if cost is not None else (3000.0 if is_dma else self.cost[eng])
        deps = set()
        for k in reads:
            r = self.regions.get(k)
            if r is not None and r[0] is not None:
                deps.add(r[0])
        for k in writes:
            r = self.regions.get(k)
            if r is not None:
                if r[0] is not None:
                    deps.add(r[0])
                deps.update(r[1])
        for k in reads:
            r = self.regions.get(k)
            if r is None:
                r = [None, []]
                self.regions[k] = r
            r[1].append(o.idx)
        for k in writes:
            self.regions[k] = [o.idx, []]
        deps.discard(o.idx)
        o.deps = deps
        o.slot = None
        self.ops.append(o)
        return o

    def op(self, eng, emit, reads=(), writes=(), cost=None):
        return self._add(eng, emit, reads, writes, False, cost)

    def dma(self, q, out, in_, reads=(), writes=(), cost=None, **kw):
        def emit(e, out=out, in_=in_, kw=kw):
            return e.dma_start(out=out, in_=in_, **kw)
        return self._add(q, emit, reads, writes, True, cost)

    def _schedule(self):
        ops = self.ops
        n = len(ops)
        succs = [[] for _ in range(n)]
        indeg = [0] * n
        for o in ops:
            for d in o.deps:
                succs[d].append(o.idx)
            indeg[o.idx] = len(o.deps)
        lastd = {}
        dchain = {}
        for o in ops:
            if o.is_dma:
                if o.eng in lastd:
                    dchain[o.idx] = lastd[o.eng]
                lastd[o.eng] = o.idx
        prio = [0.0] * n
        for i in range(n - 1, -1, -1):
            m = 0.0
            for s_ in succs[i]:
                if prio[s_] > m:
                    m = prio[s_]
            prio[i] = ops[i].cost + m
        order = {e: [] for e in ENGS}
        if not self.reorder:
            for o in ops:
                order[o.eng].append(o)
            return order
        ready = {e: [] for e in ENGS}
        ready_t = [0.0] * n
        fin = [0.0] * n
        issued = [False] * n
        free_at = {e: 0.0 for e in ENGS}
        for o in ops:
            if indeg[o.idx] == 0:
                ready[o.eng].append(o.idx)
        remaining = n
        HOP = 150.0
        while remaining:
            best = None
            for e in ENGS:
                rl = ready[e]
                if not rl:
                    continue
                fa = free_at[e]
                cb = None
                for i in rl:
                    o = ops[i]
                    if o.is_dma and i in dchain and not issued[dchain[i]]:
                        continue
                    st = ready_t[i] if ready_t[i] > fa else fa
                    key = (st, -prio[i], i)
                    if cb is None or key < cb:
                        cb = key
                if cb is not None and (best is None or cb < best[0]):
                    best = (cb, e)
            assert best is not None, "scheduler deadlock"
            (st, _, i), e = best
            o = ops[i]
            ready[e].remove(i)
            issued[i] = True
            if e == "pe" and self.filler is not None and free_at[e] > 0.0:
                gap = st - free_at[e]
                if gap > 1200.0:
                    k = min(int((gap - 500.0) / self.filler_cost), 60)
                    for _ in range(k):
                        f = _Op()
                        f.idx = -1
                        f.eng = "pe"
                        f.emit = self.filler
                        f.is_dma = False
                        f.signal = False
                        order[e].append(f)
                    self.nfill += k
            if o.is_dma:
                free_at[e] = st + 80.0
                fin[i] = st + o.cost
            else:
                free_at[e] = st + o.cost
                fin[i] = st + o.cost
            order[e].append(o)
            remaining -= 1
            for s_ in succs[i]:
                t = fin[i] + (0.0 if (ops[s_].eng == e and e == "pe") else HOP)
                if t > ready_t[s_]:
                    ready_t[s_] = t
                indeg[s_] -= 1
                if indeg[s_] == 0:
                    ready[ops[s_].eng].append(s_)
        self.est_time = max(fin) if n else 0.0
        return order

    def flush(self):
        nc = self.nc
        ops = self.ops
        if not ops:
            return
        per_eng = self._schedule()
        for e in ENGS:
            for p_, o in enumerate(per_eng[e]):
                o.pos = p_
            if e != "pe":
                assert all(o.idx >= 0 for o in per_eng[e])
        for e in ("sp", "pool", "act"):
            for o in per_eng[e]:
                if o.is_dma:
                    s = self.dnext[e]
                    self.dnext[e] = (s + 1) % NSLOT
                    o.slot = s
                    o.prev_dval = self.dcount[(e, s)]
                    self.dcount[(e, s)] += 16
                    o.dval = self.dcount[(e, s)]
        red = []
        for o in ops:
            comp = {}
            dmas = []
            for d in o.deps:
                p = ops[d]
                if p.is_dma:
                    dmas.append(d)
                else:
                    if p.eng == "pe" and o.eng == "pe" and not o.is_dma:
                        continue
                    if p.eng not in comp or ops[comp[p.eng]].pos < p.pos:
                        comp[p.eng] = d
            red.append((comp, dmas))
            for d in comp.values():
                ops[d].signal = True
        cnt = self.cnt
        for e in ENGS:
            for o in per_eng[e]:
                if o.idx < 0 or o.is_dma or not o.signal:
                    continue
                key = (o.eng, o.epoch)
                cnt[key] = cnt.get(key, 0) + 1
                o.semval = cnt[key]
        sems = self.sems
        dsems = self.dsems
        n_ep = self.n_epochs
        dcount = self.dcount

        def emit_engine(e, eng_name):
            waited = {}
            dwaited = {}
            for o in per_eng[eng_name]:
                if o.idx < 0:
                    o.emit(e)
                    continue
                comp, dmas = red[o.idx]
                for pe_name, d in comp.items():
                    p = ops[d]
                    done = False
                    for ep in range(p.epoch, n_ep):
                        w = waited.get((pe_name, ep), 0)
                        if ep == p.epoch and w >= p.semval:
                            done = True
                        if ep > p.epoch and w > 0:
                            done = True
                    if done:
                        continue
                    e.wait_ge(sems[(pe_name, p.epoch)], p.semval)
                    waited[(pe_name, p.epoch)] = p.semval
                for d in dmas:
                    p = ops[d]
                    k = (p.eng, p.slot)
                    if dwaited.get(k, 0) >= p.dval:
                        continue
                    e.wait_ge(dsems[k], p.dval)
                    dwaited[k] = p.dval
                if o.is_dma:
                    k = (o.eng, o.slot)
                    if o.prev_dval > 0 and dwaited.get(k, 0) < o.prev_dval:
                        e.wait_ge(dsems[k], o.prev_dval)
                        dwaited[k] = o.prev_dval
                    inst = o.emit(e)
                    inst.then_inc(dsems[k], 16)
                else:
                    inst = o.emit(e)
                    if o.signal:
                        inst.then_inc(sems[(o.eng, o.epoch)], 1)
            if eng_name in ("sp", "pool", "act"):
                for s in range(NSLOT):
                    k = (eng_name, s)
                    if dcount[k] > 0 and dwaited.get(k, 0) < dcount[k]:
                        e.wait_ge(dsems[k], dcount[k])

        with nc.Block() as block:
            @block.tensor
            def _(e):
                emit_engine(e, "pe")

            @block.scalar
            def _(e):
                emit_engine(e, "act")

            @block.vector
            def _(e):
                emit_engine(e, "dve")

            @block.gpsimd
            def _(e):
                emit_engine(e, "pool")

            @block.sync
            def _(e):
                emit_engine(e, "sp")
        self.ops = []
        self.regions = {}
        self.nflush += 1


def build_nc(debug=False):
    nc = bass.Bass("TRN2", target_bir_lowering=False)
    try:
        nc.allow_low_precision("bf16 matmul operands with fp32 accumulation by design")
    except Exception:
        pass

    def din(name, shape):
        return nc.dram_tensor(name, list(shape), F32, kind="ExternalInput").ap()

    def dout(name, shape):
        return nc.dram_tensor(name, list(shape), F32, kind="ExternalOutput").ap()

    x_p = din("x_p", (SEQ, D))
    x_s = din("x_s", (NS, D))
    st_re = din("st_re", (2, 16, 2048))
    st_im = din("st_im", (2, 16, 2048))
    st_pool = din("st_pool", (2, 16, 15, 512))
    st_conv = din("st_conv", (2, 16, 2, 512))
    norm_mix = din("norm_mix", (4, D))
    norm_ffn = din("norm_ffn", (4, D))
    norm_final = din("norm_final", (1, D))
    w_in_even = din("w_in_even", (2, D, 1536))
    w_out_even = din("w_out_even", (2, D, D))
    lam_re = din("s5_lambda_re", (2, 32, 64))
    lam_im = din("s5_lambda_im", (2, 32, 64))
    log_dt = din("s5_log_dt", (2, 32))
    b_re = din("s5_b_re", (2, 32, 64, 16))
    b_im = din("s5_b_im", (2, 32, 64, 16))
    c_re = din("s5_c_re", (2, 32, 16, 64))
    c_im = din("s5_c_im", (2, 32, 16, 64))
    s5_d = din("s5_d", (2, 512))
    glu_w = din("s5_glu_w", (2, 512, 512))
    glu_b = din("s5_glu_b", (2, 512))
    sgu_norm = din("sgu_norm", (2, 512))
    sgu_w = din("sgu_w", (2, 8, 128, 128))
    sgu_b = din("sgu_b", (2, 8, 128))
    w_in_odd = din("w_in_odd", (2, D, 2048))
    w_out_odd = din("w_out_odd", (2, D, D))
    pool_w = din("pool_w", (2, 4, 128, 128))
    pool_scale = din("pool_scale", (2, 512))
    conv_w = din("conv_w", (2, 3, 512))
    conv_b = din("conv_b", (2, 512))
    ffn_g = din("ffn_w_gate", (4, D, DFF))
    ffn_u = din("ffn_w_up", (4, D, DFF))
    ffn_d = din("ffn_w_down", (4, DFF, D))

    y_p = dout("y_p", (SEQ, D))
    y_s = dout("y_s", (NS, D))
    o_p_re = dout("o_p_re", (2, 16, 128))
    o_p_im = dout("o_p_im", (2, 16, 128))
    o_p_pool = dout("o_p_pool", (2, 15, 512))
    o_p_conv = dout("o_p_conv", (2, 2, 512))
    o_s_re = dout("o_s_re", (2, 16, 2048))
    o_s_im = dout("o_s_im", (2, 16, 2048))
    o_s_v = dout("o_s_v", (2, NS, 512))
    o_s_pool = dout("o_s_pool", (2, 240, 512))
    o_s_conv = dout("o_s_conv", (2, 32, 512))
    dbg = dout("dbg", (128, 4096)) if debug else None

    es = ExitStack()
    with es:
        es.enter_context(nc.allow_non_contiguous_dma(reason="small strided parameter loads"))
        P = Prog(nc, es)

        _uid = [0]

        def sb(stk, name, shape, dt=F32):
            _uid[0] += 1
            return stk.enter_context(nc.sbuf_tensor("%s_u%d" % (name, _uid[0]), list(shape), dt))

        PS = es.enter_context(nc.psum_tensor("PS", [128, 8, 512], F32))
        ps_rr = [0]

        def psget(n=1):
            b = ps_rr[0]
            if b + n > 8:
                b = 0
            ps_rr[0] = (b + n) % 8
            return b

        def pk(b, n=1):
            return [("ps", b + i) for i in range(n)]

        hres = sb(es, "hres", [128, 8, T])
        ident = sb(es, "ident", [128, 128])
        identb = sb(es, "identb", [128, 128], BF16)
        onesb = sb(es, "onesb", [128, 128], BF16)
        iot = sb(es, "iot", [128, 128])
        iop = sb(es, "iop", [128, 1])
        pstage = sb(es, "pstage", [128, 128])
        pvec = sb(es, "pvec", [128, 120])
        gmix = pvec[:, 0:32].rearrange("p (l k) -> p l k", l=4)
        gffn = pvec[:, 32:64].rearrange("p (l k) -> p l k", l=4)
        glub = pvec[:, 64:72].rearrange("p (l k) -> p l k", l=2)
        pscale = pvec[:, 72:80].rearrange("p (l k) -> p l k", l=2)
        cw = pvec[:, 80:104].rearrange("p (l c k) -> p l c k", l=2, c=3)
        cb = pvec[:, 104:112].rearrange("p (l k) -> p l k", l=2)
        dcol = pvec[:, 112:120].rearrange("p (l k) -> p l k", l=2)
        epsc = sb(es, "epsc", [128, 1])
        s5car = sb(es, "s5car", [128, 16, 2])
        xchalo = sb(es, "xchalo", [128, 4, 15])
        zhalo = sb(es, "zhalo", [128, 4, 2])

        def V(e):
            return e

        def load_w(dst, src3, key, nsplit):
            K = dst.shape[1]
            step = K // nsplit
            for i in range(nsplit):
                nb = 128 * step * dst.shape[2] * 4
                P.dma("pool", dst[:, i * step:(i + 1) * step, :],
                      src3.rearrange("(k p) n -> p k n", p=128)[:, i * step:(i + 1) * step, :], writes=[(key, i)],
                      cost=2500.0 + nb / 150.0)

        WinP = sb(es, "WinP", [128, 8, 2048], BF16)
        WoutP = sb(es, "WoutP", [128, 8, D], BF16)
        WsmP = sb(es, "WsmP", [128, 2048], BF16)

        def load_mixer_weights(layer, only_in=False, skip_in=False):
            i = layer // 2
            if layer % 2 == 0:
                if not skip_in:
                    load_w(WinP[:, :, 0:1536], w_in_even[i], "Win", 4)
                if only_in:
                    return
                load_w(WsmP[:, :].rearrange("p (k n) -> p k n", k=4), glu_w[i], "Wsm", 1)
                load_w(WoutP, w_out_even[i], "Wout", 2)
            else:
                load_w(WinP, w_in_odd[i], "Win", 4)
                load_w(WoutP, w_out_odd[i], "Wout", 2)
                P.dma("pool", WsmP[:, 0:512].rearrange("p (g d) -> p g d", g=4), pool_w[i].rearrange("g c d -> c g d"),
                      writes=["Wsm"])

        load_mixer_weights(0, only_in=True)
        P.op("pool", lambda e: e.iota(iot[:], pattern=[[1, 128]], base=0, channel_multiplier=0,
                                      allow_small_or_imprecise_dtypes=True), writes=["iot"])
        P.op("pool", lambda e: e.iota(iop[:], pattern=[[1, 1]], base=0, channel_multiplier=1,
                                      allow_small_or_imprecise_dtypes=True), writes=["iop"])
        P.op("dve", lambda e: e.tensor_scalar(out=ident[:], in0=iot[:], scalar1=iop[:, 0:1], scalar2=None,
                                              op0=ALU.is_equal), reads=["iot", "iop"], writes=["ident"])
        P.op("dve", lambda e: e.tensor_copy(out=identb[:], in_=ident[:]), reads=["ident"], writes=["identb"])
        P.op("dve", lambda e: e.memset(onesb[:], 1.0), writes=["onesb"])
        P.op("dve", lambda e: e.memset(epsc[:], EPS), writes=["epsc"])
        P.filler = None; _unused_filler = lambda e: e.matmul(PS[:, 7, 0:128], lhsT=onesb[:], rhs=onesb[:], start=True, stop=True)
        with ExitStack() as st:
            pass
        pst_rows = [(norm_mix.rearrange("l (k p) -> (l k) p", p=128), 32), (norm_ffn.rearrange("l (k p) -> (l k) p", p=128), 32),
                    (glu_b.rearrange("l (k p) -> (l k) p", p=128), 8), (pool_scale.rearrange("l (k p) -> (l k) p", p=128), 8),
                    (conv_w.rearrange("l c (k p) -> (l c k) p", p=128), 24), (conv_b.rearrange("l (k p) -> (l k) p", p=128), 8),
                    (s5_d.rearrange("l (k p) -> (l k) p", p=128), 8)]
        r0 = 0
        for j_, (src_, nr_) in enumerate(pst_rows):
            P.dma("sp", pstage[r0:r0 + nr_, :], src_, writes=[("pstage", j_)])
            r0 += nr_
        P.op("pe", lambda e: e.transpose(PS[:, 5, 0:120], pstage[0:120, :], ident[0:120, 0:120]),
             reads=[("pstage", j_) for j_ in range(7)] + ["ident"], writes=[("ps", 5)])
        P.op("act", lambda e: e.activation(out=pvec[:, :], in_=PS[:, 5, 0:120], func=AF.Copy), reads=[("ps", 5)],
             writes=["gmix", "gffn", "glub", "pscale", "cw", "cb", "dcol"])

        def load_x(xt):
            nsub = SEQ // 128 + 1
            for si in range(nsub):
                n = 128 if si < SEQ // 128 else NS
                src = x_p[si * 128:(si + 1) * 128, :] if si < SEQ // 128 else x_s[:, :]
                xb = xt[si % 2]
                xk = "xt%d" % (si % 2)
                P.dma("sp", xb[0:n, :], src, writes=[xk])
                for half in range(2):
                    b = psget()
                    for q in range(4):
                        k = half * 4 + q
                        P.op("pe", lambda e, b=b, q=q, k=k, xb=xb, n=n: e.transpose(
                            PS[:, b, q * 128:q * 128 + n], xb[0:n, k * 128:(k + 1) * 128], ident[0:n, 0:n]),
                            reads=[xk, "ident"], writes=pk(b))
                    eng = "act" if half == 0 else "dve"
                    if eng == "act":
                        P.op("act", lambda e, b=b, half=half, si=si, n=n: e.activation(
                            out=hres[:, half * 4:half * 4 + 4, si * 128:si * 128 + n],
                            in_=PS[:, b, :].rearrange("p (q t) -> p q t", q=4)[:, :, 0:n], func=AF.Copy),
                            reads=pk(b), writes=[("h", si, half)])
                    else:
                        P.op("dve", lambda e, b=b, half=half, si=si, n=n: e.tensor_copy(
                            out=hres[:, half * 4:half * 4 + 4, si * 128:si * 128 + n],
                            in_=PS[:, b, :].rearrange("p (q t) -> p q t", q=4)[:, :, 0:n]),
                            reads=pk(b), writes=[("h", si, half)])

        mtiles = [(i * TM, TM, False) for i in range(SEQ // TM)] + [(SEQ, NS, True)]
        ftiles = [(0, 448, False), (448, 448, False), (896, 448, False), (1344, 448, False), (1792, 320, False)]

        def hkeys(t0, n):
            ks = []
            a = (t0 // TM) * TM
            while a < t0 + n:
                ks.append(("hres", a))
                a += TM
            return ks

        def s5_setup(stk, i, XB, YC, BD, TC, TS, R4, A4, bmask):
            with ExitStack() as st:
                def t16(name):
                    return sb(st, "s5_" + name, [128, 16])
                LRI = sb(st, "s5_LRI", [128, 32])
                LR = LRI[:, 0:16]
                LI = LRI[:, 16:32]
                LDT, DT, Z, MAG, ANG = [t16(n_) for n_ in ("LDT", "DT", "Z", "MAG", "ANG")]
                SN, CS, ta, tb, tc_, td = [t16(n_) for n_ in ("SN", "CS", "ta", "tb", "tc", "td")]
                FR, FI = t16("FR"), t16("FI")
                AR = [t16("AR%d" % k) for k in range(5)]
                AI = [t16("AI%d" % k) for k in range(5)]
                BR = sb(st, "s5_BR", [128, 16, 32]); BI = sb(st, "s5_BI", [128, 16, 32])
                BBr = sb(st, "s5_BBr", [128, 16, 32]); BBi = sb(st, "s5_BBi", [128, 16, 32])
                CTr = sb(st, "s5_CTr", [128, 16, 32]); CTi = sb(st, "s5_CTi", [128, 16, 32])
                Yr = sb(st, "s5_Yr", [128, 16, 32]); Yi = sb(st, "s5_Yi", [128, 16, 32])
                W1 = sb(st, "s5_W1", [128, 16, 32]); W2 = sb(st, "s5_W2", [128, 16, 32])
                W3 = sb(st, "s5_W3", [128, 16, 32]); W4 = sb(st, "s5_W4", [128, 16, 32])
                Xr_ = sb(st, "s5_Xr", [128, 16, 32]); Xi_ = sb(st, "s5_Xi", [128, 16, 32])
                CNr = sb(st, "s5_CNr", [128, 4, 128]); CNi = sb(st, "s5_CNi", [128, 4, 128])
                cnt = [0]

                def dv(fn, reads, writes, eng="dve"):
                    P.op(eng, fn, reads=reads, writes=writes)

                def tt(out, a, b, op, r, w, eng="dve"):
                    dv(lambda e: e.tensor_tensor(out=out, in0=a, in1=b, op=op), r, w, eng=eng)

                def ts(out, a, s1, op0, r, w, s2=None, op1=None):
                    if op1 is None:
                        dv(lambda e: e.tensor_scalar(out=out, in0=a, scalar1=s1, scalar2=None, op0=op0), r, w)
                    else:
                        dv(lambda e: e.tensor_scalar(out=out, in0=a, scalar1=s1, scalar2=s2, op0=op0, op1=op1), r, w)

                lst = sb(st, "s5_lst", [32, 128])
                P.dma("sp", lst[0:16, :], lam_re[i].rearrange("(P g) n -> P (g n)", g=2), writes=[("lst", 0)])
                P.dma("sp", lst[16:32, :], lam_im[i].rearrange("(P g) n -> P (g n)", g=2), writes=[("lst", 1)])
                bl_ = psget()
                P.op("pe", lambda e: e.transpose(PS[:, bl_, 0:32], lst[:, :], ident[0:32, 0:32]),
                     reads=[("lst", 0), ("lst", 1), "ident"], writes=pk(bl_))
                P.op("act", lambda e: e.activation(out=LRI[:, :], in_=PS[:, bl_, 0:32], func=AF.Copy), reads=pk(bl_),
                     writes=["LR", "LI"])
                for g2 in range(2):
                    P.dma("sp", LDT[64 * g2:64 * g2 + 64, :],
                          log_dt[i:i + 1, :].rearrange("o (P g) -> o g P", g=2)[:, g2, :].broadcast_to([64, 16]),
                          writes=[("LDT", g2)])
                for tl in (BR, BI, CNr, CNi):
                    dv(lambda e, tl=tl: e.memset(tl[:], 0.0), [], ["z_" + tl.name], eng="pool")
                for (tl, src) in ((BR, b_re), (BI, b_im)):
                    for g2 in range(2):
                        P.dma("sp", tl[64 * g2:64 * g2 + 64, :, 16 * g2:16 * g2 + 16],
                              src[i].rearrange("(P g) n q -> g n P q", g=2)[g2], reads=["z_" + tl.name],
                              writes=[("ld_" + tl.name, g2)])
                for (tl, src) in ((CNr, c_re), (CNi, c_im)):
                    for p4 in range(4):
                        for g2 in range(2):
                            P.dma("act", tl[32 * p4 + 16 * g2:32 * p4 + 16 * g2 + 16, :, 64 * g2:64 * g2 + 64],
                                  src[i].rearrange("(f a g) p n -> a g p f n", a=4, g=2)[p4, g2],
                                  reads=["z_" + tl.name], writes=[("ld_" + tl.name, p4, g2)])
                for (src, dst) in ((CNr, CTr), (CNi, CTi)):
                    b = psget()
                    for ft in range(4):
                        P.op("pe", lambda e, ft=ft, b=b, src=src: e.transpose(
                            PS[:, b, ft * 128:(ft + 1) * 128], src[:, ft, :], ident[:]),
                            reads=[("ld_" + src.name, a_, b_) for a_ in range(4) for b_ in range(2)] + ["ident"], writes=pk(b))
                    P.op("act", lambda e, b=b, dst=dst: e.activation(
                        out=dst[:].rearrange("p a b -> p (a b)"), in_=PS[:, b, :], func=AF.Copy),
                        reads=pk(b), writes=[dst.name])
                dv(lambda e: e.activation(out=DT[:], in_=LDT[:], func=AF.Exp), [("LDT", 0), ("LDT", 1)], ["DT"], eng="act")
                tt(Z[:], LR[:], DT[:], ALU.mult, ["LR", "DT"], ["Z"])
                ts(MAG[:], Z[:], 1.0 / 120.0, ALU.mult, ["Z"], ["MAG"], 1.0 / 24.0, ALU.add)
                for c in (1.0 / 6.0, 0.5, 1.0, 1.0):
                    tt(MAG[:], MAG[:], Z[:], ALU.mult, ["MAG", "Z"], ["MAG"])
                    ts(MAG[:], MAG[:], float(c), ALU.add, ["MAG"], ["MAG"])
                tt(ANG[:], LI[:], DT[:], ALU.mult, ["LI", "DT"], ["ANG"])
                C1 = 6.28125
                C2 = 2.0 * math.pi - C1
                MAGIC = 12582912.0
                for (shift, dst) in ((0.0, SN), (0.5 * math.pi, CS)):
                    ts(ta[:], ANG[:], 1.0 / (2 * math.pi), ALU.mult, ["ANG"], ["ta"], shift / (2 * math.pi), ALU.add)
                    ts(tb[:], ta[:], MAGIC, ALU.add, ["ta"], ["tb"])
                    ts(tb[:], tb[:], -MAGIC, ALU.add, ["tb"], ["tb"])
                    dv(lambda e: e.scalar_tensor_tensor(out=ta[:], in0=tb[:], scalar=-C1, in1=ANG[:],
                                                        op0=ALU.mult, op1=ALU.add), ["tb", "ANG"], ["ta"])
                    dv(lambda e: e.scalar_tensor_tensor(out=ta[:], in0=tb[:], scalar=-C2, in1=ta[:],
                                                        op0=ALU.mult, op1=ALU.add), ["tb", "ta"], ["ta"])
                    ts(ta[:], ta[:], float(shift), ALU.add, ["ta"], ["ta"], math.pi, ALU.min)
                    ts(ta[:], ta[:], -math.pi, ALU.max, ["ta"], ["ta"])
                    dv(lambda e, dst=dst: e.activation(out=dst[:], in_=ta[:], func=AF.Sin), ["ta"], [dst.name],
                       eng="act")
                tt(AR[1][:], MAG[:], CS[:], ALU.mult, ["MAG", CS.name], ["AR1"])
                tt(AI[1][:], MAG[:], SN[:], ALU.mult, ["MAG", SN.name], ["AI1"])
                dv(lambda e: e.memset(AR[0][:], 1.0), [], ["AR0"])
                dv(lambda e: e.memset(AI[0][:], 0.0), [], ["AI0"])

                def cmul(orr, oi, ar, ai, br, bi, rk, wk, t1=None, t2=None, k1="W1", k2="W2", eng="dve"):
                    tt(t1, ar, br, ALU.mult, rk, [k1], eng)
                    tt(t2, ai, bi, ALU.mult, rk, [k2], eng)
                    tt(orr, t1, t2, ALU.subtract, [k1, k2], [wk + "r"], eng)
                    tt(t1, ar, bi, ALU.mult, rk + [wk + "r"], [k1], eng)
                    tt(t2, ai, br, ALU.mult, rk + [wk + "r"], [k2], eng)
                    tt(oi, t1, t2, ALU.add, [k1, k2], [wk + "i"], eng)

                cmul(AR[2][:], AI[2][:], AR[1][:], AI[1][:], AR[1][:], AI[1][:], ["AR1", "AI1"], "A2", tc_[:], td[:], "tc", "td")
                cmul(AR[3][:], AI[3][:], AR[2][:], AI[2][:], AR[1][:], AI[1][:], ["AR1", "AI1", "A2r", "A2i"], "A3",
                     tc_[:], td[:], "tc", "td")
                cmul(AR[4][:], AI[4][:], AR[2][:], AI[2][:], AR[2][:], AI[2][:], ["A2r", "A2i"], "A4", tc_[:], td[:], "tc", "td")
                akeys = {0: ["AR0", "AI0"], 1: ["AR1", "AI1"], 2: ["A2r", "A2i"], 3: ["A3r", "A3i"], 4: ["A4r", "A4i"]}
                dv(lambda e: e.tensor_copy(out=A4[:, :, 0], in_=AR[4][:]), akeys[4], ["A4"])
                dv(lambda e: e.tensor_copy(out=A4[:, :, 1], in_=AI[4][:]), akeys[4] + ["A4"], ["A4"])
                tt(ta[:], MAG[:], MAG[:], ALU.mult, ["MAG"], ["ta"])
                tt(R4[:], ta[:], ta[:], ALU.mult, ["ta"], ["R4"])
                ts(ta[:], AR[1][:], -1.0, ALU.add, ["AR1"], ["ta"])
                tt(tb[:], LR[:], LR[:], ALU.mult, ["LR"], ["tb"])
                tt(tc_[:], LI[:], LI[:], ALU.mult, ["LI", "A4i"], ["tc"])
                tt(tb[:], tb[:], tc_[:], ALU.add, ["tb", "tc"], ["tb"])
                dv(lambda e: e.reciprocal(out=tb[:], in_=tb[:]), ["tb"], ["tb"])
                tt(tc_[:], ta[:], LR[:], ALU.mult, ["ta", "LR"], ["tc"])
                tt(td[:], AI[1][:], LI[:], ALU.mult, ["AI1", "LI", "A4i"], ["td"])
                tt(tc_[:], tc_[:], td[:], ALU.add, ["tc", "td"], ["tc"])
                tt(FR[:], tc_[:], tb[:], ALU.mult, ["tc", "tb"], ["FR"])
                tt(tc_[:], AI[1][:], LR[:], ALU.mult, ["AI1", "LR", "FR"], ["tc"])
                tt(td[:], ta[:], LI[:], ALU.mult, ["ta", "LI", "FR"], ["td"])
                tt(tc_[:], tc_[:], td[:], ALU.subtract, ["tc", "td"], ["tc"])
                tt(FI[:], tc_[:], tb[:], ALU.mult, ["tc", "tb"], ["FI"])
                dv(lambda e: e.reciprocal(out=ta[:], in_=R4[:]), ["R4", "FI"], ["ta"])
                tt(TC[:, :, 0], AR[4][:], ta[:], ALU.mult, akeys[4] + ["ta"], ["TC"])
                tt(TS[:, :, 0], AI[4][:], ta[:], ALU.mult, akeys[4] + ["ta"], ["TS"])
                m = 1
                NCH = TM // 4
                W5 = sb(st, "s5_W5", [128, 16, 32]); W6 = sb(st, "s5_W6", [128, 16, 32])
                while m < NCH:
                    ur = TC[:, :, m - 1:m].broadcast_to([128, 16, m])
                    ui = TS[:, :, m - 1:m].broadcast_to([128, 16, m])
                    w1 = W5[:, :, 0:m]; w2 = W6[:, :, 0:m]
                    tt(w1, TC[:, :, 0:m], ur, ALU.mult, ["TC", "TS"], ["W5"], "dve")
                    tt(w2, TS[:, :, 0:m], ui, ALU.mult, ["TC", "TS"], ["W6"], "dve")
                    tt(TC[:, :, m:2 * m], w1, w2, ALU.subtract, ["W5", "W6"], ["TC"], "dve")
                    tt(w1, TC[:, :, 0:m], ui, ALU.mult, ["TC", "TS"], ["W5"], "dve")
                    tt(w2, TS[:, :, 0:m], ur, ALU.mult, ["TC", "TS"], ["W6"], "dve")
                    tt(TS[:, :, m:2 * m], w1, w2, ALU.add, ["W5", "W6"], ["TS"], "dve")
                    m *= 2

                def bc(a):
                    return a.unsqueeze(2).broadcast_to([128, 16, 32])

                cmul(BBr[:], BBi[:], BR[:], BI[:], bc(FR[:]), bc(FI[:]), [("ld_" + BR.name, 0), ("ld_" + BR.name, 1), ("ld_" + BI.name, 0), ("ld_" + BI.name, 1), "FR", "FI"],
                     "BB", W1[:], W2[:])
                for k in range(4):
                    if k == 0:
                        srcs = (BBr, BBi)
                        skeys = ["BBr", "BBi"]
                    else:
                        cmul(Xr_[:], Xi_[:], BBr[:], BBi[:], bc(AR[k][:]), bc(AI[k][:]), ["BBr", "BBi"] + akeys[k], "Xq",
                             W3[:], W4[:], "W3", "W4", eng="pool")
                        srcs = (Xr_, Xi_)
                        skeys = ["Xqr", "Xqi"]
                    s = 3 - k
                    for ri in range(2):
                        b = psget()
                        for ft in range(4):
                            P.op("pe", lambda e, ft=ft, b=b, src=srcs[ri]: e.transpose(
                                PS[:, b, ft * 128:(ft + 1) * 128],
                                src[:, 4 * ft:4 * ft + 4, :].rearrange("p a b -> p (a b)"), ident[:]),
                                reads=skeys + ["ident"], writes=pk(b))
                        P.op("act", lambda e, b=b, ri=ri, s=s: e.activation(
                            out=XB[:, :, ri, s, :], in_=PS[:, b, :].rearrange("p (f n) -> p f n", f=4), func=AF.Copy),
                            reads=pk(b), writes=["XB"])
                bBD = psget()
                for k in range(5):
                    if k == 0:
                        dv(lambda e: e.tensor_copy(out=Yr[:], in_=CTr[:]), ["s5_CTr", "XB"], ["Yr"])
                        ts(Yi[:], CTi[:], -1.0, ALU.mult, ["s5_CTi", "XB"], ["Yi"])
                    else:
                        cmul(Yr[:], Yi[:], CTr[:], CTi[:], bc(AR[k][:]), bc(AI[k][:]),
                             ["s5_CTr", "s5_CTi", "BD%d" % (k - 1), "YC"] + akeys[k], "Y", W1[:], W2[:])
                        ts(Yi[:], Yi[:], -1.0, ALU.mult, ["Yi"], ["Yi"])
                        P.op("act", lambda e, k=k: e.activation(out=YC[:, :, 0, k - 1, :], in_=Yr[:], func=AF.Copy),
                             reads=["Yr"], writes=["YC"])
                        P.op("act", lambda e, k=k: e.activation(out=YC[:, :, 1, k - 1, :], in_=Yi[:], func=AF.Copy),
                             reads=["Yi"], writes=["YC"])
                    if k < 4:
                        for ft in range(4):
                            o_ = PS[:, bBD, ft * 128:(ft + 1) * 128]
                            P.op("pe", lambda e, ft=ft, o_=o_: e.matmul(
                                o_, lhsT=BBr[:, 4 * ft:4 * ft + 4, :].rearrange("p a b -> p (a b)"),
                                rhs=Yr[:, 4 * ft:4 * ft + 4, :].rearrange("p a b -> p (a b)"), start=True, stop=False),
                                reads=["BBr", "Yr"], writes=pk(bBD))
                            P.op("pe", lambda e, ft=ft, o_=o_: e.matmul(
                                o_, lhsT=BBi[:, 4 * ft:4 * ft + 4, :].rearrange("p a b -> p (a b)"),
                                rhs=Yi[:, 4 * ft:4 * ft + 4, :].rearrange("p a b -> p (a b)"), start=False, stop=True),
                                reads=["BBi", "Yi"], writes=pk(bBD))
                        for ft in range(4):
                            if k == 0:
                                dv(lambda e, ft=ft: e.tensor_tensor(out=W1[:, 0:4, :].rearrange("p a b -> p (a b)"),
                                                                    in0=PS[:, bBD, ft * 128:(ft + 1) * 128],
                                                                    in1=bmask[:], op=ALU.mult),
                                   pk(bBD) + ["bmask"], ["W1"])
                                dv(lambda e, ft=ft: e.scalar_tensor_tensor(
                                    out=BD[:, ft, 0, :], in0=ident[:], scalar=dcol[:, i, ft:ft + 1],
                                    in1=W1[:, 0:4, :].rearrange("p a b -> p (a b)"), op0=ALU.mult, op1=ALU.add),
                                    ["W1", "ident", "dcol"], ["BD0"])
                            else:
                                dv(lambda e, ft=ft, k=k: e.tensor_tensor(out=BD[:, ft, k, :],
                                                                         in0=PS[:, bBD, ft * 128:(ft + 1) * 128],
                                                                         in1=bmask[:], op=ALU.mult),
                                   pk(bBD) + ["bmask"], ["BD%d" % k])

        def even_mixer(layer):
            i = layer // 2
            P.cost.update({"pe": 115.0, "dve": 430.0, "act": 450.0})
            with ExitStack() as st:
                NCH = TM // 4
                Win = WinP
                Wout = WoutP
                Wglu = WsmP[:, :].rearrange("p (k n) -> p k n", k=4)
                XB = sb(st, "XB", [128, 4, 2, 4, 128], BF16)
                YC = sb(st, "YC", [128, 16, 2, 4, 32], BF16)
                BD = sb(st, "BD", [128, 4, 4, 128], BF16)
                TC = sb(st, "TC", [128, 16, NCH]); TS = sb(st, "TS", [128, 16, NCH])
                R4 = sb(st, "R4", [128, 16]); A4 = sb(st, "A4", [128, 16, 2])
                wT = sb(st, "wT", [128, 8, 128], BF16)
                bbc = sb(st, "bbc", [128, 4, 128])
                gsg = sb(st, "gsg", [128, 512])
                wsc = sb(st, "wsc", [128, 4, 16])
                P.dma("sp", gsg[:], sgu_norm[i:i + 1, :].broadcast_to([128, 512]), writes=["gsg"])
                for h in range(8):
                    P.dma("sp", bbc[64 * (h % 2):64 * (h % 2) + 64, h // 2, :],
                          sgu_b[i, h:h + 1, :].broadcast_to([64, 128]), writes=[("bbc", h)])
                for h in range(8):
                    P.dma("sp", wsc[64 * (h % 2):64 * (h % 2) + 64, h // 2, :].rearrange("p (a b) -> p a b", a=4),
                          sgu_w[i, h:h + 1, 0:4, 0:4].broadcast_to([64, 4, 4]), writes=[("wsc", h)])
                P.op("dve", lambda e: e.memset(s5car[:], 0.0), writes=[("s5car", q_) for q_ in range(4)])
                with ExitStack() as st2:
                    trilT = sb(st2, "trilT", [128, 128])
                    bmask = sb(st2, "bmask", [128, 128])
                    j32 = sb(st2, "bm_j32", [128, 128])
                    S4 = sb(st2, "bm_S4", [128, 128])
                    P.op("dve", lambda e: e.tensor_scalar(out=trilT[:], in0=iot[:], scalar1=iop[:, 0:1], scalar2=None,
                                                          op0=ALU.is_ge), reads=["iot", "iop"], writes=["trilT"])
                    P.op("pool", lambda e: e.iota(j32[:], pattern=[[1, 4], [0, 32]], base=0, channel_multiplier=0,
                                                  allow_small_or_imprecise_dtypes=True), writes=["j32"])
                    P.op("dve", lambda e: e.tensor_scalar(out=S4[:], in0=j32[:], scalar1=iop[:, 0:1], scalar2=None,
                                                          op0=ALU.is_equal), reads=["j32", "iop"], writes=["S4"])
                    P.op("pe", lambda e: e.matmul(PS[:, 6, 0:128], lhsT=S4[0:4, :], rhs=S4[0:4, :], start=True, stop=True),
                         reads=["S4"], writes=[("ps", 6)])
                    P.op("dve", lambda e: e.tensor_copy(out=bmask[:], in_=PS[:, 6, 0:128]), reads=[("ps", 6)], writes=["bmask"])
                    wld = [sb(st2, "wld%d" % q, [128, 128]) for q in range(2)]
                    for h in range(8):
                        wl = wld[h % 2]
                        P.dma("sp", wl[:], sgu_w[i, h], writes=["wld%d" % (h % 2)])
                        b = psget()
                        P.op("pe", lambda e, b=b, wl=wl: e.transpose(PS[:, b, 0:128], wl[:], ident[:]),
                             reads=["wld%d" % (h % 2), "ident"], writes=pk(b))
                        P.op("dve", lambda e, b=b, h=h: e.tensor_tensor(out=wT[:, h, :], in0=PS[:, b, 0:128],
                                                                        in1=trilT[:], op=ALU.mult),
                             reads=pk(b) + ["trilT"], writes=["wT"])
                    xt_ = [sb(st2, "xt%d" % q_, [128, D]) for q_ in range(2)] if layer == 0 else None
                    s5_setup(st2, i, XB, YC, BD, TC, TS, R4, A4, bmask)
                    if layer == 0:
                        load_x(xt_)
                    P.flush()
                glubh = sb(st, "glubh", [128, 4])
                P.op("dve", lambda e: e.tensor_scalar(out=glubh[:], in0=glub[:, i, :], scalar1=0.5, scalar2=None, op0=ALU.mult),
                     writes=["glubh"])
                if layer == 0:
                    load_mixer_weights(0, skip_in=True)
                P.op("dve", lambda e: e.tensor_scalar(out=WoutP[:, 0:4, :], in0=WoutP[:, 0:4, :], scalar1=0.5, scalar2=None,
                                                      op0=ALU.mult), reads=[("Wout", 0), ("Wout", 1)], writes=["Wout"])
                xnt = sb(st, "xnt", [128, 8, TM], BF16)
                uaL = [sb(st, "ua%d" % q, [128, 4, TM], BF16) for q in range(2)]
                ubL = [sb(st, "ub%d" % q, [128, 4, TM], BF16) for q in range(2)]
                vn = sb(st, "vn", [128, 512])
                vnbL = [sb(st, "vnb%d" % q, [128, 2, 512], BF16) for q in range(2)]
                vjunk = sb(st, "vjunk", [128, 512], BF16)
                vss = sb(st, "vss", [128, 2])
                ymix = sb(st, "ymix", [128, 8, TM], BF16)
                tA = sb(st, "tA", [128, 4, NCH]); tB = sb(st, "tB", [128, 4, NCH])
                Gin = sb(st, "Gin", [128, 4, 2, NCH])
                wtail = [WinP[:, k_, 1536:2048].bitcast(F32).rearrange("p (a c) -> p a c", a=4) for k_ in range(8)]
                tC, tD, tE, tF, tG, tH = wtail[0:6]
                GsL = [sb(st, "Gs0", [128, 4, 2, NCH]),
                       WinP[:, 6:8, 1536:2048].bitcast(F32).rearrange("p k (a c) -> p a k c", a=4)]
                Hf = sb(st, "Hf", [128, 4, 2, NCH + 1])
                Hb = sb(st, "Hb", [128, 4, 2, NCH], BF16)
                sqy = sb(st, "sqy", [128, TM])
                zf = sb(st, "zf", [128, 4, TM])
                zb = sb(st, "zb", [128, 4, TM], BF16)
                sg2 = sb(st, "sg2", [128, TM])
                stmp = sb(st, "stmp", [128, TM])
                vT = sb(st, "vT", [128, 4, NS])
                sacc = sb(st, "sacc", [128, 16, 4])
                h0s = sb(st, "h0s", [16, 1024])
                h0T = sb(st, "h0T", [128, 16, 2, 16])
                hend = sb(st, "hend", [128, 16, 2, 16])
                hoP = sb(st, "hoP", [16, 2, 128])
                def front(ti, t0, n, is_s):
                    nch = n // 4
                    par = ti % 2
                    ua = uaL[par]; ub = ubL[par]; vnb = vnbL[par]
                    hk = ("hres", t0)
                    rmsnorm_tile(st, "m", t0, n, gmix[:, layer, :], xnt, "xnt") if ti == 0 else \
                        rmsnorm_tile_again("m", t0, n, gmix[:, layer, :], xnt, "xnt")
                    for ft in range(4):
                        b = psget()
                        for k in range(8):
                            P.op("pe", lambda e, k=k, ft=ft, b=b: e.matmul(
                                PS[:, b, 0:n], lhsT=Win[:, k, ft * 128:(ft + 1) * 128], rhs=xnt[:, k, 0:n],
                                start=(k == 0), stop=(k == 7)), reads=["Win", "xnt"], writes=pk(b))
                        P.op("act", lambda e, ft=ft, b=b: e.activation(out=ua[:, ft, 0:n], in_=PS[:, b, 0:n],
                                                                       func=AF.Copy),
                             reads=pk(b), writes=[("ua", par, ft)])
                    for ft in range(4):
                        b = psget()
                        for k in range(8):
                            P.op("pe", lambda e, k=k, ft=ft, b=b: e.matmul(
                                PS[:, b, 0:n], lhsT=Win[:, k, 512 + ft * 128:512 + (ft + 1) * 128], rhs=xnt[:, k, 0:n],
                                start=(k == 0), stop=(k == 7)), reads=["Win", "xnt"], writes=pk(b))
                        P.op("act", lambda e, ft=ft, b=b: e.activation(out=ub[:, ft, 0:n], in_=PS[:, b, 0:n],
                                                                       func=AF.Copy),
                             reads=pk(b), writes=[("ub", par, ft)])
                    nsub = (n + 127) // 128
                    for sj in range(nsub):
                        m = min(128, n - sj * 128)
                        b = psget()
                        for k in range(8):
                            P.op("pe", lambda e, k=k, b=b, sj=sj, m=m: e.matmul(
                                PS[0:m, b, :], lhsT=xnt[:, k, sj * 128:sj * 128 + m], rhs=Win[:, k, 1024:1536],
                                start=(k == 0), stop=(k == 7)), reads=["Win", "xnt"], writes=pk(b))
                        P.op("act", lambda e, b=b, sj=sj, m=m: e.activation(
                            out=vjunk[0:m, :], in_=PS[0:m, b, :], func=AF.Square, accum_out=vss[0:m, sj:sj + 1]),
                            reads=pk(b), writes=["vjunk", ("vss", sj)])
                        P.op("act", lambda e, sj=sj, m=m: e.activation(
                            out=vss[0:m, sj:sj + 1], in_=vss[0:m, sj:sj + 1], func=AF.Ln, bias=epsc[0:m, 0:1],
                            scale=1.0 / 512.0), reads=[("vss", sj), "epsc"], writes=[("vss", sj)], cost=250.0)
                        P.op("act", lambda e, sj=sj, m=m: e.activation(
                            out=vss[0:m, sj:sj + 1], in_=vss[0:m, sj:sj + 1], func=AF.Exp, scale=-0.5),
                            reads=[("vss", sj)], writes=[("vss", sj)], cost=250.0)
                        P.op("dve", lambda e, b=b, sj=sj, m=m: e.scalar_tensor_tensor(
                            out=vnb[0:m, sj, :], in0=PS[0:m, b, :], scalar=vss[0:m, sj:sj + 1], in1=gsg[0:m, :],
                            op0=ALU.mult, op1=ALU.mult), reads=pk(b) + [("vss", sj), "gsg"], writes=[("vnb", par, sj)])
                        if is_s:
                            P.op("dve", lambda e, b=b, sj=sj, m=m: e.scalar_tensor_tensor(
                                out=vn[0:m, :], in0=PS[0:m, b, :], scalar=vss[0:m, sj:sj + 1], in1=gsg[0:m, :],
                                op0=ALU.mult, op1=ALU.mult), reads=pk(b) + [("vss", sj), "gsg"], writes=[("vn", 0)])
                    if is_s:
                        P.dma("sp", o_s_v[i], vn[0:NS, :], reads=[("vn", 0)])
                        for ri in range(2):
                            b = psget()
                            for hf in range(2):
                                P.dma("sp", h0s[:, :], (st_re if ri == 0 else st_im)[i][:, hf * 1024:(hf + 1) * 1024],
                                      writes=["h0s"])
                                for q in range(8):
                                    Pp = hf * 8 + q
                                    P.op("pe", lambda e, b=b, Pp=Pp, q=q: e.transpose(
                                        PS[:, b, Pp * 16:(Pp + 1) * 16], h0s[:, q * 128:(q + 1) * 128],
                                        ident[0:16, 0:16]), reads=["h0s", "ident"], writes=pk(b))
                            P.op("dve", lambda e, b=b, ri=ri: e.tensor_copy(
                                out=h0T[:, :, ri, :], in_=PS[:, b, 0:256].rearrange("p (a b) -> p a b", a=16)),
                                reads=pk(b), writes=["h0T"])
                def back(ti, t0, n, is_s):
                    nch = n // 4
                    par = ti % 2
                    ua = uaL[par]; ub = ubL[par]; vnb = vnbL[par]
                    for ft in range(4):
                        if not is_s:
                            P.op("pool", lambda e, ft=ft: e.tensor_copy(out=Hf[:, :, :, 0], in_=s5car[:, 4 * ft:4 * ft + 4, :]),
                                 reads=[("s5car", ft)], writes=["Hf0"])
                        b4 = psget(4)
                        for p4 in range(4):
                            for ri in range(2):
                                for s in range(4):
                                    P.op("pe", lambda e, p4=p4, ri=ri, s=s, ft=ft, b4=b4: e.matmul(
                                        PS[:, b4 + p4, ri * NCH:ri * NCH + nch],
                                        lhsT=XB[32 * p4:32 * p4 + 32, ft, ri, s, :],
                                        rhs=ua[32 * p4:32 * p4 + 32, ft, s:n:4],
                                        start=(s == 0), stop=(s == 3), tile_position=(32 * p4, 0)),
                                        reads=["XB", ("ua", par, ft)], writes=pk(b4, 4), cost=40.0)
                        Xr = PS[:, b4:b4 + 4, 0:nch]
                        Xi = PS[:, b4:b4 + 4, NCH:NCH + nch]
                        if not is_s:
                            Cc = TC[:, 4 * ft:4 * ft + 4, 0:nch]
                            Ss = TS[:, 4 * ft:4 * ft + 4, 0:nch]
                            tAa = tA[:, :, 0:nch]; tBb = tB[:, :, 0:nch]
                            GinR = Gin[:, :, 0, 0:nch]; GinI = Gin[:, :, 1, 0:nch]
                            x4 = pk(b4, 4)
                            gq = ft % 2
                            Gsq = GsL[gq]

                            def tt(out, a, bb, op, r, w, eng="dve"):
                                P.op(eng, lambda e: e.tensor_tensor(out=out, in0=a, in1=bb, op=op), reads=r, writes=w)
                            tCc = tC[:, :, 0:nch]; tDd = tD[:, :, 0:nch]
                            tt(tAa, Xr, Cc, ALU.mult, x4 + ["TC"], ["tA"])
                            tt(tBb, Xi, Ss, ALU.mult, x4 + ["TS"], ["tB"])
                            tt(tCc, Xi, Cc, ALU.mult, x4 + ["TC"], ["tC"])
                            tt(tDd, Xr, Ss, ALU.mult, x4 + ["TS"], ["tD"])
                            tt(GinR, tAa, tBb, ALU.add, ["tA", "tB"], ["GinR"])
                            tt(GinI, tCc, tDd, ALU.subtract, ["tC", "tD"], ["GinI"])
                            for p4 in range(4):
                                Pp = 4 * ft + p4
                                for ri in range(2):
                                    P.op("dve", lambda e, p4=p4, ri=ri, Pp=Pp, Gsq=Gsq: e.tensor_tensor_scan(
                                        out=Gsq[:, p4, ri, 0:nch], data0=R4[:, Pp:Pp + 1].broadcast_to([128, nch]),
                                        data1=Gin[:, p4, ri, 0:nch], initial=s5car[:, Pp, ri:ri + 1],
                                        op0=ALU.mult, op1=ALU.add),
                                        reads=["GinR" if ri == 0 else "GinI", "R4", ("s5car", ft)],
                                        writes=[("Gs", gq, p4, ri)], cost=350.0)
                            GR = Gsq[:, :, 0, 0:nch]; GI = Gsq[:, :, 1, 0:nch]
                            gk = [("Gs", gq, a_, b_) for a_ in range(4) for b_ in range(2)]
                            tEe = tE[:, :, 0:nch]; tFf = tF[:, :, 0:nch]; tGg = tG[:, :, 0:nch]; tHh = tH[:, :, 0:nch]
                            tt(tEe, GR, Cc, ALU.mult, gk + ["TC"], ["tE"], "pool")
                            tt(tFf, GI, Ss, ALU.mult, gk + ["TS"], ["tF"], "pool")
                            tt(tGg, GR, Ss, ALU.mult, gk + ["TS"], ["tG"], "pool")
                            tt(tHh, GI, Cc, ALU.mult, gk + ["TC"], ["tH"], "pool")
                            tt(Hf[:, :, 0, 1:nch + 1], tEe, tFf, ALU.subtract, ["tE", "tF", "Hf0"], ["HfR"], "pool")
                            tt(Hf[:, :, 1, 1:nch + 1], tGg, tHh, ALU.add, ["tG", "tH", "Hf0"], ["HfI"], "pool")
                            P.op("act", lambda e: e.activation(out=Hb[:, :, :, 0:nch], in_=Hf[:, :, :, 0:nch], func=AF.Copy),
                                 reads=["HfR", "HfI", "Hf0"], writes=["Hb"])
                            P.op("pool", lambda e, ft=ft: e.tensor_copy(out=s5car[:, 4 * ft:4 * ft + 4, :],
                                                                        in_=Hf[:, :, :, nch]),
                                 reads=["HfR", "HfI"] + gk, writes=[("s5car", ft)])
                        else:
                            h0r = h0T[:, 4 * ft:4 * ft + 4, 0, :]; h0i = h0T[:, 4 * ft:4 * ft + 4, 1, :]
                            a4r = A4[:, 4 * ft:4 * ft + 4, 0:1].broadcast_to([128, 4, 16])
                            a4i = A4[:, 4 * ft:4 * ft + 4, 1:2].broadcast_to([128, 4, 16])
                            tAa = tA[:, :, 0:16]; tBb = tB[:, :, 0:16]
                            x4 = pk(b4, 4)

                            def tt(out, a, bb, op, r, w):
                                P.op("dve", lambda e: e.tensor_tensor(out=out, in0=a, in1=bb, op=op), reads=r, writes=w)
                            tt(tAa, h0r, a4r, ALU.mult, ["h0T", "A4"], ["tA"])
                            tt(tBb, h0i, a4i, ALU.mult, ["h0T", "A4"], ["tB"])
                            tt(tAa, tAa, tBb, ALU.subtract, ["tA", "tB"], ["tA"])
                            tt(hend[:, 4 * ft:4 * ft + 4, 0, :], tAa, Xr, ALU.add, ["tA"] + x4, [("hend", ft, 0)])
                            tt(tAa, h0r, a4i, ALU.mult, ["h0T", "A4", ("hend", ft, 0)], ["tA"])
                            tt(tBb, h0i, a4r, ALU.mult, ["h0T", "A4", ("hend", ft, 0)], ["tB"])
                            tt(tAa, tAa, tBb, ALU.add, ["tA", "tB"], ["tA"])
                            tt(hend[:, 4 * ft:4 * ft + 4, 1, :], tAa, Xi, ALU.add, ["tA"] + x4, [("hend", ft, 1)])
                            P.op("act", lambda e, ft=ft: e.activation(out=Hb[:, :, :, 0:16],
                                                                      in_=h0T[:, 4 * ft:4 * ft + 4, :, :], func=AF.Copy),
                                 reads=["h0T"], writes=["Hb"])
                        by = psget()
                        for t in range(4):
                            o_ = PS[:, by, t * NCH:t * NCH + nch]
                            for tau in range(t + 1):
                                P.op("pe", lambda e, t=t, tau=tau, ft=ft, o_=o_: e.matmul(
                                    o_, lhsT=BD[:, ft, tau, :], rhs=ua[:, ft, (t - tau):n:4],
                                    start=(tau == 0), stop=False), reads=[("ua", par, ft)], writes=pk(by), cost=60.0)
                            for p4 in range(4):
                                for ri in range(2):
                                    last = (ri == 1)
                                    P.op("pe", lambda e, t=t, p4=p4, ri=ri, ft=ft, by=by, last=last: e.matmul(
                                        PS[32 * p4:32 * p4 + 32, by, t * NCH:t * NCH + nch],
                                        lhsT=YC[:, 4 * ft + p4, ri, t, :], rhs=Hb[:, p4, ri, 0:nch],
                                        start=False, stop=last, tile_position=(0, 32 * p4)),
                                        reads=["Hb"], writes=pk(by), cost=45.0)
                        yv = PS[:, by, 0:4 * NCH].rearrange("p (t c) -> p c t", t=4)[:, 0:nch, :]
                        sq3 = sqy[:, 0:n].rearrange("p (c t) -> p c t", t=4)
                        z3 = zf[:, ft, 0:n].rearrange("p (c t) -> p c t", t=4)
                        P.op("act", lambda e, yv=yv, z3=z3: e.activation(out=z3, in_=yv, func=AF.Gelu_apprx_tanh),
                             reads=pk(by), writes=[("zf", ft)])
                        P.op("act", lambda e, ft=ft: e.activation(out=zb[:, ft, 0:n], in_=zf[:, ft, 0:n], func=AF.Copy),
                             reads=[("zf", ft)], writes=[("zb", ft)])
                    for fo in range(4):
                        b = psget()
                        for fi in range(4):
                            P.op("pe", lambda e, fi=fi, fo=fo, b=b: e.matmul(
                                PS[:, b, 0:n], lhsT=Wglu[:, fi, fo * 128:(fo + 1) * 128], rhs=zb[:, fi, 0:n],
                                start=(fi == 0), stop=(fi == 3)), reads=["Wsm", ("Wsm", 0)] + [("zb", q) for q in range(4)],
                                writes=pk(b))
                        P.op("act", lambda e, fo=fo, b=b: e.activation(out=sg2[:, 0:n], in_=PS[:, b, 0:n], func=AF.Tanh,
                                                                       bias=glubh[:, fo:fo + 1], scale=0.5),
                             reads=pk(b) + ["glubh"], writes=["sg2"])
                        P.op("dve", lambda e, fo=fo: e.scalar_tensor_tensor(out=ymix[:, fo, 0:n], in0=sg2[:, 0:n], scalar=1.0,
                                                                            in1=zf[:, fo, 0:n], op0=ALU.add, op1=ALU.mult),
                             reads=["sg2", ("zf", fo)], writes=["ymix"])
                    if not is_s:
                        for hp in range(4):
                            b = psget()
                            for j in range(n // 128):
                                for h2 in range(2):
                                    h = 2 * hp + h2
                                    P.op("pe", lambda e, b=b, j=j, h2=h2, h=h: e.matmul(
                                        PS[64 * h2:64 * h2 + 64, b, j * 128:(j + 1) * 128],
                                        lhsT=vnb[:, j, h * 64:(h + 1) * 64], rhs=wT[:, h, :],
                                        start=True, stop=True, tile_position=(0, 64 * h2)),
                                        reads=["wT", ("vnb", par, j)], writes=pk(b))
                            P.op("dve", lambda e, b=b, hp=hp: e.tensor_tensor(
                                out=stmp[:, 0:n].rearrange("p (j i) -> p j i", i=128),
                                in0=PS[:, b, 0:n].rearrange("p (j i) -> p j i", i=128),
                                in1=bbc[:, hp, :].unsqueeze(1).broadcast_to([128, n // 128, 128]), op=ALU.add),
                                reads=pk(b) + ["bbc"], writes=["stmp"])
                            P.op("dve", lambda e, hp=hp: e.tensor_tensor(out=ymix[:, 4 + hp, 0:n], in0=stmp[:, 0:n],
                                                                         in1=ub[:, hp, 0:n], op=ALU.mult),
                                 reads=["stmp", ("ub", par, hp)], writes=["ymix"])
                    else:
                        b = psget()
                        for ft in range(4):
                            P.op("pe", lambda e, b=b, ft=ft: e.transpose(PS[:, b, ft * NS:(ft + 1) * NS],
                                                                         vn[0:NS, ft * 128:(ft + 1) * 128],
                                                                         ident[0:NS, 0:NS]),
                                 reads=[("vn", 0), "ident"], writes=pk(b))
                        P.op("dve", lambda e, b=b: e.tensor_copy(out=vT[:], in_=PS[:, b, 0:4 * NS].rearrange("p (f t) -> p f t", f=4)),
                             reads=pk(b), writes=["vT"])
                        for ft in range(4):
                            v3 = vT[:, ft, :].rearrange("p (b j) -> p b j", j=4)
                            for ii in range(4):
                                P.op("dve", lambda e, ft=ft, ii=ii, v3=v3: e.tensor_scalar(
                                    out=sacc[:, :, ii], in0=v3[:, :, 0], scalar1=wsc[:, ft, 4 * ii:4 * ii + 1],
                                    scalar2=bbc[:, ft, ii:ii + 1], op0=ALU.mult, op1=ALU.add),
                                    reads=["vT", "wsc", "bbc"], writes=["sacc"])
                                for jj in range(1, ii + 1):
                                    P.op("dve", lambda e, ft=ft, ii=ii, jj=jj, v3=v3: e.scalar_tensor_tensor(
                                        out=sacc[:, :, ii], in0=v3[:, :, jj], scalar=wsc[:, ft, 4 * ii + jj:4 * ii + jj + 1],
                                        in1=sacc[:, :, ii], op0=ALU.mult, op1=ALU.add),
                                        reads=["vT", "wsc", "sacc"], writes=["sacc"])
                            P.op("dve", lambda e, ft=ft: e.tensor_tensor(
                                out=ymix[:, 4 + ft, 0:NS], in0=sacc[:].rearrange("p b i -> p (b i)"),
                                in1=ub[:, ft, 0:NS], op=ALU.mult), reads=["sacc", ("ub", par, ft)], writes=["ymix"])
                    out_proj_tile(Wout, "Wout", ymix, "ymix", t0, n)
                    if is_s:
                        for ri in range(2):
                            for hf in range(2):
                                for h2 in range(2):
                                    half = hf * 2 + h2
                                    b = psget()
                                    for q in range(4):
                                        Pp = half * 4 + q
                                        P.op("pe", lambda e, b=b, q=q, Pp=Pp, ri=ri: e.transpose(
                                            PS[0:16, b, q * 128:(q + 1) * 128], hend[:, Pp, ri, :], ident[:]),
                                            reads=[("hend", Pp // 4, ri), "ident"], writes=pk(b))
                                    P.op("act", lambda e, b=b, h2=h2: e.activation(
                                        out=h0s[:, h2 * 512:(h2 + 1) * 512], in_=PS[0:16, b, :], func=AF.Copy),
                                        reads=pk(b), writes=["h0s"])
                                P.dma("sp", (o_s_re if ri == 0 else o_s_im)[i][:, hf * 1024:(hf + 1) * 1024], h0s[:, :],
                                      reads=["h0s"])
                    if (not is_s) and t0 + n == SEQ:
                        for ri in range(2):
                            b = psget()
                            P.op("pe", lambda e, b=b, ri=ri: e.transpose(PS[0:16, b, 0:128], s5car[:, :, ri], ident[:]),
                                 reads=[("s5car", q_) for q_ in range(4)] + ["ident"], writes=pk(b))
                            P.op("act", lambda e, b=b, ri=ri: e.activation(out=hoP[:, ri, :], in_=PS[0:16, b, 0:128], func=AF.Copy),
                                 reads=pk(b), writes=["hoP"])
                        P.dma("sp", o_p_re[i], hoP[:, 0, :], reads=["hoP"])
                        P.dma("sp", o_p_im[i], hoP[:, 1, :], reads=["hoP"])
                seq = list(enumerate(mtiles))
                for idx, (ti, (t0, n, is_s)) in enumerate(seq):
                    front(ti, t0, n, is_s)
                    if idx >= 1:
                        pti, (pt0, pn, ps_) = seq[idx - 1]
                        back(pti, pt0, pn, ps_)
                lti, (lt0, ln, ls_) = seq[-1]
                back(lti, lt0, ln, ls_)
                P.flush()

        _norm_scr = {}

        def rmsnorm_tile_again(tag, t0, n, gvec, xn_out, xn_key):
            _rms_ops(tag, t0, n, gvec, xn_out, xn_key, _norm_scr[tag])

        def _rms_ops(tag, t0, n, gvec, xn_out, xn_key, srs, xoff=0):
            sr, sr2 = srs
            hk = hkeys(t0, n)
            sqv = xn_out[:, :, xoff:xoff + n]
            P.op("act", lambda e: e.activation(out=sqv, in_=hres[:, :, t0:t0 + n], func=AF.Square),
                 reads=hk, writes=[xn_key])
            b = psget()
            for k in range(8):
                P.op("pe", lambda e, k=k, b=b: e.matmul(PS[:, b, 0:n], lhsT=onesb[:], rhs=xn_out[:, k, xoff:xoff + n],
                                                        start=(k == 0), stop=(k == 7)),
                     reads=[xn_key, "onesb"], writes=pk(b))
            P.op("act", lambda e, b=b: e.activation(out=sr[:, 0:n], in_=PS[:, b, 0:n], func=AF.Ln,
                                                    bias=epsc[:, 0:1], scale=1.0 / D),
                 reads=pk(b) + ["epsc"], writes=["sr_" + tag])
            P.op("act", lambda e: e.activation(out=sr[:, 0:n], in_=sr[:, 0:n], func=AF.Exp, scale=-0.5),
                 reads=["sr_" + tag], writes=["sr_" + tag])
            for k in range(8):
                P.op("dve", lambda e, k=k: e.scalar_tensor_tensor(
                    out=xn_out[:, k, xoff:xoff + n], in0=hres[:, k, t0:t0 + n], scalar=gvec[:, k:k + 1],
                    in1=sr2[:, 0:n], op0=ALU.mult, op1=ALU.mult),
                    reads=hk + ["sr_" + tag, "gmix", "gffn"], writes=[xn_key])

        def rmsnorm_tile(stk, tag, t0, n, gvec, xn_out, xn_key, xoff=0):
            nmax = TM if tag == "m" else TF
            sr = sb(stk, "sr_" + tag, [128, nmax])
            sr2 = sr
            _norm_scr[tag] = (sr, sr2)
            _rms_ops(tag, t0, n, gvec, xn_out, xn_key, (sr, sr2), xoff=xoff)

        def out_proj_tile(Wout, wkey, ymix, ykey, t0, n):
            hk = hkeys(t0, n)
            for fo in range(8):
                b = psget()
                for k in range(8):
                    P.op("pe", lambda e, k=k, fo=fo, b=b: e.matmul(
                        PS[:, b, 0:n], lhsT=Wout[:, k, fo * 128:(fo + 1) * 128], rhs=ymix[:, k, 0:n],
                        start=(k == 0), stop=(k == 7)), reads=[wkey, ykey], writes=pk(b))
                P.op("dve", lambda e, fo=fo, b=b: e.tensor_tensor(
                    out=hres[:, fo, t0:t0 + n], in0=hres[:, fo, t0:t0 + n], in1=PS[:, b, 0:n], op=ALU.add),
                    reads=pk(b) + hk, writes=hk)

        def odd_mixer(layer):
            i = layer // 2
            P.cost.update({"pe": 115.0, "dve": 430.0, "act": 450.0})
            with ExitStack() as st:
                Win = WinP
                Wout = WoutP
                Wp = WsmP[:, 0:512].rearrange("p (g d) -> p g d", g=4)
                xnt = sb(st, "xnto", [128, 8, TM], BF16)
                XCL = [sb(st, "XC%d" % q, [128, 4, 15 + TM]) for q in range(2)]
                PA = sb(st, "PA", [128, 15 + TM]); PB = sb(st, "PB", [128, 15 + TM])
                diff = sb(st, "diff", [128, 4, TM], BF16)
                xdL = [sb(st, "xd%d" % q, [128, 4, TM]) for q in range(2)]
                bgL = [sb(st, "bg%d" % q, [128, 4, TM]) for q in range(2)]
                ZL = [sb(st, "Z%d" % q, [128, 4, 2 + TM]) for q in range(2)]
                ca = sb(st, "ca", [128, TM])
                ymix = sb(st, "ymixo", [128, 8, TM], BF16)
                invn = sb(st, "invn", [128, 4, 15])
                XCs = sb(st, "XCs", [128, 4, 16, 19])
                PAs = sb(st, "PAs", [128, 16, 19]); PBs = sb(st, "PBs", [128, 16, 19])
                Zs = sb(st, "Zs", [128, 4, 16, 6])
                spl = [sb(st, "spl%d" % q, [128, 512]) for q in range(2)]
                scl = sb(st, "scl", [32, 512])
                otp = sb(st, "otp", [128, 512])
                otc = sb(st, "otc", [32, 512])
                opp = sb(st, "opp", [16, 512])
                opc = sb(st, "opc", [2, 512])
                xct = sb(st, "xct", [128, 128])
                zct = sb(st, "zct", [128, 32])
                P.op("pool", lambda e: e.iota(invn[:], pattern=[[0, 4], [1, 15]], base=1, channel_multiplier=0,
                                              allow_small_or_imprecise_dtypes=True), writes=["invn"])
                for gi in range(4):
                    P.op("dve", lambda e, gi=gi: e.tensor_scalar(out=invn[:, gi, :], in0=invn[:, gi, :],
                                                                 scalar1=float(2 ** (gi + 1)), scalar2=None, op0=ALU.min),
                         reads=["invn"], writes=["invn"])
                P.op("dve", lambda e: e.reciprocal(out=invn[:], in_=invn[:]), reads=["invn"], writes=["invn"])
                P.op("dve", lambda e: e.memset(xchalo[:], 0.0), writes=["xchalo"])
                P.op("dve", lambda e: e.memset(zhalo[:], 0.0), writes=["zhalo"])
                P.op("pool", lambda e: e.memset(PA[:], 0.0), writes=["P0"])
                P.op("pool", lambda e: e.memset(PB[:], 0.0), writes=["P1"])
                P.op("pool", lambda e: e.memset(PAs[:], 0.0), writes=["Ps0"])
                P.op("pool", lambda e: e.memset(PBs[:], 0.0), writes=["Ps1"])
                def front(ti, t0, n, is_s):
                    par = ti % 2
                    XC = XCL[par]; xd = xdL[par]; bg = bgL[par]; Z = ZL[par]
                    if ti == 0:
                        rmsnorm_tile(st, "m", t0, n, gmix[:, layer, :], xnt, "xnt")
                    else:
                        rmsnorm_tile_again("m", t0, n, gmix[:, layer, :], xnt, "xnt")
                    if not is_s:
                        P.op("dve", lambda e: e.tensor_copy(out=XC[:, :, 0:15], in_=xchalo[:]), reads=["xchalo"],
                             writes=[("XChalo", par)])
                        P.op("dve", lambda e: e.tensor_copy(out=Z[:, :, 0:2], in_=zhalo[:]), reads=["zhalo"],
                             writes=[("Zhalo", par)])
                    else:
                        P.dma("sp", spl[0][:], st_pool[i].rearrange("b r c -> (b r) c")[0:128, :], writes=["spl0"])
                        P.dma("sp", spl[1][0:112, :], st_pool[i].rearrange("b r c -> (b r) c")[128:240, :], writes=["spl1"])
                        P.dma("sp", scl[:], st_conv[i].rearrange("b r c -> (b r) c"), writes=["scl"])
                        for ft in range(4):
                            b = psget()
                            P.op("pe", lambda e, b=b, ft=ft: e.transpose(PS[:, b, 0:128], spl[0][:, ft * 128:(ft + 1) * 128], ident[:]),
                                 reads=["spl0", "ident"], writes=pk(b))
                            P.op("pe", lambda e, b=b, ft=ft: e.transpose(PS[:, b, 128:240], spl[1][0:112, ft * 128:(ft + 1) * 128],
                                                                         ident[0:112, 0:112]),
                                 reads=["spl1", "ident"], writes=pk(b))
                            P.op("pe", lambda e, b=b, ft=ft: e.transpose(PS[:, b, 256:288], scl[:, ft * 128:(ft + 1) * 128],
                                                                         ident[0:32, 0:32]),
                                 reads=["scl", "ident"], writes=pk(b))
                            P.op("dve", lambda e, b=b, ft=ft: e.tensor_copy(
                                out=XCs[:, ft, :, 0:15], in_=PS[:, b, 0:240].rearrange("p (b r) -> p b r", r=15)),
                                reads=pk(b), writes=[("XCs", ft)])
                            P.op("dve", lambda e, b=b, ft=ft: e.tensor_copy(
                                out=Zs[:, ft, :, 0:2], in_=PS[:, b, 256:288].rearrange("p (b r) -> p b r", r=2)),
                                reads=pk(b), writes=[("Zs", ft)])
                    for ft in range(4):
                        b = psget()
                        for k in range(8):
                            P.op("pe", lambda e, k=k, ft=ft, b=b: e.matmul(
                                PS[:, b, 0:n], lhsT=Win[:, k, ft * 128:(ft + 1) * 128], rhs=xnt[:, k, 0:n],
                                start=(k == 0), stop=(k == 7)), reads=["Win", "xnt"], writes=pk(b))
                        if not is_s:
                            P.op("act", lambda e, ft=ft, b=b: e.activation(out=XC[:, ft, 15:15 + n], in_=PS[:, b, 0:n], func=AF.Copy),
                                 reads=pk(b), writes=[("XC", par, ft)])
                        else:
                            P.op("act", lambda e, ft=ft, b=b: e.activation(
                                out=XCs[:, ft, :, 15:19], in_=PS[:, b, 0:NS].rearrange("p (b t) -> p b t", t=4), func=AF.Copy),
                                reads=pk(b) + [("XCs", ft)], writes=[("XCs", ft)])
                    for ft in range(4):
                        b = psget()
                        for k in range(8):
                            P.op("pe", lambda e, k=k, ft=ft, b=b: e.matmul(
                                PS[:, b, 0:n], lhsT=Win[:, k, 512 + ft * 128:512 + (ft + 1) * 128], rhs=xnt[:, k, 0:n],
                                start=(k == 0), stop=(k == 7)), reads=["Win", "xnt"], writes=pk(b))
                        P.op("act", lambda e, ft=ft, b=b: e.activation(out=xd[:, ft, 0:n], in_=PS[:, b, 0:n], func=AF.Copy),
                             reads=pk(b), writes=[("xd", par, ft)])
                    for ft in range(4):
                        b = psget()
                        for k in range(8):
                            P.op("pe", lambda e, k=k, ft=ft, b=b: e.matmul(
                                PS[:, b, 0:n], lhsT=Win[:, k, 1024 + ft * 128:1024 + (ft + 1) * 128], rhs=xnt[:, k, 0:n],
                                start=(k == 0), stop=(k == 7)), reads=["Win", "xnt"], writes=pk(b))
                        P.op("act", lambda e, ft=ft, b=b: e.activation(out=bg[:, ft, 0:n], in_=PS[:, b, 0:n], func=AF.Copy),
                             reads=pk(b), writes=[("bg", par, ft)])
                    for ft in range(4):
                        b = psget()
                        for k in range(8):
                            P.op("pe", lambda e, k=k, ft=ft, b=b: e.matmul(
                                PS[:, b, 0:n], lhsT=Win[:, k, 1536 + ft * 128:1536 + (ft + 1) * 128], rhs=xnt[:, k, 0:n],
                                start=(k == 0), stop=(k == 7)), reads=["Win", "xnt"], writes=pk(b))
                        if not is_s:
                            P.op("dve", lambda e, ft=ft, b=b: e.tensor_tensor(out=Z[:, ft, 2:2 + n], in0=PS[:, b, 0:n],
                                                                              in1=xd[:, ft, 0:n], op=ALU.mult),
                                 reads=pk(b) + [("xd", par, ft)], writes=[("Z", par, ft)])
                        else:
                            P.op("dve", lambda e, ft=ft, b=b: e.tensor_tensor(
                                out=Zs[:, ft, :, 2:6], in0=PS[:, b, 0:NS].rearrange("p (b t) -> p b t", t=4),
                                in1=xd[:, ft, 0:NS].rearrange("p (b t) -> p b t", t=4), op=ALU.mult),
                                reads=pk(b) + [("xd", par, ft), ("Zs", ft)], writes=[("Zs", ft)])
                    if not is_s:
                        P.op("dve", lambda e: e.tensor_copy(out=xchalo[:], in_=XC[:, :, n:n + 15]),
                             reads=[("XC", par, q) for q in range(4)] + [(("XChalo", par), par)], writes=["xchalo"])
                        P.op("dve", lambda e: e.tensor_copy(out=zhalo[:], in_=Z[:, :, n:n + 2]),
                             reads=[("Z", par, q) for q in range(4)] + [(("Zhalo", par), par)], writes=["zhalo"])
                def back(ti, t0, n, is_s):
                    par = ti % 2
                    XC = XCL[par]; xd = xdL[par]; bg = bgL[par]; Z = ZL[par]
                    for gi in range(4):
                        w = 2 ** (gi + 1)
                        if not is_s:
                            L = 15 + n
                            src = XC[:, gi, 0:L]
                            bufs = [PA, PB]
                            cur = src
                            ckey = [("XC", par, gi), ("XChalo", par)]
                            d = 1
                            q = 0
                            while d < w:
                                dst = bufs[q % 2]
                                dk = "P%d" % (q % 2)
                                P.op("pool", lambda e, cur=cur, dst=dst, d=d, L=L: e.tensor_tensor(
                                    out=dst[:, d:L], in0=cur[:, d:L], in1=cur[:, 0:L - d], op=ALU.add),
                                    reads=ckey, writes=[dk], cost=800.0)
                                cur = dst[:, 0:L]
                                ckey = [dk]
                                d *= 2
                                q += 1
                            P.op("dve", lambda e, cur=cur, gi=gi, w=w: e.scalar_tensor_tensor(
                                out=diff[:, gi, 0:n], in0=cur[:, 15:15 + n], scalar=1.0 / w, in1=XC[:, gi, 15:15 + n],
                                op0=ALU.mult, op1=ALU.subtract), reads=ckey + [("XC", par, gi)], writes=[("diff", gi)])
                            if t0 == 0:
                                P.op("dve", lambda e, cur=cur, gi=gi: e.tensor_tensor(
                                    out=ca[:, 0:15], in0=cur[:, 15:30], in1=invn[:, gi, :], op=ALU.mult),
                                    reads=ckey + ["invn"], writes=["ca"])
                                P.op("dve", lambda e, gi=gi: e.tensor_tensor(
                                    out=diff[:, gi, 0:15], in0=ca[:, 0:15], in1=XC[:, gi, 15:30], op=ALU.subtract),
                                    reads=["ca", ("XC", par, gi), ("diff", gi)], writes=[("diff", gi)])
                        else:
                            L = 19
                            cur = XCs[:, gi, :, :]
                            ckey = [("XCs", gi)]
                            bufs = [PAs, PBs]
                            d = 1
                            q = 0
                            while d < w:
                                dst = bufs[q % 2]
                                dk = "Ps%d" % (q % 2)
                                P.op("pool", lambda e, cur=cur, dst=dst, d=d: e.tensor_tensor(
                                    out=dst[:, :, d:19], in0=cur[:, :, d:19], in1=cur[:, :, 0:19 - d], op=ALU.add),
                                    reads=ckey, writes=[dk], cost=800.0)
                                cur = dst[:, :, :]
                                ckey = [dk]
                                d *= 2
                                q += 1
                            P.op("dve", lambda e, cur=cur, gi=gi, w=w: e.scalar_tensor_tensor(
                                out=diff[:, gi, 0:NS].rearrange("p (b t) -> p b t", t=4), in0=cur[:, :, 15:19],
                                scalar=1.0 / w, in1=XCs[:, gi, :, 15:19], op0=ALU.mult, op1=ALU.subtract),
                                reads=ckey + [("XCs", gi)], writes=[("diff", gi)])
                        b = psget()
                        P.op("pe", lambda e, gi=gi, b=b: e.matmul(PS[:, b, 0:n], lhsT=Wp[:, gi, :], rhs=diff[:, gi, 0:n],
                                                                  start=True, stop=True),
                             reads=["Wsm", ("diff", gi)], writes=pk(b))
                        P.op("act", lambda e, gi=gi, b=b: e.activation(out=ymix[:, gi, 0:n], in_=PS[:, b, 0:n], func=AF.Copy,
                                                                       scale=pscale[:, i, gi:gi + 1]),
                             reads=pk(b) + ["pscale"], writes=["ymix"])
                    for ft in range(4):
                        if not is_s:
                            z0 = Z[:, ft, 0:n]; z1 = Z[:, ft, 1:n + 1]; z2 = Z[:, ft, 2:n + 2]
                            cav = ca[:, 0:n]
                            bgv = bg[:, ft, 0:n]
                            yv = ymix[:, 4 + ft, 0:n]
                            zk = [("Z", par, ft), ("Zhalo", par)]
                        else:
                            z0 = Zs[:, ft, :, 0:4]; z1 = Zs[:, ft, :, 1:5]; z2 = Zs[:, ft, :, 2:6]
                            cav = ca[:, 0:NS].rearrange("p (b t) -> p b t", t=4)
                            bgv = bg[:, ft, 0:NS].rearrange("p (b t) -> p b t", t=4)
                            yv = ymix[:, 4 + ft, 0:NS].rearrange("p (b t) -> p b t", t=4)
                            zk = [("Zs", ft)]
                        P.op("dve", lambda e, ft=ft, z0=z0, cav=cav: e.tensor_scalar(
                            out=cav, in0=z0, scalar1=cw[:, i, 0, ft:ft + 1], scalar2=cb[:, i, ft:ft + 1],
                            op0=ALU.mult, op1=ALU.add), reads=zk + ["cw", "cb"], writes=["ca"])
                        P.op("dve", lambda e, ft=ft, z1=z1, cav=cav: e.scalar_tensor_tensor(
                            out=cav, in0=z1, scalar=cw[:, i, 1, ft:ft + 1], in1=cav, op0=ALU.mult, op1=ALU.add),
                            reads=zk + ["cw", "ca"], writes=["ca"])
                        P.op("dve", lambda e, ft=ft, z2=z2, cav=cav: e.scalar_tensor_tensor(
                            out=cav, in0=z2, scalar=cw[:, i, 2, ft:ft + 1], in1=cav, op0=ALU.mult, op1=ALU.add),
                            reads=zk + ["cw", "ca"], writes=["ca"])
                        P.op("dve", lambda e, cav=cav, bgv=bgv, yv=yv: e.tensor_tensor(out=yv, in0=cav, in1=bgv, op=ALU.mult),
                             reads=["ca", ("bg", par, ft)], writes=["ymix"])
                    out_proj_tile(Wout, "Wout", ymix, "ymix", t0, n)
                    if (not is_s) and t0 + n == SEQ:
                        for ft in range(4):
                            b = psget()
                            P.op("pe", lambda e, b=b, ft=ft: e.transpose(PS[0:15, b, 0:128], xchalo[:, ft, :], ident[:]),
                                 reads=["xchalo", "ident"], writes=pk(b))
                            P.op("pe", lambda e, b=b, ft=ft: e.transpose(PS[0:2, b, 128:256], zhalo[:, ft, :], ident[:]),
                                 reads=["zhalo", "ident"], writes=pk(b))
                            P.op("act", lambda e, b=b, ft=ft: e.activation(out=opp[0:15, ft * 128:(ft + 1) * 128],
                                                                           in_=PS[0:15, b, 0:128], func=AF.Copy),
                                 reads=pk(b), writes=["opp"])
                            P.op("act", lambda e, b=b, ft=ft: e.activation(out=opc[0:2, ft * 128:(ft + 1) * 128],
                                                                           in_=PS[0:2, b, 128:256], func=AF.Copy),
                                 reads=pk(b), writes=["opc"])
                        P.dma("sp", o_p_pool[i], opp[0:15, :], reads=["opp"])
                        P.dma("sp", o_p_conv[i], opc[0:2, :], reads=["opc"])
                    if is_s:
                        for half in range(2):
                            for ft in range(4):
                                P.op("dve", lambda e, ft=ft, half=half: e.tensor_copy(
                                    out=xct[:, 0:120].rearrange("p (b r) -> p b r", r=15),
                                    in_=XCs[:, ft, half * 8:half * 8 + 8, 4:19]), reads=[("XCs", ft)], writes=["xct"])
                                b = psget()
                                P.op("pe", lambda e, b=b: e.transpose(PS[0:120, b, 0:128], xct[:, 0:120], ident[:]),
                                     reads=["xct", "ident"], writes=pk(b))
                                P.op("act", lambda e, b=b, ft=ft: e.activation(out=otp[0:120, ft * 128:(ft + 1) * 128],
                                                                               in_=PS[0:120, b, 0:128], func=AF.Copy),
                                     reads=pk(b), writes=["otp"])
                            P.dma("sp", o_s_pool[i, half * 120:half * 120 + 120, :], otp[0:120, :], reads=["otp"])
                        for ft in range(4):
                            P.op("dve", lambda e, ft=ft: e.tensor_copy(
                                out=zct[:, 0:32].rearrange("p (b r) -> p b r", r=2), in_=Zs[:, ft, :, 4:6]),
                                reads=[("Zs", ft)], writes=["zct"])
                            b = psget()
                            P.op("pe", lambda e, b=b: e.transpose(PS[0:32, b, 0:128], zct[:, 0:32], ident[:]),
                                 reads=["zct", "ident"], writes=pk(b))
                            P.op("act", lambda e, b=b, ft=ft: e.activation(out=otc[:, ft * 128:(ft + 1) * 128],
                                                                           in_=PS[0:32, b, 0:128], func=AF.Copy),
                                 reads=pk(b), writes=["otc"])
                        P.dma("sp", o_s_conv[i], otc[:, :], reads=["otc"])
                seq = list(enumerate(mtiles))
                for idx, (ti, (t0, n, is_s)) in enumerate(seq):
                    front(ti, t0, n, is_s)
                    if idx >= 1:
                        pti, (pt0, pn, ps_) = seq[idx - 1]
                        back(pti, pt0, pn, ps_)
                lti, (lt0, ln, ls_) = seq[-1]
                back(lti, lt0, ln, ls_)
                P.flush()

        def epilogue(st):
            gfin = WinP[:, 2, :].bitcast(F32)
            P.dma("sp", gfin, norm_final.broadcast_to([128, D]), writes=["gfin"])
            junk = sb(st, "fjunk", [128, 512], BF16)
            ss = sb(st, "fss", [128, 2, 2])
            yo = [WinP[:, q, :].bitcast(F32) for q in range(2)]
            nsub = SEQ // 128 + 1
            for si in range(nsub):
                n = 128 if si < SEQ // 128 else NS
                dst = y_p[si * 128:(si + 1) * 128, :] if si < SEQ // 128 else y_s[:, :]
                par = si % 2
                yb = yo[par]
                yk = "yo%d" % par
                hk = hkeys(si * 128, n)
                bb = psget(2)
                for k in range(8):
                    P.op("pe", lambda e, k=k, bb=bb, si=si, n=n: e.transpose(
                        PS[0:n, bb + k // 4, (k % 4) * 128:(k % 4 + 1) * 128], hres[:, k, si * 128:si * 128 + n], ident[:]),
                        reads=["ident"] + hk, writes=pk(bb, 2), cost=110.0)
                for half in range(2):
                    P.op("act", lambda e, bb=bb, half=half, n=n, par=par: e.activation(
                        out=junk[0:n, :], in_=PS[0:n, bb + half, :], func=AF.Square, accum_out=ss[0:n, par, half:half + 1]),
                        reads=pk(bb, 2), writes=["fjunk", ("fss", par, half)])
                P.op("dve", lambda e, n=n, par=par: e.tensor_tensor(out=ss[0:n, par, 0:1], in0=ss[0:n, par, 0:1],
                                                                     in1=ss[0:n, par, 1:2], op=ALU.add),
                     reads=[("fss", par, 0), ("fss", par, 1)], writes=[("fss", par, 0)], cost=100.0)
                P.op("act", lambda e, n=n, par=par: e.activation(out=ss[0:n, par, 0:1], in_=ss[0:n, par, 0:1], func=AF.Sqrt,
                                                                  bias=epsc[0:n, 0:1], scale=1.0 / D),
                     reads=[("fss", par, 0)], writes=[("fss", par, 0)], cost=250.0)
                P.op("dve", lambda e, n=n, par=par: e.reciprocal(out=ss[0:n, par, 0:1], in_=ss[0:n, par, 0:1]),
                     reads=[("fss", par, 0)], writes=[("fss", par, 0)], cost=100.0)
                for half in range(2):
                    P.op("dve", lambda e, bb=bb, half=half, n=n, yb=yb, par=par: e.scalar_tensor_tensor(
                        out=yb[0:n, half * 512:(half + 1) * 512], in0=PS[0:n, bb + half, :], scalar=ss[0:n, par, 0:1],
                        in1=gfin[0:n, half * 512:(half + 1) * 512], op0=ALU.mult, op1=ALU.mult),
                        reads=pk(bb, 2) + [("fss", par, 0), "gfin"], writes=[yk], cost=750.0)
                P.dma("sp", dst, yb[0:n, :], reads=[yk])

        def ffn(layer):
            widths = [384] * 7 + [128]
            offs = [sum(widths[:j]) for j in range(len(widths))]
            with ExitStack() as st:
                xn = sb(st, "xn_all", [128, 8, T], BF16)
                Wg = [sb(st, "Wg%d" % q, [128, 8, 384], BF16) for q in range(2)]
                Wu = [sb(st, "Wu%d" % q, [128, 8, 384], BF16) for q in range(2)]
                Wd = [sb(st, "Wd%d" % q, [128, 3, D], BF16) for q in range(2)]
                sl = [sb(st, "sl%d" % q, [128, TF]) for q in range(2)]
                hb = [sb(st, "hb%d" % q, [128, 3, TF], BF16) for q in range(2)]

                P.cost.update({"pe": 195.0, "dve": 630.0, "act": 560.0})

                def load_slice(j):
                    q = j % 2
                    w = widths[j]
                    o = offs[j]
                    c = 2500.0 + 128 * 8 * w * 4 / 150.0
                    for kh in range(2):
                        P.dma("pool", Wg[q][:, 4 * kh:4 * kh + 4, 0:w],
                              ffn_g[layer].rearrange("(k p) n -> p k n", p=128)[:, 4 * kh:4 * kh + 4, o:o + w],
                              writes=[("Wg", q, kh)], cost=c / 2)
                    for kh in range(2):
                        P.dma("pool", Wu[q][:, 4 * kh:4 * kh + 4, 0:w],
                              ffn_u[layer].rearrange("(k p) n -> p k n", p=128)[:, 4 * kh:4 * kh + 4, o:o + w],
                              writes=[("Wu", q, kh)], cost=c / 2)
                    P.dma("pool", Wd[q][:, 0:w // 128, :],
                          ffn_d[layer].rearrange("(k p) n -> p k n", p=128)[:, o // 128:(o + w) // 128, :],
                          writes=[("Wd", q)], cost=c)
                load_slice(0)
                load_slice(1)
                if layer + 1 < 4:
                    load_mixer_weights(layer + 1)
                for ti, (t0, n, is_s) in enumerate(ftiles):
                    if ti == 0:
                        rmsnorm_tile(st, "f", t0, n, gffn[:, layer, :], xn, ("xn", t0), xoff=t0)
                    else:
                        _rms_ops("f", t0, n, gffn[:, layer, :], xn, ("xn", t0), _norm_scr["f"], xoff=t0)
                hbi = 0
                for j in range(len(widths)):
                    q = j % 2
                    nhc = widths[j] // 128
                    for (t0, n, is_s) in ftiles:
                        hk = hkeys(t0, n)
                        hbuf = hb[hbi % 2]
                        hkey = "hb%d" % (hbi % 2)
                        hbi += 1
                        for hc in range(nhc):
                            bgt = psget()
                            for k in range(8):
                                P.op("pe", lambda e, k=k, hc=hc, bgt=bgt, q=q, t0=t0, n=n: e.matmul(
                                    PS[:, bgt, 0:n], lhsT=Wg[q][:, k, hc * 128:(hc + 1) * 128], rhs=xn[:, k, t0:t0 + n],
                                    start=(k == 0), stop=(k == 7)), reads=[("Wg", q, k // 4), ("xn", t0)], writes=pk(bgt),
                                    cost=n / 2.35 + 6)
                            but = psget()
                            for k in range(8):
                                P.op("pe", lambda e, k=k, hc=hc, but=but, q=q, t0=t0, n=n: e.matmul(
                                    PS[:, but, 0:n], lhsT=Wu[q][:, k, hc * 128:(hc + 1) * 128], rhs=xn[:, k, t0:t0 + n],
                                    start=(k == 0), stop=(k == 7)), reads=[("Wu", q, k // 4), ("xn", t0)], writes=pk(but),
                                    cost=n / 2.35 + 6)
                            slt = sl[hc % 2]
                            slk = "sl%d" % (hc % 2)
                            P.op("act", lambda e, bgt=bgt, slt=slt, n=n: e.activation(out=slt[:, 0:n], in_=PS[:, bgt, 0:n], func=AF.Silu),
                                 reads=pk(bgt), writes=[slk], cost=(224 + n) / 1.2)
                            P.op("dve", lambda e, but=but, slt=slt, hbuf=hbuf, hc=hc, n=n: e.tensor_tensor(
                                out=hbuf[:, hc, 0:n], in0=slt[:, 0:n], in1=PS[:, but, 0:n], op=ALU.mult),
                                reads=pk(but) + [slk], writes=[(hkey, hc)], cost=(160 + n) / 0.96)
                        for fo in range(8):
                            b = psget()
                            for hc in range(nhc):
                                P.op("pe", lambda e, hc=hc, fo=fo, b=b, q=q, hbuf=hbuf, n=n, nhc=nhc: e.matmul(
                                    PS[:, b, 0:n], lhsT=Wd[q][:, hc, fo * 128:(fo + 1) * 128], rhs=hbuf[:, hc, 0:n],
                                    start=(hc == 0), stop=(hc == nhc - 1)), reads=[("Wd", q), (hkey, hc)], writes=pk(b),
                                    cost=n / 2.35 + 6)
                            P.op("dve", lambda e, fo=fo, b=b, t0=t0, n=n: e.tensor_tensor(
                                out=hres[:, fo, t0:t0 + n], in0=hres[:, fo, t0:t0 + n], in1=PS[:, b, 0:n], op=ALU.add),
                                reads=pk(b) + hk, writes=hk, cost=(160 + n) / 0.96)
                    if j + 2 < len(widths):
                        load_slice(j + 2)
                if layer == 3:
                    epilogue(st)
                P.flush()

        for layer in range(4):
            if layer > 0:
                P.next_epoch()
            if layer % 2 == 0:
                even_mixer(layer)
            else:
                odd_mixer(layer)
            ffn(layer)

    return nc


_NC_CACHE = {}


def kernel(**inputs):
    f = lambda a: np.ascontiguousarray(np.asarray(a, dtype=np.float32))
    inp = {k: f(v) for k, v in inputs.items()}
    if "nc" not in _NC_CACHE:
        _NC_CACHE["nc"] = build_nc()
    nc = _NC_CACHE["nc"]
    shared = {}
    for k in ("norm_mix", "norm_ffn", "w_in_even", "w_out_even", "s5_lambda_re", "s5_lambda_im", "s5_log_dt",
              "s5_b_re", "s5_b_im", "s5_c_re", "s5_c_im", "s5_glu_w", "s5_glu_b", "sgu_norm", "sgu_w", "sgu_b",
              "w_in_odd", "w_out_odd", "pool_w", "pool_scale", "conv_w", "conv_b", "ffn_w_gate", "ffn_w_up",
              "ffn_w_down"):
        shared[k] = inp[k]
    shared["norm_final"] = inp["norm_final"].reshape(1, D)
    shared["s5_d"] = inp["s5_d"].reshape(2, 512)
    in_maps = []
    for c in range(NCORES):
        m = dict(shared)
        m["x_p"] = inp["x_prompt"][c]
        m["x_s"] = np.ascontiguousarray(inp["x_sample"][16 * c:16 * c + 16].reshape(NS, D))
        m["st_re"] = np.ascontiguousarray(inp["state_s5_re"][:, 16 * c:16 * c + 16].reshape(2, 16, 2048))
        m["st_im"] = np.ascontiguousarray(inp["state_s5_im"][:, 16 * c:16 * c + 16].reshape(2, 16, 2048))
        m["st_pool"] = np.ascontiguousarray(inp["state_pool"][:, 16 * c:16 * c + 16])
        m["st_conv"] = np.ascontiguousarray(inp["state_conv"][:, 16 * c:16 * c + 16])
        in_maps.append(m)
    res = run_bass_kernel_spmd(nc, in_maps, core_ids=list(range(NCORES)))
    R = res.results
    y_prompt = np.stack([R[c]["y_p"] for c in range(NCORES)], 0).reshape(8, SEQ, D)
    y_sample = np.concatenate([R[c]["y_s"].reshape(16, 4, D) for c in range(NCORES)], 0)
    p_re = np.stack([R[c]["o_p_re"].reshape(2, 32, 64) for c in range(NCORES)], 1)
    p_im = np.stack([R[c]["o_p_im"].reshape(2, 32, 64) for c in range(NCORES)], 1)
    p_pool = np.stack([R[c]["o_p_pool"] for c in range(NCORES)], 1)
    p_conv = np.stack([R[c]["o_p_conv"] for c in range(NCORES)], 1)
    s_re = np.concatenate([R[c]["o_s_re"].reshape(2, 16, 32, 64) for c in range(NCORES)], 1)
    s_im = np.concatenate([R[c]["o_s_im"].reshape(2, 16, 32, 64) for c in range(NCORES)], 1)
    s_v = np.concatenate([R[c]["o_s_v"].reshape(2, 16, 4, 512) for c in range(NCORES)], 1)
    s_pool = np.concatenate([R[c]["o_s_pool"].reshape(2, 16, 15, 512) for c in range(NCORES)], 1)
    s_conv = np.concatenate([R[c]["o_s_conv"].reshape(2, 16, 2, 512) for c in range(NCORES)], 1)
    outs = (y_prompt, y_sample, p_re, p_im, p_pool, p_conv, s_re, s_im, s_v, s_pool, s_conv)
    return tuple(np.ascontiguousarray(o.astype(np.float32)) for o in outs)
```

```python
import math
import numpy as np
from contextlib import ExitStack
import concourse.bass as bass
import concourse.mybir as mybir
from concourse.bass_utils import run_bass_kernel_spmd

F32 = mybir.dt.float32
BF16 = mybir.dt.bfloat16
I32 = mybir.dt.int32
ALU = mybir.AluOpType
AF = mybir.ActivationFunctionType

ENGS = ("pe", "act", "dve", "pool", "sp")
NSLOT = 12
NCORES = 8
D = 1024
SEQ = 2048
NS = 64
T = SEQ + NS
DFF = 2816
EPS = 1e-6
TM = 256
TF = 448
FS = 256
NSL = DFF // FS


class _Op(object):
    __slots__ = ("idx", "eng", "emit", "deps", "signal", "epoch", "semval",
                 "is_dma", "slot", "dval", "prev_dval", "cost", "pos")


class Prog(object):
    def __init__(self, nc, es, n_epochs=6):
        self.nc = nc
        self.ops = []
        self.regions = {}
        self.epoch = 0
        self.n_epochs = n_epochs
        self.sems = {}
        self.cnt = {}
        for e in ENGS:
            for ep in range(n_epochs):
                self.sems[(e, ep)] = es.enter_context(nc.semaphore("s_%s_%d" % (e, ep)))
        self.dsems = {}
        self.dcount = {}
        self.dnext = {}
        for q in ("sp", "pool", "act"):
            self.dnext[q] = 0
            for s in range(NSLOT):
                self.dsems[(q, s)] = es.enter_context(nc.semaphore("d_%s_%d" % (q, s)))
                self.dcount[(q, s)] = 0
        self.nflush = 0
        self.cost = {"pe": 115.0, "act": 450.0, "dve": 430.0, "pool": 600.0, "sp": 100.0}
        self.reorder = True
        self.filler = None
        self.filler_cost = 170.0
        self.nfill = 0

    def next_epoch(self):
        assert not self.ops
        self.epoch = min(self.epoch + 1, self.n_epochs - 1)

    def _add(self, eng, emit, reads, writes, is_dma, cost):
        o = _Op()
        o.idx = len(self.ops)
        o.eng = eng
        o.emit = emit
        o.signal = False
        o.epoch = self.epoch
        o.semval = None
        o.is_dma = is_dma
        o.cost = cost if cost is not None else (3000.0 if is_dma else self.cost[eng])
        deps = set()
        for k in reads:
            r = self.regions.get(k)
            if r is not None and r[0] is not None:
                deps.add(r[0])
        for k in writes:
            r = self.regions.get(k)
            if r is not None:
                if r[0] is not None:
                    deps.add(r[0])
                deps.update(r[1])
        for k in reads:
            r = self.regions.get(k)
            if r is None:
                r = [None, []]
                self.regions[k] = r
            r[1].append(o.idx)
        for k in writes:
            self.regions[k] = [o.idx, []]
        deps.discard(o.idx)
        o.deps = deps
        o.slot = None
        self.ops.append(o)
        return o

    def op(self, eng, emit, reads=(), writes=(), cost=None):
        return self._add(eng, emit, reads, writes, False, cost)

    def dma(self, q, out, in_, reads=(), writes=(), cost=None, **kw):
        def emit(e, out=out, in_=in_, kw=kw):
            return e.dma_start(out=out, in_=in_, **kw)
        return self._add(q, emit, reads, writes, True, cost)

    def _schedule(self):
        ops = self.ops
        n = len(ops)
        succs = [[] for _ in range(n)]
        indeg = [0] * n
        for o in ops:
            for d in o.deps:
                succs[d].append(o.idx)
            indeg[o.idx] = len(o.deps)
        lastd = {}
        dchain = {}
        for o in ops:
            if o.is_dma:
                if o.eng in lastd:
                    dchain[o.idx] = lastd[o.eng]
                lastd[o.eng] = o.idx
        prio = [0.0] * n
        for i in range(n - 1, -1, -1):
            m = 0.0
            for s_ in succs[i]:
                if prio[s_] > m:
                    m = prio[s_]
            prio[i] = ops[i].cost + m
        order = {e: [] for e in ENGS}
        if not self.reorder:
            for o in ops:
                order[o.eng].append(o)
            return order
        ready = {e: [] for e in ENGS}
        ready_t = [0.0] * n
        fin = [0.0] * n
        issued = [False] * n
        free_at = {e: 0.0 for e in ENGS}
        for o in ops:
            if indeg[o.idx] == 0:
                ready[o.eng].append(o.idx)
        remaining = n
        HOP = 150.0
        while remaining:
            best = None
            for e in ENGS:
                rl = ready[e]
                if not rl:
                    continue
                fa = free_at[e]
                cb = None
                for i in rl:
                    o = ops[i]
                    if o.is_dma and i in dchain and not issued[dchain[i]]:
                        continue
                    st = ready_t[i] if ready_t[i] > fa else fa
                    key = (st, -prio[i], i)
                    if cb is None or key < cb:
                        cb = key
                if cb is not None and (best is None or cb < best[0]):
                    best = (cb, e)
            assert best is not None, "scheduler deadlock"
            (st, _, i), e = best
            o = ops[i]
            ready[e].remove(i)
            issued[i] = True
            if e == "pe" and self.filler is not None and free_at[e] > 0.0:
                gap = st - free_at[e]
                if gap > 1200.0:
                    k = min(int((gap - 500.0) / self.filler_cost), 60)
                    for _ in range(k):
                        f = _Op()
                        f.idx = -1
                        f.eng = "pe"
                        f.emit = self.filler
                        f.is_dma = False
                        f.signal = False
                        order[e].append(f)
                    self.nfill += k
            if o.is_dma:
                free_at[e] = st + 80.0
                fin[i] = st + o.cost
            else:
                free_at[e] = st + o.cost
                fin[i] = st + o.cost
            order[e].append(o)
            remaining -= 1
            for s_ in succs[i]:
                t = fin[i] + (0.0 if (ops[s_].eng == e and e == "pe") else HOP)
                if t > ready_t[s_]:
                    ready_t[s_] = t
                indeg[s_] -= 1
                if indeg[s_] == 0:
                    ready[ops[s_].eng].append(s_)
        self.est_time = max(fin) if n else 0.0
        return order

    def flush(self):
        nc = self.nc
        ops = self.ops
        if not ops:
            return
        per_eng = self._schedule()
        for e in ENGS:
            for p_, o in enumerate(per_eng[e]):
                o.pos = p_
            if e != "pe":
                assert all(o.idx >= 0 for o in per_eng[e])
        for e in ("sp", "pool", "act"):
            for o in per_eng[e]:
                if o.is_dma:
                    s = self.dnext[e]
                    self.dnext[e] = (s + 1) % NSLOT
                    o.slot = s
                    o.prev_dval = self.dcount[(e, s)]
                    self.dcount[(e, s)] += 16
                    o.dval = self.dcount[(e, s)]
        red = []
        for o in ops:
            comp = {}
            dmas = []
            for d in o.deps:
                p = ops[d]
                if p.is_dma:
                    dmas.append(d)
                else:
                    if p.eng == "pe" and o.eng == "pe" and not o.is_dma:
                        continue
                    if p.eng not in comp or ops[comp[p.eng]].pos < p.pos:
                        comp[p.eng] = d
            red.append((comp, dmas))
            for d in comp.values():
                ops[d].signal = True
        cnt = self.cnt
        for e in ENGS:
            for o in per_eng[e]:
                if o.idx < 0 or o.is_dma or not o.signal:
                    continue
                key = (o.eng, o.epoch)
                cnt[key] = cnt.get(key, 0) + 1
                o.semval = cnt[key]
        sems = self.sems
        dsems = self.dsems
        n_ep = self.n_epochs
        dcount = self.dcount

        def emit_engine(e, eng_name):
            waited = {}
            dwaited = {}
            for o in per_eng[eng_name]:
                if o.idx < 0:
                    o.emit(e)
                    continue
                comp, dmas = red[o.idx]
                for pe_name, d in comp.items():
                    p = ops[d]
                    done = False
                    for ep in range(p.epoch, n_ep):
                        w = waited.get((pe_name, ep), 0)
                        if ep == p.epoch and w >= p.semval:
                            done = True
                        if ep > p.epoch and w > 0:
                            done = True
                    if done:
                        continue
                    e.wait_ge(sems[(pe_name, p.epoch)], p.semval)
                    waited[(pe_name, p.epoch)] = p.semval
                for d in dmas:
                    p = ops[d]
                    k = (p.eng, p.slot)
                    if dwaited.get(k, 0) >= p.dval:
                        continue
                    e.wait_ge(dsems[k], p.dval)
                    dwaited[k] = p.dval
                if o.is_dma:
                    k = (o.eng, o.slot)
                    if o.prev_dval > 0 and dwaited.get(k, 0) < o.prev_dval:
                        e.wait_ge(dsems[k], o.prev_dval)
                        dwaited[k] = o.prev_dval
                    inst = o.emit(e)
                    inst.then_inc(dsems[k], 16)
                else:
                    inst = o.emit(e)
                    if o.signal:
                        inst.then_inc(sems[(o.eng, o.epoch)], 1)
            if eng_name in ("sp", "pool", "act"):
                for s in range(NSLOT):
                    k = (eng_name, s)
                    if dcount[k] > 0 and dwaited.get(k, 0) < dcount[k]:
                        e.wait_ge(dsems[k], dcount[k])

        with nc.Block() as block:
            @block.tensor
            def _(e):
                emit_engine(e, "pe")

            @block.scalar
            def _(e):
                emit_engine(e, "act")

            @block.vector
            def _(e):
                emit_engine(e, "dve")

            @block.gpsimd
            def _(e):
                emit_engine(e, "pool")

            @block.sync
            def _(e):
                emit_engine(e, "sp")
        self.ops = []
        self.regions = {}
        self.nflush += 1


def build_nc(debug=False):
    nc = bass.Bass("TRN2", target_bir_lowering=False)
    try:
        nc.allow_low_precision("bf16 matmul operands with fp32 accumulation by design")
    except Exception:
        pass

    def din(name, shape):
        return nc.dram_tensor(name, list(shape), F32, kind="ExternalInput").ap()

    def dout(name, shape):
        return nc.dram_tensor(name, list(shape), F32, kind="ExternalOutput").ap()

    x_p = din("x_p", (SEQ, D))
    x_s = din("x_s", (NS, D))
    st_re = din("st_re", (2, 16, 2048))
    st_im = din("st_im", (2, 16, 2048))
    st_pool = din("st_pool", (2, 16, 15, 512))
    st_conv = din("st_conv", (2, 16, 2, 512))
    norm_mix = din("norm_mix", (4, D))
    norm_ffn = din("norm_ffn", (4, D))
    norm_final = din("norm_final", (1, D))
    w_in_even = din("w_in_even", (2, D, 1536))
    w_out_even = din("w_out_even", (2, D, D))
    lam_re = din("s5_lambda_re", (2, 32, 64))
    lam_im = din("s5_lambda_im", (2, 32, 64))
    log_dt = din("s5_log_dt", (2, 32))
    b_re = din("s5_b_re", (2, 32, 64, 16))
    b_im = din("s5_b_im", (2, 32, 64, 16))
    c_re = din("s5_c_re", (2, 32, 16, 64))
    c_im = din("s5_c_im", (2, 32, 16, 64))
    s5_d = din("s5_d", (2, 512))
    glu_w = din("s5_glu_w", (2, 512, 512))
    glu_b = din("s5_glu_b", (2, 512))
    sgu_norm = din("sgu_norm", (2, 512))
    sgu_w = din("sgu_w", (2, 8, 128, 128))
    sgu_b = din("sgu_b", (2, 8, 128))
    w_in_odd = din("w_in_odd", (2, D, 2048))
    w_out_odd = din("w_out_odd", (2, D, D))
    pool_w = din("pool_w", (2, 4, 128, 128))
    pool_scale = din("pool_scale", (2, 512))
    conv_w = din("conv_w", (2, 3, 512))
    conv_b = din("conv_b", (2, 512))
    ffn_g = din("ffn_w_gate", (4, D, DFF))
    ffn_u = din("ffn_w_up", (4, D, DFF))
    ffn_d = din("ffn_w_down", (4, DFF, D))

    y_p = dout("y_p", (SEQ, D))
    y_s = dout("y_s", (NS, D))
    o_p_re = dout("o_p_re", (2, 16, 128))
    o_p_im = dout("o_p_im", (2, 16, 128))
    o_p_pool = dout("o_p_pool", (2, 15, 512))
    o_p_conv = dout("o_p_conv", (2, 2, 512))
    o_s_re = dout("o_s_re", (2, 16, 2048))
    o_s_im = dout("o_s_im", (2, 16, 2048))
    o_s_v = dout("o_s_v", (2, NS, 512))
    o_s_pool = dout("o_s_pool", (2, 240, 512))
    o_s_conv = dout("o_s_conv", (2, 32, 512))
    dbg = dout("dbg", (128, 4096)) if debug else None

    es = ExitStack()
    with es:
        es.enter_context(nc.allow_non_contiguous_dma(reason="small strided parameter loads"))
        P = Prog(nc, es)

        _uid = [0]

        def sb(stk, name, shape, dt=F32):
            _uid[0] += 1
            return stk.enter_context(nc.sbuf_tensor("%s_u%d" % (name, _uid[0]), list(shape), dt))

        PS = es.enter_context(nc.psum_tensor("PS", [128, 8, 512], F32))
        ps_rr = [0]

        def psget(n=1):
            b = ps_rr[0]
            if b + n > 8:
                b = 0
            ps_rr[0] = (b + n) % 8
            return b

        def pk(b, n=1):
            return [("ps", b + i) for i in range(n)]

        hres = sb(es, "hres", [128, 8, T])
        ident = sb(es, "ident", [128, 128])
        identb = sb(es, "identb", [128, 128], BF16)
        onesb = sb(es, "onesb", [128, 128], BF16)
        iot = sb(es, "iot", [128, 128])
        iop = sb(es, "iop", [128, 1])
        pstage = sb(es, "pstage", [128, 128])
        pvec = sb(es, "pvec", [128, 120])
        gmix = pvec[:, 0:32].rearrange("p (l k) -> p l k", l=4)
        gffn = pvec[:, 32:64].rearrange("p (l k) -> p l k", l=4)
        glub = pvec[:, 64:72].rearrange("p (l k) -> p l k", l=2)
        pscale = pvec[:, 72:80].rearrange("p (l k) -> p l k", l=2)
        cw = pvec[:, 80:104].rearrange("p (l c k) -> p l c k", l=2, c=3)
        cb = pvec[:, 104:112].rearrange("p (l k) -> p l k", l=2)
        dcol = pvec[:, 112:120].rearrange("p (l k) -> p l k", l=2)
        epsc = sb(es, "epsc", [128, 1])
        s5car = sb(es, "s5car", [128, 16, 2])
        xchalo = sb(es, "xchalo", [128, 4, 15])
        zhalo = sb(es, "zhalo", [128, 4, 2])

        def V(e):
            return e

        def load_w(dst, src3, key, nsplit):
            K = dst.shape[1]
            step = K // nsplit
            for i in range(nsplit):
                nb = 128 * step * dst.shape[2] * 4
                P.dma("pool", dst[:, i * step:(i + 1) * step, :],
                      src3.rearrange("(k p) n -> p k n", p=128)[:, i * step:(i + 1) * step, :], writes=[(key, i)],
                      cost=2500.0 + nb / 150.0)

        WinP = sb(es, "WinP", [128, 8, 2048], BF16)
        WoutP = sb(es, "WoutP", [128, 8, D], BF16)
        WsmP = sb(es, "WsmP", [128, 2048], BF16)

        def load_mixer_weights(layer, only_in=False, skip_in=False):
            i = layer // 2
            if layer % 2 == 0:
                if not skip_in:
                    load_w(WinP[:, :, 0:1536], w_in_even[i], "Win", 4)
                if only_in:
                    return
                load_w(WsmP[:, :].rearrange("p (k n) -> p k n", k=4), glu_w[i], "Wsm", 1)
                load_w(WoutP, w_out_even[i], "Wout", 2)
            else:
                load_w(WinP, w_in_odd[i], "Win", 4)
                load_w(WoutP, w_out_odd[i], "Wout", 2)
                P.dma("pool", WsmP[:, 0:512].rearrange("p (g d) -> p g d", g=4), pool_w[i].rearrange("g c d -> c g d"),
                      writes=["Wsm"])

        load_mixer_weights(0, only_in=True)
        P.op("pool", lambda e: e.iota(iot[:], pattern=[[1, 128]], base=0, channel_multiplier=0,
                                      allow_small_or_imprecise_dtypes=True), writes=["iot"])
        P.op("pool", lambda e: e.iota(iop[:], pattern=[[1, 1]], base=0, channel_multiplier=1,
                                      allow_small_or_imprecise_dtypes=True), writes=["iop"])
        P.op("dve", lambda e: e.tensor_scalar(out=ident[:], in0=iot[:], scalar1=iop[:, 0:1], scalar2=None,
                                              op0=ALU.is_equal), reads=["iot", "iop"], writes=["ident"])
        P.op("dve", lambda e: e.tensor_copy(out=identb[:], in_=ident[:]), reads=["ident"], writes=["identb"])
        P.op("dve", lambda e: e.memset(onesb[:], 1.0), writes=["onesb"])
        P.op("dve", lambda e: e.memset(epsc[:], EPS), writes=["epsc"])
        P.filler = None; _unused_filler = lambda e: e.matmul(PS[:, 7, 0:128], lhsT=onesb[:], rhs=onesb[:], start=True, stop=True)
        with ExitStack() as st:
            pass
        pst_rows = [(norm_mix.rearrange("l (k p) -> (l k) p", p=128), 32), (norm_ffn.rearrange("l (k p) -> (l k) p", p=128), 32),
                    (glu_b.rearrange("l (k p) -> (l k) p", p=128), 8), (pool_scale.rearrange("l (k p) -> (l k) p", p=128), 8),
                    (conv_w.rearrange("l c (k p) -> (l c k) p", p=128), 24), (conv_b.rearrange("l (k p) -> (l k) p", p=128), 8),
                    (s5_d.rearrange("l (k p) -> (l k) p", p=128), 8)]
        r0 = 0
        for j_, (src_, nr_) in enumerate(pst_rows):
            P.dma("sp", pstage[r0:r0 + nr_, :], src_, writes=[("pstage", j_)])
            r0 += nr_
        P.op("pe", lambda e: e.transpose(PS[:, 5, 0:120], pstage[0:120, :], ident[0:120, 0:120]),
             reads=[("pstage", j_) for j_ in range(7)] + ["ident"], writes=[("ps", 5)])
        P.op("act", lambda e: e.activation(out=pvec[:, :], in_=PS[:, 5, 0:120], func=AF.Copy), reads=[("ps", 5)],
             writes=["gmix", "gffn", "glub", "pscale", "cw", "cb", "dcol"])

        def load_x(xt):
            nsub = SEQ // 128 + 1
            for si in range(nsub):
                n = 128 if si < SEQ // 128 else NS
                src = x_p[si * 128:(si + 1) * 128, :] if si < SEQ // 128 else x_s[:, :]
                xb = xt[si % 2]
                xk = "xt%d" % (si % 2)
                P.dma("sp", xb[0:n, :], src, writes=[xk])
                for half in range(2):
                    b = psget()
                    for q in range(4):
                        k = half * 4 + q
                        P.op("pe", lambda e, b=b, q=q, k=k, xb=xb, n=n: e.transpose(
                            PS[:, b, q * 128:q * 128 + n], xb[0:n, k * 128:(k + 1) * 128], ident[0:n, 0:n]),
                            reads=[xk, "ident"], writes=pk(b))
                    eng = "act" if half == 0 else "dve"
                    if eng == "act":
                        P.op("act", lambda e, b=b, half=half, si=si, n=n: e.activation(
                            out=hres[:, half * 4:half * 4 + 4, si * 128:si * 128 + n],
                            in_=PS[:, b, :].rearrange("p (q t) -> p q t", q=4)[:, :, 0:n], func=AF.Copy),
                            reads=pk(b), writes=[("h", si, half)])
                    else:
                        P.op("dve", lambda e, b=b, half=half, si=si, n=n: e.tensor_copy(
                            out=hres[:, half * 4:half * 4 + 4, si * 128:si * 128 + n],
                            in_=PS[:, b, :].rearrange("p (q t) -> p q t", q=4)[:, :, 0:n]),
                            reads=pk(b), writes=[("h", si, half)])

        mtiles = [(i * TM, TM, False) for i in range(SEQ // TM)] + [(SEQ, NS, True)]
        ftiles = [(0, 448, False), (448, 448, False), (896, 448, False), (1344, 448, False), (1792, 320, False)]

        def hkeys(t0, n):
            ks = []
            a = (t0 // TM) * TM
            while a < t0 + n:
                ks.append(("hres", a))
                a += TM
            return ks

        def s5_setup(stk, i, XB, YC, BD, TC, TS, R4, A4, bmask):
            with ExitStack() as st:
                def t16(name):
                    return sb(st, "s5_" + name, [128, 16])
                LRI = sb(st, "s5_LRI", [128, 32])
                LR = LRI[:, 0:16]
                LI = LRI[:, 16:32]
                LDT, DT, Z, MAG, ANG = [t16(n_) for n_ in ("LDT", "DT", "Z", "MAG", "ANG")]
                SN, CS, ta, tb, tc_, td = [t16(n_) for n_ in ("SN", "CS", "ta", "tb", "tc", "td")]
                FR, FI = t16("FR"), t16("FI")
                AR = [t16("AR%d" % k) for k in range(5)]
                AI = [t16("AI%d" % k) for k in range(5)]
                BR = sb(st, "s5_BR", [128, 16, 32]); BI = sb(st, "s5_BI", [128, 16, 32])
                BBr = sb(st, "s5_BBr", [128, 16, 32]); BBi = sb(st, "s5_BBi", [128, 16, 32])
                CTr = sb(st, "s5_CTr", [128, 16, 32]); CTi = sb(st, "s5_CTi", [128, 16, 32])
                Yr = sb(st, "s5_Yr", [128, 16, 32]); Yi = sb(st, "s5_Yi", [128, 16, 32])
                W1 = sb(st, "s5_W1", [128, 16, 32]); W2 = sb(st, "s5_W2", [128, 16, 32])
                W3 = sb(st, "s5_W3", [128, 16, 32]); W4 = sb(st, "s5_W4", [128, 16, 32])
                Xr_ = sb(st, "s5_Xr", [128, 16, 32]); Xi_ = sb(st, "s5_Xi", [128, 16, 32])
                CNr = sb(st, "s5_CNr", [128, 4, 128]); CNi = sb(st, "s5_CNi", [128, 4, 128])
                cnt = [0]

                def dv(fn, reads, writes, eng="dve"):
                    P.op(eng, fn, reads=reads, writes=writes)

                def tt(out, a, b, op, r, w, eng="dve"):
                    dv(lambda e: e.tensor_tensor(out=out, in0=a, in1=b, op=op), r, w, eng=eng)

                def ts(out, a, s1, op0, r, w, s2=None, op1=None):
                    if op1 is None:
                        dv(lambda e: e.tensor_scalar(out=out, in0=a, scalar1=s1, scalar2=None, op0=op0), r, w)
                    else:
                        dv(lambda e: e.tensor_scalar(out=out, in0=a, scalar1=s1, scalar2=s2, op0=op0, op1=op1), r, w)

                lst = sb(st, "s5_lst", [32, 128])
                P.dma("sp", lst[0:16, :], lam_re[i].rearrange("(P g) n -> P (g n)", g=2), writes=[("lst", 0)])
                P.dma("sp", lst[16:32, :], lam_im[i].rearrange("(P g) n -> P (g n)", g=2), writes=[("lst", 1)])
                bl_ = psget()
                P.op("pe", lambda e: e.transpose(PS[:, bl_, 0:32], lst[:, :], ident[0:32, 0:32]),
                     reads=[("lst", 0), ("lst", 1), "ident"], writes=pk(bl_))
                P.op("act", lambda e: e.activation(out=LRI[:, :], in_=PS[:, bl_, 0:32], func=AF.Copy), reads=pk(bl_),
                     writes=["LR", "LI"])
                for g2 in range(2):
                    P.dma("sp", LDT[64 * g2:64 * g2 + 64, :],
                          log_dt[i:i + 1, :].rearrange("o (P g) -> o g P", g=2)[:, g2, :].broadcast_to([64, 16]),
                          writes=[("LDT", g2)])
                for tl in (BR, BI, CNr, CNi):
                    dv(lambda e, tl=tl: e.memset(tl[:], 0.0), [], ["z_" + tl.name], eng="pool")
                for (tl, src) in ((BR, b_re), (BI, b_im)):
                    for g2 in range(2):
                        P.dma("sp", tl[64 * g2:64 * g2 + 64, :, 16 * g2:16 * g2 + 16],
                              src[i].rearrange("(P g) n q -> g n P q", g=2)[g2], reads=["z_" + tl.name],
                              writes=[("ld_" + tl.name, g2)])
                for (tl, src) in ((CNr, c_re), (CNi, c_im)):
                    for p4 in range(4):
                        for g2 in range(2):
                            P.dma("act", tl[32 * p4 + 16 * g2:32 * p4 + 16 * g2 + 16, :, 64 * g2:64 * g2 + 64],
                                  src[i].rearrange("(f a g) p n -> a g p f n", a=4, g=2)[p4, g2],
                                  reads=["z_" + tl.name], writes=[("ld_" + tl.name, p4, g2)])
                for (src, dst) in ((CNr, CTr), (CNi, CTi)):
                    b = psget()
                    for ft in range(4):
                        P.op("pe", lambda e, ft=ft, b=b, src=src: e.transpose(
                            PS[:, b, ft * 128:(ft + 1) * 128], src[:, ft, :], ident[:]),
                            reads=[("ld_" + src.name, a_, b_) for a_ in range(4) for b_ in range(2)] + ["ident"], writes=pk(b))
                    P.op("act", lambda e, b=b, dst=dst: e.activation(
                        out=dst[:].rearrange("p a b -> p (a b)"), in_=PS[:, b, :], func=AF.Copy),
                        reads=pk(b), writes=[dst.name])
                dv(lambda e: e.activation(out=DT[:], in_=LDT[:], func=AF.Exp), [("LDT", 0), ("LDT", 1)], ["DT"], eng="act")
                tt(Z[:], LR[:], DT[:], ALU.mult, ["LR", "DT"], ["Z"])
                ts(MAG[:], Z[:], 1.0 / 120.0, ALU.mult, ["Z"], ["MAG"], 1.0 / 24.0, ALU.add)
                for c in (1.0 / 6.0, 0.5, 1.0, 1.0):
                    tt(MAG[:], MAG[:], Z[:], ALU.mult, ["MAG", "Z"], ["MAG"])
                    ts(MAG[:], MAG[:], float(c), ALU.add, ["MAG"], ["MAG"])
                tt(ANG[:], LI[:], DT[:], ALU.mult, ["LI", "DT"], ["ANG"])
                C1 = 6.28125
                C2 = 2.0 * math.pi - C1
                MAGIC = 12582912.0
                for (shift, dst) in ((0.0, SN), (0.5 * math.pi, CS)):
                    ts(ta[:], ANG[:], 1.0 / (2 * math.pi), ALU.mult, ["ANG"], ["ta"], shift / (2 * math.pi), ALU.add)
                    ts(tb[:], ta[:], MAGIC, ALU.add, ["ta"], ["tb"])
                    ts(tb[:], tb[:], -MAGIC, ALU.add, ["tb"], ["tb"])
                    dv(lambda e: e.scalar_tensor_tensor(out=ta[:], in0=tb[:], scalar=-C1, in1=ANG[:],
                                                        op0=ALU.mult, op1=ALU.add), ["tb", "ANG"], ["ta"])
                    dv(lambda e: e.scalar_tensor_tensor(out=ta[:], in0=tb[:], scalar=-C2, in1=ta[:],
                                                        op0=ALU.mult, op1=ALU.add), ["tb", "ta"], ["ta"])
                    ts(ta[:], ta[:], float(shift), ALU.add, ["ta"], ["ta"], math.pi, ALU.min)
                    ts(ta[:], ta[:], -math.pi, ALU.max, ["ta"], ["ta"])
                    dv(lambda e, dst=dst: e.activation(out=dst[:], in_=ta[:], func=AF.Sin), ["ta"], [dst.name],
                       eng="act")
                tt(AR[1][:], MAG[:], CS[:], ALU.mult, ["MAG", CS.name], ["AR1"])
                tt(AI[1][:], MAG[:], SN[:], ALU.mult, ["MAG", SN.name], ["AI1"])
                dv(lambda e: e.memset(AR[0][:], 1.0), [], ["AR0"])
                dv(lambda e: e.memset(AI[0][:], 0.0), [], ["AI0"])

                def cmul(orr, oi, ar, ai, br, bi, rk, wk, t1=None, t2=None, k1="W1", k2="W2", eng="dve"):
                    tt(t1, ar, br, ALU.mult, rk, [k1], eng)
                    tt(t2, ai, bi, ALU.mult, rk, [k2], eng)
                    tt(orr, t1, t2, ALU.subtract, [k1, k2], [wk + "r"], eng)
                    tt(t1, ar, bi, ALU.mult, rk + [wk + "r"], [k1], eng)
                    tt(t2, ai, br, ALU.mult, rk + [wk + "r"], [k2], eng)
                    tt(oi, t1, t2, ALU.add, [k1, k2], [wk + "i"], eng)

                cmul(AR[2][:], AI[2][:], AR[1][:], AI[1][:], AR[1][:], AI[1][:], ["AR1", "AI1"], "A2", tc_[:], td[:], "tc", "td")
                cmul(AR[3][:], AI[3][:], AR[2][:], AI[2][:], AR[1][:], AI[1][:], ["AR1", "AI1", "A2r", "A2i"], "A3",
                     tc_[:], td[:], "tc", "td")
                cmul(AR[4][:], AI[4][:], AR[2][:], AI[2][:], AR[2][:], AI[2][:], ["A2r", "A2i"], "A4", tc_[:], td[:], "tc", "td")
                akeys = {0: ["AR0", "AI0"], 1: ["AR1", "AI1"], 2: ["A2r", "A2i"], 3: ["A3r", "A3i"], 4: ["A4r", "A4i"]}
                dv(lambda e: e.tensor_copy(out=A4[:, :, 0], in_=AR[4][:]), akeys[4], ["A4"])
                dv(lambda e: e.tensor_copy(out=A4[:, :, 1], in_=AI[4][:]), akeys[4] + ["A4"], ["A4"])
                tt(ta[:], MAG[:], MAG[:], ALU.mult, ["MAG"], ["ta"])
                tt(R4[:], ta[:], ta[:], ALU.mult, ["ta"], ["R4"])
                ts(ta[:], AR[1][:], -1.0, ALU.add, ["AR1"], ["ta"])
                tt(tb[:], LR[:], LR[:], ALU.mult, ["LR"], ["tb"])
                tt(tc_[:], LI[:], LI[:], ALU.mult, ["LI", "A4i"], ["tc"])
                tt(tb[:], tb[:], tc_[:], ALU.add, ["tb", "tc"], ["tb"])
                dv(lambda e: e.reciprocal(out=tb[:], in_=tb[:]), ["tb"], ["tb"])
                tt(tc_[:], ta[:], LR[:], ALU.mult, ["ta", "LR"], ["tc"])
                tt(td[:], AI[1][:], LI[:], ALU.mult, ["AI1", "LI", "A4i"], ["td"])
                tt(tc_[:], tc_[:], td[:], ALU.add, ["tc", "td"], ["tc"])
                tt(FR[:], tc_[:], tb[:], ALU.mult, ["tc", "tb"], ["FR"])
                tt(tc_[:], AI[1][:], LR[:], ALU.mult, ["AI1", "LR", "FR"], ["tc"])
                tt(td[:], ta[:], LI[:], ALU.mult, ["ta", "LI", "FR"], ["td"])
                tt(tc_[:], tc_[:], td[:], ALU.subtract, ["tc", "td"], ["tc"])
                tt(FI[:], tc_[:], tb[:], ALU.mult, ["tc", "tb"], ["FI"])
                dv(lambda e: e.reciprocal(out=ta[:], in_=R4[:]), ["R4", "FI"], ["ta"])
                tt(TC[:, :, 0], AR[4][:], ta[:], ALU.mult, akeys[4] + ["ta"], ["TC"])
                tt(TS[:, :, 0], AI[4][:], ta[:], ALU.mult, akeys[4] + ["ta"], ["TS"])
                m = 1
                NCH = TM // 4
                W5 = sb(st, "s5_W5", [128, 16, 32]); W6 = sb(st, "s5_W6", [128, 16, 32])
                while m < NCH:
                    ur = TC[:, :, m - 1:m].broadcast_to([128, 16, m])
                    ui = TS[:, :, m - 1:m].broadcast_to([128, 16, m])
                    w1 = W5[:, :, 0:m]; w2 = W6[:, :, 0:m]
                    tt(w1, TC[:, :, 0:m], ur, ALU.mult, ["TC", "TS"], ["W5"], "dve")
                    tt(w2, TS[:, :, 0:m], ui, ALU.mult, ["TC", "TS"], ["W6"], "dve")
                    tt(TC[:, :, m:2 * m], w1, w2, ALU.subtract, ["W5", "W6"], ["TC"], "dve")
                    tt(w1, TC[:, :, 0:m], ui, ALU.mult, ["TC", "TS"], ["W5"], "dve")
                    tt(w2, TS[:, :, 0:m], ur, ALU.mult, ["TC", "TS"], ["W6"], "dve")
                    tt(TS[:, :, m:2 * m], w1, w2, ALU.add, ["W5", "W6"], ["TS"], "dve")
                    m *= 2

                def bc(a):
                    return a.unsqueeze(2).broadcast_to([128, 16, 32])

                cmul(BBr[:], BBi[:], BR[:], BI[:], bc(FR[:]), bc(FI[:]), [("ld_" + BR.name, 0), ("ld_" + BR.name, 1), ("ld_" + BI.name, 0), ("ld_" + BI.name, 1), "FR", "FI"],
                     "BB", W1[:], W2[:])
                for k in range(4):
                    if k == 0:
                        srcs = (BBr, BBi)
                        skeys = ["BBr", "BBi"]
                    else:
                        cmul(Xr_[:], Xi_[:], BBr[:], BBi[:], bc(AR[k][:]), bc(AI[k][:]), ["BBr", "BBi"] + akeys[k], "Xq",
                             W3[:], W4[:], "W3", "W4", eng="pool")
                        srcs = (Xr_, Xi_)
                        skeys = ["Xqr", "Xqi"]
                    s = 3 - k
                    for ri in range(2):
                        b = psget()
                        for ft in range(4):
                            P.op("pe", lambda e, ft=ft, b=b, src=srcs[ri]: e.transpose(
                                PS[:, b, ft * 128:(ft + 1) * 128],
                                src[:, 4 * ft:4 * ft + 4, :].rearrange("p a b -> p (a b)"), ident[:]),
                                reads=skeys + ["ident"], writes=pk(b))
                        P.op("act", lambda e, b=b, ri=ri, s=s: e.activation(
                            out=XB[:, :, ri, s, :], in_=PS[:, b, :].rearrange("p (f n) -> p f n", f=4), func=AF.Copy),
                            reads=pk(b), writes=["XB"])
                bBD = psget()
                for k in range(5):
                    if k == 0:
                        dv(lambda e: e.tensor_copy(out=Yr[:], in_=CTr[:]), ["s5_CTr", "XB"], ["Yr"])
                        ts(Yi[:], CTi[:], -1.0, ALU.mult, ["s5_CTi", "XB"], ["Yi"])
                    else:
                        cmul(Yr[:], Yi[:], CTr[:], CTi[:], bc(AR[k][:]), bc(AI[k][:]),
                             ["s5_CTr", "s5_CTi", "BD%d" % (k - 1), "YC"] + akeys[k], "Y", W1[:], W2[:])
                        ts(Yi[:], Yi[:], -1.0, ALU.mult, ["Yi"], ["Yi"])
                        P.op("act", lambda e, k=k: e.activation(out=YC[:, :, 0, k - 1, :], in_=Yr[:], func=AF.Copy),
                             reads=["Yr"], writes=["YC"])
                        P.op("act", lambda e, k=k: e.activation(out=YC[:, :, 1, k - 1, :], in_=Yi[:], func=AF.Copy),
                             reads=["Yi"], writes=["YC"])
                    if k < 4:
                        for ft in range(4):
                            o_ = PS[:, bBD, ft * 128:(ft + 1) * 128]
                            P.op("pe", lambda e, ft=ft, o_=o_: e.matmul(
                                o_, lhsT=BBr[:, 4 * ft:4 * ft + 4, :].rearrange("p a b -> p (a b)"),
                                rhs=Yr[:, 4 * ft:4 * ft + 4, :].rearrange("p a b -> p (a b)"), start=True, stop=False),
                                reads=["BBr", "Yr"], writes=pk(bBD))
                            P.op("pe", lambda e, ft=ft, o_=o_: e.matmul(
                                o_, lhsT=BBi[:, 4 * ft:4 * ft + 4, :].rearrange("p a b -> p (a b)"),
                                rhs=Yi[:, 4 * ft:4 * ft + 4, :].rearrange("p a b -> p (a b)"), start=False, stop=True),
                                reads=["BBi", "Yi"], writes=pk(bBD))
                        for ft in range(4):
                            if k == 0:
                                dv(lambda e, ft=ft: e.tensor_tensor(out=W1[:, 0:4, :].rearrange("p a b -> p (a b)"),
                                                                    in0=PS[:, bBD, ft * 128:(ft + 1) * 128],
                                                                    in1=bmask[:], op=ALU.mult),
                                   pk(bBD) + ["bmask"], ["W1"])
                                dv(lambda e, ft=ft: e.scalar_tensor_tensor(
                                    out=BD[:, ft, 0, :], in0=ident[:], scalar=dcol[:, i, ft:ft + 1],
                                    in1=W1[:, 0:4, :].rearrange("p a b -> p (a b)"), op0=ALU.mult, op1=ALU.add),
                                    ["W1", "ident", "dcol"], ["BD0"])
                            else:
                                dv(lambda e, ft=ft, k=k: e.tensor_tensor(out=BD[:, ft, k, :],
                                                                         in0=PS[:, bBD, ft * 128:(ft + 1) * 128],
                                                                         in1=bmask[:], op=ALU.mult),
                                   pk(bBD) + ["bmask"], ["BD%d" % k])

        def even_mixer(layer):
            i = layer // 2
            P.cost.update({"pe": 115.0, "dve": 430.0, "act": 450.0})
            with ExitStack() as st:
                NCH = TM // 4
                Win = WinP
                Wout = WoutP
                Wglu = WsmP[:, :].rearrange("p (k n) -> p k n", k=4)
                XB = sb(st, "XB", [128, 4, 2, 4, 128], BF16)
                YC = sb(st, "YC", [128, 16, 2, 4, 32], BF16)
                BD = sb(st, "BD", [128, 4, 4, 128], BF16)
                TC = sb(st, "TC", [128, 16, NCH]); TS = sb(st, "TS", [128, 16, NCH])
                R4 = sb(st, "R4", [128, 16]); A4 = sb(st, "A4", [128, 16, 2])
                wT = sb(st, "wT", [128, 8, 128], BF16)
                bbc = sb(st, "bbc", [128, 4, 128])
                gsg = sb(st, "gsg", [128, 512])
                wsc = sb(st, "wsc", [128, 4, 16])
                P.dma("sp", gsg[:], sgu_norm[i:i + 1, :].broadcast_to([128, 512]), writes=["gsg"])
                for h in range(8):
                    P.dma("sp", bbc[64 * (h % 2):64 * (h % 2) + 64, h // 2, :],
                          sgu_b[i, h:h + 1, :].broadcast_to([64, 128]), writes=[("bbc", h)])
                for h in range(8):
                    P.dma("sp", wsc[64 * (h % 2):64 * (h % 2) + 64, h // 2, :].rearrange("p (a b) -> p a b", a=4),
                          sgu_w[i, h:h + 1, 0:4, 0:4].broadcast_to([64, 4, 4]), writes=[("wsc", h)])
                P.op("dve", lambda e: e.memset(s5car[:], 0.0), writes=[("s5car", q_) for q_ in range(4)])
                with ExitStack() as st2:
                    trilT = sb(st2, "trilT", [128, 128])
                    bmask = sb(st2, "bmask", [128, 128])
                    j32 = sb(st2, "bm_j32", [128, 128])
                    S4 = sb(st2, "bm_S4", [128, 128])
                    P.op("dve", lambda e: e.tensor_scalar(out=trilT[:], in0=iot[:], scalar1=iop[:, 0:1], scalar2=None,
                                                          op0=ALU.is_ge), reads=["iot", "iop"], writes=["trilT"])
                    P.op("pool", lambda e: e.iota(j32[:], pattern=[[1, 4], [0, 32]], base=0, channel_multiplier=0,
                                                  allow_small_or_imprecise_dtypes=True), writes=["j32"])
                    P.op("dve", lambda e: e.tensor_scalar(out=S4[:], in0=j32[:], scalar1=iop[:, 0:1], scalar2=None,
                                                          op0=ALU.is_equal), reads=["j32", "iop"], writes=["S4"])
                    P.op("pe", lambda e: e.matmul(PS[:, 6, 0:128], lhsT=S4[0:4, :], rhs=S4[0:4, :], start=True, stop=True),
                         reads=["S4"], writes=[("ps", 6)])
                    P.op("dve", lambda e: e.tensor_copy(out=bmask[:], in_=PS[:, 6, 0:128]), reads=[("ps", 6)], writes=["bmask"])
                    wld = [sb(st2, "wld%d" % q, [128, 128]) for q in range(2)]
                    for h in range(8):
                        wl = wld[h % 2]
                        P.dma("sp", wl[:], sgu_w[i, h], writes=["wld%d" % (h % 2)])
                        b = psget()
                        P.op("pe", lambda e, b=b, wl=wl: e.transpose(PS[:, b, 0:128], wl[:], ident[:]),
                             reads=["wld%d" % (h % 2), "ident"], writes=pk(b))
                        P.op("dve", lambda e, b=b, h=h: e.tensor_tensor(out=wT[:, h, :], in0=PS[:, b, 0:128],
                                                                        in1=trilT[:], op=ALU.mult),
                             reads=pk(b) + ["trilT"], writes=["wT"])
                    xt_ = [sb(st2, "xt%d" % q_, [128, D]) for q_ in range(2)] if layer == 0 else None
                    s5_setup(st2, i, XB, YC, BD, TC, TS, R4, A4, bmask)
                    if layer == 0:
                        load_x(xt_)
                    P.flush()
                glubh = sb(st, "glubh", [128, 4])
                P.op("dve", lambda e: e.tensor_scalar(out=glubh[:], in0=glub[:, i, :], scalar1=0.5, scalar2=None, op0=ALU.mult),
                     writes=["glubh"])
                for k_ in range(8):
                    P.op("dve", lambda e, k_=k_: e.tensor_scalar(out=WinP[:, k_, 0:1536], in0=WinP[:, k_, 0:1536],
                                                                 scalar1=gmix[:, layer, k_:k_ + 1], scalar2=None, op0=ALU.mult),
                         writes=["Win"], cost=700.0)
                if layer == 0:
                    load_mixer_weights(0, skip_in=True)
                P.op("dve", lambda e: e.tensor_scalar(out=WoutP[:, 0:4, :], in0=WoutP[:, 0:4, :], scalar1=0.5, scalar2=None,
                                                      op0=ALU.mult), reads=[("Wout", 0), ("Wout", 1)], writes=["Wout"])
                xnt = sb(st, "xnt", [128, 8, TM], BF16)
                uaL = [sb(st, "ua%d" % q, [128, 4, TM], BF16) for q in range(2)]
                ubL = [sb(st, "ub%d" % q, [128, 4, TM], BF16) for q in range(2)]
                vn = sb(st, "vn", [128, 512])
                vnbL = [sb(st, "vnb%d" % q, [128, 2, 512], BF16) for q in range(2)]
                vjunk = sb(st, "vjunk", [128, 512], BF16)
                vss = sb(st, "vss", [128, 2])
                ymix = sb(st, "ymix", [128, 8, TM], BF16)
                tA = sb(st, "tA", [128, 4, NCH]); tB = sb(st, "tB", [128, 4, NCH])
                Gin = sb(st, "Gin", [128, 4, 2, NCH])
                wtail = [WinP[:, k_, 1536:2048].bitcast(F32).rearrange("p (a c) -> p a c", a=4) for k_ in range(8)]
                tC, tD, tE, tF, tG, tH = wtail[0:6]
                GsL = [sb(st, "Gs0", [128, 4, 2, NCH]),
                       WinP[:, 6:8, 1536:2048].bitcast(F32).rearrange("p k (a c) -> p a k c", a=4)]
                Hf = sb(st, "Hf", [128, 4, 2, NCH + 1])
                Hb = sb(st, "Hb", [128, 4, 2, NCH], BF16)
                sqy = sb(st, "sqy", [128, TM])
                zf = sb(st, "zf", [128, 4, TM])
                zb = sb(st, "zb", [128, 4, TM], BF16)
                sg2 = sb(st, "sg2", [128, TM])
                stmp = sb(st, "stmp", [128, TM])
                vT = sb(st, "vT", [128, 4, NS])
                sacc = sb(st, "sacc", [128, 16, 4])
                h0s = sb(st, "h0s", [16, 1024])
                h0T = sb(st, "h0T", [128, 16, 2, 16])
                hend = sb(st, "hend", [128, 16, 2, 16])
                hoP = sb(st, "hoP", [16, 2, 128])
                def front(ti, t0, n, is_s):
                    nch = n // 4
                    par = ti % 2
                    ua = uaL[par]; ub = ubL[par]; vnb = vnbL[par]
                    hk = ("hres", t0)
                    rmsnorm_tile(st, "m", t0, n, gmix[:, layer, :], xnt, "xnt") if ti == 0 else \
                        rmsnorm_tile_again("m", t0, n, gmix[:, layer, :], xnt, "xnt")
                    for ft in range(4):
                        b = psget()
                        for k in range(8):
                            P.op("pe", lambda e, k=k, ft=ft, b=b: e.matmul(
                                PS[:, b, 0:n], lhsT=Win[:, k, ft * 128:(ft + 1) * 128], rhs=xnt[:, k, 0:n],
                                start=(k == 0), stop=(k == 7)), reads=["Win", "xnt"], writes=pk(b))
                        P.op("act", lambda e, ft=ft, b=b: e.activation(out=ua[:, ft, 0:n], in_=PS[:, b, 0:n],
                                                                       func=AF.Copy),
                             reads=pk(b), writes=[("ua", par, ft)])
                    for ft in range(4):
                        b = psget()
                        for k in range(8):
                            P.op("pe", lambda e, k=k, ft=ft, b=b: e.matmul(
                                PS[:, b, 0:n], lhsT=Win[:, k, 512 + ft * 128:512 + (ft + 1) * 128], rhs=xnt[:, k, 0:n],
                                start=(k == 0), stop=(k == 7)), reads=["Win", "xnt"], writes=pk(b))
                        P.op("act", lambda e, ft=ft, b=b: e.activation(out=ub[:, ft, 0:n], in_=PS[:, b, 0:n],
                                                                       func=AF.Copy),
                             reads=pk(b), writes=[("ub", par, ft)])
                    nsub = (n + 127) // 128
                    for sj in range(nsub):
                        m = min(128, n - sj * 128)
                        b = psget()
                        for k in range(8):
                            P.op("pe", lambda e, k=k, b=b, sj=sj, m=m: e.matmul(
                                PS[0:m, b, :], lhsT=xnt[:, k, sj * 128:sj * 128 + m], rhs=Win[:, k, 1024:1536],
                                start=(k == 0), stop=(k == 7)), reads=["Win", "xnt"], writes=pk(b))
                        P.op("act", lambda e, b=b, sj=sj, m=m: e.activation(
                            out=vjunk[0:m, :], in_=PS[0:m, b, :], func=AF.Square, accum_out=vss[0:m, sj:sj + 1]),
                            reads=pk(b), writes=["vjunk", ("vss", sj)])
                        P.op("act", lambda e, sj=sj, m=m: e.activation(
                            out=vss[0:m, sj:sj + 1], in_=vss[0:m, sj:sj + 1], func=AF.Ln, bias=epsc[0:m, 0:1],
                            scale=1.0 / 512.0), reads=[("vss", sj), "epsc"], writes=[("vss", sj)], cost=250.0)
                        P.op("act", lambda e, sj=sj, m=m: e.activation(
                            out=vss[0:m, sj:sj + 1], in_=vss[0:m, sj:sj + 1], func=AF.Exp, scale=-0.5),
                            reads=[("vss", sj)], writes=[("vss", sj)], cost=250.0)
                        P.op("dve", lambda e, b=b, sj=sj, m=m: e.scalar_tensor_tensor(
                            out=vnb[0:m, sj, :], in0=PS[0:m, b, :], scalar=vss[0:m, sj:sj + 1], in1=gsg[0:m, :],
                            op0=ALU.mult, op1=ALU.mult), reads=pk(b) + [("vss", sj), "gsg"], writes=[("vnb", par, sj)])
                        if is_s:
                            P.op("dve", lambda e, b=b, sj=sj, m=m: e.scalar_tensor_tensor(
                                out=vn[0:m, :], in0=PS[0:m, b, :], scalar=vss[0:m, sj:sj + 1], in1=gsg[0:m, :],
                                op0=ALU.mult, op1=ALU.mult), reads=pk(b) + [("vss", sj), "gsg"], writes=[("vn", 0)])
                    if is_s:
                        P.dma("sp", o_s_v[i], vn[0:NS, :], reads=[("vn", 0)])
                        for ri in range(2):
                            b = psget()
                            for hf in range(2):
                                P.dma("sp", h0s[:, :], (st_re if ri == 0 else st_im)[i][:, hf * 1024:(hf + 1) * 1024],
                                      writes=["h0s"])
                                for q in range(8):
                                    Pp = hf * 8 + q
                                    P.op("pe", lambda e, b=b, Pp=Pp, q=q: e.transpose(
                                        PS[:, b, Pp * 16:(Pp + 1) * 16], h0s[:, q * 128:(q + 1) * 128],
                                        ident[0:16, 0:16]), reads=["h0s", "ident"], writes=pk(b))
                            P.op("dve", lambda e, b=b, ri=ri: e.tensor_copy(
                                out=h0T[:, :, ri, :], in_=PS[:, b, 0:256].rearrange("p (a b) -> p a b", a=16)),
                                reads=pk(b), writes=["h0T"])
                def back(ti, t0, n, is_s):
                    nch = n // 4
                    par = ti % 2
                    ua = uaL[par]; ub = ubL[par]; vnb = vnbL[par]
                    for ft in range(4):
                        if not is_s:
                            P.op("pool", lambda e, ft=ft: e.tensor_copy(out=Hf[:, :, :, 0], in_=s5car[:, 4 * ft:4 * ft + 4, :]),
                                 reads=[("s5car", ft)], writes=["Hf0"])
                        b4 = psget(4)
                        for p4 in range(4):
                            for ri in range(2):
                                for s in range(4):
                                    P.op("pe", lambda e, p4=p4, ri=ri, s=s, ft=ft, b4=b4: e.matmul(
                                        PS[:, b4 + p4, ri * NCH:ri * NCH + nch],
                                        lhsT=XB[32 * p4:32 * p4 + 32, ft, ri, s, :],
                                        rhs=ua[32 * p4:32 * p4 + 32, ft, s:n:4],
                                        start=(s == 0), stop=(s == 3), tile_position=(32 * p4, 0)),
                                        reads=["XB", ("ua", par, ft)], writes=pk(b4, 4), cost=40.0)
                        Xr = PS[:, b4:b4 + 4, 0:nch]
                        Xi = PS[:, b4:b4 + 4, NCH:NCH + nch]
                        if not is_s:
                            Cc = TC[:, 4 * ft:4 * ft + 4, 0:nch]
                            Ss = TS[:, 4 * ft:4 * ft + 4, 0:nch]
                            tAa = tA[:, :, 0:nch]; tBb = tB[:, :, 0:nch]
                            GinR = Gin[:, :, 0, 0:nch]; GinI = Gin[:, :, 1, 0:nch]
                            x4 = pk(b4, 4)
                            gq = ft % 2
                            Gsq = GsL[gq]

                            def tt(out, a, bb, op, r, w, eng="dve"):
                                P.op(eng, lambda e: e.tensor_tensor(out=out, in0=a, in1=bb, op=op), reads=r, writes=w)
                            tCc = tC[:, :, 0:nch]; tDd = tD[:, :, 0:nch]
                            tt(tAa, Xr, Cc, ALU.mult, x4 + ["TC"], ["tA"])
                            tt(tBb, Xi, Ss, ALU.mult, x4 + ["TS"], ["tB"])
                            tt(tCc, Xi, Cc, ALU.mult, x4 + ["TC"], ["tC"])
                            tt(tDd, Xr, Ss, ALU.mult, x4 + ["TS"], ["tD"])
                            tt(GinR, tAa, tBb, ALU.add, ["tA", "tB"], ["GinR"])
                            tt(GinI, tCc, tDd, ALU.subtract, ["tC", "tD"], ["GinI"])
                            for p4 in range(4):
                                Pp = 4 * ft + p4
                                for ri in range(2):
                                    P.op("dve", lambda e, p4=p4, ri=ri, Pp=Pp, Gsq=Gsq: e.tensor_tensor_scan(
                                        out=Gsq[:, p4, ri, 0:nch], data0=R4[:, Pp:Pp + 1].broadcast_to([128, nch]),
                                        data1=Gin[:, p4, ri, 0:nch], initial=s5car[:, Pp, ri:ri + 1],
                                        op0=ALU.mult, op1=ALU.add),
                                        reads=["GinR" if ri == 0 else "GinI", "R4", ("s5car", ft)],
                                        writes=[("Gs", gq, p4, ri)], cost=350.0)
                            GR = Gsq[:, :, 0, 0:nch]; GI = Gsq[:, :, 1, 0:nch]
                            gk = [("Gs", gq, a_, b_) for a_ in range(4) for b_ in range(2)]
                            tEe = tE[:, :, 0:nch]; tFf = tF[:, :, 0:nch]; tGg = tG[:, :, 0:nch]; tHh = tH[:, :, 0:nch]
                            tt(tEe, GR, Cc, ALU.mult, gk + ["TC"], ["tE"], "pool")
                            tt(tFf, GI, Ss, ALU.mult, gk + ["TS"], ["tF"], "pool")
                            tt(tGg, GR, Ss, ALU.mult, gk + ["TS"], ["tG"], "pool")
                            tt(tHh, GI, Cc, ALU.mult, gk + ["TC"], ["tH"], "pool")
                            tt(Hf[:, :, 0, 1:nch + 1], tEe, tFf, ALU.subtract, ["tE", "tF", "Hf0"], ["HfR"], "pool")
                            tt(Hf[:, :, 1, 1:nch + 1], tGg, tHh, ALU.add, ["tG", "tH", "Hf0"], ["HfI"], "pool")
                            P.op("act", lambda e: e.activation(out=Hb[:, :, :, 0:nch], in_=Hf[:, :, :, 0:nch], func=AF.Copy),
                                 reads=["HfR", "HfI", "Hf0"], writes=["Hb"])
                            P.op("pool", lambda e, ft=ft: e.tensor_copy(out=s5car[:, 4 * ft:4 * ft + 4, :],
                                                                        in_=Hf[:, :, :, nch]),
                                 reads=["HfR", "HfI"] + gk, writes=[("s5car", ft)])
                        else:
                            h0r = h0T[:, 4 * ft:4 * ft + 4, 0, :]; h0i = h0T[:, 4 * ft:4 * ft + 4, 1, :]
                            a4r = A4[:, 4 * ft:4 * ft + 4, 0:1].broadcast_to([128, 4, 16])
                            a4i = A4[:, 4 * ft:4 * ft + 4, 1:2].broadcast_to([128, 4, 16])
                            tAa = tA[:, :, 0:16]; tBb = tB[:, :, 0:16]
                            x4 = pk(b4, 4)

                            def tt(out, a, bb, op, r, w):
                                P.op("dve", lambda e: e.tensor_tensor(out=out, in0=a, in1=bb, op=op), reads=r, writes=w)
                            tt(tAa, h0r, a4r, ALU.mult, ["h0T", "A4"], ["tA"])
                            tt(tBb, h0i, a4i, ALU.mult, ["h0T", "A4"], ["tB"])
                            tt(tAa, tAa, tBb, ALU.subtract, ["tA", "tB"], ["tA"])
                            tt(hend[:, 4 * ft:4 * ft + 4, 0, :], tAa, Xr, ALU.add, ["tA"] + x4, [("hend", ft, 0)])
                            tt(tAa, h0r, a4i, ALU.mult, ["h0T", "A4", ("hend", ft, 0)], ["tA"])
                            tt(tBb, h0i, a4r, ALU.mult, ["h0T", "A4", ("hend", ft, 0)], ["tB"])
                            tt(tAa, tAa, tBb, ALU.add, ["tA", "tB"], ["tA"])
                            tt(hend[:, 4 * ft:4 * ft + 4, 1, :], tAa, Xi, ALU.add, ["tA"] + x4, [("hend", ft, 1)])
                            P.op("act", lambda e, ft=ft: e.activation(out=Hb[:, :, :, 0:16],
                                                                      in_=h0T[:, 4 * ft:4 * ft + 4, :, :], func=AF.Copy),
                                 reads=["h0T"], writes=["Hb"])
                        by = psget()
                        for t in range(4):
                            o_ = PS[:, by, t * NCH:t * NCH + nch]
                            for tau in range(t + 1):
                                P.op("pe", lambda e, t=t, tau=tau, ft=ft, o_=o_: e.matmul(
                                    o_, lhsT=BD[:, ft, tau, :], rhs=ua[:, ft, (t - tau):n:4],
                                    start=(tau == 0), stop=False), reads=[("ua", par, ft)], writes=pk(by), cost=60.0)
                            for p4 in range(4):
                                for ri in range(2):
                                    last = (ri == 1)
                                    P.op("pe", lambda e, t=t, p4=p4, ri=ri, ft=ft, by=by, last=last: e.matmul(
                                        PS[32 * p4:32 * p4 + 32, by, t * NCH:t * NCH + nch],
                                        lhsT=YC[:, 4 * ft + p4, ri, t, :], rhs=Hb[:, p4, ri, 0:nch],
                                        start=False, stop=last, tile_position=(0, 32 * p4)),
                                        reads=["Hb"], writes=pk(by), cost=45.0)
                        yv = PS[:, by, 0:4 * NCH].rearrange("p (t c) -> p c t", t=4)[:, 0:nch, :]
                        sq3 = sqy[:, 0:n].rearrange("p (c t) -> p c t", t=4)
                        z3 = zf[:, ft, 0:n].rearrange("p (c t) -> p c t", t=4)
                        P.op("act", lambda e, yv=yv, z3=z3: e.activation(out=z3, in_=yv, func=AF.Gelu_apprx_tanh),
                             reads=pk(by), writes=[("zf", ft)])
                        P.op("act", lambda e, ft=ft: e.activation(out=zb[:, ft, 0:n], in_=zf[:, ft, 0:n], func=AF.Copy),
                             reads=[("zf", ft)], writes=[("zb", ft)])
                    for fo in range(4):
                        b = psget()
                        for fi in range(4):
                            P.op("pe", lambda e, fi=fi, fo=fo, b=b: e.matmul(
                                PS[:, b, 0:n], lhsT=Wglu[:, fi, fo * 128:(fo + 1) * 128], rhs=zb[:, fi, 0:n],
                                start=(fi == 0), stop=(fi == 3)), reads=["Wsm", ("Wsm", 0)] + [("zb", q) for q in range(4)],
                                writes=pk(b))
                        P.op("act", lambda e, fo=fo, b=b: e.activation(out=sg2[:, 0:n], in_=PS[:, b, 0:n], func=AF.Tanh,
                                                                       bias=glubh[:, fo:fo + 1], scale=0.5),
                             reads=pk(b) + ["glubh"], writes=["sg2"])
                        P.op("dve", lambda e, fo=fo: e.scalar_tensor_tensor(out=ymix[:, fo, 0:n], in0=sg2[:, 0:n], scalar=1.0,
                                                                            in1=zf[:, fo, 0:n], op0=ALU.add, op1=ALU.mult),
                             reads=["sg2", ("zf", fo)], writes=["ymix"])
                    if not is_s:
                        for hp in range(4):
                            b = psget()
                            for j in range(n // 128):
                                for h2 in range(2):
                                    h = 2 * hp + h2
                                    P.op("pe", lambda e, b=b, j=j, h2=h2, h=h: e.matmul(
                                        PS[64 * h2:64 * h2 + 64, b, j * 128:(j + 1) * 128],
                                        lhsT=vnb[:, j, h * 64:(h + 1) * 64], rhs=wT[:, h, :],
                                        start=True, stop=True, tile_position=(0, 64 * h2)),
                                        reads=["wT", ("vnb", par, j)], writes=pk(b))
                            P.op("dve", lambda e, b=b, hp=hp: e.tensor_tensor(
                                out=stmp[:, 0:n].rearrange("p (j i) -> p j i", i=128),
                                in0=PS[:, b, 0:n].rearrange("p (j i) -> p j i", i=128),
                                in1=bbc[:, hp, :].unsqueeze(1).broadcast_to([128, n // 128, 128]), op=ALU.add),
                                reads=pk(b) + ["bbc"], writes=["stmp"])
                            P.op("dve", lambda e, hp=hp: e.tensor_tensor(out=ymix[:, 4 + hp, 0:n], in0=stmp[:, 0:n],
                                                                         in1=ub[:, hp, 0:n], op=ALU.mult),
                                 reads=["stmp", ("ub", par, hp)], writes=["ymix"])
                    else:
                        b = psget()
                        for ft in range(4):
                            P.op("pe", lambda e, b=b, ft=ft: e.transpose(PS[:, b, ft * NS:(ft + 1) * NS],
                                                                         vn[0:NS, ft * 128:(ft + 1) * 128],
                                                                         ident[0:NS, 0:NS]),
                                 reads=[("vn", 0), "ident"], writes=pk(b))
                        P.op("dve", lambda e, b=b: e.tensor_copy(out=vT[:], in_=PS[:, b, 0:4 * NS].rearrange("p (f t) -> p f t", f=4)),
                             reads=pk(b), writes=["vT"])
                        for ft in range(4):
                            v3 = vT[:, ft, :].rearrange("p (b j) -> p b j", j=4)
                            for ii in range(4):
                                P.op("dve", lambda e, ft=ft, ii=ii, v3=v3: e.tensor_scalar(
                                    out=sacc[:, :, ii], in0=v3[:, :, 0], scalar1=wsc[:, ft, 4 * ii:4 * ii + 1],
                                    scalar2=bbc[:, ft, ii:ii + 1], op0=ALU.mult, op1=ALU.add),
                                    reads=["vT", "wsc", "bbc"], writes=["sacc"])
                                for jj in range(1, ii + 1):
                                    P.op("dve", lambda e, ft=ft, ii=ii, jj=jj, v3=v3: e.scalar_tensor_tensor(
                                        out=sacc[:, :, ii], in0=v3[:, :, jj], scalar=wsc[:, ft, 4 * ii + jj:4 * ii + jj + 1],
                                        in1=sacc[:, :, ii], op0=ALU.mult, op1=ALU.add),
                                        reads=["vT", "wsc", "sacc"], writes=["sacc"])
                            P.op("dve", lambda e, ft=ft: e.tensor_tensor(
                                out=ymix[:, 4 + ft, 0:NS], in0=sacc[:].rearrange("p b i -> p (b i)"),
                                in1=ub[:, ft, 0:NS], op=ALU.mult), reads=["sacc", ("ub", par, ft)], writes=["ymix"])
                    out_proj_tile(Wout, "Wout", ymix, "ymix", t0, n)
                    if is_s:
                        for ri in range(2):
                            for hf in range(2):
                                for h2 in range(2):
                                    half = hf * 2 + h2
                                    b = psget()
                                    for q in range(4):
                                        Pp = half * 4 + q
                                        P.op("pe", lambda e, b=b, q=q, Pp=Pp, ri=ri: e.transpose(
                                            PS[0:16, b, q * 128:(q + 1) * 128], hend[:, Pp, ri, :], ident[:]),
                                            reads=[("hend", Pp // 4, ri), "ident"], writes=pk(b))
                                    P.op("act", lambda e, b=b, h2=h2: e.activation(
                                        out=h0s[:, h2 * 512:(h2 + 1) * 512], in_=PS[0:16, b, :], func=AF.Copy),
                                        reads=pk(b), writes=["h0s"])
                                P.dma("sp", (o_s_re if ri == 0 else o_s_im)[i][:, hf * 1024:(hf + 1) * 1024], h0s[:, :],
                                      reads=["h0s"])
                    if (not is_s) and t0 + n == SEQ:
                        for ri in range(2):
                            b = psget()
                            P.op("pe", lambda e, b=b, ri=ri: e.transpose(PS[0:16, b, 0:128], s5car[:, :, ri], ident[:]),
                                 reads=[("s5car", q_) for q_ in range(4)] + ["ident"], writes=pk(b))
                            P.op("act", lambda e, b=b, ri=ri: e.activation(out=hoP[:, ri, :], in_=PS[0:16, b, 0:128], func=AF.Copy),
                                 reads=pk(b), writes=["hoP"])
                        P.dma("sp", o_p_re[i], hoP[:, 0, :], reads=["hoP"])
                        P.dma("sp", o_p_im[i], hoP[:, 1, :], reads=["hoP"])
                seq = list(enumerate(mtiles))
                for idx, (ti, (t0, n, is_s)) in enumerate(seq):
                    front(ti, t0, n, is_s)
                    if idx >= 1:
                        pti, (pt0, pn, ps_) = seq[idx - 1]
                        back(pti, pt0, pn, ps_)
                lti, (lt0, ln, ls_) = seq[-1]
                back(lti, lt0, ln, ls_)
                P.flush()

        _norm_scr = {}

        def rmsnorm_tile_again(tag, t0, n, gvec, xn_out, xn_key):
            _rms_ops(tag, t0, n, gvec, xn_out, xn_key, _norm_scr[tag])

        def _rms_ops(tag, t0, n, gvec, xn_out, xn_key, srs, xoff=0):
            sr, sr2 = srs
            hk = hkeys(t0, n)
            sqv = xn_out[:, :, xoff:xoff + n]
            P.op("act", lambda e: e.activation(out=sqv, in_=hres[:, :, t0:t0 + n], func=AF.Square),
                 reads=hk, writes=[xn_key])
            b = psget()
            for k in range(8):
                P.op("pe", lambda e, k=k, b=b: e.matmul(PS[:, b, 0:n], lhsT=onesb[:], rhs=xn_out[:, k, xoff:xoff + n],
                                                        start=(k == 0), stop=(k == 7)),
                     reads=[xn_key, "onesb"], writes=pk(b))
            P.op("act", lambda e, b=b: e.activation(out=sr[:, 0:n], in_=PS[:, b, 0:n], func=AF.Ln,
                                                    bias=epsc[:, 0:1], scale=1.0 / D),
                 reads=pk(b) + ["epsc"], writes=["sr_" + tag])
            P.op("act", lambda e: e.activation(out=sr[:, 0:n], in_=sr[:, 0:n], func=AF.Exp, scale=-0.5),
                 reads=["sr_" + tag], writes=["sr_" + tag])
            if tag != "f":
                P.op("dve", lambda e: e.tensor_tensor(
                    out=xn_out[:, :, xoff:xoff + n], in0=hres[:, :, t0:t0 + n],
                    in1=sr2[:, 0:n].unsqueeze(1).broadcast_to([128, 8, n]), op=ALU.mult),
                    reads=hk + ["sr_" + tag], writes=[xn_key], cost=(160 + 8 * n) / 0.96)
                return
            for k in range(8):
                P.op("dve", lambda e, k=k: e.scalar_tensor_tensor(
                    out=xn_out[:, k, xoff:xoff + n], in0=hres[:, k, t0:t0 + n], scalar=gvec[:, k:k + 1],
                    in1=sr2[:, 0:n], op0=ALU.mult, op1=ALU.mult),
                    reads=hk + ["sr_" + tag, "gmix", "gffn"], writes=[xn_key])

        def rmsnorm_tile(stk, tag, t0, n, gvec, xn_out, xn_key, xoff=0):
            nmax = TM if tag == "m" else TF
            sr = sb(stk, "sr_" + tag, [128, nmax])
            sr2 = sr
            _norm_scr[tag] = (sr, sr2)
            _rms_ops(tag, t0, n, gvec, xn_out, xn_key, (sr, sr2), xoff=xoff)

        def out_proj_tile(Wout, wkey, ymix, ykey, t0, n):
            hk = hkeys(t0, n)
            for fo in range(8):
                b = psget()
                for k in range(8):
                    P.op("pe", lambda e, k=k, fo=fo, b=b: e.matmul(
                        PS[:, b, 0:n], lhsT=Wout[:, k, fo * 128:(fo + 1) * 128], rhs=ymix[:, k, 0:n],
                        start=(k == 0), stop=(k == 7)), reads=[wkey, ykey], writes=pk(b))
                P.op("dve", lambda e, fo=fo, b=b: e.tensor_tensor(
                    out=hres[:, fo, t0:t0 + n], in0=hres[:, fo, t0:t0 + n], in1=PS[:, b, 0:n], op=ALU.add),
                    reads=pk(b) + hk, writes=hk)

        def odd_mixer(layer):
            i = layer // 2
            P.cost.update({"pe": 115.0, "dve": 430.0, "act": 450.0})
            with ExitStack() as st:
                Win = WinP
                Wout = WoutP
                Wp = WsmP[:, 0:512].rearrange("p (g d) -> p g d", g=4)
                for k_ in range(8):
                    P.op("dve", lambda e, k_=k_: e.tensor_scalar(out=WinP[:, k_, :], in0=WinP[:, k_, :],
                                                                 scalar1=gmix[:, layer, k_:k_ + 1], scalar2=None, op0=ALU.mult),
                         writes=["Win"], cost=850.0)
                xnt = sb(st, "xnto", [128, 8, TM], BF16)
                XCL = [sb(st, "XC%d" % q, [128, 4, 15 + TM]) for q in range(2)]
                PA = sb(st, "PA", [128, 15 + TM]); PB = sb(st, "PB", [128, 15 + TM])
                diff = sb(st, "diff", [128, 4, TM], BF16)
                xdL = [sb(st, "xd%d" % q, [128, 4, TM]) for q in range(2)]
                bgL = [sb(st, "bg%d" % q, [128, 4, TM]) for q in range(2)]
                ZL = [sb(st, "Z%d" % q, [128, 4, 2 + TM]) for q in range(2)]
                ca = sb(st, "ca", [128, TM])
                ymix = sb(st, "ymixo", [128, 8, TM], BF16)
                invn = sb(st, "invn", [128, 4, 15])
                XCs = sb(st, "XCs", [128, 4, 16, 19])
                PAs = sb(st, "PAs", [128, 16, 19]); PBs = sb(st, "PBs", [128, 16, 19])
                Zs = sb(st, "Zs", [128, 4, 16, 6])
                spl = [sb(st, "spl%d" % q, [128, 512]) for q in range(2)]
                scl = sb(st, "scl", [32, 512])
                otp = sb(st, "otp", [128, 512])
                otc = sb(st, "otc", [32, 512])
                opp = sb(st, "opp", [16, 512])
                opc = sb(st, "opc", [2, 512])
                xct = sb(st, "xct", [128, 128])
                zct = sb(st, "zct", [128, 32])
                P.op("pool", lambda e: e.iota(invn[:], pattern=[[0, 4], [1, 15]], base=1, channel_multiplier=0,
                                              allow_small_or_imprecise_dtypes=True), writes=["invn"])
                for gi in range(4):
                    P.op("dve", lambda e, gi=gi: e.tensor_scalar(out=invn[:, gi, :], in0=invn[:, gi, :],
                                                                 scalar1=float(2 ** (gi + 1)), scalar2=None, op0=ALU.min),
                         reads=["invn"], writes=["invn"])
                P.op("dve", lambda e: e.reciprocal(out=invn[:], in_=invn[:]), reads=["invn"], writes=["invn"])
                P.op("dve", lambda e: e.memset(xchalo[:], 0.0), writes=["xchalo"])
                P.op("dve", lambda e: e.memset(zhalo[:], 0.0), writes=["zhalo"])
                P.op("pool", lambda e: e.memset(PA[:], 0.0), writes=["P0"])
                P.op("pool", lambda e: e.memset(PB[:], 0.0), writes=["P1"])
                P.op("pool", lambda e: e.memset(PAs[:], 0.0), writes=["Ps0"])
                P.op("pool", lambda e: e.memset(PBs[:], 0.0), writes=["Ps1"])
                def front(ti, t0, n, is_s):
                    par = ti % 2
                    XC = XCL[par]; xd = xdL[par]; bg = bgL[par]; Z = ZL[par]
                    if ti == 0:
                        rmsnorm_tile(st, "m", t0, n, gmix[:, layer, :], xnt, "xnt")
                    else:
                        rmsnorm_tile_again("m", t0, n, gmix[:, layer, :], xnt, "xnt")
                    if not is_s:
                        P.op("dve", lambda e: e.tensor_copy(out=XC[:, :, 0:15], in_=xchalo[:]), reads=["xchalo"],
                             writes=[("XChalo", par)])
                        P.op("dve", lambda e: e.tensor_copy(out=Z[:, :, 0:2], in_=zhalo[:]), reads=["zhalo"],
                             writes=[("Zhalo", par)])
                    else:
                        P.dma("sp", spl[0][:], st_pool[i].rearrange("b r c -> (b r) c")[0:128, :], writes=["spl0"])
                        P.dma("sp", spl[1][0:112, :], st_pool[i].rearrange("b r c -> (b r) c")[128:240, :], writes=["spl1"])
                        P.dma("sp", scl[:], st_conv[i].rearrange("b r c -> (b r) c"), writes=["scl"])
                        for ft in range(4):
                            b = psget()
                            P.op("pe", lambda e, b=b, ft=ft: e.transpose(PS[:, b, 0:128], spl[0][:, ft * 128:(ft + 1) * 128], ident[:]),
                                 reads=["spl0", "ident"], writes=pk(b))
                            P.op("pe", lambda e, b=b, ft=ft: e.transpose(PS[:, b, 128:240], spl[1][0:112, ft * 128:(ft + 1) * 128],
                                                                         ident[0:112, 0:112]),
                                 reads=["spl1", "ident"], writes=pk(b))
                            P.op("pe", lambda e, b=b, ft=ft: e.transpose(PS[:, b, 256:288], scl[:, ft * 128:(ft + 1) * 128],
                                                                         ident[0:32, 0:32]),
                                 reads=["scl", "ident"], writes=pk(b))
                            P.op("dve", lambda e, b=b, ft=ft: e.tensor_copy(
                                out=XCs[:, ft, :, 0:15], in_=PS[:, b, 0:240].rearrange("p (b r) -> p b r", r=15)),
                                reads=pk(b), writes=[("XCs", ft)])
                            P.op("dve", lambda e, b=b, ft=ft: e.tensor_copy(
                                out=Zs[:, ft, :, 0:2], in_=PS[:, b, 256:288].rearrange("p (b r) -> p b r", r=2)),
                                reads=pk(b), writes=[("Zs", ft)])
                    for ft in range(4):
                        b = psget()
                        for k in range(8):
                            P.op("pe", lambda e, k=k, ft=ft, b=b: e.matmul(
                                PS[:, b, 0:n], lhsT=Win[:, k, ft * 128:(ft + 1) * 128], rhs=xnt[:, k, 0:n],
                                start=(k == 0), stop=(k == 7)), reads=["Win", "xnt"], writes=pk(b))
                        if not is_s:
                            P.op("act", lambda e, ft=ft, b=b: e.activation(out=XC[:, ft, 15:15 + n], in_=PS[:, b, 0:n], func=AF.Copy),
                                 reads=pk(b), writes=[("XC", par, ft)])
                        else:
                            P.op("act", lambda e, ft=ft, b=b: e.activation(
                                out=XCs[:, ft, :, 15:19], in_=PS[:, b, 0:NS].rearrange("p (b t) -> p b t", t=4), func=AF.Copy),
                                reads=pk(b) + [("XCs", ft)], writes=[("XCs", ft)])
                    for ft in range(4):
                        b = psget()
                        for k in range(8):
                            P.op("pe", lambda e, k=k, ft=ft, b=b: e.matmul(
                                PS[:, b, 0:n], lhsT=Win[:, k, 512 + ft * 128:512 + (ft + 1) * 128], rhs=xnt[:, k, 0:n],
                                start=(k == 0), stop=(k == 7)), reads=["Win", "xnt"], writes=pk(b))
                        P.op("act", lambda e, ft=ft, b=b: e.activation(out=xd[:, ft, 0:n], in_=PS[:, b, 0:n], func=AF.Copy),
                             reads=pk(b), writes=[("xd", par, ft)])
                    for ft in range(4):
                        b = psget()
                        for k in range(8):
                            P.op("pe", lambda e, k=k, ft=ft, b=b: e.matmul(
                                PS[:, b, 0:n], lhsT=Win[:, k, 1024 + ft * 128:1024 + (ft + 1) * 128], rhs=xnt[:, k, 0:n],
                                start=(k == 0), stop=(k == 7)), reads=["Win", "xnt"], writes=pk(b))
                        P.op("act", lambda e, ft=ft, b=b: e.activation(out=bg[:, ft, 0:n], in_=PS[:, b, 0:n], func=AF.Copy),
                             reads=pk(b), writes=[("bg", par, ft)])
                    for ft in range(4):
                        b = psget()
                        for k in range(8):
                            P.op("pe", lambda e, k=k, ft=ft, b=b: e.matmul(
                                PS[:, b, 0:n], lhsT=Win[:, k, 1536 + ft * 128:1536 + (ft + 1) * 128], rhs=xnt[:, k, 0:n],
                                start=(k == 0), stop=(k == 7)), reads=["Win", "xnt"], writes=pk(b))
                        if not is_s:
                            P.op("dve", lambda e, ft=ft, b=b: e.tensor_tensor(out=Z[:, ft, 2:2 + n], in0=PS[:, b, 0:n],
                                                                              in1=xd[:, ft, 0:n], op=ALU.mult),
                                 reads=pk(b) + [("xd", par, ft)], writes=[("Z", par, ft)])
                        else:
                            P.op("dve", lambda e, ft=ft, b=b: e.tensor_tensor(
                                out=Zs[:, ft, :, 2:6], in0=PS[:, b, 0:NS].rearrange("p (b t) -> p b t", t=4),
                                in1=xd[:, ft, 0:NS].rearrange("p (b t) -> p b t", t=4), op=ALU.mult),
                                reads=pk(b) + [("xd", par, ft), ("Zs", ft)], writes=[("Zs", ft)])
                    if not is_s:
                        P.op("dve", lambda e: e.tensor_copy(out=xchalo[:], in_=XC[:, :, n:n + 15]),
                             reads=[("XC", par, q) for q in range(4)] + [(("XChalo", par), par)], writes=["xchalo"])
                        P.op("dve", lambda e: e.tensor_copy(out=zhalo[:], in_=Z[:, :, n:n + 2]),
                             reads=[("Z", par, q) for q in range(4)] + [(("Zhalo", par), par)], writes=["zhalo"])
                def back(ti, t0, n, is_s):
                    par = ti % 2
                    XC = XCL[par]; xd = xdL[par]; bg = bgL[par]; Z = ZL[par]
                    for gi in range(4):
                        w = 2 ** (gi + 1)
                        if not is_s:
                            L = 15 + n
                            src = XC[:, gi, 0:L]
                            bufs = [PA, PB]
                            cur = src
                            ckey = [("XC", par, gi), ("XChalo", par)]
                            d = 1
                            q = 0
                            while d < w:
                                dst = bufs[q % 2]
                                dk = "P%d" % (q % 2)
                                P.op("pool", lambda e, cur=cur, dst=dst, d=d, L=L: e.tensor_tensor(
                                    out=dst[:, d:L], in0=cur[:, d:L], in1=cur[:, 0:L - d], op=ALU.add),
                                    reads=ckey, writes=[dk], cost=800.0)
                                cur = dst[:, 0:L]
                                ckey = [dk]
                                d *= 2
                                q += 1
                            P.op("dve", lambda e, cur=cur, gi=gi, w=w: e.scalar_tensor_tensor(
                                out=diff[:, gi, 0:n], in0=cur[:, 15:15 + n], scalar=1.0 / w, in1=XC[:, gi, 15:15 + n],
                                op0=ALU.mult, op1=ALU.subtract), reads=ckey + [("XC", par, gi)], writes=[("diff", gi)])
                            if t0 == 0:
                                P.op("dve", lambda e, cur=cur, gi=gi: e.tensor_tensor(
                                    out=ca[:, 0:15], in0=cur[:, 15:30], in1=invn[:, gi, :], op=ALU.mult),
                                    reads=ckey + ["invn"], writes=["ca"])
                                P.op("dve", lambda e, gi=gi: e.tensor_tensor(
                                    out=diff[:, gi, 0:15], in0=ca[:, 0:15], in1=XC[:, gi, 15:30], op=ALU.subtract),
                                    reads=["ca", ("XC", par, gi), ("diff", gi)], writes=[("diff", gi)])
                        else:
                            L = 19
                            cur = XCs[:, gi, :, :]
                            ckey = [("XCs", gi)]
                            bufs = [PAs, PBs]
                            d = 1
                            q = 0
                            while d < w:
                                dst = bufs[q % 2]
                                dk = "Ps%d" % (q % 2)
                                P.op("pool", lambda e, cur=cur, dst=dst, d=d: e.tensor_tensor(
                                    out=dst[:, :, d:19], in0=cur[:, :, d:19], in1=cur[:, :, 0:19 - d], op=ALU.add),
                                    reads=ckey, writes=[dk], cost=800.0)
                                cur = dst[:, :, :]
                                ckey = [dk]
                                d *= 2
                                q += 1
                            P.op("dve", lambda e, cur=cur, gi=gi, w=w: e.scalar_tensor_tensor(
                                out=diff[:, gi, 0:NS].rearrange("p (b t) -> p b t", t=4), in0=cur[:, :, 15:19],
                                scalar=1.0 / w, in1=XCs[:, gi, :, 15:19], op0=ALU.mult, op1=ALU.subtract),
                                reads=ckey + [("XCs", gi)], writes=[("diff", gi)])
                        b = psget()
                        P.op("pe", lambda e, gi=gi, b=b: e.matmul(PS[:, b, 0:n], lhsT=Wp[:, gi, :], rhs=diff[:, gi, 0:n],
                                                                  start=True, stop=True),
                             reads=["Wsm", ("diff", gi)], writes=pk(b))
                        P.op("act", lambda e, gi=gi, b=b: e.activation(out=ymix[:, gi, 0:n], in_=PS[:, b, 0:n], func=AF.Copy,
                                                                       scale=pscale[:, i, gi:gi + 1]),
                             reads=pk(b) + ["pscale"], writes=["ymix"])
                    for ft in range(4):
                        if not is_s:
                            z0 = Z[:, ft, 0:n]; z1 = Z[:, ft, 1:n + 1]; z2 = Z[:, ft, 2:n + 2]
                            cav = ca[:, 0:n]
                            bgv = bg[:, ft, 0:n]
                            yv = ymix[:, 4 + ft, 0:n]
                            zk = [("Z", par, ft), ("Zhalo", par)]
                        else:
                            z0 = Zs[:, ft, :, 0:4]; z1 = Zs[:, ft, :, 1:5]; z2 = Zs[:, ft, :, 2:6]
                            cav = ca[:, 0:NS].rearrange("p (b t) -> p b t", t=4)
                            bgv = bg[:, ft, 0:NS].rearrange("p (b t) -> p b t", t=4)
                            yv = ymix[:, 4 + ft, 0:NS].rearrange("p (b t) -> p b t", t=4)
                            zk = [("Zs", ft)]
                        P.op("act", lambda e, ft=ft, z0=z0, cav=cav: e.activation(
                            out=cav, in_=z0, func=AF.Identity, scale=cw[:, i, 0, ft:ft + 1], bias=cb[:, i, ft:ft + 1]),
                            reads=zk + ["cw", "cb"], writes=["ca"])
                        P.op("dve", lambda e, ft=ft, z1=z1, cav=cav: e.scalar_tensor_tensor(
                            out=cav, in0=z1, scalar=cw[:, i, 1, ft:ft + 1], in1=cav, op0=ALU.mult, op1=ALU.add),
                            reads=zk + ["cw", "ca"], writes=["ca"])
                        P.op("dve", lambda e, ft=ft, z2=z2, cav=cav: e.scalar_tensor_tensor(
                            out=cav, in0=z2, scalar=cw[:, i, 2, ft:ft + 1], in1=cav, op0=ALU.mult, op1=ALU.add),
                            reads=zk + ["cw", "ca"], writes=["ca"])
                        P.op("dve", lambda e, cav=cav, bgv=bgv, yv=yv: e.tensor_tensor(out=yv, in0=cav, in1=bgv, op=ALU.mult),
                             reads=["ca", ("bg", par, ft)], writes=["ymix"])
                    out_proj_tile(Wout, "Wout", ymix, "ymix", t0, n)
                    if (not is_s) and t0 + n == SEQ:
                        for ft in range(4):
                            b = psget()
                            P.op("pe", lambda e, b=b, ft=ft: e.transpose(PS[0:15, b, 0:128], xchalo[:, ft, :], ident[:]),
                                 reads=["xchalo", "ident"], writes=pk(b))
                            P.op("pe", lambda e, b=b, ft=ft: e.transpose(PS[0:2, b, 128:256], zhalo[:, ft, :], ident[:]),
                                 reads=["zhalo", "ident"], writes=pk(b))
                            P.op("act", lambda e, b=b, ft=ft: e.activation(out=opp[0:15, ft * 128:(ft + 1) * 128],
                                                                           in_=PS[0:15, b, 0:128], func=AF.Copy),
                                 reads=pk(b), writes=["opp"])
                            P.op("act", lambda e, b=b, ft=ft: e.activation(out=opc[0:2, ft * 128:(ft + 1) * 128],
                                                                           in_=PS[0:2, b, 128:256], func=AF.Copy),
                                 reads=pk(b), writes=["opc"])
                        P.dma("sp", o_p_pool[i], opp[0:15, :], reads=["opp"])
                        P.dma("sp", o_p_conv[i], opc[0:2, :], reads=["opc"])
                    if is_s:
                        for half in range(2):
                            for ft in range(4):
                                P.op("dve", lambda e, ft=ft, half=half: e.tensor_copy(
                                    out=xct[:, 0:120].rearrange("p (b r) -> p b r", r=15),
                                    in_=XCs[:, ft, half * 8:half * 8 + 8, 4:19]), reads=[("XCs", ft)], writes=["xct"])
                                b = psget()
                                P.op("pe", lambda e, b=b: e.transpose(PS[0:120, b, 0:128], xct[:, 0:120], ident[:]),
                                     reads=["xct", "ident"], writes=pk(b))
                                P.op("act", lambda e, b=b, ft=ft: e.activation(out=otp[0:120, ft * 128:(ft + 1) * 128],
                                                                               in_=PS[0:120, b, 0:128], func=AF.Copy),
                                     reads=pk(b), writes=["otp"])
                            P.dma("sp", o_s_pool[i, half * 120:half * 120 + 120, :], otp[0:120, :], reads=["otp"])
                        for ft in range(4):
                            P.op("dve", lambda e, ft=ft: e.tensor_copy(
                                out=zct[:, 0:32].rearrange("p (b r) -> p b r", r=2), in_=Zs[:, ft, :, 4:6]),
                                reads=[("Zs", ft)], writes=["zct"])
                            b = psget()
                            P.op("pe", lambda e, b=b: e.transpose(PS[0:32, b, 0:128], zct[:, 0:32], ident[:]),
                                 reads=["zct", "ident"], writes=pk(b))
                            P.op("act", lambda e, b=b, ft=ft: e.activation(out=otc[:, ft * 128:(ft + 1) * 128],
                                                                           in_=PS[0:32, b, 0:128], func=AF.Copy),
                                 reads=pk(b), writes=["otc"])
                        P.dma("sp", o_s_conv[i], otc[:, :], reads=["otc"])
                seq = list(enumerate(mtiles))
                for idx, (ti, (t0, n, is_s)) in enumerate(seq):
                    front(ti, t0, n, is_s)
                    if idx >= 1:
                        pti, (pt0, pn, ps_) = seq[idx - 1]
                        back(pti, pt0, pn, ps_)
                lti, (lt0, ln, ls_) = seq[-1]
                back(lti, lt0, ln, ls_)
                P.flush()

        def epilogue(st):
            gfin = WinP[:, 2, :].bitcast(F32)
            P.dma("sp", gfin, norm_final.broadcast_to([128, D]), writes=["gfin"])
            junk = sb(st, "fjunk", [128, 512], BF16)
            ss = sb(st, "fss", [128, 2, 2])
            yo = [WinP[:, q, :].bitcast(F32) for q in range(2)]
            nsub = SEQ // 128 + 1
            for si in range(nsub):
                n = 128 if si < SEQ // 128 else NS
                dst = y_p[si * 128:(si + 1) * 128, :] if si < SEQ // 128 else y_s[:, :]
                par = si % 2
                yb = yo[par]
                yk = "yo%d" % par
                hk = hkeys(si * 128, n)
                bb = psget(2)
                for k in range(8):
                    P.op("pe", lambda e, k=k, bb=bb, si=si, n=n: e.transpose(
                        PS[0:n, bb + k // 4, (k % 4) * 128:(k % 4 + 1) * 128], hres[:, k, si * 128:si * 128 + n], ident[:]),
                        reads=["ident"] + hk, writes=pk(bb, 2), cost=110.0)
                for half in range(2):
                    P.op("act", lambda e, bb=bb, half=half, n=n, par=par: e.activation(
                        out=junk[0:n, :], in_=PS[0:n, bb + half, :], func=AF.Square, accum_out=ss[0:n, par, half:half + 1]),
                        reads=pk(bb, 2), writes=["fjunk", ("fss", par, half)])
                P.op("dve", lambda e, n=n, par=par: e.tensor_tensor(out=ss[0:n, par, 0:1], in0=ss[0:n, par, 0:1],
                                                                     in1=ss[0:n, par, 1:2], op=ALU.add),
                     reads=[("fss", par, 0), ("fss", par, 1)], writes=[("fss", par, 0)], cost=100.0)
                P.op("act", lambda e, n=n, par=par: e.activation(out=ss[0:n, par, 0:1], in_=ss[0:n, par, 0:1], func=AF.Sqrt,
                                                                  bias=epsc[0:n, 0:1], scale=1.0 / D),
                     reads=[("fss", par, 0)], writes=[("fss", par, 0)], cost=250.0)
                P.op("dve", lambda e, n=n, par=par: e.reciprocal(out=ss[0:n, par, 0:1], in_=ss[0:n, par, 0:1]),
                     reads=[("fss", par, 0)], writes=[("fss", par, 0)], cost=100.0)
                for half in range(2):
                    P.op("dve", lambda e, bb=bb, half=half, n=n, yb=yb, par=par: e.scalar_tensor_tensor(
                        out=yb[0:n, half * 512:(half + 1) * 512], in0=PS[0:n, bb + half, :], scalar=ss[0:n, par, 0:1],
                        in1=gfin[0:n, half * 512:(half + 1) * 512], op0=ALU.mult, op1=ALU.mult),
                        reads=pk(bb, 2) + [("fss", par, 0), "gfin"], writes=[yk], cost=750.0)
                P.dma("sp", dst, yb[0:n, :], reads=[yk])

        def ffn(layer):
            widths = [384] * 7 + [128]
            offs = [sum(widths[:j]) for j in range(len(widths))]
            with ExitStack() as st:
                xn = sb(st, "xn_all", [128, 8, T], BF16)
                Wg = [sb(st, "Wg%d" % q, [128, 8, 384], BF16) for q in range(2)]
                Wu = [sb(st, "Wu%d" % q, [128, 8, 384], BF16) for q in range(2)]
                Wd = [sb(st, "Wd%d" % q, [128, 3, D], BF16) for q in range(2)]
                sl = [sb(st, "sl%d" % q, [128, TF]) for q in range(2)]
                hb = [sb(st, "hb%d" % q, [128, 3, TF], BF16) for q in range(2)]

                P.cost.update({"pe": 195.0, "dve": 630.0, "act": 560.0})

                def load_slice(j):
                    q = j % 2
                    w = widths[j]
                    o = offs[j]
                    c = 2500.0 + 128 * 8 * w * 4 / 150.0
                    for kh in range(2):
                        P.dma("pool", Wg[q][:, 4 * kh:4 * kh + 4, 0:w],
                              ffn_g[layer].rearrange("(k p) n -> p k n", p=128)[:, 4 * kh:4 * kh + 4, o:o + w],
                              writes=[("Wg", q, kh)], cost=c / 2)
                    for kh in range(2):
                        P.dma("pool", Wu[q][:, 4 * kh:4 * kh + 4, 0:w],
                              ffn_u[layer].rearrange("(k p) n -> p k n", p=128)[:, 4 * kh:4 * kh + 4, o:o + w],
                              writes=[("Wu", q, kh)], cost=c / 2)
                    P.dma("pool", Wd[q][:, 0:w // 128, :],
                          ffn_d[layer].rearrange("(k p) n -> p k n", p=128)[:, o // 128:(o + w) // 128, :],
                          writes=[("Wd", q)], cost=c)
                load_slice(0)
                load_slice(1)
                if layer + 1 < 4:
                    load_mixer_weights(layer + 1)
                for ti, (t0, n, is_s) in enumerate(ftiles):
                    if ti == 0:
                        rmsnorm_tile(st, "f", t0, n, gffn[:, layer, :], xn, ("xn", t0), xoff=t0)
                    else:
                        _rms_ops("f", t0, n, gffn[:, layer, :], xn, ("xn", t0), _norm_scr["f"], xoff=t0)
                hbi = 0
                for j in range(len(widths)):
                    q = j % 2
                    nhc = widths[j] // 128
                    for (t0, n, is_s) in ftiles:
                        hk = hkeys(t0, n)
                        hbuf = hb[hbi % 2]
                        hkey = "hb%d" % (hbi % 2)
                        hbi += 1
                        for hc in range(nhc):
                            bgt = psget()
                            for k in range(8):
                                P.op("pe", lambda e, k=k, hc=hc, bgt=bgt, q=q, t0=t0, n=n: e.matmul(
                                    PS[:, bgt, 0:n], lhsT=Wg[q][:, k, hc * 128:(hc + 1) * 128], rhs=xn[:, k, t0:t0 + n],
                                    start=(k == 0), stop=(k == 7)), reads=[("Wg", q, k // 4), ("xn", t0)], writes=pk(bgt),
                                    cost=n / 2.35 + 6)
                            but = psget()
                            for k in range(8):
                                P.op("pe", lambda e, k=k, hc=hc, but=but, q=q, t0=t0, n=n: e.matmul(
                                    PS[:, but, 0:n], lhsT=Wu[q][:, k, hc * 128:(hc + 1) * 128], rhs=xn[:, k, t0:t0 + n],
                                    start=(k == 0), stop=(k == 7)), reads=[("Wu", q, k // 4), ("xn", t0)], writes=pk(but),
                                    cost=n / 2.35 + 6)
                            slt = sl[hc % 2]
                            slk = "sl%d" % (hc % 2)
                            P.op("act", lambda e, bgt=bgt, slt=slt, n=n: e.activation(out=slt[:, 0:n], in_=PS[:, bgt, 0:n], func=AF.Silu),
                                 reads=pk(bgt), writes=[slk], cost=(224 + n) / 1.2)
                            P.op("dve", lambda e, but=but, slt=slt, hbuf=hbuf, hc=hc, n=n: e.tensor_tensor(
                                out=hbuf[:, hc, 0:n], in0=slt[:, 0:n], in1=PS[:, but, 0:n], op=ALU.mult),
                                reads=pk(but) + [slk], writes=[(hkey, hc)], cost=(160 + n) / 0.96)
                        for fo in range(8):
                            b = psget()
                            for hc in range(nhc):
                                P.op("pe", lambda e, hc=hc, fo=fo, b=b, q=q, hbuf=hbuf, n=n, nhc=nhc: e.matmul(
                                    PS[:, b, 0:n], lhsT=Wd[q][:, hc, fo * 128:(fo + 1) * 128], rhs=hbuf[:, hc, 0:n],
                                    start=(hc == 0), stop=(hc == nhc - 1)), reads=[("Wd", q), (hkey, hc)], writes=pk(b),
                                    cost=n / 2.35 + 6)
                            P.op("dve", lambda e, fo=fo, b=b, t0=t0, n=n: e.tensor_tensor(
                                out=hres[:, fo, t0:t0 + n], in0=hres[:, fo, t0:t0 + n], in1=PS[:, b, 0:n], op=ALU.add),
                                reads=pk(b) + hk, writes=hk, cost=(160 + n) / 0.96)
                    if j + 2 < len(widths):
                        load_slice(j + 2)
                if layer == 3:
                    epilogue(st)
                P.flush()

        for layer in range(4):
            if layer > 0:
                P.next_epoch()
            if layer % 2 == 0:
                even_mixer(layer)
            else:
                odd_mixer(layer)
            ffn(layer)

    return nc


_NC_CACHE = {}


def kernel(**inputs):
    f = lambda a: np.ascontiguousarray(np.asarray(a, dtype=np.float32))
    inp = {k: f(v) for k, v in inputs.items()}
    if "nc" not in _NC_CACHE:
        _NC_CACHE["nc"] = build_nc()
    nc = _NC_CACHE["nc"]
    shared = {}
    for k in ("norm_mix", "norm_ffn", "w_in_even", "w_out_even", "s5_lambda_re", "s5_lambda_im", "s5_log_dt",
              "s5_b_re", "s5_b_im", "s5_c_re", "s5_c_im", "s5_glu_w", "s5_glu_b", "sgu_norm", "sgu_w", "sgu_b",
              "w_in_odd", "w_out_odd", "pool_w", "pool_scale", "conv_w", "conv_b", "ffn_w_gate", "ffn_w_up",
              "ffn_w_down"):
        shared[k] = inp[k]
    shared["norm_final"] = inp["norm_final"].reshape(1, D)
    shared["s5_d"] = inp["s5_d"].reshape(2, 512)
    in_maps = []
    for c in range(NCORES):
        m = dict(shared)
        m["x_p"] = inp["x_prompt"][c]
        m["x_s"] = np.ascontiguousarray(inp["x_sample"][16 * c:16 * c + 16].reshape(NS, D))
        m["st_re"] = np.ascontiguousarray(inp["state_s5_re"][:, 16 * c:16 * c + 16].reshape(2, 16, 2048))
        m["st_im"] = np.ascontiguousarray(inp["state_s5_im"][:, 16 * c:16 * c + 16].reshape(2, 16, 2048))
        m["st_pool"] = np.ascontiguousarray(inp["state_pool"][:, 16 * c:16 * c + 16])
        m["st_conv"] = np.ascontiguousarray(inp["state_conv"][:, 16 * c:16 * c + 16])
        in_maps.append(m)
    res = run_bass_kernel_spmd(nc, in_maps, core_ids=list(range(NCORES)))
    R = res.results
    y_prompt = np.stack([R[c]["y_p"] for c in range(NCORES)], 0).reshape(8, SEQ, D)
    y_sample = np.concatenate([R[c]["y_s"].reshape(16, 4, D) for c in range(NCORES)], 0)
    p_re = np.stack([R[c]["o_p_re"].reshape(2, 32, 64) for c in range(NCORES)], 1)
    p_im = np.stack([R[c]["o_p_im"].reshape(2, 32, 64) for c in range(NCORES)], 1)
    p_pool = np.stack([R[c]["o_p_pool"] for c in range(NCORES)], 1)
    p_conv = np.stack([R[c]["o_p_conv"] for c in range(NCORES)], 1)
    s_re = np.concatenate([R[c]["o_s_re"].reshape(2, 16, 32, 64) for c in range(NCORES)], 1)
    s_im = np.concatenate([R[c]["o_s_im"].reshape(2, 16, 32, 64) for c in range(NCORES)], 1)
    s_v = np.concatenate([R[c]["o_s_v"].reshape(2, 16, 4, 512) for c in range(NCORES)], 1)
    s_pool = np.concatenate([R[c]["o_s_pool"].reshape(2, 16, 15, 512) for c in range(NCORES)], 1)
    s_conv = np.concatenate([R[c]["o_s_conv"].reshape(2, 16, 2, 512) for c in range(NCORES)], 1)
    outs = (y_prompt, y_sample, p_re, p_im, p_pool, p_conv, s_re, s_im, s_v, s_pool, s_conv)
    return tuple(np.ascontiguousarray(o.astype(np.float32)) for o in outs)
```

```python
import math
import numpy as np
from contextlib import ExitStack
import concourse.bass as bass
import concourse.mybir as mybir
from concourse.bass_utils import run_bass_kernel_spmd

F32 = mybir.dt.float32
BF16 = mybir.dt.bfloat16
I32 = mybir.dt.int32
ALU = mybir.AluOpType
AF = mybir.ActivationFunctionType

ENGS = ("pe", "act", "dve", "pool", "sp")
NSLOT = 12
NCORES = 8
D = 1024
SEQ = 2048
NS = 64
T = SEQ + NS
DFF = 2816
EPS = 1e-6
TM = 256
TF = 448
FS = 256
NSL = DFF // FS


class _Op(object):
    __slots__ = ("idx", "eng", "emit", "deps", "signal", "epoch", "semval",
                 "is_dma", "slot", "dval", "prev_dval", "cost", "pos")


class Prog(object):
    def __init__(self, nc, es, n_epochs=6):
        self.nc = nc
        self.ops = []
        self.regions = {}
        self.epoch = 0
        self.n_epochs = n_epochs
        self.sems = {}
        self.cnt = {}
        for e in ENGS:
            for ep in range(n_epochs):
                self.sems[(e, ep)] = es.enter_context(nc.semaphore("s_%s_%d" % (e, ep)))
        self.dsems = {}
        self.dcount = {}
        self.dnext = {}
        for q in ("sp", "pool", "act"):
            self.dnext[q] = 0
            for s in range(NSLOT):
                self.dsems[(q, s)] = es.enter_context(nc.semaphore("d_%s_%d" % (q, s)))
                self.dcount[(q, s)] = 0
        self.nflush = 0
        self.cost = {"pe": 115.0, "act": 450.0, "dve": 430.0, "pool": 600.0, "sp": 100.0}
        self.reorder = True
        self.filler = None
        self.filler_cost = 170.0
        self.nfill = 0

    def next_epoch(self):
        assert not self.ops
        self.epoch = min(self.epoch + 1, self.n_epochs - 1)

    def _add(self, eng, emit, reads, writes, is_dma, cost):
        o = _Op()
        o.idx = len(self.ops)
        o.eng = eng
        o.emit = emit
        o.signal = False
        o.epoch = self.epoch
        o.semval = None
        o.is_dma = is_dma
        o.cost = cost if cost is not None else (3000.0 if is_dma else self.cost[eng])
        deps = set()
        for k in reads:
            r = self.regions.get(k)
            if r is not None and r[0] is not None:
                deps.add(r[0])
        for k in writes:
            r = self.regions.get(k)
            if r is not None:
                if r[0] is not None:
                    deps.add(r[0])
                deps.update(r[1])
        for k in reads:
            r = self.regions.get(k)
            if r is None:
                r = [None, []]
                self.regions[k] = r
            r[1].append(o.idx)
        for k in writes:
            self.regions[k] = [o.idx, []]
        deps.discard(o.idx)
        o.deps = deps
        o.slot = None
        self.ops.append(o)
        return o

    def op(self, eng, emit, reads=(), writes=(), cost=None):
        return self._add(eng, emit, reads, writes, False, cost)

    def dma(self, q, out, in_, reads=(), writes=(), cost=None, **kw):
        def emit(e, out=out, in_=in_, kw=kw):
            return e.dma_start(out=out, in_=in_, **kw)
        return self._add(q, emit, reads, writes, True, cost)

    def _schedule(self):
        ops = self.ops
        n = len(ops)
        succs = [[] for _ in range(n)]
        indeg = [0] * n
        for o in ops:
            for d in o.deps:
                succs[d].append(o.idx)
            indeg[o.idx] = len(o.deps)
        lastd = {}
        dchain = {}
        for o in ops:
            if o.is_dma:
                if o.eng in lastd:
                    dchain[o.idx] = lastd[o.eng]
                lastd[o.eng] = o.idx
        prio = [0.0] * n
        for i in range(n - 1, -1, -1):
            m = 0.0
            for s_ in succs[i]:
                if prio[s_] > m:
                    m = prio[s_]
            prio[i] = ops[i].cost + m
        order = {e: [] for e in ENGS}
        if not self.reorder:
            for o in ops:
                order[o.eng].append(o)
            return order
        ready = {e: [] for e in ENGS}
        ready_t = [0.0] * n
        fin = [0.0] * n
        issued = [False] * n
        free_at = {e: 0.0 for e in ENGS}
        for o in ops:
            if indeg[o.idx] == 0:
                ready[o.eng].append(o.idx)
        remaining = n
        HOP = 150.0
        while remaining:
            best = None
            for e in ENGS:
                rl = ready[e]
                if not rl:
                    continue
                fa = free_at[e]
                cb = None
                for i in rl:
                    o = ops[i]
                    if o.is_dma and i in dchain and not issued[dchain[i]]:
                        continue
                    st = ready_t[i] if ready_t[i] > fa else fa
                    key = (st, -prio[i], i)
                    if cb is None or key < cb:
                        cb = key
                if cb is not None and (best is None or cb < best[0]):
                    best = (cb, e)
            assert best is not None, "scheduler deadlock"
            (st, _, i), e = best
            o = ops[i]
            ready[e].remove(i)
            issued[i] = True
            if e == "pe" and self.filler is not None and free_at[e] > 0.0:
                gap = st - free_at[e]
                if gap > 1200.0:
                    k = min(int((gap - 500.0) / self.filler_cost), 60)
                    for _ in range(k):
                        f = _Op()
                        f.idx = -1
                        f.eng = "pe"
                        f.emit = self.filler
                        f.is_dma = False
                        f.signal = False
                        order[e].append(f)
                    self.nfill += k
            if o.is_dma:
                free_at[e] = st + 80.0
                fin[i] = st + o.cost
            else:
                free_at[e] = st + o.cost
                fin[i] = st + o.cost
            order[e].append(o)
            remaining -= 1
            for s_ in succs[i]:
                t = fin[i] + (0.0 if (ops[s_].eng == e and e == "pe") else HOP)
                if t > ready_t[s_]:
                    ready_t[s_] = t
                indeg[s_] -= 1
                if indeg[s_] == 0:
                    ready[ops[s_].eng].append(s_)
        self.est_time = max(fin) if n else 0.0
        return order

    def flush(self):
        nc = self.nc
        ops = self.ops
        if not ops:
            return
        per_eng = self._schedule()
        for e in ENGS:
            for p_, o in enumerate(per_eng[e]):
                o.pos = p_
            if e != "pe":
                assert all(o.idx >= 0 for o in per_eng[e])
        for e in ("sp", "pool", "act"):
            for o in per_eng[e]:
                if o.is_dma:
                    s = self.dnext[e]
                    self.dnext[e] = (s + 1) % NSLOT
                    o.slot = s
                    o.prev_dval = self.dcount[(e, s)]
                    self.dcount[(e, s)] += 16
                    o.dval = self.dcount[(e, s)]
        red = []
        for o in ops:
            comp = {}
            dmas = []
            for d in o.deps:
                p = ops[d]
                if p.is_dma:
                    dmas.append(d)
                else:
                    if p.eng == "pe" and o.eng == "pe" and not o.is_dma:
                        continue
                    if p.eng not in comp or ops[comp[p.eng]].pos < p.pos:
                        comp[p.eng] = d
            red.append((comp, dmas))
            for d in comp.values():
                ops[d].signal = True
        cnt = self.cnt
        for e in ENGS:
            for o in per_eng[e]:
                if o.idx < 0 or o.is_dma or not o.signal:
                    continue
                key = (o.eng, o.epoch)
                cnt[key] = cnt.get(key, 0) + 1
                o.semval = cnt[key]
        sems = self.sems
        dsems = self.dsems
        n_ep = self.n_epochs
        dcount = self.dcount

        def emit_engine(e, eng_name):
            waited = {}
            dwaited = {}
            for o in per_eng[eng_name]:
                if o.idx < 0:
                    o.emit(e)
                    continue
                comp, dmas = red[o.idx]
                for pe_name, d in comp.items():
                    p = ops[d]
                    done = False
                    for ep in range(p.epoch, n_ep):
                        w = waited.get((pe_name, ep), 0)
                        if ep == p.epoch and w >= p.semval:
                            done = True
                        if ep > p.epoch and w > 0:
                            done = True
                    if done:
                        continue
                    e.wait_ge(sems[(pe_name, p.epoch)], p.semval)
                    waited[(pe_name, p.epoch)] = p.semval
                for d in dmas:
                    p = ops[d]
                    k = (p.eng, p.slot)
                    if dwaited.get(k, 0) >= p.dval:
                        continue
                    e.wait_ge(dsems[k], p.dval)
                    dwaited[k] = p.dval
                if o.is_dma:
                    k = (o.eng, o.slot)
                    if o.prev_dval > 0 and dwaited.get(k, 0) < o.prev_dval:
                        e.wait_ge(dsems[k], o.prev_dval)
                        dwaited[k] = o.prev_dval
                    inst = o.emit(e)
                    inst.then_inc(dsems[k], 16)
                else:
                    inst = o.emit(e)
                    if o.signal:
                        inst.then_inc(sems[(o.eng, o.epoch)], 1)
            if eng_name in ("sp", "pool", "act"):
                for s in range(NSLOT):
                    k = (eng_name, s)
                    if dcount[k] > 0 and dwaited.get(k, 0) < dcount[k]:
                        e.wait_ge(dsems[k], dcount[k])

        with nc.Block() as block:
            @block.tensor
            def _(e):
                emit_engine(e, "pe")

            @block.scalar
            def _(e):
                emit_engine(e, "act")

            @block.vector
            def _(e):
                emit_engine(e, "dve")

            @block.gpsimd
            def _(e):
                emit_engine(e, "pool")

            @block.sync
            def _(e):
                emit_engine(e, "sp")
        self.ops = []
        self.regions = {}
        self.nflush += 1


def build_nc(debug=False):
    nc = bass.Bass("TRN2", target_bir_lowering=False)
    try:
        nc.allow_low_precision("bf16 matmul operands with fp32 accumulation by design")
    except Exception:
        pass

    def din(name, shape):
        return nc.dram_tensor(name, list(shape), F32, kind="ExternalInput").ap()

    def dout(name, shape):
        return nc.dram_tensor(name, list(shape), F32, kind="ExternalOutput").ap()

    x_p = din("x_p", (SEQ, D))
    x_s = din("x_s", (NS, D))
    st_re = din("st_re", (2, 16, 2048))
    st_im = din("st_im", (2, 16, 2048))
    st_pool = din("st_pool", (2, 16, 15, 512))
    st_conv = din("st_conv", (2, 16, 2, 512))
    norm_mix = din("norm_mix", (4, D))
    norm_ffn = din("norm_ffn", (4, D))
    norm_final = din("norm_final", (1, D))
    w_in_even = din("w_in_even", (2, D, 1536))
    w_out_even = din("w_out_even", (2, D, D))
    lam_re = din("s5_lambda_re", (2, 32, 64))
    lam_im = din("s5_lambda_im", (2, 32, 64))
    log_dt = din("s5_log_dt", (2, 32))
    b_re = din("s5_b_re", (2, 32, 64, 16))
    b_im = din("s5_b_im", (2, 32, 64, 16))
    c_re = din("s5_c_re", (2, 32, 16, 64))
    c_im = din("s5_c_im", (2, 32, 16, 64))
    s5_d = din("s5_d", (2, 512))
    glu_w = din("s5_glu_w", (2, 512, 512))
    glu_b = din("s5_glu_b", (2, 512))
    sgu_norm = din("sgu_norm", (2, 512))
    sgu_w = din("sgu_w", (2, 8, 128, 128))
    sgu_b = din("sgu_b", (2, 8, 128))
    w_in_odd = din("w_in_odd", (2, D, 2048))
    w_out_odd = din("w_out_odd", (2, D, D))
    pool_w = din("pool_w", (2, 4, 128, 128))
    pool_scale = din("pool_scale", (2, 512))
    conv_w = din("conv_w", (2, 3, 512))
    conv_b = din("conv_b", (2, 512))
    ffn_g = din("ffn_w_gate", (4, D, DFF))
    ffn_u = din("ffn_w_up", (4, D, DFF))
    ffn_d = din("ffn_w_down", (4, DFF, D))

    y_p = dout("y_p", (SEQ, D))
    y_s = dout("y_s", (NS, D))
    o_p_re = dout("o_p_re", (2, 16, 128))
    o_p_im = dout("o_p_im", (2, 16, 128))
    o_p_pool = dout("o_p_pool", (2, 15, 512))
    o_p_conv = dout("o_p_conv", (2, 2, 512))
    o_s_re = dout("o_s_re", (2, 16, 2048))
    o_s_im = dout("o_s_im", (2, 16, 2048))
    o_s_v = dout("o_s_v", (2, NS, 512))
    o_s_pool = dout("o_s_pool", (2, 240, 512))
    o_s_conv = dout("o_s_conv", (2, 32, 512))
    dbg = dout("dbg", (128, 4096)) if debug else None

    es = ExitStack()
    with es:
        es.enter_context(nc.allow_non_contiguous_dma(reason="small strided parameter loads"))
        P = Prog(nc, es)

        _uid = [0]

        def sb(stk, name, shape, dt=F32):
            _uid[0] += 1
            return stk.enter_context(nc.sbuf_tensor("%s_u%d" % (name, _uid[0]), list(shape), dt))

        PS = es.enter_context(nc.psum_tensor("PS", [128, 8, 512], F32))
        ps_rr = [0]

        def psget(n=1):
            b = ps_rr[0]
            if b + n > 8:
                b = 0
            ps_rr[0] = (b + n) % 8
            return b

        def pk(b, n=1):
            return [("ps", b + i) for i in range(n)]

        hres = sb(es, "hres", [128, 8, T])
        ident = sb(es, "ident", [128, 128])
        identb = sb(es, "identb", [128, 128], BF16)
        onesb = sb(es, "onesb", [128, 128], BF16)
        iot = sb(es, "iot", [128, 128])
        iop = sb(es, "iop", [128, 1])
        pstage = sb(es, "pstage", [128, 128])
        pvec = sb(es, "pvec", [128, 128])
        gsgT = pvec[:, 120:128].rearrange("p (l k) -> p l k", l=2)
        gmix = pvec[:, 0:32].rearrange("p (l k) -> p l k", l=4)
        gffn = pvec[:, 32:64].rearrange("p (l k) -> p l k", l=4)
        glub = pvec[:, 64:72].rearrange("p (l k) -> p l k", l=2)
        pscale = pvec[:, 72:80].rearrange("p (l k) -> p l k", l=2)
        cw = pvec[:, 80:104].rearrange("p (l c k) -> p l c k", l=2, c=3)
        cb = pvec[:, 104:112].rearrange("p (l k) -> p l k", l=2)
        dcol = pvec[:, 112:120].rearrange("p (l k) -> p l k", l=2)
        epsc = sb(es, "epsc", [128, 1])
        s5car = sb(es, "s5car", [128, 16, 2])
        xchalo = sb(es, "xchalo", [128, 4, 15])
        zhalo = sb(es, "zhalo", [128, 4, 2])

        def V(e):
            return e

        def load_w(dst, src3, key, nsplit):
            K = dst.shape[1]
            step = K // nsplit
            for i in range(nsplit):
                nb = 128 * step * dst.shape[2] * 4
                P.dma("pool", dst[:, i * step:(i + 1) * step, :],
                      src3.rearrange("(k p) n -> p k n", p=128)[:, i * step:(i + 1) * step, :], writes=[(key, i)],
                      cost=2500.0 + nb / 150.0)

        WinP = sb(es, "WinP", [128, 8, 2048], BF16)
        WoutP = sb(es, "WoutP", [128, 8, D], BF16)
        WsmP = sb(es, "WsmP", [128, 2048], BF16)

        def load_mixer_weights(layer, only_in=False, skip_in=False):
            i = layer // 2
            if layer % 2 == 0:
                if not skip_in:
                    load_w(WinP[:, :, 0:1536], w_in_even[i], "Win", 4)
                if only_in:
                    return
                load_w(WsmP[:, :].rearrange("p (k n) -> p k n", k=4), glu_w[i], "Wsm", 1)
                load_w(WoutP, w_out_even[i], "Wout", 2)
            else:
                load_w(WinP, w_in_odd[i], "Win", 4)
                load_w(WoutP, w_out_odd[i], "Wout", 2)
                P.dma("pool", WsmP[:, 0:512].rearrange("p (g d) -> p g d", g=4), pool_w[i].rearrange("g c d -> c g d"),
                      writes=["Wsm"])

        load_mixer_weights(0, only_in=True)
        P.op("pool", lambda e: e.iota(iot[:], pattern=[[1, 128]], base=0, channel_multiplier=0,
                                      allow_small_or_imprecise_dtypes=True), writes=["iot"])
        P.op("pool", lambda e: e.iota(iop[:], pattern=[[1, 1]], base=0, channel_multiplier=1,
                                      allow_small_or_imprecise_dtypes=True), writes=["iop"])
        P.op("dve", lambda e: e.tensor_scalar(out=ident[:], in0=iot[:], scalar1=iop[:, 0:1], scalar2=None,
                                              op0=ALU.is_equal), reads=["iot", "iop"], writes=["ident"])
        P.op("dve", lambda e: e.tensor_copy(out=identb[:], in_=ident[:]), reads=["ident"], writes=["identb"])
        P.op("dve", lambda e: e.memset(onesb[:], 1.0), writes=["onesb"])
        P.op("dve", lambda e: e.memset(epsc[:], EPS), writes=["epsc"])
        P.filler = None; _unused_filler = lambda e: e.matmul(PS[:, 7, 0:128], lhsT=onesb[:], rhs=onesb[:], start=True, stop=True)
        with ExitStack() as st:
            pass
        pst_rows = [(norm_mix.rearrange("l (k p) -> (l k) p", p=128), 32), (norm_ffn.rearrange("l (k p) -> (l k) p", p=128), 32),
                    (glu_b.rearrange("l (k p) -> (l k) p", p=128), 8), (pool_scale.rearrange("l (k p) -> (l k) p", p=128), 8),
                    (conv_w.rearrange("l c (k p) -> (l c k) p", p=128), 24), (conv_b.rearrange("l (k p) -> (l k) p", p=128), 8),
                    (s5_d.rearrange("l (k p) -> (l k) p", p=128), 8), (sgu_norm.rearrange("l (k p) -> (l k) p", p=128), 8)]
        r0 = 0
        for j_, (src_, nr_) in enumerate(pst_rows):
            P.dma("sp", pstage[r0:r0 + nr_, :], src_, writes=[("pstage", j_)])
            r0 += nr_
        P.op("pe", lambda e: e.transpose(PS[:, 5, 0:128], pstage[0:128, :], ident[0:128, 0:128]),
             reads=[("pstage", j_) for j_ in range(8)] + ["ident"], writes=[("ps", 5)])
        P.op("act", lambda e: e.activation(out=pvec[:, :], in_=PS[:, 5, 0:128], func=AF.Copy), reads=[("ps", 5)],
             writes=["gmix", "gffn", "glub", "pscale", "cw", "cb", "dcol"])

        def load_x(xt):
            nsub = SEQ // 128 + 1
            for si in range(nsub):
                n = 128 if si < SEQ // 128 else NS
                src = x_p[si * 128:(si + 1) * 128, :] if si < SEQ // 128 else x_s[:, :]
                xb = xt[si % 2]
                xk = "xt%d" % (si % 2)
                P.dma("sp", xb[0:n, :], src, writes=[xk])
                for half in range(2):
                    b = psget()
                    for q in range(4):
                        k = half * 4 + q
                        P.op("pe", lambda e, b=b, q=q, k=k, xb=xb, n=n: e.transpose(
                            PS[:, b, q * 128:q * 128 + n], xb[0:n, k * 128:(k + 1) * 128], ident[0:n, 0:n]),
                            reads=[xk, "ident"], writes=pk(b))
                    eng = "act" if half == 0 else "dve"
                    if eng == "act":
                        P.op("act", lambda e, b=b, half=half, si=si, n=n: e.activation(
                            out=hres[:, half * 4:half * 4 + 4, si * 128:si * 128 + n],
                            in_=PS[:, b, :].rearrange("p (q t) -> p q t", q=4)[:, :, 0:n], func=AF.Copy),
                            reads=pk(b), writes=[("h", si, half)])
                    else:
                        P.op("dve", lambda e, b=b, half=half, si=si, n=n: e.tensor_copy(
                            out=hres[:, half * 4:half * 4 + 4, si * 128:si * 128 + n],
                            in_=PS[:, b, :].rearrange("p (q t) -> p q t", q=4)[:, :, 0:n]),
                            reads=pk(b), writes=[("h", si, half)])

        mtiles = [(i * TM, TM, False) for i in range(SEQ // TM)] + [(SEQ, NS, True)]
        ftiles = [(0, 448, False), (448, 448, False), (896, 448, False), (1344, 448, False), (1792, 320, False)]

        def hkeys(t0, n):
            ks = []
            a = (t0 // TM) * TM
            while a < t0 + n:
                ks.append(("hres", a))
                a += TM
            return ks

        def s5_setup(stk, i, XB, YC, BD, TC, TS, R4, A4, bmask):
            with ExitStack() as st:
                def t16(name):
                    return sb(st, "s5_" + name, [128, 16])
                LRI = sb(st, "s5_LRI", [128, 32])
                LR = LRI[:, 0:16]
                LI = LRI[:, 16:32]
                LDT, DT, Z, MAG, ANG = [t16(n_) for n_ in ("LDT", "DT", "Z", "MAG", "ANG")]
                SN, CS, ta, tb, tc_, td = [t16(n_) for n_ in ("SN", "CS", "ta", "tb", "tc", "td")]
                FR, FI = t16("FR"), t16("FI")
                AR = [t16("AR%d" % k) for k in range(5)]
                AI = [t16("AI%d" % k) for k in range(5)]
                BR = sb(st, "s5_BR", [128, 16, 32]); BI = sb(st, "s5_BI", [128, 16, 32])
                BBr = sb(st, "s5_BBr", [128, 16, 32]); BBi = sb(st, "s5_BBi", [128, 16, 32])
                CTr = sb(st, "s5_CTr", [128, 16, 32]); CTi = sb(st, "s5_CTi", [128, 16, 32])
                Yr = sb(st, "s5_Yr", [128, 16, 32]); Yi = sb(st, "s5_Yi", [128, 16, 32])
                W1 = sb(st, "s5_W1", [128, 16, 32]); W2 = sb(st, "s5_W2", [128, 16, 32])
                W3 = sb(st, "s5_W3", [128, 16, 32]); W4 = sb(st, "s5_W4", [128, 16, 32])
                Xr_ = sb(st, "s5_Xr", [128, 16, 32]); Xi_ = sb(st, "s5_Xi", [128, 16, 32])
                CNr = sb(st, "s5_CNr", [128, 4, 128]); CNi = sb(st, "s5_CNi", [128, 4, 128])
                cnt = [0]

                def dv(fn, reads, writes, eng="dve"):
                    P.op(eng, fn, reads=reads, writes=writes)

                def tt(out, a, b, op, r, w, eng="dve"):
                    dv(lambda e: e.tensor_tensor(out=out, in0=a, in1=b, op=op), r, w, eng=eng)

                def ts(out, a, s1, op0, r, w, s2=None, op1=None):
                    if op1 is None:
                        dv(lambda e: e.tensor_scalar(out=out, in0=a, scalar1=s1, scalar2=None, op0=op0), r, w)
                    else:
                        dv(lambda e: e.tensor_scalar(out=out, in0=a, scalar1=s1, scalar2=s2, op0=op0, op1=op1), r, w)

                lst = sb(st, "s5_lst", [32, 128])
                P.dma("sp", lst[0:16, :], lam_re[i].rearrange("(P g) n -> P (g n)", g=2), writes=[("lst", 0)])
                P.dma("sp", lst[16:32, :], lam_im[i].rearrange("(P g) n -> P (g n)", g=2), writes=[("lst", 1)])
                bl_ = psget()
                P.op("pe", lambda e: e.transpose(PS[:, bl_, 0:32], lst[:, :], ident[0:32, 0:32]),
                     reads=[("lst", 0), ("lst", 1), "ident"], writes=pk(bl_))
                P.op("act", lambda e: e.activation(out=LRI[:, :], in_=PS[:, bl_, 0:32], func=AF.Copy), reads=pk(bl_),
                     writes=["LR", "LI"])
                for g2 in range(2):
                    P.dma("sp", LDT[64 * g2:64 * g2 + 64, :],
                          log_dt[i:i + 1, :].rearrange("o (P g) -> o g P", g=2)[:, g2, :].broadcast_to([64, 16]),
                          writes=[("LDT", g2)])
                for tl in (BR, BI, CNr, CNi):
                    dv(lambda e, tl=tl: e.memset(tl[:], 0.0), [], ["z_" + tl.name], eng="pool")
                for (tl, src) in ((BR, b_re), (BI, b_im)):
                    for g2 in range(2):
                        P.dma("sp", tl[64 * g2:64 * g2 + 64, :, 16 * g2:16 * g2 + 16],
                              src[i].rearrange("(P g) n q -> g n P q", g=2)[g2], reads=["z_" + tl.name],
                              writes=[("ld_" + tl.name, g2)])
                for (tl, src) in ((CNr, c_re), (CNi, c_im)):
                    for p4 in range(4):
                        for g2 in range(2):
                            P.dma("act", tl[32 * p4 + 16 * g2:32 * p4 + 16 * g2 + 16, :, 64 * g2:64 * g2 + 64],
                                  src[i].rearrange("(f a g) p n -> a g p f n", a=4, g=2)[p4, g2],
                                  reads=["z_" + tl.name], writes=[("ld_" + tl.name, p4, g2)])
                for (src, dst) in ((CNr, CTr), (CNi, CTi)):
                    b = psget()
                    for ft in range(4):
                        P.op("pe", lambda e, ft=ft, b=b, src=src: e.transpose(
                            PS[:, b, ft * 128:(ft + 1) * 128], src[:, ft, :], ident[:]),
                            reads=[("ld_" + src.name, a_, b_) for a_ in range(4) for b_ in range(2)] + ["ident"], writes=pk(b))
                    P.op("act", lambda e, b=b, dst=dst: e.activation(
                        out=dst[:].rearrange("p a b -> p (a b)"), in_=PS[:, b, :], func=AF.Copy),
                        reads=pk(b), writes=[dst.name])
                dv(lambda e: e.activation(out=DT[:], in_=LDT[:], func=AF.Exp), [("LDT", 0), ("LDT", 1)], ["DT"], eng="act")
                tt(Z[:], LR[:], DT[:], ALU.mult, ["LR", "DT"], ["Z"])
                ts(MAG[:], Z[:], 1.0 / 120.0, ALU.mult, ["Z"], ["MAG"], 1.0 / 24.0, ALU.add)
                for c in (1.0 / 6.0, 0.5, 1.0, 1.0):
                    tt(MAG[:], MAG[:], Z[:], ALU.mult, ["MAG", "Z"], ["MAG"])
                    ts(MAG[:], MAG[:], float(c), ALU.add, ["MAG"], ["MAG"])
                tt(ANG[:], LI[:], DT[:], ALU.mult, ["LI", "DT"], ["ANG"])
                C1 = 6.28125
                C2 = 2.0 * math.pi - C1
                MAGIC = 12582912.0
                for (shift, dst) in ((0.0, SN), (0.5 * math.pi, CS)):
                    ts(ta[:], ANG[:], 1.0 / (2 * math.pi), ALU.mult, ["ANG"], ["ta"], shift / (2 * math.pi), ALU.add)
                    ts(tb[:], ta[:], MAGIC, ALU.add, ["ta"], ["tb"])
                    ts(tb[:], tb[:], -MAGIC, ALU.add, ["tb"], ["tb"])
                    dv(lambda e: e.scalar_tensor_tensor(out=ta[:], in0=tb[:], scalar=-C1, in1=ANG[:],
                                                        op0=ALU.mult, op1=ALU.add), ["tb", "ANG"], ["ta"])
                    dv(lambda e: e.scalar_tensor_tensor(out=ta[:], in0=tb[:], scalar=-C2, in1=ta[:],
                                                        op0=ALU.mult, op1=ALU.add), ["tb", "ta"], ["ta"])
                    ts(ta[:], ta[:], float(shift), ALU.add, ["ta"], ["ta"], math.pi, ALU.min)
                    ts(ta[:], ta[:], -math.pi, ALU.max, ["ta"], ["ta"])
                    dv(lambda e, dst=dst: e.activation(out=dst[:], in_=ta[:], func=AF.Sin), ["ta"], [dst.name],
                       eng="act")
                tt(AR[1][:], MAG[:], CS[:], ALU.mult, ["MAG", CS.name], ["AR1"])
                tt(AI[1][:], MAG[:], SN[:], ALU.mult, ["MAG", SN.name], ["AI1"])
                dv(lambda e: e.memset(AR[0][:], 1.0), [], ["AR0"])
                dv(lambda e: e.memset(AI[0][:], 0.0), [], ["AI0"])

                def cmul(orr, oi, ar, ai, br, bi, rk, wk, t1=None, t2=None, k1="W1", k2="W2", eng="dve"):
                    tt(t1, ar, br, ALU.mult, rk, [k1], eng)
                    tt(t2, ai, bi, ALU.mult, rk, [k2], eng)
                    tt(orr, t1, t2, ALU.subtract, [k1, k2], [wk + "r"], eng)
                    tt(t1, ar, bi, ALU.mult, rk + [wk + "r"], [k1], eng)
                    tt(t2, ai, br, ALU.mult, rk + [wk + "r"], [k2], eng)
                    tt(oi, t1, t2, ALU.add, [k1, k2], [wk + "i"], eng)

                cmul(AR[2][:], AI[2][:], AR[1][:], AI[1][:], AR[1][:], AI[1][:], ["AR1", "AI1"], "A2", tc_[:], td[:], "tc", "td")
                cmul(AR[3][:], AI[3][:], AR[2][:], AI[2][:], AR[1][:], AI[1][:], ["AR1", "AI1", "A2r", "A2i"], "A3",
                     tc_[:], td[:], "tc", "td")
                cmul(AR[4][:], AI[4][:], AR[2][:], AI[2][:], AR[2][:], AI[2][:], ["A2r", "A2i"], "A4", tc_[:], td[:], "tc", "td")
                akeys = {0: ["AR0", "AI0"], 1: ["AR1", "AI1"], 2: ["A2r", "A2i"], 3: ["A3r", "A3i"], 4: ["A4r", "A4i"]}
                dv(lambda e: e.tensor_copy(out=A4[:, :, 0], in_=AR[4][:]), akeys[4], ["A4"])
                dv(lambda e: e.tensor_copy(out=A4[:, :, 1], in_=AI[4][:]), akeys[4] + ["A4"], ["A4"])
                tt(ta[:], MAG[:], MAG[:], ALU.mult, ["MAG"], ["ta"])
                tt(R4[:], ta[:], ta[:], ALU.mult, ["ta"], ["R4"])
                ts(ta[:], AR[1][:], -1.0, ALU.add, ["AR1"], ["ta"])
                tt(tb[:], LR[:], LR[:], ALU.mult, ["LR"], ["tb"])
                tt(tc_[:], LI[:], LI[:], ALU.mult, ["LI", "A4i"], ["tc"])
                tt(tb[:], tb[:], tc_[:], ALU.add, ["tb", "tc"], ["tb"])
                dv(lambda e: e.reciprocal(out=tb[:], in_=tb[:]), ["tb"], ["tb"])
                tt(tc_[:], ta[:], LR[:], ALU.mult, ["ta", "LR"], ["tc"])
                tt(td[:], AI[1][:], LI[:], ALU.mult, ["AI1", "LI", "A4i"], ["td"])
                tt(tc_[:], tc_[:], td[:], ALU.add, ["tc", "td"], ["tc"])
                tt(FR[:], tc_[:], tb[:], ALU.mult, ["tc", "tb"], ["FR"])
                tt(tc_[:], AI[1][:], LR[:], ALU.mult, ["AI1", "LR", "FR"], ["tc"])
                tt(td[:], ta[:], LI[:], ALU.mult, ["ta", "LI", "FR"], ["td"])
                tt(tc_[:], tc_[:], td[:], ALU.subtract, ["tc", "td"], ["tc"])
                tt(FI[:], tc_[:], tb[:], ALU.mult, ["tc", "tb"], ["FI"])
                dv(lambda e: e.reciprocal(out=ta[:], in_=R4[:]), ["R4", "FI"], ["ta"])
                tt(TC[:, :, 0], AR[4][:], ta[:], ALU.mult, akeys[4] + ["ta"], ["TC"])
                tt(TS[:, :, 0], AI[4][:], ta[:], ALU.mult, akeys[4] + ["ta"], ["TS"])
                m = 1
                NCH = TM // 4
                W5 = sb(st, "s5_W5", [128, 16, 32]); W6 = sb(st, "s5_W6", [128, 16, 32])
                while m < NCH:
                    ur = TC[:, :, m - 1:m].broadcast_to([128, 16, m])
                    ui = TS[:, :, m - 1:m].broadcast_to([128, 16, m])
                    w1 = W5[:, :, 0:m]; w2 = W6[:, :, 0:m]
                    tt(w1, TC[:, :, 0:m], ur, ALU.mult, ["TC", "TS"], ["W5"], "dve")
                    tt(w2, TS[:, :, 0:m], ui, ALU.mult, ["TC", "TS"], ["W6"], "dve")
                    tt(TC[:, :, m:2 * m], w1, w2, ALU.subtract, ["W5", "W6"], ["TC"], "dve")
                    tt(w1, TC[:, :, 0:m], ui, ALU.mult, ["TC", "TS"], ["W5"], "dve")
                    tt(w2, TS[:, :, 0:m], ur, ALU.mult, ["TC", "TS"], ["W6"], "dve")
                    tt(TS[:, :, m:2 * m], w1, w2, ALU.add, ["W5", "W6"], ["TS"], "dve")
                    m *= 2

                def bc(a):
                    return a.unsqueeze(2).broadcast_to([128, 16, 32])

                cmul(BBr[:], BBi[:], BR[:], BI[:], bc(FR[:]), bc(FI[:]), [("ld_" + BR.name, 0), ("ld_" + BR.name, 1), ("ld_" + BI.name, 0), ("ld_" + BI.name, 1), "FR", "FI"],
                     "BB", W1[:], W2[:])
                for k in range(4):
                    if k == 0:
                        srcs = (BBr, BBi)
                        skeys = ["BBr", "BBi"]
                    else:
                        cmul(Xr_[:], Xi_[:], BBr[:], BBi[:], bc(AR[k][:]), bc(AI[k][:]), ["BBr", "BBi"] + akeys[k], "Xq",
                             W3[:], W4[:], "W3", "W4", eng="pool")
                        srcs = (Xr_, Xi_)
                        skeys = ["Xqr", "Xqi"]
                    s = 3 - k
                    for ri in range(2):
                        b = psget()
                        for ft in range(4):
                            P.op("pe", lambda e, ft=ft, b=b, src=srcs[ri]: e.transpose(
                                PS[:, b, ft * 128:(ft + 1) * 128],
                                src[:, 4 * ft:4 * ft + 4, :].rearrange("p a b -> p (a b)"), ident[:]),
                                reads=skeys + ["ident"], writes=pk(b))
                        P.op("act", lambda e, b=b, ri=ri, s=s: e.activation(
                            out=XB[:, :, ri, s, :], in_=PS[:, b, :].rearrange("p (f n) -> p f n", f=4), func=AF.Copy),
                            reads=pk(b), writes=["XB"])
                bBD = psget()
                for k in range(5):
                    if k == 0:
                        dv(lambda e: e.tensor_copy(out=Yr[:], in_=CTr[:]), ["s5_CTr", "XB"], ["Yr"])
                        ts(Yi[:], CTi[:], -1.0, ALU.mult, ["s5_CTi", "XB"], ["Yi"])
                    else:
                        cmul(Yr[:], Yi[:], CTr[:], CTi[:], bc(AR[k][:]), bc(AI[k][:]),
                             ["s5_CTr", "s5_CTi", "BD%d" % (k - 1), "YC"] + akeys[k], "Y", W1[:], W2[:])
                        ts(Yi[:], Yi[:], -1.0, ALU.mult, ["Yi"], ["Yi"])
                        P.op("act", lambda e, k=k: e.activation(out=YC[:, :, 0, k - 1, :], in_=Yr[:], func=AF.Copy),
                             reads=["Yr"], writes=["YC"])
                        P.op("act", lambda e, k=k: e.activation(out=YC[:, :, 1, k - 1, :], in_=Yi[:], func=AF.Copy),
                             reads=["Yi"], writes=["YC"])
                    if k < 4:
                        for ft in range(4):
                            o_ = PS[:, bBD, ft * 128:(ft + 1) * 128]
                            P.op("pe", lambda e, ft=ft, o_=o_: e.matmul(
                                o_, lhsT=BBr[:, 4 * ft:4 * ft + 4, :].rearrange("p a b -> p (a b)"),
                                rhs=Yr[:, 4 * ft:4 * ft + 4, :].rearrange("p a b -> p (a b)"), start=True, stop=False),
                                reads=["BBr", "Yr"], writes=pk(bBD))
                            P.op("pe", lambda e, ft=ft, o_=o_: e.matmul(
                                o_, lhsT=BBi[:, 4 * ft:4 * ft + 4, :].rearrange("p a b -> p (a b)"),
                                rhs=Yi[:, 4 * ft:4 * ft + 4, :].rearrange("p a b -> p (a b)"), start=False, stop=True),
                                reads=["BBi", "Yi"], writes=pk(bBD))
                        for ft in range(4):
                            if k == 0:
                                dv(lambda e, ft=ft: e.tensor_tensor(out=W1[:, 0:4, :].rearrange("p a b -> p (a b)"),
                                                                    in0=PS[:, bBD, ft * 128:(ft + 1) * 128],
                                                                    in1=bmask[:], op=ALU.mult),
                                   pk(bBD) + ["bmask"], ["W1"])
                                dv(lambda e, ft=ft: e.scalar_tensor_tensor(
                                    out=BD[:, ft, 0, :], in0=ident[:], scalar=dcol[:, i, ft:ft + 1],
                                    in1=W1[:, 0:4, :].rearrange("p a b -> p (a b)"), op0=ALU.mult, op1=ALU.add),
                                    ["W1", "ident", "dcol"], ["BD0"])
                            else:
                                dv(lambda e, ft=ft, k=k: e.tensor_tensor(out=BD[:, ft, k, :],
                                                                         in0=PS[:, bBD, ft * 128:(ft + 1) * 128],
                                                                         in1=bmask[:], op=ALU.mult),
                                   pk(bBD) + ["bmask"], ["BD%d" % k])

        def even_mixer(layer):
            i = layer // 2
            P.cost.update({"pe": 115.0, "dve": 430.0, "act": 450.0})
            with ExitStack() as st:
                NCH = TM // 4
                Win = WinP
                Wout = WoutP
                Wglu = WsmP[:, :].rearrange("p (k n) -> p k n", k=4)
                XB = sb(st, "XB", [128, 4, 2, 4, 128], BF16)
                YC = sb(st, "YC", [128, 16, 2, 4, 32], BF16)
                BD = sb(st, "BD", [128, 4, 4, 128], BF16)
                TC = sb(st, "TC", [128, 16, NCH]); TS = sb(st, "TS", [128, 16, NCH])
                R4 = sb(st, "R4", [128, 16]); A4 = sb(st, "A4", [128, 16, 2])
                wT = sb(st, "wT", [128, 8, 128], BF16)
                bbc = sb(st, "bbc", [128, 4, 128])
                gsg = sb(st, "gsg", [128, 512])
                wsc = sb(st, "wsc", [128, 4, 16])
                P.dma("sp", gsg[:], sgu_norm[i:i + 1, :].broadcast_to([128, 512]), writes=["gsg"])
                for h in range(8):
                    P.dma("sp", bbc[64 * (h % 2):64 * (h % 2) + 64, h // 2, :],
                          sgu_b[i, h:h + 1, :].broadcast_to([64, 128]), writes=[("bbc", h)])
                for h in range(8):
                    P.dma("sp", wsc[64 * (h % 2):64 * (h % 2) + 64, h // 2, :].rearrange("p (a b) -> p a b", a=4),
                          sgu_w[i, h:h + 1, 0:4, 0:4].broadcast_to([64, 4, 4]), writes=[("wsc", h)])
                P.op("dve", lambda e: e.memset(s5car[:], 0.0), writes=[("s5car", q_) for q_ in range(4)])
                with ExitStack() as st2:
                    trilT = sb(st2, "trilT", [128, 128])
                    bmask = sb(st2, "bmask", [128, 128])
                    j32 = sb(st2, "bm_j32", [128, 128])
                    S4 = sb(st2, "bm_S4", [128, 128])
                    P.op("dve", lambda e: e.tensor_scalar(out=trilT[:], in0=iot[:], scalar1=iop[:, 0:1], scalar2=None,
                                                          op0=ALU.is_ge), reads=["iot", "iop"], writes=["trilT"])
                    P.op("pool", lambda e: e.iota(j32[:], pattern=[[1, 4], [0, 32]], base=0, channel_multiplier=0,
                                                  allow_small_or_imprecise_dtypes=True), writes=["j32"])
                    P.op("dve", lambda e: e.tensor_scalar(out=S4[:], in0=j32[:], scalar1=iop[:, 0:1], scalar2=None,
                                                          op0=ALU.is_equal), reads=["j32", "iop"], writes=["S4"])
                    P.op("pe", lambda e: e.matmul(PS[:, 6, 0:128], lhsT=S4[0:4, :], rhs=S4[0:4, :], start=True, stop=True),
                         reads=["S4"], writes=[("ps", 6)])
                    P.op("dve", lambda e: e.tensor_copy(out=bmask[:], in_=PS[:, 6, 0:128]), reads=[("ps", 6)], writes=["bmask"])
                    wld = [sb(st2, "wld%d" % q, [128, 128]) for q in range(2)]
                    for h in range(8):
                        wl = wld[h % 2]
                        P.dma("sp", wl[:], sgu_w[i, h], writes=["wld%d" % (h % 2)])
                        b = psget()
                        P.op("pe", lambda e, b=b, wl=wl: e.transpose(PS[:, b, 0:128], wl[:], ident[:]),
                             reads=["wld%d" % (h % 2), "ident"], writes=pk(b))
                        P.op("dve", lambda e, b=b, h=h: e.tensor_tensor(out=wT[:, h, :], in0=PS[:, b, 0:128],
                                                                        in1=trilT[:], op=ALU.mult),
                             reads=pk(b) + ["trilT"], writes=["wT"])
                    xt_ = [sb(st2, "xt%d" % q_, [128, D]) for q_ in range(2)] if layer == 0 else None
                    s5_setup(st2, i, XB, YC, BD, TC, TS, R4, A4, bmask)
                    if layer == 0:
                        load_x(xt_)
                    P.flush()
                glubh = sb(st, "glubh", [128, 4])
                P.op("dve", lambda e: e.tensor_scalar(out=glubh[:], in0=glub[:, i, :], scalar1=0.5, scalar2=None, op0=ALU.mult),
                     writes=["glubh"])
                for k_ in range(8):
                    P.op("dve", lambda e, k_=k_: e.tensor_scalar(out=WinP[:, k_, 0:1536], in0=WinP[:, k_, 0:1536],
                                                                 scalar1=gmix[:, layer, k_:k_ + 1], scalar2=None, op0=ALU.mult),
                         writes=["Win"], cost=700.0)
                if layer == 0:
                    load_mixer_weights(0, skip_in=True)
                P.op("dve", lambda e: e.tensor_scalar(out=WoutP[:, 0:4, :], in0=WoutP[:, 0:4, :], scalar1=0.5, scalar2=None,
                                                      op0=ALU.mult), reads=[("Wout", 0), ("Wout", 1)], writes=["Wout"])
                xnt = sb(st, "xnt", [128, 8, TM], BF16)
                uaL = [sb(st, "ua%d" % q, [128, 4, TM], BF16) for q in range(2)]
                ubL = [sb(st, "ub%d" % q, [128, 4, TM], BF16) for q in range(2)]
                vn = sb(st, "vn", [128, 512])
                vnbL = [sb(st, "vnb%d" % q, [128, 2, 512], BF16) for q in range(2)]
                vjunk = sb(st, "vjunk", [128, 512], BF16)
                vss = sb(st, "vss", [128, 2])
                ymix = sb(st, "ymix", [128, 8, TM], BF16)
                tA = sb(st, "tA", [128, 4, NCH]); tB = sb(st, "tB", [128, 4, NCH])
                Gin = sb(st, "Gin", [128, 4, 2, NCH])
                wtail = [WinP[:, k_, 1536:2048].bitcast(F32).rearrange("p (a c) -> p a c", a=4) for k_ in range(8)]
                tC, tD, tE, tF, tG, tH = wtail[0:6]
                GsL = [sb(st, "Gs0", [128, 4, 2, NCH]),
                       WinP[:, 6:8, 1536:2048].bitcast(F32).rearrange("p k (a c) -> p a k c", a=4)]
                Hf = sb(st, "Hf", [128, 4, 2, NCH + 1])
                Hb = sb(st, "Hb", [128, 4, 2, NCH], BF16)
                sqy = sb(st, "sqy", [128, TM])
                zf = sb(st, "zf", [128, 4, TM])
                zb = sb(st, "zb", [128, 4, TM], BF16)
                sg2 = sb(st, "sg2", [128, TM])
                stmp = sb(st, "stmp", [128, TM])
                vT = sb(st, "vT", [128, 4, NS])
                sacc = sb(st, "sacc", [128, 16, 4])
                h0s = sb(st, "h0s", [16, 1024])
                h0T = sb(st, "h0T", [128, 16, 2, 16])
                hend = sb(st, "hend", [128, 16, 2, 16])
                hoP = sb(st, "hoP", [16, 2, 128])
                def front(ti, t0, n, is_s):
                    nch = n // 4
                    par = ti % 2
                    ua = uaL[par]; ub = ubL[par]; vnb = vnbL[par]
                    hk = ("hres", t0)
                    rmsnorm_tile(st, "m", t0, n, gmix[:, layer, :], xnt, "xnt") if ti == 0 else \
                        rmsnorm_tile_again("m", t0, n, gmix[:, layer, :], xnt, "xnt")
                    for ft in range(4):
                        b = psget()
                        for k in range(8):
                            P.op("pe", lambda e, k=k, ft=ft, b=b: e.matmul(
                                PS[:, b, 0:n], lhsT=Win[:, k, ft * 128:(ft + 1) * 128], rhs=xnt[:, k, 0:n],
                                start=(k == 0), stop=(k == 7)), reads=["Win", "xnt"], writes=pk(b))
                        P.op("act", lambda e, ft=ft, b=b: e.activation(out=ua[:, ft, 0:n], in_=PS[:, b, 0:n],
                                                                       func=AF.Copy),
                             reads=pk(b), writes=[("ua", par, ft)])
                    for ft in range(4):
                        b = psget()
                        for k in range(8):
                            P.op("pe", lambda e, k=k, ft=ft, b=b: e.matmul(
                                PS[:, b, 0:n], lhsT=Win[:, k, 512 + ft * 128:512 + (ft + 1) * 128], rhs=xnt[:, k, 0:n],
                                start=(k == 0), stop=(k == 7)), reads=["Win", "xnt"], writes=pk(b))
                        P.op("act", lambda e, ft=ft, b=b: e.activation(out=ub[:, ft, 0:n], in_=PS[:, b, 0:n],
                                                                       func=AF.Copy),
                             reads=pk(b), writes=[("ub", par, ft)])
                    nsub = (n + 127) // 128
                    for sj in range(nsub):
                        m = min(128, n - sj * 128)
                        b = psget()
                        for k in range(8):
                            P.op("pe", lambda e, k=k, b=b, sj=sj, m=m: e.matmul(
                                PS[0:m, b, :], lhsT=xnt[:, k, sj * 128:sj * 128 + m], rhs=Win[:, k, 1024:1536],
                                start=(k == 0), stop=(k == 7)), reads=["Win", "xnt"], writes=pk(b))
                        P.op("act", lambda e, b=b, sj=sj, m=m: e.activation(
                            out=vjunk[0:m, :], in_=PS[0:m, b, :], func=AF.Square, accum_out=vss[0:m, sj:sj + 1]),
                            reads=pk(b), writes=["vjunk", ("vss", sj)])
                        P.op("act", lambda e, sj=sj, m=m: e.activation(
                            out=vss[0:m, sj:sj + 1], in_=vss[0:m, sj:sj + 1], func=AF.Ln, bias=epsc[0:m, 0:1],
                            scale=1.0 / 512.0), reads=[("vss", sj), "epsc"], writes=[("vss", sj)], cost=250.0)
                        P.op("act", lambda e, sj=sj, m=m: e.activation(
                            out=vss[0:m, sj:sj + 1], in_=vss[0:m, sj:sj + 1], func=AF.Exp, scale=-0.5),
                            reads=[("vss", sj)], writes=[("vss", sj)], cost=250.0)
                        if not is_s:
                            P.op("act", lambda e, b=b, sj=sj, m=m: e.activation(
                                out=vnb[0:m, sj, :], in_=PS[0:m, b, :], func=AF.Copy, scale=vss[0:m, sj:sj + 1]),
                                reads=pk(b) + [("vss", sj)], writes=[("vnb", par, sj)], cost=(224 + 512) / 1.2)
                        else:
                            P.op("dve", lambda e, b=b, sj=sj, m=m: e.scalar_tensor_tensor(
                                out=vnb[0:m, sj, :], in0=PS[0:m, b, :], scalar=vss[0:m, sj:sj + 1], in1=gsg[0:m, :],
                                op0=ALU.mult, op1=ALU.mult), reads=pk(b) + [("vss", sj), "gsg"], writes=[("vnb", par, sj)])
                        if is_s:
                            P.op("dve", lambda e, b=b, sj=sj, m=m: e.scalar_tensor_tensor(
                                out=vn[0:m, :], in0=PS[0:m, b, :], scalar=vss[0:m, sj:sj + 1], in1=gsg[0:m, :],
                                op0=ALU.mult, op1=ALU.mult), reads=pk(b) + [("vss", sj), "gsg"], writes=[("vn", 0)])
                    if is_s:
                        P.dma("sp", o_s_v[i], vn[0:NS, :], reads=[("vn", 0)])
                        for ri in range(2):
                            b = psget()
                            for hf in range(2):
                                P.dma("sp", h0s[:, :], (st_re if ri == 0 else st_im)[i][:, hf * 1024:(hf + 1) * 1024],
                                      writes=["h0s"])
                                for q in range(8):
                                    Pp = hf * 8 + q
                                    P.op("pe", lambda e, b=b, Pp=Pp, q=q: e.transpose(
                                        PS[:, b, Pp * 16:(Pp + 1) * 16], h0s[:, q * 128:(q + 1) * 128],
                                        ident[0:16, 0:16]), reads=["h0s", "ident"], writes=pk(b))
                            P.op("dve", lambda e, b=b, ri=ri: e.tensor_copy(
                                out=h0T[:, :, ri, :], in_=PS[:, b, 0:256].rearrange("p (a b) -> p a b", a=16)),
                                reads=pk(b), writes=["h0T"])
                def back(ti, t0, n, is_s):
                    nch = n // 4
                    par = ti % 2
                    ua = uaL[par]; ub = ubL[par]; vnb = vnbL[par]
                    for ft in range(4):
                        if not is_s:
                            P.op("pool", lambda e, ft=ft: e.tensor_copy(out=Hf[:, :, :, 0], in_=s5car[:, 4 * ft:4 * ft + 4, :]),
                                 reads=[("s5car", ft)], writes=["Hf0"])
                        b4 = psget(4)
                        for p4 in range(4):
                            for ri in range(2):
                                for s in range(4):
                                    P.op("pe", lambda e, p4=p4, ri=ri, s=s, ft=ft, b4=b4: e.matmul(
                                        PS[:, b4 + p4, ri * NCH:ri * NCH + nch],
                                        lhsT=XB[32 * p4:32 * p4 + 32, ft, ri, s, :],
                                        rhs=ua[32 * p4:32 * p4 + 32, ft, s:n:4],
                                        start=(s == 0), stop=(s == 3), tile_position=(32 * p4, 0)),
                                        reads=["XB", ("ua", par, ft)], writes=pk(b4, 4), cost=40.0)
                        Xr = PS[:, b4:b4 + 4, 0:nch]
                        Xi = PS[:, b4:b4 + 4, NCH:NCH + nch]
                        if not is_s:
                            Cc = TC[:, 4 * ft:4 * ft + 4, 0:nch]
                            Ss = TS[:, 4 * ft:4 * ft + 4, 0:nch]
                            tAa = tA[:, :, 0:nch]; tBb = tB[:, :, 0:nch]
                            GinR = Gin[:, :, 0, 0:nch]; GinI = Gin[:, :, 1, 0:nch]
                            x4 = pk(b4, 4)
                            gq = ft % 2
                            Gsq = GsL[gq]

                            def tt(out, a, bb, op, r, w, eng="dve"):
                                P.op(eng, lambda e: e.tensor_tensor(out=out, in0=a, in1=bb, op=op), reads=r, writes=w)
                            tCc = tC[:, :, 0:nch]; tDd = tD[:, :, 0:nch]
                            tt(tAa, Xr, Cc, ALU.mult, x4 + ["TC"], ["tA"])
                            tt(tBb, Xi, Ss, ALU.mult, x4 + ["TS"], ["tB"])
                            tt(tCc, Xi, Cc, ALU.mult, x4 + ["TC"], ["tC"])
                            tt(tDd, Xr, Ss, ALU.mult, x4 + ["TS"], ["tD"])
                            tt(GinR, tAa, tBb, ALU.add, ["tA", "tB"], ["GinR"])
                            tt(GinI, tCc, tDd, ALU.subtract, ["tC", "tD"], ["GinI"])
                            for p4 in range(4):
                                Pp = 4 * ft + p4
                                for ri in range(2):
                                    P.op("dve", lambda e, p4=p4, ri=ri, Pp=Pp, Gsq=Gsq: e.tensor_tensor_scan(
                                        out=Gsq[:, p4, ri, 0:nch], data0=R4[:, Pp:Pp + 1].broadcast_to([128, nch]),
                                        data1=Gin[:, p4, ri, 0:nch], initial=s5car[:, Pp, ri:ri + 1],
                                        op0=ALU.mult, op1=ALU.add),
                                        reads=["GinR" if ri == 0 else "GinI", "R4", ("s5car", ft)],
                                        writes=[("Gs", gq, p4, ri)], cost=350.0)
                            GR = Gsq[:, :, 0, 0:nch]; GI = Gsq[:, :, 1, 0:nch]
                            gk = [("Gs", gq, a_, b_) for a_ in range(4) for b_ in range(2)]
                            tEe = tE[:, :, 0:nch]; tFf = tF[:, :, 0:nch]; tGg = tG[:, :, 0:nch]; tHh = tH[:, :, 0:nch]
                            tt(tEe, GR, Cc, ALU.mult, gk + ["TC"], ["tE"], "pool")
                            tt(tFf, GI, Ss, ALU.mult, gk + ["TS"], ["tF"], "pool")
                            tt(tGg, GR, Ss, ALU.mult, gk + ["TS"], ["tG"], "pool")
                            tt(tHh, GI, Cc, ALU.mult, gk + ["TC"], ["tH"], "pool")
                            tt(Hf[:, :, 0, 1:nch + 1], tEe, tFf, ALU.subtract, ["tE", "tF", "Hf0"], ["HfR"], "pool")
                            tt(Hf[:, :, 1, 1:nch + 1], tGg, tHh, ALU.add, ["tG", "tH", "Hf0"], ["HfI"], "pool")
                            P.op("act", lambda e: e.activation(out=Hb[:, :, :, 0:nch], in_=Hf[:, :, :, 0:nch], func=AF.Copy),
                                 reads=["HfR", "HfI", "Hf0"], writes=["Hb"])
                            P.op("pool", lambda e, ft=ft: e.tensor_copy(out=s5car[:, 4 * ft:4 * ft + 4, :],
                                                                        in_=Hf[:, :, :, nch]),
                                 reads=["HfR", "HfI"] + gk, writes=[("s5car", ft)])
                        else:
                            h0r = h0T[:, 4 * ft:4 * ft + 4, 0, :]; h0i = h0T[:, 4 * ft:4 * ft + 4, 1, :]
                            a4r = A4[:, 4 * ft:4 * ft + 4, 0:1].broadcast_to([128, 4, 16])
                            a4i = A4[:, 4 * ft:4 * ft + 4, 1:2].broadcast_to([128, 4, 16])
                            tAa = tA[:, :, 0:16]; tBb = tB[:, :, 0:16]
                            x4 = pk(b4, 4)

                            def tt(out, a, bb, op, r, w):
                                P.op("dve", lambda e: e.tensor_tensor(out=out, in0=a, in1=bb, op=op), reads=r, writes=w)
                            tt(tAa, h0r, a4r, ALU.mult, ["h0T", "A4"], ["tA"])
                            tt(tBb, h0i, a4i, ALU.mult, ["h0T", "A4"], ["tB"])
                            tt(tAa, tAa, tBb, ALU.subtract, ["tA", "tB"], ["tA"])
                            tt(hend[:, 4 * ft:4 * ft + 4, 0, :], tAa, Xr, ALU.add, ["tA"] + x4, [("hend", ft, 0)])
                            tt(tAa, h0r, a4i, ALU.mult, ["h0T", "A4", ("hend", ft, 0)], ["tA"])
                            tt(tBb, h0i, a4r, ALU.mult, ["h0T", "A4", ("hend", ft, 0)], ["tB"])
                            tt(tAa, tAa, tBb, ALU.add, ["tA", "tB"], ["tA"])
                            tt(hend[:, 4 * ft:4 * ft + 4, 1, :], tAa, Xi, ALU.add, ["tA"] + x4, [("hend", ft, 1)])
                            P.op("act", lambda e, ft=ft: e.activation(out=Hb[:, :, :, 0:16],
                                                                      in_=h0T[:, 4 * ft:4 * ft + 4, :, :], func=AF.Copy),
                                 reads=["h0T"], writes=["Hb"])
                        by = psget()
                        for t in range(4):
                            o_ = PS[:, by, t * NCH:t * NCH + nch]
                            for tau in range(t + 1):
                                P.op("pe", lambda e, t=t, tau=tau, ft=ft, o_=o_: e.matmul(
                                    o_, lhsT=BD[:, ft, tau, :], rhs=ua[:, ft, (t - tau):n:4],
                                    start=(tau == 0), stop=False), reads=[("ua", par, ft)], writes=pk(by), cost=60.0)
                            for p4 in range(4):
                                for ri in range(2):
                                    last = (ri == 1)
                                    P.op("pe", lambda e, t=t, p4=p4, ri=ri, ft=ft, by=by, last=last: e.matmul(
                                        PS[32 * p4:32 * p4 + 32, by, t * NCH:t * NCH + nch],
                                        lhsT=YC[:, 4 * ft + p4, ri, t, :], rhs=Hb[:, p4, ri, 0:nch],
                                        start=False, stop=last, tile_position=(0, 32 * p4)),
                                        reads=["Hb"], writes=pk(by), cost=45.0)
                        yv = PS[:, by, 0:4 * NCH].rearrange("p (t c) -> p c t", t=4)[:, 0:nch, :]
                        sq3 = sqy[:, 0:n].rearrange("p (c t) -> p c t", t=4)
                        z3 = zf[:, ft, 0:n].rearrange("p (c t) -> p c t", t=4)
                        P.op("act", lambda e, yv=yv, z3=z3: e.activation(out=z3, in_=yv, func=AF.Gelu_apprx_tanh),
                             reads=pk(by), writes=[("zf", ft)])
                        P.op("act", lambda e, ft=ft: e.activation(out=zb[:, ft, 0:n], in_=zf[:, ft, 0:n], func=AF.Copy),
                             reads=[("zf", ft)], writes=[("zb", ft)])
                    for fo in range(4):
                        b = psget()
                        for fi in range(4):
                            P.op("pe", lambda e, fi=fi, fo=fo, b=b: e.matmul(
                                PS[:, b, 0:n], lhsT=Wglu[:, fi, fo * 128:(fo + 1) * 128], rhs=zb[:, fi, 0:n],
                                start=(fi == 0), stop=(fi == 3)), reads=["Wsm", ("Wsm", 0)] + [("zb", q) for q in range(4)],
                                writes=pk(b))
                        P.op("act", lambda e, fo=fo, b=b: e.activation(out=sg2[:, 0:n], in_=PS[:, b, 0:n], func=AF.Tanh,
                                                                       bias=glubh[:, fo:fo + 1], scale=0.5),
                             reads=pk(b) + ["glubh"], writes=["sg2"])
                        P.op("dve", lambda e, fo=fo: e.scalar_tensor_tensor(out=ymix[:, fo, 0:n], in0=sg2[:, 0:n], scalar=1.0,
                                                                            in1=zf[:, fo, 0:n], op0=ALU.add, op1=ALU.mult),
                             reads=["sg2", ("zf", fo)], writes=["ymix"])
                    if not is_s:
                        for hp in range(4):
                            b = psget()
                            for j in range(n // 128):
                                for h2 in range(2):
                                    h = 2 * hp + h2
                                    P.op("pe", lambda e, b=b, j=j, h2=h2, h=h: e.matmul(
                                        PS[64 * h2:64 * h2 + 64, b, j * 128:(j + 1) * 128],
                                        lhsT=vnb[:, j, h * 64:(h + 1) * 64], rhs=wT[:, h, :],
                                        start=True, stop=True, tile_position=(0, 64 * h2)),
                                        reads=["wT", ("vnb", par, j)], writes=pk(b))
                            P.op("dve", lambda e, b=b, hp=hp: e.scalar_tensor_tensor(
                                out=stmp[:, 0:n].rearrange("p (j i) -> p j i", i=128),
                                in0=PS[:, b, 0:n].rearrange("p (j i) -> p j i", i=128), scalar=gsgT[:, i, hp:hp + 1],
                                in1=bbc[:, hp, :].unsqueeze(1).broadcast_to([128, n // 128, 128]),
                                op0=ALU.mult, op1=ALU.add),
                                reads=pk(b) + ["bbc"], writes=["stmp"])
                            P.op("dve", lambda e, hp=hp: e.tensor_tensor(out=ymix[:, 4 + hp, 0:n], in0=stmp[:, 0:n],
                                                                         in1=ub[:, hp, 0:n], op=ALU.mult),
                                 reads=["stmp", ("ub", par, hp)], writes=["ymix"])
                    else:
                        b = psget()
                        for ft in range(4):
                            P.op("pe", lambda e, b=b, ft=ft: e.transpose(PS[:, b, ft * NS:(ft + 1) * NS],
                                                                         vn[0:NS, ft * 128:(ft + 1) * 128],
                                                                         ident[0:NS, 0:NS]),
                                 reads=[("vn", 0), "ident"], writes=pk(b))
                        P.op("dve", lambda e, b=b: e.tensor_copy(out=vT[:], in_=PS[:, b, 0:4 * NS].rearrange("p (f t) -> p f t", f=4)),
                             reads=pk(b), writes=["vT"])
                        for ft in range(4):
                            v3 = vT[:, ft, :].rearrange("p (b j) -> p b j", j=4)
                            for ii in range(4):
                                P.op("dve", lambda e, ft=ft, ii=ii, v3=v3: e.tensor_scalar(
                                    out=sacc[:, :, ii], in0=v3[:, :, 0], scalar1=wsc[:, ft, 4 * ii:4 * ii + 1],
                                    scalar2=bbc[:, ft, ii:ii + 1], op0=ALU.mult, op1=ALU.add),
                                    reads=["vT", "wsc", "bbc"], writes=["sacc"])
                                for jj in range(1, ii + 1):
                                    P.op("dve", lambda e, ft=ft, ii=ii, jj=jj, v3=v3: e.scalar_tensor_tensor(
                                        out=sacc[:, :, ii], in0=v3[:, :, jj], scalar=wsc[:, ft, 4 * ii + jj:4 * ii + jj + 1],
                                        in1=sacc[:, :, ii], op0=ALU.mult, op1=ALU.add),
                                        reads=["vT", "wsc", "sacc"], writes=["sacc"])
                            P.op("dve", lambda e, ft=ft: e.tensor_tensor(
                                out=ymix[:, 4 + ft, 0:NS], in0=sacc[:].rearrange("p b i -> p (b i)"),
                                in1=ub[:, ft, 0:NS], op=ALU.mult), reads=["sacc", ("ub", par, ft)], writes=["ymix"])
                    out_proj_tile(Wout, "Wout", ymix, "ymix", t0, n, pair=False)
                    if is_s:
                        for ri in range(2):
                            for hf in range(2):
                                for h2 in range(2):
                                    half = hf * 2 + h2
                                    b = psget()
                                    for q in range(4):
                                        Pp = half * 4 + q
                                        P.op("pe", lambda e, b=b, q=q, Pp=Pp, ri=ri: e.transpose(
                                            PS[0:16, b, q * 128:(q + 1) * 128], hend[:, Pp, ri, :], ident[:]),
                                            reads=[("hend", Pp // 4, ri), "ident"], writes=pk(b))
                                    P.op("act", lambda e, b=b, h2=h2: e.activation(
                                        out=h0s[:, h2 * 512:(h2 + 1) * 512], in_=PS[0:16, b, :], func=AF.Copy),
                                        reads=pk(b), writes=["h0s"])
                                P.dma("sp", (o_s_re if ri == 0 else o_s_im)[i][:, hf * 1024:(hf + 1) * 1024], h0s[:, :],
                                      reads=["h0s"])
                    if (not is_s) and t0 + n == SEQ:
                        for ri in range(2):
                            b = psget()
                            P.op("pe", lambda e, b=b, ri=ri: e.transpose(PS[0:16, b, 0:128], s5car[:, :, ri], ident[:]),
                                 reads=[("s5car", q_) for q_ in range(4)] + ["ident"], writes=pk(b))
                            P.op("act", lambda e, b=b, ri=ri: e.activation(out=hoP[:, ri, :], in_=PS[0:16, b, 0:128], func=AF.Copy),
                                 reads=pk(b), writes=["hoP"])
                        P.dma("sp", o_p_re[i], hoP[:, 0, :], reads=["hoP"])
                        P.dma("sp", o_p_im[i], hoP[:, 1, :], reads=["hoP"])
                seq = list(enumerate(mtiles))
                for idx, (ti, (t0, n, is_s)) in enumerate(seq):
                    front(ti, t0, n, is_s)
                    if idx >= 1:
                        pti, (pt0, pn, ps_) = seq[idx - 1]
                        back(pti, pt0, pn, ps_)
                lti, (lt0, ln, ls_) = seq[-1]
                back(lti, lt0, ln, ls_)
                P.flush()

        _norm_scr = {}

        def rmsnorm_tile_again(tag, t0, n, gvec, xn_out, xn_key):
            _rms_ops(tag, t0, n, gvec, xn_out, xn_key, _norm_scr[tag])

        def _rms_ops(tag, t0, n, gvec, xn_out, xn_key, srs, xoff=0):
            sr, sr2 = srs
            hk = hkeys(t0, n)
            sqv = xn_out[:, :, xoff:xoff + n]
            P.op("act", lambda e: e.activation(out=sqv, in_=hres[:, :, t0:t0 + n], func=AF.Square),
                 reads=hk, writes=[xn_key])
            b = psget()
            for k in range(8):
                P.op("pe", lambda e, k=k, b=b: e.matmul(PS[:, b, 0:n], lhsT=onesb[:], rhs=xn_out[:, k, xoff:xoff + n],
                                                        start=(k == 0), stop=(k == 7)),
                     reads=[xn_key, "onesb"], writes=pk(b))
            P.op("act", lambda e, b=b: e.activation(out=sr[:, 0:n], in_=PS[:, b, 0:n], func=AF.Ln,
                                                    bias=epsc[:, 0:1], scale=1.0 / D),
                 reads=pk(b) + ["epsc"], writes=["sr_" + tag])
            P.op("act", lambda e: e.activation(out=sr[:, 0:n], in_=sr[:, 0:n], func=AF.Exp, scale=-0.5),
                 reads=["sr_" + tag], writes=["sr_" + tag])
            if tag != "f":
                P.op("dve", lambda e: e.tensor_tensor(
                    out=xn_out[:, :, xoff:xoff + n], in0=hres[:, :, t0:t0 + n],
                    in1=sr2[:, 0:n].unsqueeze(1).broadcast_to([128, 8, n]), op=ALU.mult),
                    reads=hk + ["sr_" + tag], writes=[xn_key], cost=(160 + 8 * n) / 0.96)
                return
            for k in range(8):
                P.op("dve", lambda e, k=k: e.scalar_tensor_tensor(
                    out=xn_out[:, k, xoff:xoff + n], in0=hres[:, k, t0:t0 + n], scalar=gvec[:, k:k + 1],
                    in1=sr2[:, 0:n], op0=ALU.mult, op1=ALU.mult),
                    reads=hk + ["sr_" + tag, "gmix", "gffn"], writes=[xn_key])

        def rmsnorm_tile(stk, tag, t0, n, gvec, xn_out, xn_key, xoff=0):
            nmax = TM if tag == "m" else TF
            sr = sb(stk, "sr_" + tag, [128, nmax])
            sr2 = sr
            _norm_scr[tag] = (sr, sr2)
            _rms_ops(tag, t0, n, gvec, xn_out, xn_key, (sr, sr2), xoff=xoff)

        def out_proj_single(Wout, wkey, ymix, ykey, t0, n):
            hk = hkeys(t0, n)
            for fo in range(8):
                b = psget()
                for k in range(8):
                    P.op("pe", lambda e, k=k, fo=fo, b=b: e.matmul(
                        PS[:, b, 0:n], lhsT=Wout[:, k, fo * 128:(fo + 1) * 128], rhs=ymix[:, k, 0:n],
                        start=(k == 0), stop=(k == 7)), reads=[wkey, ykey], writes=pk(b))
                P.op("dve", lambda e, fo=fo, b=b: e.tensor_tensor(
                    out=hres[:, fo, t0:t0 + n], in0=hres[:, fo, t0:t0 + n], in1=PS[:, b, 0:n], op=ALU.add),
                    reads=pk(b) + hk, writes=hk)

        def out_proj_tile(Wout, wkey, ymix, ykey, t0, n, pair=True):
            hk = hkeys(t0, n)
            if not pair:
                return out_proj_single(Wout, wkey, ymix, ykey, t0, n)
            for fp in range(4):
                b = psget(2)
                for h_ in range(2):
                    fo = 2 * fp + h_
                    for k in range(8):
                        P.op("pe", lambda e, k=k, fo=fo, b=b, h_=h_: e.matmul(
                            PS[:, b + h_, 0:n], lhsT=Wout[:, k, fo * 128:(fo + 1) * 128], rhs=ymix[:, k, 0:n],
                            start=(k == 0), stop=(k == 7)), reads=[wkey, ykey], writes=pk(b + h_))
                P.op("dve", lambda e, fp=fp, b=b: e.tensor_tensor(
                    out=hres[:, 2 * fp:2 * fp + 2, t0:t0 + n], in0=hres[:, 2 * fp:2 * fp + 2, t0:t0 + n],
                    in1=PS[:, b:b + 2, 0:n], op=ALU.add),
                    reads=pk(b, 2) + hk, writes=hk, cost=(160 + 2 * n) / 0.96)

        def odd_mixer(layer):
            i = layer // 2
            P.cost.update({"pe": 115.0, "dve": 430.0, "act": 450.0})
            with ExitStack() as st:
                Win = WinP
                Wout = WoutP
                Wp = WsmP[:, 0:512].rearrange("p (g d) -> p g d", g=4)
                for k_ in range(8):
                    P.op("dve", lambda e, k_=k_: e.tensor_scalar(out=WinP[:, k_, :], in0=WinP[:, k_, :],
                                                                 scalar1=gmix[:, layer, k_:k_ + 1], scalar2=None, op0=ALU.mult),
                         writes=["Win"], cost=850.0)
                xnt = sb(st, "xnto", [128, 8, TM], BF16)
                XCL = [sb(st, "XC%d" % q, [128, 4, 15 + TM]) for q in range(2)]
                PA = sb(st, "PA", [128, 15 + TM]); PB = sb(st, "PB", [128, 15 + TM])
                diff = sb(st, "diff", [128, 4, TM], BF16)
                xdL = [sb(st, "xd%d" % q, [128, 4, TM]) for q in range(2)]
                bgL = [sb(st, "bg%d" % q, [128, 4, TM]) for q in range(2)]
                ZL = [sb(st, "Z%d" % q, [128, 4, 2 + TM]) for q in range(2)]
                ca = sb(st, "ca", [128, TM])
                ymix = sb(st, "ymixo", [128, 8, TM], BF16)
                invn = sb(st, "invn", [128, 4, 15])
                XCs = sb(st, "XCs", [128, 4, 16, 19])
                PAs = sb(st, "PAs", [128, 16, 19]); PBs = sb(st, "PBs", [128, 16, 19])
                Zs = sb(st, "Zs", [128, 4, 16, 6])
                spl = [sb(st, "spl%d" % q, [128, 512]) for q in range(2)]
                scl = sb(st, "scl", [32, 512])
                otp = sb(st, "otp", [128, 512])
                otc = sb(st, "otc", [32, 512])
                opp = sb(st, "opp", [16, 512])
                opc = sb(st, "opc", [2, 512])
                xct = sb(st, "xct", [128, 128])
                zct = sb(st, "zct", [128, 32])
                P.op("pool", lambda e: e.iota(invn[:], pattern=[[0, 4], [1, 15]], base=1, channel_multiplier=0,
                                              allow_small_or_imprecise_dtypes=True), writes=["invn"])
                for gi in range(4):
                    P.op("dve", lambda e, gi=gi: e.tensor_scalar(out=invn[:, gi, :], in0=invn[:, gi, :],
                                                                 scalar1=float(2 ** (gi + 1)), scalar2=None, op0=ALU.min),
                         reads=["invn"], writes=["invn"])
                P.op("dve", lambda e: e.reciprocal(out=invn[:], in_=invn[:]), reads=["invn"], writes=["invn"])
                P.op("dve", lambda e: e.memset(xchalo[:], 0.0), writes=["xchalo"])
                P.op("dve", lambda e: e.memset(zhalo[:], 0.0), writes=["zhalo"])
                P.op("pool", lambda e: e.memset(PA[:], 0.0), writes=["P0"])
                P.op("pool", lambda e: e.memset(PB[:], 0.0), writes=["P1"])
                P.op("pool", lambda e: e.memset(PAs[:], 0.0), writes=["Ps0"])
                P.op("pool", lambda e: e.memset(PBs[:], 0.0), writes=["Ps1"])
                def front(ti, t0, n, is_s):
                    par = ti % 2
                    XC = XCL[par]; xd = xdL[par]; bg = bgL[par]; Z = ZL[par]
                    if ti == 0:
                        rmsnorm_tile(st, "m", t0, n, gmix[:, layer, :], xnt, "xnt")
                    else:
                        rmsnorm_tile_again("m", t0, n, gmix[:, layer, :], xnt, "xnt")
                    if not is_s:
                        P.op("dve", lambda e: e.tensor_copy(out=XC[:, :, 0:15], in_=xchalo[:]), reads=["xchalo"],
                             writes=[("XChalo", par)])
                        P.op("dve", lambda e: e.tensor_copy(out=Z[:, :, 0:2], in_=zhalo[:]), reads=["zhalo"],
                             writes=[("Zhalo", par)])
                    else:
                        P.dma("sp", spl[0][:], st_pool[i].rearrange("b r c -> (b r) c")[0:128, :], writes=["spl0"])
                        P.dma("sp", spl[1][0:112, :], st_pool[i].rearrange("b r c -> (b r) c")[128:240, :], writes=["spl1"])
                        P.dma("sp", scl[:], st_conv[i].rearrange("b r c -> (b r) c"), writes=["scl"])
                        for ft in range(4):
                            b = psget()
                            P.op("pe", lambda e, b=b, ft=ft: e.transpose(PS[:, b, 0:128], spl[0][:, ft * 128:(ft + 1) * 128], ident[:]),
                                 reads=["spl0", "ident"], writes=pk(b))
                            P.op("pe", lambda e, b=b, ft=ft: e.transpose(PS[:, b, 128:240], spl[1][0:112, ft * 128:(ft + 1) * 128],
                                                                         ident[0:112, 0:112]),
                                 reads=["spl1", "ident"], writes=pk(b))
                            P.op("pe", lambda e, b=b, ft=ft: e.transpose(PS[:, b, 256:288], scl[:, ft * 128:(ft + 1) * 128],
                                                                         ident[0:32, 0:32]),
                                 reads=["scl", "ident"], writes=pk(b))
                            P.op("dve", lambda e, b=b, ft=ft: e.tensor_copy(
                                out=XCs[:, ft, :, 0:15], in_=PS[:, b, 0:240].rearrange("p (b r) -> p b r", r=15)),
                                reads=pk(b), writes=[("XCs", ft)])
                            P.op("dve", lambda e, b=b, ft=ft: e.tensor_copy(
                                out=Zs[:, ft, :, 0:2], in_=PS[:, b, 256:288].rearrange("p (b r) -> p b r", r=2)),
                                reads=pk(b), writes=[("Zs", ft)])
                    for ft in range(4):
                        b = psget()
                        for k in range(8):
                            P.op("pe", lambda e, k=k, ft=ft, b=b: e.matmul(
                                PS[:, b, 0:n], lhsT=Win[:, k, ft * 128:(ft + 1) * 128], rhs=xnt[:, k, 0:n],
                                start=(k == 0), stop=(k == 7)), reads=["Win", "xnt"], writes=pk(b))
                        if not is_s:
                            P.op("act", lambda e, ft=ft, b=b: e.activation(out=XC[:, ft, 15:15 + n], in_=PS[:, b, 0:n], func=AF.Copy),
                                 reads=pk(b), writes=[("XC", par, ft)])
                        else:
                            P.op("act", lambda e, ft=ft, b=b: e.activation(
                                out=XCs[:, ft, :, 15:19], in_=PS[:, b, 0:NS].rearrange("p (b t) -> p b t", t=4), func=AF.Copy),
                                reads=pk(b) + [("XCs", ft)], writes=[("XCs", ft)])
                    for ft in range(4):
                        b = psget()
                        for k in range(8):
                            P.op("pe", lambda e, k=k, ft=ft, b=b: e.matmul(
                                PS[:, b, 0:n], lhsT=Win[:, k, 512 + ft * 128:512 + (ft + 1) * 128], rhs=xnt[:, k, 0:n],
                                start=(k == 0), stop=(k == 7)), reads=["Win", "xnt"], writes=pk(b))
                        P.op("act", lambda e, ft=ft, b=b: e.activation(out=xd[:, ft, 0:n], in_=PS[:, b, 0:n], func=AF.Copy),
                             reads=pk(b), writes=[("xd", par, ft)])
                    for ft in range(4):
                        b = psget()
                        for k in range(8):
                            P.op("pe", lambda e, k=k, ft=ft, b=b: e.matmul(
                                PS[:, b, 0:n], lhsT=Win[:, k, 1024 + ft * 128:1024 + (ft + 1) * 128], rhs=xnt[:, k, 0:n],
                                start=(k == 0), stop=(k == 7)), reads=["Win", "xnt"], writes=pk(b))
                        P.op("act", lambda e, ft=ft, b=b: e.activation(out=bg[:, ft, 0:n], in_=PS[:, b, 0:n], func=AF.Copy),
                             reads=pk(b), writes=[("bg", par, ft)])
                    for ft in range(4):
                        b = psget()
                        for k in range(8):
                            P.op("pe", lambda e, k=k, ft=ft, b=b: e.matmul(
                                PS[:, b, 0:n], lhsT=Win[:, k, 1536 + ft * 128:1536 + (ft + 1) * 128], rhs=xnt[:, k, 0:n],
                                start=(k == 0), stop=(k == 7)), reads=["Win", "xnt"], writes=pk(b))
                        if not is_s:
                            P.op("dve", lambda e, ft=ft, b=b: e.tensor_tensor(out=Z[:, ft, 2:2 + n], in0=PS[:, b, 0:n],
                                                                              in1=xd[:, ft, 0:n], op=ALU.mult),
                                 reads=pk(b) + [("xd", par, ft)], writes=[("Z", par, ft)])
                        else:
                            P.op("dve", lambda e, ft=ft, b=b: e.tensor_tensor(
                                out=Zs[:, ft, :, 2:6], in0=PS[:, b, 0:NS].rearrange("p (b t) -> p b t", t=4),
                                in1=xd[:, ft, 0:NS].rearrange("p (b t) -> p b t", t=4), op=ALU.mult),
                                reads=pk(b) + [("xd", par, ft), ("Zs", ft)], writes=[("Zs", ft)])
                    if not is_s:
                        P.op("dve", lambda e: e.tensor_copy(out=xchalo[:], in_=XC[:, :, n:n + 15]),
                             reads=[("XC", par, q) for q in range(4)] + [(("XChalo", par), par)], writes=["xchalo"])
                        P.op("dve", lambda e: e.tensor_copy(out=zhalo[:], in_=Z[:, :, n:n + 2]),
                             reads=[("Z", par, q) for q in range(4)] + [(("Zhalo", par), par)], writes=["zhalo"])
                def back(ti, t0, n, is_s):
                    par = ti % 2
                    XC = XCL[par]; xd = xdL[par]; bg = bgL[par]; Z = ZL[par]
                    for gi in range(4):
                        w = 2 ** (gi + 1)
                        if not is_s:
                            L = 15 + n
                            src = XC[:, gi, 0:L]
                            bufs = [PA, PB]
                            cur = src
                            ckey = [("XC", par, gi), ("XChalo", par)]
                            d = 1
                            q = 0
                            while d < w:
                                dst = bufs[q % 2]
                                dk = "P%d" % (q % 2)
                                P.op("pool", lambda e, cur=cur, dst=dst, d=d, L=L: e.tensor_tensor(
                                    out=dst[:, d:L], in0=cur[:, d:L], in1=cur[:, 0:L - d], op=ALU.add),
                                    reads=ckey, writes=[dk], cost=800.0)
                                cur = dst[:, 0:L]
                                ckey = [dk]
                                d *= 2
                                q += 1
                            P.op("dve", lambda e, cur=cur, gi=gi, w=w: e.scalar_tensor_tensor(
                                out=diff[:, gi, 0:n], in0=cur[:, 15:15 + n], scalar=1.0 / w, in1=XC[:, gi, 15:15 + n],
                                op0=ALU.mult, op1=ALU.subtract), reads=ckey + [("XC", par, gi)], writes=[("diff", gi)])
                            if t0 == 0:
                                P.op("dve", lambda e, cur=cur, gi=gi: e.tensor_tensor(
                                    out=ca[:, 0:15], in0=cur[:, 15:30], in1=invn[:, gi, :], op=ALU.mult),
                                    reads=ckey + ["invn"], writes=["ca"])
                                P.op("dve", lambda e, gi=gi: e.tensor_tensor(
                                    out=diff[:, gi, 0:15], in0=ca[:, 0:15], in1=XC[:, gi, 15:30], op=ALU.subtract),
                                    reads=["ca", ("XC", par, gi), ("diff", gi)], writes=[("diff", gi)])
                        else:
                            L = 19
                            cur = XCs[:, gi, :, :]
                            ckey = [("XCs", gi)]
                            bufs = [PAs, PBs]
                            d = 1
                            q = 0
                            while d < w:
                                dst = bufs[q % 2]
                                dk = "Ps%d" % (q % 2)
                                P.op("pool", lambda e, cur=cur, dst=dst, d=d: e.tensor_tensor(
                                    out=dst[:, :, d:19], in0=cur[:, :, d:19], in1=cur[:, :, 0:19 - d], op=ALU.add),
                                    reads=ckey, writes=[dk], cost=800.0)
                                cur = dst[:, :, :]
                                ckey = [dk]
                                d *= 2
                                q += 1
                            P.op("dve", lambda e, cur=cur, gi=gi, w=w: e.scalar_tensor_tensor(
                                out=diff[:, gi, 0:NS].rearrange("p (b t) -> p b t", t=4), in0=cur[:, :, 15:19],
                                scalar=1.0 / w, in1=XCs[:, gi, :, 15:19], op0=ALU.mult, op1=ALU.subtract),
                                reads=ckey + [("XCs", gi)], writes=[("diff", gi)])
                        b = psget()
                        P.op("pe", lambda e, gi=gi, b=b: e.matmul(PS[:, b, 0:n], lhsT=Wp[:, gi, :], rhs=diff[:, gi, 0:n],
                                                                  start=True, stop=True),
                             reads=["Wsm", ("diff", gi)], writes=pk(b))
                        P.op("act", lambda e, gi=gi, b=b: e.activation(out=ymix[:, gi, 0:n], in_=PS[:, b, 0:n], func=AF.Copy,
                                                                       scale=pscale[:, i, gi:gi + 1]),
                             reads=pk(b) + ["pscale"], writes=["ymix"])
                    for ft in range(4):
                        if not is_s:
                            z0 = Z[:, ft, 0:n]; z1 = Z[:, ft, 1:n + 1]; z2 = Z[:, ft, 2:n + 2]
                            cav = ca[:, 0:n]
                            bgv = bg[:, ft, 0:n]
                            yv = ymix[:, 4 + ft, 0:n]
                            zk = [("Z", par, ft), ("Zhalo", par)]
                        else:
                            z0 = Zs[:, ft, :, 0:4]; z1 = Zs[:, ft, :, 1:5]; z2 = Zs[:, ft, :, 2:6]
                            cav = ca[:, 0:NS].rearrange("p (b t) -> p b t", t=4)
                            bgv = bg[:, ft, 0:NS].rearrange("p (b t) -> p b t", t=4)
                            yv = ymix[:, 4 + ft, 0:NS].rearrange("p (b t) -> p b t", t=4)
                            zk = [("Zs", ft)]
                        P.op("act", lambda e, ft=ft, z0=z0, cav=cav: e.activation(
                            out=cav, in_=z0, func=AF.Identity, scale=cw[:, i, 0, ft:ft + 1], bias=cb[:, i, ft:ft + 1]),
                            reads=zk + ["cw", "cb"], writes=["ca"])
                        P.op("dve", lambda e, ft=ft, z1=z1, cav=cav: e.scalar_tensor_tensor(
                            out=cav, in0=z1, scalar=cw[:, i, 1, ft:ft + 1], in1=cav, op0=ALU.mult, op1=ALU.add),
                            reads=zk + ["cw", "ca"], writes=["ca"])
                        P.op("dve", lambda e, ft=ft, z2=z2, cav=cav: e.scalar_tensor_tensor(
                            out=cav, in0=z2, scalar=cw[:, i, 2, ft:ft + 1], in1=cav, op0=ALU.mult, op1=ALU.add),
                            reads=zk + ["cw", "ca"], writes=["ca"])
                        P.op("dve", lambda e, cav=cav, bgv=bgv, yv=yv: e.tensor_tensor(out=yv, in0=cav, in1=bgv, op=ALU.mult),
                             reads=["ca", ("bg", par, ft)], writes=["ymix"])
                    out_proj_tile(Wout, "Wout", ymix, "ymix", t0, n)
                    if (not is_s) and t0 + n == SEQ:
                        for ft in range(4):
                            b = psget()
                            P.op("pe", lambda e, b=b, ft=ft: e.transpose(PS[0:15, b, 0:128], xchalo[:, ft, :], ident[:]),
                                 reads=["xchalo", "ident"], writes=pk(b))
                            P.op("pe", lambda e, b=b, ft=ft: e.transpose(PS[0:2, b, 128:256], zhalo[:, ft, :], ident[:]),
                                 reads=["zhalo", "ident"], writes=pk(b))
                            P.op("act", lambda e, b=b, ft=ft: e.activation(out=opp[0:15, ft * 128:(ft + 1) * 128],
                                                                           in_=PS[0:15, b, 0:128], func=AF.Copy),
                                 reads=pk(b), writes=["opp"])
                            P.op("act", lambda e, b=b, ft=ft: e.activation(out=opc[0:2, ft * 128:(ft + 1) * 128],
                                                                           in_=PS[0:2, b, 128:256], func=AF.Copy),
                                 reads=pk(b), writes=["opc"])
                        P.dma("sp", o_p_pool[i], opp[0:15, :], reads=["opp"])
                        P.dma("sp", o_p_conv[i], opc[0:2, :], reads=["opc"])
                    if is_s:
                        for half in range(2):
                            for ft in range(4):
                                P.op("dve", lambda e, ft=ft, half=half: e.tensor_copy(
                                    out=xct[:, 0:120].rearrange("p (b r) -> p b r", r=15),
                                    in_=XCs[:, ft, half * 8:half * 8 + 8, 4:19]), reads=[("XCs", ft)], writes=["xct"])
                                b = psget()
                                P.op("pe", lambda e, b=b: e.transpose(PS[0:120, b, 0:128], xct[:, 0:120], ident[:]),
                                     reads=["xct", "ident"], writes=pk(b))
                                P.op("act", lambda e, b=b, ft=ft: e.activation(out=otp[0:120, ft * 128:(ft + 1) * 128],
                                                                               in_=PS[0:120, b, 0:128], func=AF.Copy),
                                     reads=pk(b), writes=["otp"])
                            P.dma("sp", o_s_pool[i, half * 120:half * 120 + 120, :], otp[0:120, :], reads=["otp"])
                        for ft in range(4):
                            P.op("dve", lambda e, ft=ft: e.tensor_copy(
                                out=zct[:, 0:32].rearrange("p (b r) -> p b r", r=2), in_=Zs[:, ft, :, 4:6]),
                                reads=[("Zs", ft)], writes=["zct"])
                            b = psget()
                            P.op("pe", lambda e, b=b: e.transpose(PS[0:32, b, 0:128], zct[:, 0:32], ident[:]),
                                 reads=["zct", "ident"], writes=pk(b))
                            P.op("act", lambda e, b=b, ft=ft: e.activation(out=otc[:, ft * 128:(ft + 1) * 128],
                                                                           in_=PS[0:32, b, 0:128], func=AF.Copy),
                                 reads=pk(b), writes=["otc"])
                        P.dma("sp", o_s_conv[i], otc[:, :], reads=["otc"])
                seq = list(enumerate(mtiles))
                for idx, (ti, (t0, n, is_s)) in enumerate(seq):
                    front(ti, t0, n, is_s)
                    if idx >= 1:
                        pti, (pt0, pn, ps_) = seq[idx - 1]
                        back(pti, pt0, pn, ps_)
                lti, (lt0, ln, ls_) = seq[-1]
                back(lti, lt0, ln, ls_)
                P.flush()

        def epilogue(st):
            gfin = WinP[:, 2, :].bitcast(F32)
            P.dma("sp", gfin, norm_final.broadcast_to([128, D]), writes=["gfin"])
            junk = sb(st, "fjunk", [128, 512], BF16)
            ss = sb(st, "fss", [128, 2, 2])
            yo = [WinP[:, q, :].bitcast(F32) for q in range(2)]
            nsub = SEQ // 128 + 1
            for si in range(nsub):
                n = 128 if si < SEQ // 128 else NS
                dst = y_p[si * 128:(si + 1) * 128, :] if si < SEQ // 128 else y_s[:, :]
                par = si % 2
                yb = yo[par]
                yk = "yo%d" % par
                hk = hkeys(si * 128, n)
                bb = psget(2)
                for k in range(8):
                    P.op("pe", lambda e, k=k, bb=bb, si=si, n=n: e.transpose(
                        PS[0:n, bb + k // 4, (k % 4) * 128:(k % 4 + 1) * 128], hres[:, k, si * 128:si * 128 + n], ident[:]),
                        reads=["ident"] + hk, writes=pk(bb, 2), cost=110.0)
                for half in range(2):
                    P.op("act", lambda e, bb=bb, half=half, n=n, par=par: e.activation(
                        out=junk[0:n, :], in_=PS[0:n, bb + half, :], func=AF.Square, accum_out=ss[0:n, par, half:half + 1]),
                        reads=pk(bb, 2), writes=["fjunk", ("fss", par, half)])
                P.op("dve", lambda e, n=n, par=par: e.tensor_tensor(out=ss[0:n, par, 0:1], in0=ss[0:n, par, 0:1],
                                                                     in1=ss[0:n, par, 1:2], op=ALU.add),
                     reads=[("fss", par, 0), ("fss", par, 1)], writes=[("fss", par, 0)], cost=100.0)
                P.op("act", lambda e, n=n, par=par: e.activation(out=ss[0:n, par, 0:1], in_=ss[0:n, par, 0:1], func=AF.Sqrt,
                                                                  bias=epsc[0:n, 0:1], scale=1.0 / D),
                     reads=[("fss", par, 0)], writes=[("fss", par, 0)], cost=250.0)
                P.op("dve", lambda e, n=n, par=par: e.reciprocal(out=ss[0:n, par, 0:1], in_=ss[0:n, par, 0:1]),
                     reads=[("fss", par, 0)], writes=[("fss", par, 0)], cost=100.0)
                for half in range(2):
                    P.op("dve", lambda e, bb=bb, half=half, n=n, yb=yb, par=par: e.scalar_tensor_tensor(
                        out=yb[0:n, half * 512:(half + 1) * 512], in0=PS[0:n, bb + half, :], scalar=ss[0:n, par, 0:1],
                        in1=gfin[0:n, half * 512:(half + 1) * 512], op0=ALU.mult, op1=ALU.mult),
                        reads=pk(bb, 2) + [("fss", par, 0), "gfin"], writes=[yk], cost=750.0)
                P.dma("sp", dst, yb[0:n, :], reads=[yk])

        def ffn(layer):
            widths = [384] * 7 + [128]
            offs = [sum(widths[:j]) for j in range(len(widths))]
            with ExitStack() as st:
                xn = sb(st, "xn_all", [128, 8, T], BF16)
                Wg = [sb(st, "Wg%d" % q, [128, 8, 384], BF16) for q in range(2)]
                Wu = [sb(st, "Wu%d" % q, [128, 8, 384], BF16) for q in range(2)]
                Wd = [sb(st, "Wd%d" % q, [128, 3, D], BF16) for q in range(2)]
                sl = [sb(st, "sl%d" % q, [128, TF]) for q in range(2)]
                hb = [sb(st, "hb%d" % q, [128, 3, TF], BF16) for q in range(2)]

                P.cost.update({"pe": 195.0, "dve": 630.0, "act": 560.0})

                def load_slice(j):
                    q = j % 2
                    w = widths[j]
                    o = offs[j]
                    c = 2500.0 + 128 * 8 * w * 4 / 150.0
                    for kh in range(2):
                        P.dma("pool", Wg[q][:, 4 * kh:4 * kh + 4, 0:w],
                              ffn_g[layer].rearrange("(k p) n -> p k n", p=128)[:, 4 * kh:4 * kh + 4, o:o + w],
                              writes=[("Wg", q, kh)], cost=c / 2)
                    for kh in range(2):
                        P.dma("pool", Wu[q][:, 4 * kh:4 * kh + 4, 0:w],
                              ffn_u[layer].rearrange("(k p) n -> p k n", p=128)[:, 4 * kh:4 * kh + 4, o:o + w],
                              writes=[("Wu", q, kh)], cost=c / 2)
                    P.dma("pool", Wd[q][:, 0:w // 128, :],
                          ffn_d[layer].rearrange("(k p) n -> p k n", p=128)[:, o // 128:(o + w) // 128, :],
                          writes=[("Wd", q)], cost=c)
                load_slice(0)
                load_slice(1)
                if layer + 1 < 4:
                    load_mixer_weights(layer + 1)
                for ti, (t0, n, is_s) in enumerate(ftiles):
                    if ti == 0:
                        rmsnorm_tile(st, "f", t0, n, gffn[:, layer, :], xn, ("xn", t0), xoff=t0)
                    else:
                        _rms_ops("f", t0, n, gffn[:, layer, :], xn, ("xn", t0), _norm_scr["f"], xoff=t0)
                hbi = 0
                for j in range(len(widths)):
                    q = j % 2
                    nhc = widths[j] // 128
                    for (t0, n, is_s) in ftiles:
                        hk = hkeys(t0, n)
                        hbuf = hb[hbi % 2]
                        hkey = "hb%d" % (hbi % 2)
                        hbi += 1
                        for hc in range(nhc):
                            bgt = psget()
                            for k in range(8):
                                P.op("pe", lambda e, k=k, hc=hc, bgt=bgt, q=q, t0=t0, n=n: e.matmul(
                                    PS[:, bgt, 0:n], lhsT=Wg[q][:, k, hc * 128:(hc + 1) * 128], rhs=xn[:, k, t0:t0 + n],
                                    start=(k == 0), stop=(k == 7)), reads=[("Wg", q, k // 4), ("xn", t0)], writes=pk(bgt),
                                    cost=n / 2.35 + 6)
                            but = psget()
                            for k in range(8):
                                P.op("pe", lambda e, k=k, hc=hc, but=but, q=q, t0=t0, n=n: e.matmul(
                                    PS[:, but, 0:n], lhsT=Wu[q][:, k, hc * 128:(hc + 1) * 128], rhs=xn[:, k, t0:t0 + n],
                                    start=(k == 0), stop=(k == 7)), reads=[("Wu", q, k // 4), ("xn", t0)], writes=pk(but),
                                    cost=n / 2.35 + 6)
                            slt = sl[hc % 2]
                            slk = "sl%d" % (hc % 2)
                            P.op("act", lambda e, bgt=bgt, slt=slt, n=n: e.activation(out=slt[:, 0:n], in_=PS[:, bgt, 0:n], func=AF.Silu),
                                 reads=pk(bgt), writes=[slk], cost=(224 + n) / 1.2)
                            P.op("dve", lambda e, but=but, slt=slt, hbuf=hbuf, hc=hc, n=n: e.tensor_tensor(
                                out=hbuf[:, hc, 0:n], in0=slt[:, 0:n], in1=PS[:, but, 0:n], op=ALU.mult),
                                reads=pk(but) + [slk], writes=[(hkey, hc)], cost=(160 + n) / 0.96)
                        for fo in range(8):
                            b = psget()
                            for hc in range(nhc):
                                P.op("pe", lambda e, hc=hc, fo=fo, b=b, q=q, hbuf=hbuf, n=n, nhc=nhc: e.matmul(
                                    PS[:, b, 0:n], lhsT=Wd[q][:, hc, fo * 128:(fo + 1) * 128], rhs=hbuf[:, hc, 0:n],
                                    start=(hc == 0), stop=(hc == nhc - 1)), reads=[("Wd", q), (hkey, hc)], writes=pk(b),
                                    cost=n / 2.35 + 6)
                            P.op("dve", lambda e, fo=fo, b=b, t0=t0, n=n: e.tensor_tensor(
                                out=hres[:, fo, t0:t0 + n], in0=hres[:, fo, t0:t0 + n], in1=PS[:, b, 0:n], op=ALU.add),
                                reads=pk(b) + hk, writes=hk, cost=(160 + n) / 0.96)
                    if j + 2 < len(widths):
                        load_slice(j + 2)
                if layer == 3:
                    epilogue(st)
                P.flush()

        for layer in range(4):
            if layer > 0:
                P.next_epoch()
            if layer % 2 == 0:
                even_mixer(layer)
            else:
                odd_mixer(layer)
            ffn(layer)

    return nc


_NC_CACHE = {}


def kernel(**inputs):
    f = lambda a: np.ascontiguousarray(np.asarray(a, dtype=np.float32))
    inp = {k: f(v) for k, v in inputs.items()}
    if "nc" not in _NC_CACHE:
        _NC_CACHE["nc"] = build_nc()
    nc = _NC_CACHE["nc"]
    shared = {}
    for k in ("norm_mix", "norm_ffn", "w_in_even", "w_out_even", "s5_lambda_re", "s5_lambda_im", "s5_log_dt",
              "s5_b_re", "s5_b_im", "s5_c_re", "s5_c_im", "s5_glu_w", "s5_glu_b", "sgu_norm", "sgu_w", "sgu_b",
              "w_in_odd", "w_out_odd", "pool_w", "pool_scale", "conv_w", "conv_b", "ffn_w_gate", "ffn_w_up",
              "ffn_w_down"):
        shared[k] = inp[k]
    shared["norm_final"] = inp["norm_final"].reshape(1, D)
    shared["s5_d"] = inp["s5_d"].reshape(2, 512)
    in_maps = []
    for c in range(NCORES):
        m = dict(shared)
        m["x_p"] = inp["x_prompt"][c]
        m["x_s"] = np.ascontiguousarray(inp["x_sample"][16 * c:16 * c + 16].reshape(NS, D))
        m["st_re"] = np.ascontiguousarray(inp["state_s5_re"][:, 16 * c:16 * c + 16].reshape(2, 16, 2048))
        m["st_im"] = np.ascontiguousarray(inp["state_s5_im"][:, 16 * c:16 * c + 16].reshape(2, 16, 2048))
        m["st_pool"] = np.ascontiguousarray(inp["state_pool"][:, 16 * c:16 * c + 16])
        m["st_conv"] = np.ascontiguousarray(inp["state_conv"][:, 16 * c:16 * c + 16])
        in_maps.append(m)
    res = run_bass_kernel_spmd(nc, in_maps, core_ids=list(range(NCORES)))
    R = res.results
    y_prompt = np.stack([R[c]["y_p"] for c in range(NCORES)], 0).reshape(8, SEQ, D)
    y_sample = np.concatenate([R[c]["y_s"].reshape(16, 4, D) for c in range(NCORES)], 0)
    p_re = np.stack([R[c]["o_p_re"].reshape(2, 32, 64) for c in range(NCORES)], 1)
    p_im = np.stack([R[c]["o_p_im"].reshape(2, 32, 64) for c in range(NCORES)], 1)
    p_pool = np.stack([R[c]["o_p_pool"] for c in range(NCORES)], 1)
    p_conv = np.stack([R[c]["o_p_conv"] for c in range(NCORES)], 1)
    s_re = np.concatenate([R[c]["o_s_re"].reshape(2, 16, 32, 64) for c in range(NCORES)], 1)
    s_im = np.concatenate([R[c]["o_s_im"].reshape(2, 16, 32, 64) for c in range(NCORES)], 1)
    s_v = np.concatenate([R[c]["o_s_v"].reshape(2, 16, 4, 512) for c in range(NCORES)], 1)
    s_pool = np.concatenate([R[c]["o_s_pool"].reshape(2, 16, 15, 512) for c in range(NCORES)], 1)
    s_conv = np.concatenate([R[c]["o_s_conv"].reshape(2, 16, 2, 512) for c in range(NCORES)], 1)
    outs = (y_prompt, y_sample, p_re, p_im, p_pool, p_conv, s_re, s_im, s_v, s_pool, s_conv)
    return tuple(np.ascontiguousarray(o.astype(np.float32)) for o in outs)
```
